# Optimizing a Trainium2 kernel written in Bass

```python
import math
import jax, jax.numpy as jnp
from jax import lax
import numpy as np

D_MODEL = 1024
BATCH = 8
SEQ = 2048
DEPTH = 1
DEC_BATCH = 128
DEC_SEQ = 8
PAST_LEN = 16384
PAGE_SIZE = 128

MIX_WIDTH = D_MODEL
RWKV_WIDTH = MIX_WIDTH // 2
RWKV_HEAD = 64
RWKV_HEADS = RWKV_WIDTH // RWKV_HEAD
HGRN_WIDTH = MIX_WIDTH - RWKV_WIDTH
HGRN_HEAD = 128
HGRN_HEADS = HGRN_WIDTH // HGRN_HEAD
DECAY_RANK = 64
AICL_RANK = 64
GATE_RANK = 128
RWKV_PROJ = 3 * RWKV_WIDTH + DECAY_RANK + AICL_RANK + GATE_RANK
HGRN_PROJ = 4 * HGRN_WIDTH
PROJ = RWKV_PROJ + HGRN_PROJ
RWKV_SPLITS = (RWKV_WIDTH, 2 * RWKV_WIDTH, 3 * RWKV_WIDTH, 3 * RWKV_WIDTH + DECAY_RANK,
               3 * RWKV_WIDTH + DECAY_RANK + AICL_RANK)
HGRN_SPLITS = (HGRN_WIDTH, 2 * HGRN_WIDTH, 3 * HGRN_WIDTH)
D_FF = 4 * D_MODEL
HGRN_CHUNK = 16
ALPHA = (2.0 * DEPTH) ** 0.25
BETA = (8.0 * DEPTH) ** -0.25
LN_EPS = 1e-5
GN_EPS = RWKV_HEAD * 1e-5
RMS_EPS = 1e-6

kernel_name = 'hybrid_rwkv7_hgrn2_deepnorm_step'


def layer_norm(x, g, b):
    x = x.astype(jnp.float32)
    mu = jnp.mean(x, -1, keepdims=True)
    var = jnp.mean(jnp.square(x - mu), -1, keepdims=True)
    return (x - mu) * lax.rsqrt(var + LN_EPS) * g + b


def rwkv7_recurrence(r, w, k, v, a, b, s0):
    def step(S, inp):
        r_t, w_t, k_t, v_t, a_t, b_t = inp
        sa = jnp.einsum('bhij,bhj->bhi', S, a_t)
        S = S * w_t[:, :, None, :] + sa[..., None] * b_t[:, :, None, :] + v_t[..., None] * k_t[:, :, None, :]
        y = jnp.einsum('bhij,bhj->bhi', S, r_t)
        return S, y
    xs = tuple(jnp.swapaxes(t, 0, 1) for t in (r, w, k, v, a, b))
    S, ys = lax.scan(step, s0, xs)
    return jnp.swapaxes(ys, 0, 1), S


def hgrn2_chunked(q, logf, k, i, s0):
    B, T, H, K = q.shape
    C = HGRN_CHUNK
    pad = (-T) % C
    padw = ((0, 0), (0, pad), (0, 0), (0, 0))
    q, logf, k, i = (jnp.pad(t, padw) for t in (q, logf, k, i))
    nc = (T + pad) // C
    def chunks(t):
        return t.reshape(B, nc, C, H, t.shape[-1]).transpose(1, 0, 3, 2, 4)
    mask = jnp.tril(jnp.ones((C, C), dtype=bool))
    def step(S, inp):
        qc, gc, kc, ic = inp
        bc = jnp.cumsum(gc, axis=2)
        diff = bc[:, :, :, None, :] - bc[:, :, None, :, :]
        dec = jnp.exp(jnp.where(mask[:, :, None], diff, -jnp.inf))
        att = jnp.einsum('bhtk,bhsk,bhtsk->bhts', qc, kc, dec)
        o = jnp.einsum('bhts,bhsv->bhtv', att, ic) + jnp.einsum('bhtk,bhkv->bhtv', qc * jnp.exp(bc), S)
        blast = bc[:, :, -1:, :]
        S = jnp.exp(blast[:, :, 0, :])[..., None] * S + jnp.einsum('bhsk,bhsv->bhkv', kc * jnp.exp(blast - bc), ic)
        return S, o
    S, os_ = lax.scan(step, s0, (chunks(q), chunks(logf), chunks(k), chunks(i)))
    o = os_.transpose(1, 0, 3, 2, 4).reshape(B, nc * C, H, -1)[:, :T]
    return o, S


def mixer(x, s_rwkv, s_hgrn, s_shift, lb, w_in, shift_mu, w0, w1u, a0, a1u, g1u, k_k, k_a, r_k,
          ln_x_w, ln_x_b, hg_norm_w, w_out):
    f32 = jnp.float32
    B, T, _ = x.shape
    proj = jnp.einsum('btd,dp->btp', x, w_in).astype(f32)
    p_rw, p_hg = proj[..., :RWKV_PROJ], proj[..., RWKV_PROJ:]
    prev = jnp.concatenate([s_shift[:, None, :].astype(f32), p_rw[:, :-1]], axis=1)
    xs = p_rw + shift_mu * (prev - p_rw)
    r, k, v, wd, ad, gd = jnp.split(xs, RWKV_SPLITS, axis=-1)
    w_log = -jax.nn.softplus(-(w0 + jnp.tanh(wd) @ w1u)) - 0.5
    decay = jnp.exp(-jnp.exp(w_log))
    a_lr = jax.nn.sigmoid(a0 + ad @ a1u)
    g = jax.nn.sigmoid(gd) @ g1u
    def heads(t):
        return t.reshape(B, T, RWKV_HEADS, RWKV_HEAD)
    kk = heads(k * k_k)
    kk = kk / jnp.maximum(jnp.sqrt(jnp.sum(jnp.square(kk), -1, keepdims=True)), 1e-12)
    k = k * (1.0 + (a_lr - 1.0) * k_a)
    rh, kh, vh, ah = heads(r), heads(k), heads(v), heads(a_lr)
    y, s_rw_new = rwkv7_recurrence(rh, heads(decay), kh, vh, -kk, kk * ah, s_rwkv.astype(f32))
    mu = jnp.mean(y, -1, keepdims=True)
    var = jnp.mean(jnp.square(y - mu), -1, keepdims=True)
    yn = ((y - mu) * lax.rsqrt(var + GN_EPS)).reshape(B, T, RWKV_WIDTH) * ln_x_w + ln_x_b
    bonus = jnp.sum(rh * kh * r_k, -1, keepdims=True) * vh
    o_rw = (yn + bonus.reshape(B, T, RWKV_WIDTH)) * g
    q_h, f_h, i_h, g_h = jnp.split(p_hg, HGRN_SPLITS, axis=-1)
    fg = lb + (1.0 - lb) * jax.nn.sigmoid(f_h)
    def hh(t):
        return t.reshape(B, T, HGRN_HEADS, HGRN_HEAD)
    o_h, s_hg_new = hgrn2_chunked(hh(jax.nn.silu(q_h)), hh(jnp.log(fg)), hh(1.0 - fg), hh(i_h),
                                  s_hgrn.astype(f32))
    o_h = o_h * lax.rsqrt(jnp.mean(jnp.square(o_h), -1, keepdims=True) + RMS_EPS)
    o_hg = o_h.reshape(B, T, HGRN_WIDTH) * hg_norm_w * jax.nn.silu(g_h)
    mix = jnp.einsum('btm,md->btd', jnp.concatenate([o_rw, o_hg], axis=-1), w_out)
    return mix, s_rw_new, s_hg_new, p_rw[:, -1]


def run_group(x, s_rwkv, s_hgrn, s_shift, lb_logits, weights):
    (w_in, shift_mu, w0, w1u, a0, a1u, g1u, k_k, k_a, r_k, ln_x_w, ln_x_b, hg_norm_w, w_out,
     ln1_g, ln1_b, w_up, w_down, ln2_g, ln2_b) = weights
    lb_all = jnp.cumsum(jax.nn.softmax(lb_logits.astype(jnp.float32), axis=0), axis=0)
    h = x
    rws, hgs, shs = [], [], []
    for l in range(DEPTH):
        mix, srw, shg, ssh = mixer(h, s_rwkv[l], s_hgrn[l], s_shift[l], lb_all[l], w_in[l], shift_mu[l],
                                   w0[l], w1u[l], a0[l], a1u[l], g1u[l], k_k[l], k_a[l], r_k[l],
                                   ln_x_w[l], ln_x_b[l], hg_norm_w[l], w_out[l])
        h1 = layer_norm(ALPHA * h.astype(jnp.float32) + mix, ln1_g[l], ln1_b[l])
        up = jnp.square(jax.nn.relu(jnp.einsum('btd,df->btf', h1, w_up[l])))
        ff = jnp.einsum('btf,fd->btd', up, w_down[l])
        h = layer_norm(ALPHA * h1 + ff, ln2_g[l], ln2_b[l]).astype(x.dtype)
        rws.append(srw)
        hgs.append(shg)
        shs.append(ssh)
    return (h, jnp.stack(rws).astype(s_rwkv.dtype), jnp.stack(hgs).astype(s_hgrn.dtype),
            jnp.stack(shs).astype(s_shift.dtype))


def setup_inputs(seed: int = 0) -> dict:
    key = jax.random.key(seed)
    ks = jax.random.split(key, 32)
    n = jax.random.normal
    L = DEPTH
    w_in = n(ks[0], (L, D_MODEL, PROJ)) * D_MODEL ** -0.5
    v_lo = 2 * RWKV_WIDTH
    i_lo = RWKV_PROJ + 2 * HGRN_WIDTH
    w_in = w_in.at[:, :, v_lo:v_lo + RWKV_WIDTH].multiply(BETA).at[:, :, i_lo:i_lo + HGRN_WIDTH].multiply(BETA)
    return {
        'x_prompt': n(ks[1], (BATCH, SEQ, D_MODEL)),
        'x_sample': n(ks[2], (DEC_BATCH, DEC_SEQ, D_MODEL)),
        'state_rwkv': 0.5 * n(ks[3], (L, DEC_BATCH, RWKV_HEADS, RWKV_HEAD, RWKV_HEAD)),
        'state_hgrn': 0.5 * n(ks[4], (L, DEC_BATCH, HGRN_HEADS, HGRN_HEAD, HGRN_HEAD)),
        'state_shift': n(ks[5], (L, DEC_BATCH, RWKV_PROJ)),
        'w_in': w_in,
        'shift_mu': jax.random.uniform(ks[6], (L, RWKV_PROJ)),
        'w0': jax.random.uniform(ks[7], (L, RWKV_WIDTH), minval=-6.0, maxval=-0.5),
        'w1u': 0.1 * n(ks[8], (L, DECAY_RANK, RWKV_WIDTH)),
        'a0': 0.1 * n(ks[9], (L, RWKV_WIDTH)),
        'a1u': 0.1 * n(ks[10], (L, AICL_RANK, RWKV_WIDTH)),
        'g1u': n(ks[11], (L, GATE_RANK, RWKV_WIDTH)) * GATE_RANK ** -0.5,
        'k_k': 0.85 + 0.02 * n(ks[12], (L, RWKV_WIDTH)),
        'k_a': 1.0 + 0.02 * n(ks[13], (L, RWKV_WIDTH)),
        'r_k': 0.1 * n(ks[14], (L, RWKV_HEADS, RWKV_HEAD)),
        'ln_x_w': 1.0 + 0.02 * n(ks[15], (L, RWKV_WIDTH)),
        'ln_x_b': 0.02 * n(ks[16], (L, RWKV_WIDTH)),
        'lb_logits': 0.1 * n(ks[17], (L + 1, HGRN_WIDTH)),
        'hg_norm_w': 1.0 + 0.02 * n(ks[18], (L, HGRN_WIDTH)),
        'w_out': n(ks[19], (L, MIX_WIDTH, D_MODEL)) * MIX_WIDTH ** -0.5 * BETA,
        'ln1_g': 1.0 + 0.02 * n(ks[20], (L, D_MODEL)),
        'ln1_b': 0.02 * n(ks[21], (L, D_MODEL)),
        'w_up': n(ks[22], (L, D_MODEL, D_FF)) * D_MODEL ** -0.5,
        'w_down': n(ks[23], (L, D_FF, D_MODEL)) * D_FF ** -0.5 * BETA,
        'ln2_g': 1.0 + 0.02 * n(ks[24], (L, D_MODEL)),
        'ln2_b': 0.02 * n(ks[25], (L, D_MODEL)),
    }


def reference(x_prompt, x_sample, state_rwkv, state_hgrn, state_shift, w_in, shift_mu, w0, w1u, a0, a1u,
              g1u, k_k, k_a, r_k, ln_x_w, ln_x_b, lb_logits, hg_norm_w, w_out, ln1_g, ln1_b, w_up, w_down,
              ln2_g, ln2_b):
    weights = (w_in, shift_mu, w0, w1u, a0, a1u, g1u, k_k, k_a, r_k, ln_x_w, ln_x_b, hg_norm_w, w_out,
               ln1_g, ln1_b, w_up, w_down, ln2_g, ln2_b)
    bp = x_prompt.shape[0]
    z_rw = jnp.zeros((DEPTH, bp, RWKV_HEADS, RWKV_HEAD, RWKV_HEAD), jnp.float32)
    z_hg = jnp.zeros((DEPTH, bp, HGRN_HEADS, HGRN_HEAD, HGRN_HEAD), jnp.float32)
    z_sh = jnp.zeros((DEPTH, bp, RWKV_PROJ), jnp.float32)
    y_prompt, rw_p, hg_p, sh_p = run_group(x_prompt, z_rw, z_hg, z_sh, lb_logits, weights)
    y_sample, rw_s, hg_s, sh_s = run_group(x_sample, state_rwkv, state_hgrn, state_shift, lb_logits, weights)
    return (y_prompt, y_sample, rw_p, rw_s, hg_p, hg_s, sh_p, sh_s)
```

```python
import numpy as np
import concourse.bass as bass
import concourse.mybir as mybir
from concourse.bass_utils import run_bass_kernel_spmd

F32 = mybir.dt.float32
BF16 = mybir.dt.bfloat16
AF = mybir.ActivationFunctionType
ALU = mybir.AluOpType
AX = mybir.AxisListType

DEBUG_LINES = False
LINE_OF = {}
PE, ACT, DVE, POOL, SP = "pe", "act", "dve", "pool", "sp"
ENGINES = (PE, ACT, DVE, POOL, SP)
DMA_RING = {SP: 12, ACT: 4, POOL: 8}


class Res:
    __slots__ = ("name", "last_w", "readers")

    def __init__(self, name):
        self.name = name
        self.last_w = None
        self.readers = []


class Op:
    __slots__ = ("eng", "fn", "deps", "is_dma", "signal", "idx", "dma_no", "extra_wait", "rg")

    def __init__(self, eng, fn, is_dma):
        self.eng = eng
        self.fn = fn
        self.deps = set()
        self.is_dma = is_dma
        self.signal = False
        self.idx = None
        self.dma_no = None
        self.extra_wait = None
        self.rg = None


def _pe_inorder_ok(d, o):
    return d.rg is None or o.rg is None or d.rg == o.rg


class Prog:
    def __init__(self):
        self.ops = {e: [] for e in ENGINES}
        self.order = []
        self.n_dma = {e: 0 for e in ENGINES}
        self.dma_ops = {e: [] for e in ENGINES}
        self.out_dmas = []

    def op(self, eng, fn, reads=(), writes=(), dma=False, out=False):
        o = Op(eng, fn, dma)
        for r in reads:
            if r.last_w is not None:
                o.deps.add(r.last_w)
        for w in writes:
            if w.last_w is not None:
                o.deps.add(w.last_w)
            for rd in w.readers:
                o.deps.add(rd)
        for r in reads:
            r.readers.append(o)
        for w in writes:
            w.last_w = o
            w.readers = []
        if getattr(self, "barrier_left", None) and eng in self.barrier_left:
            self.barrier_left.discard(eng)
            o.deps.update(self.pending_barrier)
        o.deps.discard(o)
        if dma:
            o.dma_no = self.n_dma[eng]
            self.n_dma[eng] += 1
            self.dma_ops[eng].append(o)
            o.signal = True
            if out:
                self.out_dmas.append(o)
        self.ops[eng].append(o)
        self.order.append(o)
        return o

    def barrier(self):
        pend = []
        for e in ENGINES:
            comp = [o for o in self.ops[e] if not o.is_dma]
            if comp:
                pend.append(comp[-1])
            pend.extend(self.dma_ops[e][-DMA_RING.get(e, 0):] if e in DMA_RING else [])
        self.pending_barrier = pend
        self.barrier_left = set(ENGINES)

    def prepare(self, sems, rings):
        for o in self.order:
            for d in o.deps:
                if d.is_dma:
                    continue
                if d.eng == o.eng and d.eng == PE and not o.is_dma and _pe_inorder_ok(d, o):
                    continue
                d.signal = True
        sig = {}
        for e in ENGINES:
            c = 0
            for o in self.ops[e]:
                if o.is_dma:
                    R = len(rings[e])
                    sig[o] = (rings[e][o.dma_no % R], 16 * (o.dma_no // R + 1))
                elif o.signal:
                    c += 1
                    sig[o] = (sems[e], c)
        self.sig = sig
        self.rings = rings
        self.stats = {e: len(self.ops[e]) for e in ENGINES}

    def emit_engine(self, e, eng):
        sig, rings = self.sig, self.rings
        waited = {}

        def wait(sem, val):
            k = id(sem)
            if waited.get(k, 0) >= val:
                return
            waited[k] = val
            eng.wait_ge(sem, val)

        for o in self.ops[e]:
            for d in o.deps:
                if d not in sig:
                    continue
                if d.eng == e and not d.is_dma and not o.is_dma and e == PE and _pe_inorder_ok(d, o):
                    continue
                s, v = sig[d]
                wait(s, v)
            if o.is_dma:
                R = len(rings[e])
                if o.dma_no >= R:
                    wait(rings[e][o.dma_no % R], 16 * (o.dma_no // R))
            ins = o.fn(eng)
            if DEBUG_LINES:
                LINE_OF[str(getattr(getattr(ins, "ins", ins), "name", ins))] = o.extra_wait
            if o in sig:
                s, v = sig[o]
                ins.then_inc(s, 16 if o.is_dma else 1)
        if e == SP:
            for q in ENGINES:
                if q not in rings:
                    continue
                for o in self.dma_ops[q][-len(rings[q]):]:
                    s, v = sig[o]
                    wait(s, v)


class V:
    __slots__ = ("ap", "key")

    def __init__(self, ap, key):
        self.ap = ap
        self.key = key

    def __getitem__(self, idx):
        return V(self.ap[idx], self.key)

    def k(self, sub):
        return V(self.ap, (self.key, sub))

    @property
    def shape(self):
        return self.ap.shape

    def rearrange(self, *a, **kw):
        return V(self.ap.rearrange(*a, **kw), self.key)

    def unsqueeze(self, ax):
        return V(self.ap.unsqueeze(ax), self.key)

    def to_broadcast(self, shape):
        return V(self.ap.to_broadcast(list(shape)), self.key)

    def bitcast(self, dt):
        return V(self.ap.bitcast(dt), self.key)


def _ap(x):
    return x.ap if isinstance(x, V) else x


def _isnum(x):
    return isinstance(x, (int, float))


class Arena:
    def __init__(self, nc, es, nbytes):
        self.n2 = nbytes // 2
        self.t = es.enter_context(nc.sbuf_tensor("arena", [128, self.n2], BF16))
        self.off = 0
        self.cnt = 0
        self.peak = 0

    def alloc(self, shape, dt, name=None):
        shape = list(shape)
        esz = 4 if dt == F32 else 2
        n = int(np.prod(shape[1:]))
        nbytes = (n * esz + 3) // 4 * 4
        o = self.off
        assert o + nbytes <= self.n2 * 2, f"arena overflow allocating {name} {shape}: {o}+{nbytes} > {self.n2 * 2}"
        self.off += nbytes
        self.peak = max(self.peak, self.off)
        ap = self.t[0:shape[0], o // 2:o // 2 + nbytes // 2]
        if esz == 4:
            ap = ap.bitcast(F32)
        ap = ap[:, 0:n]
        if len(shape) == 3:
            ap = ap.rearrange("p (a b) -> p a b", a=shape[1])
        elif len(shape) == 4:
            ap = ap.rearrange("p (a b c) -> p a b c", a=shape[1], b=shape[2])
        self.cnt += 1
        return V(ap, name or f"t{self.cnt}")


class Ctx:
    def __init__(self, nc, P, arena):
        self.nc, self.P, self.arena = nc, P, arena
        self.res = {}

    def sb(self, shape, dt, name=None):
        return self.arena.alloc(shape, dt, name)

    def R(self, x):
        k = x.key if isinstance(x, V) else (x.name, None)
        r = self.res.get(k)
        if r is None:
            r = self.res[k] = Res(k)
        return r

    def rec(self, eng, fn, outs, ins, dma=False, out=False):
        reads = [self.R(i) for i in ins if i is not None and not _isnum(i)]
        writes = [self.R(o) for o in outs]
        writes += [r for r in reads if isinstance(r.name, str) and r.name.startswith("ps")]
        o_ = self.P.op(eng, fn, reads=reads, writes=writes, dma=dma, out=out)
        if DEBUG_LINES:
            import sys as _s
            f = _s._getframe(1)
            while f.f_code.co_name not in ("mixer_tile", "build", "layernorm") and f.f_back is not None:
                f = f.f_back
            o_.extra_wait = f.f_lineno
        return o_

    def mm(self, out, lhsT, rhs, start=True, stop=True):
        o, l, r = _ap(out), _ap(lhsT), _ap(rhs)
        op = self.rec(PE, lambda e: e.matmul(o, lhsT=l, rhs=r, start=start, stop=stop,
                                             skip_group_check=True), [out], [lhsT, rhs])
        kr = l.shape[0]
        if kr < 128:
            op.rg = (kr, l.base_partition())
        return op

    def tr(self, out, in_, ident):
        o, i, d = _ap(out), _ap(in_), _ap(ident)
        return self.rec(PE, lambda e: e.transpose(o, i, d), [out], [in_, ident])

    def act(self, out, in_, func, bias=None, scale=1.0):
        o, i = _ap(out), _ap(in_)
        kw = {}
        if bias is not None:
            kw["bias"] = _ap(bias)
        s = _ap(scale)
        return self.rec(ACT, lambda e: e.activation(out=o, in_=i, func=func, scale=s, **kw), [out],
                        [in_, bias, scale])

    def tt(self, eng, out, a, b, op):
        o, x, y = _ap(out), _ap(a), _ap(b)
        return self.rec(eng, lambda e: e.tensor_tensor(out=o, in0=x, in1=y, op=op), [out], [a, b])

    def ts(self, eng, out, a, s1, op0, s2=None, op1=None):
        o, x, v1, v2 = _ap(out), _ap(a), _ap(s1), _ap(s2)
        if op1 is None:
            f = lambda e: e.tensor_scalar(out=o, in0=x, scalar1=v1, scalar2=None, op0=op0)
        else:
            f = lambda e: e.tensor_scalar(out=o, in0=x, scalar1=v1, scalar2=v2, op0=op0, op1=op1)
        return self.rec(eng, f, [out], [a, s1, s2])

    def stt(self, eng, out, in0, scalar, in1, op0, op1):
        o, x, y, s = _ap(out), _ap(in0), _ap(in1), _ap(scalar)
        return self.rec(eng, lambda e: e.scalar_tensor_tensor(out=o, in0=x, scalar=s, in1=y, op0=op0, op1=op1),
                        [out], [in0, in1, scalar])

    def cp(self, eng, out, in_):
        o, i = _ap(out), _ap(in_)
        if eng == ACT:
            return self.rec(ACT, lambda e: e.activation(out=o, in_=i, func=AF.Copy), [out], [in_])
        return self.rec(eng, lambda e: e.tensor_copy(out=o, in_=i), [out], [in_])

    def recip(self, out, in_):
        o, i = _ap(out), _ap(in_)
        return self.rec(DVE, lambda e: e.reciprocal(out=o, in_=i), [out], [in_])

    def scan(self, out, d0, d1):
        o, a, b = _ap(out), _ap(d0), _ap(d1)
        return self.rec(DVE, lambda e: e.tensor_tensor_scan(out=o, data0=a, data1=b, initial=0.0,
                                                            op0=ALU.mult, op1=ALU.add), [out], [d0, d1])

    def red(self, eng, out, in_, op=ALU.add):
        o, i = _ap(out), _ap(in_)
        return self.rec(eng, lambda e: e.tensor_reduce(out=o, in_=i, axis=AX.X, op=op), [out], [in_])

    def memset(self, eng, out, val):
        o = _ap(out)
        return self.rec(eng, lambda e: e.memset(o, val), [out], [])

    def dma(self, eng, out, in_, out_final=False, extra_out=()):
        o, i = _ap(out), _ap(in_)
        return self.rec(eng, lambda e: e.dma_start(out=o, in_=i), [out, *extra_out], [in_], dma=True, out=out_final)

    def bn_stats(self, out, in_):
        o, i = _ap(out), _ap(in_)
        return self.rec(DVE, lambda e: e.bn_stats(out=o, in_=i), [out], [in_])

    def bn_aggr(self, out, in_):
        o, i = _ap(out), _ap(in_)
        return self.rec(DVE, lambda e: e.bn_aggr(out=o, in_=i), [out], [in_])


D = 1024
PJ = 3840
RWP = 1792
NTP = 16
NB = 16
TS = 8
DFF = 4096
ALPHA = 2.0 ** 0.25
CDEC = -float(np.exp(-0.5))
LN_EPS = 1e-5
GN_EPS = 64e-5
RMS_EPS = 1e-6
ARENA_BYTES = 211000


def make_consts():
    s = np.arange(128)[:, None]
    t = np.arange(128)[None, :]
    cols = {}
    cols["ident"] = (s == t)
    cols["mS_p"] = (s < t)
    cols["mI_p"] = (s <= t)
    cols["mST_p"] = (t < s)
    same = (s // TS) == (t // TS)
    cols["mS_s"] = (s < t) & same
    cols["mI_s"] = (s <= t) & same
    cols["mST_s"] = (t < s) & same
    cols["reset_p"] = np.broadcast_to(t != 0, (128, 128))
    cols["reset_s"] = np.broadcast_to((t % TS) != 0, (128, 128))
    cols["bdones"] = (s // 64) == (t // 64)
    cols["hsel"] = (s // 64) == np.arange(2)[None, :]
    cols["cm"] = (s // TS) == np.arange(NB)[None, :]
    cols["i64s"] = (s % 64) == np.arange(64)[None, :]
    off = {}
    parts = []
    o = 0
    for k, v in cols.items():
        v = np.asarray(v, np.float32)
        off[k] = (o, o + v.shape[1])
        o += v.shape[1]
        parts.append(v)
    return np.ascontiguousarray(np.concatenate(parts, axis=1)), off


CONSTS, COFF = make_consts()
NCONST = CONSTS.shape[1]


class _Stop(Exception):
    pass


def build(NT=NTP, SAMPLE=True, DBG=False, STAGE=99):
    from contextlib import ExitStack
    nc = bass.Bass("TRN2", target_bir_lowering=False)

    def din(name, shape):
        return nc.dram_tensor(name, list(shape), F32, kind="ExternalInput").ap()

    def dout(name, shape):
        return nc.dram_tensor(name, list(shape), F32, kind="ExternalOutput").ap()

    xp = din("xp", [NTP * 128, D]); xsm = din("xs", [128, D])
    srw = din("srw", [NB, 8, 64, 64]); shg = din("shg", [NB, 4, 128, 128]); ssh = din("ssh", [NB, RWP])
    w_in = din("w_in", [D, PJ]); shift_mu = din("shift_mu", [RWP]); w0 = din("w0", [512])
    w1u = din("w1u", [64, 512]); a0 = din("a0", [512]); a1u = din("a1u", [64, 512]); g1u = din("g1u", [128, 512])
    k_k = din("k_k", [512]); k_a = din("k_a", [512]); r_k = din("r_k", [512])
    ln_x_w = din("ln_x_w", [512]); ln_x_b = din("ln_x_b", [512]); lb_logits = din("lb_logits", [2, 512])
    hg_norm_w = din("hg_norm_w", [512]); w_out = din("w_out", [D, D]); ln1_g = din("ln1_g", [D]); ln1_b = din("ln1_b", [D])
    w_up = din("w_up", [D, DFF]); w_down = din("w_down", [DFF, D]); ln2_g = din("ln2_g", [D]); ln2_b = din("ln2_b", [D])
    cst_d = din("consts", [128, NCONST])
    yp = dout("yp", [NTP * 128, D]); ys = dout("ys", [128, D])
    rwp = dout("rwp", [8, 64, 64]); rws = dout("rws", [NB, 8, 64, 64])
    hgp = dout("hgp", [4, 128, 128]); hgs = dout("hgs", [NB, 4, 128, 128])
    shp = dout("shp", [RWP]); shs = dout("shs", [NB, RWP])
    h1scr = nc.dram_tensor("h1scr", [(NTP + 1) * 128, D], F32).ap()
    if DBG:
        dbg_d = dout("dbg", [128, 4096])

    def row(v):
        return v.rearrange("(o n) -> o n", o=1)

    P = Prog()
    with ExitStack() as es:
        arena = Arena(nc, es, ARENA_BYTES)
        C = Ctx(nc, P, arena)
        sb = C.sb
        ps = [V(es.enter_context(nc.psum_tensor(f"ps{i}", [128, 512], F32))[:], f"ps{i}") for i in range(8)]

        def psv(i, a):
            return ps[i].rearrange("p (a t) -> p a t", a=a)

        def psb(i):
            return ps[i].bitcast(BF16)

        def bc3(ap2, n):
            return ap2.unsqueeze(2).to_broadcast([ap2.shape[0], ap2.shape[1], n])

        def v3(t):
            return t.rearrange("p (a t) -> p a t", a=4)

        ident_bf = sb([128, 128], BF16, "ident_bf")
        lng = sb([128, D], F32, "lng"); lnb = sb([128, D], F32, "lnb")
        C.dma(SP, lng, row(ln1_g).partition_broadcast(128))
        C.dma(SP, lnb, row(ln1_b).partition_broadcast(128))
        bnst = sb([128, 12], F32, "bnst"); mv = sb([128, 2], F32, "mv"); rstd1 = sb([128, 1], F32, "rstd1")
        if DBG:
            dbg = sb([128, 4096], F32, "dbg")
            C.memset(POOL, dbg, 0.0)
            dbg_pos = [0]
            dbg_map = {}

            def dump(name, v, n):
                a = dbg_pos[0]
                shape = list(v.shape)
                dst = dbg[0:shape[0], a:a + n]
                if len(shape) == 3:
                    dst = dst.rearrange("p (a b) -> p a b", a=shape[1])
                C.cp(POOL, dst, v)
                dbg_pos[0] += n
                dbg_map[name] = (a, n, shape)
            build.dbg_map = dbg_map
        else:
            def dump(name, v, n):
                return None
        phase_mark = arena.off

        def layernorm(src, dst, eps):
            for half in range(2):
                C.bn_stats(bnst[:, half * 6:(half + 1) * 6], src[:, half * 512:(half + 1) * 512])
            C.bn_aggr(mv, bnst)
            C.act(rstd1, mv[:, 1:2], AF.Ln, bias=eps)
            C.act(rstd1, rstd1, AF.Exp, scale=-0.5)
            C.ts(DVE, dst, src, mv[:, 0:1], ALU.subtract, rstd1[:, 0:1], ALU.mult)
            C.tt(POOL, dst, dst, lng, ALU.mult)
            C.tt(POOL, dst, dst, lnb, ALU.add)

        cst = sb([128, NCONST], F32, "cst")
        C.dma(SP, cst, cst_d)

        def cc(name):
            a, b = COFF[name]
            return cst[:, a:b]

        C.cp(POOL, ident_bf, cc("ident"))
        ones_row = sb([1, 128], F32, "ones_row")
        C.memset(POOL, ones_row, 1.0)
        mu14 = sb([128, 14], F32, "mu14")
        kk4 = sb([128, 4], F32, "kk4"); ka4 = sb([128, 4], F32, "ka4"); rk4 = sb([128, 4], F32, "rk4")
        lbl = sb([128, 2, 4], F32, "lbl")
        with nc.allow_non_contiguous_dma(reason="tiny per-channel parameter vectors"):
            C.dma(SP, mu14, shift_mu.rearrange("(c p) -> p c", p=128))
            C.dma(SP, kk4, k_k.rearrange("(c p) -> p c", p=128))
            C.dma(SP, ka4, k_a.rearrange("(c p) -> p c", p=128))
            C.dma(SP, rk4, r_k.rearrange("(c p) -> p c", p=128))
            C.dma(SP, lbl, lb_logits.rearrange("l (c p) -> p l c", p=128))
        w0row = sb([1, 512], F32, "w0row"); a0row = sb([1, 512], F32, "a0row")
        C.dma(SP, w0row, row(w0))
        C.dma(SP, a0row, row(a0))
        WA = sb([128, 512], F32, "WA")
        C.dma(SP, WA[0:64, :].k("lo"), w1u)
        C.dma(SP, WA[64:128, :].k("hi"), a1u)
        g1u_bf = sb([128, 512], BF16, "g1u_bf")
        C.dma(POOL, g1u_bf, g1u)
        lnxw = sb([128, 512], F32, "lnxw"); lnxb = sb([128, 512], F32, "lnxb"); hgw = sb([128, 512], F32, "hgw")
        C.dma(SP, lnxw, row(ln_x_w).partition_broadcast(128))
        C.dma(SP, lnxb, row(ln_x_b).partition_broadcast(128))
        C.dma(SP, hgw, row(hg_norm_w).partition_broadcast(128))
        lb4 = sb([128, 4], F32, "lb4"); oml4 = sb([128, 4], F32, "oml4"); etmp = sb([128, 4], F32, "etmp")
        C.tt(DVE, etmp, lbl[:, 1, :], lbl[:, 0, :], ALU.subtract)
        C.act(etmp, etmp, AF.Exp)
        C.ts(DVE, lb4, etmp, 1.0, ALU.add)
        C.recip(lb4, lb4)
        C.tt(DVE, oml4, etmp, lb4, ALU.mult)
        RKsel = sb([128, 4, 2], F32, "RKsel")
        C.tt(DVE, RKsel, bc3(rk4, 2), cc("hsel").unsqueeze(1).to_broadcast([128, 4, 2]), ALU.mult)

        ov_lo = arena.off
        w_in_bf = sb([128, 8, PJ], BF16, "w_in_bf")
        ov_hi = arena.off
        for kc in range(8):
            C.dma(POOL, w_in_bf[:, kc, :].k(kc), w_in[kc * 128:(kc + 1) * 128, :])
        w_out_bf = sb([128, 8, D], BF16, "w_out_bf")
        for kc in range(8):
            C.dma(POOL, w_out_bf[:, kc, :].k(kc), w_out[kc * 128:(kc + 1) * 128, :])

        ST = sb([128, 4, 64], F32, "ST"); ST_bf = sb([128, 4, 64], BF16, "ST_bf")
        SH = sb([128, 4, 128], F32, "SH"); SH_bf = sb([128, 4, 128], BF16, "SH_bf")
        plast = sb([128, 14], F32, "plast")
        for t_ in (ST, ST_bf, SH, SH_bf, plast):
            C.memset(POOL, t_, 0.0)

        x_t = [sb([128, D], F32, "x_t0")]
        x_bf = sb([128, D], BF16, "x_bf")
        xT = sb([128, 8, 128], BF16, "xT")
        pr_ = sb([128, 14, 129], F32, "pr")
        xs = sb([128, 14, 128], F32, "xsft")
        T = [sb([128, 512], F32, f"T{i}") for i in range(10)]
        SHtmp = v3(T[9])
        STtmp = T[1][:, 0:256].rearrange("p (a v) -> p a v", a=4)
        z12 = sb([128, 128], F32, "z12"); sg_bf = sb([128, 128], BF16, "sg_bf"); sgt = sb([128, 128], F32, "sgt")
        g_tm = sb([128, 512], BF16, "g_tm")
        AR = sb([128, 4, 2, 128], BF16, "AR")
        bT = sb([128, 4, 128], BF16, "bT"); kT = sb([128, 4, 128], BF16, "kT")
        bhT = sb([128, 4, 128], BF16, "bhT"); khT = sb([128, 4, 128], BF16, "khT"); vT = sb([128, 4, 128], BF16, "vT")
        A_tm = sb([128, 512], BF16, "A_tm"); Bh_tm = sb([128, 512], BF16, "Bh_tm")
        Kh_tm = sb([128, 512], BF16, "Kh_tm"); V_tm = sb([128, 512], BF16, "V_tm")
        bon8 = sb([128, 8], F32, "bon8")
        PT = [sb([128, 8, 256], BF16, f"PT{i}") for i in range(2)]
        RR = [sb([128, 8, 128], BF16, f"RR{i}") for i in range(2)]
        NrbT = sb([128, 8, 128], BF16, "NrbT"); AkT = sb([128, 8, 128], BF16, "AkT"); NrkT = sb([128, 8, 128], BF16, "NrkT")
        TTf = sb([128, 8, 128], BF16, "TTf")
        W1T = sb([128, 4, 128], BF16, "W1T")
        Z_tm = sb([128, 512], BF16, "Z_tm"); U_tm = sb([128, 512], BF16, "U_tm")
        st16 = sb([128, 16], F32, "st16"); m8 = sb([128, 8], F32, "m8"); r8 = sb([128, 8], F32, "r8")
        o_all = sb([128, D], BF16, "o_all"); oT = sb([128, 8, 128], BF16, "oT")
        h1pre = sb([128, D], F32, "h1pre")
        qTb = sb([128, 4, 128], BF16, "qTb"); kTb = sb([128, 4, 128], BF16, "kTb"); khTb = sb([128, 4, 128], BF16, "khTb")
        khat_tm = sb([128, 512], BF16, "khat_tm"); i_tm = sb([128, 512], BF16, "i_tm")
        gs_t = sb([128, 512], BF16, "gs_t"); attT = sb([128, 4, 128], BF16, "attT")
        s4 = sb([128, 4], F32, "s4"); rr4 = sb([128, 4], F32, "rr4")
        gC = sb([128, 4, NB], F32, "gC"); decH = sb([128, 4, NB], F32, "decH")
        identb4 = ident_bf.unsqueeze(1).to_broadcast([128, 4, 128])
        save_off = arena.off
        arena.off = ov_lo
        scrA = [sb([128, 2048], F32, f"scrA{i}") for i in range(2)]
        S0T32 = sb([128, NB, 4, 64], F32, "S0T32")
        S0Tb = sb([128, NB, 4, 64], BF16, "S0Tb")
        sshT = sb([128, 14, NB], F32, "sshT"); lastp = sb([128, 14, NB], F32, "lastp")
        EW = [sb([128, 4, 128], BF16, f"EW{i}") for i in range(2)]
        ER = [sb([128, 4, 128], BF16, f"ER{i}") for i in range(2)]
        EQ = [sb([128, 4, 128], BF16, f"EQ{i}") for i in range(2)]
        Ub = sb([128, 512], BF16, "Ub"); Vb = sb([128, 512], BF16, "Vb"); khb = sb([128, 512], BF16, "khb")
        Dg = sb([128, 4, 64], F32, "Dg")
        S0h = [sb([128, 4, 128], F32, f"S0h{i}") for i in range(2)]
        S0hb = sb([128, 4, 128], BF16, "S0hb")
        Sn = sb([128, 512], F32, "Sn")
        fence_t = sb([128, 2], F32, "fence_t")
        assert arena.off <= ov_hi, (arena.off, ov_hi)
        ov_bufs = scrA + [S0T32, S0Tb, sshT, lastp] + EW + ER + EQ + [Ub, Vb, khb, Dg] + S0h + [S0hb, Sn, fence_t]
        arena.off = save_off
        ssh_tm = scrA[0][0:NB, 0:RWP]
        hh_order = (0, 2, 4, 6, 1, 3, 5, 7)
        heads_of = [(0, 1, 2, 3), (4, 5, 6, 7)]

        def ck(n):
            if STAGE == n:
                raise _Stop()

        def mixer_tile(ti, x_src, h1_dst, sample):
            nch = NB if sample else 1
            Cn = TS if sample else 128
            L = 3 if sample else 7
            sfx = "_s" if sample else "_p"
            mS, mI, mST, reset = cc("mS" + sfx), cc("mI" + sfx), cc("mST" + sfx), cc("reset" + sfx)
            mSb = mS.unsqueeze(1).to_broadcast([128, 4, 128])
            mIb = mI.unsqueeze(1).to_broadcast([128, 4, 128])
            mSTb = mST.unsqueeze(1).to_broadcast([128, 4, 128])
            xt = x_t[0]
            C.dma(SP, xt, x_src)
            C.dma(POOL, x_bf, x_src)
            ck(1)
            for kc in range(8):
                C.tr(psb(7)[:, kc * 128:(kc + 1) * 128], x_bf[:, kc * 128:(kc + 1) * 128], ident_bf)
            C.cp(ACT, xT, psb(7).rearrange("p (a t) -> p a t", a=8))
            ck(2)
            for c in range(22):
                if c < 14:
                    bank, slot = c // 4, c % 4
                else:
                    bank, slot = 4 + (c - 14) // 4, (c - 14) % 4
                for kc in range(8):
                    C.mm(ps[bank][:, slot * 128:(slot + 1) * 128], w_in_bf[:, kc, c * 128:(c + 1) * 128].k(kc),
                         xT[:, kc, :], start=(kc == 0), stop=(kc == 7))
            for j, bank in ((0, 6), (1, 7)):
                c0 = RWP + 1024 + j * 512
                for kc in range(8):
                    C.mm(ps[bank], xT[:, kc, :], w_in_bf[:, kc, c0:c0 + 512].k(kc), start=(kc == 0), stop=(kc == 7))
            ck(3)
            for bnk in range(3):
                C.cp(ACT, pr_[:, bnk * 4:(bnk + 1) * 4, 1:129], psv(bnk, 4))
            C.cp(ACT, pr_[:, 12:14, 1:129], psv(3, 4)[:, 0:2, :])
            prev, cur = pr_[:, :, 0:128], pr_[:, :, 1:129]
            if not sample:
                C.cp(POOL, pr_[:, :, 0:1], plast.unsqueeze(2))
                C.cp(POOL, plast.unsqueeze(2), pr_[:, :, 128:129])
            else:
                C.memset(POOL, pr_[:, :, 0:1], 0.0)
                C.rec(POOL, lambda e: e.memset(_ap(fence_t), 0.0),
                      [w_in_bf[:, kc, :].k(kc) for kc in range(8)] + ov_bufs
                      + [S0T32[:, b, :, :].k(b) for b in range(NB)] + [S0Tb[:, b, :, :].k(b) for b in range(NB)], [])
                for t_ in EW + ER + EQ:
                    C.memset(POOL, t_, 0.0)
                C.dma(SP, ssh_tm, ssh)
                for c in range(14):
                    C.tr(ps[0][:, c * NB:(c + 1) * NB], ssh_tm[:, c * 128:(c + 1) * 128], cc("ident")[0:NB, 0:NB])
                C.cp(DVE, sshT, ps[0][:, 0:14 * NB].rearrange("p (c b) -> p c b", c=14))
            C.tt(POOL, xs, prev, cur, ALU.subtract)
            C.tt(POOL, xs, xs, bc3(mu14, 128), ALU.mult)
            C.tt(POOL, xs, xs, cur, ALU.add)
            if sample:
                cur4 = cur.rearrange("p c (b t) -> p c b t", t=TS)
                xs4 = xs.rearrange("p c (b t) -> p c b t", t=TS)
                cur0, xs0 = cur4[:, :, :, 0], xs4[:, :, :, 0]
                C.tt(POOL, xs0, sshT, cur0, ALU.subtract)
                C.tt(POOL, xs0, xs0, bc3(mu14, NB), ALU.mult)
                C.tt(POOL, xs0, xs0, cur0, ALU.add)
                C.cp(POOL, lastp, cur4[:, :, :, TS - 1])
                for c in range(14):
                    C.tr(ps[c // 4][0:NB, (c % 4) * 128:(c % 4 + 1) * 128], lastp[:, c, :], cc("ident"))
                for bnk in range(4):
                    w_ = 512 if bnk < 3 else 256
                    C.cp(ACT, ssh_tm[:, bnk * 512:bnk * 512 + w_], ps[bnk][0:NB, 0:w_])
                C.dma(SP, shs, ssh_tm, out_final=True)
            r_ = xs[:, 0:4, :]; k_ = xs[:, 4:8, :]; v_ = xs[:, 8:12, :]

            ck(4)
            sig, kq, fgl, bcs, eb, enb, ebl, sq_, eg = T[1], T[4], T[0], T[2], T[3], T[5], T[6], T[7], T[8]
            C.act(sig, ps[5], AF.Exp, scale=-1.0)
            C.ts(DVE, sig, sig, 1.0, ALU.add)
            C.recip(sig, sig)
            C.tt(DVE, v3(kq), v3(sig), bc3(oml4, 128), ALU.mult)
            C.tt(DVE, v3(fgl), v3(kq), bc3(lb4, 128), ALU.add)
            C.tt(POOL, v3(kq), bc3(oml4, 128), v3(kq), ALU.subtract)
            C.act(fgl, fgl, AF.Ln)
            for h in range(4):
                C.scan(bcs[:, h * 128:(h + 1) * 128], reset, fgl[:, h * 128:(h + 1) * 128])
            C.act(eb, bcs, AF.Exp)
            C.act(enb, bcs, AF.Exp, scale=-1.0)
            bc4 = bcs.rearrange("p (a n c) -> p a n c", a=4, n=nch)
            C.tt(POOL, ebl.rearrange("p (a n c) -> p a n c", a=4, n=nch),
                 bc4[:, :, :, Cn - 1:Cn].to_broadcast([128, 4, nch, Cn]), bc4, ALU.subtract)
            C.act(ebl, ebl, AF.Exp)
            C.cp(POOL, decH[:, :, 0:nch].unsqueeze(3),
                 eb.rearrange("p (a n c) -> p a n c", a=4, n=nch)[:, :, :, Cn - 1:Cn])
            C.act(sq_, ps[4], AF.Exp, scale=-1.0)
            C.ts(DVE, sq_, sq_, 1.0, ALU.add)
            C.recip(sq_, sq_)
            C.tt(DVE, sq_, ps[4], sq_, ALU.mult)
            C.tt(DVE, qTb, v3(sq_), v3(eb), ALU.mult)
            C.tt(POOL, kTb, v3(kq), v3(enb), ALU.mult)
            C.tt(POOL, khTb, v3(kq), v3(ebl), ALU.mult)
            C.cp(ACT, i_tm, ps[6])
            C.act(eg, ps[7], AF.Exp, scale=-1.0)
            C.ts(DVE, eg, eg, 1.0, ALU.add)
            C.recip(eg, eg)
            C.tt(DVE, gs_t, ps[7], eg, ALU.mult)

            ck(5)
            C.act(z12[0:64, :], xs[0:64, 12, :], AF.Exp, scale=-2.0)
            C.ts(DVE, z12[0:64, :], z12[0:64, :], 1.0, ALU.add)
            C.recip(z12[0:64, :], z12[0:64, :])
            C.ts(DVE, z12[0:64, :], z12[0:64, :], 2.0, ALU.mult, -1.0, ALU.add)
            C.cp(POOL, z12[64:128, :], xs[64:128, 12, :])
            for pr in range(4):
                sl = slice(pr * 128, (pr + 1) * 128)
                C.mm(ps[0][:, sl], WA[0:64, sl].k("lo"), z12[0:64, :], start=True, stop=False)
                C.mm(ps[0][:, sl], w0row[0:1, sl], ones_row[0:1, :], start=False, stop=True)
            for pr in range(4):
                sl = slice(pr * 128, (pr + 1) * 128)
                C.mm(ps[1][:, sl], WA[64:128, sl].k("hi"), z12[64:128, :], start=True, stop=False)
                C.mm(ps[1][:, sl], a0row[0:1, sl], ones_row[0:1, :], start=False, stop=True)
            C.act(sgt, xs[:, 13, :], AF.Exp, scale=-1.0)
            C.ts(DVE, sgt, sgt, 1.0, ALU.add)
            C.recip(sg_bf, sgt)
            C.mm(ps[3], sg_bf, g1u_bf)
            C.cp(ACT, g_tm, ps[3])
            sw, alr, cumS, gex, gin, ginv, glast, kkk, tq, k2 = T[0], T[1], T[2], T[3], T[4], T[5], T[6], T[7], T[8], T[9]
            C.act(sw, ps[0], AF.Exp, scale=-1.0)
            C.ts(DVE, sw, sw, 1.0, ALU.add)
            C.recip(sw, sw)
            C.act(alr, ps[1], AF.Exp, scale=-1.0)
            C.ts(DVE, alr, alr, 1.0, ALU.add)
            C.recip(alr, alr)
            for pr in range(4):
                C.scan(cumS[:, pr * 128:(pr + 1) * 128], reset, sw[:, pr * 128:(pr + 1) * 128])
            C.tt(POOL, gex, cumS, sw, ALU.subtract)
            C.act(gex, gex, AF.Exp, scale=CDEC)
            C.act(gin, cumS, AF.Exp, scale=CDEC)
            C.act(ginv, cumS, AF.Exp, scale=-CDEC)
            cs4 = cumS.rearrange("p (a n c) -> p a n c", a=4, n=nch)
            C.tt(POOL, glast.rearrange("p (a n c) -> p a n c", a=4, n=nch),
                 cs4[:, :, :, Cn - 1:Cn].to_broadcast([128, 4, nch, Cn]), cs4, ALU.subtract)
            C.act(glast, glast, AF.Exp, scale=CDEC)
            C.cp(POOL, gC[:, :, 0:nch].unsqueeze(3),
                 gin.rearrange("p (a n c) -> p a n c", a=4, n=nch)[:, :, :, Cn - 1:Cn])
            C.tt(POOL, v3(kkk), k_, bc3(kk4, 128), ALU.mult)
            C.act(tq, kkk, AF.Square)
            for pr in range(4):
                sl = slice(pr * 128, (pr + 1) * 128)
                C.mm(ps[2][:, sl], cc("bdones"), tq[:, sl])
            C.act(tq, ps[2], AF.Ln, bias=1e-24)
            C.act(tq, tq, AF.Exp, scale=-0.5)
            C.tt(DVE, kkk, kkk, tq, ALU.mult)
            C.stt(DVE, v3(k2), v3(alr), -1.0, bc3(ka4, 128), ALU.add, ALU.mult)
            C.stt(DVE, v3(k2), v3(k2), 1.0, k_, ALU.add, ALU.mult)
            C.tt(DVE, alr, kkk, alr, ALU.mult)
            b_ = alr
            C.tt(DVE, AR[:, :, 1, :], r_, v3(gin), ALU.mult)
            C.stt(DVE, AR[:, :, 0, :], v3(kkk), -1.0, v3(gex), ALU.mult, ALU.mult)
            C.tt(POOL, bT, v3(b_), v3(ginv), ALU.mult)
            C.tt(POOL, kT, v3(k2), v3(ginv), ALU.mult)
            C.tt(DVE, bhT, v3(b_), v3(glast), ALU.mult)
            C.tt(POOL, khT, v3(k2), v3(glast), ALU.mult)
            C.cp(POOL, vT, v_)
            C.tt(DVE, v3(tq), r_, v3(k2), ALU.mult)
            for pr in range(4):
                C.mm(ps[1][:, pr * 2:(pr + 1) * 2], tq[:, pr * 128:(pr + 1) * 128], RKsel[:, pr, :])
            C.cp(DVE, bon8, ps[1][:, 0:8])
            for (src, bank, half) in ((lambda pr: AR[:, pr, 0, :], 4, 0), (lambda pr: bhT[:, pr, :], 4, 1),
                                      (lambda pr: khT[:, pr, :], 5, 0), (lambda pr: vT[:, pr, :], 5, 1)):
                for pr in range(4):
                    o0 = half * 512 + pr * 128
                    C.tr(psb(bank)[:, o0:o0 + 128], src(pr), ident_bf)
            C.cp(ACT, A_tm, psb(4)[:, 0:512]); C.cp(ACT, Bh_tm, psb(4)[:, 512:1024])
            C.cp(ACT, Kh_tm, psb(5)[:, 0:512]); C.cp(ACT, V_tm, psb(5)[:, 512:1024])

            ck(6)
            for h in range(4):
                C.tr(psb(6)[:, h * 128:(h + 1) * 128], khTb[:, h, :], ident_bf)
            C.cp(ACT, khat_tm, psb(6)[:, 0:512])
            for h in range(4):
                C.mm(ps[6][:, h * 128:(h + 1) * 128], kTb[:, h, :], qTb[:, h, :])
            C.tt(DVE, attT, psv(6, 4), mIb, ALU.mult)
            if not sample:
                for h in range(4):
                    hsl = slice(h * 128, (h + 1) * 128)
                    C.mm(ps[7][:, hsl], attT[:, h, :], i_tm[:, hsl], start=True, stop=False)
                    C.mm(ps[7][:, hsl], qTb[:, h, :], SH_bf[:, h, :], start=False, stop=True)
                for h in range(4):
                    hsl = slice(h * 128, (h + 1) * 128)
                    C.mm(ps[6][:, hsl], khat_tm[:, hsl], i_tm[:, hsl])
                C.tt(POOL, SHtmp, SH, bc3(decH[:, :, 0], 128), ALU.mult)
                C.tt(DVE, SH, SHtmp, psv(6, 4), ALU.add)
                C.cp(POOL, SH_bf, SH)
            else:
                for h in range(4):
                    hsl = slice(h * 128, (h + 1) * 128)
                    C.mm(ps[7][:, hsl], attT[:, h, :], i_tm[:, hsl], start=(h == 0), stop=False)
                for b in range(NB):
                    s0 = S0h[b % 2]
                    csl = slice(b * TS, (b + 1) * TS)
                    C.dma(SP, s0, shg[b].rearrange("h k v -> k h v"))
                    C.cp(POOL, S0hb, s0)
                    eq = EQ[b % 2]
                    C.cp(POOL, eq[:, :, csl], qTb[:, :, csl])
                    for h in range(4):
                        C.mm(ps[7][:, h * 128:(h + 1) * 128], eq[:, h, :], S0hb[:, h, :], start=False, stop=False)
                    C.memset(POOL, eq[:, :, csl], 0.0)
                    C.ts(POOL, khb, khat_tm, cc("cm")[:, b:b + 1], ALU.mult)
                    bank = 5 + b % 2
                    for h in range(4):
                        hsl = slice(h * 128, (h + 1) * 128)
                        C.mm(ps[bank][:, hsl], khb[:, hsl], i_tm[:, hsl])
                    C.tt(POOL, s0, s0, bc3(decH[:, :, b], 128), ALU.mult)
                    C.tt(DVE, s0, s0, psv(bank, 4), ALU.add)
                    C.dma(SP, hgs[b].rearrange("h k v -> k h v"), s0, out_final=True)
            osq = T[4]
            C.act(osq, ps[7], AF.Square)
            C.red(DVE, s4, v3(osq))
            C.act(rr4, s4, AF.Ln, scale=1.0 / 128, bias=RMS_EPS)
            C.act(rr4, rr4, AF.Exp, scale=-0.5)
            C.tt(DVE, v3(osq), psv(7, 4), bc3(rr4, 128), ALU.mult)
            C.tt(POOL, osq, osq, hgw, ALU.mult)
            C.tt(POOL, o_all[:, 512:1024], osq, gs_t, ALU.mult)

            ck(7)
            if sample:
                for g in range(4):
                    sa = scrA[g % 2]
                    nat = sa[0:64, :].rearrange("p (b n) -> p b n", b=4)
                    C.dma(SP, nat.rearrange("p b (h j) -> p b h j", h=8),
                          srw[g * 4:(g + 1) * 4].rearrange("b h v j -> v b h j"))
                    for bb in range(4):
                        b = g * 4 + bb
                        bank = b % 2
                        for pr in range(4):
                            C.tr(ps[bank][:, (bb % 2) * 256 + pr * 64:(bb % 2) * 256 + (pr + 1) * 64],
                                 nat[:, bb, pr * 128:(pr + 1) * 128], cc("ident")[0:64, 0:64])
                        C.cp(ACT, S0T32[:, b, :, :].k(b),
                             ps[bank][:, (bb % 2) * 256:(bb % 2) * 256 + 256].rearrange("p (a v) -> p a v", a=4))
                        C.cp(POOL, S0Tb[:, b, :, :].k(b), S0T32[:, b, :, :].k(b))
            for hg in range(2):
                for i, h in enumerate(heads_of[hg]):
                    pr, hh = h // 2, h % 2
                    rows = slice(64 * hh, 64 * hh + 64)
                    sl = slice(i * 128, (i + 1) * 128)
                    C.mm(ps[0][:, sl], bT[rows, pr, :], AR[rows, pr, 0, :])
                    C.mm(ps[1][:, sl], bT[rows, pr, :], AR[rows, pr, 1, :])
                    C.mm(ps[2][:, sl], kT[rows, pr, :], AR[rows, pr, 0, :])
                    C.mm(ps[3][:, sl], kT[rows, pr, :], AR[rows, pr, 1, :])
                    C.mm(ps[4][:, sl], AR[rows, pr, 0, :], bT[rows, pr, :])
                hs = slice(4 * hg, 4 * hg + 4)
                C.tt(DVE, PT[0][:, hs, 0:128], psv(0, 4), mSb, ALU.mult)
                C.tt(DVE, NrbT[:, hs, :], psv(1, 4), mIb, ALU.mult)
                C.tt(DVE, AkT[:, hs, :], psv(2, 4), mSb, ALU.mult)
                C.tt(DVE, NrkT[:, hs, :], psv(3, 4), mIb, ALU.mult)
                C.tt(DVE, RR[0][:, hs, :], psv(4, 4), mSTb, ALU.mult)
                C.tt(POOL, PT[1][:, hs, 128:256], PT[0][:, hs, 0:128], identb4, ALU.add)
            ck(71)
            for k in range(L):
                cur, nxt = k % 2, 1 - k % 2
                ck(72 + k)
                for hg in range(2):
                    b0 = 3 * hg
                    for i, h in enumerate(heads_of[hg]):
                        pb = ps[b0 + i // 2]
                        o = (i % 2) * 256
                        if k == 0:
                            C.mm(pb[:, o:o + 128], RR[cur][:, h, :], PT[cur][:, h, 0:128])
                            C.mm(ps[b0 + 2][:, i * 128:(i + 1) * 128], PT[cur][:, h, 0:128], RR[cur][:, h, :])
                        elif k < L - 1:
                            C.mm(pb[:, o:o + 256], RR[cur][:, h, :], PT[cur][:, h, :])
                            C.mm(ps[b0 + 2][:, i * 128:(i + 1) * 128], PT[cur][:, h, 0:128], RR[cur][:, h, :])
                        else:
                            C.mm(pb[:, o + 128:o + 256], RR[cur][:, h, :], PT[cur][:, h, 128:256])
                    for j in range(2):
                        hs2 = slice(4 * hg + 2 * j, 4 * hg + 2 * j + 2)
                        pv = ps[b0 + j].rearrange("p (h c) -> p h c", h=2)
                        if k < L - 1:
                            C.cp(ACT, PT[nxt][:, hs2, 0:128], pv[:, :, 0:128])
                        if k >= 1:
                            dst = TTf[:, hs2, :] if k == L - 1 else PT[nxt][:, hs2, 128:256]
                            C.tt(DVE, dst, pv[:, :, 128:256], PT[cur][:, hs2, 128:256], ALU.add)
                    if k < L - 1:
                        C.cp(ACT, RR[nxt][:, 4 * hg:4 * hg + 4, :], psv(b0 + 2, 4))
            ck(8)
            for h in range(8):
                C.mm(ps[6][:, h * 64:(h + 1) * 64], AkT[:, h, :], V_tm[:, h * 64:(h + 1) * 64])
            C.cp(ACT, Z_tm, ps[6])
            for h in range(8):
                pr = h // 2
                C.mm(ps[h // 4][:, (h % 4) * 128:(h % 4 + 1) * 128], A_tm[:, pr * 128:(pr + 1) * 128], TTf[:, h, :])
            for b in range(2):
                pv = ps[b].rearrange("p (q e t) -> p q e t", q=2, e=2)
                C.cp(ACT, W1T[0:64, 2 * b:2 * b + 2, :], pv[0:64, :, 0, :])
                C.cp(ACT, W1T[64:128, 2 * b:2 * b + 2, :], pv[64:128, :, 1, :])
            for h in range(8):
                pr, hh = h // 2, h % 2
                rows = slice(64 * hh, 64 * hh + 64)
                hsl = slice(h * 64, (h + 1) * 64)
                if not sample:
                    C.mm(ps[3][:, hsl], TTf[:, h, :], Z_tm[:, hsl], start=True, stop=False)
                    C.mm(ps[3][:, hsl], W1T[rows, pr, :], ST_bf[rows, pr, :], start=False, stop=True)
                else:
                    C.mm(ps[3][:, hsl], TTf[:, h, :], Z_tm[:, hsl], start=(h == 0), stop=False)
            if sample:
                for b in range(NB):
                    ew = EW[b % 2]
                    csl = slice(b * TS, (b + 1) * TS)
                    C.cp(POOL, ew[:, :, csl], W1T[:, :, csl])
                    for h in hh_order:
                        pr, hh = h // 2, h % 2
                        rows = slice(64 * hh, 64 * hh + 64)
                        C.mm(ps[3][:, h * 64:(h + 1) * 64], ew[rows, pr, :], S0Tb[rows, b, pr, :].k(b), start=False, stop=False)
                    C.memset(POOL, ew[:, :, csl], 0.0)
            C.cp(ACT, U_tm, ps[3])
            for h in range(8):
                pr, hh = h // 2, h % 2
                rows = slice(64 * hh, 64 * hh + 64)
                hsl = slice(h * 64, (h + 1) * 64)
                C.mm(ps[4][:, hsl], NrbT[:, h, :], U_tm[:, hsl], start=(h == 0 or not sample), stop=False)
                C.mm(ps[4][:, hsl], NrkT[:, h, :], V_tm[:, hsl], start=False, stop=False)
                if not sample:
                    C.mm(ps[4][:, hsl], AR[rows, pr, 1, :], ST_bf[rows, pr, :], start=False, stop=True)
            if sample:
                for b in range(NB):
                    er = ER[b % 2]
                    csl = slice(b * TS, (b + 1) * TS)
                    C.cp(POOL, er[:, :, csl], AR[:, :, 1, csl])
                    for h in hh_order:
                        pr, hh = h // 2, h % 2
                        rows = slice(64 * hh, 64 * hh + 64)
                        C.mm(ps[4][:, h * 64:(h + 1) * 64], er[rows, pr, :], S0Tb[rows, b, pr, :].k(b), start=False, stop=False)
                    C.memset(POOL, er[:, :, csl], 0.0)
                i64b = cc("i64s").unsqueeze(1).to_broadcast([128, 4, 64])
                for b in range(NB):
                    bank = 5 + b % 2
                    C.ts(POOL, Ub, U_tm, cc("cm")[:, b:b + 1], ALU.mult)
                    C.ts(POOL, Vb, V_tm, cc("cm")[:, b:b + 1], ALU.mult)
                    C.tt(DVE, Dg, i64b, bc3(gC[:, :, b], 64), ALU.mult)
                    for h in range(8):
                        hsl = slice(h * 64, (h + 1) * 64)
                        C.mm(ps[bank][0:64, hsl], Ub[:, hsl], Bh_tm[:, hsl], start=(h == 0), stop=False)
                        C.mm(ps[bank][0:64, hsl], Vb[:, hsl], Kh_tm[:, hsl], start=False, stop=False)
                    for h in hh_order:
                        pr, hh = h // 2, h % 2
                        rows = slice(64 * hh, 64 * hh + 64)
                        C.mm(ps[bank][0:64, h * 64:(h + 1) * 64], S0T32[rows, b, pr, :].k(b), Dg[rows, pr, :], start=False, stop=False)
                    C.cp(ACT, Sn[0:64, :], ps[bank][0:64, :])
                    C.dma(SP, rws[b].rearrange("h v j -> v h j"), Sn[0:64, :].rearrange("p (h j) -> p h j", h=8), out_final=True)
            if not sample:
                for pr in range(4):
                    psl = slice(pr * 128, (pr + 1) * 128)
                    C.mm(ps[5][:, psl], Bh_tm[:, psl], U_tm[:, psl], start=True, stop=False)
                    C.mm(ps[5][:, psl], Kh_tm[:, psl], V_tm[:, psl], start=False, stop=True)
                C.tt(POOL, STtmp, ST, bc3(gC[:, :, 0], 64), ALU.mult)
                p5 = psv(5, 4)
                C.tt(DVE, ST[0:64, :, :], STtmp[0:64, :, :], p5[0:64, :, 0:64], ALU.add)
                C.tt(DVE, ST[64:128, :, :], STtmp[64:128, :, :], p5[64:128, :, 64:128], ALU.add)
                C.cp(POOL, ST_bf, ST)
            ysq, tmp2 = T[9], T[1]
            y3 = ps[4].rearrange("p (h v) -> p h v", h=8)
            yv = ysq.rearrange("p (h v) -> p h v", h=8)
            C.red(DVE, st16[:, 0:8], y3)
            C.act(ysq, ps[4], AF.Square)
            C.red(DVE, st16[:, 8:16], yv)
            C.ts(DVE, m8, st16[:, 0:8], 1.0 / 64, ALU.mult)
            C.tt(DVE, r8, m8, m8, ALU.mult)
            C.stt(DVE, r8, st16[:, 8:16], 1.0 / 64, r8, ALU.mult, ALU.subtract)
            C.act(r8, r8, AF.Ln, bias=GN_EPS)
            C.act(r8, r8, AF.Exp, scale=-0.5)
            C.tt(DVE, yv, y3, bc3(m8, 64), ALU.subtract)
            C.tt(POOL, yv, yv, bc3(r8, 64), ALU.mult)
            C.tt(POOL, ysq, ysq, lnxw, ALU.mult)
            C.tt(POOL, ysq, ysq, lnxb, ALU.add)
            C.tt(DVE, tmp2.rearrange("p (h v) -> p h v", h=8), V_tm.rearrange("p (h v) -> p h v", h=8),
                 bc3(bon8, 64), ALU.mult)
            C.tt(DVE, ysq, ysq, tmp2, ALU.add)
            C.tt(DVE, o_all[:, 0:512], ysq, g_tm, ALU.mult)

            ck(9)
            for mc in range(8):
                C.tr(psb(6)[:, mc * 128:(mc + 1) * 128], o_all[:, mc * 128:(mc + 1) * 128], ident_bf)
            C.cp(ACT, oT, psb(6).rearrange("p (a t) -> p a t", a=8))
            for half in range(2):
                for mc in range(8):
                    C.mm(ps[half], oT[:, mc, :], w_out_bf[:, mc, half * 512:(half + 1) * 512].k(mc),
                         start=(mc == 0), stop=(mc == 7))
            for half in range(2):
                hsl = slice(half * 512, (half + 1) * 512)
                C.stt(DVE, h1pre[:, hsl], xt[:, hsl], ALPHA, ps[half], ALU.mult, ALU.add)
            layernorm(h1pre, h1pre, LN_EPS)
            C.dma(SP, h1_dst, h1pre)

        stopped = False
        try:
            ck(0)
            for ti in range(NT):
                mixer_tile(ti, xp[ti * 128:(ti + 1) * 128, :], V(h1scr[ti * 128:(ti + 1) * 128, :], ("h1scr", ti)), False)
            ck(10)

            with nc.allow_non_contiguous_dma(reason="tiny state vector"):
                C.dma(SP, shp.rearrange("(c p) -> p c", p=128), plast, out_final=True)
            C.dma(SP, hgp.rearrange("h k v -> k h v"), SH, out_final=True)
            identf = cc("ident")
            for pr in range(4):
                C.tr(ps[2][0:64, pr * 128:(pr + 1) * 128], ST[:, pr, :], identf)
            rwo = T[0]
            C.cp(DVE, rwo[0:64, :], ps[2][0:64, :])
            C.dma(SP, rwp.rearrange("h v j -> v h j"), rwo[0:64, :].rearrange("p (h j) -> p h j", h=8), out_final=True)
            if SAMPLE:
                mixer_tile(NTP, xsm, V(h1scr[NTP * 128:(NTP + 1) * 128, :], ("h1scr", NTP)), True)
            if DBG:
                C.dma(SP, dbg_d, dbg, out_final=True)

            ck(11)
            P.barrier()
            arena.off = phase_mark
            w_up_bf = sb([128, 8, DFF], BF16, "w_up_bf")
            w_dn_bf = sb([128, 32, D], BF16, "w_dn_bf")
            for kc in range(8):
                C.dma(POOL, w_up_bf[:, kc, :].k(kc), w_up[kc * 128:(kc + 1) * 128, :])
            C.dma(SP, lng, row(ln2_g).partition_broadcast(128))
            C.dma(SP, lnb, row(ln2_b).partition_broadcast(128))
            for fc in range(32):
                C.dma(POOL, w_dn_bf[:, fc, :].k(fc), w_down[fc * 128:(fc + 1) * 128, :])
            upT = sb([128, 32, 512], BF16, "upT")
            h1T = sb([128, 8, 512], BF16, "h1T")
            h1b = [sb([128, D], BF16, f"h1b{i}") for i in range(2)]
            h1r = [sb([128, D], F32, f"h1r{i}") for i in range(2)]
            rl = [sb([128, 512], F32, f"rl{i}") for i in range(2)]
            pre2 = sb([128, D], F32, "pre2")
            outb = [sb([128, D], F32, f"outb{i}") for i in range(2)]
            ntiles = NT + (1 if SAMPLE else 0)
            tiles = list(range(NT)) + ([NTP] if SAMPLE else [])
            groups = [tiles[i:i + 4] for i in range(0, NT, 4)]
            if SAMPLE:
                groups.append([NTP])
            gcount = 0
            tcount = 0
            for grp in groups:
                ng = len(grp)
                W = ng * 128
                for gi, tix in enumerate(grp):
                    hb = h1b[(tcount + gi) % 2]
                    C.dma(POOL, hb, V(h1scr[tix * 128:(tix + 1) * 128, :], ("h1scr", tix)))
                    bank = 6 + (gi % 2)
                    for kc in range(8):
                        C.tr(psb(bank)[:, kc * 128:(kc + 1) * 128], hb[:, kc * 128:(kc + 1) * 128], ident_bf)
                    C.cp(ACT, h1T[:, :, gi * 128:(gi + 1) * 128], psb(bank).rearrange("p (a t) -> p a t", a=8))
                for fc in range(32):
                    bank = fc % 2
                    for kc in range(8):
                        C.mm(ps[bank][:, 0:W], w_up_bf[:, kc, fc * 128:(fc + 1) * 128].k(kc), h1T[:, kc, 0:W],
                             start=(kc == 0), stop=(kc == 7))
                    r = rl[fc % 2]
                    C.act(r[:, 0:W], ps[bank][:, 0:W], AF.Relu)
                    C.tt(POOL if fc % 2 else DVE, upT[:, fc, 0:W], r[:, 0:W], r[:, 0:W], ALU.mult)
                for gi, tix in enumerate(grp):
                    hr = h1r[(tcount + gi) % 2]
                    C.dma(SP, hr, V(h1scr[tix * 128:(tix + 1) * 128, :], ("h1scr", tix)))
                    for half in range(2):
                        bank = 2 + ((gi * 2 + half) % 4)
                        for fc in range(32):
                            C.mm(ps[bank], upT[:, fc, gi * 128:(gi + 1) * 128], w_dn_bf[:, fc, half * 512:(half + 1) * 512].k(fc),
                                 start=(fc == 0), stop=(fc == 31))
                        hsl = slice(half * 512, (half + 1) * 512)
                        C.stt(DVE, pre2[:, hsl], hr[:, hsl], ALPHA, ps[bank], ALU.mult, ALU.add)
                    ob = outb[(tcount + gi) % 2]
                    layernorm(pre2, ob, LN_EPS)
                    dst = ys if tix == NTP else yp[tix * 128:(tix + 1) * 128, :]
                    C.dma(SP, dst, ob, out_final=True)
                tcount += ng
                gcount += 1

        except _Stop:
            C.dma(SP, shp.rearrange("(c p) -> p c", p=128), plast, out_final=True)

        sems = {e: es.enter_context(nc.semaphore(f"s_{e}")) for e in ENGINES}
        rings = {e: [es.enter_context(nc.semaphore(f"r_{e}{i}")) for i in range(n)] for e, n in DMA_RING.items()}
        P.prepare(sems, rings)
        build.stats = dict(P.stats)
        build.arena_peak = arena.peak
        with nc.allow_low_precision(reason="bf16 matmul operands, fp32 accumulation"), \
                nc.allow_non_contiguous_dma(reason="tiny per-channel vectors / state layouts"), \
                nc.Block() as block:
            block.tensor(lambda eng: P.emit_engine(PE, eng))
            block.scalar(lambda eng: P.emit_engine(ACT, eng))
            block.vector(lambda eng: P.emit_engine(DVE, eng))
            block.gpsimd(lambda eng: P.emit_engine(POOL, eng))
            block.sync(lambda eng: P.emit_engine(SP, eng))
    return nc


IN_NAMES = ["w_in", "shift_mu", "w0", "w1u", "a0", "a1u", "g1u", "k_k", "k_a", "r_k", "ln_x_w", "ln_x_b",
            "lb_logits", "hg_norm_w", "w_out", "ln1_g", "ln1_b", "w_up", "w_down", "ln2_g", "ln2_b"]


def make_in_maps(inputs, n_cores=8):
    f = lambda a: np.ascontiguousarray(np.asarray(a, dtype=np.float32))
    shared = {}
    for k in IN_NAMES:
        a = f(inputs[k])
        a = a[0] if k != "lb_logits" else a
        if k == "r_k":
            a = a.reshape(-1)
        shared[k] = np.ascontiguousarray(a)
    shared["consts"] = CONSTS
    maps = []
    for c in range(n_cores):
        m = dict(shared)
        m["xp"] = f(inputs["x_prompt"][c])
        m["xs"] = f(inputs["x_sample"][c * NB:(c + 1) * NB]).reshape(NB * TS, D)
        m["srw"] = f(inputs["state_rwkv"][0, c * NB:(c + 1) * NB])
        m["shg"] = f(inputs["state_hgrn"][0, c * NB:(c + 1) * NB])
        m["ssh"] = f(inputs["state_shift"][0, c * NB:(c + 1) * NB])
        maps.append(m)
    return maps


_NC_CACHE = {}


def kernel(**inputs):
    if "nc" not in _NC_CACHE:
        _NC_CACHE["nc"] = build()
    nc = _NC_CACHE["nc"]
    maps = make_in_maps(inputs)
    res = run_bass_kernel_spmd(nc, maps, core_ids=list(range(8)))
    R = res.results
    y_prompt = np.stack([R[c]["yp"] for c in range(8)]).astype(np.float32)
    y_sample = np.concatenate([R[c]["ys"].reshape(NB, TS, D) for c in range(8)]).astype(np.float32)
    rw_p = np.stack([R[c]["rwp"] for c in range(8)])[None].astype(np.float32)
    rw_s = np.concatenate([R[c]["rws"] for c in range(8)])[None].astype(np.float32)
    hg_p = np.stack([R[c]["hgp"] for c in range(8)])[None].astype(np.float32)
    hg_s = np.concatenate([R[c]["hgs"] for c in range(8)])[None].astype(np.float32)
    sh_p = np.stack([R[c]["shp"] for c in range(8)])[None].astype(np.float32)
    sh_s = np.concatenate([R[c]["shs"] for c in range(8)])[None].astype(np.float32)
    return (y_prompt, y_sample, rw_p, rw_s, hg_p, hg_s, sh_p, sh_s)
```

```python
import numpy as np
import concourse.bass as bass
import concourse.mybir as mybir
from concourse.bass_utils import run_bass_kernel_spmd

F32 = mybir.dt.float32
BF16 = mybir.dt.bfloat16
AF = mybir.ActivationFunctionType
ALU = mybir.AluOpType
AX = mybir.AxisListType

DEBUG_LINES = False
LINE_OF = {}
PE, ACT, DVE, POOL, SP = "pe", "act", "dve", "pool", "sp"
ENGINES = (PE, ACT, DVE, POOL, SP)
DMA_RING = {SP: 12, ACT: 4, POOL: 8}


class Res:
    __slots__ = ("name", "last_w", "readers")

    def __init__(self, name):
        self.name = name
        self.last_w = None
        self.readers = []


class Op:
    __slots__ = ("eng", "fn", "deps", "is_dma", "signal", "idx", "dma_no", "extra_wait", "rg")

    def __init__(self, eng, fn, is_dma):
        self.eng = eng
        self.fn = fn
        self.deps = set()
        self.is_dma = is_dma
        self.signal = False
        self.idx = None
        self.dma_no = None
        self.extra_wait = None
        self.rg = None


def _pe_inorder_ok(d, o):
    return d.rg is None or o.rg is None or d.rg == o.rg


class Prog:
    def __init__(self):
        self.ops = {e: [] for e in ENGINES}
        self.order = []
        self.n_dma = {e: 0 for e in ENGINES}
        self.dma_ops = {e: [] for e in ENGINES}
        self.out_dmas = []

    def op(self, eng, fn, reads=(), writes=(), dma=False, out=False):
        o = Op(eng, fn, dma)
        for r in reads:
            if r.last_w is not None:
                o.deps.add(r.last_w)
        for w in writes:
            if w.last_w is not None:
                o.deps.add(w.last_w)
            for rd in w.readers:
                o.deps.add(rd)
        for r in reads:
            r.readers.append(o)
        for w in writes:
            w.last_w = o
            w.readers = []
        if getattr(self, "barrier_left", None) and eng in self.barrier_left:
            self.barrier_left.discard(eng)
            o.deps.update(self.pending_barrier)
        o.deps.discard(o)
        if dma:
            o.dma_no = self.n_dma[eng]
            self.n_dma[eng] += 1
            self.dma_ops[eng].append(o)
            o.signal = True
            if out:
                self.out_dmas.append(o)
        self.ops[eng].append(o)
        self.order.append(o)
        return o

    def barrier(self):
        pend = []
        for e in ENGINES:
            comp = [o for o in self.ops[e] if not o.is_dma]
            if comp:
                pend.append(comp[-1])
            pend.extend(self.dma_ops[e][-DMA_RING.get(e, 0):] if e in DMA_RING else [])
        self.pending_barrier = pend
        self.barrier_left = set(ENGINES)

    def prepare(self, sems, rings):
        for o in self.order:
            for d in o.deps:
                if d.is_dma:
                    continue
                if d.eng == o.eng and d.eng == PE and not o.is_dma and _pe_inorder_ok(d, o):
                    continue
                d.signal = True
        sig = {}
        for e in ENGINES:
            c = 0
            for o in self.ops[e]:
                if o.is_dma:
                    R = len(rings[e])
                    sig[o] = (rings[e][o.dma_no % R], 16 * (o.dma_no // R + 1))
                elif o.signal:
                    c += 1
                    sig[o] = (sems[e], c)
        self.sig = sig
        self.rings = rings
        self.stats = {e: len(self.ops[e]) for e in ENGINES}

    def emit_engine(self, e, eng):
        sig, rings = self.sig, self.rings
        waited = {}

        def wait(sem, val):
            k = id(sem)
            if waited.get(k, 0) >= val:
                return
            waited[k] = val
            eng.wait_ge(sem, val)

        for o in self.ops[e]:
            for d in o.deps:
                if d not in sig:
                    continue
                if d.eng == e and not d.is_dma and not o.is_dma and e == PE and _pe_inorder_ok(d, o):
                    continue
                s, v = sig[d]
                wait(s, v)
            if o.is_dma:
                R = len(rings[e])
                if o.dma_no >= R:
                    wait(rings[e][o.dma_no % R], 16 * (o.dma_no // R))
            ins = o.fn(eng)
            if DEBUG_LINES:
                LINE_OF[str(getattr(getattr(ins, "ins", ins), "name", ins))] = o.extra_wait
            if o in sig:
                s, v = sig[o]
                ins.then_inc(s, 16 if o.is_dma else 1)
        if e == SP:
            for q in ENGINES:
                if q not in rings:
                    continue
                for o in self.dma_ops[q][-len(rings[q]):]:
                    s, v = sig[o]
                    wait(s, v)


class V:
    __slots__ = ("ap", "key")

    def __init__(self, ap, key):
        self.ap = ap
        self.key = key

    def __getitem__(self, idx):
        return V(self.ap[idx], self.key)

    def k(self, sub):
        return V(self.ap, (self.key, sub))

    @property
    def shape(self):
        return self.ap.shape

    def rearrange(self, *a, **kw):
        return V(self.ap.rearrange(*a, **kw), self.key)

    def unsqueeze(self, ax):
        return V(self.ap.unsqueeze(ax), self.key)

    def to_broadcast(self, shape):
        return V(self.ap.to_broadcast(list(shape)), self.key)

    def bitcast(self, dt):
        return V(self.ap.bitcast(dt), self.key)


def _ap(x):
    return x.ap if isinstance(x, V) else x


def _isnum(x):
    return isinstance(x, (int, float))


class Arena:
    def __init__(self, nc, es, nbytes):
        self.n2 = nbytes // 2
        self.t = es.enter_context(nc.sbuf_tensor("arena", [128, self.n2], BF16))
        self.off = 0
        self.cnt = 0
        self.peak = 0

    def alloc(self, shape, dt, name=None):
        shape = list(shape)
        esz = 4 if dt == F32 else 2
        n = int(np.prod(shape[1:]))
        nbytes = (n * esz + 3) // 4 * 4
        o = self.off
        assert o + nbytes <= self.n2 * 2, f"arena overflow allocating {name} {shape}: {o}+{nbytes} > {self.n2 * 2}"
        self.off += nbytes
        self.peak = max(self.peak, self.off)
        ap = self.t[0:shape[0], o // 2:o // 2 + nbytes // 2]
        if esz == 4:
            ap = ap.bitcast(F32)
        ap = ap[:, 0:n]
        if len(shape) == 3:
            ap = ap.rearrange("p (a b) -> p a b", a=shape[1])
        elif len(shape) == 4:
            ap = ap.rearrange("p (a b c) -> p a b c", a=shape[1], b=shape[2])
        self.cnt += 1
        return V(ap, name or f"t{self.cnt}")


class Ctx:
    def __init__(self, nc, P, arena):
        self.nc, self.P, self.arena = nc, P, arena
        self.res = {}

    def sb(self, shape, dt, name=None):
        return self.arena.alloc(shape, dt, name)

    def R(self, x):
        k = x.key if isinstance(x, V) else (x.name, None)
        r = self.res.get(k)
        if r is None:
            r = self.res[k] = Res(k)
        return r

    def rec(self, eng, fn, outs, ins, dma=False, out=False):
        reads = [self.R(i) for i in ins if i is not None and not _isnum(i)]
        writes = [self.R(o) for o in outs]
        writes += [r for r in reads if isinstance(r.name, str) and r.name.startswith("ps")]
        o_ = self.P.op(eng, fn, reads=reads, writes=writes, dma=dma, out=out)
        if DEBUG_LINES:
            import sys as _s
            f = _s._getframe(1)
            while f.f_code.co_name not in ("mixer_tile", "build", "layernorm") and f.f_back is not None:
                f = f.f_back
            o_.extra_wait = f.f_lineno
        return o_

    def mm(self, out, lhsT, rhs, start=True, stop=True):
        o, l, r = _ap(out), _ap(lhsT), _ap(rhs)
        op = self.rec(PE, lambda e: e.matmul(o, lhsT=l, rhs=r, start=start, stop=stop,
                                             skip_group_check=True), [out], [lhsT, rhs])
        kr = l.shape[0]
        if kr < 128:
            op.rg = (kr, l.base_partition())
        return op

    def tr(self, out, in_, ident):
        o, i, d = _ap(out), _ap(in_), _ap(ident)
        return self.rec(PE, lambda e: e.transpose(o, i, d), [out], [in_, ident])

    def act(self, out, in_, func, bias=None, scale=1.0):
        o, i = _ap(out), _ap(in_)
        kw = {}
        if bias is not None:
            kw["bias"] = _ap(bias)
        s = _ap(scale)
        return self.rec(ACT, lambda e: e.activation(out=o, in_=i, func=func, scale=s, **kw), [out],
                        [in_, bias, scale])

    def tt(self, eng, out, a, b, op):
        o, x, y = _ap(out), _ap(a), _ap(b)
        return self.rec(eng, lambda e: e.tensor_tensor(out=o, in0=x, in1=y, op=op), [out], [a, b])

    def ts(self, eng, out, a, s1, op0, s2=None, op1=None):
        o, x, v1, v2 = _ap(out), _ap(a), _ap(s1), _ap(s2)
        if op1 is None:
            f = lambda e: e.tensor_scalar(out=o, in0=x, scalar1=v1, scalar2=None, op0=op0)
        else:
            f = lambda e: e.tensor_scalar(out=o, in0=x, scalar1=v1, scalar2=v2, op0=op0, op1=op1)
        return self.rec(eng, f, [out], [a, s1, s2])

    def stt(self, eng, out, in0, scalar, in1, op0, op1):
        o, x, y, s = _ap(out), _ap(in0), _ap(in1), _ap(scalar)
        return self.rec(eng, lambda e: e.scalar_tensor_tensor(out=o, in0=x, scalar=s, in1=y, op0=op0, op1=op1),
                        [out], [in0, in1, scalar])

    def cp(self, eng, out, in_):
        o, i = _ap(out), _ap(in_)
        if eng == ACT:
            return self.rec(ACT, lambda e: e.activation(out=o, in_=i, func=AF.Copy), [out], [in_])
        return self.rec(eng, lambda e: e.tensor_copy(out=o, in_=i), [out], [in_])

    def recip(self, out, in_):
        o, i = _ap(out), _ap(in_)
        return self.rec(DVE, lambda e: e.reciprocal(out=o, in_=i), [out], [in_])

    def scan(self, out, d0, d1):
        o, a, b = _ap(out), _ap(d0), _ap(d1)
        return self.rec(DVE, lambda e: e.tensor_tensor_scan(out=o, data0=a, data1=b, initial=0.0,
                                                            op0=ALU.mult, op1=ALU.add), [out], [d0, d1])

    def red(self, eng, out, in_, op=ALU.add):
        o, i = _ap(out), _ap(in_)
        return self.rec(eng, lambda e: e.tensor_reduce(out=o, in_=i, axis=AX.X, op=op), [out], [in_])

    def memset(self, eng, out, val):
        o = _ap(out)
        return self.rec(eng, lambda e: e.memset(o, val), [out], [])

    def dma(self, eng, out, in_, out_final=False, extra_out=()):
        o, i = _ap(out), _ap(in_)
        return self.rec(eng, lambda e: e.dma_start(out=o, in_=i), [out, *extra_out], [in_], dma=True, out=out_final)

    def bn_stats(self, out, in_):
        o, i = _ap(out), _ap(in_)
        return self.rec(DVE, lambda e: e.bn_stats(out=o, in_=i), [out], [in_])

    def bn_aggr(self, out, in_):
        o, i = _ap(out), _ap(in_)
        return self.rec(DVE, lambda e: e.bn_aggr(out=o, in_=i), [out], [in_])


D = 1024
PJ = 3840
RWP = 1792
NTP = 16
NB = 16
TS = 8
DFF = 4096
ALPHA = 2.0 ** 0.25
CDEC = -float(np.exp(-0.5))
LN_EPS = 1e-5
GN_EPS = 64e-5
RMS_EPS = 1e-6
ARENA_BYTES = 211000


def make_consts():
    s = np.arange(128)[:, None]
    t = np.arange(128)[None, :]
    cols = {}
    cols["ident"] = (s == t)
    cols["mS_p"] = (s < t)
    cols["mI_p"] = (s <= t)
    cols["mST_p"] = (t < s)
    same = (s // TS) == (t // TS)
    cols["mS_s"] = (s < t) & same
    cols["mI_s"] = (s <= t) & same
    cols["mST_s"] = (t < s) & same
    cols["reset_p"] = np.broadcast_to(t != 0, (128, 128))
    cols["reset_s"] = np.broadcast_to((t % TS) != 0, (128, 128))
    cols["bdones"] = (s // 64) == (t // 64)
    cols["hsel"] = (s // 64) == np.arange(2)[None, :]
    cols["cm"] = (s // TS) == np.arange(NB)[None, :]
    cols["i64s"] = (s % 64) == np.arange(64)[None, :]
    off = {}
    parts = []
    o = 0
    for k, v in cols.items():
        v = np.asarray(v, np.float32)
        off[k] = (o, o + v.shape[1])
        o += v.shape[1]
        parts.append(v)
    return np.ascontiguousarray(np.concatenate(parts, axis=1)), off


CONSTS, COFF = make_consts()
NCONST = CONSTS.shape[1]


class _Stop(Exception):
    pass


def build(NT=NTP, SAMPLE=True, DBG=False, STAGE=99):
    from contextlib import ExitStack
    nc = bass.Bass("TRN2", target_bir_lowering=False)

    def din(name, shape):
        return nc.dram_tensor(name, list(shape), F32, kind="ExternalInput").ap()

    def dout(name, shape):
        return nc.dram_tensor(name, list(shape), F32, kind="ExternalOutput").ap()

    xp = din("xp", [NTP * 128, D]); xsm = din("xs", [128, D])
    srw = din("srw", [NB, 8, 64, 64]); shg = din("shg", [NB, 4, 128, 128]); ssh = din("ssh", [NB, RWP])
    w_in = din("w_in", [D, PJ]); shift_mu = din("shift_mu", [RWP]); w0 = din("w0", [512])
    w1u = din("w1u", [64, 512]); a0 = din("a0", [512]); a1u = din("a1u", [64, 512]); g1u = din("g1u", [128, 512])
    k_k = din("k_k", [512]); k_a = din("k_a", [512]); r_k = din("r_k", [512])
    ln_x_w = din("ln_x_w", [512]); ln_x_b = din("ln_x_b", [512]); lb_logits = din("lb_logits", [2, 512])
    hg_norm_w = din("hg_norm_w", [512]); w_out = din("w_out", [D, D]); ln1_g = din("ln1_g", [D]); ln1_b = din("ln1_b", [D])
    w_up = din("w_up", [D, DFF]); w_down = din("w_down", [DFF, D]); ln2_g = din("ln2_g", [D]); ln2_b = din("ln2_b", [D])
    cst_d = din("consts", [128, NCONST])
    yp = dout("yp", [NTP * 128, D]); ys = dout("ys", [128, D])
    rwp = dout("rwp", [8, 64, 64]); rws = dout("rws", [NB, 8, 64, 64])
    hgp = dout("hgp", [4, 128, 128]); hgs = dout("hgs", [NB, 4, 128, 128])
    shp = dout("shp", [RWP]); shs = dout("shs", [NB, RWP])
    h1scr = nc.dram_tensor("h1scr", [(NTP + 1) * 128, D], F32).ap()
    if DBG:
        dbg_d = dout("dbg", [128, 4096])

    def row(v):
        return v.rearrange("(o n) -> o n", o=1)

    P = Prog()
    with ExitStack() as es:
        arena = Arena(nc, es, ARENA_BYTES)
        C = Ctx(nc, P, arena)
        sb = C.sb
        ps = [V(es.enter_context(nc.psum_tensor(f"ps{i}", [128, 512], F32))[:], f"ps{i}") for i in range(8)]

        def psv(i, a):
            return ps[i].rearrange("p (a t) -> p a t", a=a)

        def psb(i):
            return ps[i].bitcast(BF16)

        def bc3(ap2, n):
            return ap2.unsqueeze(2).to_broadcast([ap2.shape[0], ap2.shape[1], n])

        def v3(t):
            return t.rearrange("p (a t) -> p a t", a=4)

        ident_bf = sb([128, 128], BF16, "ident_bf")
        lng = sb([128, D], F32, "lng"); lnb = sb([128, D], F32, "lnb")
        C.dma(SP, lng, row(ln1_g).partition_broadcast(128))
        C.dma(SP, lnb, row(ln1_b).partition_broadcast(128))
        bnst = sb([128, 12], F32, "bnst"); mv = sb([128, 2], F32, "mv"); rstd1 = sb([128, 1], F32, "rstd1")
        if DBG:
            dbg = sb([128, 4096], F32, "dbg")
            C.memset(POOL, dbg, 0.0)
            dbg_pos = [0]
            dbg_map = {}

            def dump(name, v, n):
                a = dbg_pos[0]
                shape = list(v.shape)
                dst = dbg[0:shape[0], a:a + n]
                if len(shape) == 3:
                    dst = dst.rearrange("p (a b) -> p a b", a=shape[1])
                C.cp(POOL, dst, v)
                dbg_pos[0] += n
                dbg_map[name] = (a, n, shape)
            build.dbg_map = dbg_map
        else:
            def dump(name, v, n):
                return None
        phase_mark = arena.off

        def layernorm(src, dst, eps):
            for half in range(2):
                C.bn_stats(bnst[:, half * 6:(half + 1) * 6], src[:, half * 512:(half + 1) * 512])
            C.bn_aggr(mv, bnst)
            C.act(rstd1, mv[:, 1:2], AF.Ln, bias=eps)
            C.act(rstd1, rstd1, AF.Exp, scale=-0.5)
            C.ts(DVE, dst, src, mv[:, 0:1], ALU.subtract, rstd1[:, 0:1], ALU.mult)
            C.tt(POOL, dst, dst, lng, ALU.mult)
            C.tt(POOL, dst, dst, lnb, ALU.add)

        cst = sb([128, NCONST], F32, "cst")
        C.dma(SP, cst, cst_d)

        def cc(name):
            a, b = COFF[name]
            return cst[:, a:b]

        C.cp(POOL, ident_bf, cc("ident"))
        ones_row = sb([1, 128], BF16, "ones_row")
        C.memset(POOL, ones_row, 1.0)
        bdones_bf = sb([128, 128], BF16, "bdones_bf")
        C.cp(POOL, bdones_bf, cc("bdones"))
        mu14 = sb([128, 14], F32, "mu14")
        kk4 = sb([128, 4], F32, "kk4"); ka4 = sb([128, 4], F32, "ka4"); rk4 = sb([128, 4], F32, "rk4")
        lbl = sb([128, 2, 4], F32, "lbl")
        with nc.allow_non_contiguous_dma(reason="tiny per-channel parameter vectors"):
            C.dma(SP, mu14, shift_mu.rearrange("(c p) -> p c", p=128))
            C.dma(SP, kk4, k_k.rearrange("(c p) -> p c", p=128))
            C.dma(SP, ka4, k_a.rearrange("(c p) -> p c", p=128))
            C.dma(SP, rk4, r_k.rearrange("(c p) -> p c", p=128))
            C.dma(SP, lbl, lb_logits.rearrange("l (c p) -> p l c", p=128))
        w0row = sb([1, 512], BF16, "w0row"); a0row = sb([1, 512], BF16, "a0row")
        C.dma(POOL, w0row, row(w0))
        C.dma(POOL, a0row, row(a0))
        WA = sb([128, 512], BF16, "WA")
        C.dma(POOL, WA[0:64, :].k("lo"), w1u)
        C.dma(POOL, WA[64:128, :].k("hi"), a1u)
        g1u_bf = sb([128, 512], BF16, "g1u_bf")
        C.dma(POOL, g1u_bf, g1u)
        lnxw = sb([128, 512], F32, "lnxw"); lnxb = sb([128, 512], F32, "lnxb"); hgw = sb([128, 512], F32, "hgw")
        C.dma(SP, lnxw, row(ln_x_w).partition_broadcast(128))
        C.dma(SP, lnxb, row(ln_x_b).partition_broadcast(128))
        C.dma(SP, hgw, row(hg_norm_w).partition_broadcast(128))
        lb4 = sb([128, 4], F32, "lb4"); oml4 = sb([128, 4], F32, "oml4"); etmp = sb([128, 4], F32, "etmp")
        C.tt(DVE, etmp, lbl[:, 1, :], lbl[:, 0, :], ALU.subtract)
        C.act(etmp, etmp, AF.Exp)
        C.ts(DVE, lb4, etmp, 1.0, ALU.add)
        C.recip(lb4, lb4)
        C.tt(DVE, oml4, etmp, lb4, ALU.mult)
        RKsel = sb([128, 4, 2], BF16, "RKsel")
        C.tt(DVE, RKsel, bc3(rk4, 2), cc("hsel").unsqueeze(1).to_broadcast([128, 4, 2]), ALU.mult)

        ov_lo = arena.off
        w_in_bf = sb([128, 8, PJ], BF16, "w_in_bf")
        ov_hi = arena.off
        for kc in range(8):
            C.dma(POOL, w_in_bf[:, kc, :].k(kc), w_in[kc * 128:(kc + 1) * 128, :])
        w_out_bf = sb([128, 8, D], BF16, "w_out_bf")
        for kc in range(8):
            C.dma(POOL, w_out_bf[:, kc, :].k(kc), w_out[kc * 128:(kc + 1) * 128, :])

        ST = sb([128, 4, 64], F32, "ST"); ST_bf = sb([128, 4, 64], BF16, "ST_bf")
        SH = sb([128, 4, 128], F32, "SH"); SH_bf = sb([128, 4, 128], BF16, "SH_bf")
        plast = sb([128, 14], F32, "plast")
        for t_ in (ST, ST_bf, SH, SH_bf, plast):
            C.memset(POOL, t_, 0.0)

        x_t = [sb([128, D], F32, "x_t0")]
        x_bf = sb([128, D], BF16, "x_bf")
        xT = sb([128, 8, 128], BF16, "xT")
        pr_ = sb([128, 14, 129], F32, "pr")
        xs = sb([128, 14, 128], F32, "xsft")
        T = [sb([128, 512], F32, f"T{i}") for i in range(10)]
        SHtmp = v3(T[9])
        STtmp = T[1][:, 0:256].rearrange("p (a v) -> p a v", a=4)
        z12 = sb([128, 128], BF16, "z12"); sg_bf = sb([128, 128], BF16, "sg_bf"); tqb = sb([128, 512], BF16, "tqb")
        g_tm = sb([128, 512], BF16, "g_tm")
        AR = sb([128, 4, 2, 128], BF16, "AR")
        bT = sb([128, 4, 128], BF16, "bT"); kT = sb([128, 4, 128], BF16, "kT")
        bhT = sb([128, 4, 128], BF16, "bhT"); khT = sb([128, 4, 128], BF16, "khT"); vT = sb([128, 4, 128], BF16, "vT")
        A_tm = sb([128, 512], BF16, "A_tm"); Bh_tm = sb([128, 512], BF16, "Bh_tm")
        Kh_tm = sb([128, 512], BF16, "Kh_tm"); V_tm = sb([128, 512], BF16, "V_tm")
        bon8 = sb([128, 8], F32, "bon8")
        PT = [sb([128, 8, 256], BF16, f"PT{i}") for i in range(2)]
        RR = [sb([128, 8, 128], BF16, f"RR{i}") for i in range(2)]
        NrbT = sb([128, 8, 128], BF16, "NrbT"); AkT = sb([128, 8, 128], BF16, "AkT"); NrkT = sb([128, 8, 128], BF16, "NrkT")
        TTf = sb([128, 8, 128], BF16, "TTf")
        W1T = sb([128, 4, 128], BF16, "W1T")
        Z_tm = sb([128, 512], BF16, "Z_tm"); U_tm = sb([128, 512], BF16, "U_tm")
        st16 = sb([128, 16], F32, "st16"); m8 = sb([128, 8], F32, "m8"); r8 = sb([128, 8], F32, "r8")
        o_all = sb([128, D], BF16, "o_all"); oT = sb([128, 8, 128], BF16, "oT")
        h1pre = sb([128, D], F32, "h1pre")
        qTb = sb([128, 4, 128], BF16, "qTb"); kTb = sb([128, 4, 128], BF16, "kTb"); khTb = sb([128, 4, 128], BF16, "khTb")
        khat_tm = sb([128, 512], BF16, "khat_tm"); i_tm = sb([128, 512], BF16, "i_tm")
        gs_t = sb([128, 512], BF16, "gs_t"); attT = sb([128, 4, 128], BF16, "attT")
        s4 = sb([128, 4], F32, "s4"); rr4 = sb([128, 4], F32, "rr4")
        gC = sb([128, 4, NB], F32, "gC"); decH = sb([128, 4, NB], F32, "decH")
        fence2 = sb([128, 2], F32, "fence2")
        identb4 = ident_bf.unsqueeze(1).to_broadcast([128, 4, 128])
        save_off = arena.off
        arena.off = ov_lo
        scrA = [sb([128, 2048], F32, f"scrA{i}") for i in range(2)]
        S0T32 = sb([128, NB, 4, 64], F32, "S0T32")
        S0Tb = sb([128, NB, 4, 64], BF16, "S0Tb")
        sshT = sb([128, 14, NB], F32, "sshT"); lastp = sb([128, 14, NB], F32, "lastp")
        EW = [sb([128, 4, 128], BF16, f"EW{i}") for i in range(2)]
        ER = [sb([128, 4, 128], BF16, f"ER{i}") for i in range(2)]
        EQ = [sb([128, 4, 128], BF16, f"EQ{i}") for i in range(2)]
        Ub = sb([128, 512], BF16, "Ub"); Vb = sb([128, 512], BF16, "Vb"); khb = sb([128, 512], BF16, "khb")
        Dg = sb([128, 4, 64], F32, "Dg")
        S0h = [sb([128, 4, 128], F32, f"S0h{i}") for i in range(2)]
        S0hb = sb([128, 4, 128], BF16, "S0hb")
        Sn = sb([128, 512], F32, "Sn")
        fence_t = sb([128, 2], F32, "fence_t")
        assert arena.off <= ov_hi, (arena.off, ov_hi)
        ov_bufs = scrA + [S0T32, S0Tb, sshT, lastp] + EW + ER + EQ + [Ub, Vb, khb, Dg] + S0h + [S0hb, Sn, fence_t]
        arena.off = save_off
        ssh_tm = scrA[0][0:NB, 0:RWP]
        hh_order = (0, 2, 4, 6, 1, 3, 5, 7)
        heads_of = [(0, 1, 2, 3), (4, 5, 6, 7)]

        def ck(n):
            if STAGE == n:
                raise _Stop()

        def mixer_tile(ti, x_src, h1_dst, sample):
            nch = NB if sample else 1
            Cn = TS if sample else 128
            L = 3 if sample else 7
            sfx = "_s" if sample else "_p"
            mS, mI, mST, reset = cc("mS" + sfx), cc("mI" + sfx), cc("mST" + sfx), cc("reset" + sfx)
            mSb = mS.unsqueeze(1).to_broadcast([128, 4, 128])
            mIb = mI.unsqueeze(1).to_broadcast([128, 4, 128])
            mSTb = mST.unsqueeze(1).to_broadcast([128, 4, 128])
            xt = x_t[0]
            C.dma(SP, xt, x_src)
            C.dma(POOL, x_bf, x_src)
            ck(1)
            for kc in range(8):
                C.tr(psb(7)[:, kc * 128:(kc + 1) * 128], x_bf[:, kc * 128:(kc + 1) * 128], ident_bf)
            C.cp(ACT, xT, psb(7).rearrange("p (a t) -> p a t", a=8))
            ck(2)
            for c in range(22):
                if c < 14:
                    bank, slot = c // 4, c % 4
                else:
                    bank, slot = 4 + (c - 14) // 4, (c - 14) % 4
                for kc in range(8):
                    C.mm(ps[bank][:, slot * 128:(slot + 1) * 128], w_in_bf[:, kc, c * 128:(c + 1) * 128].k(kc),
                         xT[:, kc, :], start=(kc == 0), stop=(kc == 7))
            for j, bank in ((0, 6), (1, 7)):
                c0 = RWP + 1024 + j * 512
                for kc in range(8):
                    C.mm(ps[bank], xT[:, kc, :], w_in_bf[:, kc, c0:c0 + 512].k(kc), start=(kc == 0), stop=(kc == 7))
            ck(3)
            for bnk in range(3):
                C.cp(ACT, pr_[:, bnk * 4:(bnk + 1) * 4, 1:129], psv(bnk, 4))
            C.cp(ACT, pr_[:, 12:14, 1:129], psv(3, 4)[:, 0:2, :])
            prev, cur = pr_[:, :, 0:128], pr_[:, :, 1:129]
            if not sample:
                C.cp(POOL, pr_[:, :, 0:1], plast.unsqueeze(2))
                C.cp(POOL, plast.unsqueeze(2), pr_[:, :, 128:129])
            else:
                C.memset(POOL, pr_[:, :, 0:1], 0.0)
                C.rec(POOL, lambda e: e.memset(_ap(fence_t), 0.0),
                      [w_in_bf[:, kc, :].k(kc) for kc in range(8)] + ov_bufs
                      + [S0T32[:, b, :, :].k(b) for b in range(NB)] + [S0Tb[:, b, :, :].k(b) for b in range(NB)], [])
                for t_ in EW + ER + EQ:
                    C.memset(POOL, t_, 0.0)
                C.dma(SP, ssh_tm, ssh)
                for c in range(14):
                    C.tr(ps[0][:, c * NB:(c + 1) * NB], ssh_tm[:, c * 128:(c + 1) * 128], cc("ident")[0:NB, 0:NB])
                C.cp(DVE, sshT, ps[0][:, 0:14 * NB].rearrange("p (c b) -> p c b", c=14))
            for eng_, c0, c1 in ((DVE, 0, 8), (POOL, 8, 14)):
                C.tt(eng_, xs[:, c0:c1, :].k(c0), prev[:, c0:c1, :], cur[:, c0:c1, :], ALU.subtract)
                C.tt(eng_, xs[:, c0:c1, :].k(c0), xs[:, c0:c1, :].k(c0), bc3(mu14[:, c0:c1], 128), ALU.mult)
                C.tt(eng_, xs[:, c0:c1, :].k(c0), xs[:, c0:c1, :].k(c0), cur[:, c0:c1, :], ALU.add)
            C.rec(POOL, lambda e: e.memset(_ap(fence2), 0.0), [xs, fence2], [xs[:, 0:8, :].k(0), xs[:, 8:14, :].k(8)])
            if sample:
                cur4 = cur.rearrange("p c (b t) -> p c b t", t=TS)
                xs4 = xs.rearrange("p c (b t) -> p c b t", t=TS)
                cur0, xs0 = cur4[:, :, :, 0], xs4[:, :, :, 0]
                C.tt(POOL, xs0, sshT, cur0, ALU.subtract)
                C.tt(POOL, xs0, xs0, bc3(mu14, NB), ALU.mult)
                C.tt(POOL, xs0, xs0, cur0, ALU.add)
                C.cp(POOL, lastp, cur4[:, :, :, TS - 1])
                for c in range(14):
                    C.tr(ps[c // 4][0:NB, (c % 4) * 128:(c % 4 + 1) * 128], lastp[:, c, :], cc("ident"))
                for bnk in range(4):
                    w_ = 512 if bnk < 3 else 256
                    C.cp(ACT, ssh_tm[:, bnk * 512:bnk * 512 + w_], ps[bnk][0:NB, 0:w_])
                C.dma(SP, shs, ssh_tm, out_final=True)
            r_ = xs[:, 0:4, :]; k_ = xs[:, 4:8, :]; v_ = xs[:, 8:12, :]

            ck(4)
            sig, kq, fgl, bcs, eb, enb, ebl, sq_, eg = T[1], T[4], T[0], T[2], T[3], T[5], T[6], T[7], T[8]
            C.cp(ACT, i_tm, ps[6])
            C.act(sig, ps[5], AF.Sigmoid)
            C.act(sq_, ps[4], AF.Sigmoid)
            C.act(eg, ps[7], AF.Sigmoid)
            C.act(z12[0:64, :], xs[0:64, 12, :], AF.Tanh)
            C.act(sg_bf, xs[:, 13, :], AF.Sigmoid)
            C.cp(DVE, z12[64:128, :], xs[64:128, 12, :])
            for pr in range(4):
                sl = slice(pr * 128, (pr + 1) * 128)
                C.mm(ps[0][:, sl], WA[0:64, sl].k("lo"), z12[0:64, :], start=True, stop=False)
                C.mm(ps[0][:, sl], w0row[0:1, sl], ones_row[0:1, :], start=False, stop=True)
            for pr in range(4):
                sl = slice(pr * 128, (pr + 1) * 128)
                C.mm(ps[1][:, sl], WA[64:128, sl].k("hi"), z12[64:128, :], start=True, stop=False)
                C.mm(ps[1][:, sl], a0row[0:1, sl], ones_row[0:1, :], start=False, stop=True)
            C.mm(ps[3], sg_bf, g1u_bf)
            sw, alr, cumS, gex, gin, ginv, glast, kkk, tq, k2 = T[0], T[1], T[2], T[3], T[4], T[5], T[6], T[7], T[8], T[9]
            C.tt(DVE, v3(kq), v3(sig), bc3(oml4, 128), ALU.mult)
            C.tt(DVE, v3(fgl), v3(kq), bc3(lb4, 128), ALU.add)
            C.tt(POOL, v3(kq), bc3(oml4, 128), v3(kq), ALU.subtract)
            C.tt(DVE, sq_, ps[4], sq_, ALU.mult)
            C.tt(DVE, gs_t, ps[7], eg, ALU.mult)
            sw2, alr2 = T[9], T[8]
            C.act(sw2, ps[0], AF.Sigmoid)
            C.act(alr2, ps[1], AF.Sigmoid)
            C.cp(ACT, g_tm, ps[3])
            C.act(fgl, fgl, AF.Ln)
            for h in range(4):
                C.scan(bcs[:, h * 128:(h + 1) * 128], reset, fgl[:, h * 128:(h + 1) * 128])
            C.act(eb, bcs, AF.Exp)
            C.act(enb, bcs, AF.Exp, scale=-1.0)
            bc4 = bcs.rearrange("p (a n c) -> p a n c", a=4, n=nch)
            C.tt(POOL, ebl.rearrange("p (a n c) -> p a n c", a=4, n=nch),
                 bc4[:, :, :, Cn - 1:Cn].to_broadcast([128, 4, nch, Cn]), bc4, ALU.subtract)
            C.act(ebl, ebl, AF.Exp)
            C.cp(POOL, decH[:, :, 0:nch].unsqueeze(3),
                 eb.rearrange("p (a n c) -> p a n c", a=4, n=nch)[:, :, :, Cn - 1:Cn])
            C.tt(DVE, qTb, v3(sq_), v3(eb), ALU.mult)
            C.tt(POOL, kTb, v3(kq), v3(enb), ALU.mult)
            C.tt(DVE, khTb, v3(kq), v3(ebl), ALU.mult)

            sw, alr = sw2, alr2
            cumS, gex, gin, ginv, glast, kkk, tq, k2 = T[2], T[3], T[4], T[5], T[6], T[7], T[0], T[1]
            for pr in range(4):
                C.scan(cumS[:, pr * 128:(pr + 1) * 128], reset, sw[:, pr * 128:(pr + 1) * 128])
            C.tt(POOL, gex, cumS, sw, ALU.subtract)
            C.act(gex, gex, AF.Exp, scale=CDEC)
            C.act(gin, cumS, AF.Exp, scale=CDEC)
            C.act(ginv, cumS, AF.Exp, scale=-CDEC)
            cs4 = cumS.rearrange("p (a n c) -> p a n c", a=4, n=nch)
            C.tt(POOL, glast.rearrange("p (a n c) -> p a n c", a=4, n=nch),
                 cs4[:, :, :, Cn - 1:Cn].to_broadcast([128, 4, nch, Cn]), cs4, ALU.subtract)
            C.act(glast, glast, AF.Exp, scale=CDEC)
            C.cp(POOL, gC[:, :, 0:nch].unsqueeze(3),
                 gin.rearrange("p (a n c) -> p a n c", a=4, n=nch)[:, :, :, Cn - 1:Cn])
            C.tt(POOL, v3(kkk), k_, bc3(kk4, 128), ALU.mult)
            C.act(tqb, kkk, AF.Square)
            for pr in range(4):
                sl = slice(pr * 128, (pr + 1) * 128)
                C.mm(ps[2][:, sl], bdones_bf, tqb[:, sl])
            C.act(tq, ps[2], AF.Ln, bias=1e-24)
            C.act(tq, tq, AF.Exp, scale=-0.5)
            C.tt(DVE, kkk, kkk, tq, ALU.mult)
            C.stt(DVE, v3(k2), v3(alr), -1.0, bc3(ka4, 128), ALU.add, ALU.mult)
            C.stt(DVE, v3(k2), v3(k2), 1.0, k_, ALU.add, ALU.mult)
            C.tt(DVE, alr, kkk, alr, ALU.mult)
            b_ = alr
            C.tt(DVE, AR[:, :, 1, :], r_, v3(gin), ALU.mult)
            C.stt(DVE, AR[:, :, 0, :], v3(kkk), -1.0, v3(gex), ALU.mult, ALU.mult)
            C.tt(POOL, bT, v3(b_), v3(ginv), ALU.mult)
            C.tt(POOL, kT, v3(k2), v3(ginv), ALU.mult)
            C.tt(DVE, bhT, v3(b_), v3(glast), ALU.mult)
            C.tt(POOL, khT, v3(k2), v3(glast), ALU.mult)
            C.cp(POOL, vT, v_)
            C.tt(DVE, v3(tqb), r_, v3(k2), ALU.mult)
            for pr in range(4):
                C.mm(ps[1][:, pr * 2:(pr + 1) * 2], tqb[:, pr * 128:(pr + 1) * 128], RKsel[:, pr, :])
            C.cp(DVE, bon8, ps[1][:, 0:8])
            for (src, bank, half) in ((lambda pr: AR[:, pr, 0, :], 4, 0), (lambda pr: bhT[:, pr, :], 4, 1),
                                      (lambda pr: khT[:, pr, :], 5, 0), (lambda pr: vT[:, pr, :], 5, 1)):
                for pr in range(4):
                    o0 = half * 512 + pr * 128
                    C.tr(psb(bank)[:, o0:o0 + 128], src(pr), ident_bf)
            C.cp(ACT, A_tm, psb(4)[:, 0:512]); C.cp(ACT, Bh_tm, psb(4)[:, 512:1024])
            C.cp(ACT, Kh_tm, psb(5)[:, 0:512]); C.cp(ACT, V_tm, psb(5)[:, 512:1024])

            ck(6)
            for h in range(4):
                C.tr(psb(6)[:, h * 128:(h + 1) * 128], khTb[:, h, :], ident_bf)
            C.cp(ACT, khat_tm, psb(6)[:, 0:512])
            for h in range(4):
                C.mm(ps[6][:, h * 128:(h + 1) * 128], kTb[:, h, :], qTb[:, h, :])
            C.tt(DVE, attT, psv(6, 4), mIb, ALU.mult)
            if not sample:
                for h in range(4):
                    hsl = slice(h * 128, (h + 1) * 128)
                    C.mm(ps[7][:, hsl], attT[:, h, :], i_tm[:, hsl], start=True, stop=False)
                    C.mm(ps[7][:, hsl], qTb[:, h, :], SH_bf[:, h, :], start=False, stop=True)
                for h in range(4):
                    hsl = slice(h * 128, (h + 1) * 128)
                    C.mm(ps[6][:, hsl], khat_tm[:, hsl], i_tm[:, hsl])
                C.tt(POOL, SHtmp, SH, bc3(decH[:, :, 0], 128), ALU.mult)
                C.tt(DVE, SH, SHtmp, psv(6, 4), ALU.add)
                C.cp(POOL, SH_bf, SH)
            else:
                for h in range(4):
                    hsl = slice(h * 128, (h + 1) * 128)
                    C.mm(ps[7][:, hsl], attT[:, h, :], i_tm[:, hsl], start=(h == 0), stop=False)
                for b in range(NB):
                    s0 = S0h[b % 2]
                    csl = slice(b * TS, (b + 1) * TS)
                    C.dma(SP, s0, shg[b].rearrange("h k v -> k h v"))
                    C.cp(ACT, S0hb, s0)
                    eq = EQ[b % 2]
                    C.cp(POOL, eq[:, :, csl], qTb[:, :, csl])
                    for h in range(4):
                        C.mm(ps[7][:, h * 128:(h + 1) * 128], eq[:, h, :], S0hb[:, h, :], start=False, stop=False)
                    C.memset(POOL, eq[:, :, csl], 0.0)
                    C.ts(DVE, khb, khat_tm, cc("cm")[:, b:b + 1], ALU.mult)
                    bank = 5 + b % 2
                    for h in range(4):
                        hsl = slice(h * 128, (h + 1) * 128)
                        C.mm(ps[bank][:, hsl], khb[:, hsl], i_tm[:, hsl])
                    C.tt(DVE, s0, s0, bc3(decH[:, :, b], 128), ALU.mult)
                    C.tt(DVE, s0, s0, psv(bank, 4), ALU.add)
                    C.dma(SP, hgs[b].rearrange("h k v -> k h v"), s0, out_final=True)
            osq = T[4]
            C.act(osq, ps[7], AF.Square)
            C.red(DVE, s4, v3(osq))
            C.act(rr4, s4, AF.Ln, scale=1.0 / 128, bias=RMS_EPS)
            C.act(rr4, rr4, AF.Exp, scale=-0.5)
            C.tt(DVE, v3(osq), psv(7, 4), bc3(rr4, 128), ALU.mult)
            C.tt(POOL, osq, osq, hgw, ALU.mult)
            C.tt(POOL, o_all[:, 512:1024], osq, gs_t, ALU.mult)

            ck(7)
            if sample:
                for g in range(4):
                    sa = scrA[g % 2]
                    nat = sa[0:64, :].rearrange("p (b n) -> p b n", b=4)
                    C.dma(SP, nat.rearrange("p b (h j) -> p b h j", h=8),
                          srw[g * 4:(g + 1) * 4].rearrange("b h v j -> v b h j"))
                    for bb in range(4):
                        b = g * 4 + bb
                        bank = b % 2
                        for pr in range(4):
                            C.tr(ps[bank][:, (bb % 2) * 256 + pr * 64:(bb % 2) * 256 + (pr + 1) * 64],
                                 nat[:, bb, pr * 128:(pr + 1) * 128], cc("ident")[0:64, 0:64])
                        C.cp(ACT, S0T32[:, b, :, :].k(b),
                             ps[bank][:, (bb % 2) * 256:(bb % 2) * 256 + 256].rearrange("p (a v) -> p a v", a=4))
                        C.cp(DVE, S0Tb[:, b, :, :].k(b), S0T32[:, b, :, :].k(b))
            for hg in range(2):
                for i, h in enumerate(heads_of[hg]):
                    pr, hh = h // 2, h % 2
                    rows = slice(64 * hh, 64 * hh + 64)
                    sl = slice(i * 128, (i + 1) * 128)
                    C.mm(ps[0][:, sl], bT[rows, pr, :], AR[rows, pr, 0, :])
                    C.mm(ps[1][:, sl], bT[rows, pr, :], AR[rows, pr, 1, :])
                    C.mm(ps[2][:, sl], kT[rows, pr, :], AR[rows, pr, 0, :])
                    C.mm(ps[3][:, sl], kT[rows, pr, :], AR[rows, pr, 1, :])
                    C.mm(ps[4][:, sl], AR[rows, pr, 0, :], bT[rows, pr, :])
                hs = slice(4 * hg, 4 * hg + 4)
                C.tt(DVE, PT[0][:, hs, 0:128], psv(0, 4), mSb, ALU.mult)
                C.tt(DVE, NrbT[:, hs, :], psv(1, 4), mIb, ALU.mult)
                C.tt(DVE, AkT[:, hs, :], psv(2, 4), mSb, ALU.mult)
                C.tt(DVE, NrkT[:, hs, :], psv(3, 4), mIb, ALU.mult)
                C.tt(DVE, RR[0][:, hs, :], psv(4, 4), mSTb, ALU.mult)
                C.tt(POOL, PT[1][:, hs, 128:256], PT[0][:, hs, 0:128], identb4, ALU.add)
            ck(71)
            for k in range(L):
                cur, nxt = k % 2, 1 - k % 2
                ck(72 + k)
                for hg in range(2):
                    b0 = 3 * hg
                    for i, h in enumerate(heads_of[hg]):
                        pb = ps[b0 + i // 2]
                        o = (i % 2) * 256
                        if k == 0:
                            C.mm(pb[:, o:o + 128], RR[cur][:, h, :], PT[cur][:, h, 0:128])
                            C.mm(ps[b0 + 2][:, i * 128:(i + 1) * 128], PT[cur][:, h, 0:128], RR[cur][:, h, :])
                        elif k < L - 1:
                            C.mm(pb[:, o:o + 256], RR[cur][:, h, :], PT[cur][:, h, :])
                            C.mm(ps[b0 + 2][:, i * 128:(i + 1) * 128], PT[cur][:, h, 0:128], RR[cur][:, h, :])
                        else:
                            C.mm(pb[:, o + 128:o + 256], RR[cur][:, h, :], PT[cur][:, h, 128:256])
                    for j in range(2):
                        hs2 = slice(4 * hg + 2 * j, 4 * hg + 2 * j + 2)
                        pv = ps[b0 + j].rearrange("p (h c) -> p h c", h=2)
                        if k < L - 1:
                            C.cp(ACT, PT[nxt][:, hs2, 0:128], pv[:, :, 0:128])
                        if k >= 1:
                            dst = TTf[:, hs2, :] if k == L - 1 else PT[nxt][:, hs2, 128:256]
                            C.tt(DVE, dst, pv[:, :, 128:256], PT[cur][:, hs2, 128:256], ALU.add)
                    if k < L - 1:
                        C.cp(DVE, RR[nxt][:, 4 * hg:4 * hg + 4, :], psv(b0 + 2, 4))
            ck(8)
            for h in range(8):
                C.mm(ps[6][:, h * 64:(h + 1) * 64], AkT[:, h, :], V_tm[:, h * 64:(h + 1) * 64])
            C.cp(ACT, Z_tm, ps[6])
            for h in range(8):
                pr = h // 2
                C.mm(ps[h // 4][:, (h % 4) * 128:(h % 4 + 1) * 128], A_tm[:, pr * 128:(pr + 1) * 128], TTf[:, h, :])
            for b in range(2):
                pv = ps[b].rearrange("p (q e t) -> p q e t", q=2, e=2)
                C.cp(ACT, W1T[0:64, 2 * b:2 * b + 2, :], pv[0:64, :, 0, :])
                C.cp(ACT, W1T[64:128, 2 * b:2 * b + 2, :], pv[64:128, :, 1, :])
            for h in range(8):
                pr, hh = h // 2, h % 2
                rows = slice(64 * hh, 64 * hh + 64)
                hsl = slice(h * 64, (h + 1) * 64)
                if not sample:
                    C.mm(ps[3][:, hsl], TTf[:, h, :], Z_tm[:, hsl], start=True, stop=False)
                    C.mm(ps[3][:, hsl], W1T[rows, pr, :], ST_bf[rows, pr, :], start=False, stop=True)
                else:
                    C.mm(ps[3][:, hsl], TTf[:, h, :], Z_tm[:, hsl], start=(h == 0), stop=False)
            if sample:
                for b in range(NB):
                    ew = EW[b % 2]
                    csl = slice(b * TS, (b + 1) * TS)
                    C.cp(POOL, ew[:, :, csl], W1T[:, :, csl])
                    for h in hh_order:
                        pr, hh = h // 2, h % 2
                        rows = slice(64 * hh, 64 * hh + 64)
                        C.mm(ps[3][:, h * 64:(h + 1) * 64], ew[rows, pr, :], S0Tb[rows, b, pr, :].k(b), start=False, stop=False)
                    C.memset(POOL, ew[:, :, csl], 0.0)
            C.cp(ACT, U_tm, ps[3])
            for h in range(8):
                pr, hh = h // 2, h % 2
                rows = slice(64 * hh, 64 * hh + 64)
                hsl = slice(h * 64, (h + 1) * 64)
                C.mm(ps[4][:, hsl], NrbT[:, h, :], U_tm[:, hsl], start=(h == 0 or not sample), stop=False)
                C.mm(ps[4][:, hsl], NrkT[:, h, :], V_tm[:, hsl], start=False, stop=False)
                if not sample:
                    C.mm(ps[4][:, hsl], AR[rows, pr, 1, :], ST_bf[rows, pr, :], start=False, stop=True)
            if sample:
                for b in range(NB):
                    er = ER[b % 2]
                    csl = slice(b * TS, (b + 1) * TS)
                    C.cp(POOL, er[:, :, csl], AR[:, :, 1, csl])
                    for h in hh_order:
                        pr, hh = h // 2, h % 2
                        rows = slice(64 * hh, 64 * hh + 64)
                        C.mm(ps[4][:, h * 64:(h + 1) * 64], er[rows, pr, :], S0Tb[rows, b, pr, :].k(b), start=False, stop=False)
                    C.memset(POOL, er[:, :, csl], 0.0)
                i64b = cc("i64s").unsqueeze(1).to_broadcast([128, 4, 64])
                for b in range(NB):
                    bank = 5 + b % 2
                    C.ts(DVE, Ub, U_tm, cc("cm")[:, b:b + 1], ALU.mult)
                    C.ts(DVE, Vb, V_tm, cc("cm")[:, b:b + 1], ALU.mult)
                    C.tt(DVE, Dg, i64b, bc3(gC[:, :, b], 64), ALU.mult)
                    for h in range(8):
                        hsl = slice(h * 64, (h + 1) * 64)
                        C.mm(ps[bank][0:64, hsl], Ub[:, hsl], Bh_tm[:, hsl], start=(h == 0), stop=False)
                        C.mm(ps[bank][0:64, hsl], Vb[:, hsl], Kh_tm[:, hsl], start=False, stop=False)
                    for h in hh_order:
                        pr, hh = h // 2, h % 2
                        rows = slice(64 * hh, 64 * hh + 64)
                        C.mm(ps[bank][0:64, h * 64:(h + 1) * 64], S0T32[rows, b, pr, :].k(b), Dg[rows, pr, :], start=False, stop=False)
                    C.cp(ACT, Sn[0:64, :], ps[bank][0:64, :])
                    C.dma(SP, rws[b].rearrange("h v j -> v h j"), Sn[0:64, :].rearrange("p (h j) -> p h j", h=8), out_final=True)
            if not sample:
                for pr in range(4):
                    psl = slice(pr * 128, (pr + 1) * 128)
                    C.mm(ps[5][:, psl], Bh_tm[:, psl], U_tm[:, psl], start=True, stop=False)
                    C.mm(ps[5][:, psl], Kh_tm[:, psl], V_tm[:, psl], start=False, stop=True)
                C.tt(POOL, STtmp, ST, bc3(gC[:, :, 0], 64), ALU.mult)
                p5 = psv(5, 4)
                C.tt(DVE, ST[0:64, :, :], STtmp[0:64, :, :], p5[0:64, :, 0:64], ALU.add)
                C.tt(DVE, ST[64:128, :, :], STtmp[64:128, :, :], p5[64:128, :, 64:128], ALU.add)
                C.cp(POOL, ST_bf, ST)
            ysq, tmp2 = T[9], T[1]
            y3 = ps[4].rearrange("p (h v) -> p h v", h=8)
            yv = ysq.rearrange("p (h v) -> p h v", h=8)
            C.red(DVE, st16[:, 0:8], y3)
            C.act(ysq, ps[4], AF.Square)
            C.red(DVE, st16[:, 8:16], yv)
            C.ts(DVE, m8, st16[:, 0:8], 1.0 / 64, ALU.mult)
            C.tt(DVE, r8, m8, m8, ALU.mult)
            C.stt(DVE, r8, st16[:, 8:16], 1.0 / 64, r8, ALU.mult, ALU.subtract)
            C.act(r8, r8, AF.Ln, bias=GN_EPS)
            C.act(r8, r8, AF.Exp, scale=-0.5)
            C.tt(DVE, yv, y3, bc3(m8, 64), ALU.subtract)
            C.tt(POOL, yv, yv, bc3(r8, 64), ALU.mult)
            C.tt(POOL, ysq, ysq, lnxw, ALU.mult)
            C.tt(POOL, ysq, ysq, lnxb, ALU.add)
            C.tt(DVE, tmp2.rearrange("p (h v) -> p h v", h=8), V_tm.rearrange("p (h v) -> p h v", h=8),
                 bc3(bon8, 64), ALU.mult)
            C.tt(DVE, ysq, ysq, tmp2, ALU.add)
            C.tt(DVE, o_all[:, 0:512], ysq, g_tm, ALU.mult)

            ck(9)
            for mc in range(8):
                C.tr(psb(6)[:, mc * 128:(mc + 1) * 128], o_all[:, mc * 128:(mc + 1) * 128], ident_bf)
            C.cp(ACT, oT, psb(6).rearrange("p (a t) -> p a t", a=8))
            for half in range(2):
                for mc in range(8):
                    C.mm(ps[half], oT[:, mc, :], w_out_bf[:, mc, half * 512:(half + 1) * 512].k(mc),
                         start=(mc == 0), stop=(mc == 7))
            for half in range(2):
                hsl = slice(half * 512, (half + 1) * 512)
                C.stt(DVE, h1pre[:, hsl], xt[:, hsl], ALPHA, ps[half], ALU.mult, ALU.add)
            layernorm(h1pre, h1pre, LN_EPS)
            C.dma(SP, h1_dst, h1pre)

        stopped = False
        try:
            ck(0)
            for ti in range(NT):
                mixer_tile(ti, xp[ti * 128:(ti + 1) * 128, :], V(h1scr[ti * 128:(ti + 1) * 128, :], ("h1scr", ti)), False)
            ck(10)

            with nc.allow_non_contiguous_dma(reason="tiny state vector"):
                C.dma(SP, shp.rearrange("(c p) -> p c", p=128), plast, out_final=True)
            C.dma(SP, hgp.rearrange("h k v -> k h v"), SH, out_final=True)
            identf = cc("ident")
            for pr in range(4):
                C.tr(ps[2][0:64, pr * 128:(pr + 1) * 128], ST[:, pr, :], identf)
            rwo = T[0]
            C.cp(DVE, rwo[0:64, :], ps[2][0:64, :])
            C.dma(SP, rwp.rearrange("h v j -> v h j"), rwo[0:64, :].rearrange("p (h j) -> p h j", h=8), out_final=True)
            if SAMPLE:
                mixer_tile(NTP, xsm, V(h1scr[NTP * 128:(NTP + 1) * 128, :], ("h1scr", NTP)), True)
            if DBG:
                C.dma(SP, dbg_d, dbg, out_final=True)

            ck(11)
            P.barrier()
            arena.off = phase_mark
            w_up_bf = sb([128, 8, DFF], BF16, "w_up_bf")
            w_dn_bf = sb([128, 32, D], BF16, "w_dn_bf")
            for cb in range(8):
                C.dma(POOL, w_up_bf[:, :, cb * 512:(cb + 1) * 512].k(cb),
                      w_up[:, cb * 512:(cb + 1) * 512].rearrange("(kc p) n -> p kc n", p=128))
            C.dma(SP, lng, row(ln2_g).partition_broadcast(128))
            C.dma(SP, lnb, row(ln2_b).partition_broadcast(128))
            for fc in range(32):
                C.dma(POOL, w_dn_bf[:, fc, :].k(fc), w_down[fc * 128:(fc + 1) * 128, :])
            upT = sb([128, 32, 512], BF16, "upT")
            h1T = sb([128, 8, 512], BF16, "h1T")
            h1b = [sb([128, D], BF16, f"h1b{i}") for i in range(2)]
            h1r = [sb([128, D], F32, f"h1r{i}") for i in range(2)]
            rl = [sb([128, 512], F32, f"rl{i}") for i in range(2)]
            pre2 = sb([128, D], F32, "pre2")
            outb = [sb([128, D], F32, f"outb{i}") for i in range(2)]
            ntiles = NT + (1 if SAMPLE else 0)
            tiles = list(range(NT)) + ([NTP] if SAMPLE else [])
            groups = [tiles[i:i + 4] for i in range(0, NT, 4)]
            if SAMPLE:
                groups.append([NTP])
            gcount = 0
            tcount = 0
            for grp in groups:
                ng = len(grp)
                W = ng * 128
                for gi, tix in enumerate(grp):
                    hb = h1b[(tcount + gi) % 2]
                    C.dma(POOL, hb, V(h1scr[tix * 128:(tix + 1) * 128, :], ("h1scr", tix)))
                    bank = 6 + (gi % 2)
                    for kc in range(8):
                        C.tr(psb(bank)[:, kc * 128:(kc + 1) * 128], hb[:, kc * 128:(kc + 1) * 128], ident_bf)
                    C.cp(ACT, h1T[:, :, gi * 128:(gi + 1) * 128], psb(bank).rearrange("p (a t) -> p a t", a=8))
                for fc in range(32):
                    bank = fc % 2
                    for kc in range(8):
                        C.mm(ps[bank][:, 0:W], w_up_bf[:, kc, fc * 128:(fc + 1) * 128].k(fc // 4), h1T[:, kc, 0:W],
                             start=(kc == 0), stop=(kc == 7))
                    r = rl[fc % 2]
                    C.act(r[:, 0:W], ps[bank][:, 0:W], AF.Relu)
                    C.tt(POOL if fc % 2 else DVE, upT[:, fc, 0:W], r[:, 0:W], r[:, 0:W], ALU.mult)
                for gi, tix in enumerate(grp):
                    hr = h1r[(tcount + gi) % 2]
                    C.dma(SP, hr, V(h1scr[tix * 128:(tix + 1) * 128, :], ("h1scr", tix)))
                    for half in range(2):
                        bank = 2 + ((gi * 2 + half) % 4)
                        for fc in range(32):
                            C.mm(ps[bank], upT[:, fc, gi * 128:(gi + 1) * 128], w_dn_bf[:, fc, half * 512:(half + 1) * 512].k(fc),
                                 start=(fc == 0), stop=(fc == 31))
                        hsl = slice(half * 512, (half + 1) * 512)
                        C.stt(DVE, pre2[:, hsl], hr[:, hsl], ALPHA, ps[bank], ALU.mult, ALU.add)
                    ob = outb[(tcount + gi) % 2]
                    layernorm(pre2, ob, LN_EPS)
                    dst = ys if tix == NTP else yp[tix * 128:(tix + 1) * 128, :]
                    C.dma(SP, dst, ob, out_final=True)
                tcount += ng
                gcount += 1

        except _Stop:
            C.dma(SP, shp.rearrange("(c p) -> p c", p=128), plast, out_final=True)

        sems = {e: es.enter_context(nc.semaphore(f"s_{e}")) for e in ENGINES}
        rings = {e: [es.enter_context(nc.semaphore(f"r_{e}{i}")) for i in range(n)] for e, n in DMA_RING.items()}
        P.prepare(sems, rings)
        build.stats = dict(P.stats)
        build.arena_peak = arena.peak
        with nc.allow_low_precision(reason="bf16 matmul operands, fp32 accumulation"), \
                nc.allow_non_contiguous_dma(reason="tiny per-channel vectors / state layouts"), \
                nc.Block() as block:
            block.tensor(lambda eng: P.emit_engine(PE, eng))
            block.scalar(lambda eng: P.emit_engine(ACT, eng))
            block.vector(lambda eng: P.emit_engine(DVE, eng))
            block.gpsimd(lambda eng: P.emit_engine(POOL, eng))
            block.sync(lambda eng: P.emit_engine(SP, eng))
    return nc


IN_NAMES = ["w_in", "shift_mu", "w0", "w1u", "a0", "a1u", "g1u", "k_k", "k_a", "r_k", "ln_x_w", "ln_x_b",
            "lb_logits", "hg_norm_w", "w_out", "ln1_g", "ln1_b", "w_up", "w_down", "ln2_g", "ln2_b"]


def make_in_maps(inputs, n_cores=8):
    f = lambda a: np.ascontiguousarray(np.asarray(a, dtype=np.float32))
    shared = {}
    for k in IN_NAMES:
        a = f(inputs[k])
        a = a[0] if k != "lb_logits" else a
        if k == "r_k":
            a = a.reshape(-1)
        shared[k] = np.ascontiguousarray(a)
    shared["consts"] = CONSTS
    maps = []
    for c in range(n_cores):
        m = dict(shared)
        m["xp"] = f(inputs["x_prompt"][c])
        m["xs"] = f(inputs["x_sample"][c * NB:(c + 1) * NB]).reshape(NB * TS, D)
        m["srw"] = f(inputs["state_rwkv"][0, c * NB:(c + 1) * NB])
        m["shg"] = f(inputs["state_hgrn"][0, c * NB:(c + 1) * NB])
        m["ssh"] = f(inputs["state_shift"][0, c * NB:(c + 1) * NB])
        maps.append(m)
    return maps


_NC_CACHE = {}


def kernel(**inputs):
    if "nc" not in _NC_CACHE:
        _NC_CACHE["nc"] = build()
    nc = _NC_CACHE["nc"]
    maps = make_in_maps(inputs)
    res = run_bass_kernel_spmd(nc, maps, core_ids=list(range(8)))
    R = res.results
    y_prompt = np.stack([R[c]["yp"] for c in range(8)]).astype(np.float32)
    y_sample = np.concatenate([R[c]["ys"].reshape(NB, TS, D) for c in range(8)]).astype(np.float32)
    rw_p = np.stack([R[c]["rwp"] for c in range(8)])[None].astype(np.float32)
    rw_s = np.concatenate([R[c]["rws"] for c in range(8)])[None].astype(np.float32)
    hg_p = np.stack([R[c]["hgp"] for c in range(8)])[None].astype(np.float32)
    hg_s = np.concatenate([R[c]["hgs"] for c in range(8)])[None].astype(np.float32)
    sh_p = np.stack([R[c]["shp"] for c in range(8)])[None].astype(np.float32)
    sh_s = np.concatenate([R[c]["shs"] for c in range(8)])[None].astype(np.float32)
    return (y_prompt, y_sample, rw_p, rw_s, hg_p, hg_s, sh_p, sh_s)
```

```python
import numpy as np
import concourse.bass as bass
import concourse.mybir as mybir
from concourse.bass_utils import run_bass_kernel_spmd

F32 = mybir.dt.float32
BF16 = mybir.dt.bfloat16
AF = mybir.ActivationFunctionType
ALU = mybir.AluOpType
AX = mybir.AxisListType

DEBUG_LINES = False
LINE_OF = {}
PE, ACT, DVE, POOL, SP = "pe", "act", "dve", "pool", "sp"
ENGINES = (PE, ACT, DVE, POOL, SP)
DMA_RING = {SP: 12, ACT: 4, POOL: 8}


class Res:
    __slots__ = ("name", "last_w", "readers")

    def __init__(self, name):
        self.name = name
        self.last_w = None
        self.readers = []


class Op:
    __slots__ = ("eng", "fn", "deps", "is_dma", "signal", "idx", "dma_no", "extra_wait", "rg")

    def __init__(self, eng, fn, is_dma):
        self.eng = eng
        self.fn = fn
        self.deps = set()
        self.is_dma = is_dma
        self.signal = False
        self.idx = None
        self.dma_no = None
        self.extra_wait = None
        self.rg = None


def _pe_inorder_ok(d, o):
    return d.rg is None or o.rg is None or d.rg == o.rg


class Prog:
    def __init__(self):
        self.ops = {e: [] for e in ENGINES}
        self.order = []
        self.n_dma = {e: 0 for e in ENGINES}
        self.dma_ops = {e: [] for e in ENGINES}
        self.out_dmas = []

    def op(self, eng, fn, reads=(), writes=(), dma=False, out=False):
        o = Op(eng, fn, dma)
        for r in reads:
            if r.last_w is not None:
                o.deps.add(r.last_w)
        for w in writes:
            if w.last_w is not None:
                o.deps.add(w.last_w)
            for rd in w.readers:
                o.deps.add(rd)
        for r in reads:
            r.readers.append(o)
        for w in writes:
            w.last_w = o
            w.readers = []
        if getattr(self, "barrier_left", None) and eng in self.barrier_left:
            self.barrier_left.discard(eng)
            o.deps.update(self.pending_barrier)
        o.deps.discard(o)
        if dma:
            o.dma_no = self.n_dma[eng]
            self.n_dma[eng] += 1
            self.dma_ops[eng].append(o)
            o.signal = True
            if out:
                self.out_dmas.append(o)
        self.ops[eng].append(o)
        self.order.append(o)
        return o

    def barrier(self):
        pend = []
        for e in ENGINES:
            comp = [o for o in self.ops[e] if not o.is_dma]
            if comp:
                pend.append(comp[-1])
            pend.extend(self.dma_ops[e][-DMA_RING.get(e, 0):] if e in DMA_RING else [])
        self.pending_barrier = pend
        self.barrier_left = set(ENGINES)

    def prepare(self, sems, rings):
        for o in self.order:
            for d in o.deps:
                if d.is_dma:
                    continue
                if d.eng == o.eng and d.eng == PE and not o.is_dma and _pe_inorder_ok(d, o):
                    continue
                d.signal = True
        sig = {}
        for e in ENGINES:
            c = 0
            for o in self.ops[e]:
                if o.is_dma:
                    R = len(rings[e])
                    sig[o] = (rings[e][o.dma_no % R], 16 * (o.dma_no // R + 1))
                elif o.signal:
                    c += 1
                    sig[o] = (sems[e], c)
        self.sig = sig
        self.rings = rings
        self.stats = {e: len(self.ops[e]) for e in ENGINES}

    def emit_engine(self, e, eng):
        sig, rings = self.sig, self.rings
        waited = {}

        def wait(sem, val):
            k = id(sem)
            if waited.get(k, 0) >= val:
                return
            waited[k] = val
            eng.wait_ge(sem, val)

        for o in self.ops[e]:
            for d in o.deps:
                if d not in sig:
                    continue
                if d.eng == e and not d.is_dma and not o.is_dma and e == PE and _pe_inorder_ok(d, o):
                    continue
                s, v = sig[d]
                wait(s, v)
            if o.is_dma:
                R = len(rings[e])
                if o.dma_no >= R:
                    wait(rings[e][o.dma_no % R], 16 * (o.dma_no // R))
            ins = o.fn(eng)
            if DEBUG_LINES:
                LINE_OF[str(getattr(getattr(ins, "ins", ins), "name", ins))] = o.extra_wait
            if o in sig:
                s, v = sig[o]
                ins.then_inc(s, 16 if o.is_dma else 1)
        if e == SP:
            for q in ENGINES:
                if q not in rings:
                    continue
                for o in self.dma_ops[q][-len(rings[q]):]:
                    s, v = sig[o]
                    wait(s, v)


class V:
    __slots__ = ("ap", "key")

    def __init__(self, ap, key):
        self.ap = ap
        self.key = key

    def __getitem__(self, idx):
        return V(self.ap[idx], self.key)

    def k(self, sub):
        return V(self.ap, (self.key, sub))

    @property
    def shape(self):
        return self.ap.shape

    def rearrange(self, *a, **kw):
        return V(self.ap.rearrange(*a, **kw), self.key)

    def unsqueeze(self, ax):
        return V(self.ap.unsqueeze(ax), self.key)

    def to_broadcast(self, shape):
        return V(self.ap.to_broadcast(list(shape)), self.key)

    def bitcast(self, dt):
        return V(self.ap.bitcast(dt), self.key)


def _ap(x):
    return x.ap if isinstance(x, V) else x


def _isnum(x):
    return isinstance(x, (int, float))


class Arena:
    def __init__(self, nc, es, nbytes):
        self.n2 = nbytes // 2
        self.t = es.enter_context(nc.sbuf_tensor("arena", [128, self.n2], BF16))
        self.off = 0
        self.cnt = 0
        self.peak = 0

    def alloc(self, shape, dt, name=None):
        shape = list(shape)
        esz = 4 if dt == F32 else 2
        n = int(np.prod(shape[1:]))
        nbytes = (n * esz + 3) // 4 * 4
        o = self.off
        assert o + nbytes <= self.n2 * 2, f"arena overflow allocating {name} {shape}: {o}+{nbytes} > {self.n2 * 2}"
        self.off += nbytes
        self.peak = max(self.peak, self.off)
        ap = self.t[0:shape[0], o // 2:o // 2 + nbytes // 2]
        if esz == 4:
            ap = ap.bitcast(F32)
        ap = ap[:, 0:n]
        if len(shape) == 3:
            ap = ap.rearrange("p (a b) -> p a b", a=shape[1])
        elif len(shape) == 4:
            ap = ap.rearrange("p (a b c) -> p a b c", a=shape[1], b=shape[2])
        self.cnt += 1
        return V(ap, name or f"t{self.cnt}")


class Ctx:
    def __init__(self, nc, P, arena):
        self.nc, self.P, self.arena = nc, P, arena
        self.res = {}

    def sb(self, shape, dt, name=None):
        return self.arena.alloc(shape, dt, name)

    def R(self, x):
        k = x.key if isinstance(x, V) else (x.name, None)
        r = self.res.get(k)
        if r is None:
            r = self.res[k] = Res(k)
        return r

    def rec(self, eng, fn, outs, ins, dma=False, out=False):
        reads = [self.R(i) for i in ins if i is not None and not _isnum(i)]
        writes = [self.R(o) for o in outs]
        writes += [r for r in reads if isinstance(r.name, str) and r.name.startswith("ps")]
        o_ = self.P.op(eng, fn, reads=reads, writes=writes, dma=dma, out=out)
        if DEBUG_LINES:
            import sys as _s
            f = _s._getframe(1)
            while f.f_code.co_name not in ("mixer_tile", "build", "layernorm") and f.f_back is not None:
                f = f.f_back
            o_.extra_wait = f.f_lineno
        return o_

    def mm(self, out, lhsT, rhs, start=True, stop=True):
        o, l, r = _ap(out), _ap(lhsT), _ap(rhs)
        op = self.rec(PE, lambda e: e.matmul(o, lhsT=l, rhs=r, start=start, stop=stop,
                                             skip_group_check=True), [out], [lhsT, rhs])
        kr = l.shape[0]
        if kr < 128:
            op.rg = (kr, l.base_partition())
        return op

    def tr(self, out, in_, ident):
        o, i, d = _ap(out), _ap(in_), _ap(ident)
        return self.rec(PE, lambda e: e.transpose(o, i, d), [out], [in_, ident])

    def act(self, out, in_, func, bias=None, scale=1.0):
        o, i = _ap(out), _ap(in_)
        kw = {}
        if bias is not None:
            kw["bias"] = _ap(bias)
        s = _ap(scale)
        return self.rec(ACT, lambda e: e.activation(out=o, in_=i, func=func, scale=s, **kw), [out],
                        [in_, bias, scale])

    def tt(self, eng, out, a, b, op):
        o, x, y = _ap(out), _ap(a), _ap(b)
        return self.rec(eng, lambda e: e.tensor_tensor(out=o, in0=x, in1=y, op=op), [out], [a, b])

    def ts(self, eng, out, a, s1, op0, s2=None, op1=None):
        o, x, v1, v2 = _ap(out), _ap(a), _ap(s1), _ap(s2)
        if op1 is None:
            f = lambda e: e.tensor_scalar(out=o, in0=x, scalar1=v1, scalar2=None, op0=op0)
        else:
            f = lambda e: e.tensor_scalar(out=o, in0=x, scalar1=v1, scalar2=v2, op0=op0, op1=op1)
        return self.rec(eng, f, [out], [a, s1, s2])

    def stt(self, eng, out, in0, scalar, in1, op0, op1):
        o, x, y, s = _ap(out), _ap(in0), _ap(in1), _ap(scalar)
        return self.rec(eng, lambda e: e.scalar_tensor_tensor(out=o, in0=x, scalar=s, in1=y, op0=op0, op1=op1),
                        [out], [in0, in1, scalar])

    def cp(self, eng, out, in_):
        o, i = _ap(out), _ap(in_)
        if eng == ACT:
            return self.rec(ACT, lambda e: e.activation(out=o, in_=i, func=AF.Copy), [out], [in_])
        return self.rec(eng, lambda e: e.tensor_copy(out=o, in_=i), [out], [in_])

    def recip(self, out, in_):
        o, i = _ap(out), _ap(in_)
        return self.rec(DVE, lambda e: e.reciprocal(out=o, in_=i), [out], [in_])

    def scan(self, out, d0, d1):
        o, a, b = _ap(out), _ap(d0), _ap(d1)
        return self.rec(DVE, lambda e: e.tensor_tensor_scan(out=o, data0=a, data1=b, initial=0.0,
                                                            op0=ALU.mult, op1=ALU.add), [out], [d0, d1])

    def red(self, eng, out, in_, op=ALU.add):
        o, i = _ap(out), _ap(in_)
        return self.rec(eng, lambda e: e.tensor_reduce(out=o, in_=i, axis=AX.X, op=op), [out], [in_])

    def memset(self, eng, out, val):
        o = _ap(out)
        return self.rec(eng, lambda e: e.memset(o, val), [out], [])

    def dma(self, eng, out, in_, out_final=False, extra_out=()):
        o, i = _ap(out), _ap(in_)
        return self.rec(eng, lambda e: e.dma_start(out=o, in_=i), [out, *extra_out], [in_], dma=True, out=out_final)

    def bn_stats(self, out, in_):
        o, i = _ap(out), _ap(in_)
        return self.rec(DVE, lambda e: e.bn_stats(out=o, in_=i), [out], [in_])

    def bn_aggr(self, out, in_):
        o, i = _ap(out), _ap(in_)
        return self.rec(DVE, lambda e: e.bn_aggr(out=o, in_=i), [out], [in_])


D = 1024
PJ = 3840
RWP = 1792
NTP = 16
NB = 16
TS = 8
DFF = 4096
ALPHA = 2.0 ** 0.25
CDEC = -float(np.exp(-0.5))
LN_EPS = 1e-5
GN_EPS = 64e-5
RMS_EPS = 1e-6
ARENA_BYTES = 211000


def make_consts():
    s = np.arange(128)[:, None]
    t = np.arange(128)[None, :]
    cols = {}
    cols["ident"] = (s == t)
    cols["mS_p"] = (s < t)
    cols["mI_p"] = (s <= t)
    cols["mST_p"] = (t < s)
    same = (s // TS) == (t // TS)
    cols["mS_s"] = (s < t) & same
    cols["mI_s"] = (s <= t) & same
    cols["mST_s"] = (t < s) & same
    cols["reset_p"] = np.broadcast_to(t != 0, (128, 128))
    cols["reset_s"] = np.broadcast_to((t % TS) != 0, (128, 128))
    cols["bdones"] = (s // 64) == (t // 64)
    cols["hsel"] = (s // 64) == np.arange(2)[None, :]
    cols["cm"] = (s // TS) == np.arange(NB)[None, :]
    cols["i64s"] = (s % 64) == np.arange(64)[None, :]
    off = {}
    parts = []
    o = 0
    for k, v in cols.items():
        v = np.asarray(v, np.float32)
        off[k] = (o, o + v.shape[1])
        o += v.shape[1]
        parts.append(v)
    return np.ascontiguousarray(np.concatenate(parts, axis=1)), off


CONSTS, COFF = make_consts()
NCONST = CONSTS.shape[1]


class _Stop(Exception):
    pass


def build(NT=NTP, SAMPLE=True, DBG=False, STAGE=99):
    from contextlib import ExitStack
    nc = bass.Bass("TRN2", target_bir_lowering=False)

    def din(name, shape):
        return nc.dram_tensor(name, list(shape), F32, kind="ExternalInput").ap()

    def dout(name, shape):
        return nc.dram_tensor(name, list(shape), F32, kind="ExternalOutput").ap()

    xp = din("xp", [NTP * 128, D]); xsm = din("xs", [128, D])
    srw = din("srw", [NB, 8, 64, 64]); shg = din("shg", [NB, 4, 128, 128]); ssh = din("ssh", [NB, RWP])
    w_in = din("w_in", [D, PJ]); shift_mu = din("shift_mu", [RWP]); w0 = din("w0", [512])
    w1u = din("w1u", [64, 512]); a0 = din("a0", [512]); a1u = din("a1u", [64, 512]); g1u = din("g1u", [128, 512])
    k_k = din("k_k", [512]); k_a = din("k_a", [512]); r_k = din("r_k", [512])
    ln_x_w = din("ln_x_w", [512]); ln_x_b = din("ln_x_b", [512]); lb_logits = din("lb_logits", [2, 512])
    hg_norm_w = din("hg_norm_w", [512]); w_out = din("w_out", [D, D]); ln1_g = din("ln1_g", [D]); ln1_b = din("ln1_b", [D])
    w_up = din("w_up", [D, DFF]); w_down = din("w_down", [DFF, D]); ln2_g = din("ln2_g", [D]); ln2_b = din("ln2_b", [D])
    cst_d = din("consts", [128, NCONST])
    yp = dout("yp", [NTP * 128, D]); ys = dout("ys", [128, D])
    rwp = dout("rwp", [8, 64, 64]); rws = dout("rws", [NB, 8, 64, 64])
    hgp = dout("hgp", [4, 128, 128]); hgs = dout("hgs", [NB, 4, 128, 128])
    shp = dout("shp", [RWP]); shs = dout("shs", [NB, RWP])
    h1scr = nc.dram_tensor("h1scr", [(NTP + 1) * 128, D], F32).ap()
    if DBG:
        dbg_d = dout("dbg", [128, 4096])

    def row(v):
        return v.rearrange("(o n) -> o n", o=1)

    P = Prog()
    with ExitStack() as es:
        arena = Arena(nc, es, ARENA_BYTES)
        C = Ctx(nc, P, arena)
        sb = C.sb
        ps = [V(es.enter_context(nc.psum_tensor(f"ps{i}", [128, 512], F32))[:], f"ps{i}") for i in range(8)]

        def psv(i, a):
            return ps[i].rearrange("p (a t) -> p a t", a=a)

        def psb(i):
            return ps[i].bitcast(BF16)

        def bc3(ap2, n):
            return ap2.unsqueeze(2).to_broadcast([ap2.shape[0], ap2.shape[1], n])

        def v3(t):
            return t.rearrange("p (a t) -> p a t", a=4)

        ident_bf = sb([128, 128], BF16, "ident_bf")
        lng = sb([128, D], F32, "lng"); lnb = sb([128, D], F32, "lnb")
        C.dma(SP, lng, row(ln1_g).partition_broadcast(128))
        C.dma(SP, lnb, row(ln1_b).partition_broadcast(128))
        bnst = sb([128, 12], F32, "bnst"); mv = sb([128, 2], F32, "mv"); rstd1 = sb([128, 1], F32, "rstd1")
        fence_ln = sb([128, 2], F32, "fence_ln")
        if DBG:
            dbg = sb([128, 4096], F32, "dbg")
            C.memset(POOL, dbg, 0.0)
            dbg_pos = [0]
            dbg_map = {}

            def dump(name, v, n):
                a = dbg_pos[0]
                shape = list(v.shape)
                dst = dbg[0:shape[0], a:a + n]
                if len(shape) == 3:
                    dst = dst.rearrange("p (a b) -> p a b", a=shape[1])
                C.cp(POOL, dst, v)
                dbg_pos[0] += n
                dbg_map[name] = (a, n, shape)
            build.dbg_map = dbg_map
        else:
            def dump(name, v, n):
                return None
        phase_mark = arena.off

        def layernorm(src, dst, eps):
            for half in range(2):
                C.bn_stats(bnst[:, half * 6:(half + 1) * 6], src[:, half * 512:(half + 1) * 512])
            C.bn_aggr(mv, bnst)
            C.act(rstd1, mv[:, 1:2], AF.Ln, bias=eps)
            C.act(rstd1, rstd1, AF.Exp, scale=-0.5)
            C.ts(DVE, dst, src, mv[:, 0:1], ALU.subtract, rstd1[:, 0:1], ALU.mult)
            halves = []
            for eng_, hsl in ((DVE, slice(0, 640)), (POOL, slice(640, 1024))):
                dk = dst[:, hsl].k(hsl.start)
                halves.append(dk)
                o, a_, g_, b_ = _ap(dk), _ap(dst[:, hsl]), _ap(lng[:, hsl]), _ap(lnb[:, hsl])
                C.rec(eng_, lambda e, o=o, a_=a_, g_=g_: e.tensor_tensor(out=o, in0=a_, in1=g_, op=ALU.mult), [dk], [dst, lng])
                C.rec(eng_, lambda e, o=o, b_=b_: e.tensor_tensor(out=o, in0=o, in1=b_, op=ALU.add), [dk], [dk, lnb])
            C.rec(POOL, lambda e: e.memset(_ap(fence_ln), 0.0), [dst, fence_ln], halves)

        cst = sb([128, NCONST], F32, "cst")
        C.dma(SP, cst, cst_d)

        def cc(name):
            a, b = COFF[name]
            return cst[:, a:b]

        C.cp(POOL, ident_bf, cc("ident"))
        ones_row = sb([1, 128], BF16, "ones_row")
        C.memset(POOL, ones_row, 1.0)
        bdones_bf = sb([128, 128], BF16, "bdones_bf")
        C.cp(POOL, bdones_bf, cc("bdones"))
        mu14 = sb([128, 14], F32, "mu14")
        kk4 = sb([128, 4], F32, "kk4"); ka4 = sb([128, 4], F32, "ka4"); rk4 = sb([128, 4], F32, "rk4")
        w04 = sb([128, 4], F32, "w04"); a04 = sb([128, 4], F32, "a04")
        lbl = sb([128, 2, 4], F32, "lbl")
        with nc.allow_non_contiguous_dma(reason="tiny per-channel parameter vectors"):
            C.dma(SP, mu14, shift_mu.rearrange("(c p) -> p c", p=128))
            C.dma(SP, kk4, k_k.rearrange("(c p) -> p c", p=128))
            C.dma(SP, ka4, k_a.rearrange("(c p) -> p c", p=128))
            C.dma(SP, rk4, r_k.rearrange("(c p) -> p c", p=128))
            C.dma(SP, w04, w0.rearrange("(c p) -> p c", p=128))
            C.dma(SP, a04, a0.rearrange("(c p) -> p c", p=128))
            C.dma(SP, lbl, lb_logits.rearrange("l (c p) -> p l c", p=128))
        w0row = sb([1, 512], BF16, "w0row"); a0row = sb([1, 512], BF16, "a0row")
        C.dma(POOL, w0row, row(w0))
        C.dma(POOL, a0row, row(a0))
        WA = sb([128, 512], BF16, "WA")
        C.dma(POOL, WA[0:64, :].k("lo"), w1u)
        C.dma(POOL, WA[64:128, :].k("hi"), a1u)
        g1u_bf = sb([128, 512], BF16, "g1u_bf")
        C.dma(POOL, g1u_bf, g1u)
        lnxw = sb([128, 512], F32, "lnxw"); lnxb = sb([128, 512], BF16, "lnxb"); hgw = sb([128, 512], BF16, "hgw")
        C.dma(SP, lnxw, row(ln_x_w).partition_broadcast(128))
        C.dma(POOL, lnxb, row(ln_x_b).partition_broadcast(128))
        C.dma(POOL, hgw, row(hg_norm_w).partition_broadcast(128))
        lb4 = sb([128, 4], F32, "lb4"); oml4 = sb([128, 4], F32, "oml4"); etmp = sb([128, 4], F32, "etmp")
        C.tt(DVE, etmp, lbl[:, 1, :], lbl[:, 0, :], ALU.subtract)
        C.act(etmp, etmp, AF.Exp)
        C.ts(DVE, lb4, etmp, 1.0, ALU.add)
        C.recip(lb4, lb4)
        C.tt(DVE, oml4, etmp, lb4, ALU.mult)
        RKsel = sb([128, 4, 2], BF16, "RKsel")
        C.tt(DVE, RKsel, bc3(rk4, 2), cc("hsel").unsqueeze(1).to_broadcast([128, 4, 2]), ALU.mult)

        ov_lo = arena.off
        w_in_bf = sb([128, 8, PJ], BF16, "w_in_bf")
        ov_hi = arena.off
        for kc in range(8):
            C.dma(POOL, w_in_bf[:, kc, :].k(kc), w_in[kc * 128:(kc + 1) * 128, :])
        w_out_bf = sb([128, 8, D], BF16, "w_out_bf")
        for kc in range(8):
            C.dma(POOL, w_out_bf[:, kc, :].k(kc), w_out[kc * 128:(kc + 1) * 128, :])

        ST = sb([128, 4, 64], F32, "ST"); ST_bf = sb([128, 4, 64], BF16, "ST_bf")
        SH = sb([128, 4, 128], F32, "SH"); SH_bf = sb([128, 4, 128], BF16, "SH_bf")
        plast = sb([128, 14], F32, "plast")
        for t_ in (ST, ST_bf, SH, SH_bf, plast):
            C.memset(POOL, t_, 0.0)

        x_t = [sb([128, D], F32, f"x_t{i}") for i in range(2)]
        x_bf = sb([128, D], BF16, "x_bf")
        xT = sb([128, 8, 128], BF16, "xT")
        pr_ = sb([128, 14, 129], F32, "pr")
        xs = sb([128, 14, 128], F32, "xsft")
        T = [sb([128, 512], F32, f"T{i}") for i in range(10)]
        SHtmp = v3(T[9])
        STtmp = T[1][:, 0:256].rearrange("p (a v) -> p a v", a=4)
        z12 = sb([128, 128], BF16, "z12"); sg_bf = sb([128, 128], BF16, "sg_bf"); tqb = sb([128, 512], BF16, "tqb")
        g_tm = sb([128, 512], BF16, "g_tm")
        AR = sb([128, 4, 2, 128], BF16, "AR")
        bT = sb([128, 4, 128], BF16, "bT"); kT = sb([128, 4, 128], BF16, "kT")
        bhT = sb([128, 4, 128], BF16, "bhT"); khT = sb([128, 4, 128], BF16, "khT"); vT = sb([128, 4, 128], BF16, "vT")
        A_tm = sb([128, 512], BF16, "A_tm"); Bh_tm = sb([128, 512], BF16, "Bh_tm")
        Kh_tm = sb([128, 512], BF16, "Kh_tm"); V_tm = sb([128, 512], BF16, "V_tm")
        bon8 = sb([128, 8], F32, "bon8")
        Pm = [sb([128, 8, 128], BF16, f"Pm{i}") for i in range(2)]
        Tm = [sb([128, 8, 128], BF16, f"Tm{i}") for i in range(2)]
        RR = [sb([128, 8, 128], BF16, f"RR{i}") for i in range(2)]
        NrbT = sb([128, 8, 128], BF16, "NrbT"); AkT = sb([128, 8, 128], BF16, "AkT"); NrkT = sb([128, 8, 128], BF16, "NrkT")
        TTf = sb([128, 8, 128], BF16, "TTf")
        W1T = sb([128, 4, 128], BF16, "W1T")
        Z_tm = sb([128, 512], BF16, "Z_tm"); U_tm = sb([128, 512], BF16, "U_tm")
        st16 = sb([128, 16], F32, "st16"); m8 = sb([128, 8], F32, "m8"); r8 = sb([128, 8], F32, "r8")
        o_all = sb([128, D], BF16, "o_all"); oT = sb([128, 8, 128], BF16, "oT")
        h1pre = sb([128, D], F32, "h1pre")
        qTb = sb([128, 4, 128], BF16, "qTb"); kTb = sb([128, 4, 128], BF16, "kTb"); khTb = sb([128, 4, 128], BF16, "khTb")
        khat_tm = sb([128, 512], BF16, "khat_tm"); i_tm = sb([128, 512], BF16, "i_tm")
        gs_t = sb([128, 512], BF16, "gs_t"); attT = sb([128, 4, 128], BF16, "attT")
        s4 = sb([128, 4], F32, "s4"); rr4 = sb([128, 4], F32, "rr4")
        gC = sb([128, 4, NB], F32, "gC"); decH = sb([128, 4, NB], F32, "decH")
        fence2 = sb([128, 2], F32, "fence2")
        identb4 = ident_bf.unsqueeze(1).to_broadcast([128, 4, 128])
        save_off = arena.off
        arena.off = ov_lo
        scrA = [sb([128, 2048], F32, f"scrA{i}") for i in range(2)]
        S0T32 = sb([128, NB, 4, 64], F32, "S0T32")
        S0Tb = sb([128, NB, 4, 64], BF16, "S0Tb")
        sshT = sb([128, 14, NB], F32, "sshT"); lastp = sb([128, 14, NB], F32, "lastp")
        EW = [sb([128, 4, 128], BF16, f"EW{i}") for i in range(2)]
        ER = [sb([128, 4, 128], BF16, f"ER{i}") for i in range(2)]
        EQ = [sb([128, 4, 128], BF16, f"EQ{i}") for i in range(2)]
        Ub = sb([128, 512], BF16, "Ub"); Vb = sb([128, 512], BF16, "Vb"); khb = sb([128, 512], BF16, "khb")
        Dg = sb([128, 4, 64], F32, "Dg")
        S0h = [sb([128, 4, 128], F32, f"S0h{i}") for i in range(2)]
        S0hb = sb([128, 4, 128], BF16, "S0hb")
        Sn = sb([128, 512], F32, "Sn")
        fence_t = sb([128, 2], F32, "fence_t")
        assert arena.off <= ov_hi, (arena.off, ov_hi)
        ov_bufs = scrA + [S0T32, S0Tb, sshT, lastp] + EW + ER + EQ + [Ub, Vb, khb, Dg] + S0h + [S0hb, Sn, fence_t]
        arena.off = save_off
        ssh_tm = scrA[0][0:NB, 0:RWP]
        hh_order = (0, 2, 4, 6, 1, 3, 5, 7)
        heads_of = [(0, 1, 2, 3), (4, 5, 6, 7)]

        def ck(n):
            if STAGE == n:
                raise _Stop()

        def mixer_tile(ti, x_src, h1_dst, sample):
            nch = NB if sample else 1
            Cn = TS if sample else 128
            L = 3 if sample else 7
            sfx = "_s" if sample else "_p"
            mS, mI, mST, reset = cc("mS" + sfx), cc("mI" + sfx), cc("mST" + sfx), cc("reset" + sfx)
            mSb = mS.unsqueeze(1).to_broadcast([128, 4, 128])
            mIb = mI.unsqueeze(1).to_broadcast([128, 4, 128])
            mSTb = mST.unsqueeze(1).to_broadcast([128, 4, 128])
            xt = x_t[ti % 2]
            C.dma(SP, xt, x_src)
            C.dma(POOL, x_bf, x_src)
            ck(1)
            for kc in range(8):
                C.tr(psb(7)[:, kc * 128:(kc + 1) * 128], x_bf[:, kc * 128:(kc + 1) * 128], ident_bf)
            C.cp(ACT, xT, psb(7).rearrange("p (a t) -> p a t", a=8))
            ck(2)
            for c in range(22):
                if c < 14:
                    bank, slot = c // 4, c % 4
                else:
                    bank, slot = 4 + (c - 14) // 4, (c - 14) % 4
                for kc in range(8):
                    C.mm(ps[bank][:, slot * 128:(slot + 1) * 128], w_in_bf[:, kc, c * 128:(c + 1) * 128].k(kc),
                         xT[:, kc, :], start=(kc == 0), stop=(kc == 7))
            for j, bank in ((0, 6), (1, 7)):
                c0 = RWP + 1024 + j * 512
                for kc in range(8):
                    C.mm(ps[bank], xT[:, kc, :], w_in_bf[:, kc, c0:c0 + 512].k(kc), start=(kc == 0), stop=(kc == 7))
            ck(3)
            for bnk in range(3):
                C.cp(ACT, pr_[:, bnk * 4:(bnk + 1) * 4, 1:129], psv(bnk, 4))
            C.cp(ACT, pr_[:, 12:14, 1:129], psv(3, 4)[:, 0:2, :])
            prev, cur = pr_[:, :, 0:128], pr_[:, :, 1:129]
            if not sample:
                C.cp(POOL, pr_[:, :, 0:1], plast.unsqueeze(2))
                C.cp(POOL, plast.unsqueeze(2), pr_[:, :, 128:129])
            else:
                C.memset(POOL, pr_[:, :, 0:1], 0.0)
                C.rec(POOL, lambda e: e.memset(_ap(fence_t), 0.0),
                      [w_in_bf[:, kc, :].k(kc) for kc in range(8)] + ov_bufs
                      + [S0T32[:, b, :, :].k(b) for b in range(NB)] + [S0Tb[:, b, :, :].k(b) for b in range(NB)], [])
                for t_ in EW + ER + EQ:
                    C.memset(POOL, t_, 0.0)
                C.dma(SP, ssh_tm, ssh)
                for c in range(14):
                    C.tr(ps[0][:, c * NB:(c + 1) * NB], ssh_tm[:, c * 128:(c + 1) * 128], cc("ident")[0:NB, 0:NB])
                C.cp(DVE, sshT, ps[0][:, 0:14 * NB].rearrange("p (c b) -> p c b", c=14))
            for eng_, c0, c1 in ((DVE, 0, 8), (POOL, 8, 14)):
                C.tt(eng_, xs[:, c0:c1, :].k(c0), prev[:, c0:c1, :], cur[:, c0:c1, :], ALU.subtract)
                C.tt(eng_, xs[:, c0:c1, :].k(c0), xs[:, c0:c1, :].k(c0), bc3(mu14[:, c0:c1], 128), ALU.mult)
                C.tt(eng_, xs[:, c0:c1, :].k(c0), xs[:, c0:c1, :].k(c0), cur[:, c0:c1, :], ALU.add)
            C.rec(POOL, lambda e: e.memset(_ap(fence2), 0.0), [xs, fence2], [xs[:, 0:8, :].k(0), xs[:, 8:14, :].k(8)])
            if sample:
                cur4 = cur.rearrange("p c (b t) -> p c b t", t=TS)
                xs4 = xs.rearrange("p c (b t) -> p c b t", t=TS)
                cur0, xs0 = cur4[:, :, :, 0], xs4[:, :, :, 0]
                C.tt(POOL, xs0, sshT, cur0, ALU.subtract)
                C.tt(POOL, xs0, xs0, bc3(mu14, NB), ALU.mult)
                C.tt(POOL, xs0, xs0, cur0, ALU.add)
                C.cp(POOL, lastp, cur4[:, :, :, TS - 1])
                for c in range(14):
                    C.tr(ps[c // 4][0:NB, (c % 4) * 128:(c % 4 + 1) * 128], lastp[:, c, :], cc("ident"))
                for bnk in range(4):
                    w_ = 512 if bnk < 3 else 256
                    C.cp(ACT, ssh_tm[:, bnk * 512:bnk * 512 + w_], ps[bnk][0:NB, 0:w_])
                C.dma(SP, shs, ssh_tm, out_final=True)
            r_ = xs[:, 0:4, :]; k_ = xs[:, 4:8, :]; v_ = xs[:, 8:12, :]

            ck(4)
            sig, kq, fgl, bcs, eb, enb, ebl, sq_, eg = T[1], T[4], T[0], T[2], T[3], T[5], T[6], T[7], T[8]
            C.cp(ACT, i_tm, ps[6])
            C.act(sig, ps[5], AF.Sigmoid)
            C.act(sq_, ps[4], AF.Sigmoid)
            C.act(eg, ps[7], AF.Sigmoid)
            C.act(z12[0:64, :], xs[0:64, 12, :], AF.Tanh)
            C.act(sg_bf, xs[:, 13, :], AF.Sigmoid)
            C.cp(DVE, z12[64:128, :], xs[64:128, 12, :])
            for pr in range(4):
                sl = slice(pr * 128, (pr + 1) * 128)
                C.mm(ps[0][:, sl], WA[0:64, sl].k("lo"), z12[0:64, :])
            for pr in range(4):
                sl = slice(pr * 128, (pr + 1) * 128)
                C.mm(ps[1][:, sl], WA[64:128, sl].k("hi"), z12[64:128, :])
            C.mm(ps[3], sg_bf, g1u_bf)
            sw, alr, cumS, gex, gin, ginv, glast, kkk, tq, k2 = T[0], T[1], T[2], T[3], T[4], T[5], T[6], T[7], T[8], T[9]
            C.tt(DVE, v3(kq), v3(sig), bc3(oml4, 128), ALU.mult)
            C.tt(DVE, v3(fgl), v3(kq), bc3(lb4, 128), ALU.add)
            C.tt(POOL, v3(kq), bc3(oml4, 128), v3(kq), ALU.subtract)
            C.tt(DVE, sq_, ps[4], sq_, ALU.mult)
            C.tt(DVE, gs_t, ps[7], eg, ALU.mult)
            sw2, alr2 = T[9], T[8]
            for pr in range(4):
                sl = slice(pr * 128, (pr + 1) * 128)
                C.act(sw2[:, sl], ps[0][:, sl], AF.Sigmoid, bias=w04[:, pr:pr + 1])
            for pr in range(4):
                sl = slice(pr * 128, (pr + 1) * 128)
                C.act(alr2[:, sl], ps[1][:, sl], AF.Sigmoid, bias=a04[:, pr:pr + 1])
            C.cp(ACT, g_tm, ps[3])
            C.act(fgl, fgl, AF.Ln)
            for h in range(4):
                C.scan(bcs[:, h * 128:(h + 1) * 128], reset, fgl[:, h * 128:(h + 1) * 128])
            C.act(eb, bcs, AF.Exp)
            C.act(enb, bcs, AF.Exp, scale=-1.0)
            bc4 = bcs.rearrange("p (a n c) -> p a n c", a=4, n=nch)
            C.tt(POOL, ebl.rearrange("p (a n c) -> p a n c", a=4, n=nch),
                 bc4[:, :, :, Cn - 1:Cn].to_broadcast([128, 4, nch, Cn]), bc4, ALU.subtract)
            C.act(ebl, ebl, AF.Exp)
            C.cp(POOL, decH[:, :, 0:nch].unsqueeze(3),
                 eb.rearrange("p (a n c) -> p a n c", a=4, n=nch)[:, :, :, Cn - 1:Cn])
            C.tt(DVE, qTb, v3(sq_), v3(eb), ALU.mult)
            C.tt(POOL, kTb, v3(kq), v3(enb), ALU.mult)
            C.tt(DVE, khTb, v3(kq), v3(ebl), ALU.mult)

            sw, alr = sw2, alr2
            cumS, gex, gin, ginv, glast, kkk, tq, k2 = T[2], T[3], T[4], T[5], T[6], T[7], T[0], T[1]
            for pr in range(4):
                C.scan(cumS[:, pr * 128:(pr + 1) * 128], reset, sw[:, pr * 128:(pr + 1) * 128])
            C.tt(POOL, gex, cumS, sw, ALU.subtract)
            C.act(gex, gex, AF.Exp, scale=CDEC)
            C.act(gin, cumS, AF.Exp, scale=CDEC)
            C.act(ginv, cumS, AF.Exp, scale=-CDEC)
            cs4 = cumS.rearrange("p (a n c) -> p a n c", a=4, n=nch)
            C.tt(POOL, glast.rearrange("p (a n c) -> p a n c", a=4, n=nch),
                 cs4[:, :, :, Cn - 1:Cn].to_broadcast([128, 4, nch, Cn]), cs4, ALU.subtract)
            C.act(glast, glast, AF.Exp, scale=CDEC)
            C.cp(POOL, gC[:, :, 0:nch].unsqueeze(3),
                 gin.rearrange("p (a n c) -> p a n c", a=4, n=nch)[:, :, :, Cn - 1:Cn])
            C.tt(POOL, v3(kkk), k_, bc3(kk4, 128), ALU.mult)
            C.act(tqb, kkk, AF.Square)
            for pr in range(4):
                sl = slice(pr * 128, (pr + 1) * 128)
                C.mm(ps[2][:, sl], bdones_bf, tqb[:, sl])
            C.act(tq, ps[2], AF.Ln, bias=1e-24)
            C.act(tq, tq, AF.Exp, scale=-0.5)
            C.tt(DVE, kkk, kkk, tq, ALU.mult)
            C.stt(DVE, v3(k2), v3(alr), -1.0, bc3(ka4, 128), ALU.add, ALU.mult)
            C.stt(DVE, v3(k2), v3(k2), 1.0, k_, ALU.add, ALU.mult)
            C.tt(DVE, alr, kkk, alr, ALU.mult)
            b_ = alr
            C.tt(DVE, AR[:, :, 1, :], r_, v3(gin), ALU.mult)
            C.stt(DVE, AR[:, :, 0, :], v3(kkk), -1.0, v3(gex), ALU.mult, ALU.mult)
            C.tt(POOL, bT, v3(b_), v3(ginv), ALU.mult)
            C.tt(POOL, kT, v3(k2), v3(ginv), ALU.mult)
            C.tt(DVE, bhT, v3(b_), v3(glast), ALU.mult)
            C.tt(POOL, khT, v3(k2), v3(glast), ALU.mult)
            C.cp(POOL, vT, v_)
            C.tt(DVE, v3(tqb), r_, v3(k2), ALU.mult)
            for pr in range(4):
                C.mm(ps[1][:, pr * 2:(pr + 1) * 2], tqb[:, pr * 128:(pr + 1) * 128], RKsel[:, pr, :])
            C.cp(DVE, bon8, ps[1][:, 0:8])
            for (src, bank, half) in ((lambda pr: AR[:, pr, 0, :], 4, 0), (lambda pr: bhT[:, pr, :], 4, 1),
                                      (lambda pr: khT[:, pr, :], 5, 0), (lambda pr: vT[:, pr, :], 5, 1)):
                for pr in range(4):
                    o0 = half * 512 + pr * 128
                    C.tr(psb(bank)[:, o0:o0 + 128], src(pr), ident_bf)
            C.cp(ACT, A_tm, psb(4)[:, 0:512]); C.cp(ACT, Bh_tm, psb(4)[:, 512:1024])
            C.cp(ACT, Kh_tm, psb(5)[:, 0:512]); C.cp(ACT, V_tm, psb(5)[:, 512:1024])

            ck(6)
            for h in range(4):
                C.tr(psb(6)[:, h * 128:(h + 1) * 128], khTb[:, h, :], ident_bf)
            C.cp(ACT, khat_tm, psb(6)[:, 0:512])
            for h in range(4):
                C.mm(ps[6][:, h * 128:(h + 1) * 128], kTb[:, h, :], qTb[:, h, :])
            C.tt(DVE, attT, psv(6, 4), mIb, ALU.mult)
            if not sample:
                for h in range(4):
                    hsl = slice(h * 128, (h + 1) * 128)
                    C.mm(ps[7][:, hsl], attT[:, h, :], i_tm[:, hsl], start=True, stop=False)
                    C.mm(ps[7][:, hsl], qTb[:, h, :], SH_bf[:, h, :], start=False, stop=True)
                for h in range(4):
                    hsl = slice(h * 128, (h + 1) * 128)
                    C.mm(ps[6][:, hsl], khat_tm[:, hsl], i_tm[:, hsl])
                C.tt(POOL, SHtmp, SH, bc3(decH[:, :, 0], 128), ALU.mult)
                C.tt(DVE, SH, SHtmp, psv(6, 4), ALU.add)
                C.cp(POOL, SH_bf, SH)
            else:
                for h in range(4):
                    hsl = slice(h * 128, (h + 1) * 128)
                    C.mm(ps[7][:, hsl], attT[:, h, :], i_tm[:, hsl], start=(h == 0), stop=False)
                for b in range(NB):
                    s0 = S0h[b % 2]
                    csl = slice(b * TS, (b + 1) * TS)
                    C.dma(SP, s0, shg[b].rearrange("h k v -> k h v"))
                    C.cp(ACT, S0hb, s0)
                    eq = EQ[b % 2]
                    C.cp(POOL, eq[:, :, csl], qTb[:, :, csl])
                    for h in range(4):
                        C.mm(ps[7][:, h * 128:(h + 1) * 128], eq[:, h, :], S0hb[:, h, :], start=False, stop=False)
                    C.memset(POOL, eq[:, :, csl], 0.0)
                    C.ts(DVE, khb, khat_tm, cc("cm")[:, b:b + 1], ALU.mult)
                    bank = 5 + b % 2
                    for h in range(4):
                        hsl = slice(h * 128, (h + 1) * 128)
                        C.mm(ps[bank][:, hsl], khb[:, hsl], i_tm[:, hsl])
                    C.tt(DVE, s0, s0, bc3(decH[:, :, b], 128), ALU.mult)
                    C.tt(DVE, s0, s0, psv(bank, 4), ALU.add)
                    C.dma(SP, hgs[b].rearrange("h k v -> k h v"), s0, out_final=True)
            osq = T[4]
            C.act(osq, ps[7], AF.Square)
            C.red(DVE, s4, v3(osq))
            C.act(rr4, s4, AF.Ln, scale=1.0 / 128, bias=RMS_EPS)
            C.act(rr4, rr4, AF.Exp, scale=-0.5)
            C.tt(DVE, v3(osq), psv(7, 4), bc3(rr4, 128), ALU.mult)
            C.tt(POOL, osq, osq, hgw, ALU.mult)
            C.tt(POOL, o_all[:, 512:1024], osq, gs_t, ALU.mult)

            ck(7)
            if sample:
                for g in range(4):
                    sa = scrA[g % 2]
                    nat = sa[0:64, :].rearrange("p (b n) -> p b n", b=4)
                    C.dma(SP, nat.rearrange("p b (h j) -> p b h j", h=8),
                          srw[g * 4:(g + 1) * 4].rearrange("b h v j -> v b h j"))
                    for bb in range(4):
                        b = g * 4 + bb
                        bank = b % 2
                        for pr in range(4):
                            C.tr(ps[bank][:, (bb % 2) * 256 + pr * 64:(bb % 2) * 256 + (pr + 1) * 64],
                                 nat[:, bb, pr * 128:(pr + 1) * 128], cc("ident")[0:64, 0:64])
                        C.cp(ACT, S0T32[:, b, :, :].k(b),
                             ps[bank][:, (bb % 2) * 256:(bb % 2) * 256 + 256].rearrange("p (a v) -> p a v", a=4))
                        C.cp(DVE, S0Tb[:, b, :, :].k(b), S0T32[:, b, :, :].k(b))
            for hg in range(2):
                for i, h in enumerate(heads_of[hg]):
                    pr, hh = h // 2, h % 2
                    rows = slice(64 * hh, 64 * hh + 64)
                    sl = slice(i * 128, (i + 1) * 128)
                    C.mm(ps[0][:, sl], bT[rows, pr, :], AR[rows, pr, 0, :])
                    C.mm(ps[1][:, sl], bT[rows, pr, :], AR[rows, pr, 1, :])
                    C.mm(ps[2][:, sl], kT[rows, pr, :], AR[rows, pr, 0, :])
                    C.mm(ps[3][:, sl], kT[rows, pr, :], AR[rows, pr, 1, :])
                    C.mm(ps[4][:, sl], AR[rows, pr, 0, :], bT[rows, pr, :])
                hs = slice(4 * hg, 4 * hg + 4)
                C.tt(DVE, Pm[0][:, hs, :], psv(0, 4), mSb, ALU.mult)
                C.tt(DVE, NrbT[:, hs, :], psv(1, 4), mIb, ALU.mult)
                C.tt(DVE, AkT[:, hs, :], psv(2, 4), mSb, ALU.mult)
                C.tt(DVE, NrkT[:, hs, :], psv(3, 4), mIb, ALU.mult)
                C.tt(DVE, RR[0][:, hs, :], psv(4, 4), mSTb, ALU.mult)
                C.tt(POOL, Tm[1][:, hs, :], Pm[0][:, hs, :], identb4, ALU.add)
            ck(71)
            for k in range(L):
                cur, nxt = k % 2, 1 - k % 2
                ck(72 + k)
                for hg in range(2):
                    b0 = 3 * hg
                    hs = slice(4 * hg, 4 * hg + 4)
                    for i, h in enumerate(heads_of[hg]):
                        sl = slice(i * 128, (i + 1) * 128)
                        if k < L - 1:
                            C.mm(ps[b0][:, sl], RR[cur][:, h, :], Pm[cur][:, h, :])
                        if k >= 1:
                            C.mm(ps[b0 + 1][:, sl], RR[cur][:, h, :], Tm[cur][:, h, :])
                        if k < L - 1:
                            C.mm(ps[b0 + 2][:, sl], Pm[cur][:, h, :], RR[cur][:, h, :])
                    if k < L - 1:
                        C.cp(ACT, Pm[nxt][:, hs, :], psv(b0, 4))
                    if k >= 1:
                        dst = TTf[:, hs, :] if k == L - 1 else Tm[nxt][:, hs, :]
                        C.tt(DVE, dst, psv(b0 + 1, 4), Tm[cur][:, hs, :], ALU.add)
                    if k < L - 1:
                        C.cp(ACT, RR[nxt][:, hs, :], psv(b0 + 2, 4))
            ck(8)
            for h in range(8):
                C.mm(ps[6][:, h * 64:(h + 1) * 64], AkT[:, h, :], V_tm[:, h * 64:(h + 1) * 64])
            C.cp(ACT, Z_tm, ps[6])
            for h in range(8):
                pr = h // 2
                C.mm(ps[h // 4][:, (h % 4) * 128:(h % 4 + 1) * 128], A_tm[:, pr * 128:(pr + 1) * 128], TTf[:, h, :])
            for b in range(2):
                pv = ps[b].rearrange("p (q e t) -> p q e t", q=2, e=2)
                C.cp(ACT, W1T[0:64, 2 * b:2 * b + 2, :], pv[0:64, :, 0, :])
                C.cp(ACT, W1T[64:128, 2 * b:2 * b + 2, :], pv[64:128, :, 1, :])
            for h in range(8):
                pr, hh = h // 2, h % 2
                rows = slice(64 * hh, 64 * hh + 64)
                hsl = slice(h * 64, (h + 1) * 64)
                if not sample:
                    C.mm(ps[3][:, hsl], TTf[:, h, :], Z_tm[:, hsl], start=True, stop=False)
                    C.mm(ps[3][:, hsl], W1T[rows, pr, :], ST_bf[rows, pr, :], start=False, stop=True)
                else:
                    C.mm(ps[3][:, hsl], TTf[:, h, :], Z_tm[:, hsl], start=(h == 0), stop=False)
            if sample:
                for b in range(NB):
                    ew = EW[b % 2]
                    csl = slice(b * TS, (b + 1) * TS)
                    C.cp(POOL, ew[:, :, csl], W1T[:, :, csl])
                    for h in hh_order:
                        pr, hh = h // 2, h % 2
                        rows = slice(64 * hh, 64 * hh + 64)
                        C.mm(ps[3][:, h * 64:(h + 1) * 64], ew[rows, pr, :], S0Tb[rows, b, pr, :].k(b), start=False, stop=False)
                    C.memset(POOL, ew[:, :, csl], 0.0)
            C.cp(ACT, U_tm, ps[3])
            for h in range(8):
                pr, hh = h // 2, h % 2
                rows = slice(64 * hh, 64 * hh + 64)
                hsl = slice(h * 64, (h + 1) * 64)
                C.mm(ps[4][:, hsl], NrbT[:, h, :], U_tm[:, hsl], start=(h == 0 or not sample), stop=False)
                C.mm(ps[4][:, hsl], NrkT[:, h, :], V_tm[:, hsl], start=False, stop=False)
                if not sample:
                    C.mm(ps[4][:, hsl], AR[rows, pr, 1, :], ST_bf[rows, pr, :], start=False, stop=True)
            if sample:
                for b in range(NB):
                    er = ER[b % 2]
                    csl = slice(b * TS, (b + 1) * TS)
                    C.cp(POOL, er[:, :, csl], AR[:, :, 1, csl])
                    for h in hh_order:
                        pr, hh = h // 2, h % 2
                        rows = slice(64 * hh, 64 * hh + 64)
                        C.mm(ps[4][:, h * 64:(h + 1) * 64], er[rows, pr, :], S0Tb[rows, b, pr, :].k(b), start=False, stop=False)
                    C.memset(POOL, er[:, :, csl], 0.0)
                i64b = cc("i64s").unsqueeze(1).to_broadcast([128, 4, 64])
                for b in range(NB):
                    bank = 5 + b % 2
                    C.ts(DVE, Ub, U_tm, cc("cm")[:, b:b + 1], ALU.mult)
                    C.ts(DVE, Vb, V_tm, cc("cm")[:, b:b + 1], ALU.mult)
                    C.tt(DVE, Dg, i64b, bc3(gC[:, :, b], 64), ALU.mult)
                    for h in range(8):
                        hsl = slice(h * 64, (h + 1) * 64)
                        C.mm(ps[bank][0:64, hsl], Ub[:, hsl], Bh_tm[:, hsl], start=(h == 0), stop=False)
                        C.mm(ps[bank][0:64, hsl], Vb[:, hsl], Kh_tm[:, hsl], start=False, stop=False)
                    for h in hh_order:
                        pr, hh = h // 2, h % 2
                        rows = slice(64 * hh, 64 * hh + 64)
                        C.mm(ps[bank][0:64, h * 64:(h + 1) * 64], S0T32[rows, b, pr, :].k(b), Dg[rows, pr, :], start=False, stop=False)
                    C.cp(ACT, Sn[0:64, :], ps[bank][0:64, :])
                    C.dma(SP, rws[b].rearrange("h v j -> v h j"), Sn[0:64, :].rearrange("p (h j) -> p h j", h=8), out_final=True)
            if not sample:
                for pr in range(4):
                    psl = slice(pr * 128, (pr + 1) * 128)
                    C.mm(ps[5][:, psl], Bh_tm[:, psl], U_tm[:, psl], start=True, stop=False)
                    C.mm(ps[5][:, psl], Kh_tm[:, psl], V_tm[:, psl], start=False, stop=True)
                C.tt(POOL, STtmp, ST, bc3(gC[:, :, 0], 64), ALU.mult)
                p5 = psv(5, 4)
                C.tt(DVE, ST[0:64, :, :], STtmp[0:64, :, :], p5[0:64, :, 0:64], ALU.add)
                C.tt(DVE, ST[64:128, :, :], STtmp[64:128, :, :], p5[64:128, :, 64:128], ALU.add)
                C.cp(POOL, ST_bf, ST)
            ysq, tmp2 = T[9], T[1]
            y3 = ps[4].rearrange("p (h v) -> p h v", h=8)
            yv = ysq.rearrange("p (h v) -> p h v", h=8)
            C.red(DVE, st16[:, 0:8], y3)
            C.act(ysq, ps[4], AF.Square)
            C.red(DVE, st16[:, 8:16], yv)
            C.ts(DVE, m8, st16[:, 0:8], 1.0 / 64, ALU.mult)
            C.tt(DVE, r8, m8, m8, ALU.mult)
            C.stt(DVE, r8, st16[:, 8:16], 1.0 / 64, r8, ALU.mult, ALU.subtract)
            C.act(r8, r8, AF.Ln, bias=GN_EPS)
            C.act(r8, r8, AF.Exp, scale=-0.5)
            C.tt(DVE, yv, y3, bc3(m8, 64), ALU.subtract)
            C.tt(POOL, yv, yv, bc3(r8, 64), ALU.mult)
            C.tt(POOL, ysq, ysq, lnxw, ALU.mult)
            C.tt(POOL, ysq, ysq, lnxb, ALU.add)
            C.tt(DVE, tmp2.rearrange("p (h v) -> p h v", h=8), V_tm.rearrange("p (h v) -> p h v", h=8),
                 bc3(bon8, 64), ALU.mult)
            C.tt(DVE, ysq, ysq, tmp2, ALU.add)
            C.tt(DVE, o_all[:, 0:512], ysq, g_tm, ALU.mult)

            ck(9)
            for mc in range(8):
                C.tr(psb(6)[:, mc * 128:(mc + 1) * 128], o_all[:, mc * 128:(mc + 1) * 128], ident_bf)
            C.cp(ACT, oT, psb(6).rearrange("p (a t) -> p a t", a=8))
            for half in range(2):
                for mc in range(8):
                    C.mm(ps[half], oT[:, mc, :], w_out_bf[:, mc, half * 512:(half + 1) * 512].k(mc),
                         start=(mc == 0), stop=(mc == 7))
            for half in range(2):
                hsl = slice(half * 512, (half + 1) * 512)
                C.stt(DVE, h1pre[:, hsl], xt[:, hsl], ALPHA, ps[half], ALU.mult, ALU.add)
            layernorm(h1pre, h1pre, LN_EPS)
            C.dma(SP, h1_dst, h1pre)

        stopped = False
        try:
            ck(0)
            for ti in range(NT):
                mixer_tile(ti, xp[ti * 128:(ti + 1) * 128, :], V(h1scr[ti * 128:(ti + 1) * 128, :], ("h1scr", ti)), False)
            ck(10)

            with nc.allow_non_contiguous_dma(reason="tiny state vector"):
                C.dma(SP, shp.rearrange("(c p) -> p c", p=128), plast, out_final=True)
            C.dma(SP, hgp.rearrange("h k v -> k h v"), SH, out_final=True)
            identf = cc("ident")
            for pr in range(4):
                C.tr(ps[2][0:64, pr * 128:(pr + 1) * 128], ST[:, pr, :], identf)
            rwo = T[0]
            C.cp(DVE, rwo[0:64, :], ps[2][0:64, :])
            C.dma(SP, rwp.rearrange("h v j -> v h j"), rwo[0:64, :].rearrange("p (h j) -> p h j", h=8), out_final=True)
            if SAMPLE:
                mixer_tile(NTP, xsm, V(h1scr[NTP * 128:(NTP + 1) * 128, :], ("h1scr", NTP)), True)
            if DBG:
                C.dma(SP, dbg_d, dbg, out_final=True)

            ck(11)
            P.barrier()
            arena.off = phase_mark
            w_up_bf = sb([128, 8, DFF], BF16, "w_up_bf")
            w_dn_bf = sb([128, 32, D], BF16, "w_dn_bf")
            for cb in range(8):
                C.dma(POOL, w_up_bf[:, :, cb * 512:(cb + 1) * 512].k(cb),
                      w_up[:, cb * 512:(cb + 1) * 512].rearrange("(kc p) n -> p kc n", p=128))
            C.dma(SP, lng, row(ln2_g).partition_broadcast(128))
            C.dma(SP, lnb, row(ln2_b).partition_broadcast(128))
            for fc in range(32):
                C.dma(POOL, w_dn_bf[:, fc, :].k(fc), w_down[fc * 128:(fc + 1) * 128, :])
            upT = sb([128, 32, 512], BF16, "upT")
            h1T = sb([128, 8, 512], BF16, "h1T")
            h1b = [sb([128, D], BF16, f"h1b{i}") for i in range(2)]
            h1r = [sb([128, D], F32, f"h1r{i}") for i in range(2)]
            rl = [sb([128, 512], F32, f"rl{i}") for i in range(2)]
            pre2 = sb([128, D], F32, "pre2")
            outb = [sb([128, D], F32, f"outb{i}") for i in range(2)]
            ntiles = NT + (1 if SAMPLE else 0)
            tiles = list(range(NT)) + ([NTP] if SAMPLE else [])
            groups = [tiles[i:i + 4] for i in range(0, NT, 4)]
            if SAMPLE:
                groups.append([NTP])
            gcount = 0
            tcount = 0
            for grp in groups:
                ng = len(grp)
                W = ng * 128
                for gi, tix in enumerate(grp):
                    hb = h1b[(tcount + gi) % 2]
                    C.dma(POOL, hb, V(h1scr[tix * 128:(tix + 1) * 128, :], ("h1scr", tix)))
                    bank = 6 + (gi % 2)
                    for kc in range(8):
                        C.tr(psb(bank)[:, kc * 128:(kc + 1) * 128], hb[:, kc * 128:(kc + 1) * 128], ident_bf)
                    C.cp(ACT, h1T[:, :, gi * 128:(gi + 1) * 128], psb(bank).rearrange("p (a t) -> p a t", a=8))
                for fc in range(32):
                    bank = fc % 2
                    for kc in range(8):
                        C.mm(ps[bank][:, 0:W], w_up_bf[:, kc, fc * 128:(fc + 1) * 128].k(fc // 4), h1T[:, kc, 0:W],
                             start=(kc == 0), stop=(kc == 7))
                    r = rl[fc % 2]
                    C.act(r[:, 0:W], ps[bank][:, 0:W], AF.Relu)
                    C.tt(POOL if fc % 2 else DVE, upT[:, fc, 0:W], r[:, 0:W], r[:, 0:W], ALU.mult)
                for gi, tix in enumerate(grp):
                    hr = h1r[(tcount + gi) % 2]
                    C.dma(SP, hr, V(h1scr[tix * 128:(tix + 1) * 128, :], ("h1scr", tix)))
                    for half in range(2):
                        bank = 2 + ((gi * 2 + half) % 4)
                        for fc in range(32):
                            C.mm(ps[bank], upT[:, fc, gi * 128:(gi + 1) * 128], w_dn_bf[:, fc, half * 512:(half + 1) * 512].k(fc),
                                 start=(fc == 0), stop=(fc == 31))
                        hsl = slice(half * 512, (half + 1) * 512)
                        C.stt(DVE, pre2[:, hsl], hr[:, hsl], ALPHA, ps[bank], ALU.mult, ALU.add)
                    ob = outb[(tcount + gi) % 2]
                    layernorm(pre2, ob, LN_EPS)
                    dst = ys if tix == NTP else yp[tix * 128:(tix + 1) * 128, :]
                    C.dma(SP, dst, ob, out_final=True)
                tcount += ng
                gcount += 1

        except _Stop:
            C.dma(SP, shp.rearrange("(c p) -> p c", p=128), plast, out_final=True)

        sems = {e: es.enter_context(nc.semaphore(f"s_{e}")) for e in ENGINES}
        rings = {e: [es.enter_context(nc.semaphore(f"r_{e}{i}")) for i in range(n)] for e, n in DMA_RING.items()}
        P.prepare(sems, rings)
        build.stats = dict(P.stats)
        build.arena_peak = arena.peak
        with nc.allow_low_precision(reason="bf16 matmul operands, fp32 accumulation"), \
                nc.allow_non_contiguous_dma(reason="tiny per-channel vectors / state layouts"), \
                nc.Block() as block:
            block.tensor(lambda eng: P.emit_engine(PE, eng))
            block.scalar(lambda eng: P.emit_engine(ACT, eng))
            block.vector(lambda eng: P.emit_engine(DVE, eng))
            block.gpsimd(lambda eng: P.emit_engine(POOL, eng))
            block.sync(lambda eng: P.emit_engine(SP, eng))
    return nc


IN_NAMES = ["w_in", "shift_mu", "w0", "w1u", "a0", "a1u", "g1u", "k_k", "k_a", "r_k", "ln_x_w", "ln_x_b",
            "lb_logits", "hg_norm_w", "w_out", "ln1_g", "ln1_b", "w_up", "w_down", "ln2_g", "ln2_b"]


def make_in_maps(inputs, n_cores=8):
    f = lambda a: np.ascontiguousarray(np.asarray(a, dtype=np.float32))
    shared = {}
    for k in IN_NAMES:
        a = f(inputs[k])
        a = a[0] if k != "lb_logits" else a
        if k == "r_k":
            a = a.reshape(-1)
        shared[k] = np.ascontiguousarray(a)
    shared["consts"] = CONSTS
    maps = []
    for c in range(n_cores):
        m = dict(shared)
        m["xp"] = f(inputs["x_prompt"][c])
        m["xs"] = f(inputs["x_sample"][c * NB:(c + 1) * NB]).reshape(NB * TS, D)
        m["srw"] = f(inputs["state_rwkv"][0, c * NB:(c + 1) * NB])
        m["shg"] = f(inputs["state_hgrn"][0, c * NB:(c + 1) * NB])
        m["ssh"] = f(inputs["state_shift"][0, c * NB:(c + 1) * NB])
        maps.append(m)
    return maps


_NC_CACHE = {}


def kernel(**inputs):
    if "nc" not in _NC_CACHE:
        _NC_CACHE["nc"] = build()
    nc = _NC_CACHE["nc"]
    maps = make_in_maps(inputs)
    res = run_bass_kernel_spmd(nc, maps, core_ids=list(range(8)))
    R = res.results
    y_prompt = np.stack([R[c]["yp"] for c in range(8)]).astype(np.float32)
    y_sample = np.concatenate([R[c]["ys"].reshape(NB, TS, D) for c in range(8)]).astype(np.float32)
    rw_p = np.stack([R[c]["rwp"] for c in range(8)])[None].astype(np.float32)
    rw_s = np.concatenate([R[c]["rws"] for c in range(8)])[None].astype(np.float32)
    hg_p = np.stack([R[c]["hgp"] for c in range(8)])[None].astype(np.float32)
    hg_s = np.concatenate([R[c]["hgs"] for c in range(8)])[None].astype(np.float32)
    sh_p = np.stack([R[c]["shp"] for c in range(8)])[None].astype(np.float32)
    sh_s = np.concatenate([R[c]["shs"] for c in range(8)])[None].astype(np.float32)
    return (y_prompt, y_sample, rw_p, rw_s, hg_p, hg_s, sh_p, sh_s)
```

```python
import numpy as np
import concourse.bass as bass
import concourse.mybir as mybir
from concourse.bass_utils import run_bass_kernel_spmd

F32 = mybir.dt.float32
BF16 = mybir.dt.bfloat16
AF = mybir.ActivationFunctionType
ALU = mybir.AluOpType
AX = mybir.AxisListType

DEBUG_LINES = False
LINE_OF = {}
PE, ACT, DVE, POOL, SP = "pe", "act", "dve", "pool", "sp"
ENGINES = (PE, ACT, DVE, POOL, SP)
DMA_RING = {SP: 12, ACT: 4, POOL: 8}


class Res:
    __slots__ = ("name", "last_w", "readers")

    def __init__(self, name):
        self.name = name
        self.last_w = None
        self.readers = []


class Op:
    __slots__ = ("eng", "fn", "deps", "is_dma", "signal", "idx", "dma_no", "extra_wait", "rg")

    def __init__(self, eng, fn, is_dma):
        self.eng = eng
        self.fn = fn
        self.deps = set()
        self.is_dma = is_dma
        self.signal = False
        self.idx = None
        self.dma_no = None
        self.extra_wait = None
        self.rg = None


def _pe_inorder_ok(d, o):
    return d.rg is None or o.rg is None or d.rg == o.rg


class Prog:
    def __init__(self):
        self.ops = {e: [] for e in ENGINES}
        self.order = []
        self.n_dma = {e: 0 for e in ENGINES}
        self.dma_ops = {e: [] for e in ENGINES}
        self.out_dmas = []

    def op(self, eng, fn, reads=(), writes=(), dma=False, out=False):
        o = Op(eng, fn, dma)
        for r in reads:
            if r.last_w is not None:
                o.deps.add(r.last_w)
        for w in writes:
            if w.last_w is not None:
                o.deps.add(w.last_w)
            for rd in w.readers:
                o.deps.add(rd)
        for r in reads:
            r.readers.append(o)
        for w in writes:
            w.last_w = o
            w.readers = []
        if getattr(self, "barrier_left", None) and eng in self.barrier_left:
            self.barrier_left.discard(eng)
            o.deps.update(self.pending_barrier)
        o.deps.discard(o)
        if dma:
            o.dma_no = self.n_dma[eng]
            self.n_dma[eng] += 1
            self.dma_ops[eng].append(o)
            o.signal = True
            if out:
                self.out_dmas.append(o)
        self.ops[eng].append(o)
        self.order.append(o)
        return o

    def barrier(self):
        pend = []
        for e in ENGINES:
            comp = [o for o in self.ops[e] if not o.is_dma]
            if comp:
                pend.append(comp[-1])
            pend.extend(self.dma_ops[e][-DMA_RING.get(e, 0):] if e in DMA_RING else [])
        self.pending_barrier = pend
        self.barrier_left = set(ENGINES)

    def prepare(self, sems, rings):
        for o in self.order:
            for d in o.deps:
                if d.is_dma:
                    continue
                if d.eng == o.eng and d.eng == PE and not o.is_dma and _pe_inorder_ok(d, o):
                    continue
                d.signal = True
        sig = {}
        for e in ENGINES:
            c = 0
            for o in self.ops[e]:
                if o.is_dma:
                    R = len(rings[e])
                    sig[o] = (rings[e][o.dma_no % R], 16 * (o.dma_no // R + 1))
                elif o.signal:
                    c += 1
                    sig[o] = (sems[e], c)
        self.sig = sig
        self.rings = rings
        self.stats = {e: len(self.ops[e]) for e in ENGINES}

    def emit_engine(self, e, eng):
        sig, rings = self.sig, self.rings
        waited = {}

        def wait(sem, val):
            k = id(sem)
            if waited.get(k, 0) >= val:
                return
            waited[k] = val
            eng.wait_ge(sem, val)

        for o in self.ops[e]:
            for d in o.deps:
                if d not in sig:
                    continue
                if d.eng == e and not d.is_dma and not o.is_dma and e == PE and _pe_inorder_ok(d, o):
                    continue
                s, v = sig[d]
                wait(s, v)
            if o.is_dma:
                R = len(rings[e])
                if o.dma_no >= R:
                    wait(rings[e][o.dma_no % R], 16 * (o.dma_no // R))
            ins = o.fn(eng)
            if DEBUG_LINES:
                LINE_OF[str(getattr(getattr(ins, "ins", ins), "name", ins))] = o.extra_wait
            if o in sig:
                s, v = sig[o]
                ins.then_inc(s, 16 if o.is_dma else 1)
        if e == SP:
            for q in ENGINES:
                if q not in rings:
                    continue
                for o in self.dma_ops[q][-len(rings[q]):]:
                    s, v = sig[o]
                    wait(s, v)


class V:
    __slots__ = ("ap", "key")

    def __init__(self, ap, key):
        self.ap = ap
        self.key = key

    def __getitem__(self, idx):
        return V(self.ap[idx], self.key)

    def k(self, sub):
        return V(self.ap, (self.key, sub))

    @property
    def shape(self):
        return self.ap.shape

    def rearrange(self, *a, **kw):
        return V(self.ap.rearrange(*a, **kw), self.key)

    def unsqueeze(self, ax):
        return V(self.ap.unsqueeze(ax), self.key)

    def to_broadcast(self, shape):
        return V(self.ap.to_broadcast(list(shape)), self.key)

    def bitcast(self, dt):
        return V(self.ap.bitcast(dt), self.key)


def _ap(x):
    return x.ap if isinstance(x, V) else x


def _isnum(x):
    return isinstance(x, (int, float))


class Arena:
    def __init__(self, nc, es, nbytes):
        self.n2 = nbytes // 2
        self.t = es.enter_context(nc.sbuf_tensor("arena", [128, self.n2], BF16))
        self.off = 0
        self.cnt = 0
        self.peak = 0

    def alloc(self, shape, dt, name=None):
        shape = list(shape)
        esz = 4 if dt == F32 else 2
        n = int(np.prod(shape[1:]))
        nbytes = (n * esz + 3) // 4 * 4
        o = self.off
        assert o + nbytes <= self.n2 * 2, f"arena overflow allocating {name} {shape}: {o}+{nbytes} > {self.n2 * 2}"
        self.off += nbytes
        self.peak = max(self.peak, self.off)
        ap = self.t[0:shape[0], o // 2:o // 2 + nbytes // 2]
        if esz == 4:
            ap = ap.bitcast(F32)
        ap = ap[:, 0:n]
        if len(shape) == 3:
            ap = ap.rearrange("p (a b) -> p a b", a=shape[1])
        elif len(shape) == 4:
            ap = ap.rearrange("p (a b c) -> p a b c", a=shape[1], b=shape[2])
        self.cnt += 1
        return V(ap, name or f"t{self.cnt}")


class Ctx:
    def __init__(self, nc, P, arena):
        self.nc, self.P, self.arena = nc, P, arena
        self.res = {}

    def sb(self, shape, dt, name=None):
        return self.arena.alloc(shape, dt, name)

    def R(self, x):
        k = x.key if isinstance(x, V) else (x.name, None)
        r = self.res.get(k)
        if r is None:
            r = self.res[k] = Res(k)
        return r

    def rec(self, eng, fn, outs, ins, dma=False, out=False):
        reads = [self.R(i) for i in ins if i is not None and not _isnum(i)]
        writes = [self.R(o) for o in outs]
        writes += [r for r in reads if isinstance(r.name, str) and r.name.startswith("ps")]
        o_ = self.P.op(eng, fn, reads=reads, writes=writes, dma=dma, out=out)
        if DEBUG_LINES:
            import sys as _s
            f = _s._getframe(1)
            while f.f_code.co_name not in ("mixer_tile", "build", "layernorm") and f.f_back is not None:
                f = f.f_back
            o_.extra_wait = f.f_lineno
        return o_

    def mm(self, out, lhsT, rhs, start=True, stop=True):
        o, l, r = _ap(out), _ap(lhsT), _ap(rhs)
        op = self.rec(PE, lambda e: e.matmul(o, lhsT=l, rhs=r, start=start, stop=stop,
                                             skip_group_check=True), [out], [lhsT, rhs])
        kr = l.shape[0]
        if kr < 128:
            op.rg = (kr, l.base_partition())
        return op

    def tr(self, out, in_, ident):
        o, i, d = _ap(out), _ap(in_), _ap(ident)
        return self.rec(PE, lambda e: e.transpose(o, i, d), [out], [in_, ident])

    def act(self, out, in_, func, bias=None, scale=1.0):
        o, i = _ap(out), _ap(in_)
        kw = {}
        if bias is not None:
            kw["bias"] = _ap(bias)
        s = _ap(scale)
        return self.rec(ACT, lambda e: e.activation(out=o, in_=i, func=func, scale=s, **kw), [out],
                        [in_, bias, scale])

    def tt(self, eng, out, a, b, op):
        o, x, y = _ap(out), _ap(a), _ap(b)
        return self.rec(eng, lambda e: e.tensor_tensor(out=o, in0=x, in1=y, op=op), [out], [a, b])

    def ts(self, eng, out, a, s1, op0, s2=None, op1=None):
        o, x, v1, v2 = _ap(out), _ap(a), _ap(s1), _ap(s2)
        if op1 is None:
            f = lambda e: e.tensor_scalar(out=o, in0=x, scalar1=v1, scalar2=None, op0=op0)
        else:
            f = lambda e: e.tensor_scalar(out=o, in0=x, scalar1=v1, scalar2=v2, op0=op0, op1=op1)
        return self.rec(eng, f, [out], [a, s1, s2])

    def stt(self, eng, out, in0, scalar, in1, op0, op1):
        o, x, y, s = _ap(out), _ap(in0), _ap(in1), _ap(scalar)
        return self.rec(eng, lambda e: e.scalar_tensor_tensor(out=o, in0=x, scalar=s, in1=y, op0=op0, op1=op1),
                        [out], [in0, in1, scalar])

    def cp(self, eng, out, in_):
        o, i = _ap(out), _ap(in_)
        if eng == ACT:
            return self.rec(ACT, lambda e: e.activation(out=o, in_=i, func=AF.Copy), [out], [in_])
        return self.rec(eng, lambda e: e.tensor_copy(out=o, in_=i), [out], [in_])

    def recip(self, out, in_):
        o, i = _ap(out), _ap(in_)
        return self.rec(DVE, lambda e: e.reciprocal(out=o, in_=i), [out], [in_])

    def scan(self, out, d0, d1):
        o, a, b = _ap(out), _ap(d0), _ap(d1)
        return self.rec(DVE, lambda e: e.tensor_tensor_scan(out=o, data0=a, data1=b, initial=0.0,
                                                            op0=ALU.mult, op1=ALU.add), [out], [d0, d1])

    def red(self, eng, out, in_, op=ALU.add):
        o, i = _ap(out), _ap(in_)
        return self.rec(eng, lambda e: e.tensor_reduce(out=o, in_=i, axis=AX.X, op=op), [out], [in_])

    def memset(self, eng, out, val):
        o = _ap(out)
        return self.rec(eng, lambda e: e.memset(o, val), [out], [])

    def dma(self, eng, out, in_, out_final=False, extra_out=()):
        o, i = _ap(out), _ap(in_)
        return self.rec(eng, lambda e: e.dma_start(out=o, in_=i), [out, *extra_out], [in_], dma=True, out=out_final)

    def bn_stats(self, out, in_):
        o, i = _ap(out), _ap(in_)
        return self.rec(DVE, lambda e: e.bn_stats(out=o, in_=i), [out], [in_])

    def bn_aggr(self, out, in_):
        o, i = _ap(out), _ap(in_)
        return self.rec(DVE, lambda e: e.bn_aggr(out=o, in_=i), [out], [in_])


D = 1024
PJ = 3840
RWP = 1792
NTP = 16
NB = 16
TS = 8
DFF = 4096
ALPHA = 2.0 ** 0.25
CDEC = -float(np.exp(-0.5))
LN_EPS = 1e-5
GN_EPS = 64e-5
RMS_EPS = 1e-6
ARENA_BYTES = 212800


def make_consts():
    s = np.arange(128)[:, None]
    t = np.arange(128)[None, :]
    cols = {}
    cols["ident"] = (s == t)
    cols["mS_p"] = (s < t)
    cols["mI_p"] = (s <= t)
    cols["mST_p"] = (t < s)
    same = (s // TS) == (t // TS)
    cols["mS_s"] = (s < t) & same
    cols["mI_s"] = (s <= t) & same
    cols["mST_s"] = (t < s) & same
    cols["reset_p"] = np.broadcast_to(t != 0, (128, 128))
    cols["reset_s"] = np.broadcast_to((t % TS) != 0, (128, 128))
    cols["bdones"] = (s // 64) == (t // 64)
    cols["hsel"] = (s // 64) == np.arange(2)[None, :]
    cols["cm"] = (s // TS) == np.arange(NB)[None, :]
    cols["i64s"] = (s % 64) == np.arange(64)[None, :]
    off = {}
    parts = []
    o = 0
    for k, v in cols.items():
        v = np.asarray(v, np.float32)
        off[k] = (o, o + v.shape[1])
        o += v.shape[1]
        parts.append(v)
    return np.ascontiguousarray(np.concatenate(parts, axis=1)), off


CONSTS, COFF = make_consts()
NCONST = CONSTS.shape[1]


class _Stop(Exception):
    pass


def build(NT=NTP, SAMPLE=True, DBG=False, STAGE=99):
    from contextlib import ExitStack
    nc = bass.Bass("TRN2", target_bir_lowering=False)

    def din(name, shape):
        return nc.dram_tensor(name, list(shape), F32, kind="ExternalInput").ap()

    def dout(name, shape):
        return nc.dram_tensor(name, list(shape), F32, kind="ExternalOutput").ap()

    xp = din("xp", [NTP * 128, D]); xsm = din("xs", [128, D])
    srw = din("srw", [NB, 8, 64, 64]); shg = din("shg", [NB, 4, 128, 128]); ssh = din("ssh", [NB, RWP])
    w_in = din("w_in", [D, PJ]); shift_mu = din("shift_mu", [RWP]); w0 = din("w0", [512])
    w1u = din("w1u", [64, 512]); a0 = din("a0", [512]); a1u = din("a1u", [64, 512]); g1u = din("g1u", [128, 512])
    k_k = din("k_k", [512]); k_a = din("k_a", [512]); r_k = din("r_k", [512])
    ln_x_w = din("ln_x_w", [512]); ln_x_b = din("ln_x_b", [512]); lb_logits = din("lb_logits", [2, 512])
    hg_norm_w = din("hg_norm_w", [512]); w_out = din("w_out", [D, D]); ln1_g = din("ln1_g", [D]); ln1_b = din("ln1_b", [D])
    w_up = din("w_up", [D, DFF]); w_down = din("w_down", [DFF, D]); ln2_g = din("ln2_g", [D]); ln2_b = din("ln2_b", [D])
    cst_d = din("consts", [128, NCONST])
    yp = dout("yp", [NTP * 128, D]); ys = dout("ys", [128, D])
    rwp = dout("rwp", [8, 64, 64]); rws = dout("rws", [NB, 8, 64, 64])
    hgp = dout("hgp", [4, 128, 128]); hgs = dout("hgs", [NB, 4, 128, 128])
    shp = dout("shp", [RWP]); shs = dout("shs", [NB, RWP])
    h1scr = nc.dram_tensor("h1scr", [(NTP + 1) * 128, D], F32).ap()
    if DBG:
        dbg_d = dout("dbg", [128, 4096])

    def row(v):
        return v.rearrange("(o n) -> o n", o=1)

    P = Prog()
    with ExitStack() as es:
        arena = Arena(nc, es, ARENA_BYTES)
        C = Ctx(nc, P, arena)
        sb = C.sb
        ps = [V(es.enter_context(nc.psum_tensor(f"ps{i}", [128, 512], F32))[:], f"ps{i}") for i in range(8)]

        def psv(i, a):
            return ps[i].rearrange("p (a t) -> p a t", a=a)

        def psb(i):
            return ps[i].bitcast(BF16)

        def bc3(ap2, n):
            return ap2.unsqueeze(2).to_broadcast([ap2.shape[0], ap2.shape[1], n])

        def v3(t):
            return t.rearrange("p (a t) -> p a t", a=4)

        ident_bf = sb([128, 128], BF16, "ident_bf")
        lng = sb([128, D], F32, "lng"); lnb = sb([128, D], F32, "lnb")
        C.dma(SP, lng, row(ln1_g).partition_broadcast(128))
        C.dma(SP, lnb, row(ln1_b).partition_broadcast(128))
        bnst = sb([128, 12], F32, "bnst"); mv = sb([128, 2], F32, "mv"); rstd1 = sb([128, 1], F32, "rstd1")
        fence_ln = sb([128, 2], F32, "fence_ln")
        if DBG:
            dbg = sb([128, 4096], F32, "dbg")
            C.memset(POOL, dbg, 0.0)
            dbg_pos = [0]
            dbg_map = {}

            def dump(name, v, n):
                a = dbg_pos[0]
                shape = list(v.shape)
                dst = dbg[0:shape[0], a:a + n]
                if len(shape) == 3:
                    dst = dst.rearrange("p (a b) -> p a b", a=shape[1])
                C.cp(POOL, dst, v)
                dbg_pos[0] += n
                dbg_map[name] = (a, n, shape)
            build.dbg_map = dbg_map
        else:
            def dump(name, v, n):
                return None
        phase_mark = arena.off

        def layernorm(src, dst, eps):
            for half in range(2):
                C.bn_stats(bnst[:, half * 6:(half + 1) * 6], src[:, half * 512:(half + 1) * 512])
            C.bn_aggr(mv, bnst)
            C.act(rstd1, mv[:, 1:2], AF.Ln, bias=eps)
            C.act(rstd1, rstd1, AF.Exp, scale=-0.5)
            C.ts(DVE, dst, src, mv[:, 0:1], ALU.subtract, rstd1[:, 0:1], ALU.mult)
            halves = []
            for eng_, hsl in ((DVE, slice(0, 640)), (POOL, slice(640, 1024))):
                dk = dst[:, hsl].k(hsl.start)
                halves.append(dk)
                o, a_, g_, b_ = _ap(dk), _ap(dst[:, hsl]), _ap(lng[:, hsl]), _ap(lnb[:, hsl])
                C.rec(eng_, lambda e, o=o, a_=a_, g_=g_: e.tensor_tensor(out=o, in0=a_, in1=g_, op=ALU.mult), [dk], [dst, lng])
                C.rec(eng_, lambda e, o=o, b_=b_: e.tensor_tensor(out=o, in0=o, in1=b_, op=ALU.add), [dk], [dk, lnb])
            C.rec(POOL, lambda e: e.memset(_ap(fence_ln), 0.0), [dst, fence_ln], halves)

        F32_CONSTS = ("ident", "reset_p", "reset_s", "hsel", "cm", "i64s")
        BF_CONSTS = ("mS_p", "mI_p", "mST_p", "mS_s", "mI_s", "mST_s", "bdones")
        cviews = {}
        nf = sum(COFF[k][1] - COFF[k][0] for k in F32_CONSTS)
        nb_ = sum(COFF[k][1] - COFF[k][0] for k in BF_CONSTS)
        cstf = sb([128, nf], F32, "cstf"); cstb = sb([128, nb_], BF16, "cstb")
        o_ = 0
        for k_ in F32_CONSTS:
            a, b = COFF[k_]
            C.dma(SP, cstf[:, o_:o_ + b - a].k(k_), cst_d[:, a:b])
            cviews[k_] = cstf[:, o_:o_ + b - a].k(k_)
            o_ += b - a
        o_ = 0
        for k_ in BF_CONSTS:
            a, b = COFF[k_]
            C.dma(POOL, cstb[:, o_:o_ + b - a].k(k_), cst_d[:, a:b])
            cviews[k_] = cstb[:, o_:o_ + b - a].k(k_)
            o_ += b - a

        def cc(name):
            return cviews[name]

        C.cp(POOL, ident_bf, cc("ident"))
        bdones_bf = cc("bdones")
        mu14 = sb([128, 14], F32, "mu14")
        kk4 = sb([128, 4], F32, "kk4"); ka4 = sb([128, 4], F32, "ka4"); rk4 = sb([128, 4], F32, "rk4")
        w04 = sb([128, 4], F32, "w04"); a04 = sb([128, 4], F32, "a04")
        lbl = sb([128, 2, 4], F32, "lbl")
        with nc.allow_non_contiguous_dma(reason="tiny per-channel parameter vectors"):
            C.dma(SP, mu14, shift_mu.rearrange("(c p) -> p c", p=128))
            C.dma(SP, kk4, k_k.rearrange("(c p) -> p c", p=128))
            C.dma(SP, ka4, k_a.rearrange("(c p) -> p c", p=128))
            C.dma(SP, rk4, r_k.rearrange("(c p) -> p c", p=128))
            C.dma(SP, w04, w0.rearrange("(c p) -> p c", p=128))
            C.dma(SP, a04, a0.rearrange("(c p) -> p c", p=128))
            C.dma(SP, lbl, lb_logits.rearrange("l (c p) -> p l c", p=128))
        WA = sb([128, 512], BF16, "WA")
        C.dma(POOL, WA[0:64, :].k("lo"), w1u)
        C.dma(POOL, WA[64:128, :].k("hi"), a1u)
        g1u_bf = sb([128, 512], BF16, "g1u_bf")
        C.dma(POOL, g1u_bf, g1u)
        lnxw = sb([128, 512], BF16, "lnxw"); lnxb = sb([128, 512], BF16, "lnxb"); hgw = sb([128, 512], BF16, "hgw")
        C.dma(POOL, lnxw, row(ln_x_w).partition_broadcast(128))
        C.dma(POOL, lnxb, row(ln_x_b).partition_broadcast(128))
        C.dma(POOL, hgw, row(hg_norm_w).partition_broadcast(128))
        lb4 = sb([128, 4], F32, "lb4"); oml4 = sb([128, 4], F32, "oml4"); etmp = sb([128, 4], F32, "etmp")
        C.tt(DVE, etmp, lbl[:, 1, :], lbl[:, 0, :], ALU.subtract)
        C.act(etmp, etmp, AF.Exp)
        C.ts(DVE, lb4, etmp, 1.0, ALU.add)
        C.recip(lb4, lb4)
        C.tt(DVE, oml4, etmp, lb4, ALU.mult)
        RKsel = sb([128, 4, 2], BF16, "RKsel")
        C.tt(DVE, RKsel, bc3(rk4, 2), cc("hsel").unsqueeze(1).to_broadcast([128, 4, 2]), ALU.mult)

        ov_lo = arena.off
        w_in_bf = sb([128, 8, PJ], BF16, "w_in_bf")
        ov_hi = arena.off
        for kc in range(8):
            C.dma(POOL, w_in_bf[:, kc, :].k(kc), w_in[kc * 128:(kc + 1) * 128, :])
        w_out_bf = sb([128, 8, D], BF16, "w_out_bf")
        for kc in range(8):
            C.dma(POOL, w_out_bf[:, kc, :].k(kc), w_out[kc * 128:(kc + 1) * 128, :])

        ST = sb([128, 4, 64], F32, "ST"); ST_bf = sb([128, 4, 64], BF16, "ST_bf")
        SH = sb([128, 4, 128], F32, "SH"); SH_bf = sb([128, 4, 128], BF16, "SH_bf")
        plast = sb([128, 14], F32, "plast")
        for t_ in (ST, ST_bf, SH, SH_bf, plast):
            C.memset(POOL, t_, 0.0)

        def prompt_state_outputs():
            with nc.allow_non_contiguous_dma(reason="tiny state vector"):
                C.dma(SP, shp.rearrange("(c p) -> p c", p=128), plast, out_final=True)
            C.dma(SP, hgp.rearrange("h k v -> k h v"), SH, out_final=True)
            identf = cc("ident")
            for pr in range(4):
                C.tr(ps[2][0:64, pr * 128:(pr + 1) * 128], ST[:, pr, :], identf)
            rwo = T[0]
            C.cp(DVE, rwo[0:64, :], ps[2][0:64, :])
            C.dma(SP, rwp.rearrange("h v j -> v h j"), rwo[0:64, :].rearrange("p (h j) -> p h j", h=8), out_final=True)
            if DBG:
                C.dma(SP, dbg_d, dbg, out_final=True)


        x_t = [sb([128, D], F32, f"x_t{i}") for i in range(2)]
        x_bf = sb([128, D], BF16, "x_bf")
        xT = sb([128, 8, 128], BF16, "xT")
        pr_ = sb([128, 14, 129], F32, "pr")
        xs = sb([128, 14, 128], F32, "xsft")
        T = [sb([128, 512], F32, f"T{i}") for i in range(10)]
        z12 = sb([128, 128], BF16, "z12"); sg_bf = sb([128, 128], BF16, "sg_bf"); tqb = sb([128, 512], BF16, "tqb")
        bhT = sb([128, 4, 128], BF16, "bhT"); khT = sb([128, 4, 128], BF16, "khT"); vT = sb([128, 4, 128], BF16, "vT")
        khTb = sb([128, 4, 128], BF16, "khTb")
        fence2 = sb([128, 2], F32, "fence2")
        HB = []
        for i in range(2):
            HB.append(dict(
                AR=sb([128, 4, 2, 128], BF16, f"AR{i}"), bT=sb([128, 4, 128], BF16, f"bT{i}"), kT=sb([128, 4, 128], BF16, f"kT{i}"),
                A_tm=sb([128, 512], BF16, f"A_tm{i}"), Bh_tm=sb([128, 512], BF16, f"Bh_tm{i}"),
                Kh_tm=sb([128, 512], BF16, f"Kh_tm{i}"), V_tm=sb([128, 512], BF16, f"V_tm{i}"),
                qTb=sb([128, 4, 128], BF16, f"qTb{i}"), kTb=sb([128, 4, 128], BF16, f"kTb{i}"),
                khat_tm=sb([128, 512], BF16, f"khat_tm{i}"), i_tm=sb([128, 512], BF16, f"i_tm{i}"),
                gs_t=sb([128, 512], BF16, f"gs_t{i}"), g_tm=sb([128, 512], BF16, f"g_tm{i}"),
                bon8=sb([128, 8], F32, f"bon8{i}"), gC=sb([128, 4, NB], F32, f"gC{i}"), decH=sb([128, 4, NB], F32, f"decH{i}")))
        Pm = sb([128, 8, 128], BF16, "Pm"); Tm = sb([128, 8, 128], BF16, "Tm"); RR = sb([128, 8, 128], BF16, "RR")
        NrbT = sb([128, 8, 128], BF16, "NrbT"); AkT = sb([128, 8, 128], BF16, "AkT"); NrkT = sb([128, 8, 128], BF16, "NrkT")
        TTf = sb([128, 8, 128], BF16, "TTf")
        W1T = sb([128, 4, 128], BF16, "W1T")
        Z_tm = sb([128, 512], BF16, "Z_tm"); U_tm = sb([128, 512], BF16, "U_tm")
        st16 = sb([128, 16], F32, "st16"); m8 = sb([128, 8], F32, "m8"); r8 = sb([128, 8], F32, "r8")
        o_all = sb([128, D], BF16, "o_all"); oT = sb([128, 8, 128], BF16, "oT")
        h1pre = sb([128, D], F32, "h1pre")
        attT = sb([128, 4, 128], BF16, "attT")
        s4 = sb([128, 4], F32, "s4"); rr4 = sb([128, 4], F32, "rr4")
        BT0 = h1pre[:, 0:512]; BT1 = h1pre[:, 512:1024]
        SHtmp = v3(BT1)
        STtmp = BT1[:, 0:256].rearrange("p (a v) -> p a v", a=4)
        identb4 = ident_bf.unsqueeze(1).to_broadcast([128, 4, 128])
        save_off = arena.off
        arena.off = ov_lo
        scrA = [sb([128, 2048], F32, f"scrA{i}") for i in range(2)]
        S0T32 = sb([128, NB, 4, 64], F32, "S0T32")
        S0Tb = sb([128, NB, 4, 64], BF16, "S0Tb")
        sshT = sb([128, 14, NB], F32, "sshT"); lastp = sb([128, 14, NB], F32, "lastp")
        EW = [sb([128, 4, 128], BF16, f"EW{i}") for i in range(2)]
        ER = [sb([128, 4, 128], BF16, f"ER{i}") for i in range(2)]
        EQ = [sb([128, 4, 128], BF16, f"EQ{i}") for i in range(2)]
        Ub = sb([128, 512], BF16, "Ub"); Vb = sb([128, 512], BF16, "Vb"); khb = sb([128, 512], BF16, "khb")
        Dg = sb([128, 4, 64], F32, "Dg")
        S0h = [sb([128, 4, 128], F32, f"S0h{i}") for i in range(2)]
        S0hb = sb([128, 4, 128], BF16, "S0hb")
        Sn = sb([128, 512], F32, "Sn")
        fence_t = sb([128, 2], F32, "fence_t")
        assert arena.off <= ov_hi, (arena.off, ov_hi)
        ov_bufs = scrA + [S0T32, S0Tb, sshT, lastp] + EW + ER + EQ + [Ub, Vb, khb, Dg] + S0h + [S0hb, Sn, fence_t]
        arena.off = save_off
        ssh_tm = scrA[0][0:NB, 0:RWP]
        hh_order = (0, 2, 4, 6, 1, 3, 5, 7)
        heads_of = [(0, 1, 2, 3), (4, 5, 6, 7)]
        PA, PB_ = 6, 7

        cb = cc

        def cfg(sample):
            sfx = "_s" if sample else "_p"
            return dict(nch=NB if sample else 1, Cn=TS if sample else 128, L=3 if sample else 7,
                        mS=cb("mS" + sfx), mI=cb("mI" + sfx), mST=cb("mST" + sfx), reset=cc("reset" + sfx))

        def stageA(ti, x_src, sample):
            g_ = cfg(sample)
            nch, Cn, reset = g_["nch"], g_["Cn"], g_["reset"]
            H = HB[ti % 2]
            AR, bT, kT = H["AR"], H["bT"], H["kT"]
            xt = x_t[ti % 2]
            C.dma(SP, xt, x_src)
            C.dma(POOL, x_bf, x_src)
            for kc in range(8):
                C.tr(psb(PB_)[:, kc * 128:(kc + 1) * 128], x_bf[:, kc * 128:(kc + 1) * 128], ident_bf)
            C.cp(ACT, xT, psb(PB_).rearrange("p (a t) -> p a t", a=8))
            yield
            sig, kq, fgl, bcs, eb, enb, ebl, sq_, eg = T[1], T[4], T[0], T[2], T[3], T[5], T[6], T[7], T[8]

            def proj_fm(c0, n, bank):
                for j in range(n):
                    c = c0 + j
                    for kc in range(8):
                        C.mm(ps[bank][:, j * 128:(j + 1) * 128], w_in_bf[:, kc, c * 128:(c + 1) * 128].k(kc),
                             xT[:, kc, :], start=(kc == 0), stop=(kc == 7))

            def proj_tm(col0, bank):
                for kc in range(8):
                    C.mm(ps[bank], xT[:, kc, :], w_in_bf[:, kc, col0:col0 + 512].k(kc), start=(kc == 0), stop=(kc == 7))

            proj_fm(0, 4, PA); C.cp(ACT, pr_[:, 0:4, 1:129], psv(PA, 4)); yield
            proj_fm(4, 4, PB_); C.cp(ACT, pr_[:, 4:8, 1:129], psv(PB_, 4)); yield
            proj_fm(8, 4, PA); C.cp(ACT, pr_[:, 8:12, 1:129], psv(PA, 4)); yield
            proj_fm(12, 2, PB_); C.cp(ACT, pr_[:, 12:14, 1:129], psv(PB_, 4)[:, 0:2, :]); yield
            prev, cur = pr_[:, :, 0:128], pr_[:, :, 1:129]
            if not sample:
                C.cp(POOL, pr_[:, :, 0:1], plast.unsqueeze(2))
                C.cp(POOL, plast.unsqueeze(2), pr_[:, :, 128:129])
            else:
                C.memset(POOL, pr_[:, :, 0:1], 0.0)
            proj_fm(14, 4, PA)
            C.act(sq_, ps[PA], AF.Sigmoid)
            C.tt(DVE, sq_, ps[PA], sq_, ALU.mult)
            yield
            proj_fm(18, 4, PB_)
            C.act(sig, ps[PB_], AF.Sigmoid)
            yield
            proj_tm(RWP + 1024, PA)
            C.cp(ACT, H["i_tm"], ps[PA])
            yield
            proj_tm(RWP + 1536, PB_)
            C.act(eg, ps[PB_], AF.Sigmoid)
            C.tt(DVE, H["gs_t"], ps[PB_], eg, ALU.mult)
            yield
            if sample:
                C.rec(POOL, lambda e: e.memset(_ap(fence_t), 0.0),
                      [w_in_bf[:, kc, :].k(kc) for kc in range(8)] + ov_bufs
                      + [S0T32[:, b, :, :].k(b) for b in range(NB)] + [S0Tb[:, b, :, :].k(b) for b in range(NB)], [])
                for t_ in EW + ER + EQ:
                    C.memset(POOL, t_, 0.0)
                C.dma(SP, ssh_tm, ssh)
                for c in range(14):
                    C.tr(ps[PA][:, c * NB:(c + 1) * NB], ssh_tm[:, c * 128:(c + 1) * 128], cc("ident")[0:NB, 0:NB])
                C.cp(DVE, sshT, ps[PA][:, 0:14 * NB].rearrange("p (c b) -> p c b", c=14))
            for eng_, c0, c1 in ((DVE, 0, 8), (POOL, 8, 14)):
                C.tt(eng_, xs[:, c0:c1, :].k(c0), prev[:, c0:c1, :], cur[:, c0:c1, :], ALU.subtract)
                C.tt(eng_, xs[:, c0:c1, :].k(c0), xs[:, c0:c1, :].k(c0), bc3(mu14[:, c0:c1], 128), ALU.mult)
                C.tt(eng_, xs[:, c0:c1, :].k(c0), xs[:, c0:c1, :].k(c0), cur[:, c0:c1, :], ALU.add)
            C.rec(POOL, lambda e: e.memset(_ap(fence2), 0.0), [xs, fence2], [xs[:, 0:8, :].k(0), xs[:, 8:14, :].k(8)])
            if sample:
                cur4 = cur.rearrange("p c (b t) -> p c b t", t=TS)
                xs4 = xs.rearrange("p c (b t) -> p c b t", t=TS)
                cur0, xs0 = cur4[:, :, :, 0], xs4[:, :, :, 0]
                C.tt(POOL, xs0, sshT, cur0, ALU.subtract)
                C.tt(POOL, xs0, xs0, bc3(mu14, NB), ALU.mult)
                C.tt(POOL, xs0, xs0, cur0, ALU.add)
                C.cp(POOL, lastp, cur4[:, :, :, TS - 1])
                for g0 in range(0, 14, 4):
                    bk = PA if (g0 // 4) % 2 == 0 else PB_
                    n = min(4, 14 - g0)
                    for j in range(n):
                        C.tr(ps[bk][0:NB, j * 128:(j + 1) * 128], lastp[:, g0 + j, :], cc("ident"))
                    C.cp(ACT, ssh_tm[:, g0 * 128:(g0 + n) * 128], ps[bk][0:NB, 0:n * 128])
                C.dma(SP, shs, ssh_tm, out_final=True)
            yield
            r_ = xs[:, 0:4, :]; k_ = xs[:, 4:8, :]; v_ = xs[:, 8:12, :]
            C.act(z12[0:64, :], xs[0:64, 12, :], AF.Tanh)
            C.act(sg_bf, xs[:, 13, :], AF.Sigmoid)
            C.cp(DVE, z12[64:128, :], xs[64:128, 12, :])
            for pr in range(4):
                sl = slice(pr * 128, (pr + 1) * 128)
                C.mm(ps[PA][:, sl], WA[0:64, sl].k("lo"), z12[0:64, :])
            for pr in range(4):
                sl = slice(pr * 128, (pr + 1) * 128)
                C.mm(ps[PB_][:, sl], WA[64:128, sl].k("hi"), z12[64:128, :])
            sw, alr = T[9], T[8]
            for pr in range(4):
                sl = slice(pr * 128, (pr + 1) * 128)
                C.act(sw[:, sl], ps[PA][:, sl], AF.Sigmoid, bias=w04[:, pr:pr + 1])
            C.tt(DVE, v3(kq), v3(sig), bc3(oml4, 128), ALU.mult)
            C.tt(DVE, v3(fgl), v3(kq), bc3(lb4, 128), ALU.add)
            C.tt(POOL, v3(kq), bc3(oml4, 128), v3(kq), ALU.subtract)
            yield
            for pr in range(4):
                sl = slice(pr * 128, (pr + 1) * 128)
                C.act(alr[:, sl], ps[PB_][:, sl], AF.Sigmoid, bias=a04[:, pr:pr + 1])
            C.mm(ps[PA], sg_bf, g1u_bf)
            C.cp(ACT, H["g_tm"], ps[PA])
            yield
            C.act(fgl, fgl, AF.Ln)
            for h in range(4):
                C.scan(bcs[:, h * 128:(h + 1) * 128], reset, fgl[:, h * 128:(h + 1) * 128])
            C.act(eb, bcs, AF.Exp)
            C.act(enb, bcs, AF.Exp, scale=-1.0)
            bc4 = bcs.rearrange("p (a n c) -> p a n c", a=4, n=nch)
            C.tt(POOL, ebl.rearrange("p (a n c) -> p a n c", a=4, n=nch),
                 bc4[:, :, :, Cn - 1:Cn].to_broadcast([128, 4, nch, Cn]), bc4, ALU.subtract)
            C.act(ebl, ebl, AF.Exp)
            yield
            C.cp(POOL, H["decH"][:, :, 0:nch].unsqueeze(3),
                 eb.rearrange("p (a n c) -> p a n c", a=4, n=nch)[:, :, :, Cn - 1:Cn])
            C.tt(DVE, H["qTb"], v3(sq_), v3(eb), ALU.mult)
            C.tt(POOL, H["kTb"], v3(kq), v3(enb), ALU.mult)
            C.tt(DVE, khTb, v3(kq), v3(ebl), ALU.mult)
            yield
            cumS, gex, gin, ginv, glast, kkk, tq, k2 = T[2], T[3], T[4], T[5], T[6], T[7], T[0], T[1]
            for pr in range(4):
                C.scan(cumS[:, pr * 128:(pr + 1) * 128], reset, sw[:, pr * 128:(pr + 1) * 128])
            C.tt(POOL, gex, cumS, sw, ALU.subtract)
            C.act(gex, gex, AF.Exp, scale=CDEC)
            C.act(gin, cumS, AF.Exp, scale=CDEC)
            C.act(ginv, cumS, AF.Exp, scale=-CDEC)
            cs4 = cumS.rearrange("p (a n c) -> p a n c", a=4, n=nch)
            C.tt(POOL, glast.rearrange("p (a n c) -> p a n c", a=4, n=nch),
                 cs4[:, :, :, Cn - 1:Cn].to_broadcast([128, 4, nch, Cn]), cs4, ALU.subtract)
            C.act(glast, glast, AF.Exp, scale=CDEC)
            C.cp(POOL, H["gC"][:, :, 0:nch].unsqueeze(3),
                 gin.rearrange("p (a n c) -> p a n c", a=4, n=nch)[:, :, :, Cn - 1:Cn])
            yield
            C.tt(POOL, v3(kkk), k_, bc3(kk4, 128), ALU.mult)
            C.act(tqb, kkk, AF.Square)
            for pr in range(4):
                sl = slice(pr * 128, (pr + 1) * 128)
                C.mm(ps[PB_][:, sl], bdones_bf, tqb[:, sl])
            C.act(tq, ps[PB_], AF.Ln, bias=1e-24)
            C.act(tq, tq, AF.Exp, scale=-0.5)
            C.tt(DVE, kkk, kkk, tq, ALU.mult)
            C.stt(DVE, v3(k2), v3(alr), -1.0, bc3(ka4, 128), ALU.add, ALU.mult)
            C.stt(DVE, v3(k2), v3(k2), 1.0, k_, ALU.add, ALU.mult)
            C.tt(DVE, alr, kkk, alr, ALU.mult)
            b_ = alr
            yield
            C.tt(DVE, AR[:, :, 1, :], r_, v3(gin), ALU.mult)
            C.stt(DVE, AR[:, :, 0, :], v3(kkk), -1.0, v3(gex), ALU.mult, ALU.mult)
            C.tt(POOL, bT, v3(b_), v3(ginv), ALU.mult)
            C.tt(POOL, kT, v3(k2), v3(ginv), ALU.mult)
            C.tt(DVE, bhT, v3(b_), v3(glast), ALU.mult)
            C.tt(POOL, khT, v3(k2), v3(glast), ALU.mult)
            C.cp(POOL, vT, v_)
            C.tt(DVE, v3(tqb), r_, v3(k2), ALU.mult)
            for pr in range(4):
                C.mm(ps[PA][:, pr * 2:(pr + 1) * 2], tqb[:, pr * 128:(pr + 1) * 128], RKsel[:, pr, :])
            C.cp(DVE, H["bon8"], ps[PA][:, 0:8])
            yield
            for (src, bank, half) in ((lambda pr: AR[:, pr, 0, :], PA, 0), (lambda pr: bhT[:, pr, :], PA, 1),
                                      (lambda pr: khT[:, pr, :], PB_, 0), (lambda pr: vT[:, pr, :], PB_, 1)):
                for pr in range(4):
                    o0 = half * 512 + pr * 128
                    C.tr(psb(bank)[:, o0:o0 + 128], src(pr), ident_bf)
            C.cp(ACT, H["A_tm"], psb(PA)[:, 0:512]); C.cp(ACT, H["Bh_tm"], psb(PA)[:, 512:1024])
            C.cp(ACT, H["Kh_tm"], psb(PB_)[:, 0:512]); C.cp(ACT, H["V_tm"], psb(PB_)[:, 512:1024])
            yield
            for h in range(4):
                C.tr(psb(PA)[:, h * 128:(h + 1) * 128], khTb[:, h, :], ident_bf)
            C.cp(ACT, H["khat_tm"], psb(PA)[:, 0:512])
            yield

        def stageB(ti, h1_dst, sample):
            g_ = cfg(sample)
            nch, Cn, L = g_["nch"], g_["Cn"], g_["L"]
            mSb = g_["mS"].unsqueeze(1).to_broadcast([128, 4, 128])
            mIb = g_["mI"].unsqueeze(1).to_broadcast([128, 4, 128])
            mSTb = g_["mST"].unsqueeze(1).to_broadcast([128, 4, 128])
            H = HB[ti % 2]
            AR, bT, kT, A_tm, Bh_tm, Kh_tm, V_tm = H["AR"], H["bT"], H["kT"], H["A_tm"], H["Bh_tm"], H["Kh_tm"], H["V_tm"]
            qTb, kTb, khat_tm, i_tm, gs_t, g_tm, bon8, gC, decH = (H["qTb"], H["kTb"], H["khat_tm"], H["i_tm"], H["gs_t"],
                                                                    H["g_tm"], H["bon8"], H["gC"], H["decH"])
            xt = x_t[ti % 2]
            for h in range(4):
                C.mm(ps[3][:, h * 128:(h + 1) * 128], kTb[:, h, :], qTb[:, h, :])
            C.tt(DVE, attT, psv(3, 4), mIb, ALU.mult)
            if not sample:
                for h in range(4):
                    hsl = slice(h * 128, (h + 1) * 128)
                    C.mm(ps[4][:, hsl], attT[:, h, :], i_tm[:, hsl], start=True, stop=False)
                    C.mm(ps[4][:, hsl], qTb[:, h, :], SH_bf[:, h, :], start=False, stop=True)
                for h in range(4):
                    hsl = slice(h * 128, (h + 1) * 128)
                    C.mm(ps[5][:, hsl], khat_tm[:, hsl], i_tm[:, hsl])
                C.tt(POOL, SHtmp, SH, bc3(decH[:, :, 0], 128), ALU.mult)
                C.tt(DVE, SH, SHtmp, psv(5, 4), ALU.add)
                C.cp(POOL, SH_bf, SH)
            else:
                for h in range(4):
                    hsl = slice(h * 128, (h + 1) * 128)
                    C.mm(ps[4][:, hsl], attT[:, h, :], i_tm[:, hsl], start=(h == 0), stop=False)
                for b in range(NB):
                    s0 = S0h[b % 2]
                    csl = slice(b * TS, (b + 1) * TS)
                    C.dma(SP, s0, shg[b].rearrange("h k v -> k h v"))
                    C.cp(ACT, S0hb, s0)
                    eq = EQ[b % 2]
                    C.cp(POOL, eq[:, :, csl], qTb[:, :, csl])
                    for h in range(4):
                        C.mm(ps[4][:, h * 128:(h + 1) * 128], eq[:, h, :], S0hb[:, h, :], start=False, stop=False)
                    C.memset(POOL, eq[:, :, csl], 0.0)
                    C.ts(DVE, khb, khat_tm, cc("cm")[:, b:b + 1], ALU.mult)
                    bank = (5, 3)[b % 2]
                    for h in range(4):
                        hsl = slice(h * 128, (h + 1) * 128)
                        C.mm(ps[bank][:, hsl], khb[:, hsl], i_tm[:, hsl])
                    C.tt(DVE, s0, s0, bc3(decH[:, :, b], 128), ALU.mult)
                    C.tt(DVE, s0, s0, psv(bank, 4), ALU.add)
                    C.dma(SP, hgs[b].rearrange("h k v -> k h v"), s0, out_final=True)
                    yield
            osq = BT0
            C.act(osq, ps[4], AF.Square)
            C.red(DVE, s4, v3(osq))
            C.act(rr4, s4, AF.Ln, scale=1.0 / 128, bias=RMS_EPS)
            C.act(rr4, rr4, AF.Exp, scale=-0.5)
            C.tt(DVE, v3(osq), psv(4, 4), bc3(rr4, 128), ALU.mult)
            C.tt(POOL, osq, osq, hgw, ALU.mult)
            C.tt(POOL, o_all[:, 512:1024], osq, gs_t, ALU.mult)
            yield
            if sample:
                for g in range(4):
                    sa = scrA[g % 2]
                    nat = sa[0:64, :].rearrange("p (b n) -> p b n", b=4)
                    C.dma(SP, nat.rearrange("p b (h j) -> p b h j", h=8),
                          srw[g * 4:(g + 1) * 4].rearrange("b h v j -> v b h j"))
                    for bb in range(4):
                        b = g * 4 + bb
                        bank = b % 2
                        for pr in range(4):
                            C.tr(ps[bank][:, (bb % 2) * 256 + pr * 64:(bb % 2) * 256 + (pr + 1) * 64],
                                 nat[:, bb, pr * 128:(pr + 1) * 128], cc("ident")[0:64, 0:64])
                        C.cp(ACT, S0T32[:, b, :, :].k(b),
                             ps[bank][:, (bb % 2) * 256:(bb % 2) * 256 + 256].rearrange("p (a v) -> p a v", a=4))
                        C.cp(DVE, S0Tb[:, b, :, :].k(b), S0T32[:, b, :, :].k(b))
                    yield
            for hg in range(2):
                for i, h in enumerate(heads_of[hg]):
                    pr, hh = h // 2, h % 2
                    rows = slice(64 * hh, 64 * hh + 64)
                    sl = slice(i * 128, (i + 1) * 128)
                    C.mm(ps[0][:, sl], bT[rows, pr, :], AR[rows, pr, 0, :])
                    C.mm(ps[1][:, sl], bT[rows, pr, :], AR[rows, pr, 1, :])
                    C.mm(ps[2][:, sl], kT[rows, pr, :], AR[rows, pr, 0, :])
                    C.mm(ps[3][:, sl], kT[rows, pr, :], AR[rows, pr, 1, :])
                    C.mm(ps[4][:, sl], AR[rows, pr, 0, :], bT[rows, pr, :])
                hs = slice(4 * hg, 4 * hg + 4)
                C.tt(DVE, Pm[:, hs, :], psv(0, 4), mSb, ALU.mult)
                C.tt(DVE, NrbT[:, hs, :], psv(1, 4), mIb, ALU.mult)
                C.tt(DVE, AkT[:, hs, :], psv(2, 4), mSb, ALU.mult)
                C.tt(DVE, NrkT[:, hs, :], psv(3, 4), mIb, ALU.mult)
                C.tt(DVE, RR[:, hs, :], psv(4, 4), mSTb, ALU.mult)
                C.tt(POOL, Tm[:, hs, :], Pm[:, hs, :], identb4, ALU.add)
                yield
            for k in range(L):
                for hg in range(2):
                    b0 = 3 * hg
                    hs = slice(4 * hg, 4 * hg + 4)
                    for i, h in enumerate(heads_of[hg]):
                        sl = slice(i * 128, (i + 1) * 128)
                        if k < L - 1:
                            C.mm(ps[b0][:, sl], RR[:, h, :], Pm[:, h, :])
                        if k >= 1:
                            C.mm(ps[b0 + 1][:, sl], RR[:, h, :], Tm[:, h, :])
                        if k < L - 1:
                            C.mm(ps[b0 + 2][:, sl], Pm[:, h, :], RR[:, h, :])
                    if k < L - 1:
                        C.cp(ACT, Pm[:, hs, :], psv(b0, 4))
                    if k >= 1:
                        dst = TTf[:, hs, :] if k == L - 1 else Tm[:, hs, :]
                        C.tt(DVE, dst, psv(b0 + 1, 4), Tm[:, hs, :], ALU.add)
                    if k < L - 1:
                        C.cp(ACT, RR[:, hs, :], psv(b0 + 2, 4))
                    yield
            for h in range(8):
                C.mm(ps[2][:, h * 64:(h + 1) * 64], AkT[:, h, :], V_tm[:, h * 64:(h + 1) * 64])
            C.cp(ACT, Z_tm, ps[2])
            for h in range(8):
                pr = h // 2
                C.mm(ps[h // 4][:, (h % 4) * 128:(h % 4 + 1) * 128], A_tm[:, pr * 128:(pr + 1) * 128], TTf[:, h, :])
            for b in range(2):
                pv = ps[b].rearrange("p (q e t) -> p q e t", q=2, e=2)
                C.cp(ACT, W1T[0:64, 2 * b:2 * b + 2, :], pv[0:64, :, 0, :])
                C.cp(ACT, W1T[64:128, 2 * b:2 * b + 2, :], pv[64:128, :, 1, :])
            yield
            for h in range(8):
                pr, hh = h // 2, h % 2
                rows = slice(64 * hh, 64 * hh + 64)
                hsl = slice(h * 64, (h + 1) * 64)
                if not sample:
                    C.mm(ps[3][:, hsl], TTf[:, h, :], Z_tm[:, hsl], start=True, stop=False)
                    C.mm(ps[3][:, hsl], W1T[rows, pr, :], ST_bf[rows, pr, :], start=False, stop=True)
                else:
                    C.mm(ps[3][:, hsl], TTf[:, h, :], Z_tm[:, hsl], start=(h == 0), stop=False)
            if sample:
                for b in range(NB):
                    ew = EW[b % 2]
                    csl = slice(b * TS, (b + 1) * TS)
                    C.cp(POOL, ew[:, :, csl], W1T[:, :, csl])
                    for h in hh_order:
                        pr, hh = h // 2, h % 2
                        rows = slice(64 * hh, 64 * hh + 64)
                        C.mm(ps[3][:, h * 64:(h + 1) * 64], ew[rows, pr, :], S0Tb[rows, b, pr, :].k(b), start=False, stop=False)
                    C.memset(POOL, ew[:, :, csl], 0.0)
                    yield
            C.cp(ACT, U_tm, ps[3])
            for h in range(8):
                pr, hh = h // 2, h % 2
                rows = slice(64 * hh, 64 * hh + 64)
                hsl = slice(h * 64, (h + 1) * 64)
                C.mm(ps[4][:, hsl], NrbT[:, h, :], U_tm[:, hsl], start=(h == 0 or not sample), stop=False)
                C.mm(ps[4][:, hsl], NrkT[:, h, :], V_tm[:, hsl], start=False, stop=False)
                if not sample:
                    C.mm(ps[4][:, hsl], AR[rows, pr, 1, :], ST_bf[rows, pr, :], start=False, stop=True)
            yield
            if sample:
                for b in range(NB):
                    er = ER[b % 2]
                    csl = slice(b * TS, (b + 1) * TS)
                    C.cp(POOL, er[:, :, csl], AR[:, :, 1, csl])
                    for h in hh_order:
                        pr, hh = h // 2, h % 2
                        rows = slice(64 * hh, 64 * hh + 64)
                        C.mm(ps[4][:, h * 64:(h + 1) * 64], er[rows, pr, :], S0Tb[rows, b, pr, :].k(b), start=False, stop=False)
                    C.memset(POOL, er[:, :, csl], 0.0)
                    yield
                i64b = cc("i64s").unsqueeze(1).to_broadcast([128, 4, 64])
                for b in range(NB):
                    bank = b % 2
                    C.ts(DVE, Ub, U_tm, cc("cm")[:, b:b + 1], ALU.mult)
                    C.ts(DVE, Vb, V_tm, cc("cm")[:, b:b + 1], ALU.mult)
                    C.tt(DVE, Dg, i64b, bc3(gC[:, :, b], 64), ALU.mult)
                    for h in range(8):
                        hsl = slice(h * 64, (h + 1) * 64)
                        C.mm(ps[bank][0:64, hsl], Ub[:, hsl], Bh_tm[:, hsl], start=(h == 0), stop=False)
                        C.mm(ps[bank][0:64, hsl], Vb[:, hsl], Kh_tm[:, hsl], start=False, stop=False)
                    for h in hh_order:
                        pr, hh = h // 2, h % 2
                        rows = slice(64 * hh, 64 * hh + 64)
                        C.mm(ps[bank][0:64, h * 64:(h + 1) * 64], S0T32[rows, b, pr, :].k(b), Dg[rows, pr, :], start=False, stop=False)
                    C.cp(ACT, Sn[0:64, :], ps[bank][0:64, :])
                    C.dma(SP, rws[b].rearrange("h v j -> v h j"), Sn[0:64, :].rearrange("p (h j) -> p h j", h=8), out_final=True)
                    yield
            if not sample:
                for pr in range(4):
                    psl = slice(pr * 128, (pr + 1) * 128)
                    C.mm(ps[5][:, psl], Bh_tm[:, psl], U_tm[:, psl], start=True, stop=False)
                    C.mm(ps[5][:, psl], Kh_tm[:, psl], V_tm[:, psl], start=False, stop=True)
                C.tt(POOL, STtmp, ST, bc3(gC[:, :, 0], 64), ALU.mult)
                p5 = psv(5, 4)
                C.tt(DVE, ST[0:64, :, :], STtmp[0:64, :, :], p5[0:64, :, 0:64], ALU.add)
                C.tt(DVE, ST[64:128, :, :], STtmp[64:128, :, :], p5[64:128, :, 64:128], ALU.add)
                C.cp(POOL, ST_bf, ST)
            yield
            ysq, tmp2 = BT0, BT1
            y3 = ps[4].rearrange("p (h v) -> p h v", h=8)
            yv = ysq.rearrange("p (h v) -> p h v", h=8)
            C.red(DVE, st16[:, 0:8], y3)
            C.act(ysq, ps[4], AF.Square)
            C.red(DVE, st16[:, 8:16], yv)
            C.ts(DVE, m8, st16[:, 0:8], 1.0 / 64, ALU.mult)
            C.tt(DVE, r8, m8, m8, ALU.mult)
            C.stt(DVE, r8, st16[:, 8:16], 1.0 / 64, r8, ALU.mult, ALU.subtract)
            C.act(r8, r8, AF.Ln, bias=GN_EPS)
            C.act(r8, r8, AF.Exp, scale=-0.5)
            C.tt(DVE, yv, y3, bc3(m8, 64), ALU.subtract)
            C.tt(POOL, yv, yv, bc3(r8, 64), ALU.mult)
            C.tt(POOL, ysq, ysq, lnxw, ALU.mult)
            C.tt(POOL, ysq, ysq, lnxb, ALU.add)
            C.tt(DVE, tmp2.rearrange("p (h v) -> p h v", h=8), V_tm.rearrange("p (h v) -> p h v", h=8),
                 bc3(bon8, 64), ALU.mult)
            C.tt(DVE, ysq, ysq, tmp2, ALU.add)
            C.tt(DVE, o_all[:, 0:512], ysq, g_tm, ALU.mult)
            yield
            for mc in range(8):
                C.tr(psb(2)[:, mc * 128:(mc + 1) * 128], o_all[:, mc * 128:(mc + 1) * 128], ident_bf)
            C.cp(ACT, oT, psb(2).rearrange("p (a t) -> p a t", a=8))
            for half in range(2):
                for mc in range(8):
                    C.mm(ps[half], oT[:, mc, :], w_out_bf[:, mc, half * 512:(half + 1) * 512].k(mc),
                         start=(mc == 0), stop=(mc == 7))
            for half in range(2):
                hsl = slice(half * 512, (half + 1) * 512)
                C.stt(DVE, h1pre[:, hsl], xt[:, hsl], ALPHA, ps[half], ALU.mult, ALU.add)
            yield
            layernorm(h1pre, h1pre, LN_EPS)
            C.dma(SP, h1_dst, h1pre)
            yield

        def drain(g):
            for _ in g:
                pass

        def interleave(ga, gb):
            la = lb_ = True
            while la or lb_:
                if lb_:
                    for _ in range(2):
                        try:
                            next(gb)
                        except StopIteration:
                            lb_ = False
                            break
                if la:
                    try:
                        next(ga)
                    except StopIteration:
                        la = False

        jobs = [(ti, xp[ti * 128:(ti + 1) * 128, :], V(h1scr[ti * 128:(ti + 1) * 128, :], ("h1scr", ti)), False)
                for ti in range(NT)]
        if SAMPLE:
            jobs.append((NT, xsm, V(h1scr[NTP * 128:(NTP + 1) * 128, :], ("h1scr", NTP)), True))
        if True:
            if jobs:
                drain(stageA(jobs[0][0], jobs[0][1], jobs[0][3]))
            for n, (ti, x_src, h1_dst, smp) in enumerate(jobs):
                gb = stageB(ti, h1_dst, smp)
                if n + 1 < len(jobs):
                    nj = jobs[n + 1]
                    interleave(stageA(nj[0], nj[1], nj[3]), gb)
                else:
                    drain(gb)
                if n == NT - 1 and not smp:
                    prompt_state_outputs()
            if NT == 0:
                prompt_state_outputs()

        if True:
            P.barrier()
            arena.off = phase_mark
            w_up_bf = sb([128, 8, DFF], BF16, "w_up_bf")
            w_dn_bf = sb([128, 32, D], BF16, "w_dn_bf")
            for cb in range(8):
                C.dma(POOL, w_up_bf[:, :, cb * 512:(cb + 1) * 512].k(cb),
                      w_up[:, cb * 512:(cb + 1) * 512].rearrange("(kc p) n -> p kc n", p=128))
            C.dma(SP, lng, row(ln2_g).partition_broadcast(128))
            C.dma(SP, lnb, row(ln2_b).partition_broadcast(128))
            for fc in range(32):
                C.dma(POOL, w_dn_bf[:, fc, :].k(fc), w_down[fc * 128:(fc + 1) * 128, :])
            upT = sb([128, 32, 512], BF16, "upT")
            h1T = sb([128, 8, 512], BF16, "h1T")
            h1b = [sb([128, D], BF16, f"h1b{i}") for i in range(2)]
            h1r = [sb([128, D], F32, f"h1r{i}") for i in range(2)]
            rl = [sb([128, 512], F32, f"rl{i}") for i in range(2)]
            pre2 = sb([128, D], F32, "pre2")
            outb = [sb([128, D], F32, f"outb{i}") for i in range(2)]
            ntiles = NT + (1 if SAMPLE else 0)
            tiles = list(range(NT)) + ([NTP] if SAMPLE else [])
            groups = [tiles[i:i + 4] for i in range(0, NT, 4)]
            if SAMPLE:
                groups.append([NTP])
            gcount = 0
            tcount = 0
            for grp in groups:
                ng = len(grp)
                W = ng * 128
                for gi, tix in enumerate(grp):
                    hb = h1b[(tcount + gi) % 2]
                    C.dma(POOL, hb, V(h1scr[tix * 128:(tix + 1) * 128, :], ("h1scr", tix)))
                    bank = 6 + (gi % 2)
                    for kc in range(8):
                        C.tr(psb(bank)[:, kc * 128:(kc + 1) * 128], hb[:, kc * 128:(kc + 1) * 128], ident_bf)
                    C.cp(ACT, h1T[:, :, gi * 128:(gi + 1) * 128], psb(bank).rearrange("p (a t) -> p a t", a=8))
                for fc in range(32):
                    bank = fc % 2
                    for kc in range(8):
                        C.mm(ps[bank][:, 0:W], w_up_bf[:, kc, fc * 128:(fc + 1) * 128].k(fc // 4), h1T[:, kc, 0:W],
                             start=(kc == 0), stop=(kc == 7))
                    r = rl[fc % 2]
                    C.act(r[:, 0:W], ps[bank][:, 0:W], AF.Relu)
                    C.tt(POOL if fc % 2 else DVE, upT[:, fc, 0:W], r[:, 0:W], r[:, 0:W], ALU.mult)
                for gi, tix in enumerate(grp):
                    hr = h1r[(tcount + gi) % 2]
                    C.dma(SP, hr, V(h1scr[tix * 128:(tix + 1) * 128, :], ("h1scr", tix)))
                    for half in range(2):
                        bank = 2 + ((gi * 2 + half) % 4)
                        for fc in range(32):
                            C.mm(ps[bank], upT[:, fc, gi * 128:(gi + 1) * 128], w_dn_bf[:, fc, half * 512:(half + 1) * 512].k(fc),
                                 start=(fc == 0), stop=(fc == 31))
                        hsl = slice(half * 512, (half + 1) * 512)
                        C.stt(DVE, pre2[:, hsl], hr[:, hsl], ALPHA, ps[bank], ALU.mult, ALU.add)
                    ob = outb[(tcount + gi) % 2]
                    layernorm(pre2, ob, LN_EPS)
                    dst = ys if tix == NTP else yp[tix * 128:(tix + 1) * 128, :]
                    C.dma(SP, dst, ob, out_final=True)
                tcount += ng
                gcount += 1

        sems = {e: es.enter_context(nc.semaphore(f"s_{e}")) for e in ENGINES}
        rings = {e: [es.enter_context(nc.semaphore(f"r_{e}{i}")) for i in range(n)] for e, n in DMA_RING.items()}
        P.prepare(sems, rings)
        build.stats = dict(P.stats)
        build.arena_peak = arena.peak
        with nc.allow_low_precision(reason="bf16 matmul operands, fp32 accumulation"), \
                nc.allow_non_contiguous_dma(reason="tiny per-channel vectors / state layouts"), \
                nc.Block() as block:
            block.tensor(lambda eng: P.emit_engine(PE, eng))
            block.scalar(lambda eng: P.emit_engine(ACT, eng))
            block.vector(lambda eng: P.emit_engine(DVE, eng))
            block.gpsimd(lambda eng: P.emit_engine(POOL, eng))
            block.sync(lambda eng: P.emit_engine(SP, eng))
    return nc


IN_NAMES = ["w_in", "shift_mu", "w0", "w1u", "a0", "a1u", "g1u", "k_k", "k_a", "r_k", "ln_x_w", "ln_x_b",
            "lb_logits", "hg_norm_w", "w_out", "ln1_g", "ln1_b", "w_up", "w_down", "ln2_g", "ln2_b"]


def make_in_maps(inputs, n_cores=8):
    f = lambda a: np.ascontiguousarray(np.asarray(a, dtype=np.float32))
    shared = {}
    for k in IN_NAMES:
        a = f(inputs[k])
        a = a[0] if k != "lb_logits" else a
        if k == "r_k":
            a = a.reshape(-1)
        shared[k] = np.ascontiguousarray(a)
    shared["consts"] = CONSTS
    maps = []
    for c in range(n_cores):
        m = dict(shared)
        m["xp"] = f(inputs["x_prompt"][c])
        m["xs"] = f(inputs["x_sample"][c * NB:(c + 1) * NB]).reshape(NB * TS, D)
        m["srw"] = f(inputs["state_rwkv"][0, c * NB:(c + 1) * NB])
        m["shg"] = f(inputs["state_hgrn"][0, c * NB:(c + 1) * NB])
        m["ssh"] = f(inputs["state_shift"][0, c * NB:(c + 1) * NB])
        maps.append(m)
    return maps


_NC_CACHE = {}


def kernel(**inputs):
    if "nc" not in _NC_CACHE:
        _NC_CACHE["nc"] = build()
    nc = _NC_CACHE["nc"]
    maps = make_in_maps(inputs)
    res = run_bass_kernel_spmd(nc, maps, core_ids=list(range(8)))
    R = res.results
    y_prompt = np.stack([R[c]["yp"] for c in range(8)]).astype(np.float32)
    y_sample = np.concatenate([R[c]["ys"].reshape(NB, TS, D) for c in range(8)]).astype(np.float32)
    rw_p = np.stack([R[c]["rwp"] for c in range(8)])[None].astype(np.float32)
    rw_s = np.concatenate([R[c]["rws"] for c in range(8)])[None].astype(np.float32)
    hg_p = np.stack([R[c]["hgp"] for c in range(8)])[None].astype(np.float32)
    hg_s = np.concatenate([R[c]["hgs"] for c in range(8)])[None].astype(np.float32)
    sh_p = np.stack([R[c]["shp"] for c in range(8)])[None].astype(np.float32)
    sh_s = np.concatenate([R[c]["shs"] for c in range(8)])[None].astype(np.float32)
    return (y_prompt, y_sample, rw_p, rw_s, hg_p, hg_s, sh_p, sh_s)
```

```python
import numpy as np
import concourse.bass as bass
import concourse.mybir as mybir
from concourse.bass_utils import run_bass_kernel_spmd

F32 = mybir.dt.float32
BF16 = mybir.dt.bfloat16
AF = mybir.ActivationFunctionType
ALU = mybir.AluOpType
AX = mybir.AxisListType

DEBUG_LINES = False
LINE_OF = {}
PE, ACT, DVE, POOL, SP = "pe", "act", "dve", "pool", "sp"
ENGINES = (PE, ACT, DVE, POOL, SP)
DMA_RING = {SP: 12, ACT: 4, POOL: 8}


class Res:
    __slots__ = ("name", "last_w", "readers")

    def __init__(self, name):
        self.name = name
        self.last_w = None
        self.readers = []


class Op:
    __slots__ = ("eng", "fn", "deps", "is_dma", "signal", "idx", "dma_no", "extra_wait", "rg")

    def __init__(self, eng, fn, is_dma):
        self.eng = eng
        self.fn = fn
        self.deps = set()
        self.is_dma = is_dma
        self.signal = False
        self.idx = None
        self.dma_no = None
        self.extra_wait = None
        self.rg = None


def _pe_inorder_ok(d, o):
    return d.rg is None or o.rg is None or d.rg == o.rg


class Prog:
    def __init__(self):
        self.ops = {e: [] for e in ENGINES}
        self.order = []
        self.n_dma = {e: 0 for e in ENGINES}
        self.dma_ops = {e: [] for e in ENGINES}
        self.out_dmas = []

    def op(self, eng, fn, reads=(), writes=(), dma=False, out=False):
        o = Op(eng, fn, dma)
        for r in reads:
            if r.last_w is not None:
                o.deps.add(r.last_w)
        for w in writes:
            if w.last_w is not None:
                o.deps.add(w.last_w)
            for rd in w.readers:
                o.deps.add(rd)
        for r in reads:
            r.readers.append(o)
        for w in writes:
            w.last_w = o
            w.readers = []
        if getattr(self, "barrier_left", None) and eng in self.barrier_left:
            self.barrier_left.discard(eng)
            o.deps.update(self.pending_barrier)
        o.deps.discard(o)
        if dma:
            o.dma_no = self.n_dma[eng]
            self.n_dma[eng] += 1
            self.dma_ops[eng].append(o)
            o.signal = True
            if out:
                self.out_dmas.append(o)
        self.ops[eng].append(o)
        self.order.append(o)
        return o

    def barrier(self):
        pend = []
        for e in ENGINES:
            comp = [o for o in self.ops[e] if not o.is_dma]
            if comp:
                pend.append(comp[-1])
            pend.extend(self.dma_ops[e][-DMA_RING.get(e, 0):] if e in DMA_RING else [])
        self.pending_barrier = pend
        self.barrier_left = set(ENGINES)

    def prepare(self, sems, rings):
        for o in self.order:
            for d in o.deps:
                if d.is_dma:
                    continue
                if d.eng == o.eng and d.eng == PE and not o.is_dma and _pe_inorder_ok(d, o):
                    continue
                d.signal = True
        sig = {}
        for e in ENGINES:
            c = 0
            for o in self.ops[e]:
                if o.is_dma:
                    R = len(rings[e])
                    sig[o] = (rings[e][o.dma_no % R], 16 * (o.dma_no // R + 1))
                elif o.signal:
                    c += 1
                    sig[o] = (sems[e], c)
        self.sig = sig
        self.rings = rings
        self.stats = {e: len(self.ops[e]) for e in ENGINES}

    def emit_engine(self, e, eng):
        sig, rings = self.sig, self.rings
        waited = {}

        def wait(sem, val):
            k = id(sem)
            if waited.get(k, 0) >= val:
                return
            waited[k] = val
            eng.wait_ge(sem, val)

        for o in self.ops[e]:
            for d in o.deps:
                if d not in sig:
                    continue
                if d.eng == e and not d.is_dma and not o.is_dma and e == PE and _pe_inorder_ok(d, o):
                    continue
                s, v = sig[d]
                wait(s, v)
            if o.is_dma:
                R = len(rings[e])
                if o.dma_no >= R:
                    wait(rings[e][o.dma_no % R], 16 * (o.dma_no // R))
            ins = o.fn(eng)
            if DEBUG_LINES:
                LINE_OF[str(getattr(getattr(ins, "ins", ins), "name", ins))] = o.extra_wait
            if o in sig:
                s, v = sig[o]
                ins.then_inc(s, 16 if o.is_dma else 1)
        if e == SP:
            for q in ENGINES:
                if q not in rings:
                    continue
                for o in self.dma_ops[q][-len(rings[q]):]:
                    s, v = sig[o]
                    wait(s, v)


class V:
    __slots__ = ("ap", "key")

    def __init__(self, ap, key):
        self.ap = ap
        self.key = key

    def __getitem__(self, idx):
        return V(self.ap[idx], self.key)

    def k(self, sub):
        return V(self.ap, (self.key, sub))

    @property
    def shape(self):
        return self.ap.shape

    def rearrange(self, *a, **kw):
        return V(self.ap.rearrange(*a, **kw), self.key)

    def unsqueeze(self, ax):
        return V(self.ap.unsqueeze(ax), self.key)

    def to_broadcast(self, shape):
        return V(self.ap.to_broadcast(list(shape)), self.key)

    def bitcast(self, dt):
        return V(self.ap.bitcast(dt), self.key)


def _ap(x):
    return x.ap if isinstance(x, V) else x


def _isnum(x):
    return isinstance(x, (int, float))


class Arena:
    def __init__(self, nc, es, nbytes):
        self.n2 = nbytes // 2
        self.t = es.enter_context(nc.sbuf_tensor("arena", [128, self.n2], BF16))
        self.off = 0
        self.cnt = 0
        self.peak = 0

    def alloc(self, shape, dt, name=None):
        shape = list(shape)
        esz = 4 if dt == F32 else 2
        n = int(np.prod(shape[1:]))
        nbytes = (n * esz + 3) // 4 * 4
        o = self.off
        assert o + nbytes <= self.n2 * 2, f"arena overflow allocating {name} {shape}: {o}+{nbytes} > {self.n2 * 2}"
        self.off += nbytes
        self.peak = max(self.peak, self.off)
        ap = self.t[0:shape[0], o // 2:o // 2 + nbytes // 2]
        if esz == 4:
            ap = ap.bitcast(F32)
        ap = ap[:, 0:n]
        if len(shape) == 3:
            ap = ap.rearrange("p (a b) -> p a b", a=shape[1])
        elif len(shape) == 4:
            ap = ap.rearrange("p (a b c) -> p a b c", a=shape[1], b=shape[2])
        self.cnt += 1
        return V(ap, name or f"t{self.cnt}")


class Ctx:
    def __init__(self, nc, P, arena):
        self.nc, self.P, self.arena = nc, P, arena
        self.res = {}

    def sb(self, shape, dt, name=None):
        return self.arena.alloc(shape, dt, name)

    def R(self, x):
        k = x.key if isinstance(x, V) else (x.name, None)
        r = self.res.get(k)
        if r is None:
            r = self.res[k] = Res(k)
        return r

    def rec(self, eng, fn, outs, ins, dma=False, out=False):
        reads = [self.R(i) for i in ins if i is not None and not _isnum(i)]
        writes = [self.R(o) for o in outs]
        writes += [r for r in reads if isinstance(r.name, str) and r.name.startswith("ps")]
        o_ = self.P.op(eng, fn, reads=reads, writes=writes, dma=dma, out=out)
        if DEBUG_LINES:
            import sys as _s
            f = _s._getframe(1)
            while f.f_code.co_name not in ("mixer_tile", "build", "layernorm") and f.f_back is not None:
                f = f.f_back
            o_.extra_wait = f.f_lineno
        return o_

    def mm(self, out, lhsT, rhs, start=True, stop=True):
        o, l, r = _ap(out), _ap(lhsT), _ap(rhs)
        op = self.rec(PE, lambda e: e.matmul(o, lhsT=l, rhs=r, start=start, stop=stop,
                                             skip_group_check=True), [out], [lhsT, rhs])
        kr = l.shape[0]
        if kr < 128:
            op.rg = (kr, l.base_partition())
        return op

    def tr(self, out, in_, ident):
        o, i, d = _ap(out), _ap(in_), _ap(ident)
        return self.rec(PE, lambda e: e.transpose(o, i, d), [out], [in_, ident])

    def act(self, out, in_, func, bias=None, scale=1.0):
        o, i = _ap(out), _ap(in_)
        kw = {}
        if bias is not None:
            kw["bias"] = _ap(bias)
        s = _ap(scale)
        return self.rec(ACT, lambda e: e.activation(out=o, in_=i, func=func, scale=s, **kw), [out],
                        [in_, bias, scale])

    def tt(self, eng, out, a, b, op):
        o, x, y = _ap(out), _ap(a), _ap(b)
        return self.rec(eng, lambda e: e.tensor_tensor(out=o, in0=x, in1=y, op=op), [out], [a, b])

    def ts(self, eng, out, a, s1, op0, s2=None, op1=None):
        o, x, v1, v2 = _ap(out), _ap(a), _ap(s1), _ap(s2)
        if op1 is None:
            f = lambda e: e.tensor_scalar(out=o, in0=x, scalar1=v1, scalar2=None, op0=op0)
        else:
            f = lambda e: e.tensor_scalar(out=o, in0=x, scalar1=v1, scalar2=v2, op0=op0, op1=op1)
        return self.rec(eng, f, [out], [a, s1, s2])

    def stt(self, eng, out, in0, scalar, in1, op0, op1):
        o, x, y, s = _ap(out), _ap(in0), _ap(in1), _ap(scalar)
        return self.rec(eng, lambda e: e.scalar_tensor_tensor(out=o, in0=x, scalar=s, in1=y, op0=op0, op1=op1),
                        [out], [in0, in1, scalar])

    def cp(self, eng, out, in_):
        o, i = _ap(out), _ap(in_)
        if eng == ACT:
            return self.rec(ACT, lambda e: e.activation(out=o, in_=i, func=AF.Copy), [out], [in_])
        return self.rec(eng, lambda e: e.tensor_copy(out=o, in_=i), [out], [in_])

    def recip(self, out, in_):
        o, i = _ap(out), _ap(in_)
        return self.rec(DVE, lambda e: e.reciprocal(out=o, in_=i), [out], [in_])

    def scan(self, out, d0, d1):
        o, a, b = _ap(out), _ap(d0), _ap(d1)
        return self.rec(DVE, lambda e: e.tensor_tensor_scan(out=o, data0=a, data1=b, initial=0.0,
                                                            op0=ALU.mult, op1=ALU.add), [out], [d0, d1])

    def red(self, eng, out, in_, op=ALU.add):
        o, i = _ap(out), _ap(in_)
        return self.rec(eng, lambda e: e.tensor_reduce(out=o, in_=i, axis=AX.X, op=op), [out], [in_])

    def memset(self, eng, out, val):
        o = _ap(out)
        return self.rec(eng, lambda e: e.memset(o, val), [out], [])

    def dma(self, eng, out, in_, out_final=False, extra_out=()):
        o, i = _ap(out), _ap(in_)
        return self.rec(eng, lambda e: e.dma_start(out=o, in_=i), [out, *extra_out], [in_], dma=True, out=out_final)

    def bn_stats(self, out, in_):
        o, i = _ap(out), _ap(in_)
        return self.rec(DVE, lambda e: e.bn_stats(out=o, in_=i), [out], [in_])

    def bn_aggr(self, out, in_):
        o, i = _ap(out), _ap(in_)
        return self.rec(DVE, lambda e: e.bn_aggr(out=o, in_=i), [out], [in_])


D = 1024
PJ = 3840
RWP = 1792
NTP = 16
NB = 16
TS = 8
DFF = 4096
ALPHA = 2.0 ** 0.25
CDEC = -float(np.exp(-0.5))
LN_EPS = 1e-5
GN_EPS = 64e-5
RMS_EPS = 1e-6
ARENA_BYTES = 212800


def make_consts():
    s = np.arange(128)[:, None]
    t = np.arange(128)[None, :]
    cols = {}
    cols["ident"] = (s == t)
    cols["mS_p"] = (s < t)
    cols["mI_p"] = (s <= t)
    cols["mST_p"] = (t < s)
    same = (s // TS) == (t // TS)
    cols["mS_s"] = (s < t) & same
    cols["mI_s"] = (s <= t) & same
    cols["mST_s"] = (t < s) & same
    cols["reset_p"] = np.broadcast_to(t != 0, (128, 128))
    cols["reset_s"] = np.broadcast_to((t % TS) != 0, (128, 128))
    cols["bdones"] = (s // 64) == (t // 64)
    cols["hsel"] = (s // 64) == np.arange(2)[None, :]
    cols["cm"] = (s // TS) == np.arange(NB)[None, :]
    cols["i64s"] = (s % 64) == np.arange(64)[None, :]
    off = {}
    parts = []
    o = 0
    for k, v in cols.items():
        v = np.asarray(v, np.float32)
        off[k] = (o, o + v.shape[1])
        o += v.shape[1]
        parts.append(v)
    return np.ascontiguousarray(np.concatenate(parts, axis=1)), off


CONSTS, COFF = make_consts()
NCONST = CONSTS.shape[1]


class _Stop(Exception):
    pass


def build(NT=NTP, SAMPLE=True, DBG=False, STAGE=99):
    from contextlib import ExitStack
    nc = bass.Bass("TRN2", target_bir_lowering=False)

    def din(name, shape):
        return nc.dram_tensor(name, list(shape), F32, kind="ExternalInput").ap()

    def dout(name, shape):
        return nc.dram_tensor(name, list(shape), F32, kind="ExternalOutput").ap()

    xp = din("xp", [NTP * 128, D]); xsm = din("xs", [128, D])
    srw = din("srw", [NB, 8, 64, 64]); shg = din("shg", [NB, 4, 128, 128]); ssh = din("ssh", [NB, RWP])
    w_in = din("w_in", [D, PJ]); shift_mu = din("shift_mu", [RWP]); w0 = din("w0", [512])
    w1u = din("w1u", [64, 512]); a0 = din("a0", [512]); a1u = din("a1u", [64, 512]); g1u = din("g1u", [128, 512])
    k_k = din("k_k", [512]); k_a = din("k_a", [512]); r_k = din("r_k", [512])
    ln_x_w = din("ln_x_w", [512]); ln_x_b = din("ln_x_b", [512]); lb_logits = din("lb_logits", [2, 512])
    hg_norm_w = din("hg_norm_w", [512]); w_out = din("w_out", [D, D]); ln1_g = din("ln1_g", [D]); ln1_b = din("ln1_b", [D])
    w_up = din("w_up", [D, DFF]); w_down = din("w_down", [DFF, D]); ln2_g = din("ln2_g", [D]); ln2_b = din("ln2_b", [D])
    cst_d = din("consts", [128, NCONST])
    yp = dout("yp", [NTP * 128, D]); ys = dout("ys", [128, D])
    rwp = dout("rwp", [8, 64, 64]); rws = dout("rws", [NB, 8, 64, 64])
    hgp = dout("hgp", [4, 128, 128]); hgs = dout("hgs", [NB, 4, 128, 128])
    shp = dout("shp", [RWP]); shs = dout("shs", [NB, RWP])
    h1scr = nc.dram_tensor("h1scr", [(NTP + 1) * 128, D], F32).ap()
    if DBG:
        dbg_d = dout("dbg", [128, 4096])

    def row(v):
        return v.rearrange("(o n) -> o n", o=1)

    P = Prog()
    with ExitStack() as es:
        arena = Arena(nc, es, ARENA_BYTES)
        C = Ctx(nc, P, arena)
        sb = C.sb
        ps = [V(es.enter_context(nc.psum_tensor(f"ps{i}", [128, 512], F32))[:], f"ps{i}") for i in range(8)]

        def psv(i, a):
            return ps[i].rearrange("p (a t) -> p a t", a=a)

        def psb(i):
            return ps[i].bitcast(BF16)

        def bc3(ap2, n):
            return ap2.unsqueeze(2).to_broadcast([ap2.shape[0], ap2.shape[1], n])

        def v3(t):
            return t.rearrange("p (a t) -> p a t", a=4)

        ident_bf = sb([128, 128], BF16, "ident_bf")
        lng = sb([128, D], F32, "lng"); lnb = sb([128, D], F32, "lnb")
        C.dma(SP, lng, row(ln1_g).partition_broadcast(128))
        C.dma(SP, lnb, row(ln1_b).partition_broadcast(128))
        bnst = sb([128, 12], F32, "bnst"); mv = sb([128, 2], F32, "mv"); rstd1 = sb([128, 1], F32, "rstd1")
        fence_ln = sb([128, 2], F32, "fence_ln")
        if DBG:
            dbg = sb([128, 4096], F32, "dbg")
            C.memset(POOL, dbg, 0.0)
            dbg_pos = [0]
            dbg_map = {}

            def dump(name, v, n):
                a = dbg_pos[0]
                shape = list(v.shape)
                dst = dbg[0:shape[0], a:a + n]
                if len(shape) == 3:
                    dst = dst.rearrange("p (a b) -> p a b", a=shape[1])
                C.cp(POOL, dst, v)
                dbg_pos[0] += n
                dbg_map[name] = (a, n, shape)
            build.dbg_map = dbg_map
        else:
            def dump(name, v, n):
                return None
        phase_mark = arena.off

        def layernorm(src, dst, eps):
            for half in range(2):
                C.bn_stats(bnst[:, half * 6:(half + 1) * 6], src[:, half * 512:(half + 1) * 512])
            C.bn_aggr(mv, bnst)
            C.act(rstd1, mv[:, 1:2], AF.Ln, bias=eps)
            C.act(rstd1, rstd1, AF.Exp, scale=-0.5)
            C.ts(DVE, dst, src, mv[:, 0:1], ALU.subtract, rstd1[:, 0:1], ALU.mult)
            halves = []
            for eng_, hsl in ((DVE, slice(0, 640)), (POOL, slice(640, 1024))):
                dk = dst[:, hsl].k(hsl.start)
                halves.append(dk)
                o, a_, g_, b_ = _ap(dk), _ap(dst[:, hsl]), _ap(lng[:, hsl]), _ap(lnb[:, hsl])
                C.rec(eng_, lambda e, o=o, a_=a_, g_=g_: e.tensor_tensor(out=o, in0=a_, in1=g_, op=ALU.mult), [dk], [dst, lng])
                C.rec(eng_, lambda e, o=o, b_=b_: e.tensor_tensor(out=o, in0=o, in1=b_, op=ALU.add), [dk], [dk, lnb])
            C.rec(POOL, lambda e: e.memset(_ap(fence_ln), 0.0), [dst, fence_ln], halves)

        F32_CONSTS = ("ident", "reset_p", "reset_s", "hsel", "cm", "i64s")
        BF_CONSTS = ("mS_p", "mI_p", "mST_p", "mS_s", "mI_s", "mST_s", "bdones")
        cviews = {}
        nf = sum(COFF[k][1] - COFF[k][0] for k in F32_CONSTS)
        nb_ = sum(COFF[k][1] - COFF[k][0] for k in BF_CONSTS)
        cstf = sb([128, nf], F32, "cstf"); cstb = sb([128, nb_], BF16, "cstb")
        o_ = 0
        for k_ in F32_CONSTS:
            a, b = COFF[k_]
            C.dma(SP, cstf[:, o_:o_ + b - a].k(k_), cst_d[:, a:b])
            cviews[k_] = cstf[:, o_:o_ + b - a].k(k_)
            o_ += b - a
        o_ = 0
        for k_ in BF_CONSTS:
            a, b = COFF[k_]
            C.dma(POOL, cstb[:, o_:o_ + b - a].k(k_), cst_d[:, a:b])
            cviews[k_] = cstb[:, o_:o_ + b - a].k(k_)
            o_ += b - a

        def cc(name):
            return cviews[name]

        C.cp(POOL, ident_bf, cc("ident"))
        bdones_bf = cc("bdones")
        mu14 = sb([128, 14], F32, "mu14")
        kk4 = sb([128, 4], F32, "kk4"); ka4 = sb([128, 4], F32, "ka4"); rk4 = sb([128, 4], F32, "rk4")
        w04 = sb([128, 4], F32, "w04"); a04 = sb([128, 4], F32, "a04")
        lbl = sb([128, 2, 4], F32, "lbl")
        with nc.allow_non_contiguous_dma(reason="tiny per-channel parameter vectors"):
            C.dma(SP, mu14, shift_mu.rearrange("(c p) -> p c", p=128))
            C.dma(SP, kk4, k_k.rearrange("(c p) -> p c", p=128))
            C.dma(SP, ka4, k_a.rearrange("(c p) -> p c", p=128))
            C.dma(SP, rk4, r_k.rearrange("(c p) -> p c", p=128))
            C.dma(SP, w04, w0.rearrange("(c p) -> p c", p=128))
            C.dma(SP, a04, a0.rearrange("(c p) -> p c", p=128))
            C.dma(SP, lbl, lb_logits.rearrange("l (c p) -> p l c", p=128))
        WA = sb([128, 512], BF16, "WA")
        C.dma(POOL, WA[0:64, :].k("lo"), w1u)
        C.dma(POOL, WA[64:128, :].k("hi"), a1u)
        g1u_bf = sb([128, 512], BF16, "g1u_bf")
        C.dma(POOL, g1u_bf, g1u)
        lnxw = sb([128, 512], BF16, "lnxw"); lnxb = sb([128, 512], BF16, "lnxb"); hgw = sb([128, 512], BF16, "hgw")
        C.dma(POOL, lnxw, row(ln_x_w).partition_broadcast(128))
        C.dma(POOL, lnxb, row(ln_x_b).partition_broadcast(128))
        C.dma(POOL, hgw, row(hg_norm_w).partition_broadcast(128))
        lb4 = sb([128, 4], F32, "lb4"); oml4 = sb([128, 4], F32, "oml4"); etmp = sb([128, 4], F32, "etmp")
        C.tt(DVE, etmp, lbl[:, 1, :], lbl[:, 0, :], ALU.subtract)
        C.act(etmp, etmp, AF.Exp)
        C.ts(DVE, lb4, etmp, 1.0, ALU.add)
        C.recip(lb4, lb4)
        C.tt(DVE, oml4, etmp, lb4, ALU.mult)
        RKsel = sb([128, 4, 2], BF16, "RKsel")
        C.tt(DVE, RKsel, bc3(rk4, 2), cc("hsel").unsqueeze(1).to_broadcast([128, 4, 2]), ALU.mult)

        ov_lo = arena.off
        w_in_bf = sb([128, 8, PJ], BF16, "w_in_bf")
        ov_hi = arena.off
        for kc in range(8):
            C.dma(POOL, w_in_bf[:, kc, :].k(kc), w_in[kc * 128:(kc + 1) * 128, :])
        w_out_bf = sb([128, 8, D], BF16, "w_out_bf")
        for kc in range(8):
            C.dma(POOL, w_out_bf[:, kc, :].k(kc), w_out[kc * 128:(kc + 1) * 128, :])

        ST = sb([128, 4, 64], F32, "ST"); ST_bf = sb([128, 4, 64], BF16, "ST_bf")
        SH = sb([128, 4, 128], F32, "SH"); SH_bf = sb([128, 4, 128], BF16, "SH_bf")
        plast = sb([128, 14], F32, "plast")
        for t_ in (ST, ST_bf, SH, SH_bf, plast):
            C.memset(POOL, t_, 0.0)

        def prompt_state_outputs():
            with nc.allow_non_contiguous_dma(reason="tiny state vector"):
                C.dma(SP, shp.rearrange("(c p) -> p c", p=128), plast, out_final=True)
            C.dma(SP, hgp.rearrange("h k v -> k h v"), SH, out_final=True)
            identf = cc("ident")
            for pr in range(4):
                C.tr(ps[2][0:64, pr * 128:(pr + 1) * 128], ST[:, pr, :], identf)
            rwo = T[0]
            C.cp(DVE, rwo[0:64, :], ps[2][0:64, :])
            C.dma(SP, rwp.rearrange("h v j -> v h j"), rwo[0:64, :].rearrange("p (h j) -> p h j", h=8), out_final=True)
            if DBG:
                C.dma(SP, dbg_d, dbg, out_final=True)


        x_t = [sb([128, D], F32, f"x_t{i}") for i in range(2)]
        x_bf = sb([128, D], BF16, "x_bf")
        xT = sb([128, 8, 128], BF16, "xT")
        pr_ = sb([128, 14, 129], F32, "pr")
        xs = sb([128, 14, 128], F32, "xsft")
        T = [sb([128, 512], F32, f"T{i}") for i in range(10)]
        z12 = sb([128, 128], BF16, "z12"); sg_bf = sb([128, 128], BF16, "sg_bf"); tqb = sb([128, 512], BF16, "tqb")
        bhT = sb([128, 4, 128], BF16, "bhT"); khT = sb([128, 4, 128], BF16, "khT"); vT = sb([128, 4, 128], BF16, "vT")
        khTb = sb([128, 4, 128], BF16, "khTb")
        fence2 = sb([128, 2], F32, "fence2")
        HB = []
        for i in range(2):
            HB.append(dict(
                AR=sb([128, 4, 2, 128], BF16, f"AR{i}"), bT=sb([128, 4, 128], BF16, f"bT{i}"), kT=sb([128, 4, 128], BF16, f"kT{i}"),
                A_tm=sb([128, 512], BF16, f"A_tm{i}"), Bh_tm=sb([128, 512], BF16, f"Bh_tm{i}"),
                Kh_tm=sb([128, 512], BF16, f"Kh_tm{i}"), V_tm=sb([128, 512], BF16, f"V_tm{i}"),
                qTb=sb([128, 4, 128], BF16, f"qTb{i}"), kTb=sb([128, 4, 128], BF16, f"kTb{i}"),
                khat_tm=sb([128, 512], BF16, f"khat_tm{i}"), i_tm=sb([128, 512], BF16, f"i_tm{i}"),
                gs_t=sb([128, 512], BF16, f"gs_t{i}"), g_tm=sb([128, 512], BF16, f"g_tm{i}"),
                bon8=sb([128, 8], F32, f"bon8{i}"), gC=sb([128, 4, NB], F32, f"gC{i}"), decH=sb([128, 4, NB], F32, f"decH{i}")))
        Pm = sb([128, 8, 128], BF16, "Pm"); Tm = sb([128, 8, 128], BF16, "Tm"); RR = sb([128, 8, 128], BF16, "RR")
        NrbT = sb([128, 8, 128], BF16, "NrbT"); AkT = sb([128, 8, 128], BF16, "AkT"); NrkT = sb([128, 8, 128], BF16, "NrkT")
        TTf = sb([128, 8, 128], BF16, "TTf")
        W1T = sb([128, 4, 128], BF16, "W1T")
        Z_tm = sb([128, 512], BF16, "Z_tm"); U_tm = sb([128, 512], BF16, "U_tm")
        st16 = sb([128, 16], F32, "st16"); m8 = sb([128, 8], F32, "m8"); r8 = sb([128, 8], F32, "r8")
        o_all = sb([128, D], BF16, "o_all"); oT = sb([128, 8, 128], BF16, "oT")
        h1pre = sb([128, D], F32, "h1pre")
        attT = sb([128, 4, 128], BF16, "attT")
        s4 = sb([128, 4], F32, "s4"); rr4 = sb([128, 4], F32, "rr4")
        BT0 = h1pre[:, 0:512]; BT1 = h1pre[:, 512:1024]
        SHtmp = v3(BT1)
        STtmp = BT1[:, 0:256].rearrange("p (a v) -> p a v", a=4)
        identb4 = ident_bf.unsqueeze(1).to_broadcast([128, 4, 128])
        save_off = arena.off
        arena.off = ov_lo
        scrA = [sb([128, 2048], F32, f"scrA{i}") for i in range(2)]
        S0T32 = sb([128, NB, 4, 64], F32, "S0T32")
        S0Tb = sb([128, NB, 4, 64], BF16, "S0Tb")
        sshT = sb([128, 14, NB], F32, "sshT"); lastp = sb([128, 14, NB], F32, "lastp")
        EW = [sb([128, 4, 128], BF16, f"EW{i}") for i in range(2)]
        ER = [sb([128, 4, 128], BF16, f"ER{i}") for i in range(2)]
        EQ = [sb([128, 4, 128], BF16, f"EQ{i}") for i in range(2)]
        Ub = sb([128, 512], BF16, "Ub"); Vb = sb([128, 512], BF16, "Vb"); khb = sb([128, 512], BF16, "khb")
        Dg = sb([128, 4, 64], F32, "Dg")
        S0h = [sb([128, 4, 128], F32, f"S0h{i}") for i in range(2)]
        S0hb = sb([128, 4, 128], BF16, "S0hb")
        Sn = sb([128, 512], F32, "Sn")
        fence_t = sb([128, 2], F32, "fence_t")
        assert arena.off <= ov_hi, (arena.off, ov_hi)
        ov_bufs = scrA + [S0T32, S0Tb, sshT, lastp] + EW + ER + EQ + [Ub, Vb, khb, Dg] + S0h + [S0hb, Sn, fence_t]
        arena.off = save_off
        ssh_tm = scrA[0][0:NB, 0:RWP]
        hh_order = (0, 2, 4, 6, 1, 3, 5, 7)
        heads_of = [(0, 1, 2, 3), (4, 5, 6, 7)]
        PA, PB_ = 6, 7

        cb = cc

        def cfg(sample):
            sfx = "_s" if sample else "_p"
            return dict(nch=NB if sample else 1, Cn=TS if sample else 128, L=3 if sample else 7,
                        mS=cb("mS" + sfx), mI=cb("mI" + sfx), mST=cb("mST" + sfx), reset=cc("reset" + sfx))

        def stageA(ti, x_src, sample):
            g_ = cfg(sample)
            nch, Cn, reset = g_["nch"], g_["Cn"], g_["reset"]
            H = HB[ti % 2]
            AR, bT, kT = H["AR"], H["bT"], H["kT"]
            xt = x_t[ti % 2]
            C.dma(SP, xt, x_src)
            C.dma(POOL, x_bf, x_src)
            for kc in range(8):
                C.tr(psb(PB_)[:, kc * 128:(kc + 1) * 128], x_bf[:, kc * 128:(kc + 1) * 128], ident_bf)
            C.cp(ACT, xT, psb(PB_).rearrange("p (a t) -> p a t", a=8))
            yield
            sig, kq, fgl, bcs, eb, enb, ebl, sq_, eg = T[1], T[4], T[0], T[2], T[3], T[5], T[6], T[7], T[8]

            def proj_fm(c0, n, bank):
                for j in range(n):
                    c = c0 + j
                    for kc in range(8):
                        C.mm(ps[bank][:, j * 128:(j + 1) * 128], w_in_bf[:, kc, c * 128:(c + 1) * 128].k(kc),
                             xT[:, kc, :], start=(kc == 0), stop=(kc == 7))

            def proj_tm(col0, bank):
                for kc in range(8):
                    C.mm(ps[bank], xT[:, kc, :], w_in_bf[:, kc, col0:col0 + 512].k(kc), start=(kc == 0), stop=(kc == 7))

            proj_fm(0, 4, PA); C.cp(ACT, pr_[:, 0:4, 1:129], psv(PA, 4)); yield
            proj_fm(4, 4, PB_); C.cp(ACT, pr_[:, 4:8, 1:129], psv(PB_, 4)); yield
            proj_fm(8, 4, PA); C.cp(ACT, pr_[:, 8:12, 1:129], psv(PA, 4)); yield
            proj_fm(12, 2, PB_); C.cp(ACT, pr_[:, 12:14, 1:129], psv(PB_, 4)[:, 0:2, :]); yield
            prev, cur = pr_[:, :, 0:128], pr_[:, :, 1:129]
            if not sample:
                C.cp(POOL, pr_[:, :, 0:1], plast.unsqueeze(2))
                C.cp(POOL, plast.unsqueeze(2), pr_[:, :, 128:129])
            else:
                C.memset(POOL, pr_[:, :, 0:1], 0.0)
            proj_fm(14, 4, PA)
            C.act(sq_, ps[PA], AF.Sigmoid)
            C.tt(DVE, sq_, ps[PA], sq_, ALU.mult)
            yield
            proj_fm(18, 4, PB_)
            C.act(sig, ps[PB_], AF.Sigmoid)
            yield
            proj_tm(RWP + 1024, PA)
            C.cp(ACT, H["i_tm"], ps[PA])
            yield
            proj_tm(RWP + 1536, PB_)
            C.act(eg, ps[PB_], AF.Sigmoid)
            C.tt(DVE, H["gs_t"], ps[PB_], eg, ALU.mult)
            yield
            if sample:
                C.rec(POOL, lambda e: e.memset(_ap(fence_t), 0.0),
                      [w_in_bf[:, kc, :].k(kc) for kc in range(8)] + ov_bufs
                      + [S0T32[:, b, :, :].k(b) for b in range(NB)] + [S0Tb[:, b, :, :].k(b) for b in range(NB)], [])
                for t_ in EW + ER + EQ:
                    C.memset(POOL, t_, 0.0)
                C.dma(SP, ssh_tm, ssh)
                for c in range(14):
                    C.tr(ps[PA][:, c * NB:(c + 1) * NB], ssh_tm[:, c * 128:(c + 1) * 128], cc("ident")[0:NB, 0:NB])
                C.cp(DVE, sshT, ps[PA][:, 0:14 * NB].rearrange("p (c b) -> p c b", c=14))
            for eng_, c0, c1 in ((DVE, 0, 8), (POOL, 8, 14)):
                C.tt(eng_, xs[:, c0:c1, :].k(c0), prev[:, c0:c1, :], cur[:, c0:c1, :], ALU.subtract)
                C.tt(eng_, xs[:, c0:c1, :].k(c0), xs[:, c0:c1, :].k(c0), bc3(mu14[:, c0:c1], 128), ALU.mult)
                C.tt(eng_, xs[:, c0:c1, :].k(c0), xs[:, c0:c1, :].k(c0), cur[:, c0:c1, :], ALU.add)
            C.rec(POOL, lambda e: e.memset(_ap(fence2), 0.0), [xs, fence2], [xs[:, 0:8, :].k(0), xs[:, 8:14, :].k(8)])
            if sample:
                cur4 = cur.rearrange("p c (b t) -> p c b t", t=TS)
                xs4 = xs.rearrange("p c (b t) -> p c b t", t=TS)
                cur0, xs0 = cur4[:, :, :, 0], xs4[:, :, :, 0]
                C.tt(POOL, xs0, sshT, cur0, ALU.subtract)
                C.tt(POOL, xs0, xs0, bc3(mu14, NB), ALU.mult)
                C.tt(POOL, xs0, xs0, cur0, ALU.add)
                C.cp(POOL, lastp, cur4[:, :, :, TS - 1])
                for g0 in range(0, 14, 4):
                    bk = PA if (g0 // 4) % 2 == 0 else PB_
                    n = min(4, 14 - g0)
                    for j in range(n):
                        C.tr(ps[bk][0:NB, j * 128:(j + 1) * 128], lastp[:, g0 + j, :], cc("ident"))
                    C.cp(ACT, ssh_tm[:, g0 * 128:(g0 + n) * 128], ps[bk][0:NB, 0:n * 128])
                C.dma(SP, shs, ssh_tm, out_final=True)
            yield
            r_ = xs[:, 0:4, :]; k_ = xs[:, 4:8, :]; v_ = xs[:, 8:12, :]
            C.act(z12[0:64, :], xs[0:64, 12, :], AF.Tanh)
            C.act(sg_bf, xs[:, 13, :], AF.Sigmoid)
            C.cp(DVE, z12[64:128, :], xs[64:128, 12, :])
            for pr in range(4):
                sl = slice(pr * 128, (pr + 1) * 128)
                C.mm(ps[PA][:, sl], WA[0:64, sl].k("lo"), z12[0:64, :])
            for pr in range(4):
                sl = slice(pr * 128, (pr + 1) * 128)
                C.mm(ps[PB_][:, sl], WA[64:128, sl].k("hi"), z12[64:128, :])
            sw, alr = T[9], T[8]
            for pr in range(4):
                sl = slice(pr * 128, (pr + 1) * 128)
                C.act(sw[:, sl], ps[PA][:, sl], AF.Sigmoid, bias=w04[:, pr:pr + 1])
            C.tt(DVE, v3(kq), v3(sig), bc3(oml4, 128), ALU.mult)
            C.tt(DVE, v3(fgl), v3(kq), bc3(lb4, 128), ALU.add)
            C.tt(POOL, v3(kq), bc3(oml4, 128), v3(kq), ALU.subtract)
            yield
            for pr in range(4):
                sl = slice(pr * 128, (pr + 1) * 128)
                C.act(alr[:, sl], ps[PB_][:, sl], AF.Sigmoid, bias=a04[:, pr:pr + 1])
            C.mm(ps[PA], sg_bf, g1u_bf)
            C.cp(ACT, H["g_tm"], ps[PA])
            yield
            C.act(fgl, fgl, AF.Ln)
            for h in range(4):
                C.scan(bcs[:, h * 128:(h + 1) * 128], reset, fgl[:, h * 128:(h + 1) * 128])
            C.act(eb, bcs, AF.Exp)
            C.act(enb, bcs, AF.Exp, scale=-1.0)
            bc4 = bcs.rearrange("p (a n c) -> p a n c", a=4, n=nch)
            C.tt(POOL, ebl.rearrange("p (a n c) -> p a n c", a=4, n=nch),
                 bc4[:, :, :, Cn - 1:Cn].to_broadcast([128, 4, nch, Cn]), bc4, ALU.subtract)
            C.act(ebl, ebl, AF.Exp)
            yield
            C.cp(POOL, H["decH"][:, :, 0:nch].unsqueeze(3),
                 eb.rearrange("p (a n c) -> p a n c", a=4, n=nch)[:, :, :, Cn - 1:Cn])
            C.tt(DVE, H["qTb"], v3(sq_), v3(eb), ALU.mult)
            C.tt(POOL, H["kTb"], v3(kq), v3(enb), ALU.mult)
            C.tt(DVE, khTb, v3(kq), v3(ebl), ALU.mult)
            yield
            cumS, gex, gin, ginv, glast, kkk, tq, k2 = T[2], T[3], T[4], T[5], T[6], T[7], T[0], T[1]
            for pr in range(4):
                C.scan(cumS[:, pr * 128:(pr + 1) * 128], reset, sw[:, pr * 128:(pr + 1) * 128])
            C.tt(POOL, gex, cumS, sw, ALU.subtract)
            C.act(gex, gex, AF.Exp, scale=CDEC)
            C.act(gin, cumS, AF.Exp, scale=CDEC)
            C.act(ginv, cumS, AF.Exp, scale=-CDEC)
            cs4 = cumS.rearrange("p (a n c) -> p a n c", a=4, n=nch)
            C.tt(POOL, glast.rearrange("p (a n c) -> p a n c", a=4, n=nch),
                 cs4[:, :, :, Cn - 1:Cn].to_broadcast([128, 4, nch, Cn]), cs4, ALU.subtract)
            C.act(glast, glast, AF.Exp, scale=CDEC)
            C.cp(POOL, H["gC"][:, :, 0:nch].unsqueeze(3),
                 gin.rearrange("p (a n c) -> p a n c", a=4, n=nch)[:, :, :, Cn - 1:Cn])
            yield
            C.tt(POOL, v3(kkk), k_, bc3(kk4, 128), ALU.mult)
            C.act(tqb, kkk, AF.Square)
            for pr in range(4):
                sl = slice(pr * 128, (pr + 1) * 128)
                C.mm(ps[PB_][:, sl], bdones_bf, tqb[:, sl])
            C.act(tq, ps[PB_], AF.Ln, bias=1e-24)
            C.act(tq, tq, AF.Exp, scale=-0.5)
            C.tt(DVE, kkk, kkk, tq, ALU.mult)
            C.stt(DVE, v3(k2), v3(alr), -1.0, bc3(ka4, 128), ALU.add, ALU.mult)
            C.stt(DVE, v3(k2), v3(k2), 1.0, k_, ALU.add, ALU.mult)
            C.tt(DVE, alr, kkk, alr, ALU.mult)
            b_ = alr
            yield
            C.tt(DVE, AR[:, :, 1, :], r_, v3(gin), ALU.mult)
            C.stt(DVE, AR[:, :, 0, :], v3(kkk), -1.0, v3(gex), ALU.mult, ALU.mult)
            C.tt(POOL, bT, v3(b_), v3(ginv), ALU.mult)
            C.tt(POOL, kT, v3(k2), v3(ginv), ALU.mult)
            C.tt(DVE, bhT, v3(b_), v3(glast), ALU.mult)
            C.tt(POOL, khT, v3(k2), v3(glast), ALU.mult)
            C.cp(POOL, vT, v_)
            C.tt(DVE, v3(tqb), r_, v3(k2), ALU.mult)
            for pr in range(4):
                C.mm(ps[PA][:, pr * 2:(pr + 1) * 2], tqb[:, pr * 128:(pr + 1) * 128], RKsel[:, pr, :])
            C.cp(DVE, H["bon8"], ps[PA][:, 0:8])
            yield
            for (src, bank, half) in ((lambda pr: AR[:, pr, 0, :], PA, 0), (lambda pr: bhT[:, pr, :], PA, 1),
                                      (lambda pr: khT[:, pr, :], PB_, 0), (lambda pr: vT[:, pr, :], PB_, 1)):
                for pr in range(4):
                    o0 = half * 512 + pr * 128
                    C.tr(psb(bank)[:, o0:o0 + 128], src(pr), ident_bf)
            C.cp(ACT, H["A_tm"], psb(PA)[:, 0:512]); C.cp(ACT, H["Bh_tm"], psb(PA)[:, 512:1024])
            C.cp(ACT, H["Kh_tm"], psb(PB_)[:, 0:512]); C.cp(ACT, H["V_tm"], psb(PB_)[:, 512:1024])
            yield
            for h in range(4):
                C.tr(psb(PA)[:, h * 128:(h + 1) * 128], khTb[:, h, :], ident_bf)
            C.cp(ACT, H["khat_tm"], psb(PA)[:, 0:512])
            yield

        def stageB(ti, h1_dst, sample):
            g_ = cfg(sample)
            nch, Cn, L = g_["nch"], g_["Cn"], g_["L"]
            mSb = g_["mS"].unsqueeze(1).to_broadcast([128, 4, 128])
            mIb = g_["mI"].unsqueeze(1).to_broadcast([128, 4, 128])
            mSTb = g_["mST"].unsqueeze(1).to_broadcast([128, 4, 128])
            H = HB[ti % 2]
            AR, bT, kT, A_tm, Bh_tm, Kh_tm, V_tm = H["AR"], H["bT"], H["kT"], H["A_tm"], H["Bh_tm"], H["Kh_tm"], H["V_tm"]
            qTb, kTb, khat_tm, i_tm, gs_t, g_tm, bon8, gC, decH = (H["qTb"], H["kTb"], H["khat_tm"], H["i_tm"], H["gs_t"],
                                                                    H["g_tm"], H["bon8"], H["gC"], H["decH"])
            xt = x_t[ti % 2]
            for h in range(4):
                C.mm(ps[3][:, h * 128:(h + 1) * 128], kTb[:, h, :], qTb[:, h, :])
            C.tt(DVE, attT, psv(3, 4), mIb, ALU.mult)
            if not sample:
                for h in range(4):
                    hsl = slice(h * 128, (h + 1) * 128)
                    C.mm(ps[4][:, hsl], attT[:, h, :], i_tm[:, hsl], start=True, stop=False)
                    C.mm(ps[4][:, hsl], qTb[:, h, :], SH_bf[:, h, :], start=False, stop=True)
                for h in range(4):
                    hsl = slice(h * 128, (h + 1) * 128)
                    C.mm(ps[5][:, hsl], khat_tm[:, hsl], i_tm[:, hsl])
                C.tt(POOL, SHtmp, SH, bc3(decH[:, :, 0], 128), ALU.mult)
                C.tt(DVE, SH, SHtmp, psv(5, 4), ALU.add)
                C.cp(POOL, SH_bf, SH)
            else:
                for h in range(4):
                    hsl = slice(h * 128, (h + 1) * 128)
                    C.mm(ps[4][:, hsl], attT[:, h, :], i_tm[:, hsl], start=(h == 0), stop=False)
                for b in range(NB):
                    s0 = S0h[b % 2]
                    csl = slice(b * TS, (b + 1) * TS)
                    C.dma(SP, s0, shg[b].rearrange("h k v -> k h v"))
                    C.cp(ACT, S0hb, s0)
                    eq = EQ[b % 2]
                    C.cp(POOL, eq[:, :, csl], qTb[:, :, csl])
                    for h in range(4):
                        C.mm(ps[4][:, h * 128:(h + 1) * 128], eq[:, h, :], S0hb[:, h, :], start=False, stop=False)
                    C.memset(POOL, eq[:, :, csl], 0.0)
                    C.ts(DVE, khb, khat_tm, cc("cm")[:, b:b + 1], ALU.mult)
                    bank = (5, 3)[b % 2]
                    for h in range(4):
                        hsl = slice(h * 128, (h + 1) * 128)
                        C.mm(ps[bank][:, hsl], khb[:, hsl], i_tm[:, hsl])
                    C.tt(DVE, s0, s0, bc3(decH[:, :, b], 128), ALU.mult)
                    C.tt(DVE, s0, s0, psv(bank, 4), ALU.add)
                    C.dma(SP, hgs[b].rearrange("h k v -> k h v"), s0, out_final=True)
                    yield
            osq = BT0
            C.act(osq, ps[4], AF.Square)
            C.red(DVE, s4, v3(osq))
            C.act(rr4, s4, AF.Ln, scale=1.0 / 128, bias=RMS_EPS)
            C.act(rr4, rr4, AF.Exp, scale=-0.5)
            C.tt(DVE, v3(osq), psv(4, 4), bc3(rr4, 128), ALU.mult)
            C.tt(POOL, osq, osq, hgw, ALU.mult)
            C.tt(POOL, o_all[:, 512:1024], osq, gs_t, ALU.mult)
            yield
            if sample:
                for g in range(4):
                    sa = scrA[g % 2]
                    nat = sa[0:64, :].rearrange("p (b n) -> p b n", b=4)
                    C.dma(SP, nat.rearrange("p b (h j) -> p b h j", h=8),
                          srw[g * 4:(g + 1) * 4].rearrange("b h v j -> v b h j"))
                    for bb in range(4):
                        b = g * 4 + bb
                        bank = b % 2
                        for pr in range(4):
                            C.tr(ps[bank][:, (bb % 2) * 256 + pr * 64:(bb % 2) * 256 + (pr + 1) * 64],
                                 nat[:, bb, pr * 128:(pr + 1) * 128], cc("ident")[0:64, 0:64])
                        C.cp(ACT, S0T32[:, b, :, :].k(b),
                             ps[bank][:, (bb % 2) * 256:(bb % 2) * 256 + 256].rearrange("p (a v) -> p a v", a=4))
                        C.cp(DVE, S0Tb[:, b, :, :].k(b), S0T32[:, b, :, :].k(b))
                    yield
            for hg in range(2):
                for i, h in enumerate(heads_of[hg]):
                    pr, hh = h // 2, h % 2
                    rows = slice(64 * hh, 64 * hh + 64)
                    sl = slice(i * 128, (i + 1) * 128)
                    C.mm(ps[0][:, sl], bT[rows, pr, :], AR[rows, pr, 0, :])
                    C.mm(ps[1][:, sl], bT[rows, pr, :], AR[rows, pr, 1, :])
                    C.mm(ps[2][:, sl], kT[rows, pr, :], AR[rows, pr, 0, :])
                    C.mm(ps[3][:, sl], kT[rows, pr, :], AR[rows, pr, 1, :])
                    C.mm(ps[4][:, sl], AR[rows, pr, 0, :], bT[rows, pr, :])
                hs = slice(4 * hg, 4 * hg + 4)
                C.tt(DVE, Pm[:, hs, :], psv(0, 4), mSb, ALU.mult)
                C.tt(DVE, NrbT[:, hs, :], psv(1, 4), mIb, ALU.mult)
                C.tt(DVE, AkT[:, hs, :], psv(2, 4), mSb, ALU.mult)
                C.tt(DVE, NrkT[:, hs, :], psv(3, 4), mIb, ALU.mult)
                C.tt(DVE, RR[:, hs, :], psv(4, 4), mSTb, ALU.mult)
                C.tt(POOL, Tm[:, hs, :], Pm[:, hs, :], identb4, ALU.add)
                yield
            for k in range(L):
                for hg in range(2):
                    b0 = 3 * hg
                    hs = slice(4 * hg, 4 * hg + 4)
                    for i, h in enumerate(heads_of[hg]):
                        sl = slice(i * 128, (i + 1) * 128)
                        if k < L - 1:
                            C.mm(ps[b0][:, sl], RR[:, h, :], Pm[:, h, :])
                        if k >= 1:
                            C.mm(ps[b0 + 1][:, sl], RR[:, h, :], Tm[:, h, :])
                        if k < L - 1:
                            C.mm(ps[b0 + 2][:, sl], Pm[:, h, :], RR[:, h, :])
                    if k < L - 1:
                        C.cp(ACT, Pm[:, hs, :], psv(b0, 4))
                    if k >= 1:
                        dst = TTf[:, hs, :] if k == L - 1 else Tm[:, hs, :]
                        C.tt(DVE, dst, psv(b0 + 1, 4), Tm[:, hs, :], ALU.add)
                    if k < L - 1:
                        C.cp(ACT, RR[:, hs, :], psv(b0 + 2, 4))
                    yield
            for h in range(8):
                C.mm(ps[2][:, h * 64:(h + 1) * 64], AkT[:, h, :], V_tm[:, h * 64:(h + 1) * 64])
            C.cp(ACT, Z_tm, ps[2])
            for h in range(8):
                pr = h // 2
                C.mm(ps[h // 4][:, (h % 4) * 128:(h % 4 + 1) * 128], A_tm[:, pr * 128:(pr + 1) * 128], TTf[:, h, :])
            for b in range(2):
                pv = ps[b].rearrange("p (q e t) -> p q e t", q=2, e=2)
                C.cp(ACT, W1T[0:64, 2 * b:2 * b + 2, :], pv[0:64, :, 0, :])
                C.cp(ACT, W1T[64:128, 2 * b:2 * b + 2, :], pv[64:128, :, 1, :])
            yield
            for h in range(8):
                pr, hh = h // 2, h % 2
                rows = slice(64 * hh, 64 * hh + 64)
                hsl = slice(h * 64, (h + 1) * 64)
                if not sample:
                    C.mm(ps[3][:, hsl], TTf[:, h, :], Z_tm[:, hsl], start=True, stop=False)
                    C.mm(ps[3][:, hsl], W1T[rows, pr, :], ST_bf[rows, pr, :], start=False, stop=True)
                else:
                    C.mm(ps[3][:, hsl], TTf[:, h, :], Z_tm[:, hsl], start=(h == 0), stop=False)
            if sample:
                for b in range(NB):
                    ew = EW[b % 2]
                    csl = slice(b * TS, (b + 1) * TS)
                    C.cp(POOL, ew[:, :, csl], W1T[:, :, csl])
                    for h in hh_order:
                        pr, hh = h // 2, h % 2
                        rows = slice(64 * hh, 64 * hh + 64)
                        C.mm(ps[3][:, h * 64:(h + 1) * 64], ew[rows, pr, :], S0Tb[rows, b, pr, :].k(b), start=False, stop=False)
                    C.memset(POOL, ew[:, :, csl], 0.0)
                    yield
            C.cp(ACT, U_tm, ps[3])
            for h in range(8):
                pr, hh = h // 2, h % 2
                rows = slice(64 * hh, 64 * hh + 64)
                hsl = slice(h * 64, (h + 1) * 64)
                C.mm(ps[4][:, hsl], NrbT[:, h, :], U_tm[:, hsl], start=(h == 0 or not sample), stop=False)
                C.mm(ps[4][:, hsl], NrkT[:, h, :], V_tm[:, hsl], start=False, stop=False)
                if not sample:
                    C.mm(ps[4][:, hsl], AR[rows, pr, 1, :], ST_bf[rows, pr, :], start=False, stop=True)
            yield
            if sample:
                for b in range(NB):
                    er = ER[b % 2]
                    csl = slice(b * TS, (b + 1) * TS)
                    C.cp(POOL, er[:, :, csl], AR[:, :, 1, csl])
                    for h in hh_order:
                        pr, hh = h // 2, h % 2
                        rows = slice(64 * hh, 64 * hh + 64)
                        C.mm(ps[4][:, h * 64:(h + 1) * 64], er[rows, pr, :], S0Tb[rows, b, pr, :].k(b), start=False, stop=False)
                    C.memset(POOL, er[:, :, csl], 0.0)
                    yield
                i64b = cc("i64s").unsqueeze(1).to_broadcast([128, 4, 64])
                for b in range(NB):
                    bank = b % 2
                    C.ts(DVE, Ub, U_tm, cc("cm")[:, b:b + 1], ALU.mult)
                    C.ts(DVE, Vb, V_tm, cc("cm")[:, b:b + 1], ALU.mult)
                    C.tt(DVE, Dg, i64b, bc3(gC[:, :, b], 64), ALU.mult)
                    for h in range(8):
                        hsl = slice(h * 64, (h + 1) * 64)
                        C.mm(ps[bank][0:64, hsl], Ub[:, hsl], Bh_tm[:, hsl], start=(h == 0), stop=False)
                        C.mm(ps[bank][0:64, hsl], Vb[:, hsl], Kh_tm[:, hsl], start=False, stop=False)
                    for h in hh_order:
                        pr, hh = h // 2, h % 2
                        rows = slice(64 * hh, 64 * hh + 64)
                        C.mm(ps[bank][0:64, h * 64:(h + 1) * 64], S0T32[rows, b, pr, :].k(b), Dg[rows, pr, :], start=False, stop=False)
                    C.cp(ACT, Sn[0:64, :], ps[bank][0:64, :])
                    C.dma(SP, rws[b].rearrange("h v j -> v h j"), Sn[0:64, :].rearrange("p (h j) -> p h j", h=8), out_final=True)
                    yield
            if not sample:
                for pr in range(4):
                    psl = slice(pr * 128, (pr + 1) * 128)
                    C.mm(ps[5][:, psl], Bh_tm[:, psl], U_tm[:, psl], start=True, stop=False)
                    C.mm(ps[5][:, psl], Kh_tm[:, psl], V_tm[:, psl], start=False, stop=True)
                C.tt(POOL, STtmp, ST, bc3(gC[:, :, 0], 64), ALU.mult)
                p5 = psv(5, 4)
                C.tt(DVE, ST[0:64, :, :], STtmp[0:64, :, :], p5[0:64, :, 0:64], ALU.add)
                C.tt(DVE, ST[64:128, :, :], STtmp[64:128, :, :], p5[64:128, :, 64:128], ALU.add)
                C.cp(POOL, ST_bf, ST)
            yield
            ysq, tmp2 = BT0, BT1
            y3 = ps[4].rearrange("p (h v) -> p h v", h=8)
            yv = ysq.rearrange("p (h v) -> p h v", h=8)
            C.red(DVE, st16[:, 0:8], y3)
            C.act(ysq, ps[4], AF.Square)
            C.red(DVE, st16[:, 8:16], yv)
            C.ts(DVE, m8, st16[:, 0:8], 1.0 / 64, ALU.mult)
            C.tt(DVE, r8, m8, m8, ALU.mult)
            C.stt(DVE, r8, st16[:, 8:16], 1.0 / 64, r8, ALU.mult, ALU.subtract)
            C.act(r8, r8, AF.Ln, bias=GN_EPS)
            C.act(r8, r8, AF.Exp, scale=-0.5)
            C.tt(DVE, yv, y3, bc3(m8, 64), ALU.subtract)
            C.tt(POOL, yv, yv, bc3(r8, 64), ALU.mult)
            C.tt(POOL, ysq, ysq, lnxw, ALU.mult)
            C.tt(POOL, ysq, ysq, lnxb, ALU.add)
            C.tt(DVE, tmp2.rearrange("p (h v) -> p h v", h=8), V_tm.rearrange("p (h v) -> p h v", h=8),
                 bc3(bon8, 64), ALU.mult)
            C.tt(DVE, ysq, ysq, tmp2, ALU.add)
            C.tt(DVE, o_all[:, 0:512], ysq, g_tm, ALU.mult)
            yield
            for mc in range(8):
                C.tr(psb(2)[:, mc * 128:(mc + 1) * 128], o_all[:, mc * 128:(mc + 1) * 128], ident_bf)
            C.cp(ACT, oT, psb(2).rearrange("p (a t) -> p a t", a=8))
            for half in range(2):
                for mc in range(8):
                    C.mm(ps[half], oT[:, mc, :], w_out_bf[:, mc, half * 512:(half + 1) * 512].k(mc),
                         start=(mc == 0), stop=(mc == 7))
            for half in range(2):
                hsl = slice(half * 512, (half + 1) * 512)
                C.stt(DVE, h1pre[:, hsl], xt[:, hsl], ALPHA, ps[half], ALU.mult, ALU.add)
            yield
            layernorm(h1pre, h1pre, LN_EPS)
            C.dma(SP, h1_dst, h1pre)
            yield

        def drain(g):
            for _ in g:
                pass

        def interleave(ga, gb):
            la = lb_ = True
            while la or lb_:
                if lb_:
                    for _ in range(1):
                        try:
                            next(gb)
                        except StopIteration:
                            lb_ = False
                            break
                if la:
                    try:
                        next(ga)
                    except StopIteration:
                        la = False

        jobs = [(ti, xp[ti * 128:(ti + 1) * 128, :], V(h1scr[ti * 128:(ti + 1) * 128, :], ("h1scr", ti)), False)
                for ti in range(NT)]
        if SAMPLE:
            jobs.append((NT, xsm, V(h1scr[NTP * 128:(NTP + 1) * 128, :], ("h1scr", NTP)), True))
        if True:
            if jobs:
                drain(stageA(jobs[0][0], jobs[0][1], jobs[0][3]))
            for n, (ti, x_src, h1_dst, smp) in enumerate(jobs):
                gb = stageB(ti, h1_dst, smp)
                if n + 1 < len(jobs):
                    nj = jobs[n + 1]
                    interleave(stageA(nj[0], nj[1], nj[3]), gb)
                else:
                    drain(gb)
                if n == NT - 1 and not smp:
                    prompt_state_outputs()
            if NT == 0:
                prompt_state_outputs()

        if True:
            P.barrier()
            arena.off = phase_mark
            w_up_bf = sb([128, 8, DFF], BF16, "w_up_bf")
            w_dn_bf = sb([128, 32, D], BF16, "w_dn_bf")
            for cb in range(8):
                C.dma(POOL, w_up_bf[:, :, cb * 512:(cb + 1) * 512].k(cb),
                      w_up[:, cb * 512:(cb + 1) * 512].rearrange("(kc p) n -> p kc n", p=128))
            C.dma(SP, lng, row(ln2_g).partition_broadcast(128))
            C.dma(SP, lnb, row(ln2_b).partition_broadcast(128))
            for fc in range(32):
                C.dma(POOL, w_dn_bf[:, fc, :].k(fc), w_down[fc * 128:(fc + 1) * 128, :])
            upT = sb([128, 32, 512], BF16, "upT")
            h1T = sb([128, 8, 512], BF16, "h1T")
            h1b = [sb([128, D], BF16, f"h1b{i}") for i in range(2)]
            h1r = [sb([128, D], F32, f"h1r{i}") for i in range(2)]
            rl = [sb([128, 512], F32, f"rl{i}") for i in range(2)]
            pre2 = sb([128, D], F32, "pre2")
            outb = [sb([128, D], F32, f"outb{i}") for i in range(2)]
            ntiles = NT + (1 if SAMPLE else 0)
            tiles = list(range(NT)) + ([NTP] if SAMPLE else [])
            groups = [tiles[i:i + 4] for i in range(0, NT, 4)]
            if SAMPLE:
                groups.append([NTP])
            gcount = 0
            tcount = 0
            for grp in groups:
                ng = len(grp)
                W = ng * 128
                for gi, tix in enumerate(grp):
                    hb = h1b[(tcount + gi) % 2]
                    C.dma(POOL, hb, V(h1scr[tix * 128:(tix + 1) * 128, :], ("h1scr", tix)))
                    bank = 6 + (gi % 2)
                    for kc in range(8):
                        C.tr(psb(bank)[:, kc * 128:(kc + 1) * 128], hb[:, kc * 128:(kc + 1) * 128], ident_bf)
                    C.cp(ACT, h1T[:, :, gi * 128:(gi + 1) * 128], psb(bank).rearrange("p (a t) -> p a t", a=8))
                for fc in range(32):
                    bank = fc % 2
                    for kc in range(8):
                        C.mm(ps[bank][:, 0:W], w_up_bf[:, kc, fc * 128:(fc + 1) * 128].k(fc // 4), h1T[:, kc, 0:W],
                             start=(kc == 0), stop=(kc == 7))
                    r = rl[fc % 2]
                    C.act(r[:, 0:W], ps[bank][:, 0:W], AF.Relu)
                    C.tt(POOL if fc % 2 else DVE, upT[:, fc, 0:W], r[:, 0:W], r[:, 0:W], ALU.mult)
                for gi, tix in enumerate(grp):
                    hr = h1r[(tcount + gi) % 2]
                    C.dma(SP, hr, V(h1scr[tix * 128:(tix + 1) * 128, :], ("h1scr", tix)))
                    for half in range(2):
                        bank = 2 + ((gi * 2 + half) % 4)
                        for fc in range(32):
                            C.mm(ps[bank], upT[:, fc, gi * 128:(gi + 1) * 128], w_dn_bf[:, fc, half * 512:(half + 1) * 512].k(fc),
                                 start=(fc == 0), stop=(fc == 31))
                        hsl = slice(half * 512, (half + 1) * 512)
                        C.stt(DVE, pre2[:, hsl], hr[:, hsl], ALPHA, ps[bank], ALU.mult, ALU.add)
                    ob = outb[(tcount + gi) % 2]
                    layernorm(pre2, ob, LN_EPS)
                    dst = ys if tix == NTP else yp[tix * 128:(tix + 1) * 128, :]
                    C.dma(SP, dst, ob, out_final=True)
                tcount += ng
                gcount += 1

        sems = {e: es.enter_context(nc.semaphore(f"s_{e}")) for e in ENGINES}
        rings = {e: [es.enter_context(nc.semaphore(f"r_{e}{i}")) for i in range(n)] for e, n in DMA_RING.items()}
        P.prepare(sems, rings)
        build.stats = dict(P.stats)
        build.arena_peak = arena.peak
        with nc.allow_low_precision(reason="bf16 matmul operands, fp32 accumulation"), \
                nc.allow_non_contiguous_dma(reason="tiny per-channel vectors / state layouts"), \
                nc.Block() as block:
            block.tensor(lambda eng: P.emit_engine(PE, eng))
            block.scalar(lambda eng: P.emit_engine(ACT, eng))
            block.vector(lambda eng: P.emit_engine(DVE, eng))
            block.gpsimd(lambda eng: P.emit_engine(POOL, eng))
            block.sync(lambda eng: P.emit_engine(SP, eng))
    return nc


IN_NAMES = ["w_in", "shift_mu", "w0", "w1u", "a0", "a1u", "g1u", "k_k", "k_a", "r_k", "ln_x_w", "ln_x_b",
            "lb_logits", "hg_norm_w", "w_out", "ln1_g", "ln1_b", "w_up", "w_down", "ln2_g", "ln2_b"]


def make_in_maps(inputs, n_cores=8):
    f = lambda a: np.ascontiguousarray(np.asarray(a, dtype=np.float32))
    shared = {}
    for k in IN_NAMES:
        a = f(inputs[k])
        a = a[0] if k != "lb_logits" else a
        if k == "r_k":
            a = a.reshape(-1)
        shared[k] = np.ascontiguousarray(a)
    shared["consts"] = CONSTS
    maps = []
    for c in range(n_cores):
        m = dict(shared)
        m["xp"] = f(inputs["x_prompt"][c])
        m["xs"] = f(inputs["x_sample"][c * NB:(c + 1) * NB]).reshape(NB * TS, D)
        m["srw"] = f(inputs["state_rwkv"][0, c * NB:(c + 1) * NB])
        m["shg"] = f(inputs["state_hgrn"][0, c * NB:(c + 1) * NB])
        m["ssh"] = f(inputs["state_shift"][0, c * NB:(c + 1) * NB])
        maps.append(m)
    return maps


_NC_CACHE = {}


def kernel(**inputs):
    if "nc" not in _NC_CACHE:
        _NC_CACHE["nc"] = build()
    nc = _NC_CACHE["nc"]
    maps = make_in_maps(inputs)
    res = run_bass_kernel_spmd(nc, maps, core_ids=list(range(8)))
    R = res.results
    y_prompt = np.stack([R[c]["yp"] for c in range(8)]).astype(np.float32)
    y_sample = np.concatenate([R[c]["ys"].reshape(NB, TS, D) for c in range(8)]).astype(np.float32)
    rw_p = np.stack([R[c]["rwp"] for c in range(8)])[None].astype(np.float32)
    rw_s = np.concatenate([R[c]["rws"] for c in range(8)])[None].astype(np.float32)
    hg_p = np.stack([R[c]["hgp"] for c in range(8)])[None].astype(np.float32)
    hg_s = np.concatenate([R[c]["hgs"] for c in range(8)])[None].astype(np.float32)
    sh_p = np.stack([R[c]["shp"] for c in range(8)])[None].astype(np.float32)
    sh_s = np.concatenate([R[c]["shs"] for c in range(8)])[None].astype(np.float32)
    return (y_prompt, y_sample, rw_p, rw_s, hg_p, hg_s, sh_p, sh_s)
```

```python
import numpy as np
import concourse.bass as bass
import concourse.mybir as mybir
from concourse.bass_utils import run_bass_kernel_spmd

F32 = mybir.dt.float32
BF16 = mybir.dt.bfloat16
AF = mybir.ActivationFunctionType
ALU = mybir.AluOpType
AX = mybir.AxisListType

DEBUG_LINES = False
LINE_OF = {}
PE, ACT, DVE, POOL, SP = "pe", "act", "dve", "pool", "sp"
ENGINES = (PE, ACT, DVE, POOL, SP)
DMA_RING = {SP: 12, ACT: 4, POOL: 8}


class Res:
    __slots__ = ("name", "last_w", "readers")

    def __init__(self, name):
        self.name = name
        self.last_w = None
        self.readers = []


class Op:
    __slots__ = ("eng", "fn", "deps", "is_dma", "signal", "idx", "dma_no", "extra_wait", "rg")

    def __init__(self, eng, fn, is_dma):
        self.eng = eng
        self.fn = fn
        self.deps = set()
        self.is_dma = is_dma
        self.signal = False
        self.idx = None
        self.dma_no = None
        self.extra_wait = None
        self.rg = None


def _pe_inorder_ok(d, o):
    return d.rg is None or o.rg is None or d.rg == o.rg


class Prog:
    def __init__(self):
        self.ops = {e: [] for e in ENGINES}
        self.order = []
        self.n_dma = {e: 0 for e in ENGINES}
        self.dma_ops = {e: [] for e in ENGINES}
        self.out_dmas = []

    def op(self, eng, fn, reads=(), writes=(), dma=False, out=False):
        o = Op(eng, fn, dma)
        for r in reads:
            if r.last_w is not None:
                o.deps.add(r.last_w)
        for w in writes:
            if w.last_w is not None:
                o.deps.add(w.last_w)
            for rd in w.readers:
                o.deps.add(rd)
        for r in reads:
            r.readers.append(o)
        for w in writes:
            w.last_w = o
            w.readers = []
        if getattr(self, "barrier_left", None) and eng in self.barrier_left:
            self.barrier_left.discard(eng)
            o.deps.update(self.pending_barrier)
        o.deps.discard(o)
        if dma:
            o.dma_no = self.n_dma[eng]
            self.n_dma[eng] += 1
            self.dma_ops[eng].append(o)
            o.signal = True
            if out:
                self.out_dmas.append(o)
        self.ops[eng].append(o)
        self.order.append(o)
        return o

    def barrier(self):
        pend = []
        for e in ENGINES:
            comp = [o for o in self.ops[e] if not o.is_dma]
            if comp:
                pend.append(comp[-1])
            pend.extend(self.dma_ops[e][-DMA_RING.get(e, 0):] if e in DMA_RING else [])
        self.pending_barrier = pend
        self.barrier_left = set(ENGINES)

    def prepare(self, sems, rings):
        for o in self.order:
            for d in o.deps:
                if d.is_dma:
                    continue
                if d.eng == o.eng and d.eng == PE and not o.is_dma and _pe_inorder_ok(d, o):
                    continue
                d.signal = True
        sig = {}
        for e in ENGINES:
            c = 0
            for o in self.ops[e]:
                if o.is_dma:
                    R = len(rings[e])
                    sig[o] = (rings[e][o.dma_no % R], 16 * (o.dma_no // R + 1))
                elif o.signal:
                    c += 1
                    sig[o] = (sems[e], c)
        self.sig = sig
        self.rings = rings
        self.stats = {e: len(self.ops[e]) for e in ENGINES}

    def emit_engine(self, e, eng):
        sig, rings = self.sig, self.rings
        waited = {}

        def wait(sem, val):
            k = id(sem)
            if waited.get(k, 0) >= val:
                return
            waited[k] = val
            eng.wait_ge(sem, val)

        for o in self.ops[e]:
            for d in o.deps:
                if d not in sig:
                    continue
                if d.eng == e and not d.is_dma and not o.is_dma and e == PE and _pe_inorder_ok(d, o):
                    continue
                s, v = sig[d]
                wait(s, v)
            if o.is_dma:
                R = len(rings[e])
                if o.dma_no >= R:
                    wait(rings[e][o.dma_no % R], 16 * (o.dma_no // R))
            ins = o.fn(eng)
            if DEBUG_LINES:
                LINE_OF[str(getattr(getattr(ins, "ins", ins), "name", ins))] = o.extra_wait
            if o in sig:
                s, v = sig[o]
                ins.then_inc(s, 16 if o.is_dma else 1)
        if e == SP:
            for q in ENGINES:
                if q not in rings:
                    continue
                for o in self.dma_ops[q][-len(rings[q]):]:
                    s, v = sig[o]
                    wait(s, v)


class V:
    __slots__ = ("ap", "key")

    def __init__(self, ap, key):
        self.ap = ap
        self.key = key

    def __getitem__(self, idx):
        return V(self.ap[idx], self.key)

    def k(self, sub):
        return V(self.ap, (self.key, sub))

    @property
    def shape(self):
        return self.ap.shape

    def rearrange(self, *a, **kw):
        return V(self.ap.rearrange(*a, **kw), self.key)

    def unsqueeze(self, ax):
        return V(self.ap.unsqueeze(ax), self.key)

    def to_broadcast(self, shape):
        return V(self.ap.to_broadcast(list(shape)), self.key)

    def bitcast(self, dt):
        return V(self.ap.bitcast(dt), self.key)


def _ap(x):
    return x.ap if isinstance(x, V) else x


def _isnum(x):
    return isinstance(x, (int, float))


class Arena:
    def __init__(self, nc, es, nbytes):
        self.n2 = nbytes // 2
        self.t = es.enter_context(nc.sbuf_tensor("arena", [128, self.n2], BF16))
        self.off = 0
        self.cnt = 0
        self.peak = 0

    def alloc(self, shape, dt, name=None):
        shape = list(shape)
        esz = 4 if dt == F32 else 2
        n = int(np.prod(shape[1:]))
        nbytes = (n * esz + 3) // 4 * 4
        o = self.off
        assert o + nbytes <= self.n2 * 2, f"arena overflow allocating {name} {shape}: {o}+{nbytes} > {self.n2 * 2}"
        self.off += nbytes
        self.peak = max(self.peak, self.off)
        ap = self.t[0:shape[0], o // 2:o // 2 + nbytes // 2]
        if esz == 4:
            ap = ap.bitcast(F32)
        ap = ap[:, 0:n]
        if len(shape) == 3:
            ap = ap.rearrange("p (a b) -> p a b", a=shape[1])
        elif len(shape) == 4:
            ap = ap.rearrange("p (a b c) -> p a b c", a=shape[1], b=shape[2])
        self.cnt += 1
        return V(ap, name or f"t{self.cnt}")


class Ctx:
    def __init__(self, nc, P, arena):
        self.nc, self.P, self.arena = nc, P, arena
        self.res = {}

    def sb(self, shape, dt, name=None):
        return self.arena.alloc(shape, dt, name)

    def R(self, x):
        k = x.key if isinstance(x, V) else (x.name, None)
        r = self.res.get(k)
        if r is None:
            r = self.res[k] = Res(k)
        return r

    def rec(self, eng, fn, outs, ins, dma=False, out=False):
        reads = [self.R(i) for i in ins if i is not None and not _isnum(i)]
        writes = [self.R(o) for o in outs]
        writes += [r for r in reads if isinstance(r.name, str) and r.name.startswith("ps")]
        o_ = self.P.op(eng, fn, reads=reads, writes=writes, dma=dma, out=out)
        if DEBUG_LINES:
            import sys as _s
            f = _s._getframe(1)
            while f.f_code.co_name not in ("mixer_tile", "build", "layernorm") and f.f_back is not None:
                f = f.f_back
            o_.extra_wait = f.f_lineno
        return o_

    def mm(self, out, lhsT, rhs, start=True, stop=True):
        o, l, r = _ap(out), _ap(lhsT), _ap(rhs)
        op = self.rec(PE, lambda e: e.matmul(o, lhsT=l, rhs=r, start=start, stop=stop,
                                             skip_group_check=True), [out], [lhsT, rhs])
        kr = l.shape[0]
        if kr < 128:
            op.rg = (kr, l.base_partition())
        return op

    def tr(self, out, in_, ident):
        o, i, d = _ap(out), _ap(in_), _ap(ident)
        return self.rec(PE, lambda e: e.transpose(o, i, d), [out], [in_, ident])

    def act(self, out, in_, func, bias=None, scale=1.0):
        o, i = _ap(out), _ap(in_)
        kw = {}
        if bias is not None:
            kw["bias"] = _ap(bias)
        s = _ap(scale)
        return self.rec(ACT, lambda e: e.activation(out=o, in_=i, func=func, scale=s, **kw), [out],
                        [in_, bias, scale])

    def tt(self, eng, out, a, b, op):
        o, x, y = _ap(out), _ap(a), _ap(b)
        return self.rec(eng, lambda e: e.tensor_tensor(out=o, in0=x, in1=y, op=op), [out], [a, b])

    def ts(self, eng, out, a, s1, op0, s2=None, op1=None):
        o, x, v1, v2 = _ap(out), _ap(a), _ap(s1), _ap(s2)
        if op1 is None:
            f = lambda e: e.tensor_scalar(out=o, in0=x, scalar1=v1, scalar2=None, op0=op0)
        else:
            f = lambda e: e.tensor_scalar(out=o, in0=x, scalar1=v1, scalar2=v2, op0=op0, op1=op1)
        return self.rec(eng, f, [out], [a, s1, s2])

    def stt(self, eng, out, in0, scalar, in1, op0, op1):
        o, x, y, s = _ap(out), _ap(in0), _ap(in1), _ap(scalar)
        return self.rec(eng, lambda e: e.scalar_tensor_tensor(out=o, in0=x, scalar=s, in1=y, op0=op0, op1=op1),
                        [out], [in0, in1, scalar])

    def cp(self, eng, out, in_):
        o, i = _ap(out), _ap(in_)
        if eng == ACT:
            return self.rec(ACT, lambda e: e.activation(out=o, in_=i, func=AF.Copy), [out], [in_])
        return self.rec(eng, lambda e: e.tensor_copy(out=o, in_=i), [out], [in_])

    def recip(self, out, in_):
        o, i = _ap(out), _ap(in_)
        return self.rec(DVE, lambda e: e.reciprocal(out=o, in_=i), [out], [in_])

    def scan(self, out, d0, d1):
        o, a, b = _ap(out), _ap(d0), _ap(d1)
        return self.rec(DVE, lambda e: e.tensor_tensor_scan(out=o, data0=a, data1=b, initial=0.0,
                                                            op0=ALU.mult, op1=ALU.add), [out], [d0, d1])

    def red(self, eng, out, in_, op=ALU.add):
        o, i = _ap(out), _ap(in_)
        return self.rec(eng, lambda e: e.tensor_reduce(out=o, in_=i, axis=AX.X, op=op), [out], [in_])

    def memset(self, eng, out, val):
        o = _ap(out)
        return self.rec(eng, lambda e: e.memset(o, val), [out], [])

    def dma(self, eng, out, in_, out_final=False, extra_out=()):
        o, i = _ap(out), _ap(in_)
        return self.rec(eng, lambda e: e.dma_start(out=o, in_=i), [out, *extra_out], [in_], dma=True, out=out_final)

    def bn_stats(self, out, in_):
        o, i = _ap(out), _ap(in_)
        return self.rec(DVE, lambda e: e.bn_stats(out=o, in_=i), [out], [in_])

    def bn_aggr(self, out, in_):
        o, i = _ap(out), _ap(in_)
        return self.rec(DVE, lambda e: e.bn_aggr(out=o, in_=i), [out], [in_])


D = 1024
PJ = 3840
RWP = 1792
NTP = 16
NB = 16
TS = 8
DFF = 4096
ALPHA = 2.0 ** 0.25
CDEC = -float(np.exp(-0.5))
LN_EPS = 1e-5
GN_EPS = 64e-5
RMS_EPS = 1e-6
ARENA_BYTES = 212800


def make_consts():
    s = np.arange(128)[:, None]
    t = np.arange(128)[None, :]
    cols = {}
    cols["ident"] = (s == t)
    cols["mS_p"] = (s < t)
    cols["mI_p"] = (s <= t)
    cols["mST_p"] = (t < s)
    same = (s // TS) == (t // TS)
    cols["mS_s"] = (s < t) & same
    cols["mI_s"] = (s <= t) & same
    cols["mST_s"] = (t < s) & same
    cols["reset_p"] = np.broadcast_to(t != 0, (128, 128))
    cols["reset_s"] = np.broadcast_to((t % TS) != 0, (128, 128))
    cols["bdones"] = (s // 64) == (t // 64)
    cols["hsel"] = (s // 64) == np.arange(2)[None, :]
    cols["cm"] = (s // TS) == np.arange(NB)[None, :]
    cols["i64s"] = (s % 64) == np.arange(64)[None, :]
    off = {}
    parts = []
    o = 0
    for k, v in cols.items():
        v = np.asarray(v, np.float32)
        off[k] = (o, o + v.shape[1])
        o += v.shape[1]
        parts.append(v)
    return np.ascontiguousarray(np.concatenate(parts, axis=1)), off


CONSTS, COFF = make_consts()
NCONST = CONSTS.shape[1]


class _Stop(Exception):
    pass


def build(NT=NTP, SAMPLE=True, DBG=False, STAGE=99):
    from contextlib import ExitStack
    nc = bass.Bass("TRN2", target_bir_lowering=False)

    def din(name, shape):
        return nc.dram_tensor(name, list(shape), F32, kind="ExternalInput").ap()

    def dout(name, shape):
        return nc.dram_tensor(name, list(shape), F32, kind="ExternalOutput").ap()

    xp = din("xp", [NTP * 128, D]); xsm = din("xs", [128, D])
    srw = din("srw", [NB, 8, 64, 64]); shg = din("shg", [NB, 4, 128, 128]); ssh = din("ssh", [NB, RWP])
    w_in = din("w_in", [D, PJ]); shift_mu = din("shift_mu", [RWP]); w0 = din("w0", [512])
    w1u = din("w1u", [64, 512]); a0 = din("a0", [512]); a1u = din("a1u", [64, 512]); g1u = din("g1u", [128, 512])
    k_k = din("k_k", [512]); k_a = din("k_a", [512]); r_k = din("r_k", [512])
    ln_x_w = din("ln_x_w", [512]); ln_x_b = din("ln_x_b", [512]); lb_logits = din("lb_logits", [2, 512])
    hg_norm_w = din("hg_norm_w", [512]); w_out = din("w_out", [D, D]); ln1_g = din("ln1_g", [D]); ln1_b = din("ln1_b", [D])
    w_up = din("w_up", [D, DFF]); w_down = din("w_down", [DFF, D]); ln2_g = din("ln2_g", [D]); ln2_b = din("ln2_b", [D])
    cst_d = din("consts", [128, NCONST])
    yp = dout("yp", [NTP * 128, D]); ys = dout("ys", [128, D])
    rwp = dout("rwp", [8, 64, 64]); rws = dout("rws", [NB, 8, 64, 64])
    hgp = dout("hgp", [4, 128, 128]); hgs = dout("hgs", [NB, 4, 128, 128])
    shp = dout("shp", [RWP]); shs = dout("shs", [NB, RWP])
    h1scr = nc.dram_tensor("h1scr", [(NTP + 1) * 128, D], F32).ap()
    if DBG:
        dbg_d = dout("dbg", [128, 4096])

    def row(v):
        return v.rearrange("(o n) -> o n", o=1)

    P = Prog()
    with ExitStack() as es:
        arena = Arena(nc, es, ARENA_BYTES)
        C = Ctx(nc, P, arena)
        sb = C.sb
        ps = [V(es.enter_context(nc.psum_tensor(f"ps{i}", [128, 512], F32))[:], f"ps{i}") for i in range(8)]

        def psv(i, a):
            return ps[i].rearrange("p (a t) -> p a t", a=a)

        def psb(i):
            return ps[i].bitcast(BF16)

        def bc3(ap2, n):
            return ap2.unsqueeze(2).to_broadcast([ap2.shape[0], ap2.shape[1], n])

        def v3(t):
            return t.rearrange("p (a t) -> p a t", a=4)

        ident_bf = sb([128, 128], BF16, "ident_bf")
        lng = sb([128, D], F32, "lng"); lnb = sb([128, D], F32, "lnb")
        C.dma(SP, lng, row(ln1_g).partition_broadcast(128))
        C.dma(SP, lnb, row(ln1_b).partition_broadcast(128))
        bnst = sb([128, 12], F32, "bnst"); mv = sb([128, 2], F32, "mv"); rstd1 = sb([128, 1], F32, "rstd1")
        fence_ln = sb([128, 2], F32, "fence_ln")
        if DBG:
            dbg = sb([128, 4096], F32, "dbg")
            C.memset(POOL, dbg, 0.0)
            dbg_pos = [0]
            dbg_map = {}

            def dump(name, v, n):
                a = dbg_pos[0]
                shape = list(v.shape)
                dst = dbg[0:shape[0], a:a + n]
                if len(shape) == 3:
                    dst = dst.rearrange("p (a b) -> p a b", a=shape[1])
                C.cp(POOL, dst, v)
                dbg_pos[0] += n
                dbg_map[name] = (a, n, shape)
            build.dbg_map = dbg_map
        else:
            def dump(name, v, n):
                return None
        phase_mark = arena.off

        def layernorm(src, dst, eps):
            for half in range(2):
                C.bn_stats(bnst[:, half * 6:(half + 1) * 6], src[:, half * 512:(half + 1) * 512])
            C.bn_aggr(mv, bnst)
            C.act(rstd1, mv[:, 1:2], AF.Ln, bias=eps)
            C.act(rstd1, rstd1, AF.Exp, scale=-0.5)
            C.ts(DVE, dst, src, mv[:, 0:1], ALU.subtract, rstd1[:, 0:1], ALU.mult)
            halves = []
            for eng_, hsl in ((DVE, slice(0, 640)), (POOL, slice(640, 1024))):
                dk = dst[:, hsl].k(hsl.start)
                halves.append(dk)
                o, a_, g_, b_ = _ap(dk), _ap(dst[:, hsl]), _ap(lng[:, hsl]), _ap(lnb[:, hsl])
                C.rec(eng_, lambda e, o=o, a_=a_, g_=g_: e.tensor_tensor(out=o, in0=a_, in1=g_, op=ALU.mult), [dk], [dst, lng])
                C.rec(eng_, lambda e, o=o, b_=b_: e.tensor_tensor(out=o, in0=o, in1=b_, op=ALU.add), [dk], [dk, lnb])
            C.rec(POOL, lambda e: e.memset(_ap(fence_ln), 0.0), [dst, fence_ln], halves)

        F32_CONSTS = ("ident", "reset_p", "reset_s", "hsel", "cm", "i64s")
        BF_CONSTS = ("mS_p", "mI_p", "mST_p", "mS_s", "mI_s", "mST_s", "bdones")
        cviews = {}
        nf = sum(COFF[k][1] - COFF[k][0] for k in F32_CONSTS)
        nb_ = sum(COFF[k][1] - COFF[k][0] for k in BF_CONSTS)
        cstf = sb([128, nf], F32, "cstf"); cstb = sb([128, nb_], BF16, "cstb")
        o_ = 0
        for k_ in F32_CONSTS:
            a, b = COFF[k_]
            C.dma(SP, cstf[:, o_:o_ + b - a].k(k_), cst_d[:, a:b])
            cviews[k_] = cstf[:, o_:o_ + b - a].k(k_)
            o_ += b - a
        o_ = 0
        for k_ in BF_CONSTS:
            a, b = COFF[k_]
            C.dma(POOL, cstb[:, o_:o_ + b - a].k(k_), cst_d[:, a:b])
            cviews[k_] = cstb[:, o_:o_ + b - a].k(k_)
            o_ += b - a

        def cc(name):
            return cviews[name]

        C.cp(POOL, ident_bf, cc("ident"))
        bdones_bf = cc("bdones")
        mu14 = sb([128, 14], F32, "mu14")
        kk4 = sb([128, 4], F32, "kk4"); ka4 = sb([128, 4], F32, "ka4"); rk4 = sb([128, 4], F32, "rk4")
        w04 = sb([128, 4], F32, "w04"); a04 = sb([128, 4], F32, "a04")
        lbl = sb([128, 2, 4], F32, "lbl")
        with nc.allow_non_contiguous_dma(reason="tiny per-channel parameter vectors"):
            C.dma(SP, mu14, shift_mu.rearrange("(c p) -> p c", p=128))
            C.dma(SP, kk4, k_k.rearrange("(c p) -> p c", p=128))
            C.dma(SP, ka4, k_a.rearrange("(c p) -> p c", p=128))
            C.dma(SP, rk4, r_k.rearrange("(c p) -> p c", p=128))
            C.dma(SP, w04, w0.rearrange("(c p) -> p c", p=128))
            C.dma(SP, a04, a0.rearrange("(c p) -> p c", p=128))
            C.dma(SP, lbl, lb_logits.rearrange("l (c p) -> p l c", p=128))
        WA = sb([128, 512], BF16, "WA")
        C.dma(POOL, WA[0:64, :].k("lo"), w1u)
        C.dma(POOL, WA[64:128, :].k("hi"), a1u)
        g1u_bf = sb([128, 512], BF16, "g1u_bf")
        C.dma(POOL, g1u_bf, g1u)
        lnxw = sb([128, 512], BF16, "lnxw"); lnxb = sb([128, 512], BF16, "lnxb"); hgw = sb([128, 512], BF16, "hgw")
        C.dma(POOL, lnxw, row(ln_x_w).partition_broadcast(128))
        C.dma(POOL, lnxb, row(ln_x_b).partition_broadcast(128))
        C.dma(POOL, hgw, row(hg_norm_w).partition_broadcast(128))
        lb4 = sb([128, 4], F32, "lb4"); oml4 = sb([128, 4], F32, "oml4"); etmp = sb([128, 4], F32, "etmp")
        C.tt(DVE, etmp, lbl[:, 1, :], lbl[:, 0, :], ALU.subtract)
        C.act(etmp, etmp, AF.Exp)
        C.ts(DVE, lb4, etmp, 1.0, ALU.add)
        C.recip(lb4, lb4)
        C.tt(DVE, oml4, etmp, lb4, ALU.mult)
        RKsel = sb([128, 4, 2], BF16, "RKsel")
        C.tt(DVE, RKsel, bc3(rk4, 2), cc("hsel").unsqueeze(1).to_broadcast([128, 4, 2]), ALU.mult)

        ov_lo = arena.off
        w_in_bf = sb([128, 8, PJ], BF16, "w_in_bf")
        ov_hi = arena.off
        for kc in range(8):
            C.dma(POOL, w_in_bf[:, kc, :].k(kc), w_in[kc * 128:(kc + 1) * 128, :])
        w_out_bf = sb([128, 8, D], BF16, "w_out_bf")
        for kc in range(8):
            C.dma(POOL, w_out_bf[:, kc, :].k(kc), w_out[kc * 128:(kc + 1) * 128, :])

        ST = sb([128, 4, 64], F32, "ST"); ST_bf = sb([128, 4, 64], BF16, "ST_bf")
        SH = sb([128, 4, 128], F32, "SH"); SH_bf = sb([128, 4, 128], BF16, "SH_bf")
        plast = sb([128, 14], F32, "plast")
        for t_ in (ST, ST_bf, SH, SH_bf, plast):
            C.memset(POOL, t_, 0.0)

        def prompt_state_outputs():
            with nc.allow_non_contiguous_dma(reason="tiny state vector"):
                C.dma(SP, shp.rearrange("(c p) -> p c", p=128), plast, out_final=True)
            C.dma(SP, hgp.rearrange("h k v -> k h v"), SH, out_final=True)
            identf = cc("ident")
            for pr in range(4):
                C.tr(ps[2][0:64, pr * 128:(pr + 1) * 128], ST[:, pr, :], identf)
            rwo = T[0]
            C.cp(DVE, rwo[0:64, :], ps[2][0:64, :])
            C.dma(SP, rwp.rearrange("h v j -> v h j"), rwo[0:64, :].rearrange("p (h j) -> p h j", h=8), out_final=True)
            if DBG:
                C.dma(SP, dbg_d, dbg, out_final=True)


        x_t = [sb([128, D], F32, f"x_t{i}") for i in range(2)]
        x_bf = sb([128, D], BF16, "x_bf")
        xT = sb([128, 8, 128], BF16, "xT")
        pr_ = sb([128, 14, 129], F32, "pr")
        xs = sb([128, 14, 128], F32, "xsft")
        T = [sb([128, 512], F32, f"T{i}") for i in range(10)]
        z12 = sb([128, 128], BF16, "z12"); sg_bf = sb([128, 128], BF16, "sg_bf"); tqb = sb([128, 512], BF16, "tqb")
        bhT = sb([128, 4, 128], BF16, "bhT"); khT = sb([128, 4, 128], BF16, "khT"); vT = sb([128, 4, 128], BF16, "vT")
        khTb = sb([128, 4, 128], BF16, "khTb")
        fence2 = sb([128, 2], F32, "fence2")
        HB = []
        for i in range(2):
            HB.append(dict(
                AR=sb([128, 4, 2, 128], BF16, f"AR{i}"), bT=sb([128, 4, 128], BF16, f"bT{i}"), kT=sb([128, 4, 128], BF16, f"kT{i}"),
                A_tm=sb([128, 512], BF16, f"A_tm{i}"), Bh_tm=sb([128, 512], BF16, f"Bh_tm{i}"),
                Kh_tm=sb([128, 512], BF16, f"Kh_tm{i}"), V_tm=sb([128, 512], BF16, f"V_tm{i}"),
                qTb=sb([128, 4, 128], BF16, f"qTb{i}"), kTb=sb([128, 4, 128], BF16, f"kTb{i}"),
                khat_tm=sb([128, 512], BF16, f"khat_tm{i}"), i_tm=sb([128, 512], BF16, f"i_tm{i}"),
                gs_t=sb([128, 512], BF16, f"gs_t{i}"), g_tm=sb([128, 512], BF16, f"g_tm{i}"),
                bon8=sb([128, 8], F32, f"bon8{i}"), gC=sb([128, 4, NB], F32, f"gC{i}"), decH=sb([128, 4, NB], F32, f"decH{i}")))
        Pm = sb([128, 8, 128], BF16, "Pm"); Tm = sb([128, 8, 128], BF16, "Tm"); RR = sb([128, 8, 128], BF16, "RR")
        NrbT = sb([128, 8, 128], BF16, "NrbT"); AkT = sb([128, 8, 128], BF16, "AkT"); NrkT = sb([128, 8, 128], BF16, "NrkT")
        TTf = sb([128, 8, 128], BF16, "TTf")
        W1T = sb([128, 4, 128], BF16, "W1T")
        Z_tm = sb([128, 512], BF16, "Z_tm"); U_tm = sb([128, 512], BF16, "U_tm")
        st16 = sb([128, 16], F32, "st16"); m8 = sb([128, 8], F32, "m8"); r8 = sb([128, 8], F32, "r8")
        o_all = sb([128, D], BF16, "o_all"); oT = sb([128, 8, 128], BF16, "oT")
        h1pre = sb([128, D], F32, "h1pre")
        attT = sb([128, 4, 128], BF16, "attT")
        s4 = sb([128, 4], F32, "s4"); rr4 = sb([128, 4], F32, "rr4")
        BT0 = h1pre[:, 0:512]; BT1 = h1pre[:, 512:1024]
        SHtmp = v3(BT1)
        STtmp = BT1[:, 0:256].rearrange("p (a v) -> p a v", a=4)
        identb4 = ident_bf.unsqueeze(1).to_broadcast([128, 4, 128])
        save_off = arena.off
        arena.off = ov_lo
        scrA = [sb([128, 2048], F32, f"scrA{i}") for i in range(2)]
        S0T32 = sb([128, NB, 4, 64], F32, "S0T32")
        S0Tb = sb([128, NB, 4, 64], BF16, "S0Tb")
        sshT = sb([128, 14, NB], F32, "sshT"); lastp = sb([128, 14, NB], F32, "lastp")
        EW = [sb([128, 4, 128], BF16, f"EW{i}") for i in range(2)]
        ER = [sb([128, 4, 128], BF16, f"ER{i}") for i in range(2)]
        EQ = [sb([128, 4, 128], BF16, f"EQ{i}") for i in range(2)]
        Ub = sb([128, 512], BF16, "Ub"); Vb = sb([128, 512], BF16, "Vb"); khb = sb([128, 512], BF16, "khb")
        Dg = sb([128, 4, 64], F32, "Dg")
        S0h = [sb([128, 4, 128], F32, f"S0h{i}") for i in range(2)]
        S0hb = sb([128, 4, 128], BF16, "S0hb")
        Sn = sb([128, 512], F32, "Sn")
        fence_t = sb([128, 2], F32, "fence_t")
        assert arena.off <= ov_hi, (arena.off, ov_hi)
        ov_bufs = scrA + [S0T32, S0Tb, sshT, lastp] + EW + ER + EQ + [Ub, Vb, khb, Dg] + S0h + [S0hb, Sn, fence_t]
        arena.off = save_off
        ssh_tm = scrA[0][0:NB, 0:RWP]
        hh_order = (0, 2, 4, 6, 1, 3, 5, 7)
        heads_of = [(0, 1, 2, 3), (4, 5, 6, 7)]
        PA, PB_ = 6, 7

        cb = cc

        def cfg(sample):
            sfx = "_s" if sample else "_p"
            return dict(nch=NB if sample else 1, Cn=TS if sample else 128, L=3 if sample else 7,
                        mS=cb("mS" + sfx), mI=cb("mI" + sfx), mST=cb("mST" + sfx), reset=cc("reset" + sfx))

        def stageA(ti, x_src, sample):
            g_ = cfg(sample)
            nch, Cn, reset = g_["nch"], g_["Cn"], g_["reset"]
            H = HB[ti % 2]
            AR, bT, kT = H["AR"], H["bT"], H["kT"]
            xt = x_t[ti % 2]
            C.dma(SP, xt, x_src)
            C.dma(POOL, x_bf, x_src)
            for kc in range(8):
                C.tr(psb(PB_)[:, kc * 128:(kc + 1) * 128], x_bf[:, kc * 128:(kc + 1) * 128], ident_bf)
            C.cp(ACT, xT, psb(PB_).rearrange("p (a t) -> p a t", a=8))
            yield
            sig, kq, fgl, bcs, eb, enb, ebl, sq_, eg = T[1], T[4], T[0], T[2], T[3], T[5], T[6], T[7], T[8]

            def proj_fm(c0, n, bank):
                for j in range(n):
                    c = c0 + j
                    for kc in range(8):
                        C.mm(ps[bank][:, j * 128:(j + 1) * 128], w_in_bf[:, kc, c * 128:(c + 1) * 128].k(kc),
                             xT[:, kc, :], start=(kc == 0), stop=(kc == 7))

            def proj_tm(col0, bank):
                for kc in range(8):
                    C.mm(ps[bank], xT[:, kc, :], w_in_bf[:, kc, col0:col0 + 512].k(kc), start=(kc == 0), stop=(kc == 7))

            proj_fm(0, 4, PA); C.cp(ACT, pr_[:, 0:4, 1:129], psv(PA, 4)); yield
            proj_fm(4, 4, PB_); C.cp(ACT, pr_[:, 4:8, 1:129], psv(PB_, 4)); yield
            proj_fm(8, 4, PA); C.cp(ACT, pr_[:, 8:12, 1:129], psv(PA, 4)); yield
            proj_fm(12, 2, PB_); C.cp(ACT, pr_[:, 12:14, 1:129], psv(PB_, 4)[:, 0:2, :]); yield
            prev, cur = pr_[:, :, 0:128], pr_[:, :, 1:129]
            if not sample:
                C.cp(POOL, pr_[:, :, 0:1], plast.unsqueeze(2))
                C.cp(POOL, plast.unsqueeze(2), pr_[:, :, 128:129])
            else:
                C.memset(POOL, pr_[:, :, 0:1], 0.0)
            proj_fm(14, 4, PA)
            C.act(sq_, ps[PA], AF.Sigmoid)
            C.tt(DVE, sq_, ps[PA], sq_, ALU.mult)
            yield
            proj_fm(18, 4, PB_)
            C.act(sig, ps[PB_], AF.Sigmoid)
            yield
            proj_tm(RWP + 1024, PA)
            C.cp(ACT, H["i_tm"], ps[PA])
            yield
            proj_tm(RWP + 1536, PB_)
            C.act(eg, ps[PB_], AF.Sigmoid)
            C.tt(DVE, H["gs_t"], ps[PB_], eg, ALU.mult)
            yield
            if sample:
                C.rec(POOL, lambda e: e.memset(_ap(fence_t), 0.0),
                      [w_in_bf[:, kc, :].k(kc) for kc in range(8)] + ov_bufs
                      + [S0T32[:, b, :, :].k(b) for b in range(NB)] + [S0Tb[:, b, :, :].k(b) for b in range(NB)], [])
                for t_ in EW + ER + EQ:
                    C.memset(POOL, t_, 0.0)
                C.dma(SP, ssh_tm, ssh)
                for c in range(14):
                    C.tr(ps[PA][:, c * NB:(c + 1) * NB], ssh_tm[:, c * 128:(c + 1) * 128], cc("ident")[0:NB, 0:NB])
                C.cp(DVE, sshT, ps[PA][:, 0:14 * NB].rearrange("p (c b) -> p c b", c=14))
            for eng_, c0, c1 in ((DVE, 0, 8), (POOL, 8, 14)):
                C.tt(eng_, xs[:, c0:c1, :].k(c0), prev[:, c0:c1, :], cur[:, c0:c1, :], ALU.subtract)
                C.tt(eng_, xs[:, c0:c1, :].k(c0), xs[:, c0:c1, :].k(c0), bc3(mu14[:, c0:c1], 128), ALU.mult)
                C.tt(eng_, xs[:, c0:c1, :].k(c0), xs[:, c0:c1, :].k(c0), cur[:, c0:c1, :], ALU.add)
            C.rec(POOL, lambda e: e.memset(_ap(fence2), 0.0), [xs, fence2], [xs[:, 0:8, :].k(0), xs[:, 8:14, :].k(8)])
            if sample:
                cur4 = cur.rearrange("p c (b t) -> p c b t", t=TS)
                xs4 = xs.rearrange("p c (b t) -> p c b t", t=TS)
                cur0, xs0 = cur4[:, :, :, 0], xs4[:, :, :, 0]
                C.tt(POOL, xs0, sshT, cur0, ALU.subtract)
                C.tt(POOL, xs0, xs0, bc3(mu14, NB), ALU.mult)
                C.tt(POOL, xs0, xs0, cur0, ALU.add)
                C.cp(POOL, lastp, cur4[:, :, :, TS - 1])
                for g0 in range(0, 14, 4):
                    bk = PA if (g0 // 4) % 2 == 0 else PB_
                    n = min(4, 14 - g0)
                    for j in range(n):
                        C.tr(ps[bk][0:NB, j * 128:(j + 1) * 128], lastp[:, g0 + j, :], cc("ident"))
                    C.cp(ACT, ssh_tm[:, g0 * 128:(g0 + n) * 128], ps[bk][0:NB, 0:n * 128])
                C.dma(SP, shs, ssh_tm, out_final=True)
            yield
            r_ = xs[:, 0:4, :]; k_ = xs[:, 4:8, :]; v_ = xs[:, 8:12, :]
            C.act(z12[0:64, :], xs[0:64, 12, :], AF.Tanh)
            C.act(sg_bf, xs[:, 13, :], AF.Sigmoid)
            C.cp(DVE, z12[64:128, :], xs[64:128, 12, :])
            for pr in range(4):
                sl = slice(pr * 128, (pr + 1) * 128)
                C.mm(ps[PA][:, sl], WA[0:64, sl].k("lo"), z12[0:64, :])
            for pr in range(4):
                sl = slice(pr * 128, (pr + 1) * 128)
                C.mm(ps[PB_][:, sl], WA[64:128, sl].k("hi"), z12[64:128, :])
            sw, alr = T[9], T[8]
            for pr in range(4):
                sl = slice(pr * 128, (pr + 1) * 128)
                C.act(sw[:, sl], ps[PA][:, sl], AF.Sigmoid, bias=w04[:, pr:pr + 1])
            C.tt(DVE, v3(kq), v3(sig), bc3(oml4, 128), ALU.mult)
            C.tt(DVE, v3(fgl), v3(kq), bc3(lb4, 128), ALU.add)
            C.tt(POOL, v3(kq), bc3(oml4, 128), v3(kq), ALU.subtract)
            yield
            for pr in range(4):
                sl = slice(pr * 128, (pr + 1) * 128)
                C.act(alr[:, sl], ps[PB_][:, sl], AF.Sigmoid, bias=a04[:, pr:pr + 1])
            C.mm(ps[PA], sg_bf, g1u_bf)
            C.cp(ACT, H["g_tm"], ps[PA])
            yield
            C.act(fgl, fgl, AF.Ln)
            for h in range(4):
                C.scan(bcs[:, h * 128:(h + 1) * 128], reset, fgl[:, h * 128:(h + 1) * 128])
            C.act(eb, bcs, AF.Exp)
            C.act(enb, bcs, AF.Exp, scale=-1.0)
            bc4 = bcs.rearrange("p (a n c) -> p a n c", a=4, n=nch)
            C.tt(POOL, ebl.rearrange("p (a n c) -> p a n c", a=4, n=nch),
                 bc4[:, :, :, Cn - 1:Cn].to_broadcast([128, 4, nch, Cn]), bc4, ALU.subtract)
            C.act(ebl, ebl, AF.Exp)
            yield
            C.cp(POOL, H["decH"][:, :, 0:nch].unsqueeze(3),
                 eb.rearrange("p (a n c) -> p a n c", a=4, n=nch)[:, :, :, Cn - 1:Cn])
            C.tt(DVE, H["qTb"], v3(sq_), v3(eb), ALU.mult)
            C.tt(POOL, H["kTb"], v3(kq), v3(enb), ALU.mult)
            C.tt(DVE, khTb, v3(kq), v3(ebl), ALU.mult)
            yield
            cumS, gex, gin, ginv, glast, kkk, tq, k2 = T[2], T[3], T[4], T[5], T[6], T[7], T[0], T[1]
            for pr in range(4):
                C.scan(cumS[:, pr * 128:(pr + 1) * 128], reset, sw[:, pr * 128:(pr + 1) * 128])
            C.tt(POOL, gex, cumS, sw, ALU.subtract)
            C.act(gex, gex, AF.Exp, scale=CDEC)
            C.act(gin, cumS, AF.Exp, scale=CDEC)
            C.act(ginv, cumS, AF.Exp, scale=-CDEC)
            cs4 = cumS.rearrange("p (a n c) -> p a n c", a=4, n=nch)
            C.tt(POOL, glast.rearrange("p (a n c) -> p a n c", a=4, n=nch),
                 cs4[:, :, :, Cn - 1:Cn].to_broadcast([128, 4, nch, Cn]), cs4, ALU.subtract)
            C.act(glast, glast, AF.Exp, scale=CDEC)
            C.cp(POOL, H["gC"][:, :, 0:nch].unsqueeze(3),
                 gin.rearrange("p (a n c) -> p a n c", a=4, n=nch)[:, :, :, Cn - 1:Cn])
            yield
            C.tt(POOL, v3(kkk), k_, bc3(kk4, 128), ALU.mult)
            C.act(tqb, kkk, AF.Square)
            for pr in range(4):
                sl = slice(pr * 128, (pr + 1) * 128)
                C.mm(ps[PB_][:, sl], bdones_bf, tqb[:, sl])
            C.act(tq, ps[PB_], AF.Ln, bias=1e-24)
            C.act(tq, tq, AF.Exp, scale=-0.5)
            C.tt(DVE, kkk, kkk, tq, ALU.mult)
            C.stt(DVE, v3(k2), v3(alr), -1.0, bc3(ka4, 128), ALU.add, ALU.mult)
            C.stt(DVE, v3(k2), v3(k2), 1.0, k_, ALU.add, ALU.mult)
            C.tt(DVE, alr, kkk, alr, ALU.mult)
            b_ = alr
            yield
            C.tt(DVE, AR[:, :, 1, :], r_, v3(gin), ALU.mult)
            C.stt(DVE, AR[:, :, 0, :], v3(kkk), -1.0, v3(gex), ALU.mult, ALU.mult)
            C.tt(POOL, bT, v3(b_), v3(ginv), ALU.mult)
            C.tt(POOL, kT, v3(k2), v3(ginv), ALU.mult)
            C.tt(DVE, bhT, v3(b_), v3(glast), ALU.mult)
            C.tt(POOL, khT, v3(k2), v3(glast), ALU.mult)
            C.cp(POOL, vT, v_)
            C.tt(DVE, v3(tqb), r_, v3(k2), ALU.mult)
            for pr in range(4):
                C.mm(ps[PA][:, pr * 2:(pr + 1) * 2], tqb[:, pr * 128:(pr + 1) * 128], RKsel[:, pr, :])
            C.cp(DVE, H["bon8"], ps[PA][:, 0:8])
            yield
            for (src, bank, half) in ((lambda pr: AR[:, pr, 0, :], PA, 0), (lambda pr: bhT[:, pr, :], PA, 1),
                                      (lambda pr: khT[:, pr, :], PB_, 0), (lambda pr: vT[:, pr, :], PB_, 1)):
                for pr in range(4):
                    o0 = half * 512 + pr * 128
                    C.tr(psb(bank)[:, o0:o0 + 128], src(pr), ident_bf)
            C.cp(ACT, H["A_tm"], psb(PA)[:, 0:512]); C.cp(ACT, H["Bh_tm"], psb(PA)[:, 512:1024])
            C.cp(ACT, H["Kh_tm"], psb(PB_)[:, 0:512]); C.cp(ACT, H["V_tm"], psb(PB_)[:, 512:1024])
            yield
            for h in range(4):
                C.tr(psb(PA)[:, h * 128:(h + 1) * 128], khTb[:, h, :], ident_bf)
            C.cp(ACT, H["khat_tm"], psb(PA)[:, 0:512])
            yield

        def stageB(ti, h1_dst, sample):
            g_ = cfg(sample)
            nch, Cn, L = g_["nch"], g_["Cn"], g_["L"]
            mSb = g_["mS"].unsqueeze(1).to_broadcast([128, 4, 128])
            mIb = g_["mI"].unsqueeze(1).to_broadcast([128, 4, 128])
            mSTb = g_["mST"].unsqueeze(1).to_broadcast([128, 4, 128])
            H = HB[ti % 2]
            AR, bT, kT, A_tm, Bh_tm, Kh_tm, V_tm = H["AR"], H["bT"], H["kT"], H["A_tm"], H["Bh_tm"], H["Kh_tm"], H["V_tm"]
            qTb, kTb, khat_tm, i_tm, gs_t, g_tm, bon8, gC, decH = (H["qTb"], H["kTb"], H["khat_tm"], H["i_tm"], H["gs_t"],
                                                                    H["g_tm"], H["bon8"], H["gC"], H["decH"])
            xt = x_t[ti % 2]
            for h in range(4):
                C.mm(ps[3][:, h * 128:(h + 1) * 128], kTb[:, h, :], qTb[:, h, :])
            C.tt(DVE, attT, psv(3, 4), mIb, ALU.mult)
            if not sample:
                for h in range(4):
                    hsl = slice(h * 128, (h + 1) * 128)
                    C.mm(ps[4][:, hsl], attT[:, h, :], i_tm[:, hsl], start=True, stop=False)
                    C.mm(ps[4][:, hsl], qTb[:, h, :], SH_bf[:, h, :], start=False, stop=True)
                for h in range(4):
                    hsl = slice(h * 128, (h + 1) * 128)
                    C.mm(ps[5][:, hsl], khat_tm[:, hsl], i_tm[:, hsl])
                C.tt(POOL, SHtmp, SH, bc3(decH[:, :, 0], 128), ALU.mult)
                C.tt(DVE, SH, SHtmp, psv(5, 4), ALU.add)
                C.cp(POOL, SH_bf, SH)
            else:
                for h in range(4):
                    hsl = slice(h * 128, (h + 1) * 128)
                    C.mm(ps[4][:, hsl], attT[:, h, :], i_tm[:, hsl], start=(h == 0), stop=False)
                for b in range(NB):
                    s0 = S0h[b % 2]
                    csl = slice(b * TS, (b + 1) * TS)
                    C.dma(SP, s0, shg[b].rearrange("h k v -> k h v"))
                    C.cp(ACT, S0hb, s0)
                    eq = EQ[b % 2]
                    C.cp(POOL, eq[:, :, csl], qTb[:, :, csl])
                    for h in range(4):
                        C.mm(ps[4][:, h * 128:(h + 1) * 128], eq[:, h, :], S0hb[:, h, :], start=False, stop=False)
                    C.memset(POOL, eq[:, :, csl], 0.0)
                    C.ts(DVE, khb, khat_tm, cc("cm")[:, b:b + 1], ALU.mult)
                    bank = (5, 3)[b % 2]
                    for h in range(4):
                        hsl = slice(h * 128, (h + 1) * 128)
                        C.mm(ps[bank][:, hsl], khb[:, hsl], i_tm[:, hsl])
                    C.tt(DVE, s0, s0, bc3(decH[:, :, b], 128), ALU.mult)
                    C.tt(DVE, s0, s0, psv(bank, 4), ALU.add)
                    C.dma(SP, hgs[b].rearrange("h k v -> k h v"), s0, out_final=True)
                    yield
            osq = BT0
            C.act(osq, ps[4], AF.Square)
            C.red(DVE, s4, v3(osq))
            C.act(rr4, s4, AF.Ln, scale=1.0 / 128, bias=RMS_EPS)
            C.act(rr4, rr4, AF.Exp, scale=-0.5)
            C.tt(DVE, v3(osq), psv(4, 4), bc3(rr4, 128), ALU.mult)
            C.tt(POOL, osq, osq, hgw, ALU.mult)
            C.tt(POOL, o_all[:, 512:1024], osq, gs_t, ALU.mult)
            yield
            if sample:
                for g in range(4):
                    sa = scrA[g % 2]
                    nat = sa[0:64, :].rearrange("p (b n) -> p b n", b=4)
                    C.dma(SP, nat.rearrange("p b (h j) -> p b h j", h=8),
                          srw[g * 4:(g + 1) * 4].rearrange("b h v j -> v b h j"))
                    for bb in range(4):
                        b = g * 4 + bb
                        bank = b % 2
                        for pr in range(4):
                            C.tr(ps[bank][:, (bb % 2) * 256 + pr * 64:(bb % 2) * 256 + (pr + 1) * 64],
                                 nat[:, bb, pr * 128:(pr + 1) * 128], cc("ident")[0:64, 0:64])
                        C.cp(ACT, S0T32[:, b, :, :].k(b),
                             ps[bank][:, (bb % 2) * 256:(bb % 2) * 256 + 256].rearrange("p (a v) -> p a v", a=4))
                        C.cp(DVE, S0Tb[:, b, :, :].k(b), S0T32[:, b, :, :].k(b))
                    yield
            for hg in range(2):
                for i, h in enumerate(heads_of[hg]):
                    pr, hh = h // 2, h % 2
                    rows = slice(64 * hh, 64 * hh + 64)
                    sl = slice(i * 128, (i + 1) * 128)
                    C.mm(ps[0][:, sl], bT[rows, pr, :], AR[rows, pr, 0, :])
                    C.mm(ps[1][:, sl], bT[rows, pr, :], AR[rows, pr, 1, :])
                    C.mm(ps[2][:, sl], kT[rows, pr, :], AR[rows, pr, 0, :])
                    C.mm(ps[3][:, sl], kT[rows, pr, :], AR[rows, pr, 1, :])
                    C.mm(ps[4][:, sl], AR[rows, pr, 0, :], bT[rows, pr, :])
                hs = slice(4 * hg, 4 * hg + 4)
                C.tt(DVE, Pm[:, hs, :], psv(0, 4), mSb, ALU.mult)
                C.tt(DVE, NrbT[:, hs, :], psv(1, 4), mIb, ALU.mult)
                C.tt(DVE, AkT[:, hs, :], psv(2, 4), mSb, ALU.mult)
                C.tt(DVE, NrkT[:, hs, :], psv(3, 4), mIb, ALU.mult)
                C.tt(DVE, RR[:, hs, :], psv(4, 4), mSTb, ALU.mult)
                C.tt(POOL, Tm[:, hs, :], Pm[:, hs, :], identb4, ALU.add)
                yield
            for k in range(L):
                for hg in range(2):
                    b0 = 3 * hg
                    hs = slice(4 * hg, 4 * hg + 4)
                    for i, h in enumerate(heads_of[hg]):
                        sl = slice(i * 128, (i + 1) * 128)
                        if k < L - 1:
                            C.mm(ps[b0][:, sl], RR[:, h, :], Pm[:, h, :])
                        if k >= 1:
                            C.mm(ps[b0 + 1][:, sl], RR[:, h, :], Tm[:, h, :])
                        if k < L - 1:
                            C.mm(ps[b0 + 2][:, sl], Pm[:, h, :], RR[:, h, :])
                    if k < L - 1:
                        C.cp(ACT, Pm[:, hs, :], psv(b0, 4))
                    if k >= 1:
                        dst = TTf[:, hs, :] if k == L - 1 else Tm[:, hs, :]
                        C.tt(DVE, dst, psv(b0 + 1, 4), Tm[:, hs, :], ALU.add)
                    if k < L - 1:
                        C.cp(ACT, RR[:, hs, :], psv(b0 + 2, 4))
                    yield
            for h in range(8):
                C.mm(ps[2][:, h * 64:(h + 1) * 64], AkT[:, h, :], V_tm[:, h * 64:(h + 1) * 64])
            C.cp(ACT, Z_tm, ps[2])
            for h in range(8):
                pr = h // 2
                C.mm(ps[h // 4][:, (h % 4) * 128:(h % 4 + 1) * 128], A_tm[:, pr * 128:(pr + 1) * 128], TTf[:, h, :])
            for b in range(2):
                pv = ps[b].rearrange("p (q e t) -> p q e t", q=2, e=2)
                C.cp(ACT, W1T[0:64, 2 * b:2 * b + 2, :], pv[0:64, :, 0, :])
                C.cp(ACT, W1T[64:128, 2 * b:2 * b + 2, :], pv[64:128, :, 1, :])
            yield
            for h in range(8):
                pr, hh = h // 2, h % 2
                rows = slice(64 * hh, 64 * hh + 64)
                hsl = slice(h * 64, (h + 1) * 64)
                if not sample:
                    C.mm(ps[3][:, hsl], TTf[:, h, :], Z_tm[:, hsl], start=True, stop=False)
                    C.mm(ps[3][:, hsl], W1T[rows, pr, :], ST_bf[rows, pr, :], start=False, stop=True)
                else:
                    C.mm(ps[3][:, hsl], TTf[:, h, :], Z_tm[:, hsl], start=(h == 0), stop=False)
            if sample:
                for b in range(NB):
                    ew = EW[b % 2]
                    csl = slice(b * TS, (b + 1) * TS)
                    C.cp(POOL, ew[:, :, csl], W1T[:, :, csl])
                    for h in hh_order:
                        pr, hh = h // 2, h % 2
                        rows = slice(64 * hh, 64 * hh + 64)
                        C.mm(ps[3][:, h * 64:(h + 1) * 64], ew[rows, pr, :], S0Tb[rows, b, pr, :].k(b), start=False, stop=False)
                    C.memset(POOL, ew[:, :, csl], 0.0)
                    yield
            C.cp(ACT, U_tm, ps[3])
            for h in range(8):
                pr, hh = h // 2, h % 2
                rows = slice(64 * hh, 64 * hh + 64)
                hsl = slice(h * 64, (h + 1) * 64)
                C.mm(ps[4][:, hsl], NrbT[:, h, :], U_tm[:, hsl], start=(h == 0 or not sample), stop=False)
                C.mm(ps[4][:, hsl], NrkT[:, h, :], V_tm[:, hsl], start=False, stop=False)
                if not sample:
                    C.mm(ps[4][:, hsl], AR[rows, pr, 1, :], ST_bf[rows, pr, :], start=False, stop=True)
            yield
            if sample:
                for b in range(NB):
                    er = ER[b % 2]
                    csl = slice(b * TS, (b + 1) * TS)
                    C.cp(POOL, er[:, :, csl], AR[:, :, 1, csl])
                    for h in hh_order:
                        pr, hh = h // 2, h % 2
                        rows = slice(64 * hh, 64 * hh + 64)
                        C.mm(ps[4][:, h * 64:(h + 1) * 64], er[rows, pr, :], S0Tb[rows, b, pr, :].k(b), start=False, stop=False)
                    C.memset(POOL, er[:, :, csl], 0.0)
                    yield
                i64b = cc("i64s").unsqueeze(1).to_broadcast([128, 4, 64])
                for b in range(NB):
                    bank = b % 2
                    C.ts(DVE, Ub, U_tm, cc("cm")[:, b:b + 1], ALU.mult)
                    C.ts(DVE, Vb, V_tm, cc("cm")[:, b:b + 1], ALU.mult)
                    C.tt(DVE, Dg, i64b, bc3(gC[:, :, b], 64), ALU.mult)
                    for h in range(8):
                        hsl = slice(h * 64, (h + 1) * 64)
                        C.mm(ps[bank][0:64, hsl], Ub[:, hsl], Bh_tm[:, hsl], start=(h == 0), stop=False)
                        C.mm(ps[bank][0:64, hsl], Vb[:, hsl], Kh_tm[:, hsl], start=False, stop=False)
                    for h in hh_order:
                        pr, hh = h // 2, h % 2
                        rows = slice(64 * hh, 64 * hh + 64)
                        C.mm(ps[bank][0:64, h * 64:(h + 1) * 64], S0T32[rows, b, pr, :].k(b), Dg[rows, pr, :], start=False, stop=False)
                    C.cp(ACT, Sn[0:64, :], ps[bank][0:64, :])
                    C.dma(SP, rws[b].rearrange("h v j -> v h j"), Sn[0:64, :].rearrange("p (h j) -> p h j", h=8), out_final=True)
                    yield
            if not sample:
                for pr in range(4):
                    psl = slice(pr * 128, (pr + 1) * 128)
                    C.mm(ps[5][:, psl], Bh_tm[:, psl], U_tm[:, psl], start=True, stop=False)
                    C.mm(ps[5][:, psl], Kh_tm[:, psl], V_tm[:, psl], start=False, stop=True)
                C.tt(POOL, STtmp, ST, bc3(gC[:, :, 0], 64), ALU.mult)
                p5 = psv(5, 4)
                C.tt(DVE, ST[0:64, :, :], STtmp[0:64, :, :], p5[0:64, :, 0:64], ALU.add)
                C.tt(DVE, ST[64:128, :, :], STtmp[64:128, :, :], p5[64:128, :, 64:128], ALU.add)
                C.cp(POOL, ST_bf, ST)
            yield
            ysq, tmp2 = BT0, BT1
            y3 = ps[4].rearrange("p (h v) -> p h v", h=8)
            yv = ysq.rearrange("p (h v) -> p h v", h=8)
            C.red(DVE, st16[:, 0:8], y3)
            C.act(ysq, ps[4], AF.Square)
            C.red(DVE, st16[:, 8:16], yv)
            C.ts(DVE, m8, st16[:, 0:8], 1.0 / 64, ALU.mult)
            C.tt(DVE, r8, m8, m8, ALU.mult)
            C.stt(DVE, r8, st16[:, 8:16], 1.0 / 64, r8, ALU.mult, ALU.subtract)
            C.act(r8, r8, AF.Ln, bias=GN_EPS)
            C.act(r8, r8, AF.Exp, scale=-0.5)
            C.tt(DVE, yv, y3, bc3(m8, 64), ALU.subtract)
            C.tt(POOL, yv, yv, bc3(r8, 64), ALU.mult)
            C.tt(POOL, ysq, ysq, lnxw, ALU.mult)
            C.tt(POOL, ysq, ysq, lnxb, ALU.add)
            C.tt(DVE, tmp2.rearrange("p (h v) -> p h v", h=8), V_tm.rearrange("p (h v) -> p h v", h=8),
                 bc3(bon8, 64), ALU.mult)
            C.tt(DVE, ysq, ysq, tmp2, ALU.add)
            C.tt(DVE, o_all[:, 0:512], ysq, g_tm, ALU.mult)
            yield
            for mc in range(8):
                C.tr(psb(2)[:, mc * 128:(mc + 1) * 128], o_all[:, mc * 128:(mc + 1) * 128], ident_bf)
            C.cp(ACT, oT, psb(2).rearrange("p (a t) -> p a t", a=8))
            for half in range(2):
                for mc in range(8):
                    C.mm(ps[half], oT[:, mc, :], w_out_bf[:, mc, half * 512:(half + 1) * 512].k(mc),
                         start=(mc == 0), stop=(mc == 7))
            for half in range(2):
                hsl = slice(half * 512, (half + 1) * 512)
                C.stt(DVE, h1pre[:, hsl], xt[:, hsl], ALPHA, ps[half], ALU.mult, ALU.add)
            yield
            layernorm(h1pre, h1pre, LN_EPS)
            C.dma(SP, h1_dst, h1pre)
            yield

        def drain(g):
            for _ in g:
                pass

        def interleave(ga, gb):
            import os
            skew = int(os.environ.get("K_SKEW", "4"))
            nb_per = int(os.environ.get("K_NB", "1"))
            na_per = int(os.environ.get("K_NA", "1"))
            la = lb_ = True
            for _ in range(skew):
                try:
                    next(gb)
                except StopIteration:
                    lb_ = False
                    break
            while la or lb_:
                if lb_:
                    for _ in range(nb_per):
                        try:
                            next(gb)
                        except StopIteration:
                            lb_ = False
                            break
                if la:
                    for _ in range(na_per):
                        try:
                            next(ga)
                        except StopIteration:
                            la = False
                            break

        jobs = [(ti, xp[ti * 128:(ti + 1) * 128, :], V(h1scr[ti * 128:(ti + 1) * 128, :], ("h1scr", ti)), False)
                for ti in range(NT)]
        if SAMPLE:
            jobs.append((NT, xsm, V(h1scr[NTP * 128:(NTP + 1) * 128, :], ("h1scr", NTP)), True))
        if True:
            if jobs:
                drain(stageA(jobs[0][0], jobs[0][1], jobs[0][3]))
            for n, (ti, x_src, h1_dst, smp) in enumerate(jobs):
                gb = stageB(ti, h1_dst, smp)
                if n + 1 < len(jobs):
                    nj = jobs[n + 1]
                    interleave(stageA(nj[0], nj[1], nj[3]), gb)
                else:
                    drain(gb)
                if n == NT - 1 and not smp:
                    prompt_state_outputs()
            if NT == 0:
                prompt_state_outputs()

        if True:
            P.barrier()
            arena.off = phase_mark
            w_up_bf = sb([128, 8, DFF], BF16, "w_up_bf")
            w_dn_bf = sb([128, 32, D], BF16, "w_dn_bf")
            for cb in range(8):
                C.dma(POOL, w_up_bf[:, :, cb * 512:(cb + 1) * 512].k(cb),
                      w_up[:, cb * 512:(cb + 1) * 512].rearrange("(kc p) n -> p kc n", p=128))
            C.dma(SP, lng, row(ln2_g).partition_broadcast(128))
            C.dma(SP, lnb, row(ln2_b).partition_broadcast(128))
            for fc in range(32):
                C.dma(POOL, w_dn_bf[:, fc, :].k(fc), w_down[fc * 128:(fc + 1) * 128, :])
            upT = sb([128, 32, 512], BF16, "upT")
            h1T = sb([128, 8, 512], BF16, "h1T")
            h1b = [sb([128, D], BF16, f"h1b{i}") for i in range(2)]
            h1r = [sb([128, D], F32, f"h1r{i}") for i in range(2)]
            rl = [sb([128, 512], F32, f"rl{i}") for i in range(2)]
            pre2 = sb([128, D], F32, "pre2")
            outb = [sb([128, D], F32, f"outb{i}") for i in range(2)]
            ntiles = NT + (1 if SAMPLE else 0)
            tiles = list(range(NT)) + ([NTP] if SAMPLE else [])
            groups = [tiles[i:i + 4] for i in range(0, NT, 4)]
            if SAMPLE:
                groups.append([NTP])
            gcount = 0
            tcount = 0
            for grp in groups:
                ng = len(grp)
                W = ng * 128
                for gi, tix in enumerate(grp):
                    hb = h1b[(tcount + gi) % 2]
                    C.dma(POOL, hb, V(h1scr[tix * 128:(tix + 1) * 128, :], ("h1scr", tix)))
                    bank = 6 + (gi % 2)
                    for kc in range(8):
                        C.tr(psb(bank)[:, kc * 128:(kc + 1) * 128], hb[:, kc * 128:(kc + 1) * 128], ident_bf)
                    C.cp(ACT, h1T[:, :, gi * 128:(gi + 1) * 128], psb(bank).rearrange("p (a t) -> p a t", a=8))
                for fc in range(32):
                    bank = fc % 2
                    for kc in range(8):
                        C.mm(ps[bank][:, 0:W], w_up_bf[:, kc, fc * 128:(fc + 1) * 128].k(fc // 4), h1T[:, kc, 0:W],
                             start=(kc == 0), stop=(kc == 7))
                    r = rl[fc % 2]
                    C.act(r[:, 0:W], ps[bank][:, 0:W], AF.Relu)
                    C.tt(POOL if fc % 2 else DVE, upT[:, fc, 0:W], r[:, 0:W], r[:, 0:W], ALU.mult)
                for gi, tix in enumerate(grp):
                    hr = h1r[(tcount + gi) % 2]
                    C.dma(SP, hr, V(h1scr[tix * 128:(tix + 1) * 128, :], ("h1scr", tix)))
                    for half in range(2):
                        bank = 2 + ((gi * 2 + half) % 4)
                        for fc in range(32):
                            C.mm(ps[bank], upT[:, fc, gi * 128:(gi + 1) * 128], w_dn_bf[:, fc, half * 512:(half + 1) * 512].k(fc),
                                 start=(fc == 0), stop=(fc == 31))
                        hsl = slice(half * 512, (half + 1) * 512)
                        C.stt(DVE, pre2[:, hsl], hr[:, hsl], ALPHA, ps[bank], ALU.mult, ALU.add)
                    ob = outb[(tcount + gi) % 2]
                    layernorm(pre2, ob, LN_EPS)
                    dst = ys if tix == NTP else yp[tix * 128:(tix + 1) * 128, :]
                    C.dma(SP, dst, ob, out_final=True)
                tcount += ng
                gcount += 1

        sems = {e: es.enter_context(nc.semaphore(f"s_{e}")) for e in ENGINES}
        rings = {e: [es.enter_context(nc.semaphore(f"r_{e}{i}")) for i in range(n)] for e, n in DMA_RING.items()}
        P.prepare(sems, rings)
        build.stats = dict(P.stats)
        build.arena_peak = arena.peak
        with nc.allow_low_precision(reason="bf16 matmul operands, fp32 accumulation"), \
                nc.allow_non_contiguous_dma(reason="tiny per-channel vectors / state layouts"), \
                nc.Block() as block:
            block.tensor(lambda eng: P.emit_engine(PE, eng))
            block.scalar(lambda eng: P.emit_engine(ACT, eng))
            block.vector(lambda eng: P.emit_engine(DVE, eng))
            block.gpsimd(lambda eng: P.emit_engine(POOL, eng))
            block.sync(lambda eng: P.emit_engine(SP, eng))
    return nc


IN_NAMES = ["w_in", "shift_mu", "w0", "w1u", "a0", "a1u", "g1u", "k_k", "k_a", "r_k", "ln_x_w", "ln_x_b",
            "lb_logits", "hg_norm_w", "w_out", "ln1_g", "ln1_b", "w_up", "w_down", "ln2_g", "ln2_b"]


def make_in_maps(inputs, n_cores=8):
    f = lambda a: np.ascontiguousarray(np.asarray(a, dtype=np.float32))
    shared = {}
    for k in IN_NAMES:
        a = f(inputs[k])
        a = a[0] if k != "lb_logits" else a
        if k == "r_k":
            a = a.reshape(-1)
        shared[k] = np.ascontiguousarray(a)
    shared["consts"] = CONSTS
    maps = []
    for c in range(n_cores):
        m = dict(shared)
        m["xp"] = f(inputs["x_prompt"][c])
        m["xs"] = f(inputs["x_sample"][c * NB:(c + 1) * NB]).reshape(NB * TS, D)
        m["srw"] = f(inputs["state_rwkv"][0, c * NB:(c + 1) * NB])
        m["shg"] = f(inputs["state_hgrn"][0, c * NB:(c + 1) * NB])
        m["ssh"] = f(inputs["state_shift"][0, c * NB:(c + 1) * NB])
        maps.append(m)
    return maps


_NC_CACHE = {}


def kernel(**inputs):
    if "nc" not in _NC_CACHE:
        _NC_CACHE["nc"] = build()
    nc = _NC_CACHE["nc"]
    maps = make_in_maps(inputs)
    res = run_bass_kernel_spmd(nc, maps, core_ids=list(range(8)))
    R = res.results
    y_prompt = np.stack([R[c]["yp"] for c in range(8)]).astype(np.float32)
    y_sample = np.concatenate([R[c]["ys"].reshape(NB, TS, D) for c in range(8)]).astype(np.float32)
    rw_p = np.stack([R[c]["rwp"] for c in range(8)])[None].astype(np.float32)
    rw_s = np.concatenate([R[c]["rws"] for c in range(8)])[None].astype(np.float32)
    hg_p = np.stack([R[c]["hgp"] for c in range(8)])[None].astype(np.float32)
    hg_s = np.concatenate([R[c]["hgs"] for c in range(8)])[None].astype(np.float32)
    sh_p = np.stack([R[c]["shp"] for c in range(8)])[None].astype(np.float32)
    sh_s = np.concatenate([R[c]["shs"] for c in range(8)])[None].astype(np.float32)
    return (y_prompt, y_sample, rw_p, rw_s, hg_p, hg_s, sh_p, sh_s)
```

```python
import numpy as np
import concourse.bass as bass
import concourse.mybir as mybir
from concourse.bass_utils import run_bass_kernel_spmd

F32 = mybir.dt.float32
BF16 = mybir.dt.bfloat16
AF = mybir.ActivationFunctionType
ALU = mybir.AluOpType
AX = mybir.AxisListType

DEBUG_LINES = False
LINE_OF = {}
PE, ACT, DVE, POOL, SP = "pe", "act", "dve", "pool", "sp"
ENGINES = (PE, ACT, DVE, POOL, SP)
DMA_RING = {SP: 12, ACT: 4, POOL: 8}


class Res:
    __slots__ = ("name", "last_w", "readers")

    def __init__(self, name):
        self.name = name
        self.last_w = None
        self.readers = []


class Op:
    __slots__ = ("eng", "fn", "deps", "is_dma", "signal", "idx", "dma_no", "extra_wait", "rg")

    def __init__(self, eng, fn, is_dma):
        self.eng = eng
        self.fn = fn
        self.deps = set()
        self.is_dma = is_dma
        self.signal = False
        self.idx = None
        self.dma_no = None
        self.extra_wait = None
        self.rg = None


def _pe_inorder_ok(d, o):
    return d.rg is None or o.rg is None or d.rg == o.rg


class Prog:
    def __init__(self):
        self.ops = {e: [] for e in ENGINES}
        self.order = []
        self.n_dma = {e: 0 for e in ENGINES}
        self.dma_ops = {e: [] for e in ENGINES}
        self.out_dmas = []

    def op(self, eng, fn, reads=(), writes=(), dma=False, out=False):
        o = Op(eng, fn, dma)
        for r in reads:
            if r.last_w is not None:
                o.deps.add(r.last_w)
        for w in writes:
            if w.last_w is not None:
                o.deps.add(w.last_w)
            for rd in w.readers:
                o.deps.add(rd)
        for r in reads:
            r.readers.append(o)
        for w in writes:
            w.last_w = o
            w.readers = []
        if getattr(self, "barrier_left", None) and eng in self.barrier_left:
            self.barrier_left.discard(eng)
            o.deps.update(self.pending_barrier)
        o.deps.discard(o)
        if dma:
            o.dma_no = self.n_dma[eng]
            self.n_dma[eng] += 1
            self.dma_ops[eng].append(o)
            o.signal = True
            if out:
                self.out_dmas.append(o)
        self.ops[eng].append(o)
        self.order.append(o)
        return o

    def barrier(self):
        pend = []
        for e in ENGINES:
            comp = [o for o in self.ops[e] if not o.is_dma]
            if comp:
                pend.append(comp[-1])
            pend.extend(self.dma_ops[e][-DMA_RING.get(e, 0):] if e in DMA_RING else [])
        self.pending_barrier = pend
        self.barrier_left = set(ENGINES)

    def prepare(self, sems, rings):
        for o in self.order:
            for d in o.deps:
                if d.is_dma:
                    continue
                if d.eng == o.eng and d.eng == PE and not o.is_dma and _pe_inorder_ok(d, o):
                    continue
                d.signal = True
        sig = {}
        for e in ENGINES:
            c = 0
            for o in self.ops[e]:
                if o.is_dma:
                    R = len(rings[e])
                    sig[o] = (rings[e][o.dma_no % R], 16 * (o.dma_no // R + 1))
                elif o.signal:
                    c += 1
                    sig[o] = (sems[e], c)
        self.sig = sig
        self.rings = rings
        self.stats = {e: len(self.ops[e]) for e in ENGINES}

    def emit_engine(self, e, eng):
        sig, rings = self.sig, self.rings
        waited = {}

        def wait(sem, val):
            k = id(sem)
            if waited.get(k, 0) >= val:
                return
            waited[k] = val
            eng.wait_ge(sem, val)

        for o in self.ops[e]:
            for d in o.deps:
                if d not in sig:
                    continue
                if d.eng == e and not d.is_dma and not o.is_dma and e == PE and _pe_inorder_ok(d, o):
                    continue
                s, v = sig[d]
                wait(s, v)
            if o.is_dma:
                R = len(rings[e])
                if o.dma_no >= R:
                    wait(rings[e][o.dma_no % R], 16 * (o.dma_no // R))
            ins = o.fn(eng)
            if DEBUG_LINES:
                LINE_OF[str(getattr(getattr(ins, "ins", ins), "name", ins))] = o.extra_wait
            if o in sig:
                s, v = sig[o]
                ins.then_inc(s, 16 if o.is_dma else 1)
        if e == SP:
            for q in ENGINES:
                if q not in rings:
                    continue
                for o in self.dma_ops[q][-len(rings[q]):]:
                    s, v = sig[o]
                    wait(s, v)


class V:
    __slots__ = ("ap", "key")

    def __init__(self, ap, key):
        self.ap = ap
        self.key = key

    def __getitem__(self, idx):
        return V(self.ap[idx], self.key)

    def k(self, sub):
        return V(self.ap, (self.key, sub))

    @property
    def shape(self):
        return self.ap.shape

    def rearrange(self, *a, **kw):
        return V(self.ap.rearrange(*a, **kw), self.key)

    def unsqueeze(self, ax):
        return V(self.ap.unsqueeze(ax), self.key)

    def to_broadcast(self, shape):
        return V(self.ap.to_broadcast(list(shape)), self.key)

    def bitcast(self, dt):
        return V(self.ap.bitcast(dt), self.key)


def _ap(x):
    return x.ap if isinstance(x, V) else x


def _isnum(x):
    return isinstance(x, (int, float))


class Arena:
    def __init__(self, nc, es, nbytes):
        self.n2 = nbytes // 2
        self.t = es.enter_context(nc.sbuf_tensor("arena", [128, self.n2], BF16))
        self.off = 0
        self.cnt = 0
        self.peak = 0

    def alloc(self, shape, dt, name=None):
        shape = list(shape)
        esz = 4 if dt == F32 else 2
        n = int(np.prod(shape[1:]))
        nbytes = (n * esz + 3) // 4 * 4
        o = self.off
        assert o + nbytes <= self.n2 * 2, f"arena overflow allocating {name} {shape}: {o}+{nbytes} > {self.n2 * 2}"
        self.off += nbytes
        self.peak = max(self.peak, self.off)
        ap = self.t[0:shape[0], o // 2:o // 2 + nbytes // 2]
        if esz == 4:
            ap = ap.bitcast(F32)
        ap = ap[:, 0:n]
        if len(shape) == 3:
            ap = ap.rearrange("p (a b) -> p a b", a=shape[1])
        elif len(shape) == 4:
            ap = ap.rearrange("p (a b c) -> p a b c", a=shape[1], b=shape[2])
        self.cnt += 1
        return V(ap, name or f"t{self.cnt}")


class Ctx:
    def __init__(self, nc, P, arena):
        self.nc, self.P, self.arena = nc, P, arena
        self.res = {}
        self.defer = None

    class _Pending:
        __slots__ = ("args", "rg")

        def __init__(self, args):
            self.args = args
            self.rg = None

    def commit(self, p):
        o_ = self._rec_now(*p.args)
        o_.rg = p.rg
        return o_

    def sb(self, shape, dt, name=None):
        return self.arena.alloc(shape, dt, name)

    def R(self, x):
        k = x.key if isinstance(x, V) else (x.name, None)
        r = self.res.get(k)
        if r is None:
            r = self.res[k] = Res(k)
        return r

    def rec(self, eng, fn, outs, ins, dma=False, out=False):
        if self.defer is not None:
            p = Ctx._Pending((eng, fn, list(outs), list(ins), dma, out))
            self.defer.append(p)
            return p
        return self._rec_now(eng, fn, outs, ins, dma, out)

    def _rec_now(self, eng, fn, outs, ins, dma=False, out=False):
        reads = [self.R(i) for i in ins if i is not None and not _isnum(i)]
        writes = [self.R(o) for o in outs]
        writes += [r for r in reads if isinstance(r.name, str) and r.name.startswith("ps")]
        o_ = self.P.op(eng, fn, reads=reads, writes=writes, dma=dma, out=out)
        if DEBUG_LINES:
            import sys as _s
            f = _s._getframe(1)
            while f.f_code.co_name not in ("mixer_tile", "build", "layernorm") and f.f_back is not None:
                f = f.f_back
            o_.extra_wait = f.f_lineno
        return o_

    def mm(self, out, lhsT, rhs, start=True, stop=True):
        o, l, r = _ap(out), _ap(lhsT), _ap(rhs)
        op = self.rec(PE, lambda e: e.matmul(o, lhsT=l, rhs=r, start=start, stop=stop,
                                             skip_group_check=True), [out], [lhsT, rhs])
        kr = l.shape[0]
        if kr < 128:
            op.rg = (kr, l.base_partition())
        return op

    def tr(self, out, in_, ident):
        o, i, d = _ap(out), _ap(in_), _ap(ident)
        return self.rec(PE, lambda e: e.transpose(o, i, d), [out], [in_, ident])

    def act(self, out, in_, func, bias=None, scale=1.0):
        o, i = _ap(out), _ap(in_)
        kw = {}
        if bias is not None:
            kw["bias"] = _ap(bias)
        s = _ap(scale)
        return self.rec(ACT, lambda e: e.activation(out=o, in_=i, func=func, scale=s, **kw), [out],
                        [in_, bias, scale])

    def tt(self, eng, out, a, b, op):
        o, x, y = _ap(out), _ap(a), _ap(b)
        return self.rec(eng, lambda e: e.tensor_tensor(out=o, in0=x, in1=y, op=op), [out], [a, b])

    def ts(self, eng, out, a, s1, op0, s2=None, op1=None):
        o, x, v1, v2 = _ap(out), _ap(a), _ap(s1), _ap(s2)
        if op1 is None:
            f = lambda e: e.tensor_scalar(out=o, in0=x, scalar1=v1, scalar2=None, op0=op0)
        else:
            f = lambda e: e.tensor_scalar(out=o, in0=x, scalar1=v1, scalar2=v2, op0=op0, op1=op1)
        return self.rec(eng, f, [out], [a, s1, s2])

    def stt(self, eng, out, in0, scalar, in1, op0, op1):
        o, x, y, s = _ap(out), _ap(in0), _ap(in1), _ap(scalar)
        return self.rec(eng, lambda e: e.scalar_tensor_tensor(out=o, in0=x, scalar=s, in1=y, op0=op0, op1=op1),
                        [out], [in0, in1, scalar])

    def cp(self, eng, out, in_):
        o, i = _ap(out), _ap(in_)
        if eng == ACT:
            return self.rec(ACT, lambda e: e.activation(out=o, in_=i, func=AF.Copy), [out], [in_])
        return self.rec(eng, lambda e: e.tensor_copy(out=o, in_=i), [out], [in_])

    def recip(self, out, in_):
        o, i = _ap(out), _ap(in_)
        return self.rec(DVE, lambda e: e.reciprocal(out=o, in_=i), [out], [in_])

    def scan(self, out, d0, d1):
        o, a, b = _ap(out), _ap(d0), _ap(d1)
        return self.rec(DVE, lambda e: e.tensor_tensor_scan(out=o, data0=a, data1=b, initial=0.0,
                                                            op0=ALU.mult, op1=ALU.add), [out], [d0, d1])

    def red(self, eng, out, in_, op=ALU.add):
        o, i = _ap(out), _ap(in_)
        return self.rec(eng, lambda e: e.tensor_reduce(out=o, in_=i, axis=AX.X, op=op), [out], [in_])

    def memset(self, eng, out, val):
        o = _ap(out)
        return self.rec(eng, lambda e: e.memset(o, val), [out], [])

    def dma(self, eng, out, in_, out_final=False, extra_out=()):
        o, i = _ap(out), _ap(in_)
        return self.rec(eng, lambda e: e.dma_start(out=o, in_=i), [out, *extra_out], [in_], dma=True, out=out_final)

    def bn_stats(self, out, in_):
        o, i = _ap(out), _ap(in_)
        return self.rec(DVE, lambda e: e.bn_stats(out=o, in_=i), [out], [in_])

    def bn_aggr(self, out, in_):
        o, i = _ap(out), _ap(in_)
        return self.rec(DVE, lambda e: e.bn_aggr(out=o, in_=i), [out], [in_])


D = 1024
PJ = 3840
RWP = 1792
NTP = 16
NB = 16
TS = 8
DFF = 4096
ALPHA = 2.0 ** 0.25
CDEC = -float(np.exp(-0.5))
LN_EPS = 1e-5
GN_EPS = 64e-5
RMS_EPS = 1e-6
ARENA_BYTES = 212800


def make_consts():
    s = np.arange(128)[:, None]
    t = np.arange(128)[None, :]
    cols = {}
    cols["ident"] = (s == t)
    cols["mS_p"] = (s < t)
    cols["mI_p"] = (s <= t)
    cols["mST_p"] = (t < s)
    same = (s // TS) == (t // TS)
    cols["mS_s"] = (s < t) & same
    cols["mI_s"] = (s <= t) & same
    cols["mST_s"] = (t < s) & same
    cols["reset_p"] = np.broadcast_to(t != 0, (128, 128))
    cols["reset_s"] = np.broadcast_to((t % TS) != 0, (128, 128))
    cols["bdones"] = (s // 64) == (t // 64)
    cols["hsel"] = (s // 64) == np.arange(2)[None, :]
    cols["cm"] = (s // TS) == np.arange(NB)[None, :]
    cols["i64s"] = (s % 64) == np.arange(64)[None, :]
    off = {}
    parts = []
    o = 0
    for k, v in cols.items():
        v = np.asarray(v, np.float32)
        off[k] = (o, o + v.shape[1])
        o += v.shape[1]
        parts.append(v)
    return np.ascontiguousarray(np.concatenate(parts, axis=1)), off


CONSTS, COFF = make_consts()
NCONST = CONSTS.shape[1]


class _Stop(Exception):
    pass


def build(NT=NTP, SAMPLE=True, DBG=False, STAGE=99):
    from contextlib import ExitStack
    nc = bass.Bass("TRN2", target_bir_lowering=False)

    def din(name, shape):
        return nc.dram_tensor(name, list(shape), F32, kind="ExternalInput").ap()

    def dout(name, shape):
        return nc.dram_tensor(name, list(shape), F32, kind="ExternalOutput").ap()

    xp = din("xp", [NTP * 128, D]); xsm = din("xs", [128, D])
    srw = din("srw", [NB, 8, 64, 64]); shg = din("shg", [NB, 4, 128, 128]); ssh = din("ssh", [NB, RWP])
    w_in = din("w_in", [D, PJ]); shift_mu = din("shift_mu", [RWP]); w0 = din("w0", [512])
    w1u = din("w1u", [64, 512]); a0 = din("a0", [512]); a1u = din("a1u", [64, 512]); g1u = din("g1u", [128, 512])
    k_k = din("k_k", [512]); k_a = din("k_a", [512]); r_k = din("r_k", [512])
    ln_x_w = din("ln_x_w", [512]); ln_x_b = din("ln_x_b", [512]); lb_logits = din("lb_logits", [2, 512])
    hg_norm_w = din("hg_norm_w", [512]); w_out = din("w_out", [D, D]); ln1_g = din("ln1_g", [D]); ln1_b = din("ln1_b", [D])
    w_up = din("w_up", [D, DFF]); w_down = din("w_down", [DFF, D]); ln2_g = din("ln2_g", [D]); ln2_b = din("ln2_b", [D])
    cst_d = din("consts", [128, NCONST])
    yp = dout("yp", [NTP * 128, D]); ys = dout("ys", [128, D])
    rwp = dout("rwp", [8, 64, 64]); rws = dout("rws", [NB, 8, 64, 64])
    hgp = dout("hgp", [4, 128, 128]); hgs = dout("hgs", [NB, 4, 128, 128])
    shp = dout("shp", [RWP]); shs = dout("shs", [NB, RWP])
    h1scr = nc.dram_tensor("h1scr", [(NTP + 1) * 128, D], F32).ap()
    if DBG:
        dbg_d = dout("dbg", [128, 4096])

    def row(v):
        return v.rearrange("(o n) -> o n", o=1)

    P = Prog()
    with ExitStack() as es:
        arena = Arena(nc, es, ARENA_BYTES)
        C = Ctx(nc, P, arena)
        sb = C.sb
        ps = [V(es.enter_context(nc.psum_tensor(f"ps{i}", [128, 512], F32))[:], f"ps{i}") for i in range(8)]

        def psv(i, a):
            return ps[i].rearrange("p (a t) -> p a t", a=a)

        def psb(i):
            return ps[i].bitcast(BF16)

        def bc3(ap2, n):
            return ap2.unsqueeze(2).to_broadcast([ap2.shape[0], ap2.shape[1], n])

        def v3(t):
            return t.rearrange("p (a t) -> p a t", a=4)

        ident_bf = sb([128, 128], BF16, "ident_bf")
        lng = sb([128, D], F32, "lng"); lnb = sb([128, D], F32, "lnb")
        C.dma(SP, lng, row(ln1_g).partition_broadcast(128))
        C.dma(SP, lnb, row(ln1_b).partition_broadcast(128))
        bnst = sb([128, 12], F32, "bnst"); mv = sb([128, 2], F32, "mv"); rstd1 = sb([128, 1], F32, "rstd1")
        fence_ln = sb([128, 2], F32, "fence_ln")
        if DBG:
            dbg = sb([128, 4096], F32, "dbg")
            C.memset(POOL, dbg, 0.0)
            dbg_pos = [0]
            dbg_map = {}

            def dump(name, v, n):
                a = dbg_pos[0]
                shape = list(v.shape)
                dst = dbg[0:shape[0], a:a + n]
                if len(shape) == 3:
                    dst = dst.rearrange("p (a b) -> p a b", a=shape[1])
                C.cp(POOL, dst, v)
                dbg_pos[0] += n
                dbg_map[name] = (a, n, shape)
            build.dbg_map = dbg_map
        else:
            def dump(name, v, n):
                return None
        phase_mark = arena.off

        def layernorm(src, dst, eps):
            for half in range(2):
                C.bn_stats(bnst[:, half * 6:(half + 1) * 6], src[:, half * 512:(half + 1) * 512])
            C.bn_aggr(mv, bnst)
            C.act(rstd1, mv[:, 1:2], AF.Ln, bias=eps)
            C.act(rstd1, rstd1, AF.Exp, scale=-0.5)
            C.ts(DVE, dst, src, mv[:, 0:1], ALU.subtract, rstd1[:, 0:1], ALU.mult)
            halves = []
            for eng_, hsl in ((DVE, slice(0, 512)), (DVE, slice(512, 1024))):
                dk = dst[:, hsl].k(hsl.start)
                halves.append(dk)
                o, a_, g_, b_ = _ap(dk), _ap(dst[:, hsl]), _ap(lng[:, hsl]), _ap(lnb[:, hsl])
                C.rec(eng_, lambda e, o=o, a_=a_, g_=g_: e.tensor_tensor(out=o, in0=a_, in1=g_, op=ALU.mult), [dk], [dst, lng])
                C.rec(eng_, lambda e, o=o, b_=b_: e.tensor_tensor(out=o, in0=o, in1=b_, op=ALU.add), [dk], [dk, lnb])
            C.rec(POOL, lambda e: e.memset(_ap(fence_ln), 0.0), [dst, fence_ln], halves)

        F32_CONSTS = ("ident", "reset_p", "reset_s", "hsel", "cm", "i64s")
        BF_CONSTS = ("mS_p", "mI_p", "mST_p", "mS_s", "mI_s", "mST_s", "bdones")
        cviews = {}
        nf = sum(COFF[k][1] - COFF[k][0] for k in F32_CONSTS)
        nb_ = sum(COFF[k][1] - COFF[k][0] for k in BF_CONSTS)
        cstf = sb([128, nf], F32, "cstf"); cstb = sb([128, nb_], BF16, "cstb")
        o_ = 0
        for k_ in F32_CONSTS:
            a, b = COFF[k_]
            C.dma(SP, cstf[:, o_:o_ + b - a].k(k_), cst_d[:, a:b])
            cviews[k_] = cstf[:, o_:o_ + b - a].k(k_)
            o_ += b - a
        o_ = 0
        for k_ in BF_CONSTS:
            a, b = COFF[k_]
            C.dma(POOL, cstb[:, o_:o_ + b - a].k(k_), cst_d[:, a:b])
            cviews[k_] = cstb[:, o_:o_ + b - a].k(k_)
            o_ += b - a

        def cc(name):
            return cviews[name]

        C.cp(POOL, ident_bf, cc("ident"))
        bdones_bf = cc("bdones")
        mu14 = sb([128, 14], F32, "mu14")
        kk4 = sb([128, 4], F32, "kk4"); ka4 = sb([128, 4], F32, "ka4"); rk4 = sb([128, 4], F32, "rk4")
        w04 = sb([128, 4], F32, "w04"); a04 = sb([128, 4], F32, "a04")
        lbl = sb([128, 2, 4], F32, "lbl")
        with nc.allow_non_contiguous_dma(reason="tiny per-channel parameter vectors"):
            C.dma(SP, mu14, shift_mu.rearrange("(c p) -> p c", p=128))
            C.dma(SP, kk4, k_k.rearrange("(c p) -> p c", p=128))
            C.dma(SP, ka4, k_a.rearrange("(c p) -> p c", p=128))
            C.dma(SP, rk4, r_k.rearrange("(c p) -> p c", p=128))
            C.dma(SP, w04, w0.rearrange("(c p) -> p c", p=128))
            C.dma(SP, a04, a0.rearrange("(c p) -> p c", p=128))
            C.dma(SP, lbl, lb_logits.rearrange("l (c p) -> p l c", p=128))
        WA = sb([128, 512], BF16, "WA")
        C.dma(POOL, WA[0:64, :].k("lo"), w1u)
        C.dma(POOL, WA[64:128, :].k("hi"), a1u)
        g1u_bf = sb([128, 512], BF16, "g1u_bf")
        C.dma(POOL, g1u_bf, g1u)
        lnxw = sb([128, 512], BF16, "lnxw"); lnxb = sb([128, 512], BF16, "lnxb"); hgw = sb([128, 512], BF16, "hgw")
        C.dma(POOL, lnxw, row(ln_x_w).partition_broadcast(128))
        C.dma(POOL, lnxb, row(ln_x_b).partition_broadcast(128))
        C.dma(POOL, hgw, row(hg_norm_w).partition_broadcast(128))
        lb4 = sb([128, 4], F32, "lb4"); oml4 = sb([128, 4], F32, "oml4"); etmp = sb([128, 4], F32, "etmp")
        C.tt(DVE, etmp, lbl[:, 1, :], lbl[:, 0, :], ALU.subtract)
        C.act(etmp, etmp, AF.Exp)
        C.ts(DVE, lb4, etmp, 1.0, ALU.add)
        C.recip(lb4, lb4)
        C.tt(DVE, oml4, etmp, lb4, ALU.mult)
        RKsel = sb([128, 4, 2], BF16, "RKsel")
        C.tt(DVE, RKsel, bc3(rk4, 2), cc("hsel").unsqueeze(1).to_broadcast([128, 4, 2]), ALU.mult)

        ov_lo = arena.off
        w_in_bf = sb([128, 8, PJ], BF16, "w_in_bf")
        ov_hi = arena.off
        for kc in range(8):
            C.dma(POOL, w_in_bf[:, kc, :].k(kc), w_in[kc * 128:(kc + 1) * 128, :])
        w_out_bf = sb([128, 8, D], BF16, "w_out_bf")
        for kc in range(8):
            C.dma(POOL, w_out_bf[:, kc, :].k(kc), w_out[kc * 128:(kc + 1) * 128, :])

        ST = sb([128, 4, 64], F32, "ST"); ST_bf = sb([128, 4, 64], BF16, "ST_bf")
        SH = sb([128, 4, 128], F32, "SH"); SH_bf = sb([128, 4, 128], BF16, "SH_bf")
        plast = sb([128, 14], F32, "plast")
        for t_ in (ST, ST_bf, SH, SH_bf, plast):
            C.memset(POOL, t_, 0.0)

        def prompt_state_outputs():
            with nc.allow_non_contiguous_dma(reason="tiny state vector"):
                C.dma(SP, shp.rearrange("(c p) -> p c", p=128), plast, out_final=True)
            C.dma(SP, hgp.rearrange("h k v -> k h v"), SH, out_final=True)
            identf = cc("ident")
            for pr in range(4):
                C.tr(ps[2][0:64, pr * 128:(pr + 1) * 128], ST[:, pr, :], identf)
            rwo = T[0]
            C.cp(DVE, rwo[0:64, :], ps[2][0:64, :])
            C.dma(SP, rwp.rearrange("h v j -> v h j"), rwo[0:64, :].rearrange("p (h j) -> p h j", h=8), out_final=True)
            if DBG:
                C.dma(SP, dbg_d, dbg, out_final=True)


        x_t = [sb([128, D], F32, f"x_t{i}") for i in range(2)]
        x_bf = sb([128, D], BF16, "x_bf")
        xT = sb([128, 8, 128], BF16, "xT")
        pr_ = sb([128, 14, 129], F32, "pr")
        xs = sb([128, 14, 128], F32, "xsft")
        T = [sb([128, 512], F32, f"T{i}") for i in range(10)]
        z12 = sb([128, 128], BF16, "z12"); sg_bf = sb([128, 128], BF16, "sg_bf"); tqb = sb([128, 512], BF16, "tqb")
        bhT = sb([128, 4, 128], BF16, "bhT"); khT = sb([128, 4, 128], BF16, "khT"); vT = sb([128, 4, 128], BF16, "vT")
        khTb = sb([128, 4, 128], BF16, "khTb")
        fence2 = sb([128, 2], F32, "fence2")
        HB = []
        for i in range(2):
            HB.append(dict(
                AR=sb([128, 4, 2, 128], BF16, f"AR{i}"), bT=sb([128, 4, 128], BF16, f"bT{i}"), kT=sb([128, 4, 128], BF16, f"kT{i}"),
                A_tm=sb([128, 512], BF16, f"A_tm{i}"), Bh_tm=sb([128, 512], BF16, f"Bh_tm{i}"),
                Kh_tm=sb([128, 512], BF16, f"Kh_tm{i}"), V_tm=sb([128, 512], BF16, f"V_tm{i}"),
                qTb=sb([128, 4, 128], BF16, f"qTb{i}"), kTb=sb([128, 4, 128], BF16, f"kTb{i}"),
                khat_tm=sb([128, 512], BF16, f"khat_tm{i}"), i_tm=sb([128, 512], BF16, f"i_tm{i}"),
                gs_t=sb([128, 512], BF16, f"gs_t{i}"), g_tm=sb([128, 512], BF16, f"g_tm{i}"),
                bon8=sb([128, 8], F32, f"bon8{i}"), gC=sb([128, 4, NB], F32, f"gC{i}"), decH=sb([128, 4, NB], F32, f"decH{i}")))
        Pm = sb([128, 8, 128], BF16, "Pm"); Tm = sb([128, 8, 128], BF16, "Tm"); RR = sb([128, 8, 128], BF16, "RR")
        NrbT = sb([128, 8, 128], BF16, "NrbT"); AkT = sb([128, 8, 128], BF16, "AkT"); NrkT = sb([128, 8, 128], BF16, "NrkT")
        TTf = sb([128, 8, 128], BF16, "TTf")
        W1T = sb([128, 4, 128], BF16, "W1T")
        Z_tm = sb([128, 512], BF16, "Z_tm"); U_tm = sb([128, 512], BF16, "U_tm")
        st16 = sb([128, 16], F32, "st16"); m8 = sb([128, 8], F32, "m8"); r8 = sb([128, 8], F32, "r8")
        o_all = sb([128, D], BF16, "o_all"); oT = sb([128, 8, 128], BF16, "oT")
        h1pre = sb([128, D], F32, "h1pre")
        attT = sb([128, 4, 128], BF16, "attT")
        s4 = sb([128, 4], F32, "s4"); rr4 = sb([128, 4], F32, "rr4")
        BT0 = h1pre[:, 0:512]; BT1 = h1pre[:, 512:1024]
        SHtmp = v3(BT1)
        STtmp = BT1[:, 0:256].rearrange("p (a v) -> p a v", a=4)
        identb4 = ident_bf.unsqueeze(1).to_broadcast([128, 4, 128])
        save_off = arena.off
        arena.off = ov_lo
        scrA = [sb([128, 2048], F32, f"scrA{i}") for i in range(2)]
        S0T32 = sb([128, NB, 4, 64], F32, "S0T32")
        S0Tb = sb([128, NB, 4, 64], BF16, "S0Tb")
        sshT = sb([128, 14, NB], F32, "sshT"); lastp = sb([128, 14, NB], F32, "lastp")
        EW = [sb([128, 4, 128], BF16, f"EW{i}") for i in range(2)]
        ER = [sb([128, 4, 128], BF16, f"ER{i}") for i in range(2)]
        EQ = [sb([128, 4, 128], BF16, f"EQ{i}") for i in range(2)]
        Ub = sb([128, 512], BF16, "Ub"); Vb = sb([128, 512], BF16, "Vb"); khb = sb([128, 512], BF16, "khb")
        Dg = sb([128, 4, 64], F32, "Dg")
        S0h = [sb([128, 4, 128], F32, f"S0h{i}") for i in range(2)]
        S0hb = sb([128, 4, 128], BF16, "S0hb")
        Sn = sb([128, 512], F32, "Sn")
        fence_t = sb([128, 2], F32, "fence_t")
        assert arena.off <= ov_hi, (arena.off, ov_hi)
        ov_bufs = scrA + [S0T32, S0Tb, sshT, lastp] + EW + ER + EQ + [Ub, Vb, khb, Dg] + S0h + [S0hb, Sn, fence_t]
        arena.off = save_off
        ssh_tm = scrA[0][0:NB, 0:RWP]
        hh_order = (0, 2, 4, 6, 1, 3, 5, 7)
        heads_of = [(0, 1, 2, 3), (4, 5, 6, 7)]
        PA, PB_ = 6, 7

        cb = cc

        def cfg(sample):
            sfx = "_s" if sample else "_p"
            return dict(nch=NB if sample else 1, Cn=TS if sample else 128, L=3 if sample else 7,
                        mS=cb("mS" + sfx), mI=cb("mI" + sfx), mST=cb("mST" + sfx), reset=cc("reset" + sfx))

        def stageA(ti, x_src, sample):
            g_ = cfg(sample)
            nch, Cn, reset = g_["nch"], g_["Cn"], g_["reset"]
            H = HB[ti % 2]
            AR, bT, kT = H["AR"], H["bT"], H["kT"]
            xt = x_t[ti % 2]
            C.dma(SP, xt, x_src)
            C.dma(POOL, x_bf, x_src)
            for kc in range(8):
                C.tr(psb(PB_)[:, kc * 128:(kc + 1) * 128], x_bf[:, kc * 128:(kc + 1) * 128], ident_bf)
            C.cp(ACT, xT, psb(PB_).rearrange("p (a t) -> p a t", a=8))
            yield
            sig, kq, fgl, bcs, eb, enb, ebl, sq_, eg = T[1], T[4], T[0], T[2], T[3], T[5], T[6], T[7], T[8]

            def proj_fm(c0, n, bank):
                for j in range(n):
                    c = c0 + j
                    for kc in range(8):
                        C.mm(ps[bank][:, j * 128:(j + 1) * 128], w_in_bf[:, kc, c * 128:(c + 1) * 128].k(kc),
                             xT[:, kc, :], start=(kc == 0), stop=(kc == 7))

            def proj_tm(col0, bank):
                for kc in range(8):
                    C.mm(ps[bank], xT[:, kc, :], w_in_bf[:, kc, col0:col0 + 512].k(kc), start=(kc == 0), stop=(kc == 7))

            proj_fm(0, 4, PA); C.cp(ACT, pr_[:, 0:4, 1:129], psv(PA, 4)); yield
            proj_fm(4, 4, PB_); C.cp(ACT, pr_[:, 4:8, 1:129], psv(PB_, 4)); yield
            proj_fm(8, 4, PA); C.cp(ACT, pr_[:, 8:12, 1:129], psv(PA, 4)); yield
            proj_fm(12, 2, PB_); C.cp(ACT, pr_[:, 12:14, 1:129], psv(PB_, 4)[:, 0:2, :]); yield
            prev, cur = pr_[:, :, 0:128], pr_[:, :, 1:129]
            if not sample:
                C.cp(DVE, pr_[:, :, 0:1], plast.unsqueeze(2))
                C.cp(DVE, plast.unsqueeze(2), pr_[:, :, 128:129])
            else:
                C.memset(POOL, pr_[:, :, 0:1], 0.0)
            proj_fm(14, 4, PA)
            C.act(sq_, ps[PA], AF.Sigmoid)
            C.tt(DVE, sq_, ps[PA], sq_, ALU.mult)
            yield
            proj_fm(18, 4, PB_)
            C.act(sig, ps[PB_], AF.Sigmoid)
            yield
            proj_tm(RWP + 1024, PA)
            C.cp(ACT, H["i_tm"], ps[PA])
            yield
            proj_tm(RWP + 1536, PB_)
            C.act(eg, ps[PB_], AF.Sigmoid)
            C.tt(DVE, H["gs_t"], ps[PB_], eg, ALU.mult)
            yield
            if sample:
                C.rec(POOL, lambda e: e.memset(_ap(fence_t), 0.0),
                      [w_in_bf[:, kc, :].k(kc) for kc in range(8)] + ov_bufs
                      + [S0T32[:, b, :, :].k(b) for b in range(NB)] + [S0Tb[:, b, :, :].k(b) for b in range(NB)], [])
                for t_ in EW + ER + EQ:
                    C.memset(POOL, t_, 0.0)
                C.dma(SP, ssh_tm, ssh)
                for c in range(14):
                    C.tr(ps[PA][:, c * NB:(c + 1) * NB], ssh_tm[:, c * 128:(c + 1) * 128], cc("ident")[0:NB, 0:NB])
                C.cp(DVE, sshT, ps[PA][:, 0:14 * NB].rearrange("p (c b) -> p c b", c=14))
            for eng_, c0, c1 in ((DVE, 0, 8), (DVE, 8, 14)):
                C.tt(eng_, xs[:, c0:c1, :].k(c0), prev[:, c0:c1, :], cur[:, c0:c1, :], ALU.subtract)
                C.tt(eng_, xs[:, c0:c1, :].k(c0), xs[:, c0:c1, :].k(c0), bc3(mu14[:, c0:c1], 128), ALU.mult)
                C.tt(eng_, xs[:, c0:c1, :].k(c0), xs[:, c0:c1, :].k(c0), cur[:, c0:c1, :], ALU.add)
            C.rec(POOL, lambda e: e.memset(_ap(fence2), 0.0), [xs, fence2], [xs[:, 0:8, :].k(0), xs[:, 8:14, :].k(8)])
            if sample:
                cur4 = cur.rearrange("p c (b t) -> p c b t", t=TS)
                xs4 = xs.rearrange("p c (b t) -> p c b t", t=TS)
                cur0, xs0 = cur4[:, :, :, 0], xs4[:, :, :, 0]
                C.tt(DVE, xs0, sshT, cur0, ALU.subtract)
                C.tt(DVE, xs0, xs0, bc3(mu14, NB), ALU.mult)
                C.tt(DVE, xs0, xs0, cur0, ALU.add)
                C.cp(DVE, lastp, cur4[:, :, :, TS - 1])
                for g0 in range(0, 14, 4):
                    bk = PA if (g0 // 4) % 2 == 0 else PB_
                    n = min(4, 14 - g0)
                    for j in range(n):
                        C.tr(ps[bk][0:NB, j * 128:(j + 1) * 128], lastp[:, g0 + j, :], cc("ident"))
                    C.cp(ACT, ssh_tm[:, g0 * 128:(g0 + n) * 128], ps[bk][0:NB, 0:n * 128])
                C.dma(SP, shs, ssh_tm, out_final=True)
            yield
            r_ = xs[:, 0:4, :]; k_ = xs[:, 4:8, :]; v_ = xs[:, 8:12, :]
            C.act(z12[0:64, :], xs[0:64, 12, :], AF.Tanh)
            C.act(sg_bf, xs[:, 13, :], AF.Sigmoid)
            C.cp(DVE, z12[64:128, :], xs[64:128, 12, :])
            for pr in range(4):
                sl = slice(pr * 128, (pr + 1) * 128)
                C.mm(ps[PA][:, sl], WA[0:64, sl].k("lo"), z12[0:64, :])
            for pr in range(4):
                sl = slice(pr * 128, (pr + 1) * 128)
                C.mm(ps[PB_][:, sl], WA[64:128, sl].k("hi"), z12[64:128, :])
            sw, alr = T[9], T[8]
            for pr in range(4):
                sl = slice(pr * 128, (pr + 1) * 128)
                C.act(sw[:, sl], ps[PA][:, sl], AF.Sigmoid, bias=w04[:, pr:pr + 1])
            C.tt(DVE, v3(kq), v3(sig), bc3(oml4, 128), ALU.mult)
            C.tt(DVE, v3(fgl), v3(kq), bc3(lb4, 128), ALU.add)
            C.tt(DVE, v3(kq), bc3(oml4, 128), v3(kq), ALU.subtract)
            yield
            for pr in range(4):
                sl = slice(pr * 128, (pr + 1) * 128)
                C.act(alr[:, sl], ps[PB_][:, sl], AF.Sigmoid, bias=a04[:, pr:pr + 1])
            C.mm(ps[PA], sg_bf, g1u_bf)
            C.cp(ACT, H["g_tm"], ps[PA])
            yield
            C.act(fgl, fgl, AF.Ln)
            for h in range(4):
                C.scan(bcs[:, h * 128:(h + 1) * 128], reset, fgl[:, h * 128:(h + 1) * 128])
            C.act(eb, bcs, AF.Exp)
            C.act(enb, bcs, AF.Exp, scale=-1.0)
            bc4 = bcs.rearrange("p (a n c) -> p a n c", a=4, n=nch)
            C.tt(DVE, ebl.rearrange("p (a n c) -> p a n c", a=4, n=nch),
                 bc4[:, :, :, Cn - 1:Cn].to_broadcast([128, 4, nch, Cn]), bc4, ALU.subtract)
            C.act(ebl, ebl, AF.Exp)
            yield
            C.cp(DVE, H["decH"][:, :, 0:nch].unsqueeze(3),
                 eb.rearrange("p (a n c) -> p a n c", a=4, n=nch)[:, :, :, Cn - 1:Cn])
            C.tt(DVE, H["qTb"], v3(sq_), v3(eb), ALU.mult)
            C.tt(DVE, H["kTb"], v3(kq), v3(enb), ALU.mult)
            C.tt(DVE, khTb, v3(kq), v3(ebl), ALU.mult)
            yield
            cumS, gex, gin, ginv, glast, kkk, tq, k2 = T[2], T[3], T[4], T[5], T[6], T[7], T[0], T[1]
            for pr in range(4):
                C.scan(cumS[:, pr * 128:(pr + 1) * 128], reset, sw[:, pr * 128:(pr + 1) * 128])
            C.tt(DVE, gex, cumS, sw, ALU.subtract)
            C.act(gex, gex, AF.Exp, scale=CDEC)
            C.act(gin, cumS, AF.Exp, scale=CDEC)
            C.act(ginv, cumS, AF.Exp, scale=-CDEC)
            cs4 = cumS.rearrange("p (a n c) -> p a n c", a=4, n=nch)
            C.tt(DVE, glast.rearrange("p (a n c) -> p a n c", a=4, n=nch),
                 cs4[:, :, :, Cn - 1:Cn].to_broadcast([128, 4, nch, Cn]), cs4, ALU.subtract)
            C.act(glast, glast, AF.Exp, scale=CDEC)
            C.cp(DVE, H["gC"][:, :, 0:nch].unsqueeze(3),
                 gin.rearrange("p (a n c) -> p a n c", a=4, n=nch)[:, :, :, Cn - 1:Cn])
            yield
            C.tt(DVE, v3(kkk), k_, bc3(kk4, 128), ALU.mult)
            C.act(tqb, kkk, AF.Square)
            for pr in range(4):
                sl = slice(pr * 128, (pr + 1) * 128)
                C.mm(ps[PB_][:, sl], bdones_bf, tqb[:, sl])
            C.act(tq, ps[PB_], AF.Ln, bias=1e-24)
            C.act(tq, tq, AF.Exp, scale=-0.5)
            C.tt(DVE, kkk, kkk, tq, ALU.mult)
            C.stt(DVE, v3(k2), v3(alr), -1.0, bc3(ka4, 128), ALU.add, ALU.mult)
            C.stt(DVE, v3(k2), v3(k2), 1.0, k_, ALU.add, ALU.mult)
            C.tt(DVE, alr, kkk, alr, ALU.mult)
            b_ = alr
            yield
            C.tt(DVE, AR[:, :, 1, :], r_, v3(gin), ALU.mult)
            C.stt(DVE, AR[:, :, 0, :], v3(kkk), -1.0, v3(gex), ALU.mult, ALU.mult)
            C.tt(DVE, bT, v3(b_), v3(ginv), ALU.mult)
            C.tt(DVE, kT, v3(k2), v3(ginv), ALU.mult)
            C.tt(DVE, bhT, v3(b_), v3(glast), ALU.mult)
            C.tt(DVE, khT, v3(k2), v3(glast), ALU.mult)
            C.cp(ACT, vT, v_)
            C.tt(DVE, v3(tqb), r_, v3(k2), ALU.mult)
            for pr in range(4):
                C.mm(ps[PA][:, pr * 2:(pr + 1) * 2], tqb[:, pr * 128:(pr + 1) * 128], RKsel[:, pr, :])
            C.cp(DVE, H["bon8"], ps[PA][:, 0:8])
            yield
            for (src, bank, half) in ((lambda pr: AR[:, pr, 0, :], PA, 0), (lambda pr: bhT[:, pr, :], PA, 1),
                                      (lambda pr: khT[:, pr, :], PB_, 0), (lambda pr: vT[:, pr, :], PB_, 1)):
                for pr in range(4):
                    o0 = half * 512 + pr * 128
                    C.tr(psb(bank)[:, o0:o0 + 128], src(pr), ident_bf)
            C.cp(ACT, H["A_tm"], psb(PA)[:, 0:512]); C.cp(ACT, H["Bh_tm"], psb(PA)[:, 512:1024])
            C.cp(ACT, H["Kh_tm"], psb(PB_)[:, 0:512]); C.cp(ACT, H["V_tm"], psb(PB_)[:, 512:1024])
            yield
            for h in range(4):
                C.tr(psb(PA)[:, h * 128:(h + 1) * 128], khTb[:, h, :], ident_bf)
            C.cp(ACT, H["khat_tm"], psb(PA)[:, 0:512])
            yield

        def stageB(ti, h1_dst, sample):
            g_ = cfg(sample)
            nch, Cn, L = g_["nch"], g_["Cn"], g_["L"]
            mSb = g_["mS"].unsqueeze(1).to_broadcast([128, 4, 128])
            mIb = g_["mI"].unsqueeze(1).to_broadcast([128, 4, 128])
            mSTb = g_["mST"].unsqueeze(1).to_broadcast([128, 4, 128])
            H = HB[ti % 2]
            AR, bT, kT, A_tm, Bh_tm, Kh_tm, V_tm = H["AR"], H["bT"], H["kT"], H["A_tm"], H["Bh_tm"], H["Kh_tm"], H["V_tm"]
            qTb, kTb, khat_tm, i_tm, gs_t, g_tm, bon8, gC, decH = (H["qTb"], H["kTb"], H["khat_tm"], H["i_tm"], H["gs_t"],
                                                                    H["g_tm"], H["bon8"], H["gC"], H["decH"])
            xt = x_t[ti % 2]
            for h in range(4):
                C.mm(ps[3][:, h * 128:(h + 1) * 128], kTb[:, h, :], qTb[:, h, :])
            C.tt(DVE, attT, psv(3, 4), mIb, ALU.mult)
            if not sample:
                for h in range(4):
                    hsl = slice(h * 128, (h + 1) * 128)
                    C.mm(ps[4][:, hsl], attT[:, h, :], i_tm[:, hsl], start=True, stop=False)
                    C.mm(ps[4][:, hsl], qTb[:, h, :], SH_bf[:, h, :], start=False, stop=True)
                for h in range(4):
                    hsl = slice(h * 128, (h + 1) * 128)
                    C.mm(ps[5][:, hsl], khat_tm[:, hsl], i_tm[:, hsl])
                C.tt(DVE, SHtmp, SH, bc3(decH[:, :, 0], 128), ALU.mult)
                C.tt(DVE, SH, SHtmp, psv(5, 4), ALU.add)
                C.cp(ACT, SH_bf, SH)
            else:
                for h in range(4):
                    hsl = slice(h * 128, (h + 1) * 128)
                    C.mm(ps[4][:, hsl], attT[:, h, :], i_tm[:, hsl], start=(h == 0), stop=False)
                for b in range(NB):
                    s0 = S0h[b % 2]
                    csl = slice(b * TS, (b + 1) * TS)
                    C.dma(SP, s0, shg[b].rearrange("h k v -> k h v"))
                    C.cp(ACT, S0hb, s0)
                    eq = EQ[b % 2]
                    C.cp(POOL, eq[:, :, csl], qTb[:, :, csl])
                    for h in range(4):
                        C.mm(ps[4][:, h * 128:(h + 1) * 128], eq[:, h, :], S0hb[:, h, :], start=False, stop=False)
                    C.memset(POOL, eq[:, :, csl], 0.0)
                    C.ts(DVE, khb, khat_tm, cc("cm")[:, b:b + 1], ALU.mult)
                    bank = (5, 3)[b % 2]
                    for h in range(4):
                        hsl = slice(h * 128, (h + 1) * 128)
                        C.mm(ps[bank][:, hsl], khb[:, hsl], i_tm[:, hsl])
                    C.tt(DVE, s0, s0, bc3(decH[:, :, b], 128), ALU.mult)
                    C.tt(DVE, s0, s0, psv(bank, 4), ALU.add)
                    C.dma(SP, hgs[b].rearrange("h k v -> k h v"), s0, out_final=True)
                    yield
            osq = BT0
            C.act(osq, ps[4], AF.Square)
            C.red(DVE, s4, v3(osq))
            C.act(rr4, s4, AF.Ln, scale=1.0 / 128, bias=RMS_EPS)
            C.act(rr4, rr4, AF.Exp, scale=-0.5)
            C.tt(DVE, v3(osq), psv(4, 4), bc3(rr4, 128), ALU.mult)
            C.tt(DVE, osq, osq, hgw, ALU.mult)
            C.tt(DVE, o_all[:, 512:1024], osq, gs_t, ALU.mult)
            yield
            if sample:
                for g in range(4):
                    sa = scrA[g % 2]
                    nat = sa[0:64, :].rearrange("p (b n) -> p b n", b=4)
                    C.dma(SP, nat.rearrange("p b (h j) -> p b h j", h=8),
                          srw[g * 4:(g + 1) * 4].rearrange("b h v j -> v b h j"))
                    for bb in range(4):
                        b = g * 4 + bb
                        bank = b % 2
                        for pr in range(4):
                            C.tr(ps[bank][:, (bb % 2) * 256 + pr * 64:(bb % 2) * 256 + (pr + 1) * 64],
                                 nat[:, bb, pr * 128:(pr + 1) * 128], cc("ident")[0:64, 0:64])
                        C.cp(ACT, S0T32[:, b, :, :].k(b),
                             ps[bank][:, (bb % 2) * 256:(bb % 2) * 256 + 256].rearrange("p (a v) -> p a v", a=4))
                        C.cp(DVE, S0Tb[:, b, :, :].k(b), S0T32[:, b, :, :].k(b))
                    yield
            for hg in range(2):
                for i, h in enumerate(heads_of[hg]):
                    pr, hh = h // 2, h % 2
                    rows = slice(64 * hh, 64 * hh + 64)
                    sl = slice(i * 128, (i + 1) * 128)
                    C.mm(ps[0][:, sl], bT[rows, pr, :], AR[rows, pr, 0, :])
                    C.mm(ps[1][:, sl], bT[rows, pr, :], AR[rows, pr, 1, :])
                    C.mm(ps[2][:, sl], kT[rows, pr, :], AR[rows, pr, 0, :])
                    C.mm(ps[3][:, sl], kT[rows, pr, :], AR[rows, pr, 1, :])
                    C.mm(ps[4][:, sl], AR[rows, pr, 0, :], bT[rows, pr, :])
                hs = slice(4 * hg, 4 * hg + 4)
                C.tt(DVE, Pm[:, hs, :], psv(0, 4), mSb, ALU.mult)
                C.tt(DVE, NrbT[:, hs, :], psv(1, 4), mIb, ALU.mult)
                C.tt(DVE, AkT[:, hs, :], psv(2, 4), mSb, ALU.mult)
                C.tt(DVE, NrkT[:, hs, :], psv(3, 4), mIb, ALU.mult)
                C.tt(DVE, RR[:, hs, :], psv(4, 4), mSTb, ALU.mult)
                C.tt(DVE, Tm[:, hs, :], Pm[:, hs, :], identb4, ALU.add)
                yield
            for k in range(L):
                for hg in range(2):
                    b0 = 3 * hg
                    hs = slice(4 * hg, 4 * hg + 4)
                    for i, h in enumerate(heads_of[hg]):
                        sl = slice(i * 128, (i + 1) * 128)
                        if k < L - 1:
                            C.mm(ps[b0][:, sl], RR[:, h, :], Pm[:, h, :])
                        if k >= 1:
                            C.mm(ps[b0 + 1][:, sl], RR[:, h, :], Tm[:, h, :])
                        if k < L - 1:
                            C.mm(ps[b0 + 2][:, sl], Pm[:, h, :], RR[:, h, :])
                    if k < L - 1:
                        C.cp(ACT, Pm[:, hs, :], psv(b0, 4))
                    if k >= 1:
                        dst = TTf[:, hs, :] if k == L - 1 else Tm[:, hs, :]
                        C.tt(DVE, dst, psv(b0 + 1, 4), Tm[:, hs, :], ALU.add)
                    if k < L - 1:
                        C.cp(ACT, RR[:, hs, :], psv(b0 + 2, 4))
                    yield
            for h in range(8):
                C.mm(ps[2][:, h * 64:(h + 1) * 64], AkT[:, h, :], V_tm[:, h * 64:(h + 1) * 64])
            C.cp(ACT, Z_tm, ps[2])
            for h in range(8):
                pr = h // 2
                C.mm(ps[h // 4][:, (h % 4) * 128:(h % 4 + 1) * 128], A_tm[:, pr * 128:(pr + 1) * 128], TTf[:, h, :])
            for b in range(2):
                pv = ps[b].rearrange("p (q e t) -> p q e t", q=2, e=2)
                C.cp(ACT, W1T[0:64, 2 * b:2 * b + 2, :], pv[0:64, :, 0, :])
                C.cp(ACT, W1T[64:128, 2 * b:2 * b + 2, :], pv[64:128, :, 1, :])
            yield
            for h in range(8):
                pr, hh = h // 2, h % 2
                rows = slice(64 * hh, 64 * hh + 64)
                hsl = slice(h * 64, (h + 1) * 64)
                if not sample:
                    C.mm(ps[3][:, hsl], TTf[:, h, :], Z_tm[:, hsl], start=True, stop=False)
                    C.mm(ps[3][:, hsl], W1T[rows, pr, :], ST_bf[rows, pr, :], start=False, stop=True)
                else:
                    C.mm(ps[3][:, hsl], TTf[:, h, :], Z_tm[:, hsl], start=(h == 0), stop=False)
            if sample:
                for b in range(NB):
                    ew = EW[b % 2]
                    csl = slice(b * TS, (b + 1) * TS)
                    C.cp(POOL, ew[:, :, csl], W1T[:, :, csl])
                    for h in hh_order:
                        pr, hh = h // 2, h % 2
                        rows = slice(64 * hh, 64 * hh + 64)
                        C.mm(ps[3][:, h * 64:(h + 1) * 64], ew[rows, pr, :], S0Tb[rows, b, pr, :].k(b), start=False, stop=False)
                    C.memset(POOL, ew[:, :, csl], 0.0)
                    yield
            C.cp(ACT, U_tm, ps[3])
            for h in range(8):
                pr, hh = h // 2, h % 2
                rows = slice(64 * hh, 64 * hh + 64)
                hsl = slice(h * 64, (h + 1) * 64)
                C.mm(ps[4][:, hsl], NrbT[:, h, :], U_tm[:, hsl], start=(h == 0 or not sample), stop=False)
                C.mm(ps[4][:, hsl], NrkT[:, h, :], V_tm[:, hsl], start=False, stop=False)
                if not sample:
                    C.mm(ps[4][:, hsl], AR[rows, pr, 1, :], ST_bf[rows, pr, :], start=False, stop=True)
            yield
            if sample:
                for b in range(NB):
                    er = ER[b % 2]
                    csl = slice(b * TS, (b + 1) * TS)
                    C.cp(POOL, er[:, :, csl], AR[:, :, 1, csl])
                    for h in hh_order:
                        pr, hh = h // 2, h % 2
                        rows = slice(64 * hh, 64 * hh + 64)
                        C.mm(ps[4][:, h * 64:(h + 1) * 64], er[rows, pr, :], S0Tb[rows, b, pr, :].k(b), start=False, stop=False)
                    C.memset(POOL, er[:, :, csl], 0.0)
                    yield
                i64b = cc("i64s").unsqueeze(1).to_broadcast([128, 4, 64])
                for b in range(NB):
                    bank = b % 2
                    C.ts(DVE, Ub, U_tm, cc("cm")[:, b:b + 1], ALU.mult)
                    C.ts(DVE, Vb, V_tm, cc("cm")[:, b:b + 1], ALU.mult)
                    C.tt(DVE, Dg, i64b, bc3(gC[:, :, b], 64), ALU.mult)
                    for h in range(8):
                        hsl = slice(h * 64, (h + 1) * 64)
                        C.mm(ps[bank][0:64, hsl], Ub[:, hsl], Bh_tm[:, hsl], start=(h == 0), stop=False)
                        C.mm(ps[bank][0:64, hsl], Vb[:, hsl], Kh_tm[:, hsl], start=False, stop=False)
                    for h in hh_order:
                        pr, hh = h // 2, h % 2
                        rows = slice(64 * hh, 64 * hh + 64)
                        C.mm(ps[bank][0:64, h * 64:(h + 1) * 64], S0T32[rows, b, pr, :].k(b), Dg[rows, pr, :], start=False, stop=False)
                    C.cp(ACT, Sn[0:64, :], ps[bank][0:64, :])
                    C.dma(SP, rws[b].rearrange("h v j -> v h j"), Sn[0:64, :].rearrange("p (h j) -> p h j", h=8), out_final=True)
                    yield
            if not sample:
                for pr in range(4):
                    psl = slice(pr * 128, (pr + 1) * 128)
                    C.mm(ps[5][:, psl], Bh_tm[:, psl], U_tm[:, psl], start=True, stop=False)
                    C.mm(ps[5][:, psl], Kh_tm[:, psl], V_tm[:, psl], start=False, stop=True)
                C.tt(DVE, STtmp, ST, bc3(gC[:, :, 0], 64), ALU.mult)
                p5 = psv(5, 4)
                C.tt(DVE, ST[0:64, :, :], STtmp[0:64, :, :], p5[0:64, :, 0:64], ALU.add)
                C.tt(DVE, ST[64:128, :, :], STtmp[64:128, :, :], p5[64:128, :, 64:128], ALU.add)
                C.cp(ACT, ST_bf, ST)
            yield
            ysq, tmp2 = BT0, BT1
            y3 = ps[4].rearrange("p (h v) -> p h v", h=8)
            yv = ysq.rearrange("p (h v) -> p h v", h=8)
            C.red(DVE, st16[:, 0:8], y3)
            C.act(ysq, ps[4], AF.Square)
            C.red(DVE, st16[:, 8:16], yv)
            C.ts(DVE, m8, st16[:, 0:8], 1.0 / 64, ALU.mult)
            C.tt(DVE, r8, m8, m8, ALU.mult)
            C.stt(DVE, r8, st16[:, 8:16], 1.0 / 64, r8, ALU.mult, ALU.subtract)
            C.act(r8, r8, AF.Ln, bias=GN_EPS)
            C.act(r8, r8, AF.Exp, scale=-0.5)
            C.tt(DVE, yv, y3, bc3(m8, 64), ALU.subtract)
            C.tt(DVE, yv, yv, bc3(r8, 64), ALU.mult)
            C.tt(DVE, ysq, ysq, lnxw, ALU.mult)
            C.tt(DVE, ysq, ysq, lnxb, ALU.add)
            C.tt(DVE, tmp2.rearrange("p (h v) -> p h v", h=8), V_tm.rearrange("p (h v) -> p h v", h=8),
                 bc3(bon8, 64), ALU.mult)
            C.tt(DVE, ysq, ysq, tmp2, ALU.add)
            C.tt(DVE, o_all[:, 0:512], ysq, g_tm, ALU.mult)
            yield
            for mc in range(8):
                C.tr(psb(2)[:, mc * 128:(mc + 1) * 128], o_all[:, mc * 128:(mc + 1) * 128], ident_bf)
            C.cp(ACT, oT, psb(2).rearrange("p (a t) -> p a t", a=8))
            for half in range(2):
                for mc in range(8):
                    C.mm(ps[half], oT[:, mc, :], w_out_bf[:, mc, half * 512:(half + 1) * 512].k(mc),
                         start=(mc == 0), stop=(mc == 7))
            for half in range(2):
                hsl = slice(half * 512, (half + 1) * 512)
                C.stt(DVE, h1pre[:, hsl], xt[:, hsl], ALPHA, ps[half], ALU.mult, ALU.add)
            yield
            layernorm(h1pre, h1pre, LN_EPS)
            C.dma(SP, h1_dst, h1pre)
            yield

        def collect(g):
            C.defer = []
            for _ in g:
                pass
            lst, C.defer = C.defer, None
            return lst

        fin = {}
        eng_free = {e: 0.0 for e in ENGINES}

        def est_dur(p):
            eng, fn, outs, ins, dma, _ = p.args
            ap = _ap(outs[0])
            n = 1
            for d_ in ap.shape[1:]:
                n *= d_
            if dma:
                return 2.0
            if eng == PE:
                return 0.03 + max(n, 48) / 2400.0 * (4.0 if ap.dtype == F32 and False else 1.0)
            if eng == ACT:
                return 0.20 + n / 1200.0
            if eng == DVE:
                return 0.08 + n / 960.0
            return 0.10 + n / 500.0

        def dep_t(d, eng):
            if d.eng == PE and eng == PE:
                return fin.get(d, 0.15) - 0.15
            return fin.get(d, 0.0) + 0.25

        def est_start(p):
            eng, fn, outs, ins, dma, _ = p.args
            t = eng_free[eng]
            for x in ins:
                if x is None or _isnum(x):
                    continue
                r = C.R(x)
                if r.last_w is not None:
                    t = max(t, dep_t(r.last_w, eng))
            for x in outs:
                r = C.R(x)
                if r.last_w is not None:
                    t = max(t, dep_t(r.last_w, eng))
                for rd in r.readers:
                    t = max(t, dep_t(rd, eng))
            return t

        def commit_timed(p):
            t0 = est_start(p)
            o_ = C.commit(p)
            d_ = est_dur(p)
            fin[o_] = t0 + d_ + (0.15 if p.args[0] == PE else 0.0)
            eng_free[p.args[0]] = t0 + (d_ if p.args[0] != SP else 0.1)
            return o_

        def interleave(ga, gb):
            la, lb_ = collect(ga), collect(gb)
            ia = ib = 0
            while ia < len(la) or ib < len(lb_):
                if ia >= len(la):
                    commit_timed(lb_[ib]); ib += 1
                elif ib >= len(lb_):
                    commit_timed(la[ia]); ia += 1
                else:
                    ta, tb = est_start(la[ia]), est_start(lb_[ib])
                    if tb <= ta:
                        commit_timed(lb_[ib]); ib += 1
                    else:
                        commit_timed(la[ia]); ia += 1

        def collect(g):
            C.defer = []
            for _ in g:
                pass
            lst, C.defer = C.defer, None
            return lst

        def drain(g):
            for p in collect(g):
                commit_timed(p)

        jobs = [(ti, xp[ti * 128:(ti + 1) * 128, :], V(h1scr[ti * 128:(ti + 1) * 128, :], ("h1scr", ti)), False)
                for ti in range(NT)]
        if SAMPLE:
            jobs.append((NT, xsm, V(h1scr[NTP * 128:(NTP + 1) * 128, :], ("h1scr", NTP)), True))
        if True:
            if jobs:
                drain(stageA(jobs[0][0], jobs[0][1], jobs[0][3]))
            for n, (ti, x_src, h1_dst, smp) in enumerate(jobs):
                gb = stageB(ti, h1_dst, smp)
                if n + 1 < len(jobs):
                    nj = jobs[n + 1]
                    interleave(stageA(nj[0], nj[1], nj[3]), gb)
                else:
                    drain(gb)
                if n == NT - 1 and not smp:
                    prompt_state_outputs()
            if NT == 0:
                prompt_state_outputs()

        if True:
            P.barrier()
            arena.off = phase_mark
            w_up_bf = sb([128, 8, DFF], BF16, "w_up_bf")
            w_dn_bf = sb([128, 32, D], BF16, "w_dn_bf")
            for cb in range(8):
                C.dma(POOL, w_up_bf[:, :, cb * 512:(cb + 1) * 512].k(cb),
                      w_up[:, cb * 512:(cb + 1) * 512].rearrange("(kc p) n -> p kc n", p=128))
            C.dma(SP, lng, row(ln2_g).partition_broadcast(128))
            C.dma(SP, lnb, row(ln2_b).partition_broadcast(128))
            for fc in range(32):
                C.dma(POOL, w_dn_bf[:, fc, :].k(fc), w_down[fc * 128:(fc + 1) * 128, :])
            upT = sb([128, 32, 512], BF16, "upT")
            h1T = sb([128, 8, 512], BF16, "h1T")
            h1b = [sb([128, D], BF16, f"h1b{i}") for i in range(2)]
            h1r = [sb([128, D], F32, f"h1r{i}") for i in range(2)]
            rl = [sb([128, 512], F32, f"rl{i}") for i in range(2)]
            pre2 = sb([128, D], F32, "pre2")
            outb = [sb([128, D], F32, f"outb{i}") for i in range(2)]
            ntiles = NT + (1 if SAMPLE else 0)
            tiles = list(range(NT)) + ([NTP] if SAMPLE else [])
            groups = [tiles[i:i + 4] for i in range(0, NT, 4)]
            if SAMPLE:
                groups.append([NTP])
            gcount = 0
            tcount = 0
            for grp in groups:
                ng = len(grp)
                W = ng * 128
                for gi, tix in enumerate(grp):
                    hb = h1b[(tcount + gi) % 2]
                    C.dma(POOL, hb, V(h1scr[tix * 128:(tix + 1) * 128, :], ("h1scr", tix)))
                    bank = 6 + (gi % 2)
                    for kc in range(8):
                        C.tr(psb(bank)[:, kc * 128:(kc + 1) * 128], hb[:, kc * 128:(kc + 1) * 128], ident_bf)
                    C.cp(ACT, h1T[:, :, gi * 128:(gi + 1) * 128], psb(bank).rearrange("p (a t) -> p a t", a=8))
                for fc in range(32):
                    bank = fc % 2
                    for kc in range(8):
                        C.mm(ps[bank][:, 0:W], w_up_bf[:, kc, fc * 128:(fc + 1) * 128].k(fc // 4), h1T[:, kc, 0:W],
                             start=(kc == 0), stop=(kc == 7))
                    r = rl[fc % 2]
                    C.act(r[:, 0:W], ps[bank][:, 0:W], AF.Relu)
                    C.tt(POOL if fc % 2 else DVE, upT[:, fc, 0:W], r[:, 0:W], r[:, 0:W], ALU.mult)
                for gi, tix in enumerate(grp):
                    hr = h1r[(tcount + gi) % 2]
                    C.dma(SP, hr, V(h1scr[tix * 128:(tix + 1) * 128, :], ("h1scr", tix)))
                    for half in range(2):
                        bank = 2 + ((gi * 2 + half) % 4)
                        for fc in range(32):
                            C.mm(ps[bank], upT[:, fc, gi * 128:(gi + 1) * 128], w_dn_bf[:, fc, half * 512:(half + 1) * 512].k(fc),
                                 start=(fc == 0), stop=(fc == 31))
                        hsl = slice(half * 512, (half + 1) * 512)
                        C.stt(DVE, pre2[:, hsl], hr[:, hsl], ALPHA, ps[bank], ALU.mult, ALU.add)
                    ob = outb[(tcount + gi) % 2]
                    layernorm(pre2, ob, LN_EPS)
                    dst = ys if tix == NTP else yp[tix * 128:(tix + 1) * 128, :]
                    C.dma(SP, dst, ob, out_final=True)
                tcount += ng
                gcount += 1

        sems = {e: es.enter_context(nc.semaphore(f"s_{e}")) for e in ENGINES}
        rings = {e: [es.enter_context(nc.semaphore(f"r_{e}{i}")) for i in range(n)] for e, n in DMA_RING.items()}
        P.prepare(sems, rings)
        build.stats = dict(P.stats)
        build.arena_peak = arena.peak
        with nc.allow_low_precision(reason="bf16 matmul operands, fp32 accumulation"), \
                nc.allow_non_contiguous_dma(reason="tiny per-channel vectors / state layouts"), \
                nc.Block() as block:
            block.tensor(lambda eng: P.emit_engine(PE, eng))
            block.scalar(lambda eng: P.emit_engine(ACT, eng))
            block.vector(lambda eng: P.emit_engine(DVE, eng))
            block.gpsimd(lambda eng: P.emit_engine(POOL, eng))
            block.sync(lambda eng: P.emit_engine(SP, eng))
    return nc


IN_NAMES = ["w_in", "shift_mu", "w0", "w1u", "a0", "a1u", "g1u", "k_k", "k_a", "r_k", "ln_x_w", "ln_x_b",
            "lb_logits", "hg_norm_w", "w_out", "ln1_g", "ln1_b", "w_up", "w_down", "ln2_g", "ln2_b"]


def make_in_maps(inputs, n_cores=8):
    f = lambda a: np.ascontiguousarray(np.asarray(a, dtype=np.float32))
    shared = {}
    for k in IN_NAMES:
        a = f(inputs[k])
        a = a[0] if k != "lb_logits" else a
        if k == "r_k":
            a = a.reshape(-1)
        shared[k] = np.ascontiguousarray(a)
    shared["consts"] = CONSTS
    maps = []
    for c in range(n_cores):
        m = dict(shared)
        m["xp"] = f(inputs["x_prompt"][c])
        m["xs"] = f(inputs["x_sample"][c * NB:(c + 1) * NB]).reshape(NB * TS, D)
        m["srw"] = f(inputs["state_rwkv"][0, c * NB:(c + 1) * NB])
        m["shg"] = f(inputs["state_hgrn"][0, c * NB:(c + 1) * NB])
        m["ssh"] = f(inputs["state_shift"][0, c * NB:(c + 1) * NB])
        maps.append(m)
    return maps


_NC_CACHE = {}


def kernel(**inputs):
    if "nc" not in _NC_CACHE:
        _NC_CACHE["nc"] = build()
    nc = _NC_CACHE["nc"]
    maps = make_in_maps(inputs)
    res = run_bass_kernel_spmd(nc, maps, core_ids=list(range(8)))
    R = res.results
    y_prompt = np.stack([R[c]["yp"] for c in range(8)]).astype(np.float32)
    y_sample = np.concatenate([R[c]["ys"].reshape(NB, TS, D) for c in range(8)]).astype(np.float32)
    rw_p = np.stack([R[c]["rwp"] for c in range(8)])[None].astype(np.float32)
    rw_s = np.concatenate([R[c]["rws"] for c in range(8)])[None].astype(np.float32)
    hg_p = np.stack([R[c]["hgp"] for c in range(8)])[None].astype(np.float32)
    hg_s = np.concatenate([R[c]["hgs"] for c in range(8)])[None].astype(np.float32)
    sh_p = np.stack([R[c]["shp"] for c in range(8)])[None].astype(np.float32)
    sh_s = np.concatenate([R[c]["shs"] for c in range(8)])[None].astype(np.float32)
    return (y_prompt, y_sample, rw_p, rw_s, hg_p, hg_s, sh_p, sh_s)
```

```python
import numpy as np
import concourse.bass as bass
import concourse.mybir as mybir
from concourse.bass_utils import run_bass_kernel_spmd

F32 = mybir.dt.float32
BF16 = mybir.dt.bfloat16
AF = mybir.ActivationFunctionType
ALU = mybir.AluOpType
AX = mybir.AxisListType

DEBUG_LINES = False
LINE_OF = {}
PE, ACT, DVE, POOL, SP = "pe", "act", "dve", "pool", "sp"
ENGINES = (PE, ACT, DVE, POOL, SP)
DMA_RING = {SP: 12, ACT: 4, POOL: 8}


class Res:
    __slots__ = ("name", "last_w", "readers")

    def __init__(self, name):
        self.name = name
        self.last_w = None
        self.readers = []


class Op:
    __slots__ = ("eng", "fn", "deps", "is_dma", "signal", "idx", "dma_no", "extra_wait", "rg")

    def __init__(self, eng, fn, is_dma):
        self.eng = eng
        self.fn = fn
        self.deps = set()
        self.is_dma = is_dma
        self.signal = False
        self.idx = None
        self.dma_no = None
        self.extra_wait = None
        self.rg = None


def _pe_inorder_ok(d, o):
    return d.rg is None or o.rg is None or d.rg == o.rg


class Prog:
    def __init__(self):
        self.ops = {e: [] for e in ENGINES}
        self.order = []
        self.n_dma = {e: 0 for e in ENGINES}
        self.dma_ops = {e: [] for e in ENGINES}
        self.out_dmas = []

    def op(self, eng, fn, reads=(), writes=(), dma=False, out=False):
        o = Op(eng, fn, dma)
        for r in reads:
            if r.last_w is not None:
                o.deps.add(r.last_w)
        for w in writes:
            if w.last_w is not None:
                o.deps.add(w.last_w)
            for rd in w.readers:
                o.deps.add(rd)
        for r in reads:
            r.readers.append(o)
        for w in writes:
            w.last_w = o
            w.readers = []
        if getattr(self, "barrier_left", None) and eng in self.barrier_left:
            self.barrier_left.discard(eng)
            o.deps.update(self.pending_barrier)
        o.deps.discard(o)
        if dma:
            o.dma_no = self.n_dma[eng]
            self.n_dma[eng] += 1
            self.dma_ops[eng].append(o)
            o.signal = True
            if out:
                self.out_dmas.append(o)
        self.ops[eng].append(o)
        self.order.append(o)
        return o

    def barrier(self):
        pend = []
        for e in ENGINES:
            comp = [o for o in self.ops[e] if not o.is_dma]
            if comp:
                pend.append(comp[-1])
            pend.extend(self.dma_ops[e][-DMA_RING.get(e, 0):] if e in DMA_RING else [])
        self.pending_barrier = pend
        self.barrier_left = set(ENGINES)

    def prepare(self, sems, rings):
        for o in self.order:
            for d in o.deps:
                if d.is_dma:
                    continue
                if d.eng == o.eng and d.eng == PE and not o.is_dma and _pe_inorder_ok(d, o):
                    continue
                d.signal = True
        sig = {}
        for e in ENGINES:
            c = 0
            for o in self.ops[e]:
                if o.is_dma:
                    R = len(rings[e])
                    sig[o] = (rings[e][o.dma_no % R], 16 * (o.dma_no // R + 1))
                elif o.signal:
                    c += 1
                    sig[o] = (sems[e], c)
        self.sig = sig
        self.rings = rings
        self.stats = {e: len(self.ops[e]) for e in ENGINES}

    def emit_engine(self, e, eng):
        sig, rings = self.sig, self.rings
        waited = {}

        def wait(sem, val):
            k = id(sem)
            if waited.get(k, 0) >= val:
                return
            waited[k] = val
            eng.wait_ge(sem, val)

        for o in self.ops[e]:
            for d in o.deps:
                if d not in sig:
                    continue
                if d.eng == e and not d.is_dma and not o.is_dma and e == PE and _pe_inorder_ok(d, o):
                    continue
                s, v = sig[d]
                wait(s, v)
            if o.is_dma:
                R = len(rings[e])
                if o.dma_no >= R:
                    wait(rings[e][o.dma_no % R], 16 * (o.dma_no // R))
            ins = o.fn(eng)
            if DEBUG_LINES:
                LINE_OF[str(getattr(getattr(ins, "ins", ins), "name", ins))] = o.extra_wait
            if o in sig:
                s, v = sig[o]
                ins.then_inc(s, 16 if o.is_dma else 1)
        if e == SP:
            for q in ENGINES:
                if q not in rings:
                    continue
                for o in self.dma_ops[q][-len(rings[q]):]:
                    s, v = sig[o]
                    wait(s, v)


class V:
    __slots__ = ("ap", "key")

    def __init__(self, ap, key):
        self.ap = ap
        self.key = key

    def __getitem__(self, idx):
        return V(self.ap[idx], self.key)

    def k(self, sub):
        return V(self.ap, (self.key, sub))

    @property
    def shape(self):
        return self.ap.shape

    def rearrange(self, *a, **kw):
        return V(self.ap.rearrange(*a, **kw), self.key)

    def unsqueeze(self, ax):
        return V(self.ap.unsqueeze(ax), self.key)

    def to_broadcast(self, shape):
        return V(self.ap.to_broadcast(list(shape)), self.key)

    def bitcast(self, dt):
        return V(self.ap.bitcast(dt), self.key)


def _ap(x):
    return x.ap if isinstance(x, V) else x


def _isnum(x):
    return isinstance(x, (int, float))


class Arena:
    def __init__(self, nc, es, nbytes):
        self.n2 = nbytes // 2
        self.t = es.enter_context(nc.sbuf_tensor("arena", [128, self.n2], BF16))
        self.off = 0
        self.cnt = 0
        self.peak = 0

    def alloc(self, shape, dt, name=None):
        shape = list(shape)
        esz = 4 if dt == F32 else 2
        n = int(np.prod(shape[1:]))
        nbytes = (n * esz + 3) // 4 * 4
        o = self.off
        assert o + nbytes <= self.n2 * 2, f"arena overflow allocating {name} {shape}: {o}+{nbytes} > {self.n2 * 2}"
        self.off += nbytes
        self.peak = max(self.peak, self.off)
        ap = self.t[0:shape[0], o // 2:o // 2 + nbytes // 2]
        if esz == 4:
            ap = ap.bitcast(F32)
        ap = ap[:, 0:n]
        if len(shape) == 3:
            ap = ap.rearrange("p (a b) -> p a b", a=shape[1])
        elif len(shape) == 4:
            ap = ap.rearrange("p (a b c) -> p a b c", a=shape[1], b=shape[2])
        self.cnt += 1
        return V(ap, name or f"t{self.cnt}")


class Ctx:
    def __init__(self, nc, P, arena):
        self.nc, self.P, self.arena = nc, P, arena
        self.res = {}
        self.defer = None

    class _Pending:
        __slots__ = ("args", "rg")

        def __init__(self, args):
            self.args = args
            self.rg = None

    def commit(self, p):
        o_ = self._rec_now(*p.args)
        o_.rg = p.rg
        return o_

    def sb(self, shape, dt, name=None):
        return self.arena.alloc(shape, dt, name)

    def R(self, x):
        k = x.key if isinstance(x, V) else (x.name, None)
        r = self.res.get(k)
        if r is None:
            r = self.res[k] = Res(k)
        return r

    def rec(self, eng, fn, outs, ins, dma=False, out=False):
        if self.defer is not None:
            p = Ctx._Pending((eng, fn, list(outs), list(ins), dma, out))
            self.defer.append(p)
            return p
        return self._rec_now(eng, fn, outs, ins, dma, out)

    def _rec_now(self, eng, fn, outs, ins, dma=False, out=False):
        reads = [self.R(i) for i in ins if i is not None and not _isnum(i)]
        writes = [self.R(o) for o in outs]
        writes += [r for r in reads if isinstance(r.name, str) and r.name.startswith("ps")]
        o_ = self.P.op(eng, fn, reads=reads, writes=writes, dma=dma, out=out)
        if DEBUG_LINES:
            import sys as _s
            f = _s._getframe(1)
            while f.f_code.co_name not in ("mixer_tile", "build", "layernorm") and f.f_back is not None:
                f = f.f_back
            o_.extra_wait = f.f_lineno
        return o_

    def mm(self, out, lhsT, rhs, start=True, stop=True):
        o, l, r = _ap(out), _ap(lhsT), _ap(rhs)
        op = self.rec(PE, lambda e: e.matmul(o, lhsT=l, rhs=r, start=start, stop=stop,
                                             skip_group_check=True), [out], [lhsT, rhs])
        kr = l.shape[0]
        if kr < 128:
            op.rg = (kr, l.base_partition())
        return op

    def tr(self, out, in_, ident):
        o, i, d = _ap(out), _ap(in_), _ap(ident)
        return self.rec(PE, lambda e: e.transpose(o, i, d), [out], [in_, ident])

    def act(self, out, in_, func, bias=None, scale=1.0):
        o, i = _ap(out), _ap(in_)
        kw = {}
        if bias is not None:
            kw["bias"] = _ap(bias)
        s = _ap(scale)
        return self.rec(ACT, lambda e: e.activation(out=o, in_=i, func=func, scale=s, **kw), [out],
                        [in_, bias, scale])

    def tt(self, eng, out, a, b, op):
        o, x, y = _ap(out), _ap(a), _ap(b)
        return self.rec(eng, lambda e: e.tensor_tensor(out=o, in0=x, in1=y, op=op), [out], [a, b])

    def ts(self, eng, out, a, s1, op0, s2=None, op1=None):
        o, x, v1, v2 = _ap(out), _ap(a), _ap(s1), _ap(s2)
        if op1 is None:
            f = lambda e: e.tensor_scalar(out=o, in0=x, scalar1=v1, scalar2=None, op0=op0)
        else:
            f = lambda e: e.tensor_scalar(out=o, in0=x, scalar1=v1, scalar2=v2, op0=op0, op1=op1)
        return self.rec(eng, f, [out], [a, s1, s2])

    def stt(self, eng, out, in0, scalar, in1, op0, op1):
        o, x, y, s = _ap(out), _ap(in0), _ap(in1), _ap(scalar)
        return self.rec(eng, lambda e: e.scalar_tensor_tensor(out=o, in0=x, scalar=s, in1=y, op0=op0, op1=op1),
                        [out], [in0, in1, scalar])

    def cp(self, eng, out, in_):
        o, i = _ap(out), _ap(in_)
        if eng == ACT:
            return self.rec(ACT, lambda e: e.activation(out=o, in_=i, func=AF.Copy), [out], [in_])
        return self.rec(eng, lambda e: e.tensor_copy(out=o, in_=i), [out], [in_])

    def recip(self, out, in_):
        o, i = _ap(out), _ap(in_)
        return self.rec(DVE, lambda e: e.reciprocal(out=o, in_=i), [out], [in_])

    def scan(self, out, d0, d1):
        o, a, b = _ap(out), _ap(d0), _ap(d1)
        return self.rec(DVE, lambda e: e.tensor_tensor_scan(out=o, data0=a, data1=b, initial=0.0,
                                                            op0=ALU.mult, op1=ALU.add), [out], [d0, d1])

    def red(self, eng, out, in_, op=ALU.add):
        o, i = _ap(out), _ap(in_)
        return self.rec(eng, lambda e: e.tensor_reduce(out=o, in_=i, axis=AX.X, op=op), [out], [in_])

    def memset(self, eng, out, val):
        o = _ap(out)
        return self.rec(eng, lambda e: e.memset(o, val), [out], [])

    def dma(self, eng, out, in_, out_final=False, extra_out=()):
        o, i = _ap(out), _ap(in_)
        return self.rec(eng, lambda e: e.dma_start(out=o, in_=i), [out, *extra_out], [in_], dma=True, out=out_final)

    def bn_stats(self, out, in_):
        o, i = _ap(out), _ap(in_)
        return self.rec(DVE, lambda e: e.bn_stats(out=o, in_=i), [out], [in_])

    def bn_aggr(self, out, in_):
        o, i = _ap(out), _ap(in_)
        return self.rec(DVE, lambda e: e.bn_aggr(out=o, in_=i), [out], [in_])


D = 1024
PJ = 3840
RWP = 1792
NTP = 16
NB = 16
TS = 8
DFF = 4096
ALPHA = 2.0 ** 0.25
CDEC = -float(np.exp(-0.5))
LN_EPS = 1e-5
GN_EPS = 64e-5
RMS_EPS = 1e-6
ARENA_BYTES = 212800


def make_consts():
    s = np.arange(128)[:, None]
    t = np.arange(128)[None, :]
    cols = {}
    cols["ident"] = (s == t)
    cols["mS_p"] = (s < t)
    cols["mI_p"] = (s <= t)
    cols["mST_p"] = (t < s)
    same = (s // TS) == (t // TS)
    cols["mS_s"] = (s < t) & same
    cols["mI_s"] = (s <= t) & same
    cols["mST_s"] = (t < s) & same
    cols["reset_p"] = np.broadcast_to(t != 0, (128, 128))
    cols["reset_s"] = np.broadcast_to((t % TS) != 0, (128, 128))
    cols["bdones"] = (s // 64) == (t // 64)
    cols["hsel"] = (s // 64) == np.arange(2)[None, :]
    cols["cm"] = (s // TS) == np.arange(NB)[None, :]
    cols["i64s"] = (s % 64) == np.arange(64)[None, :]
    off = {}
    parts = []
    o = 0
    for k, v in cols.items():
        v = np.asarray(v, np.float32)
        off[k] = (o, o + v.shape[1])
        o += v.shape[1]
        parts.append(v)
    return np.ascontiguousarray(np.concatenate(parts, axis=1)), off


CONSTS, COFF = make_consts()
NCONST = CONSTS.shape[1]


class _Stop(Exception):
    pass


def build(NT=NTP, SAMPLE=True, DBG=False, STAGE=99):
    from contextlib import ExitStack
    nc = bass.Bass("TRN2", target_bir_lowering=False)

    def din(name, shape):
        return nc.dram_tensor(name, list(shape), F32, kind="ExternalInput").ap()

    def dout(name, shape):
        return nc.dram_tensor(name, list(shape), F32, kind="ExternalOutput").ap()

    xp = din("xp", [NTP * 128, D]); xsm = din("xs", [128, D])
    srw = din("srw", [NB, 8, 64, 64]); shg = din("shg", [NB, 4, 128, 128]); ssh = din("ssh", [NB, RWP])
    w_in = din("w_in", [D, PJ]); shift_mu = din("shift_mu", [RWP]); w0 = din("w0", [512])
    w1u = din("w1u", [64, 512]); a0 = din("a0", [512]); a1u = din("a1u", [64, 512]); g1u = din("g1u", [128, 512])
    k_k = din("k_k", [512]); k_a = din("k_a", [512]); r_k = din("r_k", [512])
    ln_x_w = din("ln_x_w", [512]); ln_x_b = din("ln_x_b", [512]); lb_logits = din("lb_logits", [2, 512])
    hg_norm_w = din("hg_norm_w", [512]); w_out = din("w_out", [D, D]); ln1_g = din("ln1_g", [D]); ln1_b = din("ln1_b", [D])
    w_up = din("w_up", [D, DFF]); w_down = din("w_down", [DFF, D]); ln2_g = din("ln2_g", [D]); ln2_b = din("ln2_b", [D])
    cst_d = din("consts", [128, NCONST])
    yp = dout("yp", [NTP * 128, D]); ys = dout("ys", [128, D])
    rwp = dout("rwp", [8, 64, 64]); rws = dout("rws", [NB, 8, 64, 64])
    hgp = dout("hgp", [4, 128, 128]); hgs = dout("hgs", [NB, 4, 128, 128])
    shp = dout("shp", [RWP]); shs = dout("shs", [NB, RWP])
    h1scr = nc.dram_tensor("h1scr", [(NTP + 1) * 128, D], F32).ap()
    if DBG:
        dbg_d = dout("dbg", [128, 4096])

    def row(v):
        return v.rearrange("(o n) -> o n", o=1)

    P = Prog()
    with ExitStack() as es:
        arena = Arena(nc, es, ARENA_BYTES)
        C = Ctx(nc, P, arena)
        sb = C.sb
        ps = [V(es.enter_context(nc.psum_tensor(f"ps{i}", [128, 512], F32))[:], f"ps{i}") for i in range(8)]

        def psv(i, a):
            return ps[i].rearrange("p (a t) -> p a t", a=a)

        def psb(i):
            return ps[i].bitcast(BF16)

        def bc3(ap2, n):
            return ap2.unsqueeze(2).to_broadcast([ap2.shape[0], ap2.shape[1], n])

        def v3(t):
            return t.rearrange("p (a t) -> p a t", a=4)

        ident_bf = sb([128, 128], BF16, "ident_bf")
        lng = sb([128, D], F32, "lng"); lnb = sb([128, D], F32, "lnb")
        C.dma(SP, lng, row(ln1_g).partition_broadcast(128))
        C.dma(SP, lnb, row(ln1_b).partition_broadcast(128))
        bnst = sb([128, 12], F32, "bnst"); mv = sb([128, 2], F32, "mv"); rstd1 = sb([128, 1], F32, "rstd1")
        fence_ln = sb([128, 2], F32, "fence_ln")
        if DBG:
            dbg = sb([128, 4096], F32, "dbg")
            C.memset(POOL, dbg, 0.0)
            dbg_pos = [0]
            dbg_map = {}

            def dump(name, v, n):
                a = dbg_pos[0]
                shape = list(v.shape)
                dst = dbg[0:shape[0], a:a + n]
                if len(shape) == 3:
                    dst = dst.rearrange("p (a b) -> p a b", a=shape[1])
                C.cp(POOL, dst, v)
                dbg_pos[0] += n
                dbg_map[name] = (a, n, shape)
            build.dbg_map = dbg_map
        else:
            def dump(name, v, n):
                return None
        phase_mark = arena.off

        def layernorm(src, dst, eps):
            for half in range(2):
                C.bn_stats(bnst[:, half * 6:(half + 1) * 6], src[:, half * 512:(half + 1) * 512])
            C.bn_aggr(mv, bnst)
            C.act(rstd1, mv[:, 1:2], AF.Ln, bias=eps)
            C.act(rstd1, rstd1, AF.Exp, scale=-0.5)
            C.ts(DVE, dst, src, mv[:, 0:1], ALU.subtract, rstd1[:, 0:1], ALU.mult)
            halves = []
            for eng_, hsl in ((DVE, slice(0, 512)), (DVE, slice(512, 1024))):
                dk = dst[:, hsl].k(hsl.start)
                halves.append(dk)
                o, a_, g_, b_ = _ap(dk), _ap(dst[:, hsl]), _ap(lng[:, hsl]), _ap(lnb[:, hsl])
                C.rec(eng_, lambda e, o=o, a_=a_, g_=g_: e.tensor_tensor(out=o, in0=a_, in1=g_, op=ALU.mult), [dk], [dst, lng])
                C.rec(eng_, lambda e, o=o, b_=b_: e.tensor_tensor(out=o, in0=o, in1=b_, op=ALU.add), [dk], [dk, lnb])
            C.rec(POOL, lambda e: e.memset(_ap(fence_ln), 0.0), [dst, fence_ln], halves)

        F32_CONSTS = ("ident", "reset_p", "reset_s", "hsel", "cm", "i64s")
        BF_CONSTS = ("mS_p", "mI_p", "mST_p", "mS_s", "mI_s", "mST_s", "bdones")
        cviews = {}
        nf = sum(COFF[k][1] - COFF[k][0] for k in F32_CONSTS)
        nb_ = sum(COFF[k][1] - COFF[k][0] for k in BF_CONSTS)
        cstf = sb([128, nf], F32, "cstf"); cstb = sb([128, nb_], BF16, "cstb")
        o_ = 0
        for k_ in F32_CONSTS:
            a, b = COFF[k_]
            C.dma(SP, cstf[:, o_:o_ + b - a].k(k_), cst_d[:, a:b])
            cviews[k_] = cstf[:, o_:o_ + b - a].k(k_)
            o_ += b - a
        o_ = 0
        for k_ in BF_CONSTS:
            a, b = COFF[k_]
            C.dma(POOL, cstb[:, o_:o_ + b - a].k(k_), cst_d[:, a:b])
            cviews[k_] = cstb[:, o_:o_ + b - a].k(k_)
            o_ += b - a

        def cc(name):
            return cviews[name]

        C.cp(POOL, ident_bf, cc("ident"))
        bdones_bf = cc("bdones")
        mu14 = sb([128, 14], F32, "mu14")
        kk4 = sb([128, 4], F32, "kk4"); ka4 = sb([128, 4], F32, "ka4"); rk4 = sb([128, 4], F32, "rk4")
        w04 = sb([128, 4], F32, "w04"); a04 = sb([128, 4], F32, "a04")
        lbl = sb([128, 2, 4], F32, "lbl")
        with nc.allow_non_contiguous_dma(reason="tiny per-channel parameter vectors"):
            C.dma(SP, mu14, shift_mu.rearrange("(c p) -> p c", p=128))
            C.dma(SP, kk4, k_k.rearrange("(c p) -> p c", p=128))
            C.dma(SP, ka4, k_a.rearrange("(c p) -> p c", p=128))
            C.dma(SP, rk4, r_k.rearrange("(c p) -> p c", p=128))
            C.dma(SP, w04, w0.rearrange("(c p) -> p c", p=128))
            C.dma(SP, a04, a0.rearrange("(c p) -> p c", p=128))
            C.dma(SP, lbl, lb_logits.rearrange("l (c p) -> p l c", p=128))
        WA = sb([128, 512], BF16, "WA")
        C.dma(POOL, WA[0:64, :].k("lo"), w1u)
        C.dma(POOL, WA[64:128, :].k("hi"), a1u)
        g1u_bf = sb([128, 512], BF16, "g1u_bf")
        C.dma(POOL, g1u_bf, g1u)
        lnxw = sb([128, 512], BF16, "lnxw"); lnxb = sb([128, 512], BF16, "lnxb"); hgw = sb([128, 512], BF16, "hgw")
        C.dma(POOL, lnxw, row(ln_x_w).partition_broadcast(128))
        C.dma(POOL, lnxb, row(ln_x_b).partition_broadcast(128))
        C.dma(POOL, hgw, row(hg_norm_w).partition_broadcast(128))
        lb4 = sb([128, 4], F32, "lb4"); oml4 = sb([128, 4], F32, "oml4"); etmp = sb([128, 4], F32, "etmp")
        C.tt(DVE, etmp, lbl[:, 1, :], lbl[:, 0, :], ALU.subtract)
        C.act(etmp, etmp, AF.Exp)
        C.ts(DVE, lb4, etmp, 1.0, ALU.add)
        C.recip(lb4, lb4)
        C.tt(DVE, oml4, etmp, lb4, ALU.mult)
        RKsel = sb([128, 4, 2], BF16, "RKsel")
        C.tt(DVE, RKsel, bc3(rk4, 2), cc("hsel").unsqueeze(1).to_broadcast([128, 4, 2]), ALU.mult)

        ov_lo = arena.off
        w_in_bf = sb([128, 8, PJ], BF16, "w_in_bf")
        ov_hi = arena.off
        for kc in range(8):
            C.dma(POOL, w_in_bf[:, kc, :].k(kc), w_in[kc * 128:(kc + 1) * 128, :])
        w_out_bf = sb([128, 8, D], BF16, "w_out_bf")
        for kc in range(8):
            C.dma(POOL, w_out_bf[:, kc, :].k(kc), w_out[kc * 128:(kc + 1) * 128, :])

        ST = sb([128, 4, 64], F32, "ST"); ST_bf = sb([128, 4, 64], BF16, "ST_bf")
        SH = sb([128, 4, 128], F32, "SH"); SH_bf = sb([128, 4, 128], BF16, "SH_bf")
        plast = sb([128, 14], F32, "plast")
        for t_ in (ST, ST_bf, SH, SH_bf, plast):
            C.memset(POOL, t_, 0.0)

        def prompt_state_outputs():
            with nc.allow_non_contiguous_dma(reason="tiny state vector"):
                C.dma(SP, shp.rearrange("(c p) -> p c", p=128), plast, out_final=True)
            C.dma(SP, hgp.rearrange("h k v -> k h v"), SH, out_final=True)
            identf = cc("ident")
            for pr in range(4):
                C.tr(ps[2][0:64, pr * 128:(pr + 1) * 128], ST[:, pr, :], identf)
            rwo = T[0]
            C.cp(DVE, rwo[0:64, :], ps[2][0:64, :])
            C.dma(SP, rwp.rearrange("h v j -> v h j"), rwo[0:64, :].rearrange("p (h j) -> p h j", h=8), out_final=True)
            if DBG:
                C.dma(SP, dbg_d, dbg, out_final=True)


        x_t = [sb([128, D], F32, f"x_t{i}") for i in range(2)]
        x_bf = sb([128, D], BF16, "x_bf")
        xT = sb([128, 8, 128], BF16, "xT")
        pr_ = sb([128, 14, 129], F32, "pr")
        xs = sb([128, 14, 128], F32, "xsft")
        T = [sb([128, 512], F32, f"T{i}") for i in range(10)]
        z12 = sb([128, 128], BF16, "z12"); sg_bf = sb([128, 128], BF16, "sg_bf"); tqb = sb([128, 512], BF16, "tqb")
        bhT = sb([128, 4, 128], BF16, "bhT"); khT = sb([128, 4, 128], BF16, "khT"); vT = sb([128, 4, 128], BF16, "vT")
        khTb = sb([128, 4, 128], BF16, "khTb")
        fence2 = sb([128, 2], F32, "fence2")
        HB = []
        for i in range(2):
            HB.append(dict(
                AR=sb([128, 4, 2, 128], BF16, f"AR{i}"), bT=sb([128, 4, 128], BF16, f"bT{i}"), kT=sb([128, 4, 128], BF16, f"kT{i}"),
                A_tm=sb([128, 512], BF16, f"A_tm{i}"), Bh_tm=sb([128, 512], BF16, f"Bh_tm{i}"),
                Kh_tm=sb([128, 512], BF16, f"Kh_tm{i}"), V_tm=sb([128, 512], BF16, f"V_tm{i}"),
                qTb=sb([128, 4, 128], BF16, f"qTb{i}"), kTb=sb([128, 4, 128], BF16, f"kTb{i}"),
                khat_tm=sb([128, 512], BF16, f"khat_tm{i}"), i_tm=sb([128, 512], BF16, f"i_tm{i}"),
                gs_t=sb([128, 512], BF16, f"gs_t{i}"), g_tm=sb([128, 512], BF16, f"g_tm{i}"),
                bon8=sb([128, 8], F32, f"bon8{i}"), gC=sb([128, 4, NB], F32, f"gC{i}"), decH=sb([128, 4, NB], F32, f"decH{i}")))
        Pm = sb([128, 8, 128], BF16, "Pm"); Tm = sb([128, 8, 128], BF16, "Tm"); RR = sb([128, 8, 128], BF16, "RR")
        NrbT = sb([128, 8, 128], BF16, "NrbT"); AkT = sb([128, 8, 128], BF16, "AkT"); NrkT = sb([128, 8, 128], BF16, "NrkT")
        TTf = sb([128, 8, 128], BF16, "TTf")
        W1T = sb([128, 4, 128], BF16, "W1T")
        Z_tm = sb([128, 512], BF16, "Z_tm"); U_tm = sb([128, 512], BF16, "U_tm")
        st16 = sb([128, 16], F32, "st16"); m8 = sb([128, 8], F32, "m8"); r8 = sb([128, 8], F32, "r8")
        o_all = sb([128, D], BF16, "o_all"); oT = sb([128, 8, 128], BF16, "oT")
        h1pre = sb([128, D], F32, "h1pre")
        attT = sb([128, 4, 128], BF16, "attT")
        s4 = sb([128, 4], F32, "s4"); rr4 = sb([128, 4], F32, "rr4")
        BT0 = h1pre[:, 0:512]; BT1 = h1pre[:, 512:1024]
        SHtmp = v3(BT1)
        STtmp = BT1[:, 0:256].rearrange("p (a v) -> p a v", a=4)
        identb4 = ident_bf.unsqueeze(1).to_broadcast([128, 4, 128])
        save_off = arena.off
        arena.off = ov_lo
        scrA = [sb([128, 2048], F32, f"scrA{i}") for i in range(2)]
        S0T32 = sb([128, NB, 4, 64], F32, "S0T32")
        S0Tb = sb([128, NB, 4, 64], BF16, "S0Tb")
        sshT = sb([128, 14, NB], F32, "sshT"); lastp = sb([128, 14, NB], F32, "lastp")
        EW = [sb([128, 4, 128], BF16, f"EW{i}") for i in range(2)]
        ER = [sb([128, 4, 128], BF16, f"ER{i}") for i in range(2)]
        EQ = [sb([128, 4, 128], BF16, f"EQ{i}") for i in range(2)]
        Ub = sb([128, 512], BF16, "Ub"); Vb = sb([128, 512], BF16, "Vb"); khb = sb([128, 512], BF16, "khb")
        Dg = sb([128, 4, 64], F32, "Dg")
        S0h = [sb([128, 4, 128], F32, f"S0h{i}") for i in range(2)]
        S0hb = sb([128, 4, 128], BF16, "S0hb")
        Sn = sb([128, 512], F32, "Sn")
        fence_t = sb([128, 2], F32, "fence_t")
        assert arena.off <= ov_hi, (arena.off, ov_hi)
        ov_bufs = scrA + [S0T32, S0Tb, sshT, lastp] + EW + ER + EQ + [Ub, Vb, khb, Dg] + S0h + [S0hb, Sn, fence_t]
        arena.off = save_off
        ssh_tm = scrA[0][0:NB, 0:RWP]
        hh_order = (0, 2, 4, 6, 1, 3, 5, 7)
        heads_of = [(0, 1, 2, 3), (4, 5, 6, 7)]
        PA, PB_ = 6, 7

        cb = cc

        def cfg(sample):
            sfx = "_s" if sample else "_p"
            return dict(nch=NB if sample else 1, Cn=TS if sample else 128, L=3 if sample else 7,
                        mS=cb("mS" + sfx), mI=cb("mI" + sfx), mST=cb("mST" + sfx), reset=cc("reset" + sfx))

        def stageA(ti, x_src, sample):
            g_ = cfg(sample)
            nch, Cn, reset = g_["nch"], g_["Cn"], g_["reset"]
            H = HB[ti % 2]
            AR, bT, kT = H["AR"], H["bT"], H["kT"]
            xt = x_t[ti % 2]
            C.dma(SP, xt, x_src)
            C.dma(POOL, x_bf, x_src)
            for kc in range(8):
                C.tr(psb(PB_)[:, kc * 128:(kc + 1) * 128], x_bf[:, kc * 128:(kc + 1) * 128], ident_bf)
            C.cp(ACT, xT, psb(PB_).rearrange("p (a t) -> p a t", a=8))
            yield
            sig, kq, fgl, bcs, eb, enb, ebl, sq_, eg = T[1], T[4], T[0], T[2], T[3], T[5], T[6], T[7], T[8]

            def proj_fm(c0, n, bank):
                for j in range(n):
                    c = c0 + j
                    for kc in range(8):
                        C.mm(ps[bank][:, j * 128:(j + 1) * 128], w_in_bf[:, kc, c * 128:(c + 1) * 128].k(kc),
                             xT[:, kc, :], start=(kc == 0), stop=(kc == 7))

            def proj_tm(col0, bank):
                for kc in range(8):
                    C.mm(ps[bank], xT[:, kc, :], w_in_bf[:, kc, col0:col0 + 512].k(kc), start=(kc == 0), stop=(kc == 7))

            proj_fm(0, 4, PA); C.cp(ACT, pr_[:, 0:4, 1:129], psv(PA, 4)); yield
            proj_fm(4, 4, PB_); C.cp(ACT, pr_[:, 4:8, 1:129], psv(PB_, 4)); yield
            proj_fm(8, 4, PA); C.cp(ACT, pr_[:, 8:12, 1:129], psv(PA, 4)); yield
            proj_fm(12, 2, PB_); C.cp(ACT, pr_[:, 12:14, 1:129], psv(PB_, 4)[:, 0:2, :]); yield
            prev, cur = pr_[:, :, 0:128], pr_[:, :, 1:129]
            if not sample:
                C.cp(DVE, pr_[:, :, 0:1], plast.unsqueeze(2))
                C.cp(DVE, plast.unsqueeze(2), pr_[:, :, 128:129])
            else:
                C.memset(POOL, pr_[:, :, 0:1], 0.0)
            proj_fm(14, 4, PA)
            C.act(sq_, ps[PA], AF.Sigmoid)
            C.tt(DVE, sq_, ps[PA], sq_, ALU.mult)
            yield
            proj_fm(18, 4, PB_)
            C.act(sig, ps[PB_], AF.Sigmoid)
            yield
            proj_tm(RWP + 1024, PA)
            C.cp(ACT, H["i_tm"], ps[PA])
            yield
            proj_tm(RWP + 1536, PB_)
            C.act(eg, ps[PB_], AF.Sigmoid)
            C.tt(DVE, H["gs_t"], ps[PB_], eg, ALU.mult)
            yield
            if sample:
                C.rec(POOL, lambda e: e.memset(_ap(fence_t), 0.0),
                      [w_in_bf[:, kc, :].k(kc) for kc in range(8)] + ov_bufs
                      + [S0T32[:, b, :, :].k(b) for b in range(NB)] + [S0Tb[:, b, :, :].k(b) for b in range(NB)], [])
                for t_ in EW + ER + EQ:
                    C.memset(POOL, t_, 0.0)
                C.dma(SP, ssh_tm, ssh)
                for c in range(14):
                    C.tr(ps[PA][:, c * NB:(c + 1) * NB], ssh_tm[:, c * 128:(c + 1) * 128], cc("ident")[0:NB, 0:NB])
                C.cp(DVE, sshT, ps[PA][:, 0:14 * NB].rearrange("p (c b) -> p c b", c=14))
            for eng_, c0, c1 in ((DVE, 0, 8), (DVE, 8, 14)):
                C.tt(eng_, xs[:, c0:c1, :].k(c0), prev[:, c0:c1, :], cur[:, c0:c1, :], ALU.subtract)
                C.tt(eng_, xs[:, c0:c1, :].k(c0), xs[:, c0:c1, :].k(c0), bc3(mu14[:, c0:c1], 128), ALU.mult)
                C.tt(eng_, xs[:, c0:c1, :].k(c0), xs[:, c0:c1, :].k(c0), cur[:, c0:c1, :], ALU.add)
            C.rec(POOL, lambda e: e.memset(_ap(fence2), 0.0), [xs, fence2], [xs[:, 0:8, :].k(0), xs[:, 8:14, :].k(8)])
            if sample:
                cur4 = cur.rearrange("p c (b t) -> p c b t", t=TS)
                xs4 = xs.rearrange("p c (b t) -> p c b t", t=TS)
                cur0, xs0 = cur4[:, :, :, 0], xs4[:, :, :, 0]
                C.tt(DVE, xs0, sshT, cur0, ALU.subtract)
                C.tt(DVE, xs0, xs0, bc3(mu14, NB), ALU.mult)
                C.tt(DVE, xs0, xs0, cur0, ALU.add)
                C.cp(DVE, lastp, cur4[:, :, :, TS - 1])
                for g0 in range(0, 14, 4):
                    bk = PA if (g0 // 4) % 2 == 0 else PB_
                    n = min(4, 14 - g0)
                    for j in range(n):
                        C.tr(ps[bk][0:NB, j * 128:(j + 1) * 128], lastp[:, g0 + j, :], cc("ident"))
                    C.cp(ACT, ssh_tm[:, g0 * 128:(g0 + n) * 128], ps[bk][0:NB, 0:n * 128])
                C.dma(SP, shs, ssh_tm, out_final=True)
            yield
            r_ = xs[:, 0:4, :]; k_ = xs[:, 4:8, :]; v_ = xs[:, 8:12, :]
            C.act(z12[0:64, :], xs[0:64, 12, :], AF.Tanh)
            C.act(sg_bf, xs[:, 13, :], AF.Sigmoid)
            C.cp(DVE, z12[64:128, :], xs[64:128, 12, :])
            for pr in range(4):
                sl = slice(pr * 128, (pr + 1) * 128)
                C.mm(ps[PA][:, sl], WA[0:64, sl].k("lo"), z12[0:64, :])
            for pr in range(4):
                sl = slice(pr * 128, (pr + 1) * 128)
                C.mm(ps[PB_][:, sl], WA[64:128, sl].k("hi"), z12[64:128, :])
            sw, alr = T[9], T[8]
            for pr in range(4):
                sl = slice(pr * 128, (pr + 1) * 128)
                C.act(sw[:, sl], ps[PA][:, sl], AF.Sigmoid, bias=w04[:, pr:pr + 1])
            C.tt(DVE, v3(kq), v3(sig), bc3(oml4, 128), ALU.mult)
            C.tt(DVE, v3(fgl), v3(kq), bc3(lb4, 128), ALU.add)
            C.tt(DVE, v3(kq), bc3(oml4, 128), v3(kq), ALU.subtract)
            yield
            for pr in range(4):
                sl = slice(pr * 128, (pr + 1) * 128)
                C.act(alr[:, sl], ps[PB_][:, sl], AF.Sigmoid, bias=a04[:, pr:pr + 1])
            C.mm(ps[PA], sg_bf, g1u_bf)
            C.cp(ACT, H["g_tm"], ps[PA])
            yield
            C.act(fgl, fgl, AF.Ln)
            for h in range(4):
                C.scan(bcs[:, h * 128:(h + 1) * 128], reset, fgl[:, h * 128:(h + 1) * 128])
            C.act(eb, bcs, AF.Exp)
            C.act(enb, bcs, AF.Exp, scale=-1.0)
            bc4 = bcs.rearrange("p (a n c) -> p a n c", a=4, n=nch)
            C.tt(DVE, ebl.rearrange("p (a n c) -> p a n c", a=4, n=nch),
                 bc4[:, :, :, Cn - 1:Cn].to_broadcast([128, 4, nch, Cn]), bc4, ALU.subtract)
            C.act(ebl, ebl, AF.Exp)
            yield
            C.cp(DVE, H["decH"][:, :, 0:nch].unsqueeze(3),
                 eb.rearrange("p (a n c) -> p a n c", a=4, n=nch)[:, :, :, Cn - 1:Cn])
            C.tt(DVE, H["qTb"], v3(sq_), v3(eb), ALU.mult)
            C.tt(DVE, H["kTb"], v3(kq), v3(enb), ALU.mult)
            C.tt(DVE, khTb, v3(kq), v3(ebl), ALU.mult)
            yield
            cumS, gex, gin, ginv, glast, kkk, tq, k2 = T[2], T[3], T[4], T[5], T[6], T[7], T[0], T[1]
            for pr in range(4):
                C.scan(cumS[:, pr * 128:(pr + 1) * 128], reset, sw[:, pr * 128:(pr + 1) * 128])
            C.tt(DVE, gex, cumS, sw, ALU.subtract)
            C.act(gex, gex, AF.Exp, scale=CDEC)
            C.act(gin, cumS, AF.Exp, scale=CDEC)
            C.act(ginv, cumS, AF.Exp, scale=-CDEC)
            cs4 = cumS.rearrange("p (a n c) -> p a n c", a=4, n=nch)
            C.tt(DVE, glast.rearrange("p (a n c) -> p a n c", a=4, n=nch),
                 cs4[:, :, :, Cn - 1:Cn].to_broadcast([128, 4, nch, Cn]), cs4, ALU.subtract)
            C.act(glast, glast, AF.Exp, scale=CDEC)
            C.cp(DVE, H["gC"][:, :, 0:nch].unsqueeze(3),
                 gin.rearrange("p (a n c) -> p a n c", a=4, n=nch)[:, :, :, Cn - 1:Cn])
            yield
            C.tt(DVE, v3(kkk), k_, bc3(kk4, 128), ALU.mult)
            C.act(tqb, kkk, AF.Square)
            for pr in range(4):
                sl = slice(pr * 128, (pr + 1) * 128)
                C.mm(ps[PB_][:, sl], bdones_bf, tqb[:, sl])
            C.act(tq, ps[PB_], AF.Ln, bias=1e-24)
            C.act(tq, tq, AF.Exp, scale=-0.5)
            C.tt(DVE, kkk, kkk, tq, ALU.mult)
            C.stt(DVE, v3(k2), v3(alr), -1.0, bc3(ka4, 128), ALU.add, ALU.mult)
            C.stt(DVE, v3(k2), v3(k2), 1.0, k_, ALU.add, ALU.mult)
            C.tt(DVE, alr, kkk, alr, ALU.mult)
            b_ = alr
            yield
            C.tt(DVE, AR[:, :, 1, :], r_, v3(gin), ALU.mult)
            C.stt(DVE, AR[:, :, 0, :], v3(kkk), -1.0, v3(gex), ALU.mult, ALU.mult)
            C.tt(DVE, bT, v3(b_), v3(ginv), ALU.mult)
            C.tt(DVE, kT, v3(k2), v3(ginv), ALU.mult)
            C.tt(DVE, bhT, v3(b_), v3(glast), ALU.mult)
            C.tt(DVE, khT, v3(k2), v3(glast), ALU.mult)
            C.cp(ACT, vT, v_)
            C.tt(DVE, v3(tqb), r_, v3(k2), ALU.mult)
            for pr in range(4):
                C.mm(ps[PA][:, pr * 2:(pr + 1) * 2], tqb[:, pr * 128:(pr + 1) * 128], RKsel[:, pr, :])
            C.cp(DVE, H["bon8"], ps[PA][:, 0:8])
            yield
            for (src, bank, half) in ((lambda pr: AR[:, pr, 0, :], PA, 0), (lambda pr: bhT[:, pr, :], PA, 1),
                                      (lambda pr: khT[:, pr, :], PB_, 0), (lambda pr: vT[:, pr, :], PB_, 1)):
                for pr in range(4):
                    o0 = half * 512 + pr * 128
                    C.tr(psb(bank)[:, o0:o0 + 128], src(pr), ident_bf)
            C.cp(ACT, H["A_tm"], psb(PA)[:, 0:512]); C.cp(ACT, H["Bh_tm"], psb(PA)[:, 512:1024])
            C.cp(ACT, H["Kh_tm"], psb(PB_)[:, 0:512]); C.cp(ACT, H["V_tm"], psb(PB_)[:, 512:1024])
            yield
            for h in range(4):
                C.tr(psb(PA)[:, h * 128:(h + 1) * 128], khTb[:, h, :], ident_bf)
            C.cp(ACT, H["khat_tm"], psb(PA)[:, 0:512])
            yield

        def stageB(ti, h1_dst, sample, part="all"):
            g_ = cfg(sample)
            nch, Cn, L = g_["nch"], g_["Cn"], g_["L"]
            mSb = g_["mS"].unsqueeze(1).to_broadcast([128, 4, 128])
            mIb = g_["mI"].unsqueeze(1).to_broadcast([128, 4, 128])
            mSTb = g_["mST"].unsqueeze(1).to_broadcast([128, 4, 128])
            H = HB[ti % 2]
            AR, bT, kT, A_tm, Bh_tm, Kh_tm, V_tm = H["AR"], H["bT"], H["kT"], H["A_tm"], H["Bh_tm"], H["Kh_tm"], H["V_tm"]
            qTb, kTb, khat_tm, i_tm, gs_t, g_tm, bon8, gC, decH = (H["qTb"], H["kTb"], H["khat_tm"], H["i_tm"], H["gs_t"],
                                                                    H["g_tm"], H["bon8"], H["gC"], H["decH"])
            xt = x_t[ti % 2]
            if part in ("all", "hgrn"):
                bA, bO = (6, 7) if sample else (3, 4)
                for h in range(4):
                    C.mm(ps[bA][:, h * 128:(h + 1) * 128], kTb[:, h, :], qTb[:, h, :])
                C.tt(DVE, attT, psv(bA, 4), mIb, ALU.mult)
                if not sample:
                    for h in range(4):
                        hsl = slice(h * 128, (h + 1) * 128)
                        C.mm(ps[4][:, hsl], attT[:, h, :], i_tm[:, hsl], start=True, stop=False)
                        C.mm(ps[4][:, hsl], qTb[:, h, :], SH_bf[:, h, :], start=False, stop=True)
                    for h in range(4):
                        hsl = slice(h * 128, (h + 1) * 128)
                        C.mm(ps[5][:, hsl], khat_tm[:, hsl], i_tm[:, hsl])
                    C.tt(DVE, SHtmp, SH, bc3(decH[:, :, 0], 128), ALU.mult)
                    C.tt(DVE, SH, SHtmp, psv(5, 4), ALU.add)
                    C.cp(ACT, SH_bf, SH)
                else:
                    for h in range(4):
                        hsl = slice(h * 128, (h + 1) * 128)
                        C.mm(ps[bO][:, hsl], attT[:, h, :], i_tm[:, hsl], start=(h == 0), stop=False)
                    for b in range(NB):
                        s0 = S0h[b % 2]
                        csl = slice(b * TS, (b + 1) * TS)
                        C.dma(SP, s0, shg[b].rearrange("h k v -> k h v"))
                        C.cp(ACT, S0hb, s0)
                        eq = EQ[b % 2]
                        C.cp(POOL, eq[:, :, csl], qTb[:, :, csl])
                        for h in range(4):
                            C.mm(ps[bO][:, h * 128:(h + 1) * 128], eq[:, h, :], S0hb[:, h, :], start=False, stop=False)
                        C.memset(POOL, eq[:, :, csl], 0.0)
                        C.ts(DVE, khb, khat_tm, cc("cm")[:, b:b + 1], ALU.mult)
                        bank = bA
                        for h in range(4):
                            hsl = slice(h * 128, (h + 1) * 128)
                            C.mm(ps[bank][:, hsl], khb[:, hsl], i_tm[:, hsl])
                        C.tt(DVE, s0, s0, bc3(decH[:, :, b], 128), ALU.mult)
                        C.tt(DVE, s0, s0, psv(bank, 4), ALU.add)
                        C.dma(SP, hgs[b].rearrange("h k v -> k h v"), s0, out_final=True)
                        yield
                osq = T[0] if sample else BT0
                C.act(osq, ps[bO], AF.Square)
                C.red(DVE, s4, v3(osq))
                C.act(rr4, s4, AF.Ln, scale=1.0 / 128, bias=RMS_EPS)
                C.act(rr4, rr4, AF.Exp, scale=-0.5)
                C.tt(DVE, v3(osq), psv(bO, 4), bc3(rr4, 128), ALU.mult)
                C.tt(DVE, osq, osq, hgw, ALU.mult)
                C.tt(DVE, o_all[:, 512:1024], osq, gs_t, ALU.mult)
                yield
            if part == "hgrn":
                return
            if part in ("all", "rwkv"):
                if sample:
                    for g in range(4):
                        sa = scrA[g % 2]
                        nat = sa[0:64, :].rearrange("p (b n) -> p b n", b=4)
                        C.dma(SP, nat.rearrange("p b (h j) -> p b h j", h=8),
                              srw[g * 4:(g + 1) * 4].rearrange("b h v j -> v b h j"))
                        for bb in range(4):
                            b = g * 4 + bb
                            bank = b % 2
                            for pr in range(4):
                                C.tr(ps[bank][:, (bb % 2) * 256 + pr * 64:(bb % 2) * 256 + (pr + 1) * 64],
                                     nat[:, bb, pr * 128:(pr + 1) * 128], cc("ident")[0:64, 0:64])
                            C.cp(ACT, S0T32[:, b, :, :].k(b),
                                 ps[bank][:, (bb % 2) * 256:(bb % 2) * 256 + 256].rearrange("p (a v) -> p a v", a=4))
                            C.cp(DVE, S0Tb[:, b, :, :].k(b), S0T32[:, b, :, :].k(b))
                        yield
                for hg in range(2):
                    for i, h in enumerate(heads_of[hg]):
                        pr, hh = h // 2, h % 2
                        rows = slice(64 * hh, 64 * hh + 64)
                        sl = slice(i * 128, (i + 1) * 128)
                        C.mm(ps[0][:, sl], bT[rows, pr, :], AR[rows, pr, 0, :])
                        C.mm(ps[1][:, sl], bT[rows, pr, :], AR[rows, pr, 1, :])
                        C.mm(ps[2][:, sl], kT[rows, pr, :], AR[rows, pr, 0, :])
                        C.mm(ps[3][:, sl], kT[rows, pr, :], AR[rows, pr, 1, :])
                        C.mm(ps[4][:, sl], AR[rows, pr, 0, :], bT[rows, pr, :])
                    hs = slice(4 * hg, 4 * hg + 4)
                    C.tt(DVE, Pm[:, hs, :], psv(0, 4), mSb, ALU.mult)
                    C.tt(DVE, NrbT[:, hs, :], psv(1, 4), mIb, ALU.mult)
                    C.tt(DVE, AkT[:, hs, :], psv(2, 4), mSb, ALU.mult)
                    C.tt(DVE, NrkT[:, hs, :], psv(3, 4), mIb, ALU.mult)
                    C.tt(DVE, RR[:, hs, :], psv(4, 4), mSTb, ALU.mult)
                    C.tt(DVE, Tm[:, hs, :], Pm[:, hs, :], identb4, ALU.add)
                    yield
                for k in range(L):
                    for hg in range(2):
                        b0 = 3 * hg
                        hs = slice(4 * hg, 4 * hg + 4)
                        for i, h in enumerate(heads_of[hg]):
                            sl = slice(i * 128, (i + 1) * 128)
                            if k < L - 1:
                                C.mm(ps[b0][:, sl], RR[:, h, :], Pm[:, h, :])
                            if k >= 1:
                                C.mm(ps[b0 + 1][:, sl], RR[:, h, :], Tm[:, h, :])
                            if k < L - 1:
                                C.mm(ps[b0 + 2][:, sl], Pm[:, h, :], RR[:, h, :])
                        if k < L - 1:
                            C.cp(ACT, Pm[:, hs, :], psv(b0, 4))
                        if k >= 1:
                            dst = TTf[:, hs, :] if k == L - 1 else Tm[:, hs, :]
                            C.tt(DVE, dst, psv(b0 + 1, 4), Tm[:, hs, :], ALU.add)
                        if k < L - 1:
                            C.cp(ACT, RR[:, hs, :], psv(b0 + 2, 4))
                        yield
                for h in range(8):
                    C.mm(ps[2][:, h * 64:(h + 1) * 64], AkT[:, h, :], V_tm[:, h * 64:(h + 1) * 64])
                C.cp(ACT, Z_tm, ps[2])
                for h in range(8):
                    pr = h // 2
                    C.mm(ps[h // 4][:, (h % 4) * 128:(h % 4 + 1) * 128], A_tm[:, pr * 128:(pr + 1) * 128], TTf[:, h, :])
                for b in range(2):
                    pv = ps[b].rearrange("p (q e t) -> p q e t", q=2, e=2)
                    C.cp(ACT, W1T[0:64, 2 * b:2 * b + 2, :], pv[0:64, :, 0, :])
                    C.cp(ACT, W1T[64:128, 2 * b:2 * b + 2, :], pv[64:128, :, 1, :])
                yield
                for h in range(8):
                    pr, hh = h // 2, h % 2
                    rows = slice(64 * hh, 64 * hh + 64)
                    hsl = slice(h * 64, (h + 1) * 64)
                    if not sample:
                        C.mm(ps[3][:, hsl], TTf[:, h, :], Z_tm[:, hsl], start=True, stop=False)
                        C.mm(ps[3][:, hsl], W1T[rows, pr, :], ST_bf[rows, pr, :], start=False, stop=True)
                    else:
                        C.mm(ps[3][:, hsl], TTf[:, h, :], Z_tm[:, hsl], start=(h == 0), stop=False)
                if sample:
                    for b in range(NB):
                        ew = EW[b % 2]
                        csl = slice(b * TS, (b + 1) * TS)
                        C.cp(POOL, ew[:, :, csl], W1T[:, :, csl])
                        for h in hh_order:
                            pr, hh = h // 2, h % 2
                            rows = slice(64 * hh, 64 * hh + 64)
                            C.mm(ps[3][:, h * 64:(h + 1) * 64], ew[rows, pr, :], S0Tb[rows, b, pr, :].k(b), start=False, stop=False)
                        C.memset(POOL, ew[:, :, csl], 0.0)
                        yield
                C.cp(ACT, U_tm, ps[3])
                for h in range(8):
                    pr, hh = h // 2, h % 2
                    rows = slice(64 * hh, 64 * hh + 64)
                    hsl = slice(h * 64, (h + 1) * 64)
                    C.mm(ps[4][:, hsl], NrbT[:, h, :], U_tm[:, hsl], start=(h == 0 or not sample), stop=False)
                    C.mm(ps[4][:, hsl], NrkT[:, h, :], V_tm[:, hsl], start=False, stop=False)
                    if not sample:
                        C.mm(ps[4][:, hsl], AR[rows, pr, 1, :], ST_bf[rows, pr, :], start=False, stop=True)
                yield
                if sample:
                    for b in range(NB):
                        er = ER[b % 2]
                        csl = slice(b * TS, (b + 1) * TS)
                        C.cp(POOL, er[:, :, csl], AR[:, :, 1, csl])
                        for h in hh_order:
                            pr, hh = h // 2, h % 2
                            rows = slice(64 * hh, 64 * hh + 64)
                            C.mm(ps[4][:, h * 64:(h + 1) * 64], er[rows, pr, :], S0Tb[rows, b, pr, :].k(b), start=False, stop=False)
                        C.memset(POOL, er[:, :, csl], 0.0)
                        yield
                    i64b = cc("i64s").unsqueeze(1).to_broadcast([128, 4, 64])
                    for b in range(NB):
                        bank = b % 2
                        C.ts(DVE, Ub, U_tm, cc("cm")[:, b:b + 1], ALU.mult)
                        C.ts(DVE, Vb, V_tm, cc("cm")[:, b:b + 1], ALU.mult)
                        C.tt(DVE, Dg, i64b, bc3(gC[:, :, b], 64), ALU.mult)
                        for h in range(8):
                            hsl = slice(h * 64, (h + 1) * 64)
                            C.mm(ps[bank][0:64, hsl], Ub[:, hsl], Bh_tm[:, hsl], start=(h == 0), stop=False)
                            C.mm(ps[bank][0:64, hsl], Vb[:, hsl], Kh_tm[:, hsl], start=False, stop=False)
                        for h in hh_order:
                            pr, hh = h // 2, h % 2
                            rows = slice(64 * hh, 64 * hh + 64)
                            C.mm(ps[bank][0:64, h * 64:(h + 1) * 64], S0T32[rows, b, pr, :].k(b), Dg[rows, pr, :], start=False, stop=False)
                        C.cp(ACT, Sn[0:64, :], ps[bank][0:64, :])
                        C.dma(SP, rws[b].rearrange("h v j -> v h j"), Sn[0:64, :].rearrange("p (h j) -> p h j", h=8), out_final=True)
                        yield
                if not sample:
                    for pr in range(4):
                        psl = slice(pr * 128, (pr + 1) * 128)
                        C.mm(ps[5][:, psl], Bh_tm[:, psl], U_tm[:, psl], start=True, stop=False)
                        C.mm(ps[5][:, psl], Kh_tm[:, psl], V_tm[:, psl], start=False, stop=True)
                    C.tt(DVE, STtmp, ST, bc3(gC[:, :, 0], 64), ALU.mult)
                    p5 = psv(5, 4)
                    C.tt(DVE, ST[0:64, :, :], STtmp[0:64, :, :], p5[0:64, :, 0:64], ALU.add)
                    C.tt(DVE, ST[64:128, :, :], STtmp[64:128, :, :], p5[64:128, :, 64:128], ALU.add)
                    C.cp(ACT, ST_bf, ST)
                yield
                ysq, tmp2 = BT0, BT1
                y3 = ps[4].rearrange("p (h v) -> p h v", h=8)
                yv = ysq.rearrange("p (h v) -> p h v", h=8)
                C.red(DVE, st16[:, 0:8], y3)
                C.act(ysq, ps[4], AF.Square)
                C.red(DVE, st16[:, 8:16], yv)
                C.ts(DVE, m8, st16[:, 0:8], 1.0 / 64, ALU.mult)
                C.tt(DVE, r8, m8, m8, ALU.mult)
                C.stt(DVE, r8, st16[:, 8:16], 1.0 / 64, r8, ALU.mult, ALU.subtract)
                C.act(r8, r8, AF.Ln, bias=GN_EPS)
                C.act(r8, r8, AF.Exp, scale=-0.5)
                C.tt(DVE, yv, y3, bc3(m8, 64), ALU.subtract)
                C.tt(DVE, yv, yv, bc3(r8, 64), ALU.mult)
                C.tt(DVE, ysq, ysq, lnxw, ALU.mult)
                C.tt(DVE, ysq, ysq, lnxb, ALU.add)
                C.tt(DVE, tmp2.rearrange("p (h v) -> p h v", h=8), V_tm.rearrange("p (h v) -> p h v", h=8),
                     bc3(bon8, 64), ALU.mult)
                C.tt(DVE, ysq, ysq, tmp2, ALU.add)
                C.tt(DVE, o_all[:, 0:512], ysq, g_tm, ALU.mult)
                yield
            if part == "rwkv":
                return
            for mc in range(8):
                C.tr(psb(2)[:, mc * 128:(mc + 1) * 128], o_all[:, mc * 128:(mc + 1) * 128], ident_bf)
            C.cp(ACT, oT, psb(2).rearrange("p (a t) -> p a t", a=8))
            for half in range(2):
                for mc in range(8):
                    C.mm(ps[half], oT[:, mc, :], w_out_bf[:, mc, half * 512:(half + 1) * 512].k(mc),
                         start=(mc == 0), stop=(mc == 7))
            for half in range(2):
                hsl = slice(half * 512, (half + 1) * 512)
                C.stt(DVE, h1pre[:, hsl], xt[:, hsl], ALPHA, ps[half], ALU.mult, ALU.add)
            yield
            layernorm(h1pre, h1pre, LN_EPS)
            C.dma(SP, h1_dst, h1pre)
            yield

        def collect(g):
            C.defer = []
            for _ in g:
                pass
            lst, C.defer = C.defer, None
            return lst

        fin = {}
        eng_free = {e: 0.0 for e in ENGINES}

        def est_dur(p):
            eng, fn, outs, ins, dma, _ = p.args
            ap = _ap(outs[0])
            n = 1
            for d_ in ap.shape[1:]:
                n *= d_
            if dma:
                return 2.0
            if eng == PE:
                return 0.03 + max(n, 48) / 2400.0 * (4.0 if ap.dtype == F32 and False else 1.0)
            if eng == ACT:
                return 0.20 + n / 1200.0
            if eng == DVE:
                return 0.08 + n / 960.0
            return 0.10 + n / 500.0

        def dep_t(d, eng):
            if d.eng == PE and eng == PE:
                return fin.get(d, 0.15) - 0.15
            return fin.get(d, 0.0) + 0.25

        def est_start(p):
            eng, fn, outs, ins, dma, _ = p.args
            t = eng_free[eng]
            for x in ins:
                if x is None or _isnum(x):
                    continue
                r = C.R(x)
                if r.last_w is not None:
                    t = max(t, dep_t(r.last_w, eng))
            for x in outs:
                r = C.R(x)
                if r.last_w is not None:
                    t = max(t, dep_t(r.last_w, eng))
                for rd in r.readers:
                    t = max(t, dep_t(rd, eng))
            return t

        def commit_timed(p):
            t0 = est_start(p)
            o_ = C.commit(p)
            d_ = est_dur(p)
            fin[o_] = t0 + d_ + (0.15 if p.args[0] == PE else 0.0)
            eng_free[p.args[0]] = t0 + (d_ if p.args[0] != SP else 0.1)
            return o_

        def interleave(ga, gb):
            la, lb_ = collect(ga), collect(gb)
            ia = ib = 0
            while ia < len(la) or ib < len(lb_):
                if ia >= len(la):
                    commit_timed(lb_[ib]); ib += 1
                elif ib >= len(lb_):
                    commit_timed(la[ia]); ia += 1
                else:
                    ta, tb = est_start(la[ia]), est_start(lb_[ib])
                    if tb <= ta:
                        commit_timed(lb_[ib]); ib += 1
                    else:
                        commit_timed(la[ia]); ia += 1

        def collect(g):
            C.defer = []
            for _ in g:
                pass
            lst, C.defer = C.defer, None
            return lst

        def drain(g):
            for p in collect(g):
                commit_timed(p)

        jobs = [(ti, xp[ti * 128:(ti + 1) * 128, :], V(h1scr[ti * 128:(ti + 1) * 128, :], ("h1scr", ti)), False)
                for ti in range(NT)]
        if SAMPLE:
            jobs.append((NT, xsm, V(h1scr[NTP * 128:(NTP + 1) * 128, :], ("h1scr", NTP)), True))
        if True:
            if jobs:
                drain(stageA(jobs[0][0], jobs[0][1], jobs[0][3]))
            for n, (ti, x_src, h1_dst, smp) in enumerate(jobs):
                if n + 1 < len(jobs):
                    nj = jobs[n + 1]
                    interleave(stageA(nj[0], nj[1], nj[3]), stageB(ti, h1_dst, smp))
                elif smp:
                    interleave(stageB(ti, h1_dst, smp, "hgrn"), stageB(ti, h1_dst, smp, "rwkv"))
                    drain(stageB(ti, h1_dst, smp, "final"))
                else:
                    drain(stageB(ti, h1_dst, smp))
                if n == NT - 1 and not smp:
                    prompt_state_outputs()
            if NT == 0:
                prompt_state_outputs()

        if True:
            P.barrier()
            arena.off = phase_mark
            w_up_bf = sb([128, 8, DFF], BF16, "w_up_bf")
            w_dn_bf = sb([128, 32, D], BF16, "w_dn_bf")
            for cb in range(8):
                C.dma(POOL, w_up_bf[:, :, cb * 512:(cb + 1) * 512].k(cb),
                      w_up[:, cb * 512:(cb + 1) * 512].rearrange("(kc p) n -> p kc n", p=128))
            C.dma(SP, lng, row(ln2_g).partition_broadcast(128))
            C.dma(SP, lnb, row(ln2_b).partition_broadcast(128))
            for fc in range(32):
                C.dma(POOL, w_dn_bf[:, fc, :].k(fc), w_down[fc * 128:(fc + 1) * 128, :])
            upT = sb([128, 32, 512], BF16, "upT")
            h1T = sb([128, 8, 512], BF16, "h1T")
            h1b = [sb([128, D], BF16, f"h1b{i}") for i in range(2)]
            h1r = [sb([128, D], F32, f"h1r{i}") for i in range(2)]
            rl = [sb([128, 512], F32, f"rl{i}") for i in range(2)]
            pre2 = sb([128, D], F32, "pre2")
            outb = [sb([128, D], F32, f"outb{i}") for i in range(2)]
            ntiles = NT + (1 if SAMPLE else 0)
            tiles = list(range(NT)) + ([NTP] if SAMPLE else [])
            groups = [tiles[i:i + 4] for i in range(0, NT, 4)]
            if SAMPLE:
                groups.append([NTP])
            gcount = 0
            tcount = 0
            for grp in groups:
                ng = len(grp)
                W = ng * 128
                for gi, tix in enumerate(grp):
                    hb = h1b[(tcount + gi) % 2]
                    C.dma(POOL, hb, V(h1scr[tix * 128:(tix + 1) * 128, :], ("h1scr", tix)))
                    bank = 6 + (gi % 2)
                    for kc in range(8):
                        C.tr(psb(bank)[:, kc * 128:(kc + 1) * 128], hb[:, kc * 128:(kc + 1) * 128], ident_bf)
                    C.cp(ACT, h1T[:, :, gi * 128:(gi + 1) * 128], psb(bank).rearrange("p (a t) -> p a t", a=8))
                for fc in range(32):
                    bank = fc % 2
                    for kc in range(8):
                        C.mm(ps[bank][:, 0:W], w_up_bf[:, kc, fc * 128:(fc + 1) * 128].k(fc // 4), h1T[:, kc, 0:W],
                             start=(kc == 0), stop=(kc == 7))
                    r = rl[fc % 2]
                    C.act(r[:, 0:W], ps[bank][:, 0:W], AF.Relu)
                    C.tt(POOL if fc % 2 else DVE, upT[:, fc, 0:W], r[:, 0:W], r[:, 0:W], ALU.mult)
                for gi, tix in enumerate(grp):
                    hr = h1r[(tcount + gi) % 2]
                    C.dma(SP, hr, V(h1scr[tix * 128:(tix + 1) * 128, :], ("h1scr", tix)))
                    for half in range(2):
                        bank = 2 + ((gi * 2 + half) % 4)
                        for fc in range(32):
                            C.mm(ps[bank], upT[:, fc, gi * 128:(gi + 1) * 128], w_dn_bf[:, fc, half * 512:(half + 1) * 512].k(fc),
                                 start=(fc == 0), stop=(fc == 31))
                        hsl = slice(half * 512, (half + 1) * 512)
                        C.stt(DVE, pre2[:, hsl], hr[:, hsl], ALPHA, ps[bank], ALU.mult, ALU.add)
                    ob = outb[(tcount + gi) % 2]
                    layernorm(pre2, ob, LN_EPS)
                    dst = ys if tix == NTP else yp[tix * 128:(tix + 1) * 128, :]
                    C.dma(SP, dst, ob, out_final=True)
                tcount += ng
                gcount += 1

        sems = {e: es.enter_context(nc.semaphore(f"s_{e}")) for e in ENGINES}
        rings = {e: [es.enter_context(nc.semaphore(f"r_{e}{i}")) for i in range(n)] for e, n in DMA_RING.items()}
        P.prepare(sems, rings)
        build.stats = dict(P.stats)
        build.arena_peak = arena.peak
        with nc.allow_low_precision(reason="bf16 matmul operands, fp32 accumulation"), \
                nc.allow_non_contiguous_dma(reason="tiny per-channel vectors / state layouts"), \
                nc.Block() as block:
            block.tensor(lambda eng: P.emit_engine(PE, eng))
            block.scalar(lambda eng: P.emit_engine(ACT, eng))
            block.vector(lambda eng: P.emit_engine(DVE, eng))
            block.gpsimd(lambda eng: P.emit_engine(POOL, eng))
            block.sync(lambda eng: P.emit_engine(SP, eng))
    return nc


IN_NAMES = ["w_in", "shift_mu", "w0", "w1u", "a0", "a1u", "g1u", "k_k", "k_a", "r_k", "ln_x_w", "ln_x_b",
            "lb_logits", "hg_norm_w", "w_out", "ln1_g", "ln1_b", "w_up", "w_down", "ln2_g", "ln2_b"]


def make_in_maps(inputs, n_cores=8):
    f = lambda a: np.ascontiguousarray(np.asarray(a, dtype=np.float32))
    shared = {}
    for k in IN_NAMES:
        a = f(inputs[k])
        a = a[0] if k != "lb_logits" else a
        if k == "r_k":
            a = a.reshape(-1)
        shared[k] = np.ascontiguousarray(a)
    shared["consts"] = CONSTS
    maps = []
    for c in range(n_cores):
        m = dict(shared)
        m["xp"] = f(inputs["x_prompt"][c])
        m["xs"] = f(inputs["x_sample"][c * NB:(c + 1) * NB]).reshape(NB * TS, D)
        m["srw"] = f(inputs["state_rwkv"][0, c * NB:(c + 1) * NB])
        m["shg"] = f(inputs["state_hgrn"][0, c * NB:(c + 1) * NB])
        m["ssh"] = f(inputs["state_shift"][0, c * NB:(c + 1) * NB])
        maps.append(m)
    return maps


_NC_CACHE = {}


def kernel(**inputs):
    if "nc" not in _NC_CACHE:
        _NC_CACHE["nc"] = build()
    nc = _NC_CACHE["nc"]
    maps = make_in_maps(inputs)
    res = run_bass_kernel_spmd(nc, maps, core_ids=list(range(8)))
    R = res.results
    y_prompt = np.stack([R[c]["yp"] for c in range(8)]).astype(np.float32)
    y_sample = np.concatenate([R[c]["ys"].reshape(NB, TS, D) for c in range(8)]).astype(np.float32)
    rw_p = np.stack([R[c]["rwp"] for c in range(8)])[None].astype(np.float32)
    rw_s = np.concatenate([R[c]["rws"] for c in range(8)])[None].astype(np.float32)
    hg_p = np.stack([R[c]["hgp"] for c in range(8)])[None].astype(np.float32)
    hg_s = np.concatenate([R[c]["hgs"] for c in range(8)])[None].astype(np.float32)
    sh_p = np.stack([R[c]["shp"] for c in range(8)])[None].astype(np.float32)
    sh_s = np.concatenate([R[c]["shs"] for c in range(8)])[None].astype(np.float32)
    return (y_prompt, y_sample, rw_p, rw_s, hg_p, hg_s, sh_p, sh_s)
```

```python
import numpy as np
import concourse.bass as bass
import concourse.mybir as mybir
from concourse.bass_utils import run_bass_kernel_spmd

F32 = mybir.dt.float32
BF16 = mybir.dt.bfloat16
AF = mybir.ActivationFunctionType
ALU = mybir.AluOpType
AX = mybir.AxisListType

DEBUG_LINES = False
LINE_OF = {}
PE, ACT, DVE, POOL, SP = "pe", "act", "dve", "pool", "sp"
ENGINES = (PE, ACT, DVE, POOL, SP)
DMA_RING = {SP: 12, ACT: 4, POOL: 8}


class Res:
    __slots__ = ("name", "last_w", "readers")

    def __init__(self, name):
        self.name = name
        self.last_w = None
        self.readers = []


class Op:
    __slots__ = ("eng", "fn", "deps", "is_dma", "signal", "idx", "dma_no", "extra_wait", "rg")

    def __init__(self, eng, fn, is_dma):
        self.eng = eng
        self.fn = fn
        self.deps = set()
        self.is_dma = is_dma
        self.signal = False
        self.idx = None
        self.dma_no = None
        self.extra_wait = None
        self.rg = None


def _pe_inorder_ok(d, o):
    return d.rg is None or o.rg is None or d.rg == o.rg


class Prog:
    def __init__(self):
        self.ops = {e: [] for e in ENGINES}
        self.order = []
        self.n_dma = {e: 0 for e in ENGINES}
        self.dma_ops = {e: [] for e in ENGINES}
        self.out_dmas = []

    def op(self, eng, fn, reads=(), writes=(), dma=False, out=False):
        o = Op(eng, fn, dma)
        for r in reads:
            if r.last_w is not None:
                o.deps.add(r.last_w)
        for w in writes:
            if w.last_w is not None:
                o.deps.add(w.last_w)
            for rd in w.readers:
                o.deps.add(rd)
        for r in reads:
            r.readers.append(o)
        for w in writes:
            w.last_w = o
            w.readers = []
        if getattr(self, "barrier_left", None) and eng in self.barrier_left:
            self.barrier_left.discard(eng)
            o.deps.update(self.pending_barrier)
        o.deps.discard(o)
        if dma:
            o.dma_no = self.n_dma[eng]
            self.n_dma[eng] += 1
            self.dma_ops[eng].append(o)
            o.signal = True
            if out:
                self.out_dmas.append(o)
        self.ops[eng].append(o)
        self.order.append(o)
        return o

    def barrier(self):
        pend = []
        for e in ENGINES:
            comp = [o for o in self.ops[e] if not o.is_dma]
            if comp:
                pend.append(comp[-1])
            pend.extend(self.dma_ops[e][-DMA_RING.get(e, 0):] if e in DMA_RING else [])
        self.pending_barrier = pend
        self.barrier_left = set(ENGINES)

    def prepare(self, sems, rings):
        for o in self.order:
            for d in o.deps:
                if d.is_dma:
                    continue
                if d.eng == o.eng and d.eng == PE and not o.is_dma and _pe_inorder_ok(d, o):
                    continue
                d.signal = True
        sig = {}
        for e in ENGINES:
            c = 0
            for o in self.ops[e]:
                if o.is_dma:
                    R = len(rings[e])
                    sig[o] = (rings[e][o.dma_no % R], 16 * (o.dma_no // R + 1))
                elif o.signal:
                    c += 1
                    sig[o] = (sems[e], c)
        self.sig = sig
        self.rings = rings
        self.stats = {e: len(self.ops[e]) for e in ENGINES}

    def emit_engine(self, e, eng):
        sig, rings = self.sig, self.rings
        waited = {}

        def wait(sem, val):
            k = id(sem)
            if waited.get(k, 0) >= val:
                return
            waited[k] = val
            eng.wait_ge(sem, val)

        for o in self.ops[e]:
            for d in o.deps:
                if d not in sig:
                    continue
                if d.eng == e and not d.is_dma and not o.is_dma and e == PE and _pe_inorder_ok(d, o):
                    continue
                s, v = sig[d]
                wait(s, v)
            if o.is_dma:
                R = len(rings[e])
                if o.dma_no >= R:
                    wait(rings[e][o.dma_no % R], 16 * (o.dma_no // R))
            ins = o.fn(eng)
            if DEBUG_LINES:
                LINE_OF[str(getattr(getattr(ins, "ins", ins), "name", ins))] = o.extra_wait
            if o in sig:
                s, v = sig[o]
                ins.then_inc(s, 16 if o.is_dma else 1)
        if e == SP:
            for q in ENGINES:
                if q not in rings:
                    continue
                for o in self.dma_ops[q][-len(rings[q]):]:
                    s, v = sig[o]
                    wait(s, v)


class V:
    __slots__ = ("ap", "key")

    def __init__(self, ap, key):
        self.ap = ap
        self.key = key

    def __getitem__(self, idx):
        return V(self.ap[idx], self.key)

    def k(self, sub):
        return V(self.ap, (self.key, sub))

    @property
    def shape(self):
        return self.ap.shape

    def rearrange(self, *a, **kw):
        return V(self.ap.rearrange(*a, **kw), self.key)

    def unsqueeze(self, ax):
        return V(self.ap.unsqueeze(ax), self.key)

    def to_broadcast(self, shape):
        return V(self.ap.to_broadcast(list(shape)), self.key)

    def bitcast(self, dt):
        return V(self.ap.bitcast(dt), self.key)


def _ap(x):
    return x.ap if isinstance(x, V) else x


def _isnum(x):
    return isinstance(x, (int, float))


class Arena:
    def __init__(self, nc, es, nbytes):
        self.n2 = nbytes // 2
        self.t = es.enter_context(nc.sbuf_tensor("arena", [128, self.n2], BF16))
        self.off = 0
        self.cnt = 0
        self.peak = 0

    def alloc(self, shape, dt, name=None):
        shape = list(shape)
        esz = 4 if dt == F32 else 2
        n = int(np.prod(shape[1:]))
        nbytes = (n * esz + 3) // 4 * 4
        o = self.off
        assert o + nbytes <= self.n2 * 2, f"arena overflow allocating {name} {shape}: {o}+{nbytes} > {self.n2 * 2}"
        self.off += nbytes
        self.peak = max(self.peak, self.off)
        ap = self.t[0:shape[0], o // 2:o // 2 + nbytes // 2]
        if esz == 4:
            ap = ap.bitcast(F32)
        ap = ap[:, 0:n]
        if len(shape) == 3:
            ap = ap.rearrange("p (a b) -> p a b", a=shape[1])
        elif len(shape) == 4:
            ap = ap.rearrange("p (a b c) -> p a b c", a=shape[1], b=shape[2])
        self.cnt += 1
        return V(ap, name or f"t{self.cnt}")


class Ctx:
    def __init__(self, nc, P, arena):
        self.nc, self.P, self.arena = nc, P, arena
        self.res = {}
        self.defer = None

    class _Pending:
        __slots__ = ("args", "rg")

        def __init__(self, args):
            self.args = args
            self.rg = None

    def commit(self, p):
        o_ = self._rec_now(*p.args)
        o_.rg = p.rg
        return o_

    def sb(self, shape, dt, name=None):
        return self.arena.alloc(shape, dt, name)

    def R(self, x):
        k = x.key if isinstance(x, V) else (x.name, None)
        r = self.res.get(k)
        if r is None:
            r = self.res[k] = Res(k)
        return r

    def rec(self, eng, fn, outs, ins, dma=False, out=False):
        if self.defer is not None:
            p = Ctx._Pending((eng, fn, list(outs), list(ins), dma, out))
            self.defer.append(p)
            return p
        return self._rec_now(eng, fn, outs, ins, dma, out)

    def _rec_now(self, eng, fn, outs, ins, dma=False, out=False):
        reads = [self.R(i) for i in ins if i is not None and not _isnum(i)]
        writes = [self.R(o) for o in outs]
        writes += [r for r in reads if isinstance(r.name, str) and r.name.startswith("ps")]
        o_ = self.P.op(eng, fn, reads=reads, writes=writes, dma=dma, out=out)
        if DEBUG_LINES:
            import sys as _s
            f = _s._getframe(1)
            while f.f_code.co_name not in ("mixer_tile", "build", "layernorm") and f.f_back is not None:
                f = f.f_back
            o_.extra_wait = f.f_lineno
        return o_

    def mm(self, out, lhsT, rhs, start=True, stop=True):
        o, l, r = _ap(out), _ap(lhsT), _ap(rhs)
        op = self.rec(PE, lambda e: e.matmul(o, lhsT=l, rhs=r, start=start, stop=stop,
                                             skip_group_check=True), [out], [lhsT, rhs])
        kr = l.shape[0]
        if kr < 128:
            op.rg = (kr, l.base_partition())
        return op

    def tr(self, out, in_, ident):
        o, i, d = _ap(out), _ap(in_), _ap(ident)
        return self.rec(PE, lambda e: e.transpose(o, i, d), [out], [in_, ident])

    def act(self, out, in_, func, bias=None, scale=1.0):
        o, i = _ap(out), _ap(in_)
        kw = {}
        if bias is not None:
            kw["bias"] = _ap(bias)
        s = _ap(scale)
        return self.rec(ACT, lambda e: e.activation(out=o, in_=i, func=func, scale=s, **kw), [out],
                        [in_, bias, scale])

    def tt(self, eng, out, a, b, op):
        o, x, y = _ap(out), _ap(a), _ap(b)
        return self.rec(eng, lambda e: e.tensor_tensor(out=o, in0=x, in1=y, op=op), [out], [a, b])

    def ts(self, eng, out, a, s1, op0, s2=None, op1=None):
        o, x, v1, v2 = _ap(out), _ap(a), _ap(s1), _ap(s2)
        if op1 is None:
            f = lambda e: e.tensor_scalar(out=o, in0=x, scalar1=v1, scalar2=None, op0=op0)
        else:
            f = lambda e: e.tensor_scalar(out=o, in0=x, scalar1=v1, scalar2=v2, op0=op0, op1=op1)
        return self.rec(eng, f, [out], [a, s1, s2])

    def stt(self, eng, out, in0, scalar, in1, op0, op1):
        o, x, y, s = _ap(out), _ap(in0), _ap(in1), _ap(scalar)
        return self.rec(eng, lambda e: e.scalar_tensor_tensor(out=o, in0=x, scalar=s, in1=y, op0=op0, op1=op1),
                        [out], [in0, in1, scalar])

    def cp(self, eng, out, in_):
        o, i = _ap(out), _ap(in_)
        if eng == ACT:
            return self.rec(ACT, lambda e: e.activation(out=o, in_=i, func=AF.Copy), [out], [in_])
        return self.rec(eng, lambda e: e.tensor_copy(out=o, in_=i), [out], [in_])

    def recip(self, out, in_):
        o, i = _ap(out), _ap(in_)
        return self.rec(DVE, lambda e: e.reciprocal(out=o, in_=i), [out], [in_])

    def scan(self, out, d0, d1):
        o, a, b = _ap(out), _ap(d0), _ap(d1)
        return self.rec(DVE, lambda e: e.tensor_tensor_scan(out=o, data0=a, data1=b, initial=0.0,
                                                            op0=ALU.mult, op1=ALU.add), [out], [d0, d1])

    def red(self, eng, out, in_, op=ALU.add):
        o, i = _ap(out), _ap(in_)
        return self.rec(eng, lambda e: e.tensor_reduce(out=o, in_=i, axis=AX.X, op=op), [out], [in_])

    def memset(self, eng, out, val):
        o = _ap(out)
        return self.rec(eng, lambda e: e.memset(o, val), [out], [])

    def dma(self, eng, out, in_, out_final=False, extra_out=()):
        o, i = _ap(out), _ap(in_)
        return self.rec(eng, lambda e: e.dma_start(out=o, in_=i), [out, *extra_out], [in_], dma=True, out=out_final)

    def bn_stats(self, out, in_):
        o, i = _ap(out), _ap(in_)
        return self.rec(DVE, lambda e: e.bn_stats(out=o, in_=i), [out], [in_])

    def bn_aggr(self, out, in_):
        o, i = _ap(out), _ap(in_)
        return self.rec(DVE, lambda e: e.bn_aggr(out=o, in_=i), [out], [in_])


D = 1024
PJ = 3840
RWP = 1792
NTP = 16
NB = 16
TS = 8
DFF = 4096
ALPHA = 2.0 ** 0.25
CDEC = -float(np.exp(-0.5))
LN_EPS = 1e-5
GN_EPS = 64e-5
RMS_EPS = 1e-6
ARENA_BYTES = 212800


def make_consts():
    s = np.arange(128)[:, None]
    t = np.arange(128)[None, :]
    cols = {}
    cols["ident"] = (s == t)
    cols["mS_p"] = (s < t)
    cols["mI_p"] = (s <= t)
    cols["mST_p"] = (t < s)
    same = (s // TS) == (t // TS)
    cols["mS_s"] = (s < t) & same
    cols["mI_s"] = (s <= t) & same
    cols["mST_s"] = (t < s) & same
    cols["reset_p"] = np.broadcast_to(t != 0, (128, 128))
    cols["reset_s"] = np.broadcast_to((t % TS) != 0, (128, 128))
    cols["bdones"] = (s // 64) == (t // 64)
    cols["hsel"] = (s // 64) == np.arange(2)[None, :]
    cols["cm"] = (s // TS) == np.arange(NB)[None, :]
    cols["i64s"] = (s % 64) == np.arange(64)[None, :]
    off = {}
    parts = []
    o = 0
    for k, v in cols.items():
        v = np.asarray(v, np.float32)
        off[k] = (o, o + v.shape[1])
        o += v.shape[1]
        parts.append(v)
    return np.ascontiguousarray(np.concatenate(parts, axis=1)), off


CONSTS, COFF = make_consts()
NCONST = CONSTS.shape[1]


class _Stop(Exception):
    pass


def build(NT=NTP, SAMPLE=True, DBG=False, STAGE=99):
    from contextlib import ExitStack
    nc = bass.Bass("TRN2", target_bir_lowering=False)

    def din(name, shape):
        return nc.dram_tensor(name, list(shape), F32, kind="ExternalInput").ap()

    def dout(name, shape):
        return nc.dram_tensor(name, list(shape), F32, kind="ExternalOutput").ap()

    xp = din("xp", [NTP * 128, D]); xsm = din("xs", [128, D])
    srw = din("srw", [NB, 8, 64, 64]); shg = din("shg", [NB, 4, 128, 128]); ssh = din("ssh", [NB, RWP])
    w_in = din("w_in", [D, PJ]); shift_mu = din("shift_mu", [RWP]); w0 = din("w0", [512])
    w1u = din("w1u", [64, 512]); a0 = din("a0", [512]); a1u = din("a1u", [64, 512]); g1u = din("g1u", [128, 512])
    k_k = din("k_k", [512]); k_a = din("k_a", [512]); r_k = din("r_k", [512])
    ln_x_w = din("ln_x_w", [512]); ln_x_b = din("ln_x_b", [512]); lb_logits = din("lb_logits", [2, 512])
    hg_norm_w = din("hg_norm_w", [512]); w_out = din("w_out", [D, D]); ln1_g = din("ln1_g", [D]); ln1_b = din("ln1_b", [D])
    w_up = din("w_up", [D, DFF]); w_down = din("w_down", [DFF, D]); ln2_g = din("ln2_g", [D]); ln2_b = din("ln2_b", [D])
    cst_d = din("consts", [128, NCONST])
    yp = dout("yp", [NTP * 128, D]); ys = dout("ys", [128, D])
    rwp = dout("rwp", [8, 64, 64]); rws = dout("rws", [NB, 8, 64, 64])
    hgp = dout("hgp", [4, 128, 128]); hgs = dout("hgs", [NB, 4, 128, 128])
    shp = dout("shp", [RWP]); shs = dout("shs", [NB, RWP])
    h1scr = nc.dram_tensor("h1scr", [(NTP + 1) * 128, D], F32).ap()
    if DBG:
        dbg_d = dout("dbg", [128, 4096])

    def row(v):
        return v.rearrange("(o n) -> o n", o=1)

    P = Prog()
    with ExitStack() as es:
        arena = Arena(nc, es, ARENA_BYTES)
        C = Ctx(nc, P, arena)
        sb = C.sb
        ps = [V(es.enter_context(nc.psum_tensor(f"ps{i}", [128, 512], F32))[:], f"ps{i}") for i in range(8)]

        def psv(i, a):
            return ps[i].rearrange("p (a t) -> p a t", a=a)

        def psb(i):
            return ps[i].bitcast(BF16)

        def bc3(ap2, n):
            return ap2.unsqueeze(2).to_broadcast([ap2.shape[0], ap2.shape[1], n])

        def v3(t):
            return t.rearrange("p (a t) -> p a t", a=4)

        ident_bf = sb([128, 128], BF16, "ident_bf")
        lng = sb([128, D], F32, "lng"); lnb = sb([128, D], F32, "lnb")
        C.dma(SP, lng, row(ln1_g).partition_broadcast(128))
        C.dma(SP, lnb, row(ln1_b).partition_broadcast(128))
        bnst = sb([128, 12], F32, "bnst"); mv = sb([128, 2], F32, "mv"); rstd1 = sb([128, 1], F32, "rstd1")
        fence_ln = sb([128, 2], F32, "fence_ln")
        if DBG:
            dbg = sb([128, 4096], F32, "dbg")
            C.memset(POOL, dbg, 0.0)
            dbg_pos = [0]
            dbg_map = {}

            def dump(name, v, n):
                a = dbg_pos[0]
                shape = list(v.shape)
                dst = dbg[0:shape[0], a:a + n]
                if len(shape) == 3:
                    dst = dst.rearrange("p (a b) -> p a b", a=shape[1])
                C.cp(POOL, dst, v)
                dbg_pos[0] += n
                dbg_map[name] = (a, n, shape)
            build.dbg_map = dbg_map
        else:
            def dump(name, v, n):
                return None
        phase_mark = arena.off

        def layernorm(src, dst, eps):
            for half in range(2):
                C.bn_stats(bnst[:, half * 6:(half + 1) * 6], src[:, half * 512:(half + 1) * 512])
            C.bn_aggr(mv, bnst)
            C.act(rstd1, mv[:, 1:2], AF.Ln, bias=eps)
            C.act(rstd1, rstd1, AF.Exp, scale=-0.5)
            C.ts(DVE, dst, src, mv[:, 0:1], ALU.subtract, rstd1[:, 0:1], ALU.mult)
            halves = []
            for eng_, hsl in ((DVE, slice(0, 512)), (DVE, slice(512, 1024))):
                dk = dst[:, hsl].k(hsl.start)
                halves.append(dk)
                o, a_, g_, b_ = _ap(dk), _ap(dst[:, hsl]), _ap(lng[:, hsl]), _ap(lnb[:, hsl])
                C.rec(eng_, lambda e, o=o, a_=a_, g_=g_: e.tensor_tensor(out=o, in0=a_, in1=g_, op=ALU.mult), [dk], [dst, lng])
                C.rec(eng_, lambda e, o=o, b_=b_: e.tensor_tensor(out=o, in0=o, in1=b_, op=ALU.add), [dk], [dk, lnb])
            C.rec(POOL, lambda e: e.memset(_ap(fence_ln), 0.0), [dst, fence_ln], halves)

        F32_CONSTS = ("ident", "reset_p", "reset_s", "hsel", "cm", "i64s")
        BF_CONSTS = ("mS_p", "mI_p", "mST_p", "mS_s", "mI_s", "mST_s", "bdones")
        cviews = {}
        nf = sum(COFF[k][1] - COFF[k][0] for k in F32_CONSTS)
        nb_ = sum(COFF[k][1] - COFF[k][0] for k in BF_CONSTS)
        cstf = sb([128, nf], F32, "cstf"); cstb = sb([128, nb_], BF16, "cstb")
        o_ = 0
        for k_ in F32_CONSTS:
            a, b = COFF[k_]
            C.dma(SP, cstf[:, o_:o_ + b - a].k(k_), cst_d[:, a:b])
            cviews[k_] = cstf[:, o_:o_ + b - a].k(k_)
            o_ += b - a
        o_ = 0
        for k_ in BF_CONSTS:
            a, b = COFF[k_]
            C.dma(POOL, cstb[:, o_:o_ + b - a].k(k_), cst_d[:, a:b])
            cviews[k_] = cstb[:, o_:o_ + b - a].k(k_)
            o_ += b - a

        def cc(name):
            return cviews[name]

        C.cp(POOL, ident_bf, cc("ident"))
        bdones_bf = cc("bdones")
        mu14 = sb([128, 14], F32, "mu14")
        kk4 = sb([128, 4], F32, "kk4"); ka4 = sb([128, 4], F32, "ka4"); rk4 = sb([128, 4], F32, "rk4")
        w04 = sb([128, 4], F32, "w04"); a04 = sb([128, 4], F32, "a04")
        lbl = sb([128, 2, 4], F32, "lbl")
        with nc.allow_non_contiguous_dma(reason="tiny per-channel parameter vectors"):
            C.dma(SP, mu14, shift_mu.rearrange("(c p) -> p c", p=128))
            C.dma(SP, kk4, k_k.rearrange("(c p) -> p c", p=128))
            C.dma(SP, ka4, k_a.rearrange("(c p) -> p c", p=128))
            C.dma(SP, rk4, r_k.rearrange("(c p) -> p c", p=128))
            C.dma(SP, w04, w0.rearrange("(c p) -> p c", p=128))
            C.dma(SP, a04, a0.rearrange("(c p) -> p c", p=128))
            C.dma(SP, lbl, lb_logits.rearrange("l (c p) -> p l c", p=128))
        WA = sb([128, 512], BF16, "WA")
        C.dma(POOL, WA[0:64, :].k("lo"), w1u)
        C.dma(POOL, WA[64:128, :].k("hi"), a1u)
        g1u_bf = sb([128, 512], BF16, "g1u_bf")
        C.dma(POOL, g1u_bf, g1u)
        lnxw = sb([128, 512], BF16, "lnxw"); lnxb = sb([128, 512], BF16, "lnxb"); hgw = sb([128, 512], BF16, "hgw")
        C.dma(POOL, lnxw, row(ln_x_w).partition_broadcast(128))
        C.dma(POOL, lnxb, row(ln_x_b).partition_broadcast(128))
        C.dma(POOL, hgw, row(hg_norm_w).partition_broadcast(128))
        lb4 = sb([128, 4], F32, "lb4"); oml4 = sb([128, 4], F32, "oml4"); etmp = sb([128, 4], F32, "etmp")
        C.tt(DVE, etmp, lbl[:, 1, :], lbl[:, 0, :], ALU.subtract)
        C.act(etmp, etmp, AF.Exp)
        C.ts(DVE, lb4, etmp, 1.0, ALU.add)
        C.recip(lb4, lb4)
        C.tt(DVE, oml4, etmp, lb4, ALU.mult)
        RKsel = sb([128, 4, 2], BF16, "RKsel")
        C.tt(DVE, RKsel, bc3(rk4, 2), cc("hsel").unsqueeze(1).to_broadcast([128, 4, 2]), ALU.mult)

        ov_lo = arena.off
        w_in_bf = sb([128, 8, PJ], BF16, "w_in_bf")
        ov_hi = arena.off
        for kc in range(8):
            C.dma(POOL, w_in_bf[:, kc, :].k(kc), w_in[kc * 128:(kc + 1) * 128, :])
        w_out_bf = sb([128, 8, D], BF16, "w_out_bf")
        for kc in range(8):
            C.dma(POOL, w_out_bf[:, kc, :].k(kc), w_out[kc * 128:(kc + 1) * 128, :])

        ST = sb([128, 4, 64], F32, "ST"); ST_bf = sb([128, 4, 64], BF16, "ST_bf")
        SH = sb([128, 4, 128], F32, "SH"); SH_bf = sb([128, 4, 128], BF16, "SH_bf")
        plast = sb([128, 14], F32, "plast")
        for t_ in (ST, ST_bf, SH, SH_bf, plast):
            C.memset(POOL, t_, 0.0)

        def prompt_state_outputs():
            with nc.allow_non_contiguous_dma(reason="tiny state vector"):
                C.dma(SP, shp.rearrange("(c p) -> p c", p=128), plast, out_final=True)
            C.dma(SP, hgp.rearrange("h k v -> k h v"), SH, out_final=True)
            identf = cc("ident")
            for pr in range(4):
                C.tr(ps[2][0:64, pr * 128:(pr + 1) * 128], ST[:, pr, :], identf)
            rwo = T[0]
            C.cp(DVE, rwo[0:64, :], ps[2][0:64, :])
            C.dma(SP, rwp.rearrange("h v j -> v h j"), rwo[0:64, :].rearrange("p (h j) -> p h j", h=8), out_final=True)
            if DBG:
                C.dma(SP, dbg_d, dbg, out_final=True)


        x_t = [sb([128, D], F32, f"x_t{i}") for i in range(2)]
        x_bf = sb([128, D], BF16, "x_bf")
        xT = sb([128, 8, 128], BF16, "xT")
        pr_ = sb([128, 14, 129], F32, "pr")
        xs = sb([128, 14, 128], F32, "xsft")
        T = [sb([128, 512], F32, f"T{i}") for i in range(10)]
        z12 = sb([128, 128], BF16, "z12"); sg_bf = sb([128, 128], BF16, "sg_bf"); tqb = sb([128, 512], BF16, "tqb")
        bhT = sb([128, 4, 128], BF16, "bhT"); khT = sb([128, 4, 128], BF16, "khT"); vT = sb([128, 4, 128], BF16, "vT")
        khTb = sb([128, 4, 128], BF16, "khTb")
        fence2 = sb([128, 2], F32, "fence2")
        HB = []
        for i in range(2):
            HB.append(dict(
                AR=sb([128, 4, 2, 128], BF16, f"AR{i}"), bT=sb([128, 4, 128], BF16, f"bT{i}"), kT=sb([128, 4, 128], BF16, f"kT{i}"),
                A_tm=sb([128, 512], BF16, f"A_tm{i}"), Bh_tm=sb([128, 512], BF16, f"Bh_tm{i}"),
                Kh_tm=sb([128, 512], BF16, f"Kh_tm{i}"), V_tm=sb([128, 512], BF16, f"V_tm{i}"),
                qTb=sb([128, 4, 128], BF16, f"qTb{i}"), kTb=sb([128, 4, 128], BF16, f"kTb{i}"),
                khat_tm=sb([128, 512], BF16, f"khat_tm{i}"), i_tm=sb([128, 512], BF16, f"i_tm{i}"),
                gs_t=sb([128, 512], BF16, f"gs_t{i}"), g_tm=sb([128, 512], BF16, f"g_tm{i}"),
                bon8=sb([128, 8], F32, f"bon8{i}"), gC=sb([128, 4, NB], F32, f"gC{i}"), decH=sb([128, 4, NB], F32, f"decH{i}")))
        Pm = sb([128, 8, 128], BF16, "Pm"); Tm = sb([128, 8, 128], BF16, "Tm"); RR = sb([128, 8, 128], BF16, "RR")
        NrbT = sb([128, 8, 128], BF16, "NrbT"); AkT = sb([128, 8, 128], BF16, "AkT"); NrkT = sb([128, 8, 128], BF16, "NrkT")
        TTf = sb([128, 8, 128], BF16, "TTf")
        W1T = sb([128, 4, 128], BF16, "W1T")
        Z_tm = sb([128, 512], BF16, "Z_tm"); U_tm = sb([128, 512], BF16, "U_tm")
        st16 = sb([128, 16], F32, "st16"); m8 = sb([128, 8], F32, "m8"); r8 = sb([128, 8], F32, "r8")
        o_all = sb([128, D], BF16, "o_all"); oT = sb([128, 8, 128], BF16, "oT")
        h1pre = sb([128, D], F32, "h1pre")
        attT = sb([128, 4, 128], BF16, "attT")
        s4 = sb([128, 4], F32, "s4"); rr4 = sb([128, 4], F32, "rr4")
        BT0 = h1pre[:, 0:512]; BT1 = h1pre[:, 512:1024]
        SHtmp = v3(BT1)
        STtmp = BT1[:, 0:256].rearrange("p (a v) -> p a v", a=4)
        identb4 = ident_bf.unsqueeze(1).to_broadcast([128, 4, 128])
        save_off = arena.off
        arena.off = ov_lo
        scrA = [sb([128, 2048], F32, f"scrA{i}") for i in range(2)]
        S0T32 = sb([128, NB, 4, 64], F32, "S0T32")
        S0Tb = sb([128, NB, 4, 64], BF16, "S0Tb")
        sshT = sb([128, 14, NB], F32, "sshT"); lastp = sb([128, 14, NB], F32, "lastp")
        EW = [sb([128, 4, 128], BF16, f"EW{i}") for i in range(2)]
        ER = [sb([128, 4, 128], BF16, f"ER{i}") for i in range(2)]
        EQ = [sb([128, 4, 128], BF16, f"EQ{i}") for i in range(2)]
        Ub = sb([128, 512], BF16, "Ub"); Vb = sb([128, 512], BF16, "Vb"); khb = sb([128, 512], BF16, "khb")
        Dg = sb([128, 4, 64], F32, "Dg")
        S0h = [sb([128, 4, 128], F32, f"S0h{i}") for i in range(2)]
        S0hb = sb([128, 4, 128], BF16, "S0hb")
        Sn = sb([128, 512], F32, "Sn")
        fence_t = sb([128, 2], F32, "fence_t")
        assert arena.off <= ov_hi, (arena.off, ov_hi)
        ov_bufs = scrA + [S0T32, S0Tb, sshT, lastp] + EW + ER + EQ + [Ub, Vb, khb, Dg] + S0h + [S0hb, Sn, fence_t]
        arena.off = save_off
        ssh_tm = scrA[0][0:NB, 0:RWP]
        hh_order = (0, 2, 4, 6, 1, 3, 5, 7)
        heads_of = [(0, 1, 2, 3), (4, 5, 6, 7)]
        PA, PB_ = 6, 7

        cb = cc

        def cfg(sample):
            sfx = "_s" if sample else "_p"
            return dict(nch=NB if sample else 1, Cn=TS if sample else 128, L=3 if sample else 7,
                        mS=cb("mS" + sfx), mI=cb("mI" + sfx), mST=cb("mST" + sfx), reset=cc("reset" + sfx))

        def stageA(ti, x_src, sample):
            g_ = cfg(sample)
            nch, Cn, reset = g_["nch"], g_["Cn"], g_["reset"]
            H = HB[ti % 2]
            AR, bT, kT = H["AR"], H["bT"], H["kT"]
            xt = x_t[ti % 2]
            C.dma(SP, xt, x_src)
            C.dma(POOL, x_bf, x_src)
            for kc in range(8):
                C.tr(psb(PB_)[:, kc * 128:(kc + 1) * 128], x_bf[:, kc * 128:(kc + 1) * 128], ident_bf)
            C.cp(ACT, xT, psb(PB_).rearrange("p (a t) -> p a t", a=8))
            yield
            sig, kq, fgl, bcs, eb, enb, ebl, sq_, eg = T[1], T[4], T[0], T[2], T[3], T[5], T[6], T[7], T[8]

            def proj_fm(c0, n, bank):
                for j in range(n):
                    c = c0 + j
                    for kc in range(8):
                        C.mm(ps[bank][:, j * 128:(j + 1) * 128], w_in_bf[:, kc, c * 128:(c + 1) * 128].k(kc),
                             xT[:, kc, :], start=(kc == 0), stop=(kc == 7))

            def proj_tm(col0, bank):
                for kc in range(8):
                    C.mm(ps[bank], xT[:, kc, :], w_in_bf[:, kc, col0:col0 + 512].k(kc), start=(kc == 0), stop=(kc == 7))

            proj_fm(0, 4, PA); C.cp(ACT, pr_[:, 0:4, 1:129], psv(PA, 4)); yield
            proj_fm(4, 4, PB_); C.cp(ACT, pr_[:, 4:8, 1:129], psv(PB_, 4)); yield
            proj_fm(8, 4, PA); C.cp(ACT, pr_[:, 8:12, 1:129], psv(PA, 4)); yield
            proj_fm(12, 2, PB_); C.cp(ACT, pr_[:, 12:14, 1:129], psv(PB_, 4)[:, 0:2, :]); yield
            prev, cur = pr_[:, :, 0:128], pr_[:, :, 1:129]
            if not sample:
                C.cp(DVE, pr_[:, :, 0:1], plast.unsqueeze(2))
                C.cp(DVE, plast.unsqueeze(2), pr_[:, :, 128:129])
            else:
                C.memset(POOL, pr_[:, :, 0:1], 0.0)
            proj_fm(14, 4, PA)
            C.act(sq_, ps[PA], AF.Sigmoid)
            C.tt(DVE, sq_, ps[PA], sq_, ALU.mult)
            yield
            proj_fm(18, 4, PB_)
            C.act(sig, ps[PB_], AF.Sigmoid)
            yield
            proj_tm(RWP + 1024, PA)
            C.cp(ACT, H["i_tm"], ps[PA])
            yield
            proj_tm(RWP + 1536, PB_)
            C.act(eg, ps[PB_], AF.Sigmoid)
            C.tt(DVE, H["gs_t"], ps[PB_], eg, ALU.mult)
            yield
            if sample:
                C.rec(POOL, lambda e: e.memset(_ap(fence_t), 0.0),
                      [w_in_bf[:, kc, :].k(kc) for kc in range(8)] + ov_bufs
                      + [S0T32[:, b, :, :].k(b) for b in range(NB)] + [S0Tb[:, b, :, :].k(b) for b in range(NB)], [])
                for t_ in EW + ER + EQ:
                    C.memset(POOL, t_, 0.0)
                C.dma(SP, ssh_tm, ssh)
                for c in range(14):
                    C.tr(ps[PA][:, c * NB:(c + 1) * NB], ssh_tm[:, c * 128:(c + 1) * 128], cc("ident")[0:NB, 0:NB])
                C.cp(DVE, sshT, ps[PA][:, 0:14 * NB].rearrange("p (c b) -> p c b", c=14))
            for eng_, c0, c1 in ((DVE, 0, 8), (DVE, 8, 14)):
                C.tt(eng_, xs[:, c0:c1, :].k(c0), prev[:, c0:c1, :], cur[:, c0:c1, :], ALU.subtract)
                C.tt(eng_, xs[:, c0:c1, :].k(c0), xs[:, c0:c1, :].k(c0), bc3(mu14[:, c0:c1], 128), ALU.mult)
                C.tt(eng_, xs[:, c0:c1, :].k(c0), xs[:, c0:c1, :].k(c0), cur[:, c0:c1, :], ALU.add)
            C.rec(POOL, lambda e: e.memset(_ap(fence2), 0.0), [xs, fence2], [xs[:, 0:8, :].k(0), xs[:, 8:14, :].k(8)])
            if sample:
                cur4 = cur.rearrange("p c (b t) -> p c b t", t=TS)
                xs4 = xs.rearrange("p c (b t) -> p c b t", t=TS)
                cur0, xs0 = cur4[:, :, :, 0], xs4[:, :, :, 0]
                C.tt(DVE, xs0, sshT, cur0, ALU.subtract)
                C.tt(DVE, xs0, xs0, bc3(mu14, NB), ALU.mult)
                C.tt(DVE, xs0, xs0, cur0, ALU.add)
                C.cp(DVE, lastp, cur4[:, :, :, TS - 1])
                for g0 in range(0, 14, 4):
                    bk = PA if (g0 // 4) % 2 == 0 else PB_
                    n = min(4, 14 - g0)
                    for j in range(n):
                        C.tr(ps[bk][0:NB, j * 128:(j + 1) * 128], lastp[:, g0 + j, :], cc("ident"))
                    C.cp(ACT, ssh_tm[:, g0 * 128:(g0 + n) * 128], ps[bk][0:NB, 0:n * 128])
                C.dma(SP, shs, ssh_tm, out_final=True)
            yield
            r_ = xs[:, 0:4, :]; k_ = xs[:, 4:8, :]; v_ = xs[:, 8:12, :]
            C.act(z12[0:64, :], xs[0:64, 12, :], AF.Tanh)
            C.act(sg_bf, xs[:, 13, :], AF.Sigmoid)
            C.cp(DVE, z12[64:128, :], xs[64:128, 12, :])
            for pr in range(4):
                sl = slice(pr * 128, (pr + 1) * 128)
                C.mm(ps[PA][:, sl], WA[0:64, sl].k("lo"), z12[0:64, :])
            for pr in range(4):
                sl = slice(pr * 128, (pr + 1) * 128)
                C.mm(ps[PB_][:, sl], WA[64:128, sl].k("hi"), z12[64:128, :])
            sw, alr = T[9], T[8]
            for pr in range(4):
                sl = slice(pr * 128, (pr + 1) * 128)
                C.act(sw[:, sl], ps[PA][:, sl], AF.Sigmoid, bias=w04[:, pr:pr + 1])
            C.tt(DVE, v3(kq), v3(sig), bc3(oml4, 128), ALU.mult)
            C.tt(DVE, v3(fgl), v3(kq), bc3(lb4, 128), ALU.add)
            C.tt(DVE, v3(kq), bc3(oml4, 128), v3(kq), ALU.subtract)
            yield
            for pr in range(4):
                sl = slice(pr * 128, (pr + 1) * 128)
                C.act(alr[:, sl], ps[PB_][:, sl], AF.Sigmoid, bias=a04[:, pr:pr + 1])
            C.mm(ps[PA], sg_bf, g1u_bf)
            C.cp(ACT, H["g_tm"], ps[PA])
            yield
            C.act(fgl, fgl, AF.Ln)
            for h in range(4):
                C.scan(bcs[:, h * 128:(h + 1) * 128], reset, fgl[:, h * 128:(h + 1) * 128])
            C.act(eb, bcs, AF.Exp)
            C.act(enb, bcs, AF.Exp, scale=-1.0)
            bc4 = bcs.rearrange("p (a n c) -> p a n c", a=4, n=nch)
            C.tt(DVE, ebl.rearrange("p (a n c) -> p a n c", a=4, n=nch),
                 bc4[:, :, :, Cn - 1:Cn].to_broadcast([128, 4, nch, Cn]), bc4, ALU.subtract)
            C.act(ebl, ebl, AF.Exp)
            yield
            C.cp(DVE, H["decH"][:, :, 0:nch].unsqueeze(3),
                 eb.rearrange("p (a n c) -> p a n c", a=4, n=nch)[:, :, :, Cn - 1:Cn])
            C.tt(DVE, H["qTb"], v3(sq_), v3(eb), ALU.mult)
            C.tt(DVE, H["kTb"], v3(kq), v3(enb), ALU.mult)
            C.tt(DVE, khTb, v3(kq), v3(ebl), ALU.mult)
            yield
            cumS, gex, gin, ginv, glast, kkk, tq, k2 = T[2], T[3], T[4], T[5], T[6], T[7], T[0], T[1]
            for pr in range(4):
                C.scan(cumS[:, pr * 128:(pr + 1) * 128], reset, sw[:, pr * 128:(pr + 1) * 128])
            C.tt(DVE, gex, cumS, sw, ALU.subtract)
            C.act(gex, gex, AF.Exp, scale=CDEC)
            C.act(gin, cumS, AF.Exp, scale=CDEC)
            C.act(ginv, cumS, AF.Exp, scale=-CDEC)
            cs4 = cumS.rearrange("p (a n c) -> p a n c", a=4, n=nch)
            C.tt(DVE, glast.rearrange("p (a n c) -> p a n c", a=4, n=nch),
                 cs4[:, :, :, Cn - 1:Cn].to_broadcast([128, 4, nch, Cn]), cs4, ALU.subtract)
            C.act(glast, glast, AF.Exp, scale=CDEC)
            C.cp(DVE, H["gC"][:, :, 0:nch].unsqueeze(3),
                 gin.rearrange("p (a n c) -> p a n c", a=4, n=nch)[:, :, :, Cn - 1:Cn])
            yield
            C.tt(DVE, v3(kkk), k_, bc3(kk4, 128), ALU.mult)
            C.act(tqb, kkk, AF.Square)
            for pr in range(4):
                sl = slice(pr * 128, (pr + 1) * 128)
                C.mm(ps[PB_][:, sl], bdones_bf, tqb[:, sl])
            C.act(tq, ps[PB_], AF.Ln, bias=1e-24)
            C.act(tq, tq, AF.Exp, scale=-0.5)
            C.tt(DVE, kkk, kkk, tq, ALU.mult)
            C.stt(DVE, v3(k2), v3(alr), -1.0, bc3(ka4, 128), ALU.add, ALU.mult)
            C.stt(DVE, v3(k2), v3(k2), 1.0, k_, ALU.add, ALU.mult)
            C.tt(DVE, alr, kkk, alr, ALU.mult)
            b_ = alr
            yield
            C.tt(DVE, AR[:, :, 1, :], r_, v3(gin), ALU.mult)
            C.stt(DVE, AR[:, :, 0, :], v3(kkk), -1.0, v3(gex), ALU.mult, ALU.mult)
            C.tt(DVE, bT, v3(b_), v3(ginv), ALU.mult)
            C.tt(DVE, kT, v3(k2), v3(ginv), ALU.mult)
            C.tt(DVE, bhT, v3(b_), v3(glast), ALU.mult)
            C.tt(DVE, khT, v3(k2), v3(glast), ALU.mult)
            C.cp(ACT, vT, v_)
            C.tt(DVE, v3(tqb), r_, v3(k2), ALU.mult)
            for pr in range(4):
                C.mm(ps[PA][:, pr * 2:(pr + 1) * 2], tqb[:, pr * 128:(pr + 1) * 128], RKsel[:, pr, :])
            C.cp(DVE, H["bon8"], ps[PA][:, 0:8])
            yield
            for (src, bank, half) in ((lambda pr: AR[:, pr, 0, :], PA, 0), (lambda pr: bhT[:, pr, :], PA, 1),
                                      (lambda pr: khT[:, pr, :], PB_, 0), (lambda pr: vT[:, pr, :], PB_, 1)):
                for pr in range(4):
                    o0 = half * 512 + pr * 128
                    C.tr(psb(bank)[:, o0:o0 + 128], src(pr), ident_bf)
            C.cp(ACT, H["A_tm"], psb(PA)[:, 0:512]); C.cp(ACT, H["Bh_tm"], psb(PA)[:, 512:1024])
            C.cp(ACT, H["Kh_tm"], psb(PB_)[:, 0:512]); C.cp(ACT, H["V_tm"], psb(PB_)[:, 512:1024])
            yield
            for h in range(4):
                C.tr(psb(PA)[:, h * 128:(h + 1) * 128], khTb[:, h, :], ident_bf)
            C.cp(ACT, H["khat_tm"], psb(PA)[:, 0:512])
            yield

        def stageB(ti, h1_dst, sample, part="all"):
            g_ = cfg(sample)
            nch, Cn, L = g_["nch"], g_["Cn"], g_["L"]
            mSb = g_["mS"].unsqueeze(1).to_broadcast([128, 4, 128])
            mIb = g_["mI"].unsqueeze(1).to_broadcast([128, 4, 128])
            mSTb = g_["mST"].unsqueeze(1).to_broadcast([128, 4, 128])
            H = HB[ti % 2]
            AR, bT, kT, A_tm, Bh_tm, Kh_tm, V_tm = H["AR"], H["bT"], H["kT"], H["A_tm"], H["Bh_tm"], H["Kh_tm"], H["V_tm"]
            qTb, kTb, khat_tm, i_tm, gs_t, g_tm, bon8, gC, decH = (H["qTb"], H["kTb"], H["khat_tm"], H["i_tm"], H["gs_t"],
                                                                    H["g_tm"], H["bon8"], H["gC"], H["decH"])
            xt = x_t[ti % 2]
            if part in ("all", "hgrn"):
                bA, bO = (6, 7) if sample else (3, 4)
                for h in range(4):
                    C.mm(ps[bA][:, h * 128:(h + 1) * 128], kTb[:, h, :], qTb[:, h, :])
                C.tt(DVE, attT, psv(bA, 4), mIb, ALU.mult)
                if not sample:
                    for h in range(4):
                        hsl = slice(h * 128, (h + 1) * 128)
                        C.mm(ps[4][:, hsl], attT[:, h, :], i_tm[:, hsl], start=True, stop=False)
                        C.mm(ps[4][:, hsl], qTb[:, h, :], SH_bf[:, h, :], start=False, stop=True)
                    for h in range(4):
                        hsl = slice(h * 128, (h + 1) * 128)
                        C.mm(ps[5][:, hsl], khat_tm[:, hsl], i_tm[:, hsl])
                    C.tt(DVE, SHtmp, SH, bc3(decH[:, :, 0], 128), ALU.mult)
                    C.tt(DVE, SH, SHtmp, psv(5, 4), ALU.add)
                    C.cp(ACT, SH_bf, SH)
                else:
                    for h in range(4):
                        hsl = slice(h * 128, (h + 1) * 128)
                        C.mm(ps[bO][:, hsl], attT[:, h, :], i_tm[:, hsl], start=(h == 0), stop=False)
                    for b in range(NB):
                        s0 = S0h[b % 2]
                        csl = slice(b * TS, (b + 1) * TS)
                        C.dma(SP, s0, shg[b].rearrange("h k v -> k h v"))
                        C.cp(ACT, S0hb, s0)
                        eq = EQ[b % 2]
                        C.cp(POOL, eq[:, :, csl], qTb[:, :, csl])
                        for h in range(4):
                            C.mm(ps[bO][:, h * 128:(h + 1) * 128], eq[:, h, :], S0hb[:, h, :], start=False, stop=False)
                        C.memset(POOL, eq[:, :, csl], 0.0)
                        C.ts(DVE, khb, khat_tm, cc("cm")[:, b:b + 1], ALU.mult)
                        bank = bA
                        for h in range(4):
                            hsl = slice(h * 128, (h + 1) * 128)
                            C.mm(ps[bank][:, hsl], khb[:, hsl], i_tm[:, hsl])
                        C.tt(DVE, s0, s0, bc3(decH[:, :, b], 128), ALU.mult)
                        C.tt(DVE, s0, s0, psv(bank, 4), ALU.add)
                        C.dma(SP, hgs[b].rearrange("h k v -> k h v"), s0, out_final=True)
                        yield
                osq = T[0] if sample else BT0
                C.act(osq, ps[bO], AF.Square)
                C.red(DVE, s4, v3(osq))
                C.act(rr4, s4, AF.Ln, scale=1.0 / 128, bias=RMS_EPS)
                C.act(rr4, rr4, AF.Exp, scale=-0.5)
                C.tt(DVE, v3(osq), psv(bO, 4), bc3(rr4, 128), ALU.mult)
                C.tt(DVE, osq, osq, hgw, ALU.mult)
                C.tt(DVE, o_all[:, 512:1024], osq, gs_t, ALU.mult)
                yield
            if part == "hgrn":
                return
            if part in ("all", "rwkv"):
                if sample:
                    for g in range(4):
                        sa = scrA[g % 2]
                        nat = sa[0:64, :].rearrange("p (b n) -> p b n", b=4)
                        C.dma(SP, nat.rearrange("p b (h j) -> p b h j", h=8),
                              srw[g * 4:(g + 1) * 4].rearrange("b h v j -> v b h j"))
                        for bb in range(4):
                            b = g * 4 + bb
                            bank = b % 2
                            for pr in range(4):
                                C.tr(ps[bank][:, (bb % 2) * 256 + pr * 64:(bb % 2) * 256 + (pr + 1) * 64],
                                     nat[:, bb, pr * 128:(pr + 1) * 128], cc("ident")[0:64, 0:64])
                            C.cp(ACT, S0T32[:, b, :, :].k(b),
                                 ps[bank][:, (bb % 2) * 256:(bb % 2) * 256 + 256].rearrange("p (a v) -> p a v", a=4))
                            C.cp(DVE, S0Tb[:, b, :, :].k(b), S0T32[:, b, :, :].k(b))
                        yield
                for hg in range(2):
                    for i, h in enumerate(heads_of[hg]):
                        pr, hh = h // 2, h % 2
                        rows = slice(64 * hh, 64 * hh + 64)
                        sl = slice(i * 128, (i + 1) * 128)
                        C.mm(ps[0][:, sl], bT[rows, pr, :], AR[rows, pr, 0, :])
                        C.mm(ps[1][:, sl], bT[rows, pr, :], AR[rows, pr, 1, :])
                        C.mm(ps[2][:, sl], kT[rows, pr, :], AR[rows, pr, 0, :])
                        C.mm(ps[3][:, sl], kT[rows, pr, :], AR[rows, pr, 1, :])
                        C.mm(ps[4][:, sl], AR[rows, pr, 0, :], bT[rows, pr, :])
                    hs = slice(4 * hg, 4 * hg + 4)
                    C.tt(DVE, Pm[:, hs, :], psv(0, 4), mSb, ALU.mult)
                    C.tt(DVE, NrbT[:, hs, :], psv(1, 4), mIb, ALU.mult)
                    C.tt(DVE, AkT[:, hs, :], psv(2, 4), mSb, ALU.mult)
                    C.tt(DVE, NrkT[:, hs, :], psv(3, 4), mIb, ALU.mult)
                    C.tt(DVE, RR[:, hs, :], psv(4, 4), mSTb, ALU.mult)
                    C.tt(DVE, Tm[:, hs, :], Pm[:, hs, :], identb4, ALU.add)
                    yield
                for k in range(L):
                    for hg in range(2):
                        b0 = 3 * hg
                        hs = slice(4 * hg, 4 * hg + 4)
                        for i, h in enumerate(heads_of[hg]):
                            sl = slice(i * 128, (i + 1) * 128)
                            if k < L - 1:
                                C.mm(ps[b0][:, sl], RR[:, h, :], Pm[:, h, :])
                            if k >= 1:
                                C.mm(ps[b0 + 1][:, sl], RR[:, h, :], Tm[:, h, :])
                            if k < L - 1:
                                C.mm(ps[b0 + 2][:, sl], Pm[:, h, :], RR[:, h, :])
                        if k < L - 1:
                            C.cp(ACT, Pm[:, hs, :], psv(b0, 4))
                        if k >= 1:
                            dst = TTf[:, hs, :] if k == L - 1 else Tm[:, hs, :]
                            C.tt(DVE, dst, psv(b0 + 1, 4), Tm[:, hs, :], ALU.add)
                        if k < L - 1:
                            C.cp(ACT, RR[:, hs, :], psv(b0 + 2, 4))
                        yield
                for h in range(8):
                    C.mm(ps[2][:, h * 64:(h + 1) * 64], AkT[:, h, :], V_tm[:, h * 64:(h + 1) * 64])
                C.cp(ACT, Z_tm, ps[2])
                for h in range(8):
                    pr = h // 2
                    C.mm(ps[h // 4][:, (h % 4) * 128:(h % 4 + 1) * 128], A_tm[:, pr * 128:(pr + 1) * 128], TTf[:, h, :])
                for b in range(2):
                    pv = ps[b].rearrange("p (q e t) -> p q e t", q=2, e=2)
                    C.cp(ACT, W1T[0:64, 2 * b:2 * b + 2, :], pv[0:64, :, 0, :])
                    C.cp(ACT, W1T[64:128, 2 * b:2 * b + 2, :], pv[64:128, :, 1, :])
                yield
                for h in range(8):
                    pr, hh = h // 2, h % 2
                    rows = slice(64 * hh, 64 * hh + 64)
                    hsl = slice(h * 64, (h + 1) * 64)
                    if not sample:
                        C.mm(ps[3][:, hsl], TTf[:, h, :], Z_tm[:, hsl], start=True, stop=False)
                        C.mm(ps[3][:, hsl], W1T[rows, pr, :], ST_bf[rows, pr, :], start=False, stop=True)
                    else:
                        C.mm(ps[3][:, hsl], TTf[:, h, :], Z_tm[:, hsl], start=(h == 0), stop=False)
                if sample:
                    for b in range(NB):
                        ew = EW[b % 2]
                        csl = slice(b * TS, (b + 1) * TS)
                        C.cp(POOL, ew[:, :, csl], W1T[:, :, csl])
                        for h in hh_order:
                            pr, hh = h // 2, h % 2
                            rows = slice(64 * hh, 64 * hh + 64)
                            C.mm(ps[3][:, h * 64:(h + 1) * 64], ew[rows, pr, :], S0Tb[rows, b, pr, :].k(b), start=False, stop=False)
                        C.memset(POOL, ew[:, :, csl], 0.0)
                        yield
                C.cp(ACT, U_tm, ps[3])
                for h in range(8):
                    pr, hh = h // 2, h % 2
                    rows = slice(64 * hh, 64 * hh + 64)
                    hsl = slice(h * 64, (h + 1) * 64)
                    C.mm(ps[4][:, hsl], NrbT[:, h, :], U_tm[:, hsl], start=(h == 0 or not sample), stop=False)
                    C.mm(ps[4][:, hsl], NrkT[:, h, :], V_tm[:, hsl], start=False, stop=False)
                    if not sample:
                        C.mm(ps[4][:, hsl], AR[rows, pr, 1, :], ST_bf[rows, pr, :], start=False, stop=True)
                yield
                if sample:
                    for b in range(NB):
                        er = ER[b % 2]
                        csl = slice(b * TS, (b + 1) * TS)
                        C.cp(POOL, er[:, :, csl], AR[:, :, 1, csl])
                        for h in hh_order:
                            pr, hh = h // 2, h % 2
                            rows = slice(64 * hh, 64 * hh + 64)
                            C.mm(ps[4][:, h * 64:(h + 1) * 64], er[rows, pr, :], S0Tb[rows, b, pr, :].k(b), start=False, stop=False)
                        C.memset(POOL, er[:, :, csl], 0.0)
                        yield
                    i64b = cc("i64s").unsqueeze(1).to_broadcast([128, 4, 64])
                    for b in range(NB):
                        bank = b % 2
                        C.ts(DVE, Ub, U_tm, cc("cm")[:, b:b + 1], ALU.mult)
                        C.ts(DVE, Vb, V_tm, cc("cm")[:, b:b + 1], ALU.mult)
                        C.tt(DVE, Dg, i64b, bc3(gC[:, :, b], 64), ALU.mult)
                        for h in range(8):
                            hsl = slice(h * 64, (h + 1) * 64)
                            C.mm(ps[bank][0:64, hsl], Ub[:, hsl], Bh_tm[:, hsl], start=(h == 0), stop=False)
                            C.mm(ps[bank][0:64, hsl], Vb[:, hsl], Kh_tm[:, hsl], start=False, stop=False)
                        for h in hh_order:
                            pr, hh = h // 2, h % 2
                            rows = slice(64 * hh, 64 * hh + 64)
                            C.mm(ps[bank][0:64, h * 64:(h + 1) * 64], S0T32[rows, b, pr, :].k(b), Dg[rows, pr, :], start=False, stop=False)
                        C.cp(ACT, Sn[0:64, :], ps[bank][0:64, :])
                        C.dma(SP, rws[b].rearrange("h v j -> v h j"), Sn[0:64, :].rearrange("p (h j) -> p h j", h=8), out_final=True)
                        yield
                if not sample:
                    for pr in range(4):
                        psl = slice(pr * 128, (pr + 1) * 128)
                        C.mm(ps[5][:, psl], Bh_tm[:, psl], U_tm[:, psl], start=True, stop=False)
                        C.mm(ps[5][:, psl], Kh_tm[:, psl], V_tm[:, psl], start=False, stop=True)
                    C.tt(DVE, STtmp, ST, bc3(gC[:, :, 0], 64), ALU.mult)
                    p5 = psv(5, 4)
                    C.tt(DVE, ST[0:64, :, :], STtmp[0:64, :, :], p5[0:64, :, 0:64], ALU.add)
                    C.tt(DVE, ST[64:128, :, :], STtmp[64:128, :, :], p5[64:128, :, 64:128], ALU.add)
                    C.cp(ACT, ST_bf, ST)
                yield
                ysq, tmp2 = BT0, BT1
                y3 = ps[4].rearrange("p (h v) -> p h v", h=8)
                yv = ysq.rearrange("p (h v) -> p h v", h=8)
                C.red(DVE, st16[:, 0:8], y3)
                C.act(ysq, ps[4], AF.Square)
                C.red(DVE, st16[:, 8:16], yv)
                C.ts(DVE, m8, st16[:, 0:8], 1.0 / 64, ALU.mult)
                C.tt(DVE, r8, m8, m8, ALU.mult)
                C.stt(DVE, r8, st16[:, 8:16], 1.0 / 64, r8, ALU.mult, ALU.subtract)
                C.act(r8, r8, AF.Ln, bias=GN_EPS)
                C.act(r8, r8, AF.Exp, scale=-0.5)
                C.tt(DVE, yv, y3, bc3(m8, 64), ALU.subtract)
                C.tt(DVE, yv, yv, bc3(r8, 64), ALU.mult)
                C.tt(DVE, ysq, ysq, lnxw, ALU.mult)
                C.tt(DVE, ysq, ysq, lnxb, ALU.add)
                C.tt(DVE, tmp2.rearrange("p (h v) -> p h v", h=8), V_tm.rearrange("p (h v) -> p h v", h=8),
                     bc3(bon8, 64), ALU.mult)
                C.tt(DVE, ysq, ysq, tmp2, ALU.add)
                C.tt(DVE, o_all[:, 0:512], ysq, g_tm, ALU.mult)
                yield
            if part == "rwkv":
                return
            for mc in range(8):
                C.tr(psb(2)[:, mc * 128:(mc + 1) * 128], o_all[:, mc * 128:(mc + 1) * 128], ident_bf)
            C.cp(ACT, oT, psb(2).rearrange("p (a t) -> p a t", a=8))
            for half in range(2):
                for mc in range(8):
                    C.mm(ps[half], oT[:, mc, :], w_out_bf[:, mc, half * 512:(half + 1) * 512].k(mc),
                         start=(mc == 0), stop=(mc == 7))
            for half in range(2):
                hsl = slice(half * 512, (half + 1) * 512)
                C.stt(DVE, h1pre[:, hsl], xt[:, hsl], ALPHA, ps[half], ALU.mult, ALU.add)
            yield
            layernorm(h1pre, h1pre, LN_EPS)
            C.dma(SP, h1_dst, h1pre)
            yield

        def collect(g):
            C.defer = []
            for _ in g:
                pass
            lst, C.defer = C.defer, None
            return lst

        fin = {}
        eng_free = {e: 0.0 for e in ENGINES}

        def est_dur(p):
            eng, fn, outs, ins, dma, _ = p.args
            ap = _ap(outs[0])
            n = 1
            for d_ in ap.shape[1:]:
                n *= d_
            if dma:
                return 2.0
            if eng == PE:
                return 0.03 + max(n, 48) / 2400.0 * (4.0 if ap.dtype == F32 and False else 1.0)
            if eng == ACT:
                return 0.20 + n / 1200.0
            if eng == DVE:
                return 0.08 + n / 960.0
            return 0.10 + n / 500.0

        def dep_t(d, eng):
            if d.eng == PE and eng == PE:
                return fin.get(d, 0.15) - 0.15
            return fin.get(d, 0.0) + 0.25

        def est_start(p):
            eng, fn, outs, ins, dma, _ = p.args
            t = eng_free[eng]
            for x in ins:
                if x is None or _isnum(x):
                    continue
                r = C.R(x)
                if r.last_w is not None:
                    t = max(t, dep_t(r.last_w, eng))
            for x in outs:
                r = C.R(x)
                if r.last_w is not None:
                    t = max(t, dep_t(r.last_w, eng))
                for rd in r.readers:
                    t = max(t, dep_t(rd, eng))
            return t

        def commit_timed(p):
            t0 = est_start(p)
            o_ = C.commit(p)
            d_ = est_dur(p)
            fin[o_] = t0 + d_ + (0.15 if p.args[0] == PE else 0.0)
            eng_free[p.args[0]] = t0 + (d_ if p.args[0] != SP else 0.1)
            return o_

        def interleave(ga, gb):
            la, lb_ = collect(ga), collect(gb)
            ia = ib = 0
            while ia < len(la) or ib < len(lb_):
                if ia >= len(la):
                    commit_timed(lb_[ib]); ib += 1
                elif ib >= len(lb_):
                    commit_timed(la[ia]); ia += 1
                else:
                    ta, tb = est_start(la[ia]), est_start(lb_[ib])
                    if tb <= ta:
                        commit_timed(lb_[ib]); ib += 1
                    else:
                        commit_timed(la[ia]); ia += 1

        def collect(g):
            C.defer = []
            for _ in g:
                pass
            lst, C.defer = C.defer, None
            return lst

        def drain(g):
            for p in collect(g):
                commit_timed(p)

        jobs = [(ti, xp[ti * 128:(ti + 1) * 128, :], V(h1scr[ti * 128:(ti + 1) * 128, :], ("h1scr", ti)), False)
                for ti in range(NT)]
        if SAMPLE:
            jobs.append((NT, xsm, V(h1scr[NTP * 128:(NTP + 1) * 128, :], ("h1scr", NTP)), True))
        if True:
            if jobs:
                drain(stageA(jobs[0][0], jobs[0][1], jobs[0][3]))
            for n, (ti, x_src, h1_dst, smp) in enumerate(jobs):
                if n + 1 < len(jobs):
                    nj = jobs[n + 1]
                    interleave(stageA(nj[0], nj[1], nj[3]), stageB(ti, h1_dst, smp))
                elif smp:
                    interleave(stageB(ti, h1_dst, smp, "hgrn"), stageB(ti, h1_dst, smp, "rwkv"))
                    drain(stageB(ti, h1_dst, smp, "final"))
                else:
                    drain(stageB(ti, h1_dst, smp))
                if n == NT - 1 and not smp:
                    prompt_state_outputs()
            if NT == 0:
                prompt_state_outputs()

        if True:
            P.barrier()
            arena.off = phase_mark
            w_up_bf = sb([128, 8, DFF], BF16, "w_up_bf")
            w_dn_bf = sb([128, 32, D], BF16, "w_dn_bf")
            upT = sb([128, 32, 512], BF16, "upT")
            h1T = [sb([128, 8, 512], BF16, f"h1T{i}") for i in range(2)]
            h1b = [sb([128, D], BF16, f"h1b{i}") for i in range(2)]
            h1r = [sb([128, D], F32, f"h1r{i}") for i in range(1)]
            rl = [sb([128, 512], F32, f"rl{i}") for i in range(2)]
            outb = [sb([128, D], F32, f"outb{i}") for i in range(2)]
            tiles = list(range(NT)) + ([NTP] if SAMPLE else [])
            groups = [tiles[i:i + 4] for i in range(0, NT, 4)]
            if SAMPLE:
                groups.append([NTP])

            def load_h1b(g):
                for gi, tix in enumerate(groups[g]):
                    C.dma(POOL, h1b[gi % 2], V(h1scr[tix * 128:(tix + 1) * 128, :], ("h1scr", tix)))

            def transposes(g):
                for gi, tix in enumerate(groups[g]):
                    bank = 6 + (gi % 2)
                    for kc in range(8):
                        C.tr(psb(bank)[:, kc * 128:(kc + 1) * 128], h1b[gi % 2][:, kc * 128:(kc + 1) * 128], ident_bf)
                    C.cp(ACT, h1T[g % 2][:, :, gi * 128:(gi + 1) * 128], psb(bank).rearrange("p (a t) -> p a t", a=8))

            def load_and_transpose(g, first=False):
                for gi, tix in enumerate(groups[g]):
                    C.dma(POOL, h1b[gi % 2], V(h1scr[tix * 128:(tix + 1) * 128, :], ("h1scr", tix)))
                    bank = 6 + (gi % 2)
                    for kc in range(8):
                        C.tr(psb(bank)[:, kc * 128:(kc + 1) * 128], h1b[gi % 2][:, kc * 128:(kc + 1) * 128], ident_bf)
                    C.cp(ACT, h1T[g % 2][:, :, gi * 128:(gi + 1) * 128], psb(bank).rearrange("p (a t) -> p a t", a=8))

            if groups:
                load_and_transpose(0)
            for cb in range(8):
                C.dma(POOL, w_up_bf[:, :, cb * 512:(cb + 1) * 512].k(cb),
                      w_up[:, cb * 512:(cb + 1) * 512].rearrange("(kc p) n -> p kc n", p=128))
            C.dma(SP, lng, row(ln2_g).partition_broadcast(128))
            C.dma(SP, lnb, row(ln2_b).partition_broadcast(128))
            for fc in range(32):
                C.dma(POOL, w_dn_bf[:, fc, :].k(fc), w_down[fc * 128:(fc + 1) * 128, :])
            tcount = 0
            for g, grp in enumerate(groups):
                ng = len(grp)
                W = ng * 128
                hT = h1T[g % 2]
                for fc in range(32):
                    bank = fc % 2
                    for kc in range(8):
                        C.mm(ps[bank][:, 0:W], w_up_bf[:, kc, fc * 128:(fc + 1) * 128].k(fc // 4), hT[:, kc, 0:W],
                             start=(kc == 0), stop=(kc == 7))
                    r = rl[fc % 2]
                    C.act(r[:, 0:W], ps[bank][:, 0:W], AF.Relu)
                    C.tt(POOL if fc % 2 else DVE, upT[:, fc, 0:W], r[:, 0:W], r[:, 0:W], ALU.mult)
                if g + 1 < len(groups):
                    load_and_transpose(g + 1)
                for gi, tix in enumerate(grp):
                    hr = h1r[0]
                    ob = outb[(tcount + gi) % 2]
                    C.dma(SP, hr, V(h1scr[tix * 128:(tix + 1) * 128, :], ("h1scr", tix)))
                    for half in range(2):
                        bank = 2 + ((gi * 2 + half) % 4)
                        for fc in range(32):
                            C.mm(ps[bank], upT[:, fc, gi * 128:(gi + 1) * 128], w_dn_bf[:, fc, half * 512:(half + 1) * 512].k(fc),
                                 start=(fc == 0), stop=(fc == 31))
                        hsl = slice(half * 512, (half + 1) * 512)
                        C.stt(DVE, ob[:, hsl], hr[:, hsl], ALPHA, ps[bank], ALU.mult, ALU.add)
                    layernorm(ob, ob, LN_EPS)
                    dst = ys if tix == NTP else yp[tix * 128:(tix + 1) * 128, :]
                    C.dma(SP, dst, ob, out_final=True)
                tcount += ng

        sems = {e: es.enter_context(nc.semaphore(f"s_{e}")) for e in ENGINES}
        rings = {e: [es.enter_context(nc.semaphore(f"r_{e}{i}")) for i in range(n)] for e, n in DMA_RING.items()}
        P.prepare(sems, rings)
        build.stats = dict(P.stats)
        build.arena_peak = arena.peak
        with nc.allow_low_precision(reason="bf16 matmul operands, fp32 accumulation"), \
                nc.allow_non_contiguous_dma(reason="tiny per-channel vectors / state layouts"), \
                nc.Block() as block:
            block.tensor(lambda eng: P.emit_engine(PE, eng))
            block.scalar(lambda eng: P.emit_engine(ACT, eng))
            block.vector(lambda eng: P.emit_engine(DVE, eng))
            block.gpsimd(lambda eng: P.emit_engine(POOL, eng))
            block.sync(lambda eng: P.emit_engine(SP, eng))
    return nc


IN_NAMES = ["w_in", "shift_mu", "w0", "w1u", "a0", "a1u", "g1u", "k_k", "k_a", "r_k", "ln_x_w", "ln_x_b",
            "lb_logits", "hg_norm_w", "w_out", "ln1_g", "ln1_b", "w_up", "w_down", "ln2_g", "ln2_b"]


def make_in_maps(inputs, n_cores=8):
    f = lambda a: np.ascontiguousarray(np.asarray(a, dtype=np.float32))
    shared = {}
    for k in IN_NAMES:
        a = f(inputs[k])
        a = a[0] if k != "lb_logits" else a
        if k == "r_k":
            a = a.reshape(-1)
        shared[k] = np.ascontiguousarray(a)
    shared["consts"] = CONSTS
    maps = []
    for c in range(n_cores):
        m = dict(shared)
        m["xp"] = f(inputs["x_prompt"][c])
        m["xs"] = f(inputs["x_sample"][c * NB:(c + 1) * NB]).reshape(NB * TS, D)
        m["srw"] = f(inputs["state_rwkv"][0, c * NB:(c + 1) * NB])
        m["shg"] = f(inputs["state_hgrn"][0, c * NB:(c + 1) * NB])
        m["ssh"] = f(inputs["state_shift"][0, c * NB:(c + 1) * NB])
        maps.append(m)
    return maps


_NC_CACHE = {}


def kernel(**inputs):
    if "nc" not in _NC_CACHE:
        _NC_CACHE["nc"] = build()
    nc = _NC_CACHE["nc"]
    maps = make_in_maps(inputs)
    res = run_bass_kernel_spmd(nc, maps, core_ids=list(range(8)))
    R = res.results
    y_prompt = np.stack([R[c]["yp"] for c in range(8)]).astype(np.float32)
    y_sample = np.concatenate([R[c]["ys"].reshape(NB, TS, D) for c in range(8)]).astype(np.float32)
    rw_p = np.stack([R[c]["rwp"] for c in range(8)])[None].astype(np.float32)
    rw_s = np.concatenate([R[c]["rws"] for c in range(8)])[None].astype(np.float32)
    hg_p = np.stack([R[c]["hgp"] for c in range(8)])[None].astype(np.float32)
    hg_s = np.concatenate([R[c]["hgs"] for c in range(8)])[None].astype(np.float32)
    sh_p = np.stack([R[c]["shp"] for c in range(8)])[None].astype(np.float32)
    sh_s = np.concatenate([R[c]["shs"] for c in range(8)])[None].astype(np.float32)
    return (y_prompt, y_sample, rw_p, rw_s, hg_p, hg_s, sh_p, sh_s)
```

```python
import numpy as np
import concourse.bass as bass
import concourse.mybir as mybir
from concourse.bass_utils import run_bass_kernel_spmd

F32 = mybir.dt.float32
BF16 = mybir.dt.bfloat16
AF = mybir.ActivationFunctionType
ALU = mybir.AluOpType
AX = mybir.AxisListType

DEBUG_LINES = False
LINE_OF = {}
PE, ACT, DVE, POOL, SP = "pe", "act", "dve", "pool", "sp"
ENGINES = (PE, ACT, DVE, POOL, SP)
DMA_RING = {SP: 12, ACT: 4, POOL: 8}


class Res:
    __slots__ = ("name", "last_w", "readers")

    def __init__(self, name):
        self.name = name
        self.last_w = None
        self.readers = []


class Op:
    __slots__ = ("eng", "fn", "deps", "is_dma", "signal", "idx", "dma_no", "extra_wait", "rg")

    def __init__(self, eng, fn, is_dma):
        self.eng = eng
        self.fn = fn
        self.deps = set()
        self.is_dma = is_dma
        self.signal = False
        self.idx = None
        self.dma_no = None
        self.extra_wait = None
        self.rg = None


def _pe_inorder_ok(d, o):
    return d.rg is None or o.rg is None or d.rg == o.rg


class Prog:
    def __init__(self):
        self.ops = {e: [] for e in ENGINES}
        self.order = []
        self.n_dma = {e: 0 for e in ENGINES}
        self.dma_ops = {e: [] for e in ENGINES}
        self.out_dmas = []

    def op(self, eng, fn, reads=(), writes=(), dma=False, out=False):
        o = Op(eng, fn, dma)
        for r in reads:
            if r.last_w is not None:
                o.deps.add(r.last_w)
        for w in writes:
            if w.last_w is not None:
                o.deps.add(w.last_w)
            for rd in w.readers:
                o.deps.add(rd)
        for r in reads:
            r.readers.append(o)
        for w in writes:
            w.last_w = o
            w.readers = []
        if getattr(self, "barrier_left", None) and eng in self.barrier_left:
            self.barrier_left.discard(eng)
            o.deps.update(self.pending_barrier)
        o.deps.discard(o)
        if dma:
            o.dma_no = self.n_dma[eng]
            self.n_dma[eng] += 1
            self.dma_ops[eng].append(o)
            o.signal = True
            if out:
                self.out_dmas.append(o)
        self.ops[eng].append(o)
        self.order.append(o)
        return o

    def barrier(self):
        pend = []
        for e in ENGINES:
            comp = [o for o in self.ops[e] if not o.is_dma]
            if comp:
                pend.append(comp[-1])
            pend.extend(self.dma_ops[e][-DMA_RING.get(e, 0):] if e in DMA_RING else [])
        self.pending_barrier = pend
        self.barrier_left = set(ENGINES)

    def prepare(self, sems, rings):
        for o in self.order:
            for d in o.deps:
                if d.is_dma:
                    continue
                if d.eng == o.eng and d.eng == PE and not o.is_dma and _pe_inorder_ok(d, o):
                    continue
                d.signal = True
        sig = {}
        for e in ENGINES:
            c = 0
            for o in self.ops[e]:
                if o.is_dma:
                    R = len(rings[e])
                    sig[o] = (rings[e][o.dma_no % R], 16 * (o.dma_no // R + 1))
                elif o.signal:
                    c += 1
                    sig[o] = (sems[e], c)
        self.sig = sig
        self.rings = rings
        self.stats = {e: len(self.ops[e]) for e in ENGINES}

    def emit_engine(self, e, eng):
        sig, rings = self.sig, self.rings
        waited = {}

        def wait(sem, val):
            k = id(sem)
            if waited.get(k, 0) >= val:
                return
            waited[k] = val
            eng.wait_ge(sem, val)

        for o in self.ops[e]:
            for d in o.deps:
                if d not in sig:
                    continue
                if d.eng == e and not d.is_dma and not o.is_dma and e == PE and _pe_inorder_ok(d, o):
                    continue
                s, v = sig[d]
                wait(s, v)
            if o.is_dma:
                R = len(rings[e])
                if o.dma_no >= R:
                    wait(rings[e][o.dma_no % R], 16 * (o.dma_no // R))
            ins = o.fn(eng)
            if DEBUG_LINES:
                LINE_OF[str(getattr(getattr(ins, "ins", ins), "name", ins))] = o.extra_wait
            if o in sig:
                s, v = sig[o]
                ins.then_inc(s, 16 if o.is_dma else 1)
        if e == SP:
            for q in ENGINES:
                if q not in rings:
                    continue
                for o in self.dma_ops[q][-len(rings[q]):]:
                    s, v = sig[o]
                    wait(s, v)


class V:
    __slots__ = ("ap", "key")

    def __init__(self, ap, key):
        self.ap = ap
        self.key = key

    def __getitem__(self, idx):
        return V(self.ap[idx], self.key)

    def k(self, sub):
        return V(self.ap, (self.key, sub))

    @property
    def shape(self):
        return self.ap.shape

    def rearrange(self, *a, **kw):
        return V(self.ap.rearrange(*a, **kw), self.key)

    def unsqueeze(self, ax):
        return V(self.ap.unsqueeze(ax), self.key)

    def to_broadcast(self, shape):
        return V(self.ap.to_broadcast(list(shape)), self.key)

    def bitcast(self, dt):
        return V(self.ap.bitcast(dt), self.key)


def _ap(x):
    return x.ap if isinstance(x, V) else x


def _isnum(x):
    return isinstance(x, (int, float))


class Arena:
    def __init__(self, nc, es, nbytes):
        self.n2 = nbytes // 2
        self.t = es.enter_context(nc.sbuf_tensor("arena", [128, self.n2], BF16))
        self.off = 0
        self.cnt = 0
        self.peak = 0

    def alloc(self, shape, dt, name=None):
        shape = list(shape)
        esz = 4 if dt == F32 else 2
        n = int(np.prod(shape[1:]))
        nbytes = (n * esz + 3) // 4 * 4
        o = self.off
        assert o + nbytes <= self.n2 * 2, f"arena overflow allocating {name} {shape}: {o}+{nbytes} > {self.n2 * 2}"
        self.off += nbytes
        self.peak = max(self.peak, self.off)
        ap = self.t[0:shape[0], o // 2:o // 2 + nbytes // 2]
        if esz == 4:
            ap = ap.bitcast(F32)
        ap = ap[:, 0:n]
        if len(shape) == 3:
            ap = ap.rearrange("p (a b) -> p a b", a=shape[1])
        elif len(shape) == 4:
            ap = ap.rearrange("p (a b c) -> p a b c", a=shape[1], b=shape[2])
        self.cnt += 1
        return V(ap, name or f"t{self.cnt}")


class Ctx:
    def __init__(self, nc, P, arena):
        self.nc, self.P, self.arena = nc, P, arena
        self.res = {}
        self.defer = None

    class _Pending:
        __slots__ = ("args", "rg", "tab")

        def __init__(self, args):
            self.args = args
            self.rg = None
            self.tab = None

    def commit(self, p):
        o_ = self._rec_now(*p.args)
        o_.rg = p.rg
        return o_

    def sb(self, shape, dt, name=None):
        return self.arena.alloc(shape, dt, name)

    def R(self, x):
        k = x.key if isinstance(x, V) else (x.name, None)
        r = self.res.get(k)
        if r is None:
            r = self.res[k] = Res(k)
        return r

    def rec(self, eng, fn, outs, ins, dma=False, out=False):
        if self.defer is not None:
            p = Ctx._Pending((eng, fn, list(outs), list(ins), dma, out))
            self.defer.append(p)
            return p
        return self._rec_now(eng, fn, outs, ins, dma, out)

    def _rec_now(self, eng, fn, outs, ins, dma=False, out=False):
        reads = [self.R(i) for i in ins if i is not None and not _isnum(i)]
        writes = [self.R(o) for o in outs]
        writes += [r for r in reads if isinstance(r.name, str) and r.name.startswith("ps")]
        o_ = self.P.op(eng, fn, reads=reads, writes=writes, dma=dma, out=out)
        if DEBUG_LINES:
            import sys as _s
            f = _s._getframe(1)
            while f.f_code.co_name not in ("mixer_tile", "build", "layernorm") and f.f_back is not None:
                f = f.f_back
            o_.extra_wait = f.f_lineno
        return o_

    def mm(self, out, lhsT, rhs, start=True, stop=True):
        o, l, r = _ap(out), _ap(lhsT), _ap(rhs)
        op = self.rec(PE, lambda e: e.matmul(o, lhsT=l, rhs=r, start=start, stop=stop,
                                             skip_group_check=True), [out], [lhsT, rhs])
        kr = l.shape[0]
        if kr < 128:
            op.rg = (kr, l.base_partition())
        return op

    def tr(self, out, in_, ident):
        o, i, d = _ap(out), _ap(in_), _ap(ident)
        return self.rec(PE, lambda e: e.transpose(o, i, d), [out], [in_, ident])

    def act(self, out, in_, func, bias=None, scale=1.0):
        o, i = _ap(out), _ap(in_)
        kw = {}
        if bias is not None:
            kw["bias"] = _ap(bias)
        s = _ap(scale)
        p = self.rec(ACT, lambda e: e.activation(out=o, in_=i, func=func, scale=s, **kw), [out],
                     [in_, bias, scale])
        if isinstance(p, Ctx._Pending):
            p.tab = "sig" if func in (AF.Sigmoid, AF.Tanh) else ("exp" if func in (AF.Exp, AF.Ln) else None)
        return p

    def tt(self, eng, out, a, b, op):
        o, x, y = _ap(out), _ap(a), _ap(b)
        return self.rec(eng, lambda e: e.tensor_tensor(out=o, in0=x, in1=y, op=op), [out], [a, b])

    def ts(self, eng, out, a, s1, op0, s2=None, op1=None):
        o, x, v1, v2 = _ap(out), _ap(a), _ap(s1), _ap(s2)
        if op1 is None:
            f = lambda e: e.tensor_scalar(out=o, in0=x, scalar1=v1, scalar2=None, op0=op0)
        else:
            f = lambda e: e.tensor_scalar(out=o, in0=x, scalar1=v1, scalar2=v2, op0=op0, op1=op1)
        return self.rec(eng, f, [out], [a, s1, s2])

    def stt(self, eng, out, in0, scalar, in1, op0, op1):
        o, x, y, s = _ap(out), _ap(in0), _ap(in1), _ap(scalar)
        return self.rec(eng, lambda e: e.scalar_tensor_tensor(out=o, in0=x, scalar=s, in1=y, op0=op0, op1=op1),
                        [out], [in0, in1, scalar])

    def cp(self, eng, out, in_):
        o, i = _ap(out), _ap(in_)
        if eng == ACT:
            return self.rec(ACT, lambda e: e.activation(out=o, in_=i, func=AF.Copy), [out], [in_])
        return self.rec(eng, lambda e: e.tensor_copy(out=o, in_=i), [out], [in_])

    def recip(self, out, in_):
        o, i = _ap(out), _ap(in_)
        return self.rec(DVE, lambda e: e.reciprocal(out=o, in_=i), [out], [in_])

    def scan(self, out, d0, d1):
        o, a, b = _ap(out), _ap(d0), _ap(d1)
        return self.rec(DVE, lambda e: e.tensor_tensor_scan(out=o, data0=a, data1=b, initial=0.0,
                                                            op0=ALU.mult, op1=ALU.add), [out], [d0, d1])

    def red(self, eng, out, in_, op=ALU.add):
        o, i = _ap(out), _ap(in_)
        return self.rec(eng, lambda e: e.tensor_reduce(out=o, in_=i, axis=AX.X, op=op), [out], [in_])

    def memset(self, eng, out, val):
        o = _ap(out)
        return self.rec(eng, lambda e: e.memset(o, val), [out], [])

    def dma(self, eng, out, in_, out_final=False, extra_out=()):
        o, i = _ap(out), _ap(in_)
        return self.rec(eng, lambda e: e.dma_start(out=o, in_=i), [out, *extra_out], [in_], dma=True, out=out_final)

    def bn_stats(self, out, in_):
        o, i = _ap(out), _ap(in_)
        return self.rec(DVE, lambda e: e.bn_stats(out=o, in_=i), [out], [in_])

    def bn_aggr(self, out, in_):
        o, i = _ap(out), _ap(in_)
        return self.rec(DVE, lambda e: e.bn_aggr(out=o, in_=i), [out], [in_])


D = 1024
PJ = 3840
RWP = 1792
NTP = 16
NB = 16
TS = 8
DFF = 4096
ALPHA = 2.0 ** 0.25
CDEC = -float(np.exp(-0.5))
LN_EPS = 1e-5
GN_EPS = 64e-5
RMS_EPS = 1e-6
ARENA_BYTES = 212800


def make_consts():
    s = np.arange(128)[:, None]
    t = np.arange(128)[None, :]
    cols = {}
    cols["ident"] = (s == t)
    cols["mS_p"] = (s < t)
    cols["mI_p"] = (s <= t)
    cols["mST_p"] = (t < s)
    same = (s // TS) == (t // TS)
    cols["mS_s"] = (s < t) & same
    cols["mI_s"] = (s <= t) & same
    cols["mST_s"] = (t < s) & same
    cols["reset_p"] = np.broadcast_to(t != 0, (128, 128))
    cols["reset_s"] = np.broadcast_to((t % TS) != 0, (128, 128))
    cols["bdones"] = (s // 64) == (t // 64)
    cols["hsel"] = (s // 64) == np.arange(2)[None, :]
    cols["cm"] = (s // TS) == np.arange(NB)[None, :]
    cols["i64s"] = (s % 64) == np.arange(64)[None, :]
    off = {}
    parts = []
    o = 0
    for k, v in cols.items():
        v = np.asarray(v, np.float32)
        off[k] = (o, o + v.shape[1])
        o += v.shape[1]
        parts.append(v)
    return np.ascontiguousarray(np.concatenate(parts, axis=1)), off


CONSTS, COFF = make_consts()
NCONST = CONSTS.shape[1]


class _Stop(Exception):
    pass


def build(NT=NTP, SAMPLE=True, DBG=False, STAGE=99):
    from contextlib import ExitStack
    nc = bass.Bass("TRN2", target_bir_lowering=False)

    def din(name, shape):
        return nc.dram_tensor(name, list(shape), F32, kind="ExternalInput").ap()

    def dout(name, shape):
        return nc.dram_tensor(name, list(shape), F32, kind="ExternalOutput").ap()

    xp = din("xp", [NTP * 128, D]); xsm = din("xs", [128, D])
    srw = din("srw", [NB, 8, 64, 64]); shg = din("shg", [NB, 4, 128, 128]); ssh = din("ssh", [NB, RWP])
    w_in = din("w_in", [D, PJ]); shift_mu = din("shift_mu", [RWP]); w0 = din("w0", [512])
    w1u = din("w1u", [64, 512]); a0 = din("a0", [512]); a1u = din("a1u", [64, 512]); g1u = din("g1u", [128, 512])
    k_k = din("k_k", [512]); k_a = din("k_a", [512]); r_k = din("r_k", [512])
    ln_x_w = din("ln_x_w", [512]); ln_x_b = din("ln_x_b", [512]); lb_logits = din("lb_logits", [2, 512])
    hg_norm_w = din("hg_norm_w", [512]); w_out = din("w_out", [D, D]); ln1_g = din("ln1_g", [D]); ln1_b = din("ln1_b", [D])
    w_up = din("w_up", [D, DFF]); w_down = din("w_down", [DFF, D]); ln2_g = din("ln2_g", [D]); ln2_b = din("ln2_b", [D])
    cst_d = din("consts", [128, NCONST])
    yp = dout("yp", [NTP * 128, D]); ys = dout("ys", [128, D])
    rwp = dout("rwp", [8, 64, 64]); rws = dout("rws", [NB, 8, 64, 64])
    hgp = dout("hgp", [4, 128, 128]); hgs = dout("hgs", [NB, 4, 128, 128])
    shp = dout("shp", [RWP]); shs = dout("shs", [NB, RWP])
    h1scr = nc.dram_tensor("h1scr", [(NTP + 1) * 128, D], F32).ap()
    if DBG:
        dbg_d = dout("dbg", [128, 4096])

    def row(v):
        return v.rearrange("(o n) -> o n", o=1)

    P = Prog()
    with ExitStack() as es:
        arena = Arena(nc, es, ARENA_BYTES)
        C = Ctx(nc, P, arena)
        sb = C.sb
        ps = [V(es.enter_context(nc.psum_tensor(f"ps{i}", [128, 512], F32))[:], f"ps{i}") for i in range(8)]

        def psv(i, a):
            return ps[i].rearrange("p (a t) -> p a t", a=a)

        def psb(i):
            return ps[i].bitcast(BF16)

        def bc3(ap2, n):
            return ap2.unsqueeze(2).to_broadcast([ap2.shape[0], ap2.shape[1], n])

        def v3(t):
            return t.rearrange("p (a t) -> p a t", a=4)

        ident_bf = sb([128, 128], BF16, "ident_bf")
        lng = sb([128, D], F32, "lng"); lnb = sb([128, D], F32, "lnb")
        C.dma(SP, lng, row(ln1_g).partition_broadcast(128))
        C.dma(SP, lnb, row(ln1_b).partition_broadcast(128))
        bnst = sb([128, 12], F32, "bnst"); mv = sb([128, 2], F32, "mv"); rstd1 = sb([128, 1], F32, "rstd1")
        fence_ln = sb([128, 2], F32, "fence_ln")
        if DBG:
            dbg = sb([128, 4096], F32, "dbg")
            C.memset(POOL, dbg, 0.0)
            dbg_pos = [0]
            dbg_map = {}

            def dump(name, v, n):
                a = dbg_pos[0]
                shape = list(v.shape)
                dst = dbg[0:shape[0], a:a + n]
                if len(shape) == 3:
                    dst = dst.rearrange("p (a b) -> p a b", a=shape[1])
                C.cp(POOL, dst, v)
                dbg_pos[0] += n
                dbg_map[name] = (a, n, shape)
            build.dbg_map = dbg_map
        else:
            def dump(name, v, n):
                return None
        phase_mark = arena.off

        def layernorm(src, dst, eps):
            for half in range(2):
                C.bn_stats(bnst[:, half * 6:(half + 1) * 6], src[:, half * 512:(half + 1) * 512])
            C.bn_aggr(mv, bnst)
            C.act(rstd1, mv[:, 1:2], AF.Ln, bias=eps)
            C.act(rstd1, rstd1, AF.Exp, scale=-0.5)
            C.ts(DVE, dst, src, mv[:, 0:1], ALU.subtract, rstd1[:, 0:1], ALU.mult)
            halves = []
            for eng_, hsl in ((DVE, slice(0, 512)), (DVE, slice(512, 1024))):
                dk = dst[:, hsl].k(hsl.start)
                halves.append(dk)
                o, a_, g_, b_ = _ap(dk), _ap(dst[:, hsl]), _ap(lng[:, hsl]), _ap(lnb[:, hsl])
                C.rec(eng_, lambda e, o=o, a_=a_, g_=g_: e.tensor_tensor(out=o, in0=a_, in1=g_, op=ALU.mult), [dk], [dst, lng])
                C.rec(eng_, lambda e, o=o, b_=b_: e.tensor_tensor(out=o, in0=o, in1=b_, op=ALU.add), [dk], [dk, lnb])
            C.rec(POOL, lambda e: e.memset(_ap(fence_ln), 0.0), [dst, fence_ln], halves)

        F32_CONSTS = ("ident", "reset_p", "reset_s", "hsel", "cm", "i64s")
        BF_CONSTS = ("mS_p", "mI_p", "mST_p", "mS_s", "mI_s", "mST_s", "bdones")
        cviews = {}
        nf = sum(COFF[k][1] - COFF[k][0] for k in F32_CONSTS)
        nb_ = sum(COFF[k][1] - COFF[k][0] for k in BF_CONSTS)
        cstf = sb([128, nf], F32, "cstf"); cstb = sb([128, nb_], BF16, "cstb")
        o_ = 0
        for k_ in F32_CONSTS:
            a, b = COFF[k_]
            C.dma(SP, cstf[:, o_:o_ + b - a].k(k_), cst_d[:, a:b])
            cviews[k_] = cstf[:, o_:o_ + b - a].k(k_)
            o_ += b - a
        o_ = 0
        for k_ in BF_CONSTS:
            a, b = COFF[k_]
            C.dma(POOL, cstb[:, o_:o_ + b - a].k(k_), cst_d[:, a:b])
            cviews[k_] = cstb[:, o_:o_ + b - a].k(k_)
            o_ += b - a

        def cc(name):
            return cviews[name]

        C.cp(POOL, ident_bf, cc("ident"))
        bdones_bf = cc("bdones")
        mu14 = sb([128, 14], F32, "mu14")
        kk4 = sb([128, 4], F32, "kk4"); ka4 = sb([128, 4], F32, "ka4"); rk4 = sb([128, 4], F32, "rk4")
        w04 = sb([128, 4], F32, "w04"); a04 = sb([128, 4], F32, "a04")
        lbl = sb([128, 2, 4], F32, "lbl")
        with nc.allow_non_contiguous_dma(reason="tiny per-channel parameter vectors"):
            C.dma(SP, mu14, shift_mu.rearrange("(c p) -> p c", p=128))
            C.dma(SP, kk4, k_k.rearrange("(c p) -> p c", p=128))
            C.dma(SP, ka4, k_a.rearrange("(c p) -> p c", p=128))
            C.dma(SP, rk4, r_k.rearrange("(c p) -> p c", p=128))
            C.dma(SP, w04, w0.rearrange("(c p) -> p c", p=128))
            C.dma(SP, a04, a0.rearrange("(c p) -> p c", p=128))
            C.dma(SP, lbl, lb_logits.rearrange("l (c p) -> p l c", p=128))
        WA = sb([128, 512], BF16, "WA")
        C.dma(POOL, WA[0:64, :].k("lo"), w1u)
        C.dma(POOL, WA[64:128, :].k("hi"), a1u)
        g1u_bf = sb([128, 512], BF16, "g1u_bf")
        C.dma(POOL, g1u_bf, g1u)
        lnxw = sb([128, 512], BF16, "lnxw"); lnxb = sb([128, 512], BF16, "lnxb"); hgw = sb([128, 512], BF16, "hgw")
        C.dma(POOL, lnxw, row(ln_x_w).partition_broadcast(128))
        C.dma(POOL, lnxb, row(ln_x_b).partition_broadcast(128))
        C.dma(POOL, hgw, row(hg_norm_w).partition_broadcast(128))
        lb4 = sb([128, 4], F32, "lb4"); oml4 = sb([128, 4], F32, "oml4"); etmp = sb([128, 4], F32, "etmp")
        C.tt(DVE, etmp, lbl[:, 1, :], lbl[:, 0, :], ALU.subtract)
        C.act(etmp, etmp, AF.Exp)
        C.ts(DVE, lb4, etmp, 1.0, ALU.add)
        C.recip(lb4, lb4)
        C.tt(DVE, oml4, etmp, lb4, ALU.mult)
        RKsel = sb([128, 4, 2], BF16, "RKsel")
        C.tt(DVE, RKsel, bc3(rk4, 2), cc("hsel").unsqueeze(1).to_broadcast([128, 4, 2]), ALU.mult)

        ov_lo = arena.off
        w_in_bf = sb([128, 8, PJ], BF16, "w_in_bf")
        ov_hi = arena.off
        WIN_BLK = [(0, 512), (512, 1024), (1024, 1536), (1536, 1792), (1792, 2304), (2304, 2816), (2816, 3328), (3328, 3840)]

        def win_key(col):
            for i_, (a_, b_) in enumerate(WIN_BLK):
                if a_ <= col < b_:
                    return i_
        def issue_w_in():
            for i_, (a_, b_) in enumerate(WIN_BLK):
                C.dma(POOL, w_in_bf[:, :, a_:b_].k(i_), w_in[:, a_:b_].rearrange("(kc p) n -> p kc n", p=128))
        w_out_bf = sb([128, 8, D], BF16, "w_out_bf")
        def issue_w_out():
            for kc in range(8):
                C.dma(POOL, w_out_bf[:, kc, :].k(kc), w_out[kc * 128:(kc + 1) * 128, :])

        ST = sb([128, 4, 64], F32, "ST"); ST_bf = sb([128, 4, 64], BF16, "ST_bf")
        SH = sb([128, 4, 128], F32, "SH"); SH_bf = sb([128, 4, 128], BF16, "SH_bf")
        plast = sb([128, 14], F32, "plast")
        for t_ in (ST, ST_bf, SH, SH_bf, plast):
            C.memset(POOL, t_, 0.0)

        def prompt_state_outputs():
            with nc.allow_non_contiguous_dma(reason="tiny state vector"):
                C.dma(SP, shp.rearrange("(c p) -> p c", p=128), plast, out_final=True)
            C.dma(SP, hgp.rearrange("h k v -> k h v"), SH, out_final=True)
            identf = cc("ident")
            for pr in range(4):
                C.tr(ps[2][0:64, pr * 128:(pr + 1) * 128], ST[:, pr, :], identf)
            rwo = T[0]
            C.cp(DVE, rwo[0:64, :], ps[2][0:64, :])
            C.dma(SP, rwp.rearrange("h v j -> v h j"), rwo[0:64, :].rearrange("p (h j) -> p h j", h=8), out_final=True)
            if DBG:
                C.dma(SP, dbg_d, dbg, out_final=True)


        x_t = [sb([128, D], F32, f"x_t{i}") for i in range(2)]
        x_bf = sb([128, D], BF16, "x_bf")
        xT = sb([128, 8, 128], BF16, "xT")
        pr_ = sb([128, 14, 129], F32, "pr")
        xs = sb([128, 14, 128], F32, "xsft")
        T = [sb([128, 512], F32, f"T{i}") for i in range(10)]
        z12 = sb([128, 128], BF16, "z12"); sg_bf = sb([128, 128], BF16, "sg_bf"); tqb = sb([128, 512], BF16, "tqb")
        bhT = sb([128, 4, 128], BF16, "bhT"); khT = sb([128, 4, 128], BF16, "khT"); vT = sb([128, 4, 128], BF16, "vT")
        khTb = sb([128, 4, 128], BF16, "khTb")
        fence2 = sb([128, 2], F32, "fence2")
        HB = []
        for i in range(2):
            HB.append(dict(
                AR=sb([128, 4, 2, 128], BF16, f"AR{i}"), bT=sb([128, 4, 128], BF16, f"bT{i}"), kT=sb([128, 4, 128], BF16, f"kT{i}"),
                A_tm=sb([128, 512], BF16, f"A_tm{i}"), Bh_tm=sb([128, 512], BF16, f"Bh_tm{i}"),
                Kh_tm=sb([128, 512], BF16, f"Kh_tm{i}"), V_tm=sb([128, 512], BF16, f"V_tm{i}"),
                qTb=sb([128, 4, 128], BF16, f"qTb{i}"), kTb=sb([128, 4, 128], BF16, f"kTb{i}"),
                khat_tm=sb([128, 512], BF16, f"khat_tm{i}"), i_tm=sb([128, 512], BF16, f"i_tm{i}"),
                gs_t=sb([128, 512], BF16, f"gs_t{i}"), g_tm=sb([128, 512], BF16, f"g_tm{i}"),
                bon8=sb([128, 8], F32, f"bon8{i}"), gC=sb([128, 4, NB], F32, f"gC{i}"), decH=sb([128, 4, NB], F32, f"decH{i}")))
        Pm = sb([128, 8, 128], BF16, "Pm"); Tm = sb([128, 8, 128], BF16, "Tm"); RR = sb([128, 8, 128], BF16, "RR")
        NrbT = sb([128, 8, 128], BF16, "NrbT"); AkT = sb([128, 8, 128], BF16, "AkT"); NrkT = sb([128, 8, 128], BF16, "NrkT")
        TTf = sb([128, 8, 128], BF16, "TTf")
        W1T = sb([128, 4, 128], BF16, "W1T")
        Z_tm = sb([128, 512], BF16, "Z_tm"); U_tm = sb([128, 512], BF16, "U_tm")
        st16 = sb([128, 16], F32, "st16"); m8 = sb([128, 8], F32, "m8"); r8 = sb([128, 8], F32, "r8")
        o_all = sb([128, D], BF16, "o_all"); oT = sb([128, 8, 128], BF16, "oT")
        h1pre = sb([128, D], F32, "h1pre")
        attT = sb([128, 4, 128], BF16, "attT")
        s4 = sb([128, 4], F32, "s4"); rr4 = sb([128, 4], F32, "rr4")
        BT0 = h1pre[:, 0:512]; BT1 = h1pre[:, 512:1024]
        SHtmp = v3(BT1)
        STtmp = BT1[:, 0:256].rearrange("p (a v) -> p a v", a=4)
        identb4 = ident_bf.unsqueeze(1).to_broadcast([128, 4, 128])
        save_off = arena.off
        arena.off = ov_lo
        scrA = [sb([128, 2048], F32, f"scrA{i}") for i in range(2)]
        S0T32 = sb([128, NB, 4, 64], F32, "S0T32")
        S0Tb = sb([128, NB, 4, 64], BF16, "S0Tb")
        sshT = sb([128, 14, NB], F32, "sshT"); lastp = sb([128, 14, NB], F32, "lastp")
        EW = [sb([128, 4, 128], BF16, f"EW{i}") for i in range(2)]
        ER = [sb([128, 4, 128], BF16, f"ER{i}") for i in range(2)]
        EQ = [sb([128, 4, 128], BF16, f"EQ{i}") for i in range(2)]
        Ub = sb([128, 512], BF16, "Ub"); Vb = sb([128, 512], BF16, "Vb"); khb = sb([128, 512], BF16, "khb")
        Dg = sb([128, 4, 64], F32, "Dg")
        S0h = [sb([128, 4, 128], F32, f"S0h{i}") for i in range(2)]
        S0hb = sb([128, 4, 128], BF16, "S0hb")
        Sn = sb([128, 512], F32, "Sn")
        fence_t = sb([128, 2], F32, "fence_t")
        assert arena.off <= ov_hi, (arena.off, ov_hi)
        ov_bufs = scrA + [S0T32, S0Tb, sshT, lastp] + EW + ER + EQ + [Ub, Vb, khb, Dg] + S0h + [S0hb, Sn, fence_t]
        arena.off = save_off
        ssh_tm = scrA[0][0:NB, 0:RWP]
        hh_order = (0, 2, 4, 6, 1, 3, 5, 7)
        heads_of = [(0, 1, 2, 3), (4, 5, 6, 7)]
        PA, PB_ = 6, 7

        cb = cc

        first_tile = [True]

        def cfg(sample):
            sfx = "_s" if sample else "_p"
            return dict(nch=NB if sample else 1, Cn=TS if sample else 128, L=3 if sample else 7,
                        mS=cb("mS" + sfx), mI=cb("mI" + sfx), mST=cb("mST" + sfx), reset=cc("reset" + sfx))

        def stageA(ti, x_src, sample):
            g_ = cfg(sample)
            nch, Cn, reset = g_["nch"], g_["Cn"], g_["reset"]
            H = HB[ti % 2]
            AR, bT, kT = H["AR"], H["bT"], H["kT"]
            xt = x_t[ti % 2]
            C.dma(SP, xt, x_src)
            C.dma(POOL, x_bf, x_src)
            if first_tile[0]:
                first_tile[0] = False
                issue_w_in()
            for kc in range(8):
                C.tr(psb(PB_)[:, kc * 128:(kc + 1) * 128], x_bf[:, kc * 128:(kc + 1) * 128], ident_bf)
            C.cp(ACT, xT, psb(PB_).rearrange("p (a t) -> p a t", a=8))
            yield
            sig, kq, fgl, bcs, eb, enb, ebl, sq_, eg = T[1], T[4], T[0], T[2], T[3], T[5], T[6], T[7], T[8]

            def proj_fm(c0, n, bank):
                for j in range(n):
                    c = c0 + j
                    for kc in range(8):
                        C.mm(ps[bank][:, j * 128:(j + 1) * 128], w_in_bf[:, kc, c * 128:(c + 1) * 128].k(win_key(c * 128)),
                             xT[:, kc, :], start=(kc == 0), stop=(kc == 7))

            def proj_tm(col0, bank):
                for kc in range(8):
                    C.mm(ps[bank], xT[:, kc, :], w_in_bf[:, kc, col0:col0 + 512].k(win_key(col0)), start=(kc == 0), stop=(kc == 7))

            proj_fm(0, 4, PA); C.cp(ACT, pr_[:, 0:4, 1:129], psv(PA, 4)); yield
            proj_fm(4, 4, PB_); C.cp(ACT, pr_[:, 4:8, 1:129], psv(PB_, 4)); yield
            proj_fm(8, 4, PA); C.cp(ACT, pr_[:, 8:12, 1:129], psv(PA, 4)); yield
            proj_fm(12, 2, PB_); C.cp(ACT, pr_[:, 12:14, 1:129], psv(PB_, 4)[:, 0:2, :]); yield
            prev, cur = pr_[:, :, 0:128], pr_[:, :, 1:129]
            if not sample:
                C.cp(DVE, pr_[:, :, 0:1], plast.unsqueeze(2))
                C.cp(DVE, plast.unsqueeze(2), pr_[:, :, 128:129])
            else:
                C.memset(POOL, pr_[:, :, 0:1], 0.0)
            proj_fm(14, 4, PA)
            C.act(sq_, ps[PA], AF.Sigmoid)
            C.tt(DVE, sq_, ps[PA], sq_, ALU.mult)
            yield
            proj_fm(18, 4, PB_)
            C.act(sig, ps[PB_], AF.Sigmoid)
            yield
            proj_tm(RWP + 1024, PA)
            C.cp(ACT, H["i_tm"], ps[PA])
            yield
            proj_tm(RWP + 1536, PB_)
            C.act(eg, ps[PB_], AF.Sigmoid)
            C.tt(DVE, H["gs_t"], ps[PB_], eg, ALU.mult)
            yield
            if sample:
                C.rec(POOL, lambda e: e.memset(_ap(fence_t), 0.0),
                      [w_in_bf[:, :, a_:b_].k(i_) for i_, (a_, b_) in enumerate(WIN_BLK)] + ov_bufs
                      + [S0T32[:, b, :, :].k(b) for b in range(NB)] + [S0Tb[:, b, :, :].k(b) for b in range(NB)], [])
                for t_ in EW + ER + EQ:
                    C.memset(POOL, t_, 0.0)
                C.dma(SP, ssh_tm, ssh)
                for c in range(14):
                    C.tr(ps[PA][:, c * NB:(c + 1) * NB], ssh_tm[:, c * 128:(c + 1) * 128], cc("ident")[0:NB, 0:NB])
                C.cp(DVE, sshT, ps[PA][:, 0:14 * NB].rearrange("p (c b) -> p c b", c=14))
            for eng_, c0, c1 in ((DVE, 0, 8), (DVE, 8, 14)):
                C.tt(eng_, xs[:, c0:c1, :].k(c0), prev[:, c0:c1, :], cur[:, c0:c1, :], ALU.subtract)
                C.tt(eng_, xs[:, c0:c1, :].k(c0), xs[:, c0:c1, :].k(c0), bc3(mu14[:, c0:c1], 128), ALU.mult)
                C.tt(eng_, xs[:, c0:c1, :].k(c0), xs[:, c0:c1, :].k(c0), cur[:, c0:c1, :], ALU.add)
            C.rec(POOL, lambda e: e.memset(_ap(fence2), 0.0), [xs, fence2], [xs[:, 0:8, :].k(0), xs[:, 8:14, :].k(8)])
            if sample:
                cur4 = cur.rearrange("p c (b t) -> p c b t", t=TS)
                xs4 = xs.rearrange("p c (b t) -> p c b t", t=TS)
                cur0, xs0 = cur4[:, :, :, 0], xs4[:, :, :, 0]
                C.tt(DVE, xs0, sshT, cur0, ALU.subtract)
                C.tt(DVE, xs0, xs0, bc3(mu14, NB), ALU.mult)
                C.tt(DVE, xs0, xs0, cur0, ALU.add)
                C.cp(DVE, lastp, cur4[:, :, :, TS - 1])
                for g0 in range(0, 14, 4):
                    bk = PA if (g0 // 4) % 2 == 0 else PB_
                    n = min(4, 14 - g0)
                    for j in range(n):
                        C.tr(ps[bk][0:NB, j * 128:(j + 1) * 128], lastp[:, g0 + j, :], cc("ident"))
                    C.cp(ACT, ssh_tm[:, g0 * 128:(g0 + n) * 128], ps[bk][0:NB, 0:n * 128])
                C.dma(SP, shs, ssh_tm, out_final=True)
            yield
            r_ = xs[:, 0:4, :]; k_ = xs[:, 4:8, :]; v_ = xs[:, 8:12, :]
            C.act(z12[0:64, :], xs[0:64, 12, :], AF.Tanh)
            C.act(sg_bf, xs[:, 13, :], AF.Sigmoid)
            C.cp(DVE, z12[64:128, :], xs[64:128, 12, :])
            for pr in range(4):
                sl = slice(pr * 128, (pr + 1) * 128)
                C.mm(ps[PA][:, sl], WA[0:64, sl].k("lo"), z12[0:64, :])
            for pr in range(4):
                sl = slice(pr * 128, (pr + 1) * 128)
                C.mm(ps[PB_][:, sl], WA[64:128, sl].k("hi"), z12[64:128, :])
            sw, alr = T[9], T[8]
            for pr in range(4):
                sl = slice(pr * 128, (pr + 1) * 128)
                C.act(sw[:, sl], ps[PA][:, sl], AF.Sigmoid, bias=w04[:, pr:pr + 1])
            C.tt(DVE, v3(kq), v3(sig), bc3(oml4, 128), ALU.mult)
            C.tt(DVE, v3(fgl), v3(kq), bc3(lb4, 128), ALU.add)
            C.tt(DVE, v3(kq), bc3(oml4, 128), v3(kq), ALU.subtract)
            yield
            for pr in range(4):
                sl = slice(pr * 128, (pr + 1) * 128)
                C.act(alr[:, sl], ps[PB_][:, sl], AF.Sigmoid, bias=a04[:, pr:pr + 1])
            C.mm(ps[PA], sg_bf, g1u_bf)
            C.cp(ACT, H["g_tm"], ps[PA])
            yield
            C.act(fgl, fgl, AF.Ln)
            for h in range(4):
                C.scan(bcs[:, h * 128:(h + 1) * 128], reset, fgl[:, h * 128:(h + 1) * 128])
            C.act(eb, bcs, AF.Exp)
            C.act(enb, bcs, AF.Exp, scale=-1.0)
            bc4 = bcs.rearrange("p (a n c) -> p a n c", a=4, n=nch)
            C.tt(DVE, ebl.rearrange("p (a n c) -> p a n c", a=4, n=nch),
                 bc4[:, :, :, Cn - 1:Cn].to_broadcast([128, 4, nch, Cn]), bc4, ALU.subtract)
            C.act(ebl, ebl, AF.Exp)
            yield
            C.cp(DVE, H["decH"][:, :, 0:nch].unsqueeze(3),
                 eb.rearrange("p (a n c) -> p a n c", a=4, n=nch)[:, :, :, Cn - 1:Cn])
            C.tt(DVE, H["qTb"], v3(sq_), v3(eb), ALU.mult)
            C.tt(DVE, H["kTb"], v3(kq), v3(enb), ALU.mult)
            C.tt(DVE, khTb, v3(kq), v3(ebl), ALU.mult)
            yield
            cumS, gex, gin, ginv, glast, kkk, tq, k2 = T[2], T[3], T[4], T[5], T[6], T[7], T[0], T[1]
            for pr in range(4):
                C.scan(cumS[:, pr * 128:(pr + 1) * 128], reset, sw[:, pr * 128:(pr + 1) * 128])
            C.tt(DVE, gex, cumS, sw, ALU.subtract)
            C.act(gex, gex, AF.Exp, scale=CDEC)
            C.act(gin, cumS, AF.Exp, scale=CDEC)
            C.act(ginv, cumS, AF.Exp, scale=-CDEC)
            cs4 = cumS.rearrange("p (a n c) -> p a n c", a=4, n=nch)
            C.tt(DVE, glast.rearrange("p (a n c) -> p a n c", a=4, n=nch),
                 cs4[:, :, :, Cn - 1:Cn].to_broadcast([128, 4, nch, Cn]), cs4, ALU.subtract)
            C.act(glast, glast, AF.Exp, scale=CDEC)
            C.cp(DVE, H["gC"][:, :, 0:nch].unsqueeze(3),
                 gin.rearrange("p (a n c) -> p a n c", a=4, n=nch)[:, :, :, Cn - 1:Cn])
            yield
            C.tt(DVE, v3(kkk), k_, bc3(kk4, 128), ALU.mult)
            C.act(tqb, kkk, AF.Square)
            for pr in range(4):
                sl = slice(pr * 128, (pr + 1) * 128)
                C.mm(ps[PB_][:, sl], bdones_bf, tqb[:, sl])
            C.act(tq, ps[PB_], AF.Ln, bias=1e-24)
            C.act(tq, tq, AF.Exp, scale=-0.5)
            C.tt(DVE, kkk, kkk, tq, ALU.mult)
            C.stt(DVE, v3(k2), v3(alr), -1.0, bc3(ka4, 128), ALU.add, ALU.mult)
            C.stt(DVE, v3(k2), v3(k2), 1.0, k_, ALU.add, ALU.mult)
            C.tt(DVE, alr, kkk, alr, ALU.mult)
            b_ = alr
            yield
            C.tt(DVE, AR[:, :, 1, :], r_, v3(gin), ALU.mult)
            C.stt(DVE, AR[:, :, 0, :], v3(kkk), -1.0, v3(gex), ALU.mult, ALU.mult)
            C.tt(DVE, bT, v3(b_), v3(ginv), ALU.mult)
            C.tt(DVE, kT, v3(k2), v3(ginv), ALU.mult)
            C.tt(DVE, bhT, v3(b_), v3(glast), ALU.mult)
            C.tt(DVE, khT, v3(k2), v3(glast), ALU.mult)
            C.cp(ACT, vT, v_)
            C.tt(DVE, v3(tqb), r_, v3(k2), ALU.mult)
            for pr in range(4):
                C.mm(ps[PA][:, pr * 2:(pr + 1) * 2], tqb[:, pr * 128:(pr + 1) * 128], RKsel[:, pr, :])
            C.cp(DVE, H["bon8"], ps[PA][:, 0:8])
            yield
            for (src, bank, half) in ((lambda pr: AR[:, pr, 0, :], PA, 0), (lambda pr: bhT[:, pr, :], PA, 1),
                                      (lambda pr: khT[:, pr, :], PB_, 0), (lambda pr: vT[:, pr, :], PB_, 1)):
                for pr in range(4):
                    o0 = half * 512 + pr * 128
                    C.tr(psb(bank)[:, o0:o0 + 128], src(pr), ident_bf)
            C.cp(ACT, H["A_tm"], psb(PA)[:, 0:512]); C.cp(ACT, H["Bh_tm"], psb(PA)[:, 512:1024])
            C.cp(ACT, H["Kh_tm"], psb(PB_)[:, 0:512]); C.cp(ACT, H["V_tm"], psb(PB_)[:, 512:1024])
            yield
            for h in range(4):
                C.tr(psb(PA)[:, h * 128:(h + 1) * 128], khTb[:, h, :], ident_bf)
            C.cp(ACT, H["khat_tm"], psb(PA)[:, 0:512])
            yield

        def stageB(ti, h1_dst, sample, part="all"):
            g_ = cfg(sample)
            nch, Cn, L = g_["nch"], g_["Cn"], g_["L"]
            mSb = g_["mS"].unsqueeze(1).to_broadcast([128, 4, 128])
            mIb = g_["mI"].unsqueeze(1).to_broadcast([128, 4, 128])
            mSTb = g_["mST"].unsqueeze(1).to_broadcast([128, 4, 128])
            H = HB[ti % 2]
            AR, bT, kT, A_tm, Bh_tm, Kh_tm, V_tm = H["AR"], H["bT"], H["kT"], H["A_tm"], H["Bh_tm"], H["Kh_tm"], H["V_tm"]
            qTb, kTb, khat_tm, i_tm, gs_t, g_tm, bon8, gC, decH = (H["qTb"], H["kTb"], H["khat_tm"], H["i_tm"], H["gs_t"],
                                                                    H["g_tm"], H["bon8"], H["gC"], H["decH"])
            xt = x_t[ti % 2]
            if part in ("all", "hgrn"):
                bA, bO = (6, 7) if sample else (3, 4)
                for h in range(4):
                    C.mm(ps[bA][:, h * 128:(h + 1) * 128], kTb[:, h, :], qTb[:, h, :])
                C.tt(DVE, attT, psv(bA, 4), mIb, ALU.mult)
                if not sample:
                    for h in range(4):
                        hsl = slice(h * 128, (h + 1) * 128)
                        C.mm(ps[4][:, hsl], attT[:, h, :], i_tm[:, hsl], start=True, stop=False)
                        C.mm(ps[4][:, hsl], qTb[:, h, :], SH_bf[:, h, :], start=False, stop=True)
                    for h in range(4):
                        hsl = slice(h * 128, (h + 1) * 128)
                        C.mm(ps[5][:, hsl], khat_tm[:, hsl], i_tm[:, hsl])
                    C.tt(DVE, SHtmp, SH, bc3(decH[:, :, 0], 128), ALU.mult)
                    C.tt(DVE, SH, SHtmp, psv(5, 4), ALU.add)
                    C.cp(ACT, SH_bf, SH)
                else:
                    for h in range(4):
                        hsl = slice(h * 128, (h + 1) * 128)
                        C.mm(ps[bO][:, hsl], attT[:, h, :], i_tm[:, hsl], start=(h == 0), stop=False)
                    for b in range(NB):
                        s0 = S0h[b % 2]
                        csl = slice(b * TS, (b + 1) * TS)
                        C.dma(SP, s0, shg[b].rearrange("h k v -> k h v"))
                        C.cp(ACT, S0hb, s0)
                        eq = EQ[b % 2]
                        C.cp(POOL, eq[:, :, csl], qTb[:, :, csl])
                        for h in range(4):
                            C.mm(ps[bO][:, h * 128:(h + 1) * 128], eq[:, h, :], S0hb[:, h, :], start=False, stop=False)
                        C.memset(POOL, eq[:, :, csl], 0.0)
                        C.ts(DVE, khb, khat_tm, cc("cm")[:, b:b + 1], ALU.mult)
                        bank = bA
                        for h in range(4):
                            hsl = slice(h * 128, (h + 1) * 128)
                            C.mm(ps[bank][:, hsl], khb[:, hsl], i_tm[:, hsl])
                        C.tt(DVE, s0, s0, bc3(decH[:, :, b], 128), ALU.mult)
                        C.tt(DVE, s0, s0, psv(bank, 4), ALU.add)
                        C.dma(SP, hgs[b].rearrange("h k v -> k h v"), s0, out_final=True)
                        yield
                osq = T[0] if sample else BT0
                C.act(osq, ps[bO], AF.Square)
                C.red(DVE, s4, v3(osq))
                C.act(rr4, s4, AF.Ln, scale=1.0 / 128, bias=RMS_EPS)
                C.act(rr4, rr4, AF.Exp, scale=-0.5)
                C.tt(DVE, v3(osq), psv(bO, 4), bc3(rr4, 128), ALU.mult)
                C.tt(DVE, osq, osq, hgw, ALU.mult)
                C.tt(DVE, o_all[:, 512:1024], osq, gs_t, ALU.mult)
                yield
            if part == "hgrn":
                return
            if part in ("all", "rwkv"):
                if sample:
                    for g in range(4):
                        sa = scrA[g % 2]
                        nat = sa[0:64, :].rearrange("p (b n) -> p b n", b=4)
                        C.dma(SP, nat.rearrange("p b (h j) -> p b h j", h=8),
                              srw[g * 4:(g + 1) * 4].rearrange("b h v j -> v b h j"))
                        for bb in range(4):
                            b = g * 4 + bb
                            bank = b % 2
                            for pr in range(4):
                                C.tr(ps[bank][:, (bb % 2) * 256 + pr * 64:(bb % 2) * 256 + (pr + 1) * 64],
                                     nat[:, bb, pr * 128:(pr + 1) * 128], cc("ident")[0:64, 0:64])
                            C.cp(ACT, S0T32[:, b, :, :].k(b),
                                 ps[bank][:, (bb % 2) * 256:(bb % 2) * 256 + 256].rearrange("p (a v) -> p a v", a=4))
                            C.cp(DVE, S0Tb[:, b, :, :].k(b), S0T32[:, b, :, :].k(b))
                        yield
                for hg in range(2):
                    for i, h in enumerate(heads_of[hg]):
                        pr, hh = h // 2, h % 2
                        rows = slice(64 * hh, 64 * hh + 64)
                        sl = slice(i * 128, (i + 1) * 128)
                        C.mm(ps[0][:, sl], bT[rows, pr, :], AR[rows, pr, 0, :])
                        C.mm(ps[1][:, sl], bT[rows, pr, :], AR[rows, pr, 1, :])
                        C.mm(ps[2][:, sl], kT[rows, pr, :], AR[rows, pr, 0, :])
                        C.mm(ps[3][:, sl], kT[rows, pr, :], AR[rows, pr, 1, :])
                        C.mm(ps[4][:, sl], AR[rows, pr, 0, :], bT[rows, pr, :])
                    hs = slice(4 * hg, 4 * hg + 4)
                    C.tt(DVE, Pm[:, hs, :], psv(0, 4), mSb, ALU.mult)
                    C.tt(DVE, NrbT[:, hs, :], psv(1, 4), mIb, ALU.mult)
                    C.tt(DVE, AkT[:, hs, :], psv(2, 4), mSb, ALU.mult)
                    C.tt(DVE, NrkT[:, hs, :], psv(3, 4), mIb, ALU.mult)
                    C.tt(DVE, RR[:, hs, :], psv(4, 4), mSTb, ALU.mult)
                    C.tt(DVE, Tm[:, hs, :], Pm[:, hs, :], identb4, ALU.add)
                    yield
                for k in range(L):
                    for hg in range(2):
                        b0 = 3 * hg
                        hs = slice(4 * hg, 4 * hg + 4)
                        for i, h in enumerate(heads_of[hg]):
                            sl = slice(i * 128, (i + 1) * 128)
                            if k < L - 1:
                                C.mm(ps[b0][:, sl], RR[:, h, :], Pm[:, h, :])
                            if k >= 1:
                                C.mm(ps[b0 + 1][:, sl], RR[:, h, :], Tm[:, h, :])
                            if k < L - 1:
                                C.mm(ps[b0 + 2][:, sl], Pm[:, h, :], RR[:, h, :])
                        if k < L - 1:
                            C.cp(ACT, Pm[:, hs, :], psv(b0, 4))
                        if k >= 1:
                            dst = TTf[:, hs, :] if k == L - 1 else Tm[:, hs, :]
                            C.tt(DVE, dst, psv(b0 + 1, 4), Tm[:, hs, :], ALU.add)
                        if k < L - 1:
                            C.cp(ACT, RR[:, hs, :], psv(b0 + 2, 4))
                        yield
                for h in range(8):
                    C.mm(ps[2][:, h * 64:(h + 1) * 64], AkT[:, h, :], V_tm[:, h * 64:(h + 1) * 64])
                C.cp(ACT, Z_tm, ps[2])
                for h in range(8):
                    pr = h // 2
                    C.mm(ps[h // 4][:, (h % 4) * 128:(h % 4 + 1) * 128], A_tm[:, pr * 128:(pr + 1) * 128], TTf[:, h, :])
                for b in range(2):
                    pv = ps[b].rearrange("p (q e t) -> p q e t", q=2, e=2)
                    C.cp(ACT, W1T[0:64, 2 * b:2 * b + 2, :], pv[0:64, :, 0, :])
                    C.cp(ACT, W1T[64:128, 2 * b:2 * b + 2, :], pv[64:128, :, 1, :])
                yield
                for h in range(8):
                    pr, hh = h // 2, h % 2
                    rows = slice(64 * hh, 64 * hh + 64)
                    hsl = slice(h * 64, (h + 1) * 64)
                    if not sample:
                        C.mm(ps[3][:, hsl], TTf[:, h, :], Z_tm[:, hsl], start=True, stop=False)
                        C.mm(ps[3][:, hsl], W1T[rows, pr, :], ST_bf[rows, pr, :], start=False, stop=True)
                    else:
                        C.mm(ps[3][:, hsl], TTf[:, h, :], Z_tm[:, hsl], start=(h == 0), stop=False)
                if sample:
                    for b in range(NB):
                        ew = EW[b % 2]
                        csl = slice(b * TS, (b + 1) * TS)
                        C.cp(POOL, ew[:, :, csl], W1T[:, :, csl])
                        for h in hh_order:
                            pr, hh = h // 2, h % 2
                            rows = slice(64 * hh, 64 * hh + 64)
                            C.mm(ps[3][:, h * 64:(h + 1) * 64], ew[rows, pr, :], S0Tb[rows, b, pr, :].k(b), start=False, stop=False)
                        C.memset(POOL, ew[:, :, csl], 0.0)
                        yield
                C.cp(ACT, U_tm, ps[3])
                for h in range(8):
                    pr, hh = h // 2, h % 2
                    rows = slice(64 * hh, 64 * hh + 64)
                    hsl = slice(h * 64, (h + 1) * 64)
                    C.mm(ps[4][:, hsl], NrbT[:, h, :], U_tm[:, hsl], start=(h == 0 or not sample), stop=False)
                    C.mm(ps[4][:, hsl], NrkT[:, h, :], V_tm[:, hsl], start=False, stop=False)
                    if not sample:
                        C.mm(ps[4][:, hsl], AR[rows, pr, 1, :], ST_bf[rows, pr, :], start=False, stop=True)
                yield
                if sample:
                    for b in range(NB):
                        er = ER[b % 2]
                        csl = slice(b * TS, (b + 1) * TS)
                        C.cp(POOL, er[:, :, csl], AR[:, :, 1, csl])
                        for h in hh_order:
                            pr, hh = h // 2, h % 2
                            rows = slice(64 * hh, 64 * hh + 64)
                            C.mm(ps[4][:, h * 64:(h + 1) * 64], er[rows, pr, :], S0Tb[rows, b, pr, :].k(b), start=False, stop=False)
                        C.memset(POOL, er[:, :, csl], 0.0)
                        yield
                    i64b = cc("i64s").unsqueeze(1).to_broadcast([128, 4, 64])
                    for b in range(NB):
                        bank = b % 2
                        C.ts(DVE, Ub, U_tm, cc("cm")[:, b:b + 1], ALU.mult)
                        C.ts(DVE, Vb, V_tm, cc("cm")[:, b:b + 1], ALU.mult)
                        C.tt(DVE, Dg, i64b, bc3(gC[:, :, b], 64), ALU.mult)
                        for h in range(8):
                            hsl = slice(h * 64, (h + 1) * 64)
                            C.mm(ps[bank][0:64, hsl], Ub[:, hsl], Bh_tm[:, hsl], start=(h == 0), stop=False)
                            C.mm(ps[bank][0:64, hsl], Vb[:, hsl], Kh_tm[:, hsl], start=False, stop=False)
                        for h in hh_order:
                            pr, hh = h // 2, h % 2
                            rows = slice(64 * hh, 64 * hh + 64)
                            C.mm(ps[bank][0:64, h * 64:(h + 1) * 64], S0T32[rows, b, pr, :].k(b), Dg[rows, pr, :], start=False, stop=False)
                        C.cp(ACT, Sn[0:64, :], ps[bank][0:64, :])
                        C.dma(SP, rws[b].rearrange("h v j -> v h j"), Sn[0:64, :].rearrange("p (h j) -> p h j", h=8), out_final=True)
                        yield
                if not sample:
                    for pr in range(4):
                        psl = slice(pr * 128, (pr + 1) * 128)
                        C.mm(ps[5][:, psl], Bh_tm[:, psl], U_tm[:, psl], start=True, stop=False)
                        C.mm(ps[5][:, psl], Kh_tm[:, psl], V_tm[:, psl], start=False, stop=True)
                    C.tt(DVE, STtmp, ST, bc3(gC[:, :, 0], 64), ALU.mult)
                    p5 = psv(5, 4)
                    C.tt(DVE, ST[0:64, :, :], STtmp[0:64, :, :], p5[0:64, :, 0:64], ALU.add)
                    C.tt(DVE, ST[64:128, :, :], STtmp[64:128, :, :], p5[64:128, :, 64:128], ALU.add)
                    C.cp(ACT, ST_bf, ST)
                yield
                ysq, tmp2 = BT0, BT1
                y3 = ps[4].rearrange("p (h v) -> p h v", h=8)
                yv = ysq.rearrange("p (h v) -> p h v", h=8)
                C.red(DVE, st16[:, 0:8], y3)
                C.act(ysq, ps[4], AF.Square)
                C.red(DVE, st16[:, 8:16], yv)
                C.ts(DVE, m8, st16[:, 0:8], 1.0 / 64, ALU.mult)
                C.tt(DVE, r8, m8, m8, ALU.mult)
                C.stt(DVE, r8, st16[:, 8:16], 1.0 / 64, r8, ALU.mult, ALU.subtract)
                C.act(r8, r8, AF.Ln, bias=GN_EPS)
                C.act(r8, r8, AF.Exp, scale=-0.5)
                C.tt(DVE, yv, y3, bc3(m8, 64), ALU.subtract)
                C.tt(DVE, yv, yv, bc3(r8, 64), ALU.mult)
                C.tt(DVE, ysq, ysq, lnxw, ALU.mult)
                C.tt(DVE, ysq, ysq, lnxb, ALU.add)
                C.tt(DVE, tmp2.rearrange("p (h v) -> p h v", h=8), V_tm.rearrange("p (h v) -> p h v", h=8),
                     bc3(bon8, 64), ALU.mult)
                C.tt(DVE, ysq, ysq, tmp2, ALU.add)
                C.tt(DVE, o_all[:, 0:512], ysq, g_tm, ALU.mult)
                yield
            if part == "rwkv":
                return
            for mc in range(8):
                C.tr(psb(2)[:, mc * 128:(mc + 1) * 128], o_all[:, mc * 128:(mc + 1) * 128], ident_bf)
            C.cp(ACT, oT, psb(2).rearrange("p (a t) -> p a t", a=8))
            for half in range(2):
                for mc in range(8):
                    C.mm(ps[half], oT[:, mc, :], w_out_bf[:, mc, half * 512:(half + 1) * 512].k(mc),
                         start=(mc == 0), stop=(mc == 7))
            for half in range(2):
                hsl = slice(half * 512, (half + 1) * 512)
                C.stt(DVE, h1pre[:, hsl], xt[:, hsl], ALPHA, ps[half], ALU.mult, ALU.add)
            yield
            layernorm(h1pre, h1pre, LN_EPS)
            C.dma(SP, h1_dst, h1pre)
            yield

        def collect(g):
            C.defer = []
            for _ in g:
                pass
            lst, C.defer = C.defer, None
            return lst

        fin = {}
        eng_free = {e: 0.0 for e in ENGINES}

        def est_dur(p):
            eng, fn, outs, ins, dma, _ = p.args
            ap = _ap(outs[0])
            n = 1
            for d_ in ap.shape[1:]:
                n *= d_
            if dma:
                return 2.0
            if eng == PE:
                return 0.03 + max(n, 48) / 2400.0 * (4.0 if ap.dtype == F32 and False else 1.0)
            if eng == ACT:
                return 0.20 + n / 1200.0
            if eng == DVE:
                return 0.08 + n / 960.0
            return 0.10 + n / 500.0

        def dep_t(d, eng):
            if d.eng == PE and eng == PE:
                return fin.get(d, 0.15) - 0.15
            return fin.get(d, 0.0) + 0.25

        cur_tab = [None]

        def est_start(p):
            eng, fn, outs, ins, dma, _ = p.args
            t = eng_free[eng]
            if p.tab is not None and p.tab != cur_tab[0]:
                t += 1.3
            for x in ins:
                if x is None or _isnum(x):
                    continue
                r = C.R(x)
                if r.last_w is not None:
                    t = max(t, dep_t(r.last_w, eng))
            for x in outs:
                r = C.R(x)
                if r.last_w is not None:
                    t = max(t, dep_t(r.last_w, eng))
                for rd in r.readers:
                    t = max(t, dep_t(rd, eng))
            return t

        def commit_timed(p):
            t0 = est_start(p)
            o_ = C.commit(p)
            d_ = est_dur(p)
            if p.tab is not None:
                cur_tab[0] = p.tab
            fin[o_] = t0 + d_ + (0.15 if p.args[0] == PE else 0.0)
            eng_free[p.args[0]] = t0 + (d_ if p.args[0] != SP else 0.1)
            return o_

        def interleave(ga, gb):
            la, lb_ = collect(ga), collect(gb)
            ia = ib = 0
            while ia < len(la) or ib < len(lb_):
                if ia >= len(la):
                    commit_timed(lb_[ib]); ib += 1
                elif ib >= len(lb_):
                    commit_timed(la[ia]); ia += 1
                else:
                    ta, tb = est_start(la[ia]), est_start(lb_[ib])
                    if tb <= ta:
                        commit_timed(lb_[ib]); ib += 1
                    else:
                        commit_timed(la[ia]); ia += 1

        def collect(g):
            C.defer = []
            for _ in g:
                pass
            lst, C.defer = C.defer, None
            return lst

        def drain(g):
            for p in collect(g):
                commit_timed(p)

        jobs = [(ti, xp[ti * 128:(ti + 1) * 128, :], V(h1scr[ti * 128:(ti + 1) * 128, :], ("h1scr", ti)), False)
                for ti in range(NT)]
        if SAMPLE:
            jobs.append((NT, xsm, V(h1scr[NTP * 128:(NTP + 1) * 128, :], ("h1scr", NTP)), True))
        if True:
            if jobs:
                drain(stageA(jobs[0][0], jobs[0][1], jobs[0][3]))
            issue_w_out()
            for n, (ti, x_src, h1_dst, smp) in enumerate(jobs):
                if n + 1 < len(jobs):
                    nj = jobs[n + 1]
                    interleave(stageA(nj[0], nj[1], nj[3]), stageB(ti, h1_dst, smp))
                elif smp:
                    interleave(stageB(ti, h1_dst, smp, "hgrn"), stageB(ti, h1_dst, smp, "rwkv"))
                    drain(stageB(ti, h1_dst, smp, "final"))
                else:
                    drain(stageB(ti, h1_dst, smp))
                if n == NT - 1 and not smp:
                    prompt_state_outputs()
            if NT == 0:
                prompt_state_outputs()

        if True:
            P.barrier()
            arena.off = phase_mark
            w_up_bf = sb([128, 8, DFF], BF16, "w_up_bf")
            w_dn_bf = sb([128, 32, D], BF16, "w_dn_bf")
            upT = sb([128, 32, 512], BF16, "upT")
            h1T = [sb([128, 8, 512], BF16, f"h1T{i}") for i in range(2)]
            h1b = [sb([128, D], BF16, f"h1b{i}") for i in range(2)]
            h1r = [sb([128, D], F32, f"h1r{i}") for i in range(1)]
            rl = [sb([128, 512], F32, f"rl{i}") for i in range(2)]
            outb = [sb([128, D], F32, f"outb{i}") for i in range(2)]
            tiles = list(range(NT)) + ([NTP] if SAMPLE else [])
            groups = [tiles[i:i + 4] for i in range(0, NT, 4)]
            if SAMPLE:
                groups.append([NTP])

            def load_h1b(g):
                for gi, tix in enumerate(groups[g]):
                    C.dma(POOL, h1b[gi % 2], V(h1scr[tix * 128:(tix + 1) * 128, :], ("h1scr", tix)))

            def transposes(g):
                for gi, tix in enumerate(groups[g]):
                    bank = 6 + (gi % 2)
                    for kc in range(8):
                        C.tr(psb(bank)[:, kc * 128:(kc + 1) * 128], h1b[gi % 2][:, kc * 128:(kc + 1) * 128], ident_bf)
                    C.cp(ACT, h1T[g % 2][:, :, gi * 128:(gi + 1) * 128], psb(bank).rearrange("p (a t) -> p a t", a=8))

            def load_and_transpose(g, first=False):
                for gi, tix in enumerate(groups[g]):
                    C.dma(POOL, h1b[gi % 2], V(h1scr[tix * 128:(tix + 1) * 128, :], ("h1scr", tix)))
                    bank = 6 + (gi % 2)
                    for kc in range(8):
                        C.tr(psb(bank)[:, kc * 128:(kc + 1) * 128], h1b[gi % 2][:, kc * 128:(kc + 1) * 128], ident_bf)
                    C.cp(ACT, h1T[g % 2][:, :, gi * 128:(gi + 1) * 128], psb(bank).rearrange("p (a t) -> p a t", a=8))

            if groups:
                load_and_transpose(0)
            for cb in range(8):
                C.dma(POOL, w_up_bf[:, :, cb * 512:(cb + 1) * 512].k(cb),
                      w_up[:, cb * 512:(cb + 1) * 512].rearrange("(kc p) n -> p kc n", p=128))
            C.dma(SP, lng, row(ln2_g).partition_broadcast(128))
            C.dma(SP, lnb, row(ln2_b).partition_broadcast(128))
            for fc in range(32):
                C.dma(POOL, w_dn_bf[:, fc, :].k(fc), w_down[fc * 128:(fc + 1) * 128, :])
            tcount = 0
            for g, grp in enumerate(groups):
                ng = len(grp)
                W = ng * 128
                hT = h1T[g % 2]
                for fc in range(32):
                    bank = fc % 2
                    for kc in range(8):
                        C.mm(ps[bank][:, 0:W], w_up_bf[:, kc, fc * 128:(fc + 1) * 128].k(fc // 4), hT[:, kc, 0:W],
                             start=(kc == 0), stop=(kc == 7))
                    r = rl[fc % 2]
                    C.act(r[:, 0:W], ps[bank][:, 0:W], AF.Relu)
                    C.tt(POOL if fc % 2 else DVE, upT[:, fc, 0:W], r[:, 0:W], r[:, 0:W], ALU.mult)
                if g + 1 < len(groups):
                    load_and_transpose(g + 1)
                for gi, tix in enumerate(grp):
                    hr = h1r[0]
                    ob = outb[(tcount + gi) % 2]
                    C.dma(SP, hr, V(h1scr[tix * 128:(tix + 1) * 128, :], ("h1scr", tix)))
                    for half in range(2):
                        bank = 2 + ((gi * 2 + half) % 4)
                        for fc in range(32):
                            C.mm(ps[bank], upT[:, fc, gi * 128:(gi + 1) * 128], w_dn_bf[:, fc, half * 512:(half + 1) * 512].k(fc),
                                 start=(fc == 0), stop=(fc == 31))
                        hsl = slice(half * 512, (half + 1) * 512)
                        C.stt(DVE, ob[:, hsl], hr[:, hsl], ALPHA, ps[bank], ALU.mult, ALU.add)
                    layernorm(ob, ob, LN_EPS)
                    dst = ys if tix == NTP else yp[tix * 128:(tix + 1) * 128, :]
                    C.dma(SP, dst, ob, out_final=True)
                tcount += ng

        sems = {e: es.enter_context(nc.semaphore(f"s_{e}")) for e in ENGINES}
        rings = {e: [es.enter_context(nc.semaphore(f"r_{e}{i}")) for i in range(n)] for e, n in DMA_RING.items()}
        P.prepare(sems, rings)
        build.stats = dict(P.stats)
        build.arena_peak = arena.peak
        with nc.allow_low_precision(reason="bf16 matmul operands, fp32 accumulation"), \
                nc.allow_non_contiguous_dma(reason="tiny per-channel vectors / state layouts"), \
                nc.Block() as block:
            block.tensor(lambda eng: P.emit_engine(PE, eng))
            block.scalar(lambda eng: P.emit_engine(ACT, eng))
            block.vector(lambda eng: P.emit_engine(DVE, eng))
            block.gpsimd(lambda eng: P.emit_engine(POOL, eng))
            block.sync(lambda eng: P.emit_engine(SP, eng))
    return nc


IN_NAMES = ["w_in", "shift_mu", "w0", "w1u", "a0", "a1u", "g1u", "k_k", "k_a", "r_k", "ln_x_w", "ln_x_b",
            "lb_logits", "hg_norm_w", "w_out", "ln1_g", "ln1_b", "w_up", "w_down", "ln2_g", "ln2_b"]


def make_in_maps(inputs, n_cores=8):
    f = lambda a: np.ascontiguousarray(np.asarray(a, dtype=np.float32))
    shared = {}
    for k in IN_NAMES:
        a = f(inputs[k])
        a = a[0] if k != "lb_logits" else a
        if k == "r_k":
            a = a.reshape(-1)
        shared[k] = np.ascontiguousarray(a)
    shared["consts"] = CONSTS
    maps = []
    for c in range(n_cores):
        m = dict(shared)
        m["xp"] = f(inputs["x_prompt"][c])
        m["xs"] = f(inputs["x_sample"][c * NB:(c + 1) * NB]).reshape(NB * TS, D)
        m["srw"] = f(inputs["state_rwkv"][0, c * NB:(c + 1) * NB])
        m["shg"] = f(inputs["state_hgrn"][0, c * NB:(c + 1) * NB])
        m["ssh"] = f(inputs["state_shift"][0, c * NB:(c + 1) * NB])
        maps.append(m)
    return maps


_NC_CACHE = {}


def kernel(**inputs):
    if "nc" not in _NC_CACHE:
        _NC_CACHE["nc"] = build()
    nc = _NC_CACHE["nc"]
    maps = make_in_maps(inputs)
    res = run_bass_kernel_spmd(nc, maps, core_ids=list(range(8)))
    R = res.results
    y_prompt = np.stack([R[c]["yp"] for c in range(8)]).astype(np.float32)
    y_sample = np.concatenate([R[c]["ys"].reshape(NB, TS, D) for c in range(8)]).astype(np.float32)
    rw_p = np.stack([R[c]["rwp"] for c in range(8)])[None].astype(np.float32)
    rw_s = np.concatenate([R[c]["rws"] for c in range(8)])[None].astype(np.float32)
    hg_p = np.stack([R[c]["hgp"] for c in range(8)])[None].astype(np.float32)
    hg_s = np.concatenate([R[c]["hgs"] for c in range(8)])[None].astype(np.float32)
    sh_p = np.stack([R[c]["shp"] for c in range(8)])[None].astype(np.float32)
    sh_s = np.concatenate([R[c]["shs"] for c in range(8)])[None].astype(np.float32)
    return (y_prompt, y_sample, rw_p, rw_s, hg_p, hg_s, sh_p, sh_s)
```

```python
import numpy as np
import concourse.bass as bass
import concourse.mybir as mybir
from concourse.bass_utils import run_bass_kernel_spmd

F32 = mybir.dt.float32
BF16 = mybir.dt.bfloat16
AF = mybir.ActivationFunctionType
ALU = mybir.AluOpType
AX = mybir.AxisListType

DEBUG_LINES = False
LINE_OF = {}
PE, ACT, DVE, POOL, SP = "pe", "act", "dve", "pool", "sp"
ENGINES = (PE, ACT, DVE, POOL, SP)
DMA_RING = {SP: 12, ACT: 4, POOL: 8}


class Res:
    __slots__ = ("name", "last_w", "readers")

    def __init__(self, name):
        self.name = name
        self.last_w = None
        self.readers = []


class Op:
    __slots__ = ("eng", "fn", "deps", "is_dma", "signal", "idx", "dma_no", "extra_wait", "rg")

    def __init__(self, eng, fn, is_dma):
        self.eng = eng
        self.fn = fn
        self.deps = set()
        self.is_dma = is_dma
        self.signal = False
        self.idx = None
        self.dma_no = None
        self.extra_wait = None
        self.rg = None


def _pe_inorder_ok(d, o):
    return d.rg is None or o.rg is None or d.rg == o.rg


class Prog:
    def __init__(self):
        self.ops = {e: [] for e in ENGINES}
        self.order = []
        self.n_dma = {e: 0 for e in ENGINES}
        self.dma_ops = {e: [] for e in ENGINES}
        self.out_dmas = []

    def op(self, eng, fn, reads=(), writes=(), dma=False, out=False):
        o = Op(eng, fn, dma)
        for r in reads:
            if r.last_w is not None:
                o.deps.add(r.last_w)
        for w in writes:
            if w.last_w is not None:
                o.deps.add(w.last_w)
            for rd in w.readers:
                o.deps.add(rd)
        for r in reads:
            r.readers.append(o)
        for w in writes:
            w.last_w = o
            w.readers = []
        if getattr(self, "barrier_left", None) and eng in self.barrier_left:
            self.barrier_left.discard(eng)
            o.deps.update(self.pending_barrier)
        o.deps.discard(o)
        if dma:
            o.dma_no = self.n_dma[eng]
            self.n_dma[eng] += 1
            self.dma_ops[eng].append(o)
            o.signal = True
            if out:
                self.out_dmas.append(o)
        self.ops[eng].append(o)
        self.order.append(o)
        return o

    def barrier(self):
        pend = []
        for e in ENGINES:
            comp = [o for o in self.ops[e] if not o.is_dma]
            if comp:
                pend.append(comp[-1])
            pend.extend(self.dma_ops[e][-DMA_RING.get(e, 0):] if e in DMA_RING else [])
        self.pending_barrier = pend
        self.barrier_left = set(ENGINES)

    def prepare(self, sems, rings):
        for o in self.order:
            for d in o.deps:
                if d.is_dma:
                    continue
                if d.eng == o.eng and d.eng == PE and not o.is_dma and _pe_inorder_ok(d, o):
                    continue
                d.signal = True
        sig = {}
        for e in ENGINES:
            c = 0
            for o in self.ops[e]:
                if o.is_dma:
                    R = len(rings[e])
                    sig[o] = (rings[e][o.dma_no % R], 16 * (o.dma_no // R + 1))
                elif o.signal:
                    c += 1
                    sig[o] = (sems[e], c)
        self.sig = sig
        self.rings = rings
        self.stats = {e: len(self.ops[e]) for e in ENGINES}

    def emit_engine(self, e, eng):
        sig, rings = self.sig, self.rings
        waited = {}

        def wait(sem, val):
            k = id(sem)
            if waited.get(k, 0) >= val:
                return
            waited[k] = val
            eng.wait_ge(sem, val)

        for o in self.ops[e]:
            for d in o.deps:
                if d not in sig:
                    continue
                if d.eng == e and not d.is_dma and not o.is_dma and e == PE and _pe_inorder_ok(d, o):
                    continue
                s, v = sig[d]
                wait(s, v)
            if o.is_dma:
                R = len(rings[e])
                if o.dma_no >= R:
                    wait(rings[e][o.dma_no % R], 16 * (o.dma_no // R))
            ins = o.fn(eng)
            if DEBUG_LINES:
                LINE_OF[str(getattr(getattr(ins, "ins", ins), "name", ins))] = o.extra_wait
            if o in sig:
                s, v = sig[o]
                ins.then_inc(s, 16 if o.is_dma else 1)
        if e == SP:
            for q in ENGINES:
                if q not in rings:
                    continue
                for o in self.dma_ops[q][-len(rings[q]):]:
                    s, v = sig[o]
                    wait(s, v)


class V:
    __slots__ = ("ap", "key")

    def __init__(self, ap, key):
        self.ap = ap
        self.key = key

    def __getitem__(self, idx):
        return V(self.ap[idx], self.key)

    def k(self, sub):
        return V(self.ap, (self.key, sub))

    @property
    def shape(self):
        return self.ap.shape

    def rearrange(self, *a, **kw):
        return V(self.ap.rearrange(*a, **kw), self.key)

    def unsqueeze(self, ax):
        return V(self.ap.unsqueeze(ax), self.key)

    def to_broadcast(self, shape):
        return V(self.ap.to_broadcast(list(shape)), self.key)

    def bitcast(self, dt):
        return V(self.ap.bitcast(dt), self.key)


def _ap(x):
    return x.ap if isinstance(x, V) else x


def _isnum(x):
    return isinstance(x, (int, float))


class Arena:
    def __init__(self, nc, es, nbytes):
        self.n2 = nbytes // 2
        self.t = es.enter_context(nc.sbuf_tensor("arena", [128, self.n2], BF16))
        self.off = 0
        self.cnt = 0
        self.peak = 0

    def alloc(self, shape, dt, name=None):
        shape = list(shape)
        esz = 4 if dt == F32 else 2
        n = int(np.prod(shape[1:]))
        nbytes = (n * esz + 3) // 4 * 4
        o = self.off
        assert o + nbytes <= self.n2 * 2, f"arena overflow allocating {name} {shape}: {o}+{nbytes} > {self.n2 * 2}"
        self.off += nbytes
        self.peak = max(self.peak, self.off)
        ap = self.t[0:shape[0], o // 2:o // 2 + nbytes // 2]
        if esz == 4:
            ap = ap.bitcast(F32)
        ap = ap[:, 0:n]
        if len(shape) == 3:
            ap = ap.rearrange("p (a b) -> p a b", a=shape[1])
        elif len(shape) == 4:
            ap = ap.rearrange("p (a b c) -> p a b c", a=shape[1], b=shape[2])
        self.cnt += 1
        return V(ap, name or f"t{self.cnt}")


class Ctx:
    def __init__(self, nc, P, arena):
        self.nc, self.P, self.arena = nc, P, arena
        self.res = {}
        self.defer = None

    class _Pending:
        __slots__ = ("args", "rg", "tab")

        def __init__(self, args):
            self.args = args
            self.rg = None
            self.tab = None

    def commit(self, p):
        o_ = self._rec_now(*p.args)
        o_.rg = p.rg
        return o_

    def sb(self, shape, dt, name=None):
        return self.arena.alloc(shape, dt, name)

    def R(self, x):
        k = x.key if isinstance(x, V) else (x.name, None)
        r = self.res.get(k)
        if r is None:
            r = self.res[k] = Res(k)
        return r

    def rec(self, eng, fn, outs, ins, dma=False, out=False):
        if self.defer is not None:
            p = Ctx._Pending((eng, fn, list(outs), list(ins), dma, out))
            self.defer.append(p)
            return p
        return self._rec_now(eng, fn, outs, ins, dma, out)

    def _rec_now(self, eng, fn, outs, ins, dma=False, out=False):
        reads = [self.R(i) for i in ins if i is not None and not _isnum(i)]
        writes = [self.R(o) for o in outs]
        writes += [r for r in reads if isinstance(r.name, str) and r.name.startswith("ps")]
        o_ = self.P.op(eng, fn, reads=reads, writes=writes, dma=dma, out=out)
        if DEBUG_LINES:
            import sys as _s
            f = _s._getframe(1)
            while f.f_code.co_name not in ("mixer_tile", "build", "layernorm") and f.f_back is not None:
                f = f.f_back
            o_.extra_wait = f.f_lineno
        return o_

    def mm(self, out, lhsT, rhs, start=True, stop=True):
        o, l, r = _ap(out), _ap(lhsT), _ap(rhs)
        op = self.rec(PE, lambda e: e.matmul(o, lhsT=l, rhs=r, start=start, stop=stop,
                                             skip_group_check=True), [out], [lhsT, rhs])
        kr = l.shape[0]
        if kr < 128:
            op.rg = (kr, l.base_partition())
        return op

    def tr(self, out, in_, ident):
        o, i, d = _ap(out), _ap(in_), _ap(ident)
        return self.rec(PE, lambda e: e.transpose(o, i, d), [out], [in_, ident])

    def act(self, out, in_, func, bias=None, scale=1.0):
        o, i = _ap(out), _ap(in_)
        kw = {}
        if bias is not None:
            kw["bias"] = _ap(bias)
        s = _ap(scale)
        p = self.rec(ACT, lambda e: e.activation(out=o, in_=i, func=func, scale=s, **kw), [out],
                     [in_, bias, scale])
        if isinstance(p, Ctx._Pending):
            p.tab = "sig" if func in (AF.Sigmoid, AF.Tanh) else ("exp" if func in (AF.Exp, AF.Ln) else None)
        return p

    def tt(self, eng, out, a, b, op):
        o, x, y = _ap(out), _ap(a), _ap(b)
        return self.rec(eng, lambda e: e.tensor_tensor(out=o, in0=x, in1=y, op=op), [out], [a, b])

    def ts(self, eng, out, a, s1, op0, s2=None, op1=None):
        o, x, v1, v2 = _ap(out), _ap(a), _ap(s1), _ap(s2)
        if op1 is None:
            f = lambda e: e.tensor_scalar(out=o, in0=x, scalar1=v1, scalar2=None, op0=op0)
        else:
            f = lambda e: e.tensor_scalar(out=o, in0=x, scalar1=v1, scalar2=v2, op0=op0, op1=op1)
        return self.rec(eng, f, [out], [a, s1, s2])

    def stt(self, eng, out, in0, scalar, in1, op0, op1):
        o, x, y, s = _ap(out), _ap(in0), _ap(in1), _ap(scalar)
        return self.rec(eng, lambda e: e.scalar_tensor_tensor(out=o, in0=x, scalar=s, in1=y, op0=op0, op1=op1),
                        [out], [in0, in1, scalar])

    def cp(self, eng, out, in_):
        o, i = _ap(out), _ap(in_)
        if eng == ACT:
            return self.rec(ACT, lambda e: e.activation(out=o, in_=i, func=AF.Copy), [out], [in_])
        return self.rec(eng, lambda e: e.tensor_copy(out=o, in_=i), [out], [in_])

    def recip(self, out, in_):
        o, i = _ap(out), _ap(in_)
        return self.rec(DVE, lambda e: e.reciprocal(out=o, in_=i), [out], [in_])

    def scan(self, out, d0, d1):
        o, a, b = _ap(out), _ap(d0), _ap(d1)
        return self.rec(DVE, lambda e: e.tensor_tensor_scan(out=o, data0=a, data1=b, initial=0.0,
                                                            op0=ALU.mult, op1=ALU.add), [out], [d0, d1])

    def red(self, eng, out, in_, op=ALU.add):
        o, i = _ap(out), _ap(in_)
        return self.rec(eng, lambda e: e.tensor_reduce(out=o, in_=i, axis=AX.X, op=op), [out], [in_])

    def memset(self, eng, out, val):
        o = _ap(out)
        return self.rec(eng, lambda e: e.memset(o, val), [out], [])

    def dma(self, eng, out, in_, out_final=False, extra_out=()):
        o, i = _ap(out), _ap(in_)
        return self.rec(eng, lambda e: e.dma_start(out=o, in_=i), [out, *extra_out], [in_], dma=True, out=out_final)

    def bn_stats(self, out, in_):
        o, i = _ap(out), _ap(in_)
        return self.rec(DVE, lambda e: e.bn_stats(out=o, in_=i), [out], [in_])

    def bn_aggr(self, out, in_):
        o, i = _ap(out), _ap(in_)
        return self.rec(DVE, lambda e: e.bn_aggr(out=o, in_=i), [out], [in_])


D = 1024
PJ = 3840
RWP = 1792
NTP = 16
NB = 16
TS = 8
DFF = 4096
ALPHA = 2.0 ** 0.25
CDEC = -float(np.exp(-0.5))
LN_EPS = 1e-5
GN_EPS = 64e-5
RMS_EPS = 1e-6
ARENA_BYTES = 212800


def make_consts():
    s = np.arange(128)[:, None]
    t = np.arange(128)[None, :]
    cols = {}
    cols["ident"] = (s == t)
    cols["mS_p"] = (s < t)
    cols["mI_p"] = (s <= t)
    cols["mST_p"] = (t < s)
    same = (s // TS) == (t // TS)
    cols["mS_s"] = (s < t) & same
    cols["mI_s"] = (s <= t) & same
    cols["mST_s"] = (t < s) & same
    cols["reset_p"] = np.broadcast_to(t != 0, (128, 128))
    cols["reset_s"] = np.broadcast_to((t % TS) != 0, (128, 128))
    cols["bdones"] = (s // 64) == (t // 64)
    cols["hsel"] = (s // 64) == np.arange(2)[None, :]
    cols["cm"] = (s // TS) == np.arange(NB)[None, :]
    cols["i64s"] = (s % 64) == np.arange(64)[None, :]
    off = {}
    parts = []
    o = 0
    for k, v in cols.items():
        v = np.asarray(v, np.float32)
        off[k] = (o, o + v.shape[1])
        o += v.shape[1]
        parts.append(v)
    return np.ascontiguousarray(np.concatenate(parts, axis=1)), off


CONSTS, COFF = make_consts()
NCONST = CONSTS.shape[1]


class _Stop(Exception):
    pass


def build(NT=NTP, SAMPLE=True, DBG=False, STAGE=99):
    from contextlib import ExitStack
    nc = bass.Bass("TRN2", target_bir_lowering=False)

    def din(name, shape):
        return nc.dram_tensor(name, list(shape), F32, kind="ExternalInput").ap()

    def dout(name, shape):
        return nc.dram_tensor(name, list(shape), F32, kind="ExternalOutput").ap()

    xp = din("xp", [NTP * 128, D]); xsm = din("xs", [128, D])
    srw = din("srw", [NB, 8, 64, 64]); shg = din("shg", [NB, 4, 128, 128]); ssh = din("ssh", [NB, RWP])
    w_in = din("w_in", [D, PJ]); shift_mu = din("shift_mu", [RWP]); w0 = din("w0", [512])
    w1u = din("w1u", [64, 512]); a0 = din("a0", [512]); a1u = din("a1u", [64, 512]); g1u = din("g1u", [128, 512])
    k_k = din("k_k", [512]); k_a = din("k_a", [512]); r_k = din("r_k", [512])
    ln_x_w = din("ln_x_w", [512]); ln_x_b = din("ln_x_b", [512]); lb_logits = din("lb_logits", [2, 512])
    hg_norm_w = din("hg_norm_w", [512]); w_out = din("w_out", [D, D]); ln1_g = din("ln1_g", [D]); ln1_b = din("ln1_b", [D])
    w_up = din("w_up", [D, DFF]); w_down = din("w_down", [DFF, D]); ln2_g = din("ln2_g", [D]); ln2_b = din("ln2_b", [D])
    cst_d = din("consts", [128, NCONST])
    yp = dout("yp", [NTP * 128, D]); ys = dout("ys", [128, D])
    rwp = dout("rwp", [8, 64, 64]); rws = dout("rws", [NB, 8, 64, 64])
    hgp = dout("hgp", [4, 128, 128]); hgs = dout("hgs", [NB, 4, 128, 128])
    shp = dout("shp", [RWP]); shs = dout("shs", [NB, RWP])
    h1scr = nc.dram_tensor("h1scr", [(NTP + 1) * 128, D], F32).ap()
    if DBG:
        dbg_d = dout("dbg", [128, 4096])

    def row(v):
        return v.rearrange("(o n) -> o n", o=1)

    P = Prog()
    with ExitStack() as es:
        arena = Arena(nc, es, ARENA_BYTES)
        C = Ctx(nc, P, arena)
        sb = C.sb
        ps = [V(es.enter_context(nc.psum_tensor(f"ps{i}", [128, 512], F32))[:], f"ps{i}") for i in range(8)]

        def psv(i, a):
            return ps[i].rearrange("p (a t) -> p a t", a=a)

        def psb(i):
            return ps[i].bitcast(BF16)

        def bc3(ap2, n):
            return ap2.unsqueeze(2).to_broadcast([ap2.shape[0], ap2.shape[1], n])

        def v3(t):
            return t.rearrange("p (a t) -> p a t", a=4)

        ident_bf = sb([128, 128], BF16, "ident_bf")
        lng = sb([128, D], F32, "lng"); lnb = sb([128, D], F32, "lnb")
        C.dma(SP, lng, row(ln1_g).partition_broadcast(128))
        C.dma(SP, lnb, row(ln1_b).partition_broadcast(128))
        bnst = sb([128, 12], F32, "bnst"); mv = sb([128, 2], F32, "mv"); rstd1 = sb([128, 1], F32, "rstd1")
        fence_ln = sb([128, 2], F32, "fence_ln")
        if DBG:
            dbg = sb([128, 4096], F32, "dbg")
            C.memset(POOL, dbg, 0.0)
            dbg_pos = [0]
            dbg_map = {}

            def dump(name, v, n):
                a = dbg_pos[0]
                shape = list(v.shape)
                dst = dbg[0:shape[0], a:a + n]
                if len(shape) == 3:
                    dst = dst.rearrange("p (a b) -> p a b", a=shape[1])
                C.cp(POOL, dst, v)
                dbg_pos[0] += n
                dbg_map[name] = (a, n, shape)
            build.dbg_map = dbg_map
        else:
            def dump(name, v, n):
                return None
        phase_mark = arena.off

        def layernorm(src, dst, eps):
            for half in range(2):
                C.bn_stats(bnst[:, half * 6:(half + 1) * 6], src[:, half * 512:(half + 1) * 512])
            C.bn_aggr(mv, bnst)
            C.act(rstd1, mv[:, 1:2], AF.Ln, bias=eps)
            C.act(rstd1, rstd1, AF.Exp, scale=-0.5)
            C.ts(DVE, dst, src, mv[:, 0:1], ALU.subtract, rstd1[:, 0:1], ALU.mult)
            halves = []
            for eng_, hsl in ((DVE, slice(0, 512)), (DVE, slice(512, 1024))):
                dk = dst[:, hsl].k(hsl.start)
                halves.append(dk)
                o, a_, g_, b_ = _ap(dk), _ap(dst[:, hsl]), _ap(lng[:, hsl]), _ap(lnb[:, hsl])
                C.rec(eng_, lambda e, o=o, a_=a_, g_=g_: e.tensor_tensor(out=o, in0=a_, in1=g_, op=ALU.mult), [dk], [dst, lng])
                C.rec(eng_, lambda e, o=o, b_=b_: e.tensor_tensor(out=o, in0=o, in1=b_, op=ALU.add), [dk], [dk, lnb])
            C.rec(POOL, lambda e: e.memset(_ap(fence_ln), 0.0), [dst, fence_ln], halves)

        F32_CONSTS = ("ident", "reset_p", "reset_s", "hsel", "cm", "i64s")
        BF_CONSTS = ("mS_p", "mI_p", "mST_p", "mS_s", "mI_s", "mST_s", "bdones")
        cviews = {}
        nf = sum(COFF[k][1] - COFF[k][0] for k in F32_CONSTS)
        nb_ = sum(COFF[k][1] - COFF[k][0] for k in BF_CONSTS)
        cstf = sb([128, nf], F32, "cstf"); cstb = sb([128, nb_], BF16, "cstb")
        o_ = 0
        for k_ in F32_CONSTS:
            a, b = COFF[k_]
            C.dma(SP, cstf[:, o_:o_ + b - a].k(k_), cst_d[:, a:b])
            cviews[k_] = cstf[:, o_:o_ + b - a].k(k_)
            o_ += b - a
        o_ = 0
        for k_ in BF_CONSTS:
            a, b = COFF[k_]
            C.dma(POOL, cstb[:, o_:o_ + b - a].k(k_), cst_d[:, a:b])
            cviews[k_] = cstb[:, o_:o_ + b - a].k(k_)
            o_ += b - a

        def cc(name):
            return cviews[name]

        C.cp(POOL, ident_bf, cc("ident"))
        bdones_bf = cc("bdones")
        mu14 = sb([128, 14], F32, "mu14")
        kk4 = sb([128, 4], F32, "kk4"); ka4 = sb([128, 4], F32, "ka4"); rk4 = sb([128, 4], F32, "rk4")
        w04 = sb([128, 4], F32, "w04"); a04 = sb([128, 4], F32, "a04")
        lbl = sb([128, 2, 4], F32, "lbl")
        with nc.allow_non_contiguous_dma(reason="tiny per-channel parameter vectors"):
            C.dma(SP, mu14, shift_mu.rearrange("(c p) -> p c", p=128))
            C.dma(SP, kk4, k_k.rearrange("(c p) -> p c", p=128))
            C.dma(SP, ka4, k_a.rearrange("(c p) -> p c", p=128))
            C.dma(SP, rk4, r_k.rearrange("(c p) -> p c", p=128))
            C.dma(SP, w04, w0.rearrange("(c p) -> p c", p=128))
            C.dma(SP, a04, a0.rearrange("(c p) -> p c", p=128))
            C.dma(SP, lbl, lb_logits.rearrange("l (c p) -> p l c", p=128))
        WA = sb([128, 512], BF16, "WA")
        C.dma(POOL, WA[0:64, :].k("lo"), w1u)
        C.dma(POOL, WA[64:128, :].k("hi"), a1u)
        g1u_bf = sb([128, 512], BF16, "g1u_bf")
        C.dma(POOL, g1u_bf, g1u)
        lnxw = sb([128, 512], BF16, "lnxw"); lnxb = sb([128, 512], BF16, "lnxb"); hgw = sb([128, 512], BF16, "hgw")
        C.dma(POOL, lnxw, row(ln_x_w).partition_broadcast(128))
        C.dma(POOL, lnxb, row(ln_x_b).partition_broadcast(128))
        C.dma(POOL, hgw, row(hg_norm_w).partition_broadcast(128))
        lb4 = sb([128, 4], F32, "lb4"); oml4 = sb([128, 4], F32, "oml4"); etmp = sb([128, 4], F32, "etmp")
        C.tt(DVE, etmp, lbl[:, 1, :], lbl[:, 0, :], ALU.subtract)
        C.act(etmp, etmp, AF.Exp)
        C.ts(DVE, lb4, etmp, 1.0, ALU.add)
        C.recip(lb4, lb4)
        C.tt(DVE, oml4, etmp, lb4, ALU.mult)
        RKsel = sb([128, 4, 2], BF16, "RKsel")
        C.tt(DVE, RKsel, bc3(rk4, 2), cc("hsel").unsqueeze(1).to_broadcast([128, 4, 2]), ALU.mult)

        ov_lo = arena.off
        w_in_bf = sb([128, 8, PJ], BF16, "w_in_bf")
        ov_hi = arena.off
        WIN_BLK = [(0, 512), (512, 1024), (1024, 1536), (1536, 1792), (1792, 2304), (2304, 2816), (2816, 3328), (3328, 3840)]

        def win_key(col):
            for i_, (a_, b_) in enumerate(WIN_BLK):
                if a_ <= col < b_:
                    return i_
        def issue_w_in():
            for i_, (a_, b_) in enumerate(WIN_BLK):
                C.dma(POOL, w_in_bf[:, :, a_:b_].k(i_), w_in[:, a_:b_].rearrange("(kc p) n -> p kc n", p=128))
        w_out_bf = sb([128, 8, D], BF16, "w_out_bf")
        def issue_w_out():
            for kc in range(8):
                C.dma(POOL, w_out_bf[:, kc, :].k(kc), w_out[kc * 128:(kc + 1) * 128, :])

        ST = sb([128, 4, 64], F32, "ST"); ST_bf = sb([128, 4, 64], BF16, "ST_bf")
        SH = sb([128, 4, 128], F32, "SH"); SH_bf = sb([128, 4, 128], BF16, "SH_bf")
        plast = sb([128, 14], F32, "plast")
        for t_ in (ST, ST_bf, SH, SH_bf, plast):
            C.memset(POOL, t_, 0.0)

        def prompt_state_outputs():
            with nc.allow_non_contiguous_dma(reason="tiny state vector"):
                C.dma(SP, shp.rearrange("(c p) -> p c", p=128), plast, out_final=True)
            C.dma(SP, hgp.rearrange("h k v -> k h v"), SH, out_final=True)
            identf = cc("ident")
            for pr in range(4):
                C.tr(ps[2][0:64, pr * 128:(pr + 1) * 128], ST[:, pr, :], identf)
            rwo = T[0]
            C.cp(DVE, rwo[0:64, :], ps[2][0:64, :])
            C.dma(SP, rwp.rearrange("h v j -> v h j"), rwo[0:64, :].rearrange("p (h j) -> p h j", h=8), out_final=True)
            if DBG:
                C.dma(SP, dbg_d, dbg, out_final=True)


        x_t = [sb([128, D], F32, f"x_t{i}") for i in range(2)]
        x_bf = sb([128, D], BF16, "x_bf")
        xT = sb([128, 8, 128], BF16, "xT")
        pr_ = sb([128, 14, 129], F32, "pr")
        xs = sb([128, 14, 128], F32, "xsft")
        T = [sb([128, 512], F32, f"T{i}") for i in range(10)]
        z12 = sb([128, 128], BF16, "z12"); sg_bf = sb([128, 128], BF16, "sg_bf"); tqb = sb([128, 512], BF16, "tqb")
        bhT = sb([128, 4, 128], BF16, "bhT"); khT = sb([128, 4, 128], BF16, "khT"); vT = sb([128, 4, 128], BF16, "vT")
        khTb = sb([128, 4, 128], BF16, "khTb")
        fence2 = sb([128, 2], F32, "fence2")
        HB = []
        for i in range(2):
            HB.append(dict(
                AR=sb([128, 4, 2, 128], BF16, f"AR{i}"), bT=sb([128, 4, 128], BF16, f"bT{i}"), kT=sb([128, 4, 128], BF16, f"kT{i}"),
                A_tm=sb([128, 512], BF16, f"A_tm{i}"), Bh_tm=sb([128, 512], BF16, f"Bh_tm{i}"),
                Kh_tm=sb([128, 512], BF16, f"Kh_tm{i}"), V_tm=sb([128, 512], BF16, f"V_tm{i}"),
                qTb=sb([128, 4, 128], BF16, f"qTb{i}"), kTb=sb([128, 4, 128], BF16, f"kTb{i}"),
                khat_tm=sb([128, 512], BF16, f"khat_tm{i}"), i_tm=sb([128, 512], BF16, f"i_tm{i}"),
                gs_t=sb([128, 512], BF16, f"gs_t{i}"), g_tm=sb([128, 512], BF16, f"g_tm{i}"),
                bon8=sb([128, 8], F32, f"bon8{i}"), gC=sb([128, 4, NB], F32, f"gC{i}"), decH=sb([128, 4, NB], F32, f"decH{i}")))
        Pm = sb([128, 8, 128], BF16, "Pm"); Tm = sb([128, 8, 128], BF16, "Tm"); RR = sb([128, 8, 128], BF16, "RR")
        NrbT = sb([128, 8, 128], BF16, "NrbT"); AkT = sb([128, 8, 128], BF16, "AkT"); NrkT = sb([128, 8, 128], BF16, "NrkT")
        TTf = sb([128, 8, 128], BF16, "TTf")
        W1T = sb([128, 4, 128], BF16, "W1T")
        Z_tm = sb([128, 512], BF16, "Z_tm"); U_tm = sb([128, 512], BF16, "U_tm")
        st16 = sb([128, 16], F32, "st16"); m8 = sb([128, 8], F32, "m8"); r8 = sb([128, 8], F32, "r8")
        o_all = sb([128, D], BF16, "o_all"); oT = sb([128, 8, 128], BF16, "oT")
        h1pre = sb([128, D], F32, "h1pre")
        attT = sb([128, 4, 128], BF16, "attT")
        s4 = sb([128, 4], F32, "s4"); rr4 = sb([128, 4], F32, "rr4")
        BT0 = h1pre[:, 0:512]; BT1 = h1pre[:, 512:1024]
        SHtmp = v3(BT1)
        STtmp = BT1[:, 0:256].rearrange("p (a v) -> p a v", a=4)
        identb4 = ident_bf.unsqueeze(1).to_broadcast([128, 4, 128])
        save_off = arena.off
        arena.off = ov_lo
        scrA = [sb([128, 2048], F32, f"scrA{i}") for i in range(2)]
        S0T32 = sb([128, NB, 4, 64], F32, "S0T32")
        S0Tb = sb([128, NB, 4, 64], BF16, "S0Tb")
        sshT = sb([128, 14, NB], F32, "sshT"); lastp = sb([128, 14, NB], F32, "lastp")
        EW = [sb([128, 4, 128], BF16, f"EW{i}") for i in range(2)]
        ER = [sb([128, 4, 128], BF16, f"ER{i}") for i in range(2)]
        EQ = [sb([128, 4, 128], BF16, f"EQ{i}") for i in range(2)]
        Ub = sb([128, 512], BF16, "Ub"); Vb = sb([128, 512], BF16, "Vb"); khb = sb([128, 512], BF16, "khb")
        Dg = sb([128, 4, 64], F32, "Dg")
        S0h = [sb([128, 4, 128], F32, f"S0h{i}") for i in range(2)]
        S0hb = sb([128, 4, 128], BF16, "S0hb")
        Sn = sb([128, 512], F32, "Sn")
        fence_t = sb([128, 2], F32, "fence_t")
        assert arena.off <= ov_hi, (arena.off, ov_hi)
        ov_bufs = scrA + [S0T32, S0Tb, sshT, lastp] + EW + ER + EQ + [Ub, Vb, khb, Dg] + S0h + [S0hb, Sn, fence_t]
        arena.off = save_off
        ssh_tm = scrA[0][0:NB, 0:RWP]
        hh_order = (0, 2, 4, 6, 1, 3, 5, 7)
        heads_of = [(0, 1, 2, 3), (4, 5, 6, 7)]
        PA, PB_ = 6, 7

        cb = cc

        first_tile = [True]

        def cfg(sample):
            sfx = "_s" if sample else "_p"
            return dict(nch=NB if sample else 1, Cn=TS if sample else 128, L=3 if sample else 7,
                        mS=cb("mS" + sfx), mI=cb("mI" + sfx), mST=cb("mST" + sfx), reset=cc("reset" + sfx))

        def stageA(ti, x_src, sample):
            g_ = cfg(sample)
            nch, Cn, reset = g_["nch"], g_["Cn"], g_["reset"]
            H = HB[ti % 2]
            AR, bT, kT = H["AR"], H["bT"], H["kT"]
            xt = x_t[ti % 2]
            C.dma(SP, xt, x_src)
            C.dma(POOL, x_bf, x_src)
            if first_tile[0]:
                first_tile[0] = False
                issue_w_in()
            for kc in range(8):
                C.tr(psb(PB_)[:, kc * 128:(kc + 1) * 128], x_bf[:, kc * 128:(kc + 1) * 128], ident_bf)
            C.cp(ACT, xT, psb(PB_).rearrange("p (a t) -> p a t", a=8))
            yield
            sig, kq, fgl, bcs, eb, enb, ebl, sq_, eg = T[1], T[4], T[0], T[2], T[3], T[5], T[6], T[7], T[8]

            def proj_fm(c0, n, bank):
                for j in range(n):
                    c = c0 + j
                    for kc in range(8):
                        C.mm(ps[bank][:, j * 128:(j + 1) * 128], w_in_bf[:, kc, c * 128:(c + 1) * 128].k(win_key(c * 128)),
                             xT[:, kc, :], start=(kc == 0), stop=(kc == 7))

            def proj_tm(col0, bank):
                for kc in range(8):
                    C.mm(ps[bank], xT[:, kc, :], w_in_bf[:, kc, col0:col0 + 512].k(win_key(col0)), start=(kc == 0), stop=(kc == 7))

            proj_fm(0, 4, PA); C.cp(ACT, pr_[:, 0:4, 1:129], psv(PA, 4)); yield
            proj_fm(4, 4, PB_); C.cp(ACT, pr_[:, 4:8, 1:129], psv(PB_, 4)); yield
            proj_fm(8, 4, PA); C.cp(ACT, pr_[:, 8:12, 1:129], psv(PA, 4)); yield
            proj_fm(12, 2, PB_); C.cp(ACT, pr_[:, 12:14, 1:129], psv(PB_, 4)[:, 0:2, :]); yield
            prev, cur = pr_[:, :, 0:128], pr_[:, :, 1:129]
            if not sample:
                C.cp(DVE, pr_[:, :, 0:1], plast.unsqueeze(2))
                C.cp(DVE, plast.unsqueeze(2), pr_[:, :, 128:129])
            else:
                C.memset(POOL, pr_[:, :, 0:1], 0.0)
            proj_fm(14, 4, PA)
            C.act(sq_, ps[PA], AF.Sigmoid)
            C.tt(DVE, sq_, ps[PA], sq_, ALU.mult)
            yield
            proj_fm(18, 4, PB_)
            C.act(sig, ps[PB_], AF.Sigmoid)
            yield
            proj_tm(RWP + 1024, PA)
            C.cp(ACT, H["i_tm"], ps[PA])
            yield
            proj_tm(RWP + 1536, PB_)
            C.act(eg, ps[PB_], AF.Sigmoid)
            C.tt(DVE, H["gs_t"], ps[PB_], eg, ALU.mult)
            yield
            if sample:
                C.rec(POOL, lambda e: e.memset(_ap(fence_t), 0.0),
                      [w_in_bf[:, :, a_:b_].k(i_) for i_, (a_, b_) in enumerate(WIN_BLK)] + ov_bufs
                      + [S0T32[:, b, :, :].k(b) for b in range(NB)] + [S0Tb[:, b, :, :].k(b) for b in range(NB)], [])
                for t_ in EW + ER + EQ:
                    C.memset(POOL, t_, 0.0)
                C.dma(SP, ssh_tm, ssh)
                for c in range(14):
                    C.tr(ps[PA][:, c * NB:(c + 1) * NB], ssh_tm[:, c * 128:(c + 1) * 128], cc("ident")[0:NB, 0:NB])
                C.cp(DVE, sshT, ps[PA][:, 0:14 * NB].rearrange("p (c b) -> p c b", c=14))
            for eng_, c0, c1 in ((DVE, 0, 8), (DVE, 8, 14)):
                C.tt(eng_, xs[:, c0:c1, :].k(c0), prev[:, c0:c1, :], cur[:, c0:c1, :], ALU.subtract)
                C.tt(eng_, xs[:, c0:c1, :].k(c0), xs[:, c0:c1, :].k(c0), bc3(mu14[:, c0:c1], 128), ALU.mult)
                C.tt(eng_, xs[:, c0:c1, :].k(c0), xs[:, c0:c1, :].k(c0), cur[:, c0:c1, :], ALU.add)
            C.rec(POOL, lambda e: e.memset(_ap(fence2), 0.0), [xs, fence2], [xs[:, 0:8, :].k(0), xs[:, 8:14, :].k(8)])
            if sample:
                cur4 = cur.rearrange("p c (b t) -> p c b t", t=TS)
                xs4 = xs.rearrange("p c (b t) -> p c b t", t=TS)
                cur0, xs0 = cur4[:, :, :, 0], xs4[:, :, :, 0]
                C.tt(DVE, xs0, sshT, cur0, ALU.subtract)
                C.tt(DVE, xs0, xs0, bc3(mu14, NB), ALU.mult)
                C.tt(DVE, xs0, xs0, cur0, ALU.add)
                C.cp(DVE, lastp, cur4[:, :, :, TS - 1])
                for g0 in range(0, 14, 4):
                    bk = PA if (g0 // 4) % 2 == 0 else PB_
                    n = min(4, 14 - g0)
                    for j in range(n):
                        C.tr(ps[bk][0:NB, j * 128:(j + 1) * 128], lastp[:, g0 + j, :], cc("ident"))
                    C.cp(ACT, ssh_tm[:, g0 * 128:(g0 + n) * 128], ps[bk][0:NB, 0:n * 128])
                C.dma(SP, shs, ssh_tm, out_final=True)
            yield
            r_ = xs[:, 0:4, :]; k_ = xs[:, 4:8, :]; v_ = xs[:, 8:12, :]
            C.act(z12[0:64, :], xs[0:64, 12, :], AF.Tanh)
            C.act(sg_bf, xs[:, 13, :], AF.Sigmoid)
            C.cp(DVE, z12[64:128, :], xs[64:128, 12, :])
            for pr in range(4):
                sl = slice(pr * 128, (pr + 1) * 128)
                C.mm(ps[PA][:, sl], WA[0:64, sl].k("lo"), z12[0:64, :])
            for pr in range(4):
                sl = slice(pr * 128, (pr + 1) * 128)
                C.mm(ps[PB_][:, sl], WA[64:128, sl].k("hi"), z12[64:128, :])
            sw, alr = T[9], T[8]
            for pr in range(4):
                sl = slice(pr * 128, (pr + 1) * 128)
                C.act(sw[:, sl], ps[PA][:, sl], AF.Sigmoid, bias=w04[:, pr:pr + 1])
            C.tt(DVE, v3(kq), v3(sig), bc3(oml4, 128), ALU.mult)
            C.tt(DVE, v3(fgl), v3(kq), bc3(lb4, 128), ALU.add)
            C.tt(DVE, v3(kq), bc3(oml4, 128), v3(kq), ALU.subtract)
            yield
            for pr in range(4):
                sl = slice(pr * 128, (pr + 1) * 128)
                C.act(alr[:, sl], ps[PB_][:, sl], AF.Sigmoid, bias=a04[:, pr:pr + 1])
            C.mm(ps[PA], sg_bf, g1u_bf)
            C.cp(ACT, H["g_tm"], ps[PA])
            yield
            C.act(fgl, fgl, AF.Ln)
            for h in range(4):
                C.scan(bcs[:, h * 128:(h + 1) * 128], reset, fgl[:, h * 128:(h + 1) * 128])
            C.act(eb, bcs, AF.Exp)
            C.act(enb, bcs, AF.Exp, scale=-1.0)
            bc4 = bcs.rearrange("p (a n c) -> p a n c", a=4, n=nch)
            C.tt(DVE, ebl.rearrange("p (a n c) -> p a n c", a=4, n=nch),
                 bc4[:, :, :, Cn - 1:Cn].to_broadcast([128, 4, nch, Cn]), bc4, ALU.subtract)
            C.act(ebl, ebl, AF.Exp)
            yield
            C.cp(DVE, H["decH"][:, :, 0:nch].unsqueeze(3),
                 eb.rearrange("p (a n c) -> p a n c", a=4, n=nch)[:, :, :, Cn - 1:Cn])
            C.tt(DVE, H["qTb"], v3(sq_), v3(eb), ALU.mult)
            C.tt(DVE, H["kTb"], v3(kq), v3(enb), ALU.mult)
            C.tt(DVE, khTb, v3(kq), v3(ebl), ALU.mult)
            yield
            cumS, gex, gin, ginv, glast, kkk, tq, k2 = T[2], T[3], T[4], T[5], T[6], T[7], T[0], T[1]
            for pr in range(4):
                C.scan(cumS[:, pr * 128:(pr + 1) * 128], reset, sw[:, pr * 128:(pr + 1) * 128])
            C.tt(DVE, gex, cumS, sw, ALU.subtract)
            C.act(gex, gex, AF.Exp, scale=CDEC)
            C.act(gin, cumS, AF.Exp, scale=CDEC)
            C.act(ginv, cumS, AF.Exp, scale=-CDEC)
            cs4 = cumS.rearrange("p (a n c) -> p a n c", a=4, n=nch)
            C.tt(DVE, glast.rearrange("p (a n c) -> p a n c", a=4, n=nch),
                 cs4[:, :, :, Cn - 1:Cn].to_broadcast([128, 4, nch, Cn]), cs4, ALU.subtract)
            C.act(glast, glast, AF.Exp, scale=CDEC)
            C.cp(DVE, H["gC"][:, :, 0:nch].unsqueeze(3),
                 gin.rearrange("p (a n c) -> p a n c", a=4, n=nch)[:, :, :, Cn - 1:Cn])
            yield
            C.tt(DVE, v3(kkk), k_, bc3(kk4, 128), ALU.mult)
            C.act(tqb, kkk, AF.Square)
            for pr in range(4):
                sl = slice(pr * 128, (pr + 1) * 128)
                C.mm(ps[PB_][:, sl], bdones_bf, tqb[:, sl])
            C.act(tq, ps[PB_], AF.Ln, bias=1e-24)
            C.act(tq, tq, AF.Exp, scale=-0.5)
            C.tt(DVE, kkk, kkk, tq, ALU.mult)
            C.stt(DVE, v3(k2), v3(alr), -1.0, bc3(ka4, 128), ALU.add, ALU.mult)
            C.stt(DVE, v3(k2), v3(k2), 1.0, k_, ALU.add, ALU.mult)
            C.tt(DVE, alr, kkk, alr, ALU.mult)
            b_ = alr
            yield
            C.tt(DVE, AR[:, :, 1, :], r_, v3(gin), ALU.mult)
            C.stt(DVE, AR[:, :, 0, :], v3(kkk), -1.0, v3(gex), ALU.mult, ALU.mult)
            C.tt(DVE, bT, v3(b_), v3(ginv), ALU.mult)
            C.tt(DVE, kT, v3(k2), v3(ginv), ALU.mult)
            C.tt(DVE, bhT, v3(b_), v3(glast), ALU.mult)
            C.tt(DVE, khT, v3(k2), v3(glast), ALU.mult)
            C.cp(ACT, vT, v_)
            C.tt(DVE, v3(tqb), r_, v3(k2), ALU.mult)
            for pr in range(4):
                C.mm(ps[PA][:, pr * 2:(pr + 1) * 2], tqb[:, pr * 128:(pr + 1) * 128], RKsel[:, pr, :])
            C.cp(DVE, H["bon8"], ps[PA][:, 0:8])
            yield
            for (src, bank, half) in ((lambda pr: AR[:, pr, 0, :], PA, 0), (lambda pr: bhT[:, pr, :], PA, 1),
                                      (lambda pr: khT[:, pr, :], PB_, 0), (lambda pr: vT[:, pr, :], PB_, 1)):
                for pr in range(4):
                    o0 = half * 512 + pr * 128
                    C.tr(psb(bank)[:, o0:o0 + 128], src(pr), ident_bf)
            C.cp(ACT, H["A_tm"], psb(PA)[:, 0:512]); C.cp(ACT, H["Bh_tm"], psb(PA)[:, 512:1024])
            C.cp(ACT, H["Kh_tm"], psb(PB_)[:, 0:512]); C.cp(ACT, H["V_tm"], psb(PB_)[:, 512:1024])
            yield
            for h in range(4):
                C.tr(psb(PA)[:, h * 128:(h + 1) * 128], khTb[:, h, :], ident_bf)
            C.cp(ACT, H["khat_tm"], psb(PA)[:, 0:512])
            yield

        def stageB(ti, h1_dst, sample, part="all"):
            g_ = cfg(sample)
            nch, Cn, L = g_["nch"], g_["Cn"], g_["L"]
            mSb = g_["mS"].unsqueeze(1).to_broadcast([128, 4, 128])
            mIb = g_["mI"].unsqueeze(1).to_broadcast([128, 4, 128])
            mSTb = g_["mST"].unsqueeze(1).to_broadcast([128, 4, 128])
            H = HB[ti % 2]
            AR, bT, kT, A_tm, Bh_tm, Kh_tm, V_tm = H["AR"], H["bT"], H["kT"], H["A_tm"], H["Bh_tm"], H["Kh_tm"], H["V_tm"]
            qTb, kTb, khat_tm, i_tm, gs_t, g_tm, bon8, gC, decH = (H["qTb"], H["kTb"], H["khat_tm"], H["i_tm"], H["gs_t"],
                                                                    H["g_tm"], H["bon8"], H["gC"], H["decH"])
            xt = x_t[ti % 2]
            if part in ("all", "hgrn"):
                bA, bO = (6, 7) if sample else (3, 4)
                for h in range(4):
                    C.mm(ps[bA][:, h * 128:(h + 1) * 128], kTb[:, h, :], qTb[:, h, :])
                C.tt(DVE, attT, psv(bA, 4), mIb, ALU.mult)
                if not sample:
                    for h in range(4):
                        hsl = slice(h * 128, (h + 1) * 128)
                        C.mm(ps[4][:, hsl], attT[:, h, :], i_tm[:, hsl], start=True, stop=False)
                        C.mm(ps[4][:, hsl], qTb[:, h, :], SH_bf[:, h, :], start=False, stop=True)
                    for h in range(4):
                        hsl = slice(h * 128, (h + 1) * 128)
                        C.mm(ps[5][:, hsl], khat_tm[:, hsl], i_tm[:, hsl])
                    C.tt(DVE, SHtmp, SH, bc3(decH[:, :, 0], 128), ALU.mult)
                    C.tt(DVE, SH, SHtmp, psv(5, 4), ALU.add)
                    C.cp(ACT, SH_bf, SH)
                else:
                    for h in range(4):
                        hsl = slice(h * 128, (h + 1) * 128)
                        C.mm(ps[bO][:, hsl], attT[:, h, :], i_tm[:, hsl], start=(h == 0), stop=False)
                    for b in range(NB):
                        s0 = S0h[b % 2]
                        csl = slice(b * TS, (b + 1) * TS)
                        C.dma(SP, s0, shg[b].rearrange("h k v -> k h v"))
                        C.cp(ACT, S0hb, s0)
                        eq = EQ[b % 2]
                        C.cp(POOL, eq[:, :, csl], qTb[:, :, csl])
                        for h in range(4):
                            C.mm(ps[bO][:, h * 128:(h + 1) * 128], eq[:, h, :], S0hb[:, h, :], start=False, stop=False)
                        C.memset(POOL, eq[:, :, csl], 0.0)
                        C.ts(DVE, khb, khat_tm, cc("cm")[:, b:b + 1], ALU.mult)
                        bank = bA
                        for h in range(4):
                            hsl = slice(h * 128, (h + 1) * 128)
                            C.mm(ps[bank][:, hsl], khb[:, hsl], i_tm[:, hsl])
                        C.tt(DVE, s0, s0, bc3(decH[:, :, b], 128), ALU.mult)
                        C.tt(DVE, s0, s0, psv(bank, 4), ALU.add)
                        C.dma(SP, hgs[b].rearrange("h k v -> k h v"), s0, out_final=True)
                        yield
                osq = T[0] if sample else BT0
                C.act(osq, ps[bO], AF.Square)
                C.red(DVE, s4, v3(osq))
                C.act(rr4, s4, AF.Ln, scale=1.0 / 128, bias=RMS_EPS)
                C.act(rr4, rr4, AF.Exp, scale=-0.5)
                C.tt(DVE, v3(osq), psv(bO, 4), bc3(rr4, 128), ALU.mult)
                C.tt(DVE, osq, osq, hgw, ALU.mult)
                C.tt(DVE, o_all[:, 512:1024], osq, gs_t, ALU.mult)
                yield
            if part == "hgrn":
                return
            if part in ("all", "rwkv"):
                if sample:
                    for g in range(4):
                        sa = scrA[g % 2]
                        nat = sa[0:64, :].rearrange("p (b n) -> p b n", b=4)
                        C.dma(SP, nat.rearrange("p b (h j) -> p b h j", h=8),
                              srw[g * 4:(g + 1) * 4].rearrange("b h v j -> v b h j"))
                        for bb in range(4):
                            b = g * 4 + bb
                            bank = b % 2
                            for pr in range(4):
                                C.tr(ps[bank][:, (bb % 2) * 256 + pr * 64:(bb % 2) * 256 + (pr + 1) * 64],
                                     nat[:, bb, pr * 128:(pr + 1) * 128], cc("ident")[0:64, 0:64])
                            C.cp(ACT, S0T32[:, b, :, :].k(b),
                                 ps[bank][:, (bb % 2) * 256:(bb % 2) * 256 + 256].rearrange("p (a v) -> p a v", a=4))
                            C.cp(DVE, S0Tb[:, b, :, :].k(b), S0T32[:, b, :, :].k(b))
                        yield
                for hg in range(2):
                    for i, h in enumerate(heads_of[hg]):
                        pr, hh = h // 2, h % 2
                        rows = slice(64 * hh, 64 * hh + 64)
                        sl = slice(i * 128, (i + 1) * 128)
                        C.mm(ps[0][:, sl], bT[rows, pr, :], AR[rows, pr, 0, :])
                        C.mm(ps[1][:, sl], bT[rows, pr, :], AR[rows, pr, 1, :])
                        C.mm(ps[2][:, sl], kT[rows, pr, :], AR[rows, pr, 0, :])
                        C.mm(ps[3][:, sl], kT[rows, pr, :], AR[rows, pr, 1, :])
                        C.mm(ps[4][:, sl], AR[rows, pr, 0, :], bT[rows, pr, :])
                    hs = slice(4 * hg, 4 * hg + 4)
                    C.tt(DVE, Pm[:, hs, :], psv(0, 4), mSb, ALU.mult)
                    C.tt(DVE, NrbT[:, hs, :], psv(1, 4), mIb, ALU.mult)
                    C.tt(DVE, AkT[:, hs, :], psv(2, 4), mSb, ALU.mult)
                    C.tt(DVE, NrkT[:, hs, :], psv(3, 4), mIb, ALU.mult)
                    C.tt(DVE, RR[:, hs, :], psv(4, 4), mSTb, ALU.mult)
                    C.tt(DVE, Tm[:, hs, :], Pm[:, hs, :], identb4, ALU.add)
                    yield
                for k in range(L):
                    for hg in range(2):
                        b0 = 3 * hg
                        hs = slice(4 * hg, 4 * hg + 4)
                        for i, h in enumerate(heads_of[hg]):
                            sl = slice(i * 128, (i + 1) * 128)
                            if k < L - 1:
                                C.mm(ps[b0][:, sl], RR[:, h, :], Pm[:, h, :])
                            if k >= 1:
                                C.mm(ps[b0 + 1][:, sl], RR[:, h, :], Tm[:, h, :])
                            if k < L - 1:
                                C.mm(ps[b0 + 2][:, sl], Pm[:, h, :], RR[:, h, :])
                        if k < L - 1:
                            C.cp(ACT, Pm[:, hs, :], psv(b0, 4))
                        if k >= 1:
                            dst = TTf[:, hs, :] if k == L - 1 else Tm[:, hs, :]
                            C.tt(DVE, dst, psv(b0 + 1, 4), Tm[:, hs, :], ALU.add)
                        if k < L - 1:
                            C.cp(ACT, RR[:, hs, :], psv(b0 + 2, 4))
                        yield
                for h in range(8):
                    C.mm(ps[2][:, h * 64:(h + 1) * 64], AkT[:, h, :], V_tm[:, h * 64:(h + 1) * 64])
                C.cp(ACT, Z_tm, ps[2])
                for h in range(8):
                    pr = h // 2
                    C.mm(ps[h // 4][:, (h % 4) * 128:(h % 4 + 1) * 128], A_tm[:, pr * 128:(pr + 1) * 128], TTf[:, h, :])
                for b in range(2):
                    pv = ps[b].rearrange("p (q e t) -> p q e t", q=2, e=2)
                    C.cp(ACT, W1T[0:64, 2 * b:2 * b + 2, :], pv[0:64, :, 0, :])
                    C.cp(ACT, W1T[64:128, 2 * b:2 * b + 2, :], pv[64:128, :, 1, :])
                yield
                for h in range(8):
                    pr, hh = h // 2, h % 2
                    rows = slice(64 * hh, 64 * hh + 64)
                    hsl = slice(h * 64, (h + 1) * 64)
                    if not sample:
                        C.mm(ps[3][:, hsl], TTf[:, h, :], Z_tm[:, hsl], start=True, stop=False)
                        C.mm(ps[3][:, hsl], W1T[rows, pr, :], ST_bf[rows, pr, :], start=False, stop=True)
                    else:
                        C.mm(ps[3][:, hsl], TTf[:, h, :], Z_tm[:, hsl], start=(h == 0), stop=False)
                if sample:
                    for b in range(NB):
                        ew = EW[b % 2]
                        csl = slice(b * TS, (b + 1) * TS)
                        C.cp(POOL, ew[:, :, csl], W1T[:, :, csl])
                        for h in hh_order:
                            pr, hh = h // 2, h % 2
                            rows = slice(64 * hh, 64 * hh + 64)
                            C.mm(ps[3][:, h * 64:(h + 1) * 64], ew[rows, pr, :], S0Tb[rows, b, pr, :].k(b), start=False, stop=False)
                        C.memset(POOL, ew[:, :, csl], 0.0)
                        yield
                C.cp(ACT, U_tm, ps[3])
                for h in range(8):
                    pr, hh = h // 2, h % 2
                    rows = slice(64 * hh, 64 * hh + 64)
                    hsl = slice(h * 64, (h + 1) * 64)
                    C.mm(ps[4][:, hsl], NrbT[:, h, :], U_tm[:, hsl], start=(h == 0 or not sample), stop=False)
                    C.mm(ps[4][:, hsl], NrkT[:, h, :], V_tm[:, hsl], start=False, stop=False)
                    if not sample:
                        C.mm(ps[4][:, hsl], AR[rows, pr, 1, :], ST_bf[rows, pr, :], start=False, stop=True)
                yield
                if sample:
                    for b in range(NB):
                        er = ER[b % 2]
                        csl = slice(b * TS, (b + 1) * TS)
                        C.cp(POOL, er[:, :, csl], AR[:, :, 1, csl])
                        for h in hh_order:
                            pr, hh = h // 2, h % 2
                            rows = slice(64 * hh, 64 * hh + 64)
                            C.mm(ps[4][:, h * 64:(h + 1) * 64], er[rows, pr, :], S0Tb[rows, b, pr, :].k(b), start=False, stop=False)
                        C.memset(POOL, er[:, :, csl], 0.0)
                        yield
                    i64b = cc("i64s").unsqueeze(1).to_broadcast([128, 4, 64])
                    for b in range(NB):
                        bank = b % 2
                        C.ts(DVE, Ub, U_tm, cc("cm")[:, b:b + 1], ALU.mult)
                        C.ts(DVE, Vb, V_tm, cc("cm")[:, b:b + 1], ALU.mult)
                        C.tt(DVE, Dg, i64b, bc3(gC[:, :, b], 64), ALU.mult)
                        for h in range(8):
                            hsl = slice(h * 64, (h + 1) * 64)
                            C.mm(ps[bank][0:64, hsl], Ub[:, hsl], Bh_tm[:, hsl], start=(h == 0), stop=False)
                            C.mm(ps[bank][0:64, hsl], Vb[:, hsl], Kh_tm[:, hsl], start=False, stop=False)
                        for h in hh_order:
                            pr, hh = h // 2, h % 2
                            rows = slice(64 * hh, 64 * hh + 64)
                            C.mm(ps[bank][0:64, h * 64:(h + 1) * 64], S0T32[rows, b, pr, :].k(b), Dg[rows, pr, :], start=False, stop=False)
                        C.cp(ACT, Sn[0:64, :], ps[bank][0:64, :])
                        C.dma(SP, rws[b].rearrange("h v j -> v h j"), Sn[0:64, :].rearrange("p (h j) -> p h j", h=8), out_final=True)
                        yield
                if not sample:
                    for pr in range(4):
                        psl = slice(pr * 128, (pr + 1) * 128)
                        C.mm(ps[5][:, psl], Bh_tm[:, psl], U_tm[:, psl], start=True, stop=False)
                        C.mm(ps[5][:, psl], Kh_tm[:, psl], V_tm[:, psl], start=False, stop=True)
                    C.tt(DVE, STtmp, ST, bc3(gC[:, :, 0], 64), ALU.mult)
                    p5 = psv(5, 4)
                    C.tt(DVE, ST[0:64, :, :], STtmp[0:64, :, :], p5[0:64, :, 0:64], ALU.add)
                    C.tt(DVE, ST[64:128, :, :], STtmp[64:128, :, :], p5[64:128, :, 64:128], ALU.add)
                    C.cp(ACT, ST_bf, ST)
                yield
                ysq, tmp2 = BT0, BT1
                y3 = ps[4].rearrange("p (h v) -> p h v", h=8)
                yv = ysq.rearrange("p (h v) -> p h v", h=8)
                C.red(DVE, st16[:, 0:8], y3)
                C.act(ysq, ps[4], AF.Square)
                C.red(DVE, st16[:, 8:16], yv)
                C.ts(DVE, m8, st16[:, 0:8], 1.0 / 64, ALU.mult)
                C.tt(DVE, r8, m8, m8, ALU.mult)
                C.stt(DVE, r8, st16[:, 8:16], 1.0 / 64, r8, ALU.mult, ALU.subtract)
                C.act(r8, r8, AF.Ln, bias=GN_EPS)
                C.act(r8, r8, AF.Exp, scale=-0.5)
                C.tt(DVE, yv, y3, bc3(m8, 64), ALU.subtract)
                C.tt(DVE, yv, yv, bc3(r8, 64), ALU.mult)
                C.tt(DVE, ysq, ysq, lnxw, ALU.mult)
                C.tt(DVE, ysq, ysq, lnxb, ALU.add)
                C.tt(DVE, tmp2.rearrange("p (h v) -> p h v", h=8), V_tm.rearrange("p (h v) -> p h v", h=8),
                     bc3(bon8, 64), ALU.mult)
                C.tt(DVE, ysq, ysq, tmp2, ALU.add)
                C.tt(DVE, o_all[:, 0:512], ysq, g_tm, ALU.mult)
                yield
            if part == "rwkv":
                return
            for mc in range(8):
                C.tr(psb(2)[:, mc * 128:(mc + 1) * 128], o_all[:, mc * 128:(mc + 1) * 128], ident_bf)
            C.cp(ACT, oT, psb(2).rearrange("p (a t) -> p a t", a=8))
            for half in range(2):
                for mc in range(8):
                    C.mm(ps[half], oT[:, mc, :], w_out_bf[:, mc, half * 512:(half + 1) * 512].k(mc),
                         start=(mc == 0), stop=(mc == 7))
            for half in range(2):
                hsl = slice(half * 512, (half + 1) * 512)
                C.stt(DVE, h1pre[:, hsl], xt[:, hsl], ALPHA, ps[half], ALU.mult, ALU.add)
            yield
            layernorm(h1pre, h1pre, LN_EPS)
            C.dma(SP, h1_dst, h1pre)
            yield

        def collect(g):
            C.defer = []
            for _ in g:
                pass
            lst, C.defer = C.defer, None
            return lst

        M_SEM, M_ACT, M_DVE, M_PEL, M_TAB = 0.25, 0.20, 0.08, 0.3, 1.3
        fin = {}
        eng_free = {e: 0.0 for e in ENGINES}

        def est_dur(p):
            eng, fn, outs, ins, dma, _ = p.args
            ap = _ap(outs[0])
            n = 1
            for d_ in ap.shape[1:]:
                n *= d_
            if dma:
                return 2.0
            if eng == PE:
                return 0.03 + max(n, 48) / 2400.0 * (4.0 if ap.dtype == F32 and False else 1.0)
            if eng == ACT:
                return M_ACT + n / 1200.0
            if eng == DVE:
                return M_DVE + n / 960.0
            return 0.10 + n / 500.0

        def dep_t(d, eng):
            if d.eng == PE and eng == PE:
                return fin.get(d, M_PEL) - M_PEL
            return fin.get(d, 0.0) + M_SEM

        cur_tab = [None]

        def est_start(p):
            eng, fn, outs, ins, dma, _ = p.args
            t = eng_free[eng]
            if p.tab is not None and p.tab != cur_tab[0]:
                t += M_TAB
            for x in ins:
                if x is None or _isnum(x):
                    continue
                r = C.R(x)
                if r.last_w is not None:
                    t = max(t, dep_t(r.last_w, eng))
            for x in outs:
                r = C.R(x)
                if r.last_w is not None:
                    t = max(t, dep_t(r.last_w, eng))
                for rd in r.readers:
                    t = max(t, dep_t(rd, eng))
            return t

        def commit_timed(p):
            t0 = est_start(p)
            o_ = C.commit(p)
            d_ = est_dur(p)
            if p.tab is not None:
                cur_tab[0] = p.tab
            fin[o_] = t0 + d_ + (M_PEL if p.args[0] == PE else 0.0)
            eng_free[p.args[0]] = t0 + (d_ if p.args[0] != SP else 0.1)
            return o_

        def interleave(ga, gb):
            la, lb_ = collect(ga), collect(gb)
            ia = ib = 0
            while ia < len(la) or ib < len(lb_):
                if ia >= len(la):
                    commit_timed(lb_[ib]); ib += 1
                elif ib >= len(lb_):
                    commit_timed(la[ia]); ia += 1
                else:
                    ta, tb = est_start(la[ia]), est_start(lb_[ib])
                    if tb <= ta:
                        commit_timed(lb_[ib]); ib += 1
                    else:
                        commit_timed(la[ia]); ia += 1

        def collect(g):
            C.defer = []
            for _ in g:
                pass
            lst, C.defer = C.defer, None
            return lst

        def drain(g):
            for p in collect(g):
                commit_timed(p)

        jobs = [(ti, xp[ti * 128:(ti + 1) * 128, :], V(h1scr[ti * 128:(ti + 1) * 128, :], ("h1scr", ti)), False)
                for ti in range(NT)]
        if SAMPLE:
            jobs.append((NT, xsm, V(h1scr[NTP * 128:(NTP + 1) * 128, :], ("h1scr", NTP)), True))
        if True:
            if jobs:
                drain(stageA(jobs[0][0], jobs[0][1], jobs[0][3]))
            issue_w_out()
            for n, (ti, x_src, h1_dst, smp) in enumerate(jobs):
                if n + 1 < len(jobs):
                    nj = jobs[n + 1]
                    interleave(stageA(nj[0], nj[1], nj[3]), stageB(ti, h1_dst, smp))
                elif smp:
                    interleave(stageB(ti, h1_dst, smp, "hgrn"), stageB(ti, h1_dst, smp, "rwkv"))
                    drain(stageB(ti, h1_dst, smp, "final"))
                else:
                    drain(stageB(ti, h1_dst, smp))
                if n == NT - 1 and not smp:
                    prompt_state_outputs()
            if NT == 0:
                prompt_state_outputs()

        if True:
            P.barrier()
            arena.off = phase_mark
            w_up_bf = sb([128, 8, DFF], BF16, "w_up_bf")
            w_dn_bf = sb([128, 32, D], BF16, "w_dn_bf")
            upT = sb([128, 32, 512], BF16, "upT")
            h1T = [sb([128, 8, 512], BF16, f"h1T{i}") for i in range(2)]
            h1b = [sb([128, D], BF16, f"h1b{i}") for i in range(2)]
            h1r = [sb([128, D], F32, f"h1r{i}") for i in range(1)]
            rl = [sb([128, 512], F32, f"rl{i}") for i in range(2)]
            outb = [sb([128, D], F32, f"outb{i}") for i in range(2)]
            tiles = list(range(NT)) + ([NTP] if SAMPLE else [])
            groups = [tiles[i:i + 4] for i in range(0, NT, 4)]
            if SAMPLE:
                groups.append([NTP])

            def load_h1b(g):
                for gi, tix in enumerate(groups[g]):
                    C.dma(POOL, h1b[gi % 2], V(h1scr[tix * 128:(tix + 1) * 128, :], ("h1scr", tix)))

            def transposes(g):
                for gi, tix in enumerate(groups[g]):
                    bank = 6 + (gi % 2)
                    for kc in range(8):
                        C.tr(psb(bank)[:, kc * 128:(kc + 1) * 128], h1b[gi % 2][:, kc * 128:(kc + 1) * 128], ident_bf)
                    C.cp(ACT, h1T[g % 2][:, :, gi * 128:(gi + 1) * 128], psb(bank).rearrange("p (a t) -> p a t", a=8))

            def issue_w_up(c0, c1):
                for cb in range(c0, c1):
                    C.dma(POOL, w_up_bf[:, :, cb * 512:(cb + 1) * 512].k(cb),
                          w_up[:, cb * 512:(cb + 1) * 512].rearrange("(kc p) n -> p kc n", p=128))

            def load_and_transpose(g, first=False):
                for gi, tix in enumerate(groups[g]):
                    if first and gi == 2:
                        issue_w_up(0, 3)
                    C.dma(POOL, h1b[gi % 2], V(h1scr[tix * 128:(tix + 1) * 128, :], ("h1scr", tix)))
                    bank = 6 + (gi % 2)
                    for kc in range(8):
                        C.tr(psb(bank)[:, kc * 128:(kc + 1) * 128], h1b[gi % 2][:, kc * 128:(kc + 1) * 128], ident_bf)
                    C.cp(ACT, h1T[g % 2][:, :, gi * 128:(gi + 1) * 128], psb(bank).rearrange("p (a t) -> p a t", a=8))

            if groups and len(groups[0]) > 2:
                load_and_transpose(0, first=True)
                issue_w_up(3, 8)
            else:
                if groups:
                    load_and_transpose(0)
                issue_w_up(0, 8)
            C.dma(SP, lng, row(ln2_g).partition_broadcast(128))
            C.dma(SP, lnb, row(ln2_b).partition_broadcast(128))
            for fc in range(32):
                C.dma(POOL, w_dn_bf[:, fc, :].k(fc), w_down[fc * 128:(fc + 1) * 128, :])
            tcount = 0
            for g, grp in enumerate(groups):
                ng = len(grp)
                W = ng * 128
                hT = h1T[g % 2]
                for fc in range(32):
                    bank = fc % 2
                    for kc in range(8):
                        C.mm(ps[bank][:, 0:W], w_up_bf[:, kc, fc * 128:(fc + 1) * 128].k(fc // 4), hT[:, kc, 0:W],
                             start=(kc == 0), stop=(kc == 7))
                    r = rl[fc % 2]
                    C.act(r[:, 0:W], ps[bank][:, 0:W], AF.Relu)
                    C.tt(POOL if fc % 2 else DVE, upT[:, fc, 0:W], r[:, 0:W], r[:, 0:W], ALU.mult)
                if g + 1 < len(groups):
                    load_and_transpose(g + 1)
                for gi, tix in enumerate(grp):
                    hr = h1r[0]
                    ob = outb[(tcount + gi) % 2]
                    C.dma(SP, hr, V(h1scr[tix * 128:(tix + 1) * 128, :], ("h1scr", tix)))
                    for half in range(2):
                        bank = 2 + ((gi * 2 + half) % 4)
                        for fc in range(32):
                            C.mm(ps[bank], upT[:, fc, gi * 128:(gi + 1) * 128], w_dn_bf[:, fc, half * 512:(half + 1) * 512].k(fc),
                                 start=(fc == 0), stop=(fc == 31))
                        hsl = slice(half * 512, (half + 1) * 512)
                        C.stt(DVE, ob[:, hsl], hr[:, hsl], ALPHA, ps[bank], ALU.mult, ALU.add)
                    layernorm(ob, ob, LN_EPS)
                    dst = ys if tix == NTP else yp[tix * 128:(tix + 1) * 128, :]
                    C.dma(SP, dst, ob, out_final=True)
                tcount += ng

        sems = {e: es.enter_context(nc.semaphore(f"s_{e}")) for e in ENGINES}
        rings = {e: [es.enter_context(nc.semaphore(f"r_{e}{i}")) for i in range(n)] for e, n in DMA_RING.items()}
        P.prepare(sems, rings)
        build.stats = dict(P.stats)
        build.arena_peak = arena.peak
        with nc.allow_low_precision(reason="bf16 matmul operands, fp32 accumulation"), \
                nc.allow_non_contiguous_dma(reason="tiny per-channel vectors / state layouts"), \
                nc.Block() as block:
            block.tensor(lambda eng: P.emit_engine(PE, eng))
            block.scalar(lambda eng: P.emit_engine(ACT, eng))
            block.vector(lambda eng: P.emit_engine(DVE, eng))
            block.gpsimd(lambda eng: P.emit_engine(POOL, eng))
            block.sync(lambda eng: P.emit_engine(SP, eng))
    return nc


IN_NAMES = ["w_in", "shift_mu", "w0", "w1u", "a0", "a1u", "g1u", "k_k", "k_a", "r_k", "ln_x_w", "ln_x_b",
            "lb_logits", "hg_norm_w", "w_out", "ln1_g", "ln1_b", "w_up", "w_down", "ln2_g", "ln2_b"]


def make_in_maps(inputs, n_cores=8):
    f = lambda a: np.ascontiguousarray(np.asarray(a, dtype=np.float32))
    shared = {}
    for k in IN_NAMES:
        a = f(inputs[k])
        a = a[0] if k != "lb_logits" else a
        if k == "r_k":
            a = a.reshape(-1)
        shared[k] = np.ascontiguousarray(a)
    shared["consts"] = CONSTS
    maps = []
    for c in range(n_cores):
        m = dict(shared)
        m["xp"] = f(inputs["x_prompt"][c])
        m["xs"] = f(inputs["x_sample"][c * NB:(c + 1) * NB]).reshape(NB * TS, D)
        m["srw"] = f(inputs["state_rwkv"][0, c * NB:(c + 1) * NB])
        m["shg"] = f(inputs["state_hgrn"][0, c * NB:(c + 1) * NB])
        m["ssh"] = f(inputs["state_shift"][0, c * NB:(c + 1) * NB])
        maps.append(m)
    return maps


_NC_CACHE = {}


def kernel(**inputs):
    if "nc" not in _NC_CACHE:
        _NC_CACHE["nc"] = build()
    nc = _NC_CACHE["nc"]
    maps = make_in_maps(inputs)
    res = run_bass_kernel_spmd(nc, maps, core_ids=list(range(8)))
    R = res.results
    y_prompt = np.stack([R[c]["yp"] for c in range(8)]).astype(np.float32)
    y_sample = np.concatenate([R[c]["ys"].reshape(NB, TS, D) for c in range(8)]).astype(np.float32)
    rw_p = np.stack([R[c]["rwp"] for c in range(8)])[None].astype(np.float32)
    rw_s = np.concatenate([R[c]["rws"] for c in range(8)])[None].astype(np.float32)
    hg_p = np.stack([R[c]["hgp"] for c in range(8)])[None].astype(np.float32)
    hg_s = np.concatenate([R[c]["hgs"] for c in range(8)])[None].astype(np.float32)
    sh_p = np.stack([R[c]["shp"] for c in range(8)])[None].astype(np.float32)
    sh_s = np.concatenate([R[c]["shs"] for c in range(8)])[None].astype(np.float32)
    return (y_prompt, y_sample, rw_p, rw_s, hg_p, hg_s, sh_p, sh_s)
```

```python
import numpy as np
import concourse.bass as bass
import concourse.mybir as mybir
from concourse.bass_utils import run_bass_kernel_spmd

F32 = mybir.dt.float32
BF16 = mybir.dt.bfloat16
AF = mybir.ActivationFunctionType
ALU = mybir.AluOpType
AX = mybir.AxisListType

DEBUG_LINES = False
LINE_OF = {}
PE, ACT, DVE, POOL, SP = "pe", "act", "dve", "pool", "sp"
ENGINES = (PE, ACT, DVE, POOL, SP)
DMA_RING = {SP: 24, ACT: 4, POOL: 16}


class Res:
    __slots__ = ("name", "last_w", "readers")

    def __init__(self, name):
        self.name = name
        self.last_w = None
        self.readers = []


class Op:
    __slots__ = ("eng", "fn", "deps", "is_dma", "signal", "idx", "dma_no", "extra_wait", "rg")

    def __init__(self, eng, fn, is_dma):
        self.eng = eng
        self.fn = fn
        self.deps = set()
        self.is_dma = is_dma
        self.signal = False
        self.idx = None
        self.dma_no = None
        self.extra_wait = None
        self.rg = None


def _pe_inorder_ok(d, o):
    return d.rg is None or o.rg is None or d.rg == o.rg


class Prog:
    def __init__(self):
        self.ops = {e: [] for e in ENGINES}
        self.order = []
        self.n_dma = {e: 0 for e in ENGINES}
        self.dma_ops = {e: [] for e in ENGINES}
        self.out_dmas = []

    def op(self, eng, fn, reads=(), writes=(), dma=False, out=False):
        o = Op(eng, fn, dma)
        for r in reads:
            if r.last_w is not None:
                o.deps.add(r.last_w)
        for w in writes:
            if w.last_w is not None:
                o.deps.add(w.last_w)
            for rd in w.readers:
                o.deps.add(rd)
        for r in reads:
            r.readers.append(o)
        for w in writes:
            w.last_w = o
            w.readers = []
        if getattr(self, "barrier_left", None) and eng in self.barrier_left:
            self.barrier_left.discard(eng)
            o.deps.update(self.pending_barrier)
        o.deps.discard(o)
        if dma:
            o.dma_no = self.n_dma[eng]
            self.n_dma[eng] += 1
            self.dma_ops[eng].append(o)
            o.signal = True
            if out:
                self.out_dmas.append(o)
        self.ops[eng].append(o)
        self.order.append(o)
        return o

    def barrier(self):
        pend = []
        for e in ENGINES:
            comp = [o for o in self.ops[e] if not o.is_dma]
            if comp:
                pend.append(comp[-1])
            pend.extend(self.dma_ops[e][-DMA_RING.get(e, 0):] if e in DMA_RING else [])
        self.pending_barrier = pend
        self.barrier_left = set(ENGINES)

    def prepare(self, sems, rings):
        for o in self.order:
            for d in o.deps:
                if d.is_dma:
                    continue
                if d.eng == o.eng and d.eng == PE and not o.is_dma and _pe_inorder_ok(d, o):
                    continue
                d.signal = True
        sig = {}
        for e in ENGINES:
            c = 0
            for o in self.ops[e]:
                if o.is_dma:
                    R = len(rings[e])
                    sig[o] = (rings[e][o.dma_no % R], 16 * (o.dma_no // R + 1))
                elif o.signal:
                    c += 1
                    sig[o] = (sems[e], c)
        self.sig = sig
        self.rings = rings
        self.stats = {e: len(self.ops[e]) for e in ENGINES}

    def emit_engine(self, e, eng):
        sig, rings = self.sig, self.rings
        waited = {}

        def wait(sem, val):
            k = id(sem)
            if waited.get(k, 0) >= val:
                return
            waited[k] = val
            eng.wait_ge(sem, val)

        for o in self.ops[e]:
            for d in o.deps:
                if d not in sig:
                    continue
                if d.eng == e and not d.is_dma and not o.is_dma and e == PE and _pe_inorder_ok(d, o):
                    continue
                s, v = sig[d]
                wait(s, v)
            if o.is_dma:
                R = len(rings[e])
                if o.dma_no >= R:
                    wait(rings[e][o.dma_no % R], 16 * (o.dma_no // R))
            ins = o.fn(eng)
            if DEBUG_LINES:
                LINE_OF[str(getattr(getattr(ins, "ins", ins), "name", ins))] = o.extra_wait
            if o in sig:
                s, v = sig[o]
                ins.then_inc(s, 16 if o.is_dma else 1)
        if e == SP:
            for q in ENGINES:
                if q not in rings:
                    continue
                for o in self.dma_ops[q][-len(rings[q]):]:
                    s, v = sig[o]
                    wait(s, v)


class V:
    __slots__ = ("ap", "key")

    def __init__(self, ap, key):
        self.ap = ap
        self.key = key

    def __getitem__(self, idx):
        return V(self.ap[idx], self.key)

    def k(self, sub):
        return V(self.ap, (self.key, sub))

    @property
    def shape(self):
        return self.ap.shape

    def rearrange(self, *a, **kw):
        return V(self.ap.rearrange(*a, **kw), self.key)

    def unsqueeze(self, ax):
        return V(self.ap.unsqueeze(ax), self.key)

    def to_broadcast(self, shape):
        return V(self.ap.to_broadcast(list(shape)), self.key)

    def bitcast(self, dt):
        return V(self.ap.bitcast(dt), self.key)


def _ap(x):
    return x.ap if isinstance(x, V) else x


def _isnum(x):
    return isinstance(x, (int, float))


class Arena:
    def __init__(self, nc, es, nbytes):
        self.n2 = nbytes // 2
        self.t = es.enter_context(nc.sbuf_tensor("arena", [128, self.n2], BF16))
        self.off = 0
        self.cnt = 0
        self.peak = 0

    def alloc(self, shape, dt, name=None):
        shape = list(shape)
        esz = 4 if dt == F32 else 2
        n = int(np.prod(shape[1:]))
        nbytes = (n * esz + 3) // 4 * 4
        o = self.off
        assert o + nbytes <= self.n2 * 2, f"arena overflow allocating {name} {shape}: {o}+{nbytes} > {self.n2 * 2}"
        self.off += nbytes
        self.peak = max(self.peak, self.off)
        ap = self.t[0:shape[0], o // 2:o // 2 + nbytes // 2]
        if esz == 4:
            ap = ap.bitcast(F32)
        ap = ap[:, 0:n]
        if len(shape) == 3:
            ap = ap.rearrange("p (a b) -> p a b", a=shape[1])
        elif len(shape) == 4:
            ap = ap.rearrange("p (a b c) -> p a b c", a=shape[1], b=shape[2])
        self.cnt += 1
        return V(ap, name or f"t{self.cnt}")


class Ctx:
    def __init__(self, nc, P, arena):
        self.nc, self.P, self.arena = nc, P, arena
        self.res = {}
        self.defer = None

    class _Pending:
        __slots__ = ("args", "rg", "tab")

        def __init__(self, args):
            self.args = args
            self.rg = None
            self.tab = None

    def commit(self, p):
        o_ = self._rec_now(*p.args)
        o_.rg = p.rg
        return o_

    def sb(self, shape, dt, name=None):
        return self.arena.alloc(shape, dt, name)

    def R(self, x):
        k = x.key if isinstance(x, V) else (x.name, None)
        r = self.res.get(k)
        if r is None:
            r = self.res[k] = Res(k)
        return r

    def rec(self, eng, fn, outs, ins, dma=False, out=False):
        if self.defer is not None:
            p = Ctx._Pending((eng, fn, list(outs), list(ins), dma, out))
            self.defer.append(p)
            return p
        return self._rec_now(eng, fn, outs, ins, dma, out)

    def _rec_now(self, eng, fn, outs, ins, dma=False, out=False):
        reads = [self.R(i) for i in ins if i is not None and not _isnum(i)]
        writes = [self.R(o) for o in outs]
        writes += [r for r in reads if isinstance(r.name, str) and r.name.startswith("ps")]
        o_ = self.P.op(eng, fn, reads=reads, writes=writes, dma=dma, out=out)
        if DEBUG_LINES:
            import sys as _s
            f = _s._getframe(1)
            while f.f_code.co_name not in ("mixer_tile", "build", "layernorm") and f.f_back is not None:
                f = f.f_back
            o_.extra_wait = f.f_lineno
        return o_

    def mm(self, out, lhsT, rhs, start=True, stop=True):
        o, l, r = _ap(out), _ap(lhsT), _ap(rhs)
        op = self.rec(PE, lambda e: e.matmul(o, lhsT=l, rhs=r, start=start, stop=stop,
                                             skip_group_check=True), [out], [lhsT, rhs])
        kr = l.shape[0]
        if kr < 128:
            op.rg = (kr, l.base_partition())
        return op

    def tr(self, out, in_, ident):
        o, i, d = _ap(out), _ap(in_), _ap(ident)
        return self.rec(PE, lambda e: e.transpose(o, i, d), [out], [in_, ident])

    def act(self, out, in_, func, bias=None, scale=1.0):
        o, i = _ap(out), _ap(in_)
        kw = {}
        if bias is not None:
            kw["bias"] = _ap(bias)
        s = _ap(scale)
        p = self.rec(ACT, lambda e: e.activation(out=o, in_=i, func=func, scale=s, **kw), [out],
                     [in_, bias, scale])
        if isinstance(p, Ctx._Pending):
            p.tab = "sig" if func in (AF.Sigmoid, AF.Tanh) else ("exp" if func in (AF.Exp, AF.Ln) else None)
        return p

    def tt(self, eng, out, a, b, op):
        o, x, y = _ap(out), _ap(a), _ap(b)
        return self.rec(eng, lambda e: e.tensor_tensor(out=o, in0=x, in1=y, op=op), [out], [a, b])

    def ts(self, eng, out, a, s1, op0, s2=None, op1=None):
        o, x, v1, v2 = _ap(out), _ap(a), _ap(s1), _ap(s2)
        if op1 is None:
            f = lambda e: e.tensor_scalar(out=o, in0=x, scalar1=v1, scalar2=None, op0=op0)
        else:
            f = lambda e: e.tensor_scalar(out=o, in0=x, scalar1=v1, scalar2=v2, op0=op0, op1=op1)
        return self.rec(eng, f, [out], [a, s1, s2])

    def stt(self, eng, out, in0, scalar, in1, op0, op1):
        o, x, y, s = _ap(out), _ap(in0), _ap(in1), _ap(scalar)
        return self.rec(eng, lambda e: e.scalar_tensor_tensor(out=o, in0=x, scalar=s, in1=y, op0=op0, op1=op1),
                        [out], [in0, in1, scalar])

    def cp(self, eng, out, in_):
        o, i = _ap(out), _ap(in_)
        if eng == ACT:
            return self.rec(ACT, lambda e: e.activation(out=o, in_=i, func=AF.Copy), [out], [in_])
        return self.rec(eng, lambda e: e.tensor_copy(out=o, in_=i), [out], [in_])

    def recip(self, out, in_):
        o, i = _ap(out), _ap(in_)
        return self.rec(DVE, lambda e: e.reciprocal(out=o, in_=i), [out], [in_])

    def scan(self, out, d0, d1):
        o, a, b = _ap(out), _ap(d0), _ap(d1)
        return self.rec(DVE, lambda e: e.tensor_tensor_scan(out=o, data0=a, data1=b, initial=0.0,
                                                            op0=ALU.mult, op1=ALU.add), [out], [d0, d1])

    def red(self, eng, out, in_, op=ALU.add):
        o, i = _ap(out), _ap(in_)
        return self.rec(eng, lambda e: e.tensor_reduce(out=o, in_=i, axis=AX.X, op=op), [out], [in_])

    def memset(self, eng, out, val):
        o = _ap(out)
        return self.rec(eng, lambda e: e.memset(o, val), [out], [])

    def dma(self, eng, out, in_, out_final=False, extra_out=()):
        o, i = _ap(out), _ap(in_)
        return self.rec(eng, lambda e: e.dma_start(out=o, in_=i), [out, *extra_out], [in_], dma=True, out=out_final)

    def bn_stats(self, out, in_):
        o, i = _ap(out), _ap(in_)
        return self.rec(DVE, lambda e: e.bn_stats(out=o, in_=i), [out], [in_])

    def bn_aggr(self, out, in_):
        o, i = _ap(out), _ap(in_)
        return self.rec(DVE, lambda e: e.bn_aggr(out=o, in_=i), [out], [in_])


D = 1024
PJ = 3840
RWP = 1792
NTP = 16
NB = 16
TS = 8
DFF = 4096
ALPHA = 2.0 ** 0.25
CDEC = -float(np.exp(-0.5))
LN_EPS = 1e-5
GN_EPS = 64e-5
RMS_EPS = 1e-6
ARENA_BYTES = 212800


def make_consts():
    s = np.arange(128)[:, None]
    t = np.arange(128)[None, :]
    cols = {}
    cols["ident"] = (s == t)
    cols["mS_p"] = (s < t)
    cols["mI_p"] = (s <= t)
    cols["mST_p"] = (t < s)
    same = (s // TS) == (t // TS)
    cols["mS_s"] = (s < t) & same
    cols["mI_s"] = (s <= t) & same
    cols["mST_s"] = (t < s) & same
    cols["reset_p"] = np.broadcast_to(t != 0, (128, 128))
    cols["reset_s"] = np.broadcast_to((t % TS) != 0, (128, 128))
    cols["bdones"] = (s // 64) == (t // 64)
    cols["hsel"] = (s // 64) == np.arange(2)[None, :]
    cols["cm"] = (s // TS) == np.arange(NB)[None, :]
    cols["i64s"] = (s % 64) == np.arange(64)[None, :]
    off = {}
    parts = []
    o = 0
    for k, v in cols.items():
        v = np.asarray(v, np.float32)
        off[k] = (o, o + v.shape[1])
        o += v.shape[1]
        parts.append(v)
    return np.ascontiguousarray(np.concatenate(parts, axis=1)), off


CONSTS, COFF = make_consts()
NCONST = CONSTS.shape[1]


class _Stop(Exception):
    pass


def build(NT=NTP, SAMPLE=True, DBG=False, STAGE=99):
    from contextlib import ExitStack
    nc = bass.Bass("TRN2", target_bir_lowering=False)

    def din(name, shape):
        return nc.dram_tensor(name, list(shape), F32, kind="ExternalInput").ap()

    def dout(name, shape):
        return nc.dram_tensor(name, list(shape), F32, kind="ExternalOutput").ap()

    xp = din("xp", [NTP * 128, D]); xsm = din("xs", [128, D])
    srw = din("srw", [NB, 8, 64, 64]); shg = din("shg", [NB, 4, 128, 128]); ssh = din("ssh", [NB, RWP])
    w_in = din("w_in", [D, PJ]); shift_mu = din("shift_mu", [RWP]); w0 = din("w0", [512])
    w1u = din("w1u", [64, 512]); a0 = din("a0", [512]); a1u = din("a1u", [64, 512]); g1u = din("g1u", [128, 512])
    k_k = din("k_k", [512]); k_a = din("k_a", [512]); r_k = din("r_k", [512])
    ln_x_w = din("ln_x_w", [512]); ln_x_b = din("ln_x_b", [512]); lb_logits = din("lb_logits", [2, 512])
    hg_norm_w = din("hg_norm_w", [512]); w_out = din("w_out", [D, D]); ln1_g = din("ln1_g", [D]); ln1_b = din("ln1_b", [D])
    w_up = din("w_up", [D, DFF]); w_down = din("w_down", [DFF, D]); ln2_g = din("ln2_g", [D]); ln2_b = din("ln2_b", [D])
    cst_d = din("consts", [128, NCONST])
    yp = dout("yp", [NTP * 128, D]); ys = dout("ys", [128, D])
    rwp = dout("rwp", [8, 64, 64]); rws = dout("rws", [NB, 8, 64, 64])
    hgp = dout("hgp", [4, 128, 128]); hgs = dout("hgs", [NB, 4, 128, 128])
    shp = dout("shp", [RWP]); shs = dout("shs", [NB, RWP])
    h1scr = nc.dram_tensor("h1scr", [(NTP + 1) * 128, D], F32).ap()
    if DBG:
        dbg_d = dout("dbg", [128, 4096])

    def row(v):
        return v.rearrange("(o n) -> o n", o=1)

    P = Prog()
    with ExitStack() as es:
        arena = Arena(nc, es, ARENA_BYTES)
        C = Ctx(nc, P, arena)
        sb = C.sb
        ps = [V(es.enter_context(nc.psum_tensor(f"ps{i}", [128, 512], F32))[:], f"ps{i}") for i in range(8)]

        def psv(i, a):
            return ps[i].rearrange("p (a t) -> p a t", a=a)

        def psb(i):
            return ps[i].bitcast(BF16)

        def bc3(ap2, n):
            return ap2.unsqueeze(2).to_broadcast([ap2.shape[0], ap2.shape[1], n])

        def v3(t):
            return t.rearrange("p (a t) -> p a t", a=4)

        ident_bf = sb([128, 128], BF16, "ident_bf")
        lng = sb([128, D], F32, "lng"); lnb = sb([128, D], F32, "lnb")
        C.dma(SP, lng, row(ln1_g).partition_broadcast(128))
        C.dma(SP, lnb, row(ln1_b).partition_broadcast(128))
        bnst = sb([128, 12], F32, "bnst"); mv = sb([128, 2], F32, "mv"); rstd1 = sb([128, 1], F32, "rstd1")
        fence_ln = sb([128, 2], F32, "fence_ln")
        if DBG:
            dbg = sb([128, 4096], F32, "dbg")
            C.memset(POOL, dbg, 0.0)
            dbg_pos = [0]
            dbg_map = {}

            def dump(name, v, n):
                a = dbg_pos[0]
                shape = list(v.shape)
                dst = dbg[0:shape[0], a:a + n]
                if len(shape) == 3:
                    dst = dst.rearrange("p (a b) -> p a b", a=shape[1])
                C.cp(POOL, dst, v)
                dbg_pos[0] += n
                dbg_map[name] = (a, n, shape)
            build.dbg_map = dbg_map
        else:
            def dump(name, v, n):
                return None
        phase_mark = arena.off

        def layernorm(src, dst, eps):
            for half in range(2):
                C.bn_stats(bnst[:, half * 6:(half + 1) * 6], src[:, half * 512:(half + 1) * 512])
            C.bn_aggr(mv, bnst)
            C.act(rstd1, mv[:, 1:2], AF.Ln, bias=eps)
            C.act(rstd1, rstd1, AF.Exp, scale=-0.5)
            C.ts(DVE, dst, src, mv[:, 0:1], ALU.subtract, rstd1[:, 0:1], ALU.mult)
            halves = []
            for eng_, hsl in ((DVE, slice(0, 512)), (DVE, slice(512, 1024))):
                dk = dst[:, hsl].k(hsl.start)
                halves.append(dk)
                o, a_, g_, b_ = _ap(dk), _ap(dst[:, hsl]), _ap(lng[:, hsl]), _ap(lnb[:, hsl])
                C.rec(eng_, lambda e, o=o, a_=a_, g_=g_: e.tensor_tensor(out=o, in0=a_, in1=g_, op=ALU.mult), [dk], [dst, lng])
                C.rec(eng_, lambda e, o=o, b_=b_: e.tensor_tensor(out=o, in0=o, in1=b_, op=ALU.add), [dk], [dk, lnb])
            C.rec(POOL, lambda e: e.memset(_ap(fence_ln), 0.0), [dst, fence_ln], halves)

        F32_CONSTS = ("ident", "reset_p", "reset_s", "hsel", "cm", "i64s")
        BF_CONSTS = ("mS_p", "mI_p", "mST_p", "mS_s", "mI_s", "mST_s", "bdones")
        cviews = {}
        nf = sum(COFF[k][1] - COFF[k][0] for k in F32_CONSTS)
        nb_ = sum(COFF[k][1] - COFF[k][0] for k in BF_CONSTS)
        cstf = sb([128, nf], F32, "cstf"); cstb = sb([128, nb_], BF16, "cstb")
        o_ = 0
        for k_ in F32_CONSTS:
            a, b = COFF[k_]
            C.dma(SP, cstf[:, o_:o_ + b - a].k(k_), cst_d[:, a:b])
            cviews[k_] = cstf[:, o_:o_ + b - a].k(k_)
            o_ += b - a
        o_ = 0
        for k_ in BF_CONSTS:
            a, b = COFF[k_]
            C.dma(POOL, cstb[:, o_:o_ + b - a].k(k_), cst_d[:, a:b])
            cviews[k_] = cstb[:, o_:o_ + b - a].k(k_)
            o_ += b - a

        def cc(name):
            return cviews[name]

        C.cp(POOL, ident_bf, cc("ident"))
        bdones_bf = cc("bdones")
        mu14 = sb([128, 14], F32, "mu14")
        kk4 = sb([128, 4], F32, "kk4"); ka4 = sb([128, 4], F32, "ka4"); rk4 = sb([128, 4], F32, "rk4")
        w04 = sb([128, 4], F32, "w04"); a04 = sb([128, 4], F32, "a04")
        lbl = sb([128, 2, 4], F32, "lbl")
        with nc.allow_non_contiguous_dma(reason="tiny per-channel parameter vectors"):
            C.dma(SP, mu14, shift_mu.rearrange("(c p) -> p c", p=128))
            C.dma(SP, kk4, k_k.rearrange("(c p) -> p c", p=128))
            C.dma(SP, ka4, k_a.rearrange("(c p) -> p c", p=128))
            C.dma(SP, rk4, r_k.rearrange("(c p) -> p c", p=128))
            C.dma(SP, w04, w0.rearrange("(c p) -> p c", p=128))
            C.dma(SP, a04, a0.rearrange("(c p) -> p c", p=128))
            C.dma(SP, lbl, lb_logits.rearrange("l (c p) -> p l c", p=128))
        WA = sb([128, 512], BF16, "WA")
        C.dma(POOL, WA[0:64, :].k("lo"), w1u)
        C.dma(POOL, WA[64:128, :].k("hi"), a1u)
        g1u_bf = sb([128, 512], BF16, "g1u_bf")
        C.dma(POOL, g1u_bf, g1u)
        lnxw = sb([128, 512], BF16, "lnxw"); lnxb = sb([128, 512], BF16, "lnxb"); hgw = sb([128, 512], BF16, "hgw")
        C.dma(POOL, lnxw, row(ln_x_w).partition_broadcast(128))
        C.dma(POOL, lnxb, row(ln_x_b).partition_broadcast(128))
        C.dma(POOL, hgw, row(hg_norm_w).partition_broadcast(128))
        lb4 = sb([128, 4], F32, "lb4"); oml4 = sb([128, 4], F32, "oml4"); etmp = sb([128, 4], F32, "etmp")
        C.tt(DVE, etmp, lbl[:, 1, :], lbl[:, 0, :], ALU.subtract)
        C.act(etmp, etmp, AF.Exp)
        C.ts(DVE, lb4, etmp, 1.0, ALU.add)
        C.recip(lb4, lb4)
        C.tt(DVE, oml4, etmp, lb4, ALU.mult)
        RKsel = sb([128, 4, 2], BF16, "RKsel")
        C.tt(DVE, RKsel, bc3(rk4, 2), cc("hsel").unsqueeze(1).to_broadcast([128, 4, 2]), ALU.mult)

        ov_lo = arena.off
        w_in_bf = sb([128, 8, PJ], BF16, "w_in_bf")
        ov_hi = arena.off
        WIN_BLK = [(0, 512), (512, 1024), (1024, 1536), (1536, 1792), (1792, 2304), (2304, 2816), (2816, 3328), (3328, 3840)]

        def win_key(col):
            for i_, (a_, b_) in enumerate(WIN_BLK):
                if a_ <= col < b_:
                    return i_
        def issue_w_in():
            for i_, (a_, b_) in enumerate(WIN_BLK):
                C.dma(POOL, w_in_bf[:, :, a_:b_].k(i_), w_in[:, a_:b_].rearrange("(kc p) n -> p kc n", p=128))
        w_out_bf = sb([128, 8, D], BF16, "w_out_bf")
        def issue_w_out():
            for kc in range(8):
                C.dma(POOL, w_out_bf[:, kc, :].k(kc), w_out[kc * 128:(kc + 1) * 128, :])

        ST = sb([128, 4, 64], F32, "ST"); ST_bf = sb([128, 4, 64], BF16, "ST_bf")
        SH = sb([128, 4, 128], F32, "SH"); SH_bf = sb([128, 4, 128], BF16, "SH_bf")
        plast = sb([128, 14], F32, "plast")
        for t_ in (ST, ST_bf, SH, SH_bf, plast):
            C.memset(POOL, t_, 0.0)

        def prompt_state_outputs():
            with nc.allow_non_contiguous_dma(reason="tiny state vector"):
                C.dma(SP, shp.rearrange("(c p) -> p c", p=128), plast, out_final=True)
            C.dma(SP, hgp.rearrange("h k v -> k h v"), SH, out_final=True)
            identf = cc("ident")
            for pr in range(4):
                C.tr(ps[2][0:64, pr * 128:(pr + 1) * 128], ST[:, pr, :], identf)
            rwo = T[0]
            C.cp(DVE, rwo[0:64, :], ps[2][0:64, :])
            C.dma(SP, rwp.rearrange("h v j -> v h j"), rwo[0:64, :].rearrange("p (h j) -> p h j", h=8), out_final=True)
            if DBG:
                C.dma(SP, dbg_d, dbg, out_final=True)


        x_t = [sb([128, D], F32, f"x_t{i}") for i in range(2)]
        x_bf = sb([128, D], BF16, "x_bf")
        xT = sb([128, 8, 128], BF16, "xT")
        pr_ = sb([128, 14, 129], F32, "pr")
        xs = sb([128, 14, 128], F32, "xsft")
        T = [sb([128, 512], F32, f"T{i}") for i in range(10)]
        z12 = sb([128, 128], BF16, "z12"); sg_bf = sb([128, 128], BF16, "sg_bf"); tqb = sb([128, 512], BF16, "tqb")
        bhT = sb([128, 4, 128], BF16, "bhT"); khT = sb([128, 4, 128], BF16, "khT"); vT = sb([128, 4, 128], BF16, "vT")
        khTb = sb([128, 4, 128], BF16, "khTb")
        fence2 = sb([128, 2], F32, "fence2")
        HB = []
        for i in range(2):
            HB.append(dict(
                AR=sb([128, 4, 2, 128], BF16, f"AR{i}"), bT=sb([128, 4, 128], BF16, f"bT{i}"), kT=sb([128, 4, 128], BF16, f"kT{i}"),
                A_tm=sb([128, 512], BF16, f"A_tm{i}"), Bh_tm=sb([128, 512], BF16, f"Bh_tm{i}"),
                Kh_tm=sb([128, 512], BF16, f"Kh_tm{i}"), V_tm=sb([128, 512], BF16, f"V_tm{i}"),
                qTb=sb([128, 4, 128], BF16, f"qTb{i}"), kTb=sb([128, 4, 128], BF16, f"kTb{i}"),
                khat_tm=sb([128, 512], BF16, f"khat_tm{i}"), i_tm=sb([128, 512], BF16, f"i_tm{i}"),
                gs_t=sb([128, 512], BF16, f"gs_t{i}"), g_tm=sb([128, 512], BF16, f"g_tm{i}"),
                bon8=sb([128, 8], F32, f"bon8{i}"), gC=sb([128, 4, NB], F32, f"gC{i}"), decH=sb([128, 4, NB], F32, f"decH{i}")))
        Pm = sb([128, 8, 128], BF16, "Pm"); Tm = sb([128, 8, 128], BF16, "Tm"); RR = sb([128, 8, 128], BF16, "RR")
        NrbT = sb([128, 8, 128], BF16, "NrbT"); AkT = sb([128, 8, 128], BF16, "AkT"); NrkT = sb([128, 8, 128], BF16, "NrkT")
        TTf = sb([128, 8, 128], BF16, "TTf")
        W1T = sb([128, 4, 128], BF16, "W1T")
        Z_tm = sb([128, 512], BF16, "Z_tm"); U_tm = sb([128, 512], BF16, "U_tm")
        st16 = sb([128, 16], F32, "st16"); m8 = sb([128, 8], F32, "m8"); r8 = sb([128, 8], F32, "r8")
        o_all = sb([128, D], BF16, "o_all"); oT = sb([128, 8, 128], BF16, "oT")
        h1pre = sb([128, D], F32, "h1pre")
        attT = sb([128, 4, 128], BF16, "attT")
        s4 = sb([128, 4], F32, "s4"); rr4 = sb([128, 4], F32, "rr4")
        BT0 = h1pre[:, 0:512]; BT1 = h1pre[:, 512:1024]
        SHtmp = v3(BT1)
        STtmp = BT1[:, 0:256].rearrange("p (a v) -> p a v", a=4)
        identb4 = ident_bf.unsqueeze(1).to_broadcast([128, 4, 128])
        save_off = arena.off
        arena.off = ov_lo
        scrA = [sb([128, 2048], F32, f"scrA{i}") for i in range(2)]
        S0T32 = sb([128, NB, 4, 64], F32, "S0T32")
        S0Tb = sb([128, NB, 4, 64], BF16, "S0Tb")
        sshT = sb([128, 14, NB], F32, "sshT"); lastp = sb([128, 14, NB], F32, "lastp")
        EW = [sb([128, 4, 128], BF16, f"EW{i}") for i in range(2)]
        ER = [sb([128, 4, 128], BF16, f"ER{i}") for i in range(2)]
        EQ = [sb([128, 4, 128], BF16, f"EQ{i}") for i in range(2)]
        Ub = sb([128, 512], BF16, "Ub"); Vb = sb([128, 512], BF16, "Vb"); khb = sb([128, 512], BF16, "khb")
        Dg = sb([128, 4, 64], F32, "Dg")
        S0h = [sb([128, 4, 128], F32, f"S0h{i}") for i in range(2)]
        S0hb = sb([128, 4, 128], BF16, "S0hb")
        Sn = sb([128, 512], F32, "Sn")
        fence_t = sb([128, 2], F32, "fence_t")
        assert arena.off <= ov_hi, (arena.off, ov_hi)
        ov_bufs = scrA + [S0T32, S0Tb, sshT, lastp] + EW + ER + EQ + [Ub, Vb, khb, Dg] + S0h + [S0hb, Sn, fence_t]
        arena.off = save_off
        ssh_tm = scrA[0][0:NB, 0:RWP]
        hh_order = (0, 2, 4, 6, 1, 3, 5, 7)
        heads_of = [(0, 1, 2, 3), (4, 5, 6, 7)]
        PA, PB_ = 6, 7

        cb = cc

        first_tile = [True]

        def cfg(sample):
            sfx = "_s" if sample else "_p"
            return dict(nch=NB if sample else 1, Cn=TS if sample else 128, L=3 if sample else 7,
                        mS=cb("mS" + sfx), mI=cb("mI" + sfx), mST=cb("mST" + sfx), reset=cc("reset" + sfx))

        def stageA(ti, x_src, sample):
            g_ = cfg(sample)
            nch, Cn, reset = g_["nch"], g_["Cn"], g_["reset"]
            H = HB[ti % 2]
            AR, bT, kT = H["AR"], H["bT"], H["kT"]
            xt = x_t[ti % 2]
            C.dma(SP, xt, x_src)
            C.dma(POOL, x_bf, x_src)
            if first_tile[0]:
                first_tile[0] = False
                issue_w_in()
            for kc in range(8):
                C.tr(psb(PB_)[:, kc * 128:(kc + 1) * 128], x_bf[:, kc * 128:(kc + 1) * 128], ident_bf)
            C.cp(ACT, xT, psb(PB_).rearrange("p (a t) -> p a t", a=8))
            yield
            sig, kq, fgl, bcs, eb, enb, ebl, sq_, eg = T[1], T[4], T[0], T[2], T[3], T[5], T[6], T[7], T[8]

            def proj_fm(c0, n, bank):
                for j in range(n):
                    c = c0 + j
                    for kc in range(8):
                        C.mm(ps[bank][:, j * 128:(j + 1) * 128], w_in_bf[:, kc, c * 128:(c + 1) * 128].k(win_key(c * 128)),
                             xT[:, kc, :], start=(kc == 0), stop=(kc == 7))

            def proj_tm(col0, bank):
                for kc in range(8):
                    C.mm(ps[bank], xT[:, kc, :], w_in_bf[:, kc, col0:col0 + 512].k(win_key(col0)), start=(kc == 0), stop=(kc == 7))

            proj_fm(0, 4, PA); C.cp(ACT, pr_[:, 0:4, 1:129], psv(PA, 4)); yield
            proj_fm(4, 4, PB_); C.cp(ACT, pr_[:, 4:8, 1:129], psv(PB_, 4)); yield
            proj_fm(8, 4, PA); C.cp(ACT, pr_[:, 8:12, 1:129], psv(PA, 4)); yield
            proj_fm(12, 2, PB_); C.cp(ACT, pr_[:, 12:14, 1:129], psv(PB_, 4)[:, 0:2, :]); yield
            prev, cur = pr_[:, :, 0:128], pr_[:, :, 1:129]
            if not sample:
                C.cp(DVE, pr_[:, :, 0:1], plast.unsqueeze(2))
                C.cp(DVE, plast.unsqueeze(2), pr_[:, :, 128:129])
            else:
                C.memset(POOL, pr_[:, :, 0:1], 0.0)
            proj_fm(14, 4, PA)
            C.act(sq_, ps[PA], AF.Sigmoid)
            C.tt(DVE, sq_, ps[PA], sq_, ALU.mult)
            yield
            proj_fm(18, 4, PB_)
            C.act(sig, ps[PB_], AF.Sigmoid)
            yield
            proj_tm(RWP + 1024, PA)
            C.cp(ACT, H["i_tm"], ps[PA])
            yield
            proj_tm(RWP + 1536, PB_)
            C.act(eg, ps[PB_], AF.Sigmoid)
            C.tt(DVE, H["gs_t"], ps[PB_], eg, ALU.mult)
            yield
            if sample:
                C.rec(POOL, lambda e: e.memset(_ap(fence_t), 0.0),
                      [w_in_bf[:, :, a_:b_].k(i_) for i_, (a_, b_) in enumerate(WIN_BLK)] + ov_bufs
                      + [S0T32[:, b, :, :].k(b) for b in range(NB)] + [S0Tb[:, b, :, :].k(b) for b in range(NB)], [])
                for t_ in EW + ER + EQ:
                    C.memset(POOL, t_, 0.0)
                C.dma(SP, ssh_tm, ssh)
                for c in range(14):
                    C.tr(ps[PA][:, c * NB:(c + 1) * NB], ssh_tm[:, c * 128:(c + 1) * 128], cc("ident")[0:NB, 0:NB])
                C.cp(DVE, sshT, ps[PA][:, 0:14 * NB].rearrange("p (c b) -> p c b", c=14))
            for eng_, c0, c1 in ((DVE, 0, 8), (DVE, 8, 14)):
                C.tt(eng_, xs[:, c0:c1, :].k(c0), prev[:, c0:c1, :], cur[:, c0:c1, :], ALU.subtract)
                C.tt(eng_, xs[:, c0:c1, :].k(c0), xs[:, c0:c1, :].k(c0), bc3(mu14[:, c0:c1], 128), ALU.mult)
                C.tt(eng_, xs[:, c0:c1, :].k(c0), xs[:, c0:c1, :].k(c0), cur[:, c0:c1, :], ALU.add)
            C.rec(POOL, lambda e: e.memset(_ap(fence2), 0.0), [xs, fence2], [xs[:, 0:8, :].k(0), xs[:, 8:14, :].k(8)])
            if sample:
                cur4 = cur.rearrange("p c (b t) -> p c b t", t=TS)
                xs4 = xs.rearrange("p c (b t) -> p c b t", t=TS)
                cur0, xs0 = cur4[:, :, :, 0], xs4[:, :, :, 0]
                C.tt(DVE, xs0, sshT, cur0, ALU.subtract)
                C.tt(DVE, xs0, xs0, bc3(mu14, NB), ALU.mult)
                C.tt(DVE, xs0, xs0, cur0, ALU.add)
                C.cp(DVE, lastp, cur4[:, :, :, TS - 1])
                for g0 in range(0, 14, 4):
                    bk = PA if (g0 // 4) % 2 == 0 else PB_
                    n = min(4, 14 - g0)
                    for j in range(n):
                        C.tr(ps[bk][0:NB, j * 128:(j + 1) * 128], lastp[:, g0 + j, :], cc("ident"))
                    C.cp(ACT, ssh_tm[:, g0 * 128:(g0 + n) * 128], ps[bk][0:NB, 0:n * 128])
                C.dma(SP, shs, ssh_tm, out_final=True)
            yield
            r_ = xs[:, 0:4, :]; k_ = xs[:, 4:8, :]; v_ = xs[:, 8:12, :]
            C.act(z12[0:64, :], xs[0:64, 12, :], AF.Tanh)
            C.act(sg_bf, xs[:, 13, :], AF.Sigmoid)
            C.cp(DVE, z12[64:128, :], xs[64:128, 12, :])
            for pr in range(4):
                sl = slice(pr * 128, (pr + 1) * 128)
                C.mm(ps[PA][:, sl], WA[0:64, sl].k("lo"), z12[0:64, :])
            for pr in range(4):
                sl = slice(pr * 128, (pr + 1) * 128)
                C.mm(ps[PB_][:, sl], WA[64:128, sl].k("hi"), z12[64:128, :])
            sw, alr = T[9], T[8]
            for pr in range(4):
                sl = slice(pr * 128, (pr + 1) * 128)
                C.act(sw[:, sl], ps[PA][:, sl], AF.Sigmoid, bias=w04[:, pr:pr + 1])
            C.tt(DVE, v3(kq), v3(sig), bc3(oml4, 128), ALU.mult)
            C.tt(DVE, v3(fgl), v3(kq), bc3(lb4, 128), ALU.add)
            C.tt(DVE, v3(kq), bc3(oml4, 128), v3(kq), ALU.subtract)
            yield
            for pr in range(4):
                sl = slice(pr * 128, (pr + 1) * 128)
                C.act(alr[:, sl], ps[PB_][:, sl], AF.Sigmoid, bias=a04[:, pr:pr + 1])
            C.mm(ps[PA], sg_bf, g1u_bf)
            C.cp(ACT, H["g_tm"], ps[PA])
            yield
            C.act(fgl, fgl, AF.Ln)
            for h in range(4):
                C.scan(bcs[:, h * 128:(h + 1) * 128], reset, fgl[:, h * 128:(h + 1) * 128])
            C.act(eb, bcs, AF.Exp)
            C.act(enb, bcs, AF.Exp, scale=-1.0)
            bc4 = bcs.rearrange("p (a n c) -> p a n c", a=4, n=nch)
            C.tt(DVE, ebl.rearrange("p (a n c) -> p a n c", a=4, n=nch),
                 bc4[:, :, :, Cn - 1:Cn].to_broadcast([128, 4, nch, Cn]), bc4, ALU.subtract)
            C.act(ebl, ebl, AF.Exp)
            yield
            C.cp(DVE, H["decH"][:, :, 0:nch].unsqueeze(3),
                 eb.rearrange("p (a n c) -> p a n c", a=4, n=nch)[:, :, :, Cn - 1:Cn])
            C.tt(DVE, H["qTb"], v3(sq_), v3(eb), ALU.mult)
            C.tt(DVE, H["kTb"], v3(kq), v3(enb), ALU.mult)
            C.tt(DVE, khTb, v3(kq), v3(ebl), ALU.mult)
            yield
            cumS, gex, gin, ginv, glast, kkk, tq, k2 = T[2], T[3], T[4], T[5], T[6], T[7], T[0], T[1]
            for pr in range(4):
                C.scan(cumS[:, pr * 128:(pr + 1) * 128], reset, sw[:, pr * 128:(pr + 1) * 128])
            C.tt(DVE, gex, cumS, sw, ALU.subtract)
            C.act(gex, gex, AF.Exp, scale=CDEC)
            C.act(gin, cumS, AF.Exp, scale=CDEC)
            C.act(ginv, cumS, AF.Exp, scale=-CDEC)
            cs4 = cumS.rearrange("p (a n c) -> p a n c", a=4, n=nch)
            C.tt(DVE, glast.rearrange("p (a n c) -> p a n c", a=4, n=nch),
                 cs4[:, :, :, Cn - 1:Cn].to_broadcast([128, 4, nch, Cn]), cs4, ALU.subtract)
            C.act(glast, glast, AF.Exp, scale=CDEC)
            C.cp(DVE, H["gC"][:, :, 0:nch].unsqueeze(3),
                 gin.rearrange("p (a n c) -> p a n c", a=4, n=nch)[:, :, :, Cn - 1:Cn])
            yield
            C.tt(DVE, v3(kkk), k_, bc3(kk4, 128), ALU.mult)
            C.act(tqb, kkk, AF.Square)
            for pr in range(4):
                sl = slice(pr * 128, (pr + 1) * 128)
                C.mm(ps[PB_][:, sl], bdones_bf, tqb[:, sl])
            C.act(tq, ps[PB_], AF.Ln, bias=1e-24)
            C.act(tq, tq, AF.Exp, scale=-0.5)
            C.tt(DVE, kkk, kkk, tq, ALU.mult)
            C.stt(DVE, v3(k2), v3(alr), -1.0, bc3(ka4, 128), ALU.add, ALU.mult)
            C.stt(DVE, v3(k2), v3(k2), 1.0, k_, ALU.add, ALU.mult)
            C.tt(DVE, alr, kkk, alr, ALU.mult)
            b_ = alr
            yield
            C.tt(DVE, AR[:, :, 1, :], r_, v3(gin), ALU.mult)
            C.stt(DVE, AR[:, :, 0, :], v3(kkk), -1.0, v3(gex), ALU.mult, ALU.mult)
            C.tt(DVE, bT, v3(b_), v3(ginv), ALU.mult)
            C.tt(DVE, kT, v3(k2), v3(ginv), ALU.mult)
            C.tt(DVE, bhT, v3(b_), v3(glast), ALU.mult)
            C.tt(DVE, khT, v3(k2), v3(glast), ALU.mult)
            C.cp(ACT, vT, v_)
            C.tt(DVE, v3(tqb), r_, v3(k2), ALU.mult)
            for pr in range(4):
                C.mm(ps[PA][:, pr * 2:(pr + 1) * 2], tqb[:, pr * 128:(pr + 1) * 128], RKsel[:, pr, :])
            C.cp(DVE, H["bon8"], ps[PA][:, 0:8])
            yield
            for (src, bank, half) in ((lambda pr: AR[:, pr, 0, :], PA, 0), (lambda pr: bhT[:, pr, :], PA, 1),
                                      (lambda pr: khT[:, pr, :], PB_, 0), (lambda pr: vT[:, pr, :], PB_, 1)):
                for pr in range(4):
                    o0 = half * 512 + pr * 128
                    C.tr(psb(bank)[:, o0:o0 + 128], src(pr), ident_bf)
            C.cp(ACT, H["A_tm"], psb(PA)[:, 0:512]); C.cp(ACT, H["Bh_tm"], psb(PA)[:, 512:1024])
            C.cp(ACT, H["Kh_tm"], psb(PB_)[:, 0:512]); C.cp(ACT, H["V_tm"], psb(PB_)[:, 512:1024])
            yield
            for h in range(4):
                C.tr(psb(PA)[:, h * 128:(h + 1) * 128], khTb[:, h, :], ident_bf)
            C.cp(ACT, H["khat_tm"], psb(PA)[:, 0:512])
            yield

        def stageB(ti, h1_dst, sample, part="all"):
            g_ = cfg(sample)
            nch, Cn, L = g_["nch"], g_["Cn"], g_["L"]
            mSb = g_["mS"].unsqueeze(1).to_broadcast([128, 4, 128])
            mIb = g_["mI"].unsqueeze(1).to_broadcast([128, 4, 128])
            mSTb = g_["mST"].unsqueeze(1).to_broadcast([128, 4, 128])
            H = HB[ti % 2]
            AR, bT, kT, A_tm, Bh_tm, Kh_tm, V_tm = H["AR"], H["bT"], H["kT"], H["A_tm"], H["Bh_tm"], H["Kh_tm"], H["V_tm"]
            qTb, kTb, khat_tm, i_tm, gs_t, g_tm, bon8, gC, decH = (H["qTb"], H["kTb"], H["khat_tm"], H["i_tm"], H["gs_t"],
                                                                    H["g_tm"], H["bon8"], H["gC"], H["decH"])
            xt = x_t[ti % 2]
            if part in ("all", "hgrn"):
                bA, bO = (6, 7) if sample else (3, 4)
                for h in range(4):
                    C.mm(ps[bA][:, h * 128:(h + 1) * 128], kTb[:, h, :], qTb[:, h, :])
                C.tt(DVE, attT, psv(bA, 4), mIb, ALU.mult)
                if not sample:
                    for h in range(4):
                        hsl = slice(h * 128, (h + 1) * 128)
                        C.mm(ps[4][:, hsl], attT[:, h, :], i_tm[:, hsl], start=True, stop=False)
                        C.mm(ps[4][:, hsl], qTb[:, h, :], SH_bf[:, h, :], start=False, stop=True)
                    for h in range(4):
                        hsl = slice(h * 128, (h + 1) * 128)
                        C.mm(ps[5][:, hsl], khat_tm[:, hsl], i_tm[:, hsl])
                    C.tt(DVE, SHtmp, SH, bc3(decH[:, :, 0], 128), ALU.mult)
                    C.tt(DVE, SH, SHtmp, psv(5, 4), ALU.add)
                    C.cp(ACT, SH_bf, SH)
                else:
                    for h in range(4):
                        hsl = slice(h * 128, (h + 1) * 128)
                        C.mm(ps[bO][:, hsl], attT[:, h, :], i_tm[:, hsl], start=(h == 0), stop=False)
                    for b in range(NB):
                        s0 = S0h[b % 2]
                        csl = slice(b * TS, (b + 1) * TS)
                        C.dma(SP, s0, shg[b].rearrange("h k v -> k h v"))
                        C.cp(ACT, S0hb, s0)
                        eq = EQ[b % 2]
                        C.cp(POOL, eq[:, :, csl], qTb[:, :, csl])
                        for h in range(4):
                            C.mm(ps[bO][:, h * 128:(h + 1) * 128], eq[:, h, :], S0hb[:, h, :], start=False, stop=False)
                        C.memset(POOL, eq[:, :, csl], 0.0)
                        C.ts(DVE, khb, khat_tm, cc("cm")[:, b:b + 1], ALU.mult)
                        bank = bA
                        for h in range(4):
                            hsl = slice(h * 128, (h + 1) * 128)
                            C.mm(ps[bank][:, hsl], khb[:, hsl], i_tm[:, hsl])
                        C.tt(DVE, s0, s0, bc3(decH[:, :, b], 128), ALU.mult)
                        C.tt(DVE, s0, s0, psv(bank, 4), ALU.add)
                        C.dma(SP, hgs[b].rearrange("h k v -> k h v"), s0, out_final=True)
                        yield
                osq = T[0] if sample else BT0
                C.act(osq, ps[bO], AF.Square)
                C.red(DVE, s4, v3(osq))
                C.act(rr4, s4, AF.Ln, scale=1.0 / 128, bias=RMS_EPS)
                C.act(rr4, rr4, AF.Exp, scale=-0.5)
                C.tt(DVE, v3(osq), psv(bO, 4), bc3(rr4, 128), ALU.mult)
                C.tt(DVE, osq, osq, hgw, ALU.mult)
                C.tt(DVE, o_all[:, 512:1024], osq, gs_t, ALU.mult)
                yield
            if part == "hgrn":
                return
            if part in ("all", "rwkv"):
                if sample:
                    for g in range(4):
                        sa = scrA[g % 2]
                        nat = sa[0:64, :].rearrange("p (b n) -> p b n", b=4)
                        C.dma(SP, nat.rearrange("p b (h j) -> p b h j", h=8),
                              srw[g * 4:(g + 1) * 4].rearrange("b h v j -> v b h j"))
                        for bb in range(4):
                            b = g * 4 + bb
                            bank = b % 2
                            for pr in range(4):
                                C.tr(ps[bank][:, (bb % 2) * 256 + pr * 64:(bb % 2) * 256 + (pr + 1) * 64],
                                     nat[:, bb, pr * 128:(pr + 1) * 128], cc("ident")[0:64, 0:64])
                            C.cp(ACT, S0T32[:, b, :, :].k(b),
                                 ps[bank][:, (bb % 2) * 256:(bb % 2) * 256 + 256].rearrange("p (a v) -> p a v", a=4))
                            C.cp(DVE, S0Tb[:, b, :, :].k(b), S0T32[:, b, :, :].k(b))
                        yield
                for hg in range(2):
                    for i, h in enumerate(heads_of[hg]):
                        pr, hh = h // 2, h % 2
                        rows = slice(64 * hh, 64 * hh + 64)
                        sl = slice(i * 128, (i + 1) * 128)
                        C.mm(ps[0][:, sl], bT[rows, pr, :], AR[rows, pr, 0, :])
                        C.mm(ps[1][:, sl], bT[rows, pr, :], AR[rows, pr, 1, :])
                        C.mm(ps[2][:, sl], kT[rows, pr, :], AR[rows, pr, 0, :])
                        C.mm(ps[3][:, sl], kT[rows, pr, :], AR[rows, pr, 1, :])
                        C.mm(ps[4][:, sl], AR[rows, pr, 0, :], bT[rows, pr, :])
                    hs = slice(4 * hg, 4 * hg + 4)
                    C.tt(DVE, Pm[:, hs, :], psv(0, 4), mSb, ALU.mult)
                    C.tt(DVE, NrbT[:, hs, :], psv(1, 4), mIb, ALU.mult)
                    C.tt(DVE, AkT[:, hs, :], psv(2, 4), mSb, ALU.mult)
                    C.tt(DVE, NrkT[:, hs, :], psv(3, 4), mIb, ALU.mult)
                    C.tt(DVE, RR[:, hs, :], psv(4, 4), mSTb, ALU.mult)
                    C.tt(DVE, Tm[:, hs, :], Pm[:, hs, :], identb4, ALU.add)
                    yield
                for k in range(L):
                    for hg in range(2):
                        b0 = 3 * hg
                        hs = slice(4 * hg, 4 * hg + 4)
                        for i, h in enumerate(heads_of[hg]):
                            sl = slice(i * 128, (i + 1) * 128)
                            if k < L - 1:
                                C.mm(ps[b0][:, sl], RR[:, h, :], Pm[:, h, :])
                            if k >= 1:
                                C.mm(ps[b0 + 1][:, sl], RR[:, h, :], Tm[:, h, :])
                            if k < L - 1:
                                C.mm(ps[b0 + 2][:, sl], Pm[:, h, :], RR[:, h, :])
                        if k < L - 1:
                            C.cp(ACT, Pm[:, hs, :], psv(b0, 4))
                        if k >= 1:
                            dst = TTf[:, hs, :] if k == L - 1 else Tm[:, hs, :]
                            C.tt(DVE, dst, psv(b0 + 1, 4), Tm[:, hs, :], ALU.add)
                        if k < L - 1:
                            C.cp(ACT, RR[:, hs, :], psv(b0 + 2, 4))
                        yield
                for h in range(8):
                    C.mm(ps[2][:, h * 64:(h + 1) * 64], AkT[:, h, :], V_tm[:, h * 64:(h + 1) * 64])
                C.cp(ACT, Z_tm, ps[2])
                for h in range(8):
                    pr = h // 2
                    C.mm(ps[h // 4][:, (h % 4) * 128:(h % 4 + 1) * 128], A_tm[:, pr * 128:(pr + 1) * 128], TTf[:, h, :])
                for b in range(2):
                    pv = ps[b].rearrange("p (q e t) -> p q e t", q=2, e=2)
                    C.cp(ACT, W1T[0:64, 2 * b:2 * b + 2, :], pv[0:64, :, 0, :])
                    C.cp(ACT, W1T[64:128, 2 * b:2 * b + 2, :], pv[64:128, :, 1, :])
                yield
                for h in range(8):
                    pr, hh = h // 2, h % 2
                    rows = slice(64 * hh, 64 * hh + 64)
                    hsl = slice(h * 64, (h + 1) * 64)
                    if not sample:
                        C.mm(ps[3][:, hsl], TTf[:, h, :], Z_tm[:, hsl], start=True, stop=False)
                        C.mm(ps[3][:, hsl], W1T[rows, pr, :], ST_bf[rows, pr, :], start=False, stop=True)
                    else:
                        C.mm(ps[3][:, hsl], TTf[:, h, :], Z_tm[:, hsl], start=(h == 0), stop=False)
                if sample:
                    for b in range(NB):
                        ew = EW[b % 2]
                        csl = slice(b * TS, (b + 1) * TS)
                        C.cp(POOL, ew[:, :, csl], W1T[:, :, csl])
                        for h in hh_order:
                            pr, hh = h // 2, h % 2
                            rows = slice(64 * hh, 64 * hh + 64)
                            C.mm(ps[3][:, h * 64:(h + 1) * 64], ew[rows, pr, :], S0Tb[rows, b, pr, :].k(b), start=False, stop=False)
                        C.memset(POOL, ew[:, :, csl], 0.0)
                        yield
                C.cp(ACT, U_tm, ps[3])
                for h in range(8):
                    pr, hh = h // 2, h % 2
                    rows = slice(64 * hh, 64 * hh + 64)
                    hsl = slice(h * 64, (h + 1) * 64)
                    C.mm(ps[4][:, hsl], NrbT[:, h, :], U_tm[:, hsl], start=(h == 0 or not sample), stop=False)
                    C.mm(ps[4][:, hsl], NrkT[:, h, :], V_tm[:, hsl], start=False, stop=False)
                    if not sample:
                        C.mm(ps[4][:, hsl], AR[rows, pr, 1, :], ST_bf[rows, pr, :], start=False, stop=True)
                yield
                if sample:
                    for b in range(NB):
                        er = ER[b % 2]
                        csl = slice(b * TS, (b + 1) * TS)
                        C.cp(POOL, er[:, :, csl], AR[:, :, 1, csl])
                        for h in hh_order:
                            pr, hh = h // 2, h % 2
                            rows = slice(64 * hh, 64 * hh + 64)
                            C.mm(ps[4][:, h * 64:(h + 1) * 64], er[rows, pr, :], S0Tb[rows, b, pr, :].k(b), start=False, stop=False)
                        C.memset(POOL, er[:, :, csl], 0.0)
                        yield
                    i64b = cc("i64s").unsqueeze(1).to_broadcast([128, 4, 64])
                    for b in range(NB):
                        bank = b % 2
                        C.ts(DVE, Ub, U_tm, cc("cm")[:, b:b + 1], ALU.mult)
                        C.ts(DVE, Vb, V_tm, cc("cm")[:, b:b + 1], ALU.mult)
                        C.tt(DVE, Dg, i64b, bc3(gC[:, :, b], 64), ALU.mult)
                        for h in range(8):
                            hsl = slice(h * 64, (h + 1) * 64)
                            C.mm(ps[bank][0:64, hsl], Ub[:, hsl], Bh_tm[:, hsl], start=(h == 0), stop=False)
                            C.mm(ps[bank][0:64, hsl], Vb[:, hsl], Kh_tm[:, hsl], start=False, stop=False)
                        for h in hh_order:
                            pr, hh = h // 2, h % 2
                            rows = slice(64 * hh, 64 * hh + 64)
                            C.mm(ps[bank][0:64, h * 64:(h + 1) * 64], S0T32[rows, b, pr, :].k(b), Dg[rows, pr, :], start=False, stop=False)
                        C.cp(ACT, Sn[0:64, :], ps[bank][0:64, :])
                        C.dma(SP, rws[b].rearrange("h v j -> v h j"), Sn[0:64, :].rearrange("p (h j) -> p h j", h=8), out_final=True)
                        yield
                if not sample:
                    for pr in range(4):
                        psl = slice(pr * 128, (pr + 1) * 128)
                        C.mm(ps[5][:, psl], Bh_tm[:, psl], U_tm[:, psl], start=True, stop=False)
                        C.mm(ps[5][:, psl], Kh_tm[:, psl], V_tm[:, psl], start=False, stop=True)
                    C.tt(DVE, STtmp, ST, bc3(gC[:, :, 0], 64), ALU.mult)
                    p5 = psv(5, 4)
                    C.tt(DVE, ST[0:64, :, :], STtmp[0:64, :, :], p5[0:64, :, 0:64], ALU.add)
                    C.tt(DVE, ST[64:128, :, :], STtmp[64:128, :, :], p5[64:128, :, 64:128], ALU.add)
                    C.cp(ACT, ST_bf, ST)
                yield
                ysq, tmp2 = BT0, BT1
                y3 = ps[4].rearrange("p (h v) -> p h v", h=8)
                yv = ysq.rearrange("p (h v) -> p h v", h=8)
                C.red(DVE, st16[:, 0:8], y3)
                C.act(ysq, ps[4], AF.Square)
                C.red(DVE, st16[:, 8:16], yv)
                C.ts(DVE, m8, st16[:, 0:8], 1.0 / 64, ALU.mult)
                C.tt(DVE, r8, m8, m8, ALU.mult)
                C.stt(DVE, r8, st16[:, 8:16], 1.0 / 64, r8, ALU.mult, ALU.subtract)
                C.act(r8, r8, AF.Ln, bias=GN_EPS)
                C.act(r8, r8, AF.Exp, scale=-0.5)
                C.tt(DVE, yv, y3, bc3(m8, 64), ALU.subtract)
                C.tt(DVE, yv, yv, bc3(r8, 64), ALU.mult)
                C.tt(DVE, ysq, ysq, lnxw, ALU.mult)
                C.tt(DVE, ysq, ysq, lnxb, ALU.add)
                C.tt(DVE, tmp2.rearrange("p (h v) -> p h v", h=8), V_tm.rearrange("p (h v) -> p h v", h=8),
                     bc3(bon8, 64), ALU.mult)
                C.tt(DVE, ysq, ysq, tmp2, ALU.add)
                C.tt(DVE, o_all[:, 0:512], ysq, g_tm, ALU.mult)
                yield
            if part == "rwkv":
                return
            for mc in range(8):
                C.tr(psb(2)[:, mc * 128:(mc + 1) * 128], o_all[:, mc * 128:(mc + 1) * 128], ident_bf)
            C.cp(ACT, oT, psb(2).rearrange("p (a t) -> p a t", a=8))
            for half in range(2):
                for mc in range(8):
                    C.mm(ps[half], oT[:, mc, :], w_out_bf[:, mc, half * 512:(half + 1) * 512].k(mc),
                         start=(mc == 0), stop=(mc == 7))
            for half in range(2):
                hsl = slice(half * 512, (half + 1) * 512)
                C.stt(DVE, h1pre[:, hsl], xt[:, hsl], ALPHA, ps[half], ALU.mult, ALU.add)
            yield
            layernorm(h1pre, h1pre, LN_EPS)
            C.dma(SP, h1_dst, h1pre)
            yield

        def collect(g):
            C.defer = []
            for _ in g:
                pass
            lst, C.defer = C.defer, None
            return lst

        M_SEM, M_ACT, M_DVE, M_PEL, M_TAB = 0.25, 0.20, 0.08, 0.3, 1.3
        fin = {}
        eng_free = {e: 0.0 for e in ENGINES}

        def est_dur(p):
            eng, fn, outs, ins, dma, _ = p.args
            ap = _ap(outs[0])
            n = 1
            for d_ in ap.shape[1:]:
                n *= d_
            if dma:
                return 2.0
            if eng == PE:
                return 0.03 + max(n, 48) / 2400.0 * (4.0 if ap.dtype == F32 and False else 1.0)
            if eng == ACT:
                return M_ACT + n / 1200.0
            if eng == DVE:
                return M_DVE + n / 960.0
            return 0.10 + n / 500.0

        def dep_t(d, eng):
            if d.eng == PE and eng == PE:
                return fin.get(d, M_PEL) - M_PEL
            return fin.get(d, 0.0) + M_SEM

        cur_tab = [None]

        def est_start(p):
            eng, fn, outs, ins, dma, _ = p.args
            t = eng_free[eng]
            if p.tab is not None and p.tab != cur_tab[0]:
                t += M_TAB
            for x in ins:
                if x is None or _isnum(x):
                    continue
                r = C.R(x)
                if r.last_w is not None:
                    t = max(t, dep_t(r.last_w, eng))
            for x in outs:
                r = C.R(x)
                if r.last_w is not None:
                    t = max(t, dep_t(r.last_w, eng))
                for rd in r.readers:
                    t = max(t, dep_t(rd, eng))
            return t

        def commit_timed(p):
            t0 = est_start(p)
            o_ = C.commit(p)
            d_ = est_dur(p)
            if p.tab is not None:
                cur_tab[0] = p.tab
            fin[o_] = t0 + d_ + (M_PEL if p.args[0] == PE else 0.0)
            eng_free[p.args[0]] = t0 + (d_ if p.args[0] != SP else 0.1)
            return o_

        def interleave(ga, gb):
            la, lb_ = collect(ga), collect(gb)
            ia = ib = 0
            while ia < len(la) or ib < len(lb_):
                if ia >= len(la):
                    commit_timed(lb_[ib]); ib += 1
                elif ib >= len(lb_):
                    commit_timed(la[ia]); ia += 1
                else:
                    ta, tb = est_start(la[ia]), est_start(lb_[ib])
                    if tb <= ta:
                        commit_timed(lb_[ib]); ib += 1
                    else:
                        commit_timed(la[ia]); ia += 1

        def collect(g):
            C.defer = []
            for _ in g:
                pass
            lst, C.defer = C.defer, None
            return lst

        def drain(g):
            for p in collect(g):
                commit_timed(p)

        jobs = [(ti, xp[ti * 128:(ti + 1) * 128, :], V(h1scr[ti * 128:(ti + 1) * 128, :], ("h1scr", ti)), False)
                for ti in range(NT)]
        if SAMPLE:
            jobs.append((NT, xsm, V(h1scr[NTP * 128:(NTP + 1) * 128, :], ("h1scr", NTP)), True))
        if True:
            if jobs:
                drain(stageA(jobs[0][0], jobs[0][1], jobs[0][3]))
            issue_w_out()
            for n, (ti, x_src, h1_dst, smp) in enumerate(jobs):
                if n + 1 < len(jobs):
                    nj = jobs[n + 1]
                    interleave(stageA(nj[0], nj[1], nj[3]), stageB(ti, h1_dst, smp))
                elif smp:
                    interleave(stageB(ti, h1_dst, smp, "hgrn"), stageB(ti, h1_dst, smp, "rwkv"))
                    drain(stageB(ti, h1_dst, smp, "final"))
                else:
                    drain(stageB(ti, h1_dst, smp))
                if n == NT - 1 and not smp:
                    prompt_state_outputs()
            if NT == 0:
                prompt_state_outputs()

        if True:
            P.barrier()
            arena.off = phase_mark
            w_up_bf = sb([128, 8, DFF], BF16, "w_up_bf")
            w_dn_bf = sb([128, 32, D], BF16, "w_dn_bf")
            upT = sb([128, 32, 512], BF16, "upT")
            h1T = [sb([128, 8, 512], BF16, f"h1T{i}") for i in range(2)]
            h1b = [sb([128, D], BF16, f"h1b{i}") for i in range(2)]
            h1r = [sb([128, D], F32, f"h1r{i}") for i in range(1)]
            rl = [sb([128, 512], F32, f"rl{i}") for i in range(2)]
            outb = [sb([128, D], F32, f"outb{i}") for i in range(2)]
            tiles = list(range(NT)) + ([NTP] if SAMPLE else [])
            groups = [tiles[i:i + 4] for i in range(0, NT, 4)]
            if SAMPLE:
                groups.append([NTP])

            def load_h1b(g):
                for gi, tix in enumerate(groups[g]):
                    C.dma(POOL, h1b[gi % 2], V(h1scr[tix * 128:(tix + 1) * 128, :], ("h1scr", tix)))

            def transposes(g):
                for gi, tix in enumerate(groups[g]):
                    bank = 6 + (gi % 2)
                    for kc in range(8):
                        C.tr(psb(bank)[:, kc * 128:(kc + 1) * 128], h1b[gi % 2][:, kc * 128:(kc + 1) * 128], ident_bf)
                    C.cp(ACT, h1T[g % 2][:, :, gi * 128:(gi + 1) * 128], psb(bank).rearrange("p (a t) -> p a t", a=8))

            def issue_w_up(c0, c1):
                for cb in range(c0, c1):
                    C.dma(POOL, w_up_bf[:, :, cb * 512:(cb + 1) * 512].k(cb),
                          w_up[:, cb * 512:(cb + 1) * 512].rearrange("(kc p) n -> p kc n", p=128))

            def load_and_transpose(g, first=False):
                for gi, tix in enumerate(groups[g]):
                    if first and gi == 2:
                        issue_w_up(0, 3)
                    C.dma(POOL, h1b[gi % 2], V(h1scr[tix * 128:(tix + 1) * 128, :], ("h1scr", tix)))
                    bank = 6 + (gi % 2)
                    for kc in range(8):
                        C.tr(psb(bank)[:, kc * 128:(kc + 1) * 128], h1b[gi % 2][:, kc * 128:(kc + 1) * 128], ident_bf)
                    C.cp(ACT, h1T[g % 2][:, :, gi * 128:(gi + 1) * 128], psb(bank).rearrange("p (a t) -> p a t", a=8))

            if groups and len(groups[0]) > 2:
                load_and_transpose(0, first=True)
                issue_w_up(3, 8)
            else:
                if groups:
                    load_and_transpose(0)
                issue_w_up(0, 8)
            C.dma(SP, lng, row(ln2_g).partition_broadcast(128))
            C.dma(SP, lnb, row(ln2_b).partition_broadcast(128))
            for fc in range(32):
                C.dma(POOL, w_dn_bf[:, fc, :].k(fc), w_down[fc * 128:(fc + 1) * 128, :])
            tcount = 0
            for g, grp in enumerate(groups):
                ng = len(grp)
                W = ng * 128
                hT = h1T[g % 2]
                for fc in range(32):
                    bank = fc % 2
                    for kc in range(8):
                        C.mm(ps[bank][:, 0:W], w_up_bf[:, kc, fc * 128:(fc + 1) * 128].k(fc // 4), hT[:, kc, 0:W],
                             start=(kc == 0), stop=(kc == 7))
                    r = rl[fc % 2]
                    C.act(r[:, 0:W], ps[bank][:, 0:W], AF.Relu)
                    C.tt(POOL if fc % 2 else DVE, upT[:, fc, 0:W], r[:, 0:W], r[:, 0:W], ALU.mult)
                if g + 1 < len(groups):
                    load_and_transpose(g + 1)
                for gi, tix in enumerate(grp):
                    hr = h1r[0]
                    ob = outb[(tcount + gi) % 2]
                    C.dma(SP, hr, V(h1scr[tix * 128:(tix + 1) * 128, :], ("h1scr", tix)))
                    for half in range(2):
                        bank = 2 + ((gi * 2 + half) % 4)
                        for fc in range(32):
                            C.mm(ps[bank], upT[:, fc, gi * 128:(gi + 1) * 128], w_dn_bf[:, fc, half * 512:(half + 1) * 512].k(fc),
                                 start=(fc == 0), stop=(fc == 31))
                        hsl = slice(half * 512, (half + 1) * 512)
                        C.stt(DVE, ob[:, hsl], hr[:, hsl], ALPHA, ps[bank], ALU.mult, ALU.add)
                    layernorm(ob, ob, LN_EPS)
                    dst = ys if tix == NTP else yp[tix * 128:(tix + 1) * 128, :]
                    C.dma(SP, dst, ob, out_final=True)
                tcount += ng

        sems = {e: es.enter_context(nc.semaphore(f"s_{e}")) for e in ENGINES}
        rings = {e: [es.enter_context(nc.semaphore(f"r_{e}{i}")) for i in range(n)] for e, n in DMA_RING.items()}
        P.prepare(sems, rings)
        build.stats = dict(P.stats)
        build.arena_peak = arena.peak
        with nc.allow_low_precision(reason="bf16 matmul operands, fp32 accumulation"), \
                nc.allow_non_contiguous_dma(reason="tiny per-channel vectors / state layouts"), \
                nc.Block() as block:
            block.tensor(lambda eng: P.emit_engine(PE, eng))
            block.scalar(lambda eng: P.emit_engine(ACT, eng))
            block.vector(lambda eng: P.emit_engine(DVE, eng))
            block.gpsimd(lambda eng: P.emit_engine(POOL, eng))
            block.sync(lambda eng: P.emit_engine(SP, eng))
    return nc


IN_NAMES = ["w_in", "shift_mu", "w0", "w1u", "a0", "a1u", "g1u", "k_k", "k_a", "r_k", "ln_x_w", "ln_x_b",
            "lb_logits", "hg_norm_w", "w_out", "ln1_g", "ln1_b", "w_up", "w_down", "ln2_g", "ln2_b"]


def make_in_maps(inputs, n_cores=8):
    f = lambda a: np.ascontiguousarray(np.asarray(a, dtype=np.float32))
    shared = {}
    for k in IN_NAMES:
        a = f(inputs[k])
        a = a[0] if k != "lb_logits" else a
        if k == "r_k":
            a = a.reshape(-1)
        shared[k] = np.ascontiguousarray(a)
    shared["consts"] = CONSTS
    maps = []
    for c in range(n_cores):
        m = dict(shared)
        m["xp"] = f(inputs["x_prompt"][c])
        m["xs"] = f(inputs["x_sample"][c * NB:(c + 1) * NB]).reshape(NB * TS, D)
        m["srw"] = f(inputs["state_rwkv"][0, c * NB:(c + 1) * NB])
        m["shg"] = f(inputs["state_hgrn"][0, c * NB:(c + 1) * NB])
        m["ssh"] = f(inputs["state_shift"][0, c * NB:(c + 1) * NB])
        maps.append(m)
    return maps


_NC_CACHE = {}


def kernel(**inputs):
    if "nc" not in _NC_CACHE:
        _NC_CACHE["nc"] = build()
    nc = _NC_CACHE["nc"]
    maps = make_in_maps(inputs)
    res = run_bass_kernel_spmd(nc, maps, core_ids=list(range(8)))
    R = res.results
    y_prompt = np.stack([R[c]["yp"] for c in range(8)]).astype(np.float32)
    y_sample = np.concatenate([R[c]["ys"].reshape(NB, TS, D) for c in range(8)]).astype(np.float32)
    rw_p = np.stack([R[c]["rwp"] for c in range(8)])[None].astype(np.float32)
    rw_s = np.concatenate([R[c]["rws"] for c in range(8)])[None].astype(np.float32)
    hg_p = np.stack([R[c]["hgp"] for c in range(8)])[None].astype(np.float32)
    hg_s = np.concatenate([R[c]["hgs"] for c in range(8)])[None].astype(np.float32)
    sh_p = np.stack([R[c]["shp"] for c in range(8)])[None].astype(np.float32)
    sh_s = np.concatenate([R[c]["shs"] for c in range(8)])[None].astype(np.float32)
    return (y_prompt, y_sample, rw_p, rw_s, hg_p, hg_s, sh_p, sh_s)
```
